# Optimizing a Trainium2 kernel written in Bass

```python
import math
import jax, jax.numpy as jnp
from jax import lax
import numpy as np

D_MODEL = 1024
BATCH = 8
SEQ = 2048
DEPTH = 2

DIFF_HEADS = 4
DIFF_QK_DIM = 64
DIFF_V_DIM = 2 * DIFF_QK_DIM
DIFF_WIDTH = DIFF_HEADS * DIFF_V_DIM
GLA_HEADS = 4
GLA_WIDTH = D_MODEL - DIFF_WIDTH
GLA_V_DIM = GLA_WIDTH // GLA_HEADS
GLA_K_DIM = GLA_V_DIM // 2
GLA_KEY_WIDTH = GLA_HEADS * GLA_K_DIM
GLA_GATE_RANK = 16
GLA_GATE_TAU = 16.0
GLA_CHUNK = 64
D_FF = 4 * D_MODEL
N_BUCKETS = 32
MAX_DISTANCE = 128
Q_BLOCK = 128
LN_EPS = 1e-5
RMS_EPS = 1e-5
ALPHA = (2.0 * DEPTH) ** 0.25
BETA = (8.0 * DEPTH) ** -0.25
IN_SPLITS = (DIFF_WIDTH, DIFF_WIDTH, DIFF_WIDTH, GLA_KEY_WIDTH, GLA_KEY_WIDTH, GLA_WIDTH, GLA_WIDTH, GLA_GATE_RANK, GLA_GATE_RANK)
D_IN = sum(IN_SPLITS)
IN_OFFSETS = tuple(int(o) for o in np.cumsum(IN_SPLITS)[:-1])

kernel_name = "hybrid_diffattn_gla_deepnorm_encoder"


def layer_norm(x, g, b):
    xf = x.astype(jnp.float32)
    mu = jnp.mean(xf, axis=-1, keepdims=True)
    var = jnp.mean(jnp.square(xf - mu), axis=-1, keepdims=True)
    y = (xf - mu) * lax.rsqrt(var + LN_EPS)
    return (y * g.astype(jnp.float32) + b.astype(jnp.float32)).astype(x.dtype)


def rms_norm(x, w):
    xf = x.astype(jnp.float32)
    y = xf * lax.rsqrt(jnp.mean(jnp.square(xf), axis=-1, keepdims=True) + RMS_EPS)
    return (y * w.astype(jnp.float32)).astype(x.dtype)


def t5_bucket(rel):
    nb = N_BUCKETS // 2
    max_exact = nb // 2
    ret = jnp.where(rel > 0, nb, 0)
    n = jnp.abs(rel)
    large = max_exact + (jnp.log(jnp.maximum(n, 1).astype(jnp.float32) / max_exact)
                         / math.log(MAX_DISTANCE / max_exact) * (nb - max_exact)).astype(jnp.int32)
    large = jnp.minimum(large, nb - 1)
    return ret + jnp.where(n < max_exact, n, large)


def diff_attention(q1, q2, k1, k2, v, lam, table):
    B, H, S, dq = q1.shape
    dv = v.shape[-1]
    nb = S // Q_BLOCK

    def blocks(t):
        return t.reshape(B, H, nb, Q_BLOCK, dq).transpose(2, 0, 1, 3, 4)

    kpos = jnp.arange(S, dtype=jnp.int32)
    lam32 = lam.astype(jnp.float32)

    def one_block(args):
        q1b, q2b, start = args
        qpos = start + jnp.arange(Q_BLOCK, dtype=jnp.int32)
        bias = table[t5_bucket(kpos[None, :] - qpos[:, None])]
        bias = jnp.transpose(bias, (2, 0, 1))[None].astype(jnp.float32)
        s1 = jnp.einsum('bhqd,bhkd->bhqk', q1b, k1).astype(jnp.float32) + bias
        s2 = jnp.einsum('bhqd,bhkd->bhqk', q2b, k2).astype(jnp.float32) + bias
        a = (jax.nn.softmax(s1, axis=-1) - lam32 * jax.nn.softmax(s2, axis=-1)).astype(v.dtype)
        return jnp.einsum('bhqk,bhkd->bhqd', a, v)

    starts = jnp.arange(nb, dtype=jnp.int32) * Q_BLOCK
    out = lax.map(one_block, (blocks(q1), blocks(q2), starts))
    return out.transpose(1, 2, 0, 3, 4).reshape(B, H, S, dv)


def gla_direction(q, k, v, g, strict):
    B, H, S, dk = q.shape
    dv = v.shape[-1]
    C = GLA_CHUNK
    nc = S // C
    f32 = jnp.float32
    qc = q.astype(f32).reshape(B, H, nc, C, dk)
    kc = k.astype(f32).reshape(B, H, nc, C, dk)
    vc = v.astype(f32).reshape(B, H, nc, C, dv)
    b = jnp.cumsum(g.astype(f32).reshape(B, H, nc, C, dk), axis=3)
    b_last = b[..., -1:, :]
    q_t = qc * jnp.exp(b)
    k_t = kc * jnp.exp(-b)
    k_d = kc * jnp.exp(b_last - b)
    dec = jnp.exp(b_last[..., 0, :])
    mask = jnp.tril(jnp.ones((C, C), dtype=bool), k=-1 if strict else 0)
    scores = jnp.where(mask, jnp.einsum('bhncd,bhnsd->bhncs', q_t, k_t), 0.0)
    o_intra = jnp.einsum('bhncs,bhnse->bhnce', scores, vc)

    def step(state, inp):
        qn, kn, vn, dn = inp
        o = jnp.einsum('bhcd,bhde->bhce', qn, state)
        state = dn[..., None] * state + jnp.einsum('bhcd,bhce->bhde', kn, vn)
        return state, o

    xs = (jnp.moveaxis(q_t, 2, 0), jnp.moveaxis(k_d, 2, 0), jnp.moveaxis(vc, 2, 0), jnp.moveaxis(dec, 2, 0))
    _, o_inter = lax.scan(step, jnp.zeros((B, H, dk, dv), f32), xs)
    o = o_intra + jnp.moveaxis(o_inter, 0, 2)
    return o.reshape(B, H, S, dv)


def hybrid_layer(x, li, table, w_in, lq1, lk1, lq2, lk2, diff_norm_w, gate_up, gate_b,
                 gla_norm_w, w_o, ln1_g, ln1_b, w1, b1, w2, b2, ln2_g, ln2_b):
    B, S, _ = x.shape
    h = x @ w_in
    dq, dk_, dv_, gq, gk, gv, gr, gdf, gdb = jnp.split(h, IN_OFFSETS, axis=-1)

    dq = dq.reshape(B, S, DIFF_HEADS, 2, DIFF_QK_DIM) * (DIFF_QK_DIM ** -0.5)
    dk_ = dk_.reshape(B, S, DIFF_HEADS, 2, DIFF_QK_DIM)
    q1 = dq[..., 0, :].transpose(0, 2, 1, 3)
    q2 = dq[..., 1, :].transpose(0, 2, 1, 3)
    k1 = dk_[..., 0, :].transpose(0, 2, 1, 3)
    k2 = dk_[..., 1, :].transpose(0, 2, 1, 3)
    vd = dv_.reshape(B, S, DIFF_HEADS, DIFF_V_DIM).transpose(0, 2, 1, 3)
    lam_init = 0.8 - 0.6 * math.exp(-0.3 * li)
    lam = (jnp.exp(jnp.sum(lq1.astype(jnp.float32) * lk1.astype(jnp.float32)))
           - jnp.exp(jnp.sum(lq2.astype(jnp.float32) * lk2.astype(jnp.float32))) + lam_init)
    d_out = diff_attention(q1, q2, k1, k2, vd, lam, table)
    d_out = rms_norm(d_out, diff_norm_w) * (1.0 - lam_init)
    d_out = d_out.transpose(0, 2, 1, 3).reshape(B, S, DIFF_WIDTH)

    def heads(t, d):
        return t.reshape(B, S, GLA_HEADS, d).transpose(0, 2, 1, 3)
    gq = heads(gq, GLA_K_DIM) * (GLA_K_DIM ** -0.5)
    gk = heads(gk, GLA_K_DIM)
    gv = heads(gv, GLA_V_DIM)
    g_f = heads(jax.nn.log_sigmoid((gdf @ gate_up[0] + gate_b[0]).astype(jnp.float32)) / GLA_GATE_TAU, GLA_K_DIM)
    g_b = heads(jax.nn.log_sigmoid((gdb @ gate_up[1] + gate_b[1]).astype(jnp.float32)) / GLA_GATE_TAU, GLA_K_DIM)
    o_f = gla_direction(gq, gk, gv, g_f, False)
    flip = lambda t: jnp.flip(t, axis=2)
    o_b = flip(gla_direction(flip(gq), flip(gk), flip(gv), flip(g_b), True))
    g_out = rms_norm((o_f + o_b).astype(x.dtype), gla_norm_w)
    g_out = g_out.transpose(0, 2, 1, 3).reshape(B, S, GLA_WIDTH) * jax.nn.silu(gr)

    mix = jnp.concatenate([d_out, g_out], axis=-1) @ w_o
    x = layer_norm(ALPHA * x + mix, ln1_g, ln1_b)

    f = jnp.square(jax.nn.relu(x @ w1 + b1)) @ w2 + b2
    return layer_norm(ALPHA * x + f, ln2_g, ln2_b)


def setup_inputs(seed: int = 0) -> dict:
    key = jax.random.key(seed)
    ks = jax.random.split(key, 24)
    n = lambda k, shape, s: jax.random.normal(k, shape, jnp.float32) * s
    return {
        "x": n(ks[0], (BATCH, SEQ, D_MODEL), 1.0),
        "ln_emb_g": 1.0 + n(ks[1], (D_MODEL,), 0.02),
        "ln_emb_b": n(ks[2], (D_MODEL,), 0.02),
        "rel_bias_table": n(ks[3], (N_BUCKETS, DIFF_HEADS), 0.3),
        "w_in": n(ks[4], (DEPTH, D_MODEL, D_IN), D_MODEL ** -0.5),
        "lambda_q1": n(ks[5], (DEPTH, DIFF_QK_DIM), 0.1),
        "lambda_k1": n(ks[6], (DEPTH, DIFF_QK_DIM), 0.1),
        "lambda_q2": n(ks[7], (DEPTH, DIFF_QK_DIM), 0.1),
        "lambda_k2": n(ks[8], (DEPTH, DIFF_QK_DIM), 0.1),
        "diff_norm_w": 1.0 + n(ks[9], (DEPTH, DIFF_V_DIM), 0.02),
        "gla_gate_up": n(ks[10], (DEPTH, 2, GLA_GATE_RANK, GLA_KEY_WIDTH), GLA_GATE_RANK ** -0.5),
        "gla_gate_bias": n(ks[11], (DEPTH, 2, GLA_KEY_WIDTH), 0.1),
        "gla_norm_w": 1.0 + n(ks[12], (DEPTH, GLA_V_DIM), 0.02),
        "w_o": n(ks[13], (DEPTH, D_MODEL, D_MODEL), BETA * D_MODEL ** -0.5),
        "ln1_g": 1.0 + n(ks[14], (DEPTH, D_MODEL), 0.02),
        "ln1_b": n(ks[15], (DEPTH, D_MODEL), 0.02),
        "w_ffn1": n(ks[16], (DEPTH, D_MODEL, D_FF), D_MODEL ** -0.5),
        "b_ffn1": n(ks[17], (DEPTH, D_FF), 0.02),
        "w_ffn2": n(ks[18], (DEPTH, D_FF, D_MODEL), BETA * D_FF ** -0.5),
        "b_ffn2": n(ks[19], (DEPTH, D_MODEL), 0.02),
        "ln2_g": 1.0 + n(ks[20], (DEPTH, D_MODEL), 0.02),
        "ln2_b": n(ks[21], (DEPTH, D_MODEL), 0.02),
    }


def reference(x, ln_emb_g, ln_emb_b, rel_bias_table, w_in, lambda_q1, lambda_k1, lambda_q2, lambda_k2,
              diff_norm_w, gla_gate_up, gla_gate_bias, gla_norm_w, w_o, ln1_g, ln1_b,
              w_ffn1, b_ffn1, w_ffn2, b_ffn2, ln2_g, ln2_b):
    h = layer_norm(x, ln_emb_g, ln_emb_b)
    for li in range(DEPTH):
        h = hybrid_layer(h, li, rel_bias_table, w_in[li], lambda_q1[li], lambda_k1[li],
                         lambda_q2[li], lambda_k2[li], diff_norm_w[li], gla_gate_up[li],
                         gla_gate_bias[li], gla_norm_w[li], w_o[li], ln1_g[li], ln1_b[li],
                         w_ffn1[li], b_ffn1[li], w_ffn2[li], b_ffn2[li], ln2_g[li], ln2_b[li])
    return h
```

```python
import math
import os
from contextlib import ExitStack

import numpy as np
import concourse.bass as bass
import concourse.mybir as mybir
from concourse.bass_utils import run_bass_kernel_spmd

F32 = mybir.dt.float32
BF16 = mybir.dt.bfloat16
AF = mybir.ActivationFunctionType
ALU = mybir.AluOpType

S = 2048
D = 1024
DIN = 3104
DFF = 4096
NT = 16
ALPHA = (2.0 * 2) ** 0.25
ENGS = ("pe", "act", "dve", "pool", "sp")
EPOCH = 30000


class _Res:
    __slots__ = ("last_w", "readers")

    def __init__(self):
        self.last_w = None
        self.readers = []


class _Op:
    __slots__ = ("eng", "fn", "deps", "signal", "tok", "is_dma", "dkey")

    def __init__(self, eng, fn, is_dma, dkey):
        self.eng = eng
        self.fn = fn
        self.deps = []
        self.signal = False
        self.tok = None
        self.is_dma = is_dma
        self.dkey = dkey


class Sched:
    def __init__(self, nc):
        self.nc = nc
        self.ops = []
        self.res = {}
        self.pending = {e: [] for e in ENGS}

    def _r(self, key):
        x = self.res.get(key)
        if x is None:
            x = self.res[key] = _Res()
        return x

    def _add(self, op, reads, writes):
        deps = set()
        for k in reads:
            rs = self._r(k)
            if rs.last_w is not None:
                deps.add(rs.last_w)
        for k in writes:
            rs = self._r(k)
            if rs.last_w is not None:
                deps.add(rs.last_w)
            deps.update(rs.readers)
        for k in reads:
            self._r(k).readers.append(op)
        for k in writes:
            rs = self._r(k)
            rs.last_w = op
            rs.readers = []
        if self.pending[op.eng]:
            deps.update(self.pending[op.eng])
            self.pending[op.eng] = []
        deps.discard(op)
        op.deps = list(deps)
        self.ops.append(op)
        return op

    def op(self, eng, fn, reads=(), writes=()):
        return self._add(_Op(eng, fn, False, None), reads, writes)

    def dma(self, eng, fn, dkey=None, reads=(), writes=()):
        if dkey is None:
            dkey = ("w", writes[0])
        return self._add(_Op(eng, fn, True, dkey), reads, writes)

    def barrier(self):
        last = {}
        for o in self.ops:
            last[(o.eng, o.dkey) if o.is_dma else o.eng] = o
        b = list(last.values())
        self.pending = {e: list(b) for e in ENGS}

    def emit(self, final_wait_ops=()):
        nc = self.nc
        ops = self.ops
        for o in ops:
            for d in o.deps:
                if d.is_dma:
                    d.signal = True
                elif d.eng == "pe" and o.eng == "pe" and not o.is_dma:
                    continue
                else:
                    d.signal = True
        with ExitStack() as es:
            eng_sems = {e: [] for e in ENGS}
            cnt = {e: 0 for e in ENGS}
            dma_sems = {}
            dma_cnt = {}
            for o in ops:
                if o.is_dma:
                    if o.dkey not in dma_sems:
                        dma_sems[o.dkey] = es.enter_context(nc.semaphore("d%d" % len(dma_sems)))
                        dma_cnt[o.dkey] = 0
                    dma_cnt[o.dkey] += 16
                    o.tok = (dma_sems[o.dkey], dma_cnt[o.dkey])
                elif o.signal:
                    ep = cnt[o.eng] // EPOCH
                    if ep >= len(eng_sems[o.eng]):
                        eng_sems[o.eng].append(es.enter_context(nc.semaphore("e_%s_%d" % (o.eng, ep))))
                    cnt[o.eng] += 1
                    o.tok = (eng_sems[o.eng][ep], cnt[o.eng] - ep * EPOCH)
            per_eng = {e: [o for o in ops if o.eng == e] for e in ENGS}
            self.stats = {e: len(per_eng[e]) for e in ENGS}
            self.stats["sems"] = sum(len(v) for v in eng_sems.values()) + len(dma_sems)

            def run(e, eng):
                waited = {}
                for o in per_eng[e]:
                    need = {}
                    for d in o.deps:
                        if d.tok is None:
                            continue
                        if (not d.is_dma) and d.eng == "pe" and e == "pe" and not o.is_dma:
                            continue
                        s, v = d.tok
                        k = id(s)
                        if waited.get(k, 0) >= v:
                            continue
                        if k not in need or need[k][1] < v:
                            need[k] = (s, v)
                    for k, (s, v) in need.items():
                        eng.wait_ge(s, v)
                        waited[k] = v
                    ins = o.fn(eng)
                    if o.tok is not None:
                        ins.then_inc(o.tok[0], 16 if o.is_dma else 1)
                if e == "sp":
                    for o in final_wait_ops:
                        s, v = o.tok
                        eng.wait_ge(s, v)

            with nc.Block() as block:
                @block.sync
                def _(eng):
                    run("sp", eng)

                @block.tensor
                def _(eng):
                    run("pe", eng)

                @block.scalar
                def _(eng):
                    run("act", eng)

                @block.vector
                def _(eng):
                    run("dve", eng)

                @block.gpsimd
                def _(eng):
                    run("pool", eng)


def _t5_bucket(rel):
    nb = 16
    me = 8
    ret = np.where(rel > 0, nb, 0)
    n = np.abs(rel)
    large = me + (np.log(np.maximum(n, 1).astype(np.float32) / np.float32(me))
                  / np.float32(math.log(128 / me)) * np.float32(nb - me)).astype(np.int32)
    large = np.minimum(large, nb - 1)
    return ret + np.where(n < me, n, large)


MLEN = 1280


def _constants():
    c = {}
    c["c_ident"] = np.eye(128, dtype=np.float32)
    c["c_J"] = np.eye(128, dtype=np.float32)[::-1].copy()
    s = np.arange(128)[:, None]
    t = np.arange(128)[None, :]
    uf = np.zeros((128, 129), np.float32)
    uf[:, :128] = (s <= t)
    uf[:, 128] = 1.0
    ub = np.zeros((128, 129), np.float32)
    ub[:, :128] = (s >= t)
    ub[:, 128] = 1.0
    c["c_uf"] = uf
    c["c_ub"] = ub
    c["c_sf"] = (s > t).astype(np.float32)
    c["c_sb"] = (s < t).astype(np.float32)
    n = np.arange(MLEN)
    bk = _t5_bucket(639 - n)
    oh = np.zeros((32, MLEN), np.float32)
    oh[bk, n] = 1.0
    c["c_onehot"] = oh
    return c


class _Stop(Exception):
    pass


def build(nl=2, dbg=(), stop=None):
    nc = bass.Bass("TRN2", target_bir_lowering=False)

    def din(name, shape):
        return nc.dram_tensor(name, list(shape), F32, kind="ExternalInput").ap()

    x_d = din("x", [S, D])
    lnemb_g = din("ln_emb_g", [D])
    lnemb_b = din("ln_emb_b", [D])
    table_d = din("rel_bias_table", [32, 4])
    w_in_d = din("w_in", [2, D, DIN])
    lq1_d = din("lambda_q1", [2, 64])
    lk1_d = din("lambda_k1", [2, 64])
    lq2_d = din("lambda_q2", [2, 64])
    lk2_d = din("lambda_k2", [2, 64])
    dnw_d = din("diff_norm_w", [2, 128])
    gup_d = din("gla_gate_up", [2, 2, 16, 256])
    gbias_d = din("gla_gate_bias", [2, 2, 256])
    gnw_d = din("gla_norm_w", [2, 128])
    w_o_d = din("w_o", [2, D, D])
    ln1g_d = din("ln1_g", [2, D])
    ln1b_d = din("ln1_b", [2, D])
    w1_d = din("w_ffn1", [2, D, DFF])
    b1_d = din("b_ffn1", [2, DFF])
    w2_d = din("w_ffn2", [2, DFF, D])
    b2_d = din("b_ffn2", [2, D])
    ln2g_d = din("ln2_g", [2, D])
    ln2b_d = din("ln2_b", [2, D])
    c_ident = din("c_ident", [128, 128])
    c_J = din("c_J", [128, 128])
    c_uf = din("c_uf", [128, 129])
    c_ub = din("c_ub", [128, 129])
    c_sf = din("c_sf", [128, 128])
    c_sb = din("c_sb", [128, 128])
    c_onehot = din("c_onehot", [32, MLEN])
    out_d = nc.dram_tensor("out", [S, D], F32, kind="ExternalOutput").ap()
    xs_d = nc.dram_tensor("xs_scratch", [S, D], F32).ap()
    md_t = nc.dram_tensor("md_scratch", [4, MLEN], F32)
    eb_d = nc.dram_tensor("expb_scratch", [128, 4 * 1152], BF16).ap()
    md_d = md_t.ap()
    dbg_out = {}

    sc = Sched(nc)
    es = ExitStack()
    ARENA_BYTES = 207 * 1024
    arena = es.enter_context(nc.sbuf_tensor("arena", [128, ARENA_BYTES // 2], BF16))
    PSb = es.enter_context(nc.psum_tensor("ps", [128, 8, 1024], BF16))[:]
    PS = PSb.bitcast(F32)

    def view(off, nbytes, dt, pattern=None, **kw):
        assert off % 32 == 0, off
        a = arena[:, off // 2:(off + nbytes) // 2]
        if dt is F32:
            a = a.bitcast(F32)
        if pattern:
            a = a.rearrange(pattern, **kw)
        return a

    class Alloc:
        def __init__(self, base, size):
            self.base = base
            self.size = size
            self.pos = 0

        def reset(self):
            self.pos = 0

        def get(self, nbytes, dt, pattern=None, **kw):
            n = (nbytes + 31) // 32 * 32
            assert self.pos + n <= self.size, (self.pos, n, self.size)
            v = view(self.base + self.pos, nbytes, dt, pattern, **kw)
            self.pos += n
            return v

    R_XT = Alloc(0, 32768)
    R_X = Alloc(32768, 65536)
    R_D = Alloc(98304, 70656)
    R_W = Alloc(168960, 16384)
    R_C = Alloc(185344, ARENA_BYTES - 185344)

    XT = R_XT.get(32768, BF16, "p (c n) -> p c n", c=8)
    X = R_X.get(65536, F32, "p (t n) -> p t n", t=NT)
    WB = [R_W.get(8192, BF16, "p (c n) -> p c n", c=8) for _ in range(2)]
    R_W.reset()
    WO = R_W.get(16384, BF16, "p (c n) -> p c n", c=8)

    identb = R_C.get(256, BF16)
    Jb = R_C.get(256, BF16)
    Uf = R_C.get(516, F32)
    Ub = R_C.get(516, F32)
    Usf = R_C.get(512, F32)
    Usb = R_C.get(512, F32)
    gt = R_C.get(4096, F32)
    bt = R_C.get(4096, F32)
    b2t = R_C.get(4096, F32)
    wd_t = R_C.get(512, F32)
    wg_t = R_C.get(512, F32)
    b1c = R_C.get(128, F32)
    cb = R_C.get(32, F32, "p (s h) -> p s h", s=2)
    lamv = R_C.get(4 * 64 * 4, F32, "p (a n) -> p a n", a=4)
    lamp = R_C.get(2 * 64 * 4, F32, "p (a n) -> p a n", a=2)
    lams = R_C.get(32, F32)
    Wg = R_C.get(1024, BF16)
    st_ = [R_C.get(48, F32) for _ in range(2)]
    mv_ = [R_C.get(8, F32) for _ in range(2)]
    rs_ = [R_C.get(4, F32) for _ in range(2)]
    xb_ = [R_C.get(2048, BF16) for _ in range(2)]
    dsm = [R_C.get(64, F32) for _ in range(2)]
    gsm = [R_C.get(64, F32) for _ in range(2)]
    decs = R_C.get(2 * 2 * 16 * 4, F32, "p (d q t) -> p d q t", d=2, q=2)

    def psk(b):
        return [("ps", b, q) for q in range(4)]

    def bc_mid(ap2, n):
        a = ap2.ap
        return bass.AP(ap2.tensor, ap2.offset, [list(a[0]), [0, n], list(a[1])])

    def bc_last(ap2, n):
        a = ap2.ap
        return bass.AP(ap2.tensor, ap2.offset, [list(a[0]), list(a[1]), [0, n]])

    cur_layer = [-1]

    def dump(name, ap, shape, reads):
        nm = "%s@%d" % (name, cur_layer[0])
        if nm in dbg:
            name = nm
        elif name not in dbg or (cur_layer[0] >= 0 and cur_layer[0] != nl - 1):
            return
        t = nc.dram_tensor("dbg_" + name.replace("@", "_"), list(shape), ap.dtype, kind="ExternalOutput").ap()
        dbg_out[name] = sc.dma("sp", lambda e: e.dma_start(out=t, in_=ap), "dbg", reads=reads)

    try:
        sc.dma("pool", lambda e: e.dma_start(out=identb, in_=c_ident), writes=["identb"])
        sc.dma("pool", lambda e: e.dma_start(out=Jb, in_=c_J), writes=["Jb"])
        sc.dma("sp", lambda e: e.dma_start(out=Uf, in_=c_uf), writes=["Uf"])
        sc.dma("sp", lambda e: e.dma_start(out=Ub, in_=c_ub), writes=["Ub"])
        sc.dma("sp", lambda e: e.dma_start(out=Usf, in_=c_sf), writes=["Usf"])
        sc.dma("sp", lambda e: e.dma_start(out=Usb, in_=c_sb), writes=["Usb"])
        for si, row in enumerate((15, 31)):
            sc.dma("sp", lambda e, si=si, row=row: e.dma_start(out=cb[:, si, :], in_=table_d[row, :].partition_broadcast(128)),
                   writes=[("cb", si)])

        R_D.reset()
        tb = R_D.get(16, F32)
        oh = R_D.get(MLEN * 4, F32)
        msb = R_D.get(MLEN * 4, F32)
        sc.dma("sp", lambda e: e.dma_start(out=tb[0:32, :], in_=table_d), writes=["tb"])
        sc.dma("sp", lambda e: e.dma_start(out=oh[0:32, :], in_=c_onehot), writes=["oh"])
        sc.dma("sp", lambda e: e.dma_start(out=gt, in_=lnemb_g.partition_broadcast(128)), writes=["gt"])
        sc.dma("sp", lambda e: e.dma_start(out=bt, in_=lnemb_b.partition_broadcast(128)), writes=["bt"])
        for t in range(NT):
            sc.dma("sp", lambda e, t=t: e.dma_start(out=X[:, t, :], in_=x_d[t * 128:(t + 1) * 128, :]), writes=[("X", t)])
        for ci, (c0, cn) in enumerate(((0, 512), (512, 512), (1024, 256))):
            sc.op("pe", lambda e, ci=ci, c0=c0, cn=cn: e.matmul(PS[0:4, ci, 0:cn], lhsT=tb[0:32, :], rhs=oh[0:32, c0:c0 + cn], start=True, stop=True),
                  reads=["tb", "oh"], writes=psk(ci))
            sc.op("dve", lambda e, ci=ci, c0=c0, cn=cn: e.tensor_copy(out=msb[0:4, c0:c0 + cn], in_=PS[0:4, ci, 0:cn]),
                  reads=psk(ci), writes=["msb"])
        sc.dma("sp", lambda e: e.dma_start(out=md_d, in_=msb[0:4, :]), reads=["msb"], writes=["md"])
        R_D.reset()
        R_D.get(16384, BF16); R_D.get(16384, BF16); R_D.get(16 * 4 * 129 * 2, BF16)
        expB0 = R_D.get(4 * 1152 * 2, BF16, "p (h n) -> p h n", h=4)
        R_D.get(4096, BF16); R_D.get(1024, BF16)
        tmp_revs = [view(R_D.base + 16384 + i * 2304, 2304, BF16) for i in range(4)]
        for h in range(4):
            src = bass.AP(md_t, h * MLEN, [[1, 128], [1, 1152]])
            sc.dma("pool", lambda e, src=src, h=h: e.dma_start(out=tmp_revs[h], in_=src), reads=["md"], writes=[("tmp_rev", h)])
        for h in range(4):
            for ci, (c0, cn) in enumerate(((0, 512), (512, 512), (1024, 128))):
                sc.op("pe", lambda e, ci=ci, c0=c0, cn=cn, h=h: e.matmul(PS[:, ci, 0:cn], lhsT=Jb, rhs=tmp_revs[h][:, c0:c0 + cn], start=True, stop=True),
                      reads=["Jb", ("tmp_rev", h)], writes=psk(ci))
                sc.op("act", lambda e, h=h, ci=ci, c0=c0, cn=cn: e.activation(out=expB0[:, h, c0:c0 + cn], in_=PS[:, ci, 0:cn], func=AF.Exp),
                      reads=psk(ci), writes=[("expB", h)])
        sc.dma("sp", lambda e: e.dma_start(out=eb_d, in_=expB0.rearrange("p h n -> p (h n)")), reads=[("expB", h) for h in range(4)], writes=["eb_d"])

        def ln_a(t):
            Xt = X[:, t, :]
            kx = ("X", t)
            b = t % 2
            st, mv, rs = st_[b], mv_[b], rs_[b]
            sc.op("dve", lambda e: e.bn_stats(out=st[:, 0:6], in_=Xt[:, 0:512]), reads=[kx], writes=[("st", b, 0)])
            sc.op("dve", lambda e: e.bn_stats(out=st[:, 6:12], in_=Xt[:, 512:1024]), reads=[kx], writes=[("st", b, 1)])
            sc.op("dve", lambda e: e.bn_aggr(out=mv, in_=st), reads=[("st", b, 0), ("st", b, 1)], writes=[("mv", b)])
            sc.op("act", lambda e: e.activation(out=rs, in_=mv[:, 1:2], func=AF.Sqrt, bias=1e-5, scale=1.0), reads=[("mv", b)], writes=[("rs", b)])
            sc.op("dve", lambda e: e.reciprocal(out=rs, in_=rs), reads=[("rs", b)], writes=[("rs", b)])
            sc.op("dve", lambda e: e.tensor_scalar(out=Xt, in0=Xt, scalar1=mv[:, 0:1], scalar2=rs, op0=ALU.subtract, op1=ALU.mult),
                  reads=[kx, ("mv", b), ("rs", b)], writes=[kx])
            sc.op("dve", lambda e: e.tensor_tensor(out=Xt, in0=Xt, in1=gt, op=ALU.mult), reads=[kx, "gt"], writes=[kx])
            sc.op("pool", lambda e: e.tensor_tensor(out=Xt, in0=Xt, in1=bt, op=ALU.add), reads=[kx, "bt"], writes=[kx])

        def ln_b1(t, spill_to):
            Xt = X[:, t, :]
            kx = ("X", t)
            b = t % 2
            xb = xb_[b]
            if spill_to is not None:
                sc.dma("sp", lambda e: e.dma_start(out=spill_to[t * 128:(t + 1) * 128, :], in_=Xt), ("xs", t % 4), reads=[kx], writes=[("xsd", t)])
            if spill_to is out_d:
                return
            sc.op("act", lambda e: e.activation(out=xb, in_=Xt, func=AF.Copy), reads=[kx], writes=[("xb", b)])

        def ln_b2(t, spill_to):
            if spill_to is out_d:
                return
            b = t % 2
            xb = xb_[b]
            bank = 6 + b
            for c in range(8):
                sc.op("pe", lambda e, c=c: e.transpose(out=PSb[:, bank, c * 128:(c + 1) * 128], in_=xb[:, c * 128:(c + 1) * 128], identity=identb),
                      reads=[("xb", b), "identb"], writes=psk(bank))
            sc.op("act", lambda e: e.activation(out=XT[:, :, t * 128:(t + 1) * 128], in_=PSb[:, bank, :].rearrange("p (c n) -> p c n", c=8), func=AF.Copy),
                  reads=psk(bank), writes=[("XT", t)])

        def ln_all(spill_to, hook=None):
            ln_a(0)
            for t in range(NT):
                if t + 1 < NT:
                    ln_a(t + 1)
                ln_b1(t, spill_to)
                ln_b2(t, spill_to)
                if hook is not None and t % 4 == 3:
                    hook(t // 4)

        def load_ln_params(g_ap, b_ap):
            sc.dma("sp", lambda e: e.dma_start(out=gt, in_=g_ap.partition_broadcast(128)), writes=["gt"])
            sc.dma("sp", lambda e: e.dma_start(out=bt, in_=b_ap.partition_broadcast(128)), writes=["bt"])

        def load_win_block(l, blk, buf):
            c0 = blk * 512
            ncol = min(512, DIN - c0)
            src = w_in_d[l, :, c0:c0 + ncol].rearrange("(c p) n -> p c n", p=128)
            for hf in range(2):
                sc.dma("pool", lambda e, hf=hf: e.dma_start(out=WB[buf][:, hf * 4:(hf + 1) * 4, 0:ncol], in_=src[:, hf * 4:(hf + 1) * 4, :]),
                       writes=[("RW", buf, hf)])

        if stop == 'init':
            raise _Stop()
        load_win_block(0, 0, 0)
        load_win_block(0, 1, 1)

        if stop == 'emb':
            raise _Stop()
        evac_rr = [0]

        def evac(out, in_, reads, writes, scale=None):
            evac_rr[0] ^= 1
            if evac_rr[0]:
                if scale is None:
                    sc.op("act", lambda e: e.activation(out=out, in_=in_, func=AF.Copy), reads=reads, writes=writes)
                else:
                    sc.op("act", lambda e: e.mul(out=out, in_=in_, mul=scale), reads=reads, writes=writes)
            else:
                if scale is None:
                    sc.op("dve", lambda e: e.tensor_copy(out=out, in_=in_), reads=reads, writes=writes)
                else:
                    sc.op("dve", lambda e: e.tensor_scalar(out=out, in0=in_, scalar1=scale, scalar2=None, op0=ALU.mult), reads=reads, writes=writes)

        def do_layer(l):
            lam_init = 0.8 - 0.6 * math.exp(-0.3 * l)
            cur_layer[0] = l
            last = (l == nl - 1)
            R_D.reset()
            QT = R_D.get(16384, BF16, "p (h n) -> p h n", h=4)
            KT = R_D.get(16384, BF16, "p (h n) -> p h n", h=4)
            V = R_D.get(16 * 4 * 129 * 2, BF16, "p (t h e) -> p t h e", t=16, h=4)
            expB = R_D.get(4 * 1152 * 2, BF16, "p (h n) -> p h n", h=4)
            Eb = R_D.get(4096, BF16, "p (b m n) -> p b m n", b=2, m=2)
            d_y = R_D.get(4 * 128 * 2, BF16, "p (u n) -> p u n", u=4)
            _pu = R_D.pos
            silu_t = [R_D.get(2048, F32) for _ in range(2)]
            R_D.pos = _pu
            accS = R_D.get(8 * 129 * 4, F32, "p (a n) -> p a n", a=8)
            R_D.pos = _pu + 4608
            tmp_rev = R_D.get(1152 * 2, BF16)
            R_X.reset()
            gqT = R_X.get(8192, BF16, "p (c n) -> p c n", c=2)
            gkT = R_X.get(8192, BF16, "p (c n) -> p c n", c=2)
            gk_tok = R_X.get(8192, BF16, "p (t n) -> p t n", t=16)
            gv = R_X.get(16384, BF16, "p (t n) -> p t n", t=16)
            gr_s = R_X.get(16384, BF16, "p (t n) -> p t n", t=16)
            G33 = R_X.get(4096, BF16)

            for i, ap in enumerate((lq1_d, lk1_d, lq2_d, lk2_d)):
                sc.dma("sp", lambda e, i=i, ap=ap: e.dma_start(out=lamv[:, i, :], in_=ap[l, :].partition_broadcast(128)), writes=[("lamv", i)])
            sc.op("dve", lambda e: e.tensor_tensor(out=lamp[:, 0, :], in0=lamv[:, 0, :], in1=lamv[:, 1, :], op=ALU.mult), reads=[("lamv", 0), ("lamv", 1)], writes=["lamp"])
            sc.op("dve", lambda e: e.tensor_tensor(out=lamp[:, 1, :], in0=lamv[:, 2, :], in1=lamv[:, 3, :], op=ALU.mult), reads=[("lamv", 2), ("lamv", 3)], writes=["lamp"])
            sc.op("dve", lambda e: e.reduce_sum(out=lams[:, 0:2], in_=lamp, axis=mybir.AxisListType.X), reads=["lamp"], writes=["lams"])
            sc.op("act", lambda e: e.activation(out=lams[:, 0:2], in_=lams[:, 0:2], func=AF.Exp), reads=["lams"], writes=["lams"])
            sc.op("dve", lambda e: e.tensor_tensor(out=lams[:, 2:3], in0=lams[:, 0:1], in1=lams[:, 1:2], op=ALU.subtract), reads=["lams"], writes=["lams"])
            sc.op("dve", lambda e: e.tensor_scalar(out=lams[:, 3:4], in0=lams[:, 2:3], scalar1=lam_init, scalar2=-1.0, op0=ALU.add, op1=ALU.mult),
                  reads=["lams"], writes=["neglam"])
            neg_lam = lams[:, 3:4]
            sc.dma("sp", lambda e: e.dma_start(out=wd_t, in_=dnw_d[l, :].partition_broadcast(128)), writes=["wd"])
            sc.op("dve", lambda e: e.tensor_scalar(out=wd_t, in0=wd_t, scalar1=1.0 - lam_init, scalar2=None, op0=ALU.mult), reads=["wd"], writes=["wd"])
            sc.dma("sp", lambda e: e.dma_start(out=wg_t, in_=gnw_d[l, :].partition_broadcast(128)), writes=["wg"])
            sc.op("dve", lambda e: e.memset(Wg[0:33, :], 0.0), writes=["Wg"])
            sc.dma("pool", lambda e: e.dma_start(out=Wg[0:16, 0:256], in_=gup_d[l, 0]), writes=["Wg"])
            sc.dma("pool", lambda e: e.dma_start(out=Wg[16:32, 256:512], in_=gup_d[l, 1]), writes=["Wg"])
            sc.dma("pool", lambda e: e.dma_start(out=Wg[32:33, :], in_=gbias_d[l].rearrange("a n -> (a n)").partition_broadcast(1)), writes=["Wg"])
            sc.dma("sp", lambda e: e.dma_start(out=b1c, in_=b1_d[l].rearrange("(c p) -> p c", p=128), allow_slow_non_contiguous=True), writes=["b1c"])
            sc.dma("sp", lambda e: e.dma_start(out=b2t, in_=b2_d[l].partition_broadcast(128)), writes=["b2t"])
            sc.dma("sp", lambda e: e.dma_start(out=expB.rearrange("p h n -> p (h n)"), in_=eb_d), reads=["eb_d"], writes=[("expB", h) for h in range(4)])
            sc.op("dve", lambda e: e.memset(V[:, :, :, 128:129], 1.0), writes=[("V", t) for t in range(NT)])

            dump("XTin", XT, [128, 8, 2048], [("XT", t) for t in range(NT)])
            dump("Xin", X, [128, NT, 1024], [("X", t) for t in range(NT)])
            if stop == 'L' and l == nl - 1:
                raise _Stop()
            ps_rr = [0]

            def nextbank():
                b = ps_rr[0] % 6
                ps_rr[0] += 1
                return b

            xt_all = [("XT", t) for t in range(NT)]

            def fm_group(blk, cc, r, wb, kw):
                bank = nextbank()
                M = 32 if blk == 6 else 128
                for c in range(8):
                    sc.op("pe", lambda e, c=c: e.matmul(
                        PS[0:M, bank, :], lhsT=wb[:, c, cc * 128:cc * 128 + M], rhs=XT[:, c, r * 512:(r + 1) * 512],
                        start=(c == 0), stop=(c == 7)), reads=kw + xt_all[r * 4:(r + 1) * 4], writes=psk(bank))
                sl = slice(r * 512, (r + 1) * 512)
                if blk == 0:
                    evac(QT[:, cc, sl], PS[:, bank, :], psk(bank), [("QT", cc, r)], scale=0.125)
                elif blk == 1:
                    evac(KT[:, cc, sl], PS[:, bank, :], psk(bank), [("KT", cc, r)])
                elif blk == 3:
                    if cc < 2:
                        evac(gqT[:, cc, sl], PS[:, bank, :], psk(bank), [("gqT", 4 * r + i) for i in range(4)], scale=0.125)
                    else:
                        evac(gkT[:, cc - 2, sl], PS[:, bank, :], psk(bank), [("gkT", 4 * r + i) for i in range(4)])
                else:
                    evac(G33[0:32, sl], PS[0:32, bank, :], psk(bank), ["G33"])

            def p01(r):
                for blk in (0, 1):
                    for cc in range(4):
                        fm_group(blk, cc, r, WB[blk], [("RW", blk, 0), ("RW", blk, 1)])

            yield p01
            load_win_block(l, 2, 0)
            load_win_block(l, 3, 1)
            for blk in range(2, 7):
                buf = blk % 2
                wb = WB[buf]
                kw = [("RW", buf, 0), ("RW", buf, 1)]
                if blk == 3:
                    sc.barrier()
                if blk == 6:
                    sc.op("dve", lambda e: e.memset(G33[32:33, :], 1.0), writes=["G33"])
                if blk in (3, 6):
                    nch = 1 if blk == 6 else 4
                    for cc in range(nch):
                        for r in range(4):
                            fm_group(blk, cc, r, wb, kw)
                if blk in (2, 3, 4, 5):
                    for t in range(NT):
                        bank = nextbank()
                        c0, ncol = (256, 256) if blk == 3 else (0, 512)
                        for c in range(8):
                            sc.op("pe", lambda e, c=c, t=t, bank=bank, c0=c0, ncol=ncol, wb=wb: e.matmul(
                                PS[:, bank, 0:ncol], lhsT=XT[:, c, t * 128:(t + 1) * 128], rhs=wb[:, c, c0:c0 + ncol],
                                start=(c == 0), stop=(c == 7)), reads=kw + [("XT", t)], writes=psk(bank))
                        if blk == 2:
                            evac(V[:, t, :, 0:128], PS[:, bank, :].rearrange("p (h e) -> p h e", h=4), psk(bank), [("V", t)])
                        elif blk == 3:
                            evac(gk_tok[:, t, :], PS[:, bank, 0:256], psk(bank), [("gk_tok", t)])
                        elif blk == 4:
                            evac(gv[:, t, :], PS[:, bank, :], psk(bank), [("gv", t)])
                        else:
                            sb = t % 2
                            sc.op("act", lambda e, bank=bank, sb=sb: e.activation(out=silu_t[sb], in_=PS[:, bank, :], func=AF.Silu),
                                  reads=psk(bank), writes=[("silu", sb)])
                            sc.op("dve", lambda e, t=t, sb=sb: e.tensor_tensor(
                                out=gr_s[:, t, :].rearrange("p (h e) -> p h e", h=4), in0=silu_t[sb].rearrange("p (h e) -> p h e", h=4),
                                in1=bc_mid(wg_t, 4), op=ALU.mult), reads=[("silu", sb), "wg"], writes=[("gr_s", t)])
                if blk + 2 < 7:
                    load_win_block(l, blk + 2, buf)
            for hf in range(2):
                sc.dma("pool", lambda e, hf=hf: e.dma_start(out=WO[:, hf * 4:(hf + 1) * 4, :],
                                                             in_=w_o_d[l].rearrange("(c p) n -> p c n", p=128)[:, hf * 4:(hf + 1) * 4, :]),
                       writes=[("RW", hf, 0), ("RW", hf, 1)])
            dump("QT", QT, [128, 4, 2048], [("QT", a, b) for a in range(4) for b in range(4)])
            dump("KT", KT, [128, 4, 2048], [("KT", a, b) for a in range(4) for b in range(4)])
            dump("V", V, [128, 16, 4, 129], [("V", t) for t in range(NT)])
            dump("expB", expB, [128, 4, 1152], [("expB", h) for h in range(4)])
            dump("gqT", gqT, [128, 2, 2048], [("gqT", t) for t in range(NT)])
            dump("gr_s", gr_s, [128, 16, 512], [("gr_s", t) for t in range(NT)])
            dump("G33", G33[0:33, :], [33, 2048], ["G33"])

            if stop == 'P' and l == nl - 1:
                raise _Stop()
            steps = [(h, r, j) for h in range(4) for r in range(4) for j in range(16)]

            def acc_ap(m, u):
                idx = m * 4 + u
                return PS[:, 4 + idx // 3, (idx % 3) * 160:(idx % 3) * 160 + 129]

            def acc_keys(m, u):
                return psk(4 + (m * 4 + u) // 3)

            Eb3 = view(R_X.base + 61440, 2048, BF16, "p (m n) -> p m n", m=2)
            EbL = [Eb[:, 0, :, :], Eb[:, 1, :, :], Eb3]

            def d_scores(i):
                h, r, j = steps[i]
                d = j - 4 * r
                mixed = (-1 <= d <= 4)
                sb = i % 2
                eb = i % 3
                E = EbL[eb]
                for m in range(2):
                    bank = sb * 2 + m
                    sc.op("pe", lambda e, h=h, r=r, j=j, m=m, bank=bank: e.matmul(
                        PS[:, bank, :], lhsT=KT[64 * m:64 * m + 64, h, j * 128:(j + 1) * 128],
                        rhs=QT[64 * m:64 * m + 64, h, r * 512:(r + 1) * 512], start=True, stop=True),
                        reads=[("KT", h, j // 4), ("QT", h, r)], writes=psk(bank))
                pk2 = psk(sb * 2) + psk(sb * 2 + 1)
                ek = [("E", eb, 0), ("E", eb, 1)]
                if mixed:
                    c0 = (4 - d) * 128
                    sc.op("act", lambda e, sb=sb, E=E: e.activation(out=E, in_=PS[:, sb * 2:sb * 2 + 2, :], func=AF.Exp),
                          reads=pk2, writes=ek)
                    for m in range(2):
                        sc.op("dve", lambda e, E=E, m=m, h=h, c0=c0: e.tensor_tensor(out=E[:, m, :], in0=E[:, m, :], in1=expB[:, h, c0:c0 + 512], op=ALU.mult),
                              reads=[("E", eb, m), ("expB", h)], writes=[("E", eb, m)])
                else:
                    side = 0 if d < 0 else 1
                    sc.op("act", lambda e, sb=sb, E=E, side=side, h=h: e.activation(
                        out=E, in_=PS[:, sb * 2:sb * 2 + 2, :], func=AF.Exp, bias=cb[:, side, h:h + 1]),
                        reads=pk2 + [("cb", 0), ("cb", 1)], writes=ek)

            def d_av(i):
                h, r, j = steps[i]
                eb = i % 3
                E = EbL[eb]
                for m in range(2):
                    for u in range(4):
                        sc.op("pe", lambda e, h=h, j=j, m=m, u=u, E=E: e.matmul(
                            acc_ap(m, u), lhsT=E[:, m, u * 128:(u + 1) * 128], rhs=V[:, j, h, 0:129],
                            start=(j == 0 and (m * 4 + u) % 3 == 0), stop=(j == 15), skip_group_check=True),
                            reads=[("E", eb, m), ("V", j)], writes=acc_keys(m, u))

            sm = dsm[0]
            ka = ["accS", ("silu", 0), ("silu", 1)]

            def d_final(h, r):
                sc.op("act", lambda e: e.activation(out=accS[:, 0:3, :], in_=PS[:, 4, 0:480].rearrange("p (a n) -> p a n", a=3)[:, :, 0:129], func=AF.Copy),
                      reads=psk(4), writes=ka)
                sc.op("dve", lambda e: e.tensor_copy(out=accS[:, 3:6, :], in_=PS[:, 5, 0:480].rearrange("p (a n) -> p a n", a=3)[:, :, 0:129]),
                      reads=psk(5), writes=ka)
                sc.op("act", lambda e: e.activation(out=accS[:, 6:8, :], in_=PS[:, 6, 0:320].rearrange("p (a n) -> p a n", a=2)[:, :, 0:129], func=AF.Copy),
                      reads=psk(6), writes=ka)
                sc.op("dve", lambda e: e.reciprocal(out=sm[:, 0:8], in_=accS[:, :, 128]), reads=ka[:1], writes=["dsm"])
                sc.op("dve", lambda e: e.tensor_scalar(out=sm[:, 4:8], in0=sm[:, 4:8], scalar1=neg_lam, scalar2=None, op0=ALU.mult),
                      reads=["dsm", "neglam"], writes=["dsm"])
                sc.op("dve", lambda e: e.memset(sm[:, 8:12], 0.0), writes=["dss"])

            def d_final_u(u):
                if True:
                    sc.op("dve", lambda e, u=u: e.tensor_scalar(out=accS[:, u, 0:128], in0=accS[:, u, 0:128], scalar1=sm[:, u:u + 1], scalar2=None, op0=ALU.mult),
                          reads=["dsm"] + ka[:1], writes=ka[:1])
                    sc.op("dve", lambda e, u=u: e.scalar_tensor_tensor(out=accS[:, u, 0:128], in0=accS[:, 4 + u, 0:128], scalar=sm[:, 4 + u:5 + u],
                                                                       in1=accS[:, u, 0:128], op0=ALU.mult, op1=ALU.add),
                          reads=["dsm"] + ka[:1], writes=ka[:1])
                    sc.op("dve", lambda e, u=u: e.scalar_tensor_tensor(out=accS[:, 4 + u, 0:128], in0=accS[:, u, 0:128], scalar=1.0, in1=accS[:, u, 0:128],
                                                                       op0=ALU.mult, op1=ALU.mult, accum_out=sm[:, 8 + u:9 + u]),
                          reads=ka[:1], writes=ka[:1] + ["dss"])

            def d_final_b(h, r):
                sc.op("act", lambda e: e.activation(out=sm[:, 12:16], in_=sm[:, 8:12], func=AF.Ln, bias=1e-5, scale=1.0 / 128), reads=["dss"], writes=["drs"])
                sc.op("act", lambda e: e.activation(out=sm[:, 12:16], in_=sm[:, 12:16], func=AF.Exp, scale=-0.5), reads=["drs"], writes=["drs"])
                for u in range(4):
                    sc.op("dve", lambda e, u=u: e.scalar_tensor_tensor(out=d_y[:, u, :], in0=accS[:, u, 0:128], scalar=sm[:, 12 + u:13 + u], in1=wd_t,
                                                                       op0=ALU.mult, op1=ALU.mult),
                          reads=ka[:1] + ["drs", "wd"], writes=[("dy", u)])

            def d_final_pe(h, r):
                for u in range(4):
                    sc.op("pe", lambda e, u=u: e.transpose(out=PSb[:, 7, u * 128:(u + 1) * 128], in_=d_y[:, u, :], identity=identb),
                          reads=[("dy", u), "identb"], writes=psk(7))
                sc.op("act", lambda e, h=h, r=r: e.activation(out=XT[:, h, r * 512:(r + 1) * 512], in_=PSb[:, 7, 0:512], func=AF.Copy),
                      reads=psk(7), writes=[("XT", 4 * r + i) for i in range(4)])

            pend = []
            pend_b = []
            pend_u = []
            d_scores(0)
            d_scores(1)
            for i in range(len(steps)):
                if i + 2 < len(steps):
                    d_scores(i + 2)
                d_av(i)
                h, r, j = steps[i]
                if j == 15:
                    d_final(h, r)
                    pend.append((h, r))
                    pend_b.append((h, r))
                    pend_u.extend([0, 1, 2, 3])
                    d_final_u(pend_u.pop(0))
                elif pend_u:
                    d_final_u(pend_u.pop(0))
                elif j == 4 and pend_b:
                    d_final_b(*pend_b.pop(0))
                elif j == 7 and pend:
                    d_final_pe(*pend.pop(0))
            while pend_u:
                d_final_u(pend_u.pop(0))
            while pend_b:
                d_final_b(*pend_b.pop(0))
            while pend:
                d_final_pe(*pend.pop(0))
            dump("mixT_d", XT, [128, 8, 2048], [("XT", t) for t in range(NT)])

            if stop == 'D' and l == nl - 1:
                raise _Stop()
            sc.barrier()
            R_D.reset()
            qf = R_D.get(8192, BF16, "p (c n) -> p c n", c=2)
            kf = R_D.get(8192, BF16, "p (c n) -> p c n", c=2)
            kd_f = R_D.get(8192, BF16, "p (t n) -> p t n", t=16)
            Sbf = R_D.get(16384, BF16, "p (d q t e) -> p d q t e", d=2, q=2, t=16)
            stm2 = [R_D.get(4096, F32, "p (d q n) -> p d q n", d=2, q=2) for _ in range(2)]
            _p0 = R_D.pos
            sp_ = [R_D.get(2048, F32) for _ in range(2)]
            _p1 = R_D.pos
            ebt = [R_D.get(2 * 2 * 129 * 4, F32, "p (d q n) -> p d q n", d=2, q=2) for _ in range(2)]
            _p2 = R_D.pos
            enbt = [R_D.get(2 * 2 * 128 * 4, F32, "p (d q n) -> p d q n", d=2, q=2) for _ in range(2)]
            erem = [R_D.get(2048, F32) for _ in range(2)]
            dS = [R_D.get(1024, F32) for _ in range(4)]
            _pend = R_D.pos
            Am = [R_X.get(4 * 2 * 128 * 2, BF16, "p (h d n) -> p h d n", h=4, d=2) for _ in range(2)]
            R_D.pos = _p2
            g_y = [R_D.get(1024, BF16) for _ in range(2)]
            g_junk = R_D.get(512, F32)
            R_D.pos = _pend
            qb, kb, kd_b = gqT, gkT, gk_tok
            maskf = Uf[:, 0:128]
            maskb = Usf

            def tl(t):
                return slice(t * 128, (t + 1) * 128)

            def prep_A(t):
                b = t % 2
                sp = sp_[b]
                zb = 0 if b == 0 else 7
                sc.op("pe", lambda e: e.matmul(PS[:, zb, :], lhsT=G33[0:33, tl(t)], rhs=Wg[0:33, :], start=True, stop=True),
                      reads=["G33", "Wg"], writes=psk(zb))
                sc.op("act", lambda e: e.activation(out=sp, in_=PS[:, zb, :], func=AF.Exp, scale=-1.0), reads=psk(zb), writes=[("sp", b)])
                sc.op("act", lambda e: e.activation(out=sp, in_=sp, func=AF.Ln, bias=1.0, scale=1.0), reads=[("sp", b)], writes=[("sp", b)])

            prep_A(0)
            for t in range(NT):
                b = t % 2
                sp = sp_[b]
                if t + 1 < NT:
                    prep_A(t + 1)
                sc.op("pe", lambda e, sp=sp: e.matmul(PS[:, 1, 0:256], lhsT=Usf, rhs=sp[:, 0:256], start=True, stop=True), reads=[("sp", b), "Usf", "Usb"], writes=psk(1))
                sc.op("pe", lambda e, sp=sp: e.matmul(PS[:, 1, 256:512], lhsT=Usb, rhs=sp[:, 256:512], start=True, stop=True), reads=[("sp", b), "Usf", "Usb"], writes=psk(1))
                sc.op("act", lambda e, b=b: e.activation(out=erem[b], in_=PS[:, 1, :], func=AF.Exp, scale=-1.0 / 16), reads=psk(1), writes=[("erem", b)])
                sc.op("dve", lambda e, t=t, b=b: e.tensor_tensor(out=kd_f[:, t, :], in0=gk_tok[:, t, :], in1=erem[b][:, 0:256], op=ALU.mult),
                      reads=[("gk_tok", t), ("erem", b)], writes=[("kd_f", t)])
                sc.op("dve", lambda e, t=t, b=b: e.tensor_tensor(out=kd_b[:, t, :], in0=gk_tok[:, t, :], in1=erem[b][:, 256:512], op=ALU.mult),
                      reads=[("gk_tok", t), ("erem", b), ("kd_f", t)], writes=[("gk_tok", t)])
                for d in range(2):
                    U = Uf if d == 0 else Ub
                    for q in range(2):
                        sc.op("pe", lambda e, sp=sp, d=d, q=q, U=U: e.matmul(PS[:, 2 + d, q * 160:q * 160 + 129],
                                                                            lhsT=sp[:, d * 256 + q * 128:d * 256 + (q + 1) * 128], rhs=U, start=True, stop=True),
                              reads=[("sp", b), "Uf", "Ub"], writes=psk(2 + d))
                src4 = PS[:, 2:4, 0:320].rearrange("p a (q n) -> p a q n", q=2)
                sc.op("act", lambda e, b=b, src4=src4: e.activation(out=ebt[b], in_=src4[:, :, :, 0:129], func=AF.Exp, scale=-1.0 / 16),
                      reads=psk(2) + psk(3), writes=[("eb", b, 0), ("eb", b, 1)])
                sc.op("act", lambda e, b=b, src4=src4: e.activation(out=enbt[b], in_=src4[:, :, :, 0:128], func=AF.Exp, scale=1.0 / 16),
                      reads=psk(2) + psk(3), writes=[("enb", b, 0), ("enb", b, 1)])
                sc.op("dve", lambda e, t=t, b=b: e.tensor_tensor(out=qf[:, :, tl(t)], in0=gqT[:, :, tl(t)], in1=ebt[b][:, 0, :, 0:128], op=ALU.mult),
                      reads=[("gqT", t), ("eb", b, 0)], writes=[("qf", t)])
                sc.op("dve", lambda e, t=t, b=b: e.tensor_tensor(out=kf[:, :, tl(t)], in0=gkT[:, :, tl(t)], in1=enbt[b][:, 0, :, :], op=ALU.mult),
                      reads=[("gkT", t), ("enb", b, 0)], writes=[("kf", t)])
                sc.op("dve", lambda e, t=t, b=b: e.tensor_tensor(out=qb[:, :, tl(t)], in0=gqT[:, :, tl(t)], in1=ebt[b][:, 1, :, 0:128], op=ALU.mult),
                      reads=[("gqT", t), ("eb", b, 1), ("qf", t)], writes=[("gqT", t)])
                sc.op("dve", lambda e, t=t, b=b: e.tensor_tensor(out=kb[:, :, tl(t)], in0=gkT[:, :, tl(t)], in1=enbt[b][:, 1, :, :], op=ALU.mult),
                      reads=[("gkT", t), ("enb", b, 1), ("kf", t)], writes=[("gkT", t)])
                sc.op("dve", lambda e, t=t, b=b: e.tensor_copy(out=decs[:, :, :, t:t + 1], in_=ebt[b][:, :, :, 128:129]),
                      reads=[("eb", b, 0), ("eb", b, 1)], writes=["decs"])
            dump("qf", qf, [128, 2, 2048], [("qf", t) for t in range(NT)])
            dump("kd_f", kd_f, [128, 16, 256], [("kd_f", t) for t in range(NT)])
            dump("decs", decs, [128, 2, 2, 16], ["decs"])

            if stop == 'G1' and l == nl - 1:
                raise _Stop()
            sc.op("dve", lambda e: e.memset(stm2[0], 0.0), writes=[("stm", 0, d, q) for d in range(2) for q in range(2)])
            chains = [(d, q) for d in range(2) for q in range(2)]
            par = {c: 0 for c in chains}
            for i in range(NT):
                todo = []
                for ci, (d, q) in enumerate(chains):
                    t = i if d == 0 else NT - 1 - i
                    cur = par[(d, q)]
                    if i > 0:
                        sc.op("dve", lambda e, d=d, q=q, t=t, cur=cur: e.tensor_copy(out=Sbf[0:64, d, q, t, :], in_=stm2[cur][0:64, d, q, 0:128]),
                              reads=[("stm", cur, d, q)], writes=[("Sbf", d, q, t)])
                        sc.op("dve", lambda e, d=d, q=q, t=t, cur=cur: e.tensor_copy(out=Sbf[64:128, d, q, t, :], in_=stm2[cur][64:128, d, q, 128:256]),
                              reads=[("stm", cur, d, q)], writes=[("Sbf", d, q, t)])
                    if i == NT - 1:
                        continue
                    kd = kd_f if d == 0 else kd_b
                    kkey = "kd_f" if d == 0 else "gk_tok"
                    pslot = ci % 2
                    pk = psk(4 + pslot)
                    sc.op("pe", lambda e, kd=kd, t=t, q=q, pslot=pslot: e.matmul(PS[:, 4 + pslot, 0:256], lhsT=kd[:, t, q * 128:(q + 1) * 128],
                                                                                rhs=gv[:, t, q * 256:(q + 1) * 256], start=True, stop=True),
                          reads=[(kkey, t), ("gv", t)], writes=pk)
                    if os.environ.get("GSKIP") != "evac":
                        sc.op("act", lambda e, ci=ci, pslot=pslot: e.activation(out=dS[ci], in_=PS[:, 4 + pslot, 0:256], func=AF.Copy),
                              reads=pk, writes=[("dS", ci)])
                    todo.append((ci, d, q, t, cur))
                for (ci, d, q, t, cur) in todo:
                    if os.environ.get("GSKIP") == "upd":
                        par[(d, q)] = 1 - cur
                        continue
                    sc.op("dve", lambda e, ci=ci, d=d, q=q, t=t, cur=cur: e.scalar_tensor_tensor(
                        out=stm2[1 - cur][:, d, q, :], in0=stm2[cur][:, d, q, :], scalar=decs[:, d, q, t:t + 1], in1=dS[ci],
                        op0=ALU.mult, op1=ALU.add), reads=[("stm", cur, d, q), "decs", ("dS", ci)], writes=[("stm", 1 - cur, d, q)])
                    par[(d, q)] = 1 - cur
            dump("Sbf", Sbf, [128, 2, 2, 16, 128], [("Sbf", d, q, t) for d in range(2) for q in range(2) for t in range(NT)])
            if stop == 'G2' and l == nl - 1:
                raise _Stop()

            def g_A(t):
                b = t % 2
                A = Am[b]
                for half in range(2):
                    sbank = (4 + half) if os.environ.get('GBANK') else (2 * b + half)
                    items = []
                    for sq in range(4):
                        h, d = half + 2 * (sq // 2), sq % 2
                        q = h // 2
                        base = (h % 2) * 64
                        kk = kf if d == 0 else kb
                        qq = qf if d == 0 else qb
                        kkey = ("kf", t) if d == 0 else ("gkT", t)
                        qkey = ("qf", t) if d == 0 else ("gqT", t)
                        sc.op("pe", lambda e, kk=kk, qq=qq, q=q, base=base, sbank=sbank, sq=sq: e.matmul(
                            PS[:, sbank, sq * 128:(sq + 1) * 128], lhsT=kk[base:base + 64, q, tl(t)], rhs=qq[base:base + 64, q, tl(t)], start=True, stop=True),
                            reads=[kkey, qkey], writes=psk(sbank))
                        items.append((sq, h, d))
                        if os.environ.get("GOLD"):
                            mk = maskf if d == 0 else maskb
                            sc.op("dve", lambda e, A=A, h=h, d=d, sbank=sbank, sq=sq, mk=mk: e.tensor_tensor(
                                out=A[:, h, d, :], in0=PS[:, sbank, sq * 128:(sq + 1) * 128], in1=mk, op=ALU.mult),
                                reads=psk(sbank) + ["Uf", "Usf"], writes=[("A", b, h, d), ("sp", b)])
                    if os.environ.get("GOLD"):
                        continue
                    for (sq, h, d) in items:
                        mk = maskf if d == 0 else maskb
                        sc.op("dve", lambda e, A=A, h=h, d=d, sbank=sbank, sq=sq, mk=mk: e.tensor_tensor(
                            out=A[:, h, d, :], in0=PS[:, sbank, sq * 128:(sq + 1) * 128], in1=mk, op=ALU.mult),
                            reads=psk(sbank) + ["Uf", "Usf"], writes=[("A", b, h, d), ("sp", b)])

            def g_B(t):
                b = t % 2
                A = Am[b]
                obank = (0 + b) if os.environ.get('GBANK') else (4 + b)
                for h in range(4):
                    q = h // 2
                    base = (h % 2) * 64
                    oh_ = PS[:, obank, h * 128:(h + 1) * 128]
                    ok = psk(obank)
                    inter_f = t > 0
                    inter_b = t < NT - 1
                    sc.op("pe", lambda e, A=A, h=h, oh_=oh_: e.matmul(oh_, lhsT=A[:, h, 0, :], rhs=gv[:, t, h * 128:(h + 1) * 128], start=True, stop=False),
                          reads=[("A", b, h, 0), ("gv", t)], writes=ok)
                    sc.op("pe", lambda e, A=A, h=h, oh_=oh_, fin=(not inter_f and not inter_b): e.matmul(
                        oh_, lhsT=A[:, h, 1, :], rhs=gv[:, t, h * 128:(h + 1) * 128], start=False, stop=fin),
                        reads=[("A", b, h, 1), ("gv", t)], writes=ok)
                    if inter_f:
                        sc.op("pe", lambda e, q=q, base=base, oh_=oh_, fin=(not inter_b): e.matmul(
                            oh_, lhsT=qf[base:base + 64, q, tl(t)], rhs=Sbf[base:base + 64, 0, q, t, :], start=False, stop=fin),
                            reads=[("qf", t), ("Sbf", 0, q, t)], writes=ok)
                    if inter_b:
                        sc.op("pe", lambda e, q=q, base=base, oh_=oh_: e.matmul(
                            oh_, lhsT=qb[base:base + 64, q, tl(t)], rhs=Sbf[base:base + 64, 1, q, t, :], start=False, stop=True),
                            reads=[("gqT", t), ("Sbf", 1, q, t)], writes=ok)

            def g_norm(t):
                b = t % 2
                sm = gsm[b]
                obank = (0 + b) if os.environ.get('GBANK') else (4 + b)
                okall = psk(obank)
                for h in range(4):
                    sc.op("act", lambda e, h=h, sm=sm: e.activation(out=g_junk, in_=PS[:, obank, h * 128:(h + 1) * 128], func=AF.Square, accum_out=sm[:, h:h + 1]),
                          reads=okall, writes=["gjunk", ("gss", b), ("enb", 1, 0), ("enb", 1, 1)])
                sc.op("act", lambda e, sm=sm: e.activation(out=sm[:, 4:8], in_=sm[:, 0:4], func=AF.Sqrt, bias=1e-5, scale=1.0 / 128),
                      reads=[("gss", b)], writes=[("grs", b)])
                sc.op("dve", lambda e, sm=sm: e.reciprocal(out=sm[:, 4:8], in_=sm[:, 4:8]), reads=[("grs", b)], writes=[("grs", b)])
                for h in range(4):
                    sc.op("dve", lambda e, h=h, sm=sm, b=b: e.scalar_tensor_tensor(
                        out=g_y[b][:, h * 128:(h + 1) * 128], in0=PS[:, obank, h * 128:(h + 1) * 128], scalar=sm[:, 4 + h:5 + h],
                        in1=gr_s[:, t, h * 128:(h + 1) * 128], op0=ALU.mult, op1=ALU.mult),
                        reads=psk(obank) + [("grs", b), ("gr_s", t)], writes=[("gy", b), ("enb", 0, 0), ("enb", 0, 1)])

            def g_tr(t):
                b = t % 2
                tk = psk(6 + b)
                for h in range(4):
                    sc.op("pe", lambda e, h=h, b=b: e.transpose(out=PSb[:, 6 + b, h * 128:(h + 1) * 128], in_=g_y[b][:, h * 128:(h + 1) * 128], identity=identb),
                          reads=[("gy", b), "identb"], writes=tk)
                sc.op("act", lambda e, b=b: e.activation(out=XT[:, 4:8, tl(t)], in_=PSb[:, 6 + b, 0:512].rearrange("p (c n) -> p c n", c=4), func=AF.Copy),
                      reads=tk, writes=[("XT", t)])

            g_A(0)
            for t in range(NT):
                if t + 1 < NT:
                    g_A(t + 1)
                if os.environ.get("GSKIP") == "B":
                    continue
                g_B(t)
                if os.environ.get("GSKIP") == "norm":
                    continue
                g_norm(t)
                if os.environ.get("GSKIP") == "tr":
                    continue
                if t > 0:
                    g_tr(t - 1)
            if not os.environ.get("GSKIP"):
                g_tr(NT - 1)
            dump("mixT", XT, [128, 8, 2048], [("XT", t) for t in range(NT)])

            if stop == 'G' and l == nl - 1:
                raise _Stop()
            sc.barrier()
            load_ln_params(ln1g_d[l], ln1b_d[l])
            R_D.reset()
            W1B = [R_D.get(8192, BF16, "p (c n) -> p c n", c=8) for _ in range(2)]
            W2B = [R_D.get(8192, BF16, "p (c n) -> p c n", c=4) for _ in range(2)]
            hT = R_D.get(16384, BF16, "p (c n) -> p c n", c=4)
            relu_t = [R_D.get(2048, F32) for _ in range(2)]

            def load_ffn_block(fb, buf):
                s1 = w1_d[l, :, fb * 512:(fb + 1) * 512].rearrange("(c p) n -> p c n", p=128)
                s2 = w2_d[l, fb * 512:(fb + 1) * 512, :].rearrange("(c p) n -> p c n", p=128)
                for hf in range(2):
                    sc.dma("pool", lambda e, hf=hf: e.dma_start(out=W1B[buf][:, hf * 4:(hf + 1) * 4, :], in_=s1[:, hf * 4:(hf + 1) * 4, :]),
                           writes=[("W1B", buf, hf)])
                for hf in range(2):
                    sc.dma("pool", lambda e, hf=hf: e.dma_start(out=W2B[buf][:, hf * 2:(hf + 1) * 2, :], in_=s2[:, hf * 2:(hf + 1) * 2, :]),
                           writes=[("W2B", buf, hf)])

            load_ffn_block(0, 0)
            load_ffn_block(1, 1)
            for t in range(NT):
                sc.dma("sp", lambda e, t=t: e.dma_start(out=X[:, t, :], in_=xs_d[t * 128:(t + 1) * 128, :]), reads=[("xsd", t)], writes=[("X", t)])
            def o_mm(t):
                yb = (t % 3) * 2
                for hf in range(2):
                    for c in range(8):
                        sc.op("pe", lambda e, c=c, hf=hf, t=t, yb=yb: e.matmul(PS[:, yb + hf, :], lhsT=XT[:, c, tl(t)], rhs=WO[:, c, hf * 512:(hf + 1) * 512],
                                                                              start=(c == 0), stop=(c == 7)),
                              reads=[("XT", t), ("RW", 0, 0), ("RW", 0, 1), ("RW", 1, 0), ("RW", 1, 1)], writes=psk(yb + hf))

            def o_ln(t):
                yb = (t % 3) * 2
                sc.op("dve", lambda e, t=t, yb=yb: e.scalar_tensor_tensor(out=X[:, t, :], in0=X[:, t, :], scalar=ALPHA,
                                                                          in1=PS[:, yb:yb + 2, :].rearrange("p a n -> p (a n)"), op0=ALU.mult, op1=ALU.add),
                      reads=[("X", t)] + psk(yb) + psk(yb + 1), writes=[("X", t)])
                ln_a(t)

            for t0 in range(3):
                o_mm(t0)
                o_ln(t0)
            for t in range(NT):
                if t + 3 < NT:
                    o_mm(t + 3)
                ln_b1(t, None)
                if t + 3 < NT:
                    o_ln(t + 3)
                ln_b2(t, None)
            dump("x1T", XT, [128, 8, 2048], [("XT", t) for t in range(NT)])

            if stop == 'O' and l == nl - 1:
                raise _Stop()
            if not last:
                load_win_block(l + 1, 0, 0)
                load_win_block(l + 1, 1, 1)
            hrr = [0]
            for fb in range(8):
                buf = fb % 2
                for r in range(4):
                    for fc in range(4):
                        bank = 4 + hrr[0] % 3
                        rb = hrr[0] % 2
                        hrr[0] += 1
                        for c in range(8):
                            sc.op("pe", lambda e, c=c, fc=fc, r=r, bank=bank, buf=buf: e.matmul(
                                PS[:, bank, :], lhsT=W1B[buf][:, c, fc * 128:(fc + 1) * 128], rhs=XT[:, c, r * 512:(r + 1) * 512],
                                start=(c == 0), stop=(c == 7)), reads=[("W1B", buf, 0), ("W1B", buf, 1)] + xt_all[r * 4:(r + 1) * 4], writes=psk(bank))
                        fcol = fb * 4 + fc
                        sc.op("act", lambda e, bank=bank, rb=rb, fcol=fcol: e.activation(out=relu_t[rb], in_=PS[:, bank, :], func=AF.Relu,
                                                                                         bias=b1c[:, fcol:fcol + 1], scale=1.0),
                              reads=psk(bank) + ["b1c"], writes=[("relu", rb)])
                        sc.op("dve", lambda e, rb=rb, fc=fc, r=r: e.tensor_tensor(out=hT[:, fc, r * 512:(r + 1) * 512], in0=relu_t[rb], in1=relu_t[rb], op=ALU.mult),
                              reads=[("relu", rb)], writes=[("hT", fc, r)])
                for t in range(NT):
                    yb = (t % 2) * 2
                    for hf in range(2):
                        for fc in range(4):
                            sc.op("pe", lambda e, fc=fc, hf=hf, t=t, yb=yb, buf=buf: e.matmul(
                                PS[:, yb + hf, :], lhsT=hT[:, fc, tl(t)], rhs=W2B[buf][:, fc, hf * 512:(hf + 1) * 512],
                                start=(fc == 0), stop=(fc == 3)), reads=[("hT", fc, t // 4), ("W2B", buf, 0), ("W2B", buf, 1)], writes=psk(yb + hf))
                    if fb == 0:
                        sc.op("dve", lambda e, t=t, yb=yb: e.scalar_tensor_tensor(out=X[:, t, :], in0=X[:, t, :], scalar=ALPHA,
                                                                                  in1=PS[:, yb:yb + 2, :].rearrange("p a n -> p (a n)"), op0=ALU.mult, op1=ALU.add),
                              reads=[("X", t)] + psk(yb) + psk(yb + 1), writes=[("X", t)])
                        sc.op("pool", lambda e, t=t: e.tensor_tensor(out=X[:, t, :], in0=X[:, t, :], in1=b2t, op=ALU.add), reads=[("X", t), "b2t"], writes=[("X", t)])
                    else:
                        sc.op("dve", lambda e, t=t, yb=yb: e.tensor_tensor(out=X[:, t, :], in0=X[:, t, :], in1=PS[:, yb:yb + 2, :].rearrange("p a n -> p (a n)"), op=ALU.add),
                              reads=[("X", t)] + psk(yb) + psk(yb + 1), writes=[("X", t)])
                if fb + 2 < 8:
                    load_ffn_block(fb + 2, buf)
            if stop == 'F' and l == nl - 1:
                raise _Stop()
            yield "F"


        gens = [do_layer(_l) for _l in range(nl)]
        p01_0 = next(gens[0])
        ln_all(xs_d, hook=p01_0)
        dump("h0", X, [128, NT, 1024], [("X", t) for t in range(NT)])
        for _l in range(nl):
            next(gens[_l])
            sc.barrier()
            load_ln_params(ln2g_d[_l], ln2b_d[_l])
            if _l + 1 < nl:
                p01_n = next(gens[_l + 1])
                ln_all(xs_d, hook=p01_n)
            else:
                ln_all(out_d)
    except _Stop:
        pass
    out_dmas = [o for o in sc.ops if o.is_dma and o.dkey in [("xs", i) for i in range(4)]]
    fin = {}
    for o in out_dmas:
        fin[o.dkey] = o
    finals = list(fin.values()) + list(dbg_out.values())
    sc.emit(final_wait_ops=finals)
    es.close()
    return nc, sc


_CONST = None


def kernel(**inputs):
    global _CONST
    if _CONST is None:
        _CONST = _constants()
    nc, _ = build(2)
    x = np.ascontiguousarray(inputs["x"], dtype=np.float32)
    shared = {k: np.ascontiguousarray(v, dtype=np.float32) for k, v in inputs.items() if k != "x"}
    shared.update(_CONST)
    in_maps = []
    for b in range(8):
        m = dict(shared)
        m["x"] = x[b]
        in_maps.append(m)
    res = run_bass_kernel_spmd(nc, in_maps, core_ids=list(range(8)))
    return np.stack([r["out"] for r in res.results], axis=0).astype(np.float32)
```

```python
import math
import os
from contextlib import ExitStack

import numpy as np
import concourse.bass as bass
import concourse.mybir as mybir
from concourse.bass_utils import run_bass_kernel_spmd

F32 = mybir.dt.float32
BF16 = mybir.dt.bfloat16
AF = mybir.ActivationFunctionType
ALU = mybir.AluOpType

S = 2048
D = 1024
DIN = 3104
DFF = 4096
NT = 16
ALPHA = (2.0 * 2) ** 0.25
ENGS = ("pe", "act", "dve", "pool", "sp")
EPOCH = 30000


class _Res:
    __slots__ = ("last_w", "readers")

    def __init__(self):
        self.last_w = None
        self.readers = []


class _Op:
    __slots__ = ("eng", "fn", "deps", "signal", "tok", "is_dma", "dkey")

    def __init__(self, eng, fn, is_dma, dkey):
        self.eng = eng
        self.fn = fn
        self.deps = []
        self.signal = False
        self.tok = None
        self.is_dma = is_dma
        self.dkey = dkey


class Sched:
    def __init__(self, nc):
        self.nc = nc
        self.ops = []
        self.res = {}
        self.pending = {e: [] for e in ENGS}

    def _r(self, key):
        x = self.res.get(key)
        if x is None:
            x = self.res[key] = _Res()
        return x

    def _add(self, op, reads, writes):
        deps = set()
        for k in reads:
            rs = self._r(k)
            if rs.last_w is not None:
                deps.add(rs.last_w)
        for k in writes:
            rs = self._r(k)
            if rs.last_w is not None:
                deps.add(rs.last_w)
            deps.update(rs.readers)
        for k in reads:
            self._r(k).readers.append(op)
        for k in writes:
            rs = self._r(k)
            rs.last_w = op
            rs.readers = []
        if self.pending[op.eng]:
            deps.update(self.pending[op.eng])
            self.pending[op.eng] = []
        deps.discard(op)
        op.deps = list(deps)
        self.ops.append(op)
        return op

    def op(self, eng, fn, reads=(), writes=()):
        return self._add(_Op(eng, fn, False, None), reads, writes)

    def dma(self, eng, fn, dkey=None, reads=(), writes=()):
        if dkey is None:
            dkey = ("w", writes[0])
        return self._add(_Op(eng, fn, True, dkey), reads, writes)

    def barrier(self):
        last = {}
        for o in self.ops:
            last[(o.eng, o.dkey) if o.is_dma else o.eng] = o
        b = list(last.values())
        self.pending = {e: list(b) for e in ENGS}

    def emit(self, final_wait_ops=()):
        nc = self.nc
        ops = self.ops
        for o in ops:
            for d in o.deps:
                if d.is_dma:
                    d.signal = True
                elif d.eng == "pe" and o.eng == "pe" and not o.is_dma:
                    continue
                else:
                    d.signal = True
        with ExitStack() as es:
            eng_sems = {e: [] for e in ENGS}
            cnt = {e: 0 for e in ENGS}
            dma_sems = {}
            dma_cnt = {}
            for o in ops:
                if o.is_dma:
                    if o.dkey not in dma_sems:
                        dma_sems[o.dkey] = es.enter_context(nc.semaphore("d%d" % len(dma_sems)))
                        dma_cnt[o.dkey] = 0
                    dma_cnt[o.dkey] += 16
                    o.tok = (dma_sems[o.dkey], dma_cnt[o.dkey])
                elif o.signal:
                    ep = cnt[o.eng] // EPOCH
                    if ep >= len(eng_sems[o.eng]):
                        eng_sems[o.eng].append(es.enter_context(nc.semaphore("e_%s_%d" % (o.eng, ep))))
                    cnt[o.eng] += 1
                    o.tok = (eng_sems[o.eng][ep], cnt[o.eng] - ep * EPOCH)
            per_eng = {e: [o for o in ops if o.eng == e] for e in ENGS}
            self.stats = {e: len(per_eng[e]) for e in ENGS}
            self.stats["sems"] = sum(len(v) for v in eng_sems.values()) + len(dma_sems)

            def run(e, eng):
                waited = {}
                for o in per_eng[e]:
                    need = {}
                    for d in o.deps:
                        if d.tok is None:
                            continue
                        if (not d.is_dma) and d.eng == "pe" and e == "pe" and not o.is_dma:
                            continue
                        s, v = d.tok
                        k = id(s)
                        if waited.get(k, 0) >= v:
                            continue
                        if k not in need or need[k][1] < v:
                            need[k] = (s, v)
                    for k, (s, v) in need.items():
                        eng.wait_ge(s, v)
                        waited[k] = v
                    ins = o.fn(eng)
                    if o.tok is not None:
                        ins.then_inc(o.tok[0], 16 if o.is_dma else 1)
                if e == "sp":
                    for o in final_wait_ops:
                        s, v = o.tok
                        eng.wait_ge(s, v)

            with nc.Block() as block:
                @block.sync
                def _(eng):
                    run("sp", eng)

                @block.tensor
                def _(eng):
                    run("pe", eng)

                @block.scalar
                def _(eng):
                    run("act", eng)

                @block.vector
                def _(eng):
                    run("dve", eng)

                @block.gpsimd
                def _(eng):
                    run("pool", eng)


def _t5_bucket(rel):
    nb = 16
    me = 8
    ret = np.where(rel > 0, nb, 0)
    n = np.abs(rel)
    large = me + (np.log(np.maximum(n, 1).astype(np.float32) / np.float32(me))
                  / np.float32(math.log(128 / me)) * np.float32(nb - me)).astype(np.int32)
    large = np.minimum(large, nb - 1)
    return ret + np.where(n < me, n, large)


MLEN = 1280


def _constants():
    c = {}
    c["c_ident"] = np.eye(128, dtype=np.float32)
    c["c_J"] = np.eye(128, dtype=np.float32)[::-1].copy()
    s = np.arange(128)[:, None]
    t = np.arange(128)[None, :]
    uf = np.zeros((128, 129), np.float32)
    uf[:, :128] = (s <= t)
    uf[:, 128] = 1.0
    ub = np.zeros((128, 129), np.float32)
    ub[:, :128] = (s >= t)
    ub[:, 128] = 1.0
    c["c_uf"] = uf
    c["c_ub"] = ub
    c["c_sf"] = (s > t).astype(np.float32)
    c["c_sb"] = (s < t).astype(np.float32)
    n = np.arange(MLEN)
    bk = _t5_bucket(639 - n)
    oh = np.zeros((32, MLEN), np.float32)
    oh[bk, n] = 1.0
    c["c_onehot"] = oh
    return c


class _Stop(Exception):
    pass


def build(nl=2, dbg=(), stop=None):
    nc = bass.Bass("TRN2", target_bir_lowering=False)

    def din(name, shape):
        return nc.dram_tensor(name, list(shape), F32, kind="ExternalInput").ap()

    x_d = din("x", [S, D])
    lnemb_g = din("ln_emb_g", [D])
    lnemb_b = din("ln_emb_b", [D])
    table_d = din("rel_bias_table", [32, 4])
    w_in_d = din("w_in", [2, D, DIN])
    lq1_d = din("lambda_q1", [2, 64])
    lk1_d = din("lambda_k1", [2, 64])
    lq2_d = din("lambda_q2", [2, 64])
    lk2_d = din("lambda_k2", [2, 64])
    dnw_d = din("diff_norm_w", [2, 128])
    gup_d = din("gla_gate_up", [2, 2, 16, 256])
    gbias_d = din("gla_gate_bias", [2, 2, 256])
    gnw_d = din("gla_norm_w", [2, 128])
    w_o_d = din("w_o", [2, D, D])
    ln1g_d = din("ln1_g", [2, D])
    ln1b_d = din("ln1_b", [2, D])
    w1_d = din("w_ffn1", [2, D, DFF])
    b1_d = din("b_ffn1", [2, DFF])
    w2_d = din("w_ffn2", [2, DFF, D])
    b2_d = din("b_ffn2", [2, D])
    ln2g_d = din("ln2_g", [2, D])
    ln2b_d = din("ln2_b", [2, D])
    c_ident = din("c_ident", [128, 128])
    c_J = din("c_J", [128, 128])
    c_uf = din("c_uf", [128, 129])
    c_ub = din("c_ub", [128, 129])
    c_sf = din("c_sf", [128, 128])
    c_sb = din("c_sb", [128, 128])
    c_onehot = din("c_onehot", [32, MLEN])
    out_d = nc.dram_tensor("out", [S, D], F32, kind="ExternalOutput").ap()
    xs_d = nc.dram_tensor("xs_scratch", [S, D], F32).ap()
    md_t = nc.dram_tensor("md_scratch", [4, MLEN], F32)
    eb_d = nc.dram_tensor("expb_scratch", [128, 4 * 1152], BF16).ap()
    md_d = md_t.ap()
    dbg_out = {}

    sc = Sched(nc)
    es = ExitStack()
    ARENA_BYTES = 207 * 1024
    arena = es.enter_context(nc.sbuf_tensor("arena", [128, ARENA_BYTES // 2], BF16))
    PSb = es.enter_context(nc.psum_tensor("ps", [128, 8, 1024], BF16))[:]
    PS = PSb.bitcast(F32)

    def view(off, nbytes, dt, pattern=None, **kw):
        assert off % 32 == 0, off
        a = arena[:, off // 2:(off + nbytes) // 2]
        if dt is F32:
            a = a.bitcast(F32)
        if pattern:
            a = a.rearrange(pattern, **kw)
        return a

    class Alloc:
        def __init__(self, base, size):
            self.base = base
            self.size = size
            self.pos = 0

        def reset(self):
            self.pos = 0

        def get(self, nbytes, dt, pattern=None, **kw):
            n = (nbytes + 31) // 32 * 32
            assert self.pos + n <= self.size, (self.pos, n, self.size)
            v = view(self.base + self.pos, nbytes, dt, pattern, **kw)
            self.pos += n
            return v

    R_XT = Alloc(0, 32768)
    R_X = Alloc(32768, 65536)
    R_D = Alloc(98304, 70656)
    R_W = Alloc(168960, 16384)
    R_C = Alloc(185344, ARENA_BYTES - 185344)

    XT = R_XT.get(32768, BF16, "p (c n) -> p c n", c=8)
    X = R_X.get(65536, F32, "p (t n) -> p t n", t=NT)
    WB = [R_W.get(8192, BF16, "p (c n) -> p c n", c=8) for _ in range(2)]
    R_W.reset()
    WO = R_W.get(16384, BF16, "p (c n) -> p c n", c=8)

    identb = R_C.get(256, BF16)
    Jb = R_C.get(256, BF16)
    Uf = R_C.get(516, F32)
    Ub = R_C.get(516, F32)
    Usf = R_C.get(512, F32)
    Usb = R_C.get(512, F32)
    gt = R_C.get(4096, F32)
    bt = R_C.get(4096, F32)
    b2t = R_C.get(4096, F32)
    wd_t = R_C.get(512, F32)
    wg_t = R_C.get(512, F32)
    b1c = R_C.get(128, F32)
    cb = R_C.get(32, F32, "p (s h) -> p s h", s=2)
    lamv = R_C.get(4 * 64 * 4, F32, "p (a n) -> p a n", a=4)
    lamp = R_C.get(2 * 64 * 4, F32, "p (a n) -> p a n", a=2)
    lams = R_C.get(32, F32)
    Wg = R_C.get(1024, BF16)
    st_ = [R_C.get(48, F32) for _ in range(2)]
    mv_ = [R_C.get(8, F32) for _ in range(2)]
    rs_ = [R_C.get(4, F32) for _ in range(2)]
    xb_ = [R_C.get(2048, BF16) for _ in range(2)]
    dsm = [R_C.get(64, F32) for _ in range(2)]
    gsm = [R_C.get(64, F32) for _ in range(2)]
    decs = R_C.get(2 * 2 * 16 * 4, F32, "p (d q t) -> p d q t", d=2, q=2)

    def psk(b):
        return [("ps", b, q) for q in range(4)]

    def bc_mid(ap2, n):
        a = ap2.ap
        return bass.AP(ap2.tensor, ap2.offset, [list(a[0]), [0, n], list(a[1])])

    def bc_last(ap2, n):
        a = ap2.ap
        return bass.AP(ap2.tensor, ap2.offset, [list(a[0]), list(a[1]), [0, n]])

    cur_layer = [-1]

    def dump(name, ap, shape, reads):
        nm = "%s@%d" % (name, cur_layer[0])
        if nm in dbg:
            name = nm
        elif name not in dbg or (cur_layer[0] >= 0 and cur_layer[0] != nl - 1):
            return
        t = nc.dram_tensor("dbg_" + name.replace("@", "_"), list(shape), ap.dtype, kind="ExternalOutput").ap()
        dbg_out[name] = sc.dma("sp", lambda e: e.dma_start(out=t, in_=ap), "dbg", reads=reads)

    try:
        sc.dma("pool", lambda e: e.dma_start(out=identb, in_=c_ident), writes=["identb"])
        sc.dma("pool", lambda e: e.dma_start(out=Jb, in_=c_J), writes=["Jb"])
        sc.dma("sp", lambda e: e.dma_start(out=Uf, in_=c_uf), writes=["Uf"])
        sc.dma("sp", lambda e: e.dma_start(out=Ub, in_=c_ub), writes=["Ub"])
        sc.dma("sp", lambda e: e.dma_start(out=Usf, in_=c_sf), writes=["Usf"])
        sc.dma("sp", lambda e: e.dma_start(out=Usb, in_=c_sb), writes=["Usb"])
        for si, row in enumerate((15, 31)):
            sc.dma("sp", lambda e, si=si, row=row: e.dma_start(out=cb[:, si, :], in_=table_d[row, :].partition_broadcast(128)),
                   writes=[("cb", si)])

        R_D.reset()
        tb = R_D.get(16, F32)
        oh = R_D.get(MLEN * 4, F32)
        msb = R_D.get(MLEN * 4, F32)
        sc.dma("sp", lambda e: e.dma_start(out=tb[0:32, :], in_=table_d), writes=["tb"])
        sc.dma("sp", lambda e: e.dma_start(out=oh[0:32, :], in_=c_onehot), writes=["oh"])
        sc.dma("sp", lambda e: e.dma_start(out=gt, in_=lnemb_g.partition_broadcast(128)), writes=["gt"])
        sc.dma("sp", lambda e: e.dma_start(out=bt, in_=lnemb_b.partition_broadcast(128)), writes=["bt"])
        for t in range(NT):
            sc.dma("sp", lambda e, t=t: e.dma_start(out=X[:, t, :], in_=x_d[t * 128:(t + 1) * 128, :]), writes=[("X", t)])
        for ci, (c0, cn) in enumerate(((0, 512), (512, 512), (1024, 256))):
            sc.op("pe", lambda e, ci=ci, c0=c0, cn=cn: e.matmul(PS[0:4, ci, 0:cn], lhsT=tb[0:32, :], rhs=oh[0:32, c0:c0 + cn], start=True, stop=True),
                  reads=["tb", "oh"], writes=psk(ci))
            sc.op("dve", lambda e, ci=ci, c0=c0, cn=cn: e.tensor_copy(out=msb[0:4, c0:c0 + cn], in_=PS[0:4, ci, 0:cn]),
                  reads=psk(ci), writes=["msb"])
        sc.dma("sp", lambda e: e.dma_start(out=md_d, in_=msb[0:4, :]), reads=["msb"], writes=["md"])
        R_D.reset()
        R_D.get(16384, BF16); R_D.get(16384, BF16); R_D.get(16 * 4 * 129 * 2, BF16)
        expB0 = R_D.get(4 * 1152 * 2, BF16, "p (h n) -> p h n", h=4)
        R_D.get(4096, BF16); R_D.get(1024, BF16)
        tmp_revs = [view(R_D.base + 16384 + i * 2304, 2304, BF16) for i in range(4)]
        for h in range(4):
            src = bass.AP(md_t, h * MLEN, [[1, 128], [1, 1152]])
            sc.dma("pool", lambda e, src=src, h=h: e.dma_start(out=tmp_revs[h], in_=src), reads=["md"], writes=[("tmp_rev", h)])
        for h in range(4):
            for ci, (c0, cn) in enumerate(((0, 512), (512, 512), (1024, 128))):
                sc.op("pe", lambda e, ci=ci, c0=c0, cn=cn, h=h: e.matmul(PS[:, ci, 0:cn], lhsT=Jb, rhs=tmp_revs[h][:, c0:c0 + cn], start=True, stop=True),
                      reads=["Jb", ("tmp_rev", h)], writes=psk(ci))
                sc.op("act", lambda e, h=h, ci=ci, c0=c0, cn=cn: e.activation(out=expB0[:, h, c0:c0 + cn], in_=PS[:, ci, 0:cn], func=AF.Exp),
                      reads=psk(ci), writes=[("expB", h)])
        sc.dma("sp", lambda e: e.dma_start(out=eb_d, in_=expB0.rearrange("p h n -> p (h n)")), reads=[("expB", h) for h in range(4)], writes=["eb_d"])

        def ln_a(t):
            Xt = X[:, t, :]
            kx = ("X", t)
            b = t % 2
            st, mv, rs = st_[b], mv_[b], rs_[b]
            sc.op("dve", lambda e: e.bn_stats(out=st[:, 0:6], in_=Xt[:, 0:512]), reads=[kx], writes=[("st", b, 0)])
            sc.op("dve", lambda e: e.bn_stats(out=st[:, 6:12], in_=Xt[:, 512:1024]), reads=[kx], writes=[("st", b, 1)])
            sc.op("dve", lambda e: e.bn_aggr(out=mv, in_=st), reads=[("st", b, 0), ("st", b, 1)], writes=[("mv", b)])
            sc.op("act", lambda e: e.activation(out=rs, in_=mv[:, 1:2], func=AF.Sqrt, bias=1e-5, scale=1.0), reads=[("mv", b)], writes=[("rs", b)])
            sc.op("dve", lambda e: e.reciprocal(out=rs, in_=rs), reads=[("rs", b)], writes=[("rs", b)])
            sc.op("dve", lambda e: e.tensor_scalar(out=Xt, in0=Xt, scalar1=mv[:, 0:1], scalar2=rs, op0=ALU.subtract, op1=ALU.mult),
                  reads=[kx, ("mv", b), ("rs", b)], writes=[kx])
            sc.op("dve", lambda e: e.tensor_tensor(out=Xt, in0=Xt, in1=gt, op=ALU.mult), reads=[kx, "gt"], writes=[kx])
            sc.op("pool", lambda e: e.tensor_tensor(out=Xt, in0=Xt, in1=bt, op=ALU.add), reads=[kx, "bt"], writes=[kx])

        def ln_b1(t, spill_to):
            Xt = X[:, t, :]
            kx = ("X", t)
            b = t % 2
            xb = xb_[b]
            if spill_to is not None:
                sc.dma("sp", lambda e: e.dma_start(out=spill_to[t * 128:(t + 1) * 128, :], in_=Xt), ("xs", t % 4), reads=[kx], writes=[("xsd", t)])
            if spill_to is out_d:
                return
            sc.op("act", lambda e: e.activation(out=xb, in_=Xt, func=AF.Copy), reads=[kx], writes=[("xb", b)])

        def ln_b2(t, spill_to):
            if spill_to is out_d:
                return
            b = t % 2
            xb = xb_[b]
            bank = 6 + b
            for c in range(8):
                sc.op("pe", lambda e, c=c: e.transpose(out=PSb[:, bank, c * 128:(c + 1) * 128], in_=xb[:, c * 128:(c + 1) * 128], identity=identb),
                      reads=[("xb", b), "identb"], writes=psk(bank))
            sc.op("act", lambda e: e.activation(out=XT[:, :, t * 128:(t + 1) * 128], in_=PSb[:, bank, :].rearrange("p (c n) -> p c n", c=8), func=AF.Copy),
                  reads=psk(bank), writes=[("XT", t)])

        def ln_all(spill_to):
            ln_a(0)
            for t in range(NT):
                if t + 1 < NT:
                    ln_a(t + 1)
                ln_b1(t, spill_to)
                ln_b2(t, spill_to)

        def load_ln_params(g_ap, b_ap):
            sc.dma("sp", lambda e: e.dma_start(out=gt, in_=g_ap.partition_broadcast(128)), writes=["gt"])
            sc.dma("sp", lambda e: e.dma_start(out=bt, in_=b_ap.partition_broadcast(128)), writes=["bt"])

        def load_win_block(l, blk, buf):
            c0 = blk * 512
            ncol = min(512, DIN - c0)
            src = w_in_d[l, :, c0:c0 + ncol].rearrange("(c p) n -> p c n", p=128)
            for hf in range(2):
                sc.dma("pool", lambda e, hf=hf: e.dma_start(out=WB[buf][:, hf * 4:(hf + 1) * 4, 0:ncol], in_=src[:, hf * 4:(hf + 1) * 4, :]),
                       writes=[("RW", buf, hf)])

        if stop == 'init':
            raise _Stop()
        load_win_block(0, 0, 0)
        load_win_block(0, 1, 1)
        ln_all(xs_d)
        dump("h0", X, [128, NT, 1024], [("X", t) for t in range(NT)])

        if stop == 'emb':
            raise _Stop()
        evac_rr = [0]

        def evac(out, in_, reads, writes, scale=None):
            evac_rr[0] ^= 1
            if evac_rr[0]:
                if scale is None:
                    sc.op("act", lambda e: e.activation(out=out, in_=in_, func=AF.Copy), reads=reads, writes=writes)
                else:
                    sc.op("act", lambda e: e.mul(out=out, in_=in_, mul=scale), reads=reads, writes=writes)
            else:
                if scale is None:
                    sc.op("dve", lambda e: e.tensor_copy(out=out, in_=in_), reads=reads, writes=writes)
                else:
                    sc.op("dve", lambda e: e.tensor_scalar(out=out, in0=in_, scalar1=scale, scalar2=None, op0=ALU.mult), reads=reads, writes=writes)

        def do_layer(l):
            lam_init = 0.8 - 0.6 * math.exp(-0.3 * l)
            cur_layer[0] = l
            if l == 0:
                sc.barrier()
            last = (l == nl - 1)
            R_D.reset()
            QT = R_D.get(16384, BF16, "p (h n) -> p h n", h=4)
            KT = R_D.get(16384, BF16, "p (h n) -> p h n", h=4)
            V = R_D.get(16 * 4 * 129 * 2, BF16, "p (t h e) -> p t h e", t=16, h=4)
            expB = R_D.get(4 * 1152 * 2, BF16, "p (h n) -> p h n", h=4)
            Eb = R_D.get(4096, BF16, "p (b m n) -> p b m n", b=2, m=2)
            d_y = R_D.get(4 * 128 * 2, BF16, "p (u n) -> p u n", u=4)
            _pu = R_D.pos
            silu_t = [R_D.get(2048, F32) for _ in range(2)]
            R_D.pos = _pu
            accS = R_D.get(8 * 129 * 4, F32, "p (a n) -> p a n", a=8)
            R_D.pos = _pu + 4608
            tmp_rev = R_D.get(1152 * 2, BF16)
            R_X.reset()
            gqT = R_X.get(8192, BF16, "p (c n) -> p c n", c=2)
            gkT = R_X.get(8192, BF16, "p (c n) -> p c n", c=2)
            gk_tok = R_X.get(8192, BF16, "p (t n) -> p t n", t=16)
            gv = R_X.get(16384, BF16, "p (t n) -> p t n", t=16)
            gr_s = R_X.get(16384, BF16, "p (t n) -> p t n", t=16)
            G33 = R_X.get(4096, BF16)

            for i, ap in enumerate((lq1_d, lk1_d, lq2_d, lk2_d)):
                sc.dma("sp", lambda e, i=i, ap=ap: e.dma_start(out=lamv[:, i, :], in_=ap[l, :].partition_broadcast(128)), writes=[("lamv", i)])
            sc.op("dve", lambda e: e.tensor_tensor(out=lamp[:, 0, :], in0=lamv[:, 0, :], in1=lamv[:, 1, :], op=ALU.mult), reads=[("lamv", 0), ("lamv", 1)], writes=["lamp"])
            sc.op("dve", lambda e: e.tensor_tensor(out=lamp[:, 1, :], in0=lamv[:, 2, :], in1=lamv[:, 3, :], op=ALU.mult), reads=[("lamv", 2), ("lamv", 3)], writes=["lamp"])
            sc.op("dve", lambda e: e.reduce_sum(out=lams[:, 0:2], in_=lamp, axis=mybir.AxisListType.X), reads=["lamp"], writes=["lams"])
            sc.op("act", lambda e: e.activation(out=lams[:, 0:2], in_=lams[:, 0:2], func=AF.Exp), reads=["lams"], writes=["lams"])
            sc.op("dve", lambda e: e.tensor_tensor(out=lams[:, 2:3], in0=lams[:, 0:1], in1=lams[:, 1:2], op=ALU.subtract), reads=["lams"], writes=["lams"])
            sc.op("dve", lambda e: e.tensor_scalar(out=lams[:, 3:4], in0=lams[:, 2:3], scalar1=lam_init, scalar2=-1.0, op0=ALU.add, op1=ALU.mult),
                  reads=["lams"], writes=["neglam"])
            neg_lam = lams[:, 3:4]
            sc.dma("sp", lambda e: e.dma_start(out=wd_t, in_=dnw_d[l, :].partition_broadcast(128)), writes=["wd"])
            sc.op("dve", lambda e: e.tensor_scalar(out=wd_t, in0=wd_t, scalar1=1.0 - lam_init, scalar2=None, op0=ALU.mult), reads=["wd"], writes=["wd"])
            sc.dma("sp", lambda e: e.dma_start(out=wg_t, in_=gnw_d[l, :].partition_broadcast(128)), writes=["wg"])
            sc.op("dve", lambda e: e.memset(Wg[0:33, :], 0.0), writes=["Wg"])
            sc.dma("pool", lambda e: e.dma_start(out=Wg[0:16, 0:256], in_=gup_d[l, 0]), writes=["Wg"])
            sc.dma("pool", lambda e: e.dma_start(out=Wg[16:32, 256:512], in_=gup_d[l, 1]), writes=["Wg"])
            sc.dma("pool", lambda e: e.dma_start(out=Wg[32:33, :], in_=gbias_d[l].rearrange("a n -> (a n)").partition_broadcast(1)), writes=["Wg"])
            sc.dma("sp", lambda e: e.dma_start(out=b1c, in_=b1_d[l].rearrange("(c p) -> p c", p=128), allow_slow_non_contiguous=True), writes=["b1c"])
            sc.dma("sp", lambda e: e.dma_start(out=b2t, in_=b2_d[l].partition_broadcast(128)), writes=["b2t"])
            sc.dma("sp", lambda e: e.dma_start(out=expB.rearrange("p h n -> p (h n)"), in_=eb_d), reads=["eb_d"], writes=[("expB", h) for h in range(4)])
            sc.op("dve", lambda e: e.memset(V[:, :, :, 128:129], 1.0), writes=[("V", t) for t in range(NT)])
            sc.op("dve", lambda e: e.memset(G33[32:33, :], 1.0), writes=["G33"])

            dump("XTin", XT, [128, 8, 2048], [("XT", t) for t in range(NT)])
            dump("Xin", X, [128, NT, 1024], [("X", t) for t in range(NT)])
            if stop == 'L' and l == nl - 1:
                raise _Stop()
            ps_rr = [0]

            def nextbank():
                b = ps_rr[0] % 6
                ps_rr[0] += 1
                return b

            xt_all = [("XT", t) for t in range(NT)]
            for blk in range(7):
                buf = blk % 2
                wb = WB[buf]
                kw = [("RW", buf, 0), ("RW", buf, 1)]
                if blk in (0, 1, 3, 6):
                    nch = 1 if blk == 6 else 4
                    for cc in range(nch):
                        for r in range(4):
                            bank = nextbank()
                            M = 32 if blk == 6 else 128
                            for c in range(8):
                                sc.op("pe", lambda e, c=c, cc=cc, r=r, bank=bank, M=M, wb=wb: e.matmul(
                                    PS[0:M, bank, :], lhsT=wb[:, c, cc * 128:cc * 128 + M], rhs=XT[:, c, r * 512:(r + 1) * 512],
                                    start=(c == 0), stop=(c == 7)), reads=kw + xt_all[r * 4:(r + 1) * 4], writes=psk(bank))
                            sl = slice(r * 512, (r + 1) * 512)
                            if blk == 0:
                                evac(QT[:, cc, sl], PS[:, bank, :], psk(bank), [("QT", cc, r)], scale=0.125)
                            elif blk == 1:
                                evac(KT[:, cc, sl], PS[:, bank, :], psk(bank), [("KT", cc, r)])
                            elif blk == 3:
                                if cc < 2:
                                    evac(gqT[:, cc, sl], PS[:, bank, :], psk(bank), [("gqT", 4 * r + i) for i in range(4)], scale=0.125)
                                else:
                                    evac(gkT[:, cc - 2, sl], PS[:, bank, :], psk(bank), [("gkT", 4 * r + i) for i in range(4)])
                            else:
                                evac(G33[0:32, sl], PS[0:32, bank, :], psk(bank), ["G33"])
                if blk in (2, 3, 4, 5):
                    for t in range(NT):
                        bank = nextbank()
                        c0, ncol = (256, 256) if blk == 3 else (0, 512)
                        for c in range(8):
                            sc.op("pe", lambda e, c=c, t=t, bank=bank, c0=c0, ncol=ncol, wb=wb: e.matmul(
                                PS[:, bank, 0:ncol], lhsT=XT[:, c, t * 128:(t + 1) * 128], rhs=wb[:, c, c0:c0 + ncol],
                                start=(c == 0), stop=(c == 7)), reads=kw + [("XT", t)], writes=psk(bank))
                        if blk == 2:
                            evac(V[:, t, :, 0:128], PS[:, bank, :].rearrange("p (h e) -> p h e", h=4), psk(bank), [("V", t)])
                        elif blk == 3:
                            evac(gk_tok[:, t, :], PS[:, bank, 0:256], psk(bank), [("gk_tok", t)])
                        elif blk == 4:
                            evac(gv[:, t, :], PS[:, bank, :], psk(bank), [("gv", t)])
                        else:
                            sb = t % 2
                            sc.op("act", lambda e, bank=bank, sb=sb: e.activation(out=silu_t[sb], in_=PS[:, bank, :], func=AF.Silu),
                                  reads=psk(bank), writes=[("silu", sb)])
                            sc.op("dve", lambda e, t=t, sb=sb: e.tensor_tensor(
                                out=gr_s[:, t, :].rearrange("p (h e) -> p h e", h=4), in0=silu_t[sb].rearrange("p (h e) -> p h e", h=4),
                                in1=bc_mid(wg_t, 4), op=ALU.mult), reads=[("silu", sb), "wg"], writes=[("gr_s", t)])
                if blk + 2 < 7:
                    load_win_block(l, blk + 2, buf)
            for hf in range(2):
                sc.dma("pool", lambda e, hf=hf: e.dma_start(out=WO[:, hf * 4:(hf + 1) * 4, :],
                                                             in_=w_o_d[l].rearrange("(c p) n -> p c n", p=128)[:, hf * 4:(hf + 1) * 4, :]),
                       writes=[("RW", hf, 0), ("RW", hf, 1)])
            dump("QT", QT, [128, 4, 2048], [("QT", a, b) for a in range(4) for b in range(4)])
            dump("KT", KT, [128, 4, 2048], [("KT", a, b) for a in range(4) for b in range(4)])
            dump("V", V, [128, 16, 4, 129], [("V", t) for t in range(NT)])
            dump("expB", expB, [128, 4, 1152], [("expB", h) for h in range(4)])
            dump("gqT", gqT, [128, 2, 2048], [("gqT", t) for t in range(NT)])
            dump("gr_s", gr_s, [128, 16, 512], [("gr_s", t) for t in range(NT)])
            dump("G33", G33[0:33, :], [33, 2048], ["G33"])

            if stop == 'P' and l == nl - 1:
                raise _Stop()
            steps = [(h, r, j) for h in range(4) for r in range(4) for j in range(16)]

            def acc_ap(m, u):
                idx = m * 4 + u
                return PS[:, 4 + idx // 3, (idx % 3) * 160:(idx % 3) * 160 + 129]

            def acc_keys(m, u):
                return psk(4 + (m * 4 + u) // 3)

            Eb3 = view(R_X.base + 61440, 2048, BF16, "p (m n) -> p m n", m=2)
            EbL = [Eb[:, 0, :, :], Eb[:, 1, :, :], Eb3]

            def d_scores(i):
                h, r, j = steps[i]
                d = j - 4 * r
                mixed = (-1 <= d <= 4)
                sb = i % 2
                eb = i % 3
                E = EbL[eb]
                for m in range(2):
                    bank = sb * 2 + m
                    sc.op("pe", lambda e, h=h, r=r, j=j, m=m, bank=bank: e.matmul(
                        PS[:, bank, :], lhsT=KT[64 * m:64 * m + 64, h, j * 128:(j + 1) * 128],
                        rhs=QT[64 * m:64 * m + 64, h, r * 512:(r + 1) * 512], start=True, stop=True),
                        reads=[("KT", h, j // 4), ("QT", h, r)], writes=psk(bank))
                pk2 = psk(sb * 2) + psk(sb * 2 + 1)
                ek = [("E", eb, 0), ("E", eb, 1)]
                if mixed:
                    c0 = (4 - d) * 128
                    sc.op("act", lambda e, sb=sb, E=E: e.activation(out=E, in_=PS[:, sb * 2:sb * 2 + 2, :], func=AF.Exp),
                          reads=pk2, writes=ek)
                    for m in range(2):
                        sc.op("dve", lambda e, E=E, m=m, h=h, c0=c0: e.tensor_tensor(out=E[:, m, :], in0=E[:, m, :], in1=expB[:, h, c0:c0 + 512], op=ALU.mult),
                              reads=[("E", eb, m), ("expB", h)], writes=[("E", eb, m)])
                else:
                    side = 0 if d < 0 else 1
                    sc.op("act", lambda e, sb=sb, E=E, side=side, h=h: e.activation(
                        out=E, in_=PS[:, sb * 2:sb * 2 + 2, :], func=AF.Exp, bias=cb[:, side, h:h + 1]),
                        reads=pk2 + [("cb", 0), ("cb", 1)], writes=ek)

            def d_av(i):
                h, r, j = steps[i]
                eb = i % 3
                E = EbL[eb]
                for m in range(2):
                    for u in range(4):
                        sc.op("pe", lambda e, h=h, j=j, m=m, u=u, E=E: e.matmul(
                            acc_ap(m, u), lhsT=E[:, m, u * 128:(u + 1) * 128], rhs=V[:, j, h, 0:129],
                            start=(j == 0 and (m * 4 + u) % 3 == 0), stop=(j == 15), skip_group_check=True),
                            reads=[("E", eb, m), ("V", j)], writes=acc_keys(m, u))

            sm = dsm[0]
            ka = ["accS", ("silu", 0), ("silu", 1)]

            def d_final(h, r):
                sc.op("dve", lambda e: e.tensor_copy(out=accS[:, 0:3, :], in_=PS[:, 4, 0:480].rearrange("p (a n) -> p a n", a=3)[:, :, 0:129]),
                      reads=psk(4), writes=ka)
                sc.op("dve", lambda e: e.tensor_copy(out=accS[:, 3:6, :], in_=PS[:, 5, 0:480].rearrange("p (a n) -> p a n", a=3)[:, :, 0:129]),
                      reads=psk(5), writes=ka)
                sc.op("dve", lambda e: e.tensor_copy(out=accS[:, 6:8, :], in_=PS[:, 6, 0:320].rearrange("p (a n) -> p a n", a=2)[:, :, 0:129]),
                      reads=psk(6), writes=ka)
                sc.op("dve", lambda e: e.reciprocal(out=sm[:, 0:8], in_=accS[:, :, 128]), reads=ka[:1], writes=["dsm"])
                sc.op("dve", lambda e: e.tensor_scalar(out=sm[:, 4:8], in0=sm[:, 4:8], scalar1=neg_lam, scalar2=None, op0=ALU.mult),
                      reads=["dsm", "neglam"], writes=["dsm"])
                sc.op("dve", lambda e: e.memset(sm[:, 8:12], 0.0), writes=["dss"])

            def d_final_u(u):
                if True:
                    sc.op("dve", lambda e, u=u: e.tensor_scalar(out=accS[:, u, 0:128], in0=accS[:, u, 0:128], scalar1=sm[:, u:u + 1], scalar2=None, op0=ALU.mult),
                          reads=["dsm"] + ka[:1], writes=ka[:1])
                    sc.op("dve", lambda e, u=u: e.scalar_tensor_tensor(out=accS[:, u, 0:128], in0=accS[:, 4 + u, 0:128], scalar=sm[:, 4 + u:5 + u],
                                                                       in1=accS[:, u, 0:128], op0=ALU.mult, op1=ALU.add),
                          reads=["dsm"] + ka[:1], writes=ka[:1])
                    sc.op("dve", lambda e, u=u: e.scalar_tensor_tensor(out=accS[:, 4 + u, 0:128], in0=accS[:, u, 0:128], scalar=1.0, in1=accS[:, u, 0:128],
                                                                       op0=ALU.mult, op1=ALU.mult, accum_out=sm[:, 8 + u:9 + u]),
                          reads=ka[:1], writes=ka[:1] + ["dss"])

            def d_final_b(h, r):
                sc.op("act", lambda e: e.activation(out=sm[:, 12:16], in_=sm[:, 8:12], func=AF.Ln, bias=1e-5, scale=1.0 / 128), reads=["dss"], writes=["drs"])
                sc.op("act", lambda e: e.activation(out=sm[:, 12:16], in_=sm[:, 12:16], func=AF.Exp, scale=-0.5), reads=["drs"], writes=["drs"])
                for u in range(4):
                    sc.op("dve", lambda e, u=u: e.scalar_tensor_tensor(out=d_y[:, u, :], in0=accS[:, u, 0:128], scalar=sm[:, 12 + u:13 + u], in1=wd_t,
                                                                       op0=ALU.mult, op1=ALU.mult),
                          reads=ka[:1] + ["drs", "wd"], writes=[("dy", u)])

            def d_final_pe(h, r):
                for u in range(4):
                    sc.op("pe", lambda e, u=u: e.transpose(out=PSb[:, 7, u * 128:(u + 1) * 128], in_=d_y[:, u, :], identity=identb),
                          reads=[("dy", u), "identb"], writes=psk(7))
                sc.op("dve", lambda e, h=h, r=r: e.tensor_copy(out=XT[:, h, r * 512:(r + 1) * 512], in_=PSb[:, 7, 0:512]),
                      reads=psk(7), writes=[("XT", 4 * r + i) for i in range(4)])

            pend = []
            pend_b = []
            pend_u = []
            d_scores(0)
            d_scores(1)
            for i in range(len(steps)):
                if i + 2 < len(steps):
                    d_scores(i + 2)
                d_av(i)
                h, r, j = steps[i]
                if j == 15:
                    d_final(h, r)
                    pend.append((h, r))
                    pend_b.append((h, r))
                    pend_u.extend([0, 1, 2, 3])
                    d_final_u(pend_u.pop(0))
                elif pend_u:
                    d_final_u(pend_u.pop(0))
                elif j == 4 and pend_b:
                    d_final_b(*pend_b.pop(0))
                elif j == 7 and pend:
                    d_final_pe(*pend.pop(0))
            while pend_u:
                d_final_u(pend_u.pop(0))
            while pend_b:
                d_final_b(*pend_b.pop(0))
            while pend:
                d_final_pe(*pend.pop(0))
            dump("mixT_d", XT, [128, 8, 2048], [("XT", t) for t in range(NT)])

            if stop == 'D' and l == nl - 1:
                raise _Stop()
            sc.barrier()
            R_D.reset()
            qf = R_D.get(8192, BF16, "p (c n) -> p c n", c=2)
            kf = R_D.get(8192, BF16, "p (c n) -> p c n", c=2)
            kd_f = R_D.get(8192, BF16, "p (t n) -> p t n", t=16)
            Sbf = R_D.get(16384, BF16, "p (d q t e) -> p d q t e", d=2, q=2, t=16)
            stm2 = [R_D.get(4096, F32, "p (d q n) -> p d q n", d=2, q=2) for _ in range(2)]
            _p0 = R_D.pos
            sp_ = [R_D.get(2048, F32) for _ in range(2)]
            _p1 = R_D.pos
            ebt = [R_D.get(2 * 2 * 129 * 4, F32, "p (d q n) -> p d q n", d=2, q=2) for _ in range(2)]
            _p2 = R_D.pos
            enbt = [R_D.get(2 * 2 * 128 * 4, F32, "p (d q n) -> p d q n", d=2, q=2) for _ in range(2)]
            erem = [R_D.get(2048, F32) for _ in range(2)]
            dS = [R_D.get(1024, F32) for _ in range(4)]
            _pend = R_D.pos
            Am = [R_X.get(4 * 2 * 128 * 2, BF16, "p (h d n) -> p h d n", h=4, d=2) for _ in range(2)]
            R_D.pos = _p2
            g_y = [R_D.get(1024, BF16) for _ in range(2)]
            g_junk = R_D.get(512, F32)
            R_D.pos = _pend
            qb, kb, kd_b = gqT, gkT, gk_tok
            maskf = Uf[:, 0:128]
            maskb = Usf

            def tl(t):
                return slice(t * 128, (t + 1) * 128)

            def prep_A(t):
                b = t % 2
                sp = sp_[b]
                zb = 0 if b == 0 else 7
                sc.op("pe", lambda e: e.matmul(PS[:, zb, :], lhsT=G33[0:33, tl(t)], rhs=Wg[0:33, :], start=True, stop=True),
                      reads=["G33", "Wg"], writes=psk(zb))
                sc.op("act", lambda e: e.activation(out=sp, in_=PS[:, zb, :], func=AF.Exp, scale=-1.0), reads=psk(zb), writes=[("sp", b)])
                sc.op("act", lambda e: e.activation(out=sp, in_=sp, func=AF.Ln, bias=1.0, scale=1.0), reads=[("sp", b)], writes=[("sp", b)])

            prep_A(0)
            for t in range(NT):
                b = t % 2
                sp = sp_[b]
                if t + 1 < NT:
                    prep_A(t + 1)
                sc.op("pe", lambda e, sp=sp: e.matmul(PS[:, 1, 0:256], lhsT=Usf, rhs=sp[:, 0:256], start=True, stop=True), reads=[("sp", b), "Usf", "Usb"], writes=psk(1))
                sc.op("pe", lambda e, sp=sp: e.matmul(PS[:, 1, 256:512], lhsT=Usb, rhs=sp[:, 256:512], start=True, stop=True), reads=[("sp", b), "Usf", "Usb"], writes=psk(1))
                sc.op("act", lambda e, b=b: e.activation(out=erem[b], in_=PS[:, 1, :], func=AF.Exp, scale=-1.0 / 16), reads=psk(1), writes=[("erem", b)])
                sc.op("dve", lambda e, t=t, b=b: e.tensor_tensor(out=kd_f[:, t, :], in0=gk_tok[:, t, :], in1=erem[b][:, 0:256], op=ALU.mult),
                      reads=[("gk_tok", t), ("erem", b)], writes=[("kd_f", t)])
                sc.op("dve", lambda e, t=t, b=b: e.tensor_tensor(out=kd_b[:, t, :], in0=gk_tok[:, t, :], in1=erem[b][:, 256:512], op=ALU.mult),
                      reads=[("gk_tok", t), ("erem", b), ("kd_f", t)], writes=[("gk_tok", t)])
                for d in range(2):
                    U = Uf if d == 0 else Ub
                    for q in range(2):
                        sc.op("pe", lambda e, sp=sp, d=d, q=q, U=U: e.matmul(PS[:, 2 + d, q * 160:q * 160 + 129],
                                                                            lhsT=sp[:, d * 256 + q * 128:d * 256 + (q + 1) * 128], rhs=U, start=True, stop=True),
                              reads=[("sp", b), "Uf", "Ub"], writes=psk(2 + d))
                src4 = PS[:, 2:4, 0:320].rearrange("p a (q n) -> p a q n", q=2)
                sc.op("act", lambda e, b=b, src4=src4: e.activation(out=ebt[b], in_=src4[:, :, :, 0:129], func=AF.Exp, scale=-1.0 / 16),
                      reads=psk(2) + psk(3), writes=[("eb", b, 0), ("eb", b, 1)])
                sc.op("act", lambda e, b=b, src4=src4: e.activation(out=enbt[b], in_=src4[:, :, :, 0:128], func=AF.Exp, scale=1.0 / 16),
                      reads=psk(2) + psk(3), writes=[("enb", b, 0), ("enb", b, 1)])
                sc.op("dve", lambda e, t=t, b=b: e.tensor_tensor(out=qf[:, :, tl(t)], in0=gqT[:, :, tl(t)], in1=ebt[b][:, 0, :, 0:128], op=ALU.mult),
                      reads=[("gqT", t), ("eb", b, 0)], writes=[("qf", t)])
                sc.op("dve", lambda e, t=t, b=b: e.tensor_tensor(out=kf[:, :, tl(t)], in0=gkT[:, :, tl(t)], in1=enbt[b][:, 0, :, :], op=ALU.mult),
                      reads=[("gkT", t), ("enb", b, 0)], writes=[("kf", t)])
                sc.op("dve", lambda e, t=t, b=b: e.tensor_tensor(out=qb[:, :, tl(t)], in0=gqT[:, :, tl(t)], in1=ebt[b][:, 1, :, 0:128], op=ALU.mult),
                      reads=[("gqT", t), ("eb", b, 1), ("qf", t)], writes=[("gqT", t)])
                sc.op("dve", lambda e, t=t, b=b: e.tensor_tensor(out=kb[:, :, tl(t)], in0=gkT[:, :, tl(t)], in1=enbt[b][:, 1, :, :], op=ALU.mult),
                      reads=[("gkT", t), ("enb", b, 1), ("kf", t)], writes=[("gkT", t)])
                sc.op("dve", lambda e, t=t, b=b: e.tensor_copy(out=decs[:, :, :, t:t + 1], in_=ebt[b][:, :, :, 128:129]),
                      reads=[("eb", b, 0), ("eb", b, 1)], writes=["decs"])
            dump("qf", qf, [128, 2, 2048], [("qf", t) for t in range(NT)])
            dump("kd_f", kd_f, [128, 16, 256], [("kd_f", t) for t in range(NT)])
            dump("decs", decs, [128, 2, 2, 16], ["decs"])

            if stop == 'G1' and l == nl - 1:
                raise _Stop()
            sc.op("dve", lambda e: e.memset(stm2[0], 0.0), writes=[("stm", 0, d, q) for d in range(2) for q in range(2)])
            chains = [(d, q) for d in range(2) for q in range(2)]
            par = {c: 0 for c in chains}
            for i in range(NT):
                todo = []
                for ci, (d, q) in enumerate(chains):
                    t = i if d == 0 else NT - 1 - i
                    cur = par[(d, q)]
                    if i > 0:
                        sc.op("act", lambda e, d=d, q=q, t=t, cur=cur: e.activation(out=Sbf[0:64, d, q, t, :], in_=stm2[cur][0:64, d, q, 0:128], func=AF.Copy),
                              reads=[("stm", cur, d, q)], writes=[("Sbf", d, q, t)])
                        sc.op("dve", lambda e, d=d, q=q, t=t, cur=cur: e.tensor_copy(out=Sbf[64:128, d, q, t, :], in_=stm2[cur][64:128, d, q, 128:256]),
                              reads=[("stm", cur, d, q)], writes=[("Sbf", d, q, t)])
                    if i == NT - 1:
                        continue
                    kd = kd_f if d == 0 else kd_b
                    kkey = "kd_f" if d == 0 else "gk_tok"
                    pslot = ci % 2
                    pk = psk(4 + pslot)
                    sc.op("pe", lambda e, kd=kd, t=t, q=q, pslot=pslot: e.matmul(PS[:, 4 + pslot, 0:256], lhsT=kd[:, t, q * 128:(q + 1) * 128],
                                                                                rhs=gv[:, t, q * 256:(q + 1) * 256], start=True, stop=True),
                          reads=[(kkey, t), ("gv", t)], writes=pk)
                    if os.environ.get("GSKIP") != "evac":
                        sc.op("act", lambda e, ci=ci, pslot=pslot: e.activation(out=dS[ci], in_=PS[:, 4 + pslot, 0:256], func=AF.Copy),
                              reads=pk, writes=[("dS", ci)])
                    todo.append((ci, d, q, t, cur))
                for (ci, d, q, t, cur) in todo:
                    if os.environ.get("GSKIP") == "upd":
                        par[(d, q)] = 1 - cur
                        continue
                    sc.op("dve", lambda e, ci=ci, d=d, q=q, t=t, cur=cur: e.scalar_tensor_tensor(
                        out=stm2[1 - cur][:, d, q, :], in0=stm2[cur][:, d, q, :], scalar=decs[:, d, q, t:t + 1], in1=dS[ci],
                        op0=ALU.mult, op1=ALU.add), reads=[("stm", cur, d, q), "decs", ("dS", ci)], writes=[("stm", 1 - cur, d, q)])
                    par[(d, q)] = 1 - cur
            dump("Sbf", Sbf, [128, 2, 2, 16, 128], [("Sbf", d, q, t) for d in range(2) for q in range(2) for t in range(NT)])
            if stop == 'G2' and l == nl - 1:
                raise _Stop()

            def g_A(t):
                b = t % 2
                A = Am[b]
                for half in range(2):
                    sbank = (4 + half) if os.environ.get('GBANK') else (2 * b + half)
                    items = []
                    for sq in range(4):
                        h, d = half + 2 * (sq // 2), sq % 2
                        q = h // 2
                        base = (h % 2) * 64
                        kk = kf if d == 0 else kb
                        qq = qf if d == 0 else qb
                        kkey = ("kf", t) if d == 0 else ("gkT", t)
                        qkey = ("qf", t) if d == 0 else ("gqT", t)
                        sc.op("pe", lambda e, kk=kk, qq=qq, q=q, base=base, sbank=sbank, sq=sq: e.matmul(
                            PS[:, sbank, sq * 128:(sq + 1) * 128], lhsT=kk[base:base + 64, q, tl(t)], rhs=qq[base:base + 64, q, tl(t)], start=True, stop=True),
                            reads=[kkey, qkey], writes=psk(sbank))
                        items.append((sq, h, d))
                        if os.environ.get("GOLD"):
                            mk = maskf if d == 0 else maskb
                            sc.op("dve", lambda e, A=A, h=h, d=d, sbank=sbank, sq=sq, mk=mk: e.tensor_tensor(
                                out=A[:, h, d, :], in0=PS[:, sbank, sq * 128:(sq + 1) * 128], in1=mk, op=ALU.mult),
                                reads=psk(sbank) + ["Uf", "Usf"], writes=[("A", b, h, d), ("sp", b)])
                    if os.environ.get("GOLD"):
                        continue
                    for (sq, h, d) in items:
                        mk = maskf if d == 0 else maskb
                        sc.op("dve", lambda e, A=A, h=h, d=d, sbank=sbank, sq=sq, mk=mk: e.tensor_tensor(
                            out=A[:, h, d, :], in0=PS[:, sbank, sq * 128:(sq + 1) * 128], in1=mk, op=ALU.mult),
                            reads=psk(sbank) + ["Uf", "Usf"], writes=[("A", b, h, d), ("sp", b)])

            def g_B(t):
                b = t % 2
                A = Am[b]
                obank = (0 + b) if os.environ.get('GBANK') else (4 + b)
                for h in range(4):
                    q = h // 2
                    base = (h % 2) * 64
                    oh_ = PS[:, obank, h * 128:(h + 1) * 128]
                    ok = psk(obank)
                    inter_f = t > 0
                    inter_b = t < NT - 1
                    sc.op("pe", lambda e, A=A, h=h, oh_=oh_: e.matmul(oh_, lhsT=A[:, h, 0, :], rhs=gv[:, t, h * 128:(h + 1) * 128], start=True, stop=False),
                          reads=[("A", b, h, 0), ("gv", t)], writes=ok)
                    sc.op("pe", lambda e, A=A, h=h, oh_=oh_, fin=(not inter_f and not inter_b): e.matmul(
                        oh_, lhsT=A[:, h, 1, :], rhs=gv[:, t, h * 128:(h + 1) * 128], start=False, stop=fin),
                        reads=[("A", b, h, 1), ("gv", t)], writes=ok)
                    if inter_f:
                        sc.op("pe", lambda e, q=q, base=base, oh_=oh_, fin=(not inter_b): e.matmul(
                            oh_, lhsT=qf[base:base + 64, q, tl(t)], rhs=Sbf[base:base + 64, 0, q, t, :], start=False, stop=fin),
                            reads=[("qf", t), ("Sbf", 0, q, t)], writes=ok)
                    if inter_b:
                        sc.op("pe", lambda e, q=q, base=base, oh_=oh_: e.matmul(
                            oh_, lhsT=qb[base:base + 64, q, tl(t)], rhs=Sbf[base:base + 64, 1, q, t, :], start=False, stop=True),
                            reads=[("gqT", t), ("Sbf", 1, q, t)], writes=ok)

            def g_norm(t):
                b = t % 2
                sm = gsm[b]
                obank = (0 + b) if os.environ.get('GBANK') else (4 + b)
                okall = psk(obank)
                for h in range(4):
                    sc.op("act", lambda e, h=h, sm=sm: e.activation(out=g_junk, in_=PS[:, obank, h * 128:(h + 1) * 128], func=AF.Square, accum_out=sm[:, h:h + 1]),
                          reads=okall, writes=["gjunk", ("gss", b), ("enb", 1, 0), ("enb", 1, 1)])
                sc.op("act", lambda e, sm=sm: e.activation(out=sm[:, 4:8], in_=sm[:, 0:4], func=AF.Sqrt, bias=1e-5, scale=1.0 / 128),
                      reads=[("gss", b)], writes=[("grs", b)])
                sc.op("dve", lambda e, sm=sm: e.reciprocal(out=sm[:, 4:8], in_=sm[:, 4:8]), reads=[("grs", b)], writes=[("grs", b)])
                for h in range(4):
                    sc.op("dve", lambda e, h=h, sm=sm, b=b: e.scalar_tensor_tensor(
                        out=g_y[b][:, h * 128:(h + 1) * 128], in0=PS[:, obank, h * 128:(h + 1) * 128], scalar=sm[:, 4 + h:5 + h],
                        in1=gr_s[:, t, h * 128:(h + 1) * 128], op0=ALU.mult, op1=ALU.mult),
                        reads=psk(obank) + [("grs", b), ("gr_s", t)], writes=[("gy", b), ("enb", 0, 0), ("enb", 0, 1)])

            def g_tr(t):
                b = t % 2
                tk = psk(6 + b)
                for h in range(4):
                    sc.op("pe", lambda e, h=h, b=b: e.transpose(out=PSb[:, 6 + b, h * 128:(h + 1) * 128], in_=g_y[b][:, h * 128:(h + 1) * 128], identity=identb),
                          reads=[("gy", b), "identb"], writes=tk)
                sc.op("act", lambda e, b=b: e.activation(out=XT[:, 4:8, tl(t)], in_=PSb[:, 6 + b, 0:512].rearrange("p (c n) -> p c n", c=4), func=AF.Copy),
                      reads=tk, writes=[("XT", t)])

            g_A(0)
            for t in range(NT):
                if t + 1 < NT:
                    g_A(t + 1)
                if os.environ.get("GSKIP") == "B":
                    continue
                g_B(t)
                if os.environ.get("GSKIP") == "norm":
                    continue
                g_norm(t)
                if os.environ.get("GSKIP") == "tr":
                    continue
                if t > 0:
                    g_tr(t - 1)
            if not os.environ.get("GSKIP"):
                g_tr(NT - 1)
            dump("mixT", XT, [128, 8, 2048], [("XT", t) for t in range(NT)])

            if stop == 'G' and l == nl - 1:
                raise _Stop()
            sc.barrier()
            load_ln_params(ln1g_d[l], ln1b_d[l])
            R_D.reset()
            W1B = [R_D.get(8192, BF16, "p (c n) -> p c n", c=8) for _ in range(2)]
            W2B = [R_D.get(8192, BF16, "p (c n) -> p c n", c=4) for _ in range(2)]
            hT = R_D.get(16384, BF16, "p (c n) -> p c n", c=4)
            relu_t = [R_D.get(2048, F32) for _ in range(2)]

            def load_ffn_block(fb, buf):
                s1 = w1_d[l, :, fb * 512:(fb + 1) * 512].rearrange("(c p) n -> p c n", p=128)
                s2 = w2_d[l, fb * 512:(fb + 1) * 512, :].rearrange("(c p) n -> p c n", p=128)
                for hf in range(2):
                    sc.dma("pool", lambda e, hf=hf: e.dma_start(out=W1B[buf][:, hf * 4:(hf + 1) * 4, :], in_=s1[:, hf * 4:(hf + 1) * 4, :]),
                           writes=[("W1B", buf, hf)])
                for hf in range(2):
                    sc.dma("pool", lambda e, hf=hf: e.dma_start(out=W2B[buf][:, hf * 2:(hf + 1) * 2, :], in_=s2[:, hf * 2:(hf + 1) * 2, :]),
                           writes=[("W2B", buf, hf)])

            load_ffn_block(0, 0)
            load_ffn_block(1, 1)
            for t in range(NT):
                sc.dma("sp", lambda e, t=t: e.dma_start(out=X[:, t, :], in_=xs_d[t * 128:(t + 1) * 128, :]), reads=[("xsd", t)], writes=[("X", t)])
            def o_mm(t):
                yb = (t % 3) * 2
                for hf in range(2):
                    for c in range(8):
                        sc.op("pe", lambda e, c=c, hf=hf, t=t, yb=yb: e.matmul(PS[:, yb + hf, :], lhsT=XT[:, c, tl(t)], rhs=WO[:, c, hf * 512:(hf + 1) * 512],
                                                                              start=(c == 0), stop=(c == 7)),
                              reads=[("XT", t), ("RW", 0, 0), ("RW", 0, 1), ("RW", 1, 0), ("RW", 1, 1)], writes=psk(yb + hf))

            def o_ln(t):
                yb = (t % 3) * 2
                sc.op("dve", lambda e, t=t, yb=yb: e.scalar_tensor_tensor(out=X[:, t, :], in0=X[:, t, :], scalar=ALPHA,
                                                                          in1=PS[:, yb:yb + 2, :].rearrange("p a n -> p (a n)"), op0=ALU.mult, op1=ALU.add),
                      reads=[("X", t)] + psk(yb) + psk(yb + 1), writes=[("X", t)])
                ln_a(t)

            for t0 in range(3):
                o_mm(t0)
                o_ln(t0)
            for t in range(NT):
                if t + 3 < NT:
                    o_mm(t + 3)
                ln_b1(t, None)
                if t + 3 < NT:
                    o_ln(t + 3)
                ln_b2(t, None)
            dump("x1T", XT, [128, 8, 2048], [("XT", t) for t in range(NT)])

            if stop == 'O' and l == nl - 1:
                raise _Stop()
            if not last:
                load_win_block(l + 1, 0, 0)
                load_win_block(l + 1, 1, 1)
            hrr = [0]
            for fb in range(8):
                buf = fb % 2
                for r in range(4):
                    for fc in range(4):
                        bank = 4 + hrr[0] % 3
                        rb = hrr[0] % 2
                        hrr[0] += 1
                        for c in range(8):
                            sc.op("pe", lambda e, c=c, fc=fc, r=r, bank=bank, buf=buf: e.matmul(
                                PS[:, bank, :], lhsT=W1B[buf][:, c, fc * 128:(fc + 1) * 128], rhs=XT[:, c, r * 512:(r + 1) * 512],
                                start=(c == 0), stop=(c == 7)), reads=[("W1B", buf, 0), ("W1B", buf, 1)] + xt_all[r * 4:(r + 1) * 4], writes=psk(bank))
                        fcol = fb * 4 + fc
                        sc.op("act", lambda e, bank=bank, rb=rb, fcol=fcol: e.activation(out=relu_t[rb], in_=PS[:, bank, :], func=AF.Relu,
                                                                                         bias=b1c[:, fcol:fcol + 1], scale=1.0),
                              reads=psk(bank) + ["b1c"], writes=[("relu", rb)])
                        sc.op("dve", lambda e, rb=rb, fc=fc, r=r: e.tensor_tensor(out=hT[:, fc, r * 512:(r + 1) * 512], in0=relu_t[rb], in1=relu_t[rb], op=ALU.mult),
                              reads=[("relu", rb)], writes=[("hT", fc, r)])
                for t in range(NT):
                    yb = (t % 2) * 2
                    for hf in range(2):
                        for fc in range(4):
                            sc.op("pe", lambda e, fc=fc, hf=hf, t=t, yb=yb, buf=buf: e.matmul(
                                PS[:, yb + hf, :], lhsT=hT[:, fc, tl(t)], rhs=W2B[buf][:, fc, hf * 512:(hf + 1) * 512],
                                start=(fc == 0), stop=(fc == 3)), reads=[("hT", fc, t // 4), ("W2B", buf, 0), ("W2B", buf, 1)], writes=psk(yb + hf))
                    if fb == 0:
                        sc.op("dve", lambda e, t=t, yb=yb: e.scalar_tensor_tensor(out=X[:, t, :], in0=X[:, t, :], scalar=ALPHA,
                                                                                  in1=PS[:, yb:yb + 2, :].rearrange("p a n -> p (a n)"), op0=ALU.mult, op1=ALU.add),
                              reads=[("X", t)] + psk(yb) + psk(yb + 1), writes=[("X", t)])
                        sc.op("pool", lambda e, t=t: e.tensor_tensor(out=X[:, t, :], in0=X[:, t, :], in1=b2t, op=ALU.add), reads=[("X", t), "b2t"], writes=[("X", t)])
                    else:
                        sc.op("dve", lambda e, t=t, yb=yb: e.tensor_tensor(out=X[:, t, :], in0=X[:, t, :], in1=PS[:, yb:yb + 2, :].rearrange("p a n -> p (a n)"), op=ALU.add),
                              reads=[("X", t)] + psk(yb) + psk(yb + 1), writes=[("X", t)])
                if fb + 2 < 8:
                    load_ffn_block(fb + 2, buf)
            if stop == 'F' and l == nl - 1:
                raise _Stop()
            load_ln_params(ln2g_d[l], ln2b_d[l])
            ln_all(out_d if last else xs_d)
            sc.barrier()


        for _l in range(nl):
            do_layer(_l)
    except _Stop:
        pass
    out_dmas = [o for o in sc.ops if o.is_dma and o.dkey in [("xs", i) for i in range(4)]]
    fin = {}
    for o in out_dmas:
        fin[o.dkey] = o
    finals = list(fin.values()) + list(dbg_out.values())
    sc.emit(final_wait_ops=finals)
    es.close()
    return nc, sc


_CONST = None


def kernel(**inputs):
    global _CONST
    if _CONST is None:
        _CONST = _constants()
    nc, _ = build(2)
    x = np.ascontiguousarray(inputs["x"], dtype=np.float32)
    shared = {k: np.ascontiguousarray(v, dtype=np.float32) for k, v in inputs.items() if k != "x"}
    shared.update(_CONST)
    in_maps = []
    for b in range(8):
        m = dict(shared)
        m["x"] = x[b]
        in_maps.append(m)
    res = run_bass_kernel_spmd(nc, in_maps, core_ids=list(range(8)))
    return np.stack([r["out"] for r in res.results], axis=0).astype(np.float32)
```

```python
import math
import os
from contextlib import ExitStack

import numpy as np
import concourse.bass as bass
import concourse.mybir as mybir
from concourse.bass_utils import run_bass_kernel_spmd

F32 = mybir.dt.float32
BF16 = mybir.dt.bfloat16
AF = mybir.ActivationFunctionType
ALU = mybir.AluOpType

S = 2048
D = 1024
DIN = 3104
DFF = 4096
NT = 16
ALPHA = (2.0 * 2) ** 0.25
ENGS = ("pe", "act", "dve", "pool", "sp")
EPOCH = 30000


class _Res:
    __slots__ = ("last_w", "readers")

    def __init__(self):
        self.last_w = None
        self.readers = []


class _Op:
    __slots__ = ("eng", "fn", "deps", "signal", "tok", "is_dma", "dkey")

    def __init__(self, eng, fn, is_dma, dkey):
        self.eng = eng
        self.fn = fn
        self.deps = []
        self.signal = False
        self.tok = None
        self.is_dma = is_dma
        self.dkey = dkey


class Sched:
    def __init__(self, nc):
        self.nc = nc
        self.ops = []
        self.res = {}
        self.pending = {e: [] for e in ENGS}

    def _r(self, key):
        x = self.res.get(key)
        if x is None:
            x = self.res[key] = _Res()
        return x

    def _add(self, op, reads, writes):
        deps = set()
        for k in reads:
            rs = self._r(k)
            if rs.last_w is not None:
                deps.add(rs.last_w)
        for k in writes:
            rs = self._r(k)
            if rs.last_w is not None:
                deps.add(rs.last_w)
            deps.update(rs.readers)
        for k in reads:
            self._r(k).readers.append(op)
        for k in writes:
            rs = self._r(k)
            rs.last_w = op
            rs.readers = []
        if self.pending[op.eng]:
            deps.update(self.pending[op.eng])
            self.pending[op.eng] = []
        deps.discard(op)
        op.deps = list(deps)
        self.ops.append(op)
        return op

    def op(self, eng, fn, reads=(), writes=()):
        return self._add(_Op(eng, fn, False, None), reads, writes)

    def dma(self, eng, fn, dkey=None, reads=(), writes=()):
        if dkey is None:
            dkey = ("w", writes[0])
        return self._add(_Op(eng, fn, True, dkey), reads, writes)

    def barrier(self):
        last = {}
        for o in self.ops:
            last[(o.eng, o.dkey) if o.is_dma else o.eng] = o
        b = list(last.values())
        self.pending = {e: list(b) for e in ENGS}

    def emit(self, final_wait_ops=()):
        nc = self.nc
        ops = self.ops
        for o in ops:
            for d in o.deps:
                if d.is_dma:
                    d.signal = True
                elif d.eng == "pe" and o.eng == "pe" and not o.is_dma:
                    continue
                else:
                    d.signal = True
        with ExitStack() as es:
            eng_sems = {e: [] for e in ENGS}
            cnt = {e: 0 for e in ENGS}
            dma_sems = {}
            dma_cnt = {}
            for o in ops:
                if o.is_dma:
                    if o.dkey not in dma_sems:
                        dma_sems[o.dkey] = es.enter_context(nc.semaphore("d%d" % len(dma_sems)))
                        dma_cnt[o.dkey] = 0
                    dma_cnt[o.dkey] += 16
                    o.tok = (dma_sems[o.dkey], dma_cnt[o.dkey])
                elif o.signal:
                    ep = cnt[o.eng] // EPOCH
                    if ep >= len(eng_sems[o.eng]):
                        eng_sems[o.eng].append(es.enter_context(nc.semaphore("e_%s_%d" % (o.eng, ep))))
                    cnt[o.eng] += 1
                    o.tok = (eng_sems[o.eng][ep], cnt[o.eng] - ep * EPOCH)
            per_eng = {e: [o for o in ops if o.eng == e] for e in ENGS}
            self.stats = {e: len(per_eng[e]) for e in ENGS}
            self.stats["sems"] = sum(len(v) for v in eng_sems.values()) + len(dma_sems)

            def run(e, eng):
                waited = {}
                for o in per_eng[e]:
                    need = {}
                    for d in o.deps:
                        if d.tok is None:
                            continue
                        if (not d.is_dma) and d.eng == "pe" and e == "pe" and not o.is_dma:
                            continue
                        s, v = d.tok
                        k = id(s)
                        if waited.get(k, 0) >= v:
                            continue
                        if k not in need or need[k][1] < v:
                            need[k] = (s, v)
                    for k, (s, v) in need.items():
                        eng.wait_ge(s, v)
                        waited[k] = v
                    ins = o.fn(eng)
                    if o.tok is not None:
                        ins.then_inc(o.tok[0], 16 if o.is_dma else 1)
                if e == "sp":
                    for o in final_wait_ops:
                        s, v = o.tok
                        eng.wait_ge(s, v)

            with nc.Block() as block:
                @block.sync
                def _(eng):
                    run("sp", eng)

                @block.tensor
                def _(eng):
                    run("pe", eng)

                @block.scalar
                def _(eng):
                    run("act", eng)

                @block.vector
                def _(eng):
                    run("dve", eng)

                @block.gpsimd
                def _(eng):
                    run("pool", eng)


def _t5_bucket(rel):
    nb = 16
    me = 8
    ret = np.where(rel > 0, nb, 0)
    n = np.abs(rel)
    large = me + (np.log(np.maximum(n, 1).astype(np.float32) / np.float32(me))
                  / np.float32(math.log(128 / me)) * np.float32(nb - me)).astype(np.int32)
    large = np.minimum(large, nb - 1)
    return ret + np.where(n < me, n, large)


MLEN = 1280


def _constants():
    c = {}
    c["c_ident"] = np.eye(128, dtype=np.float32)
    c["c_J"] = np.eye(128, dtype=np.float32)[::-1].copy()
    s = np.arange(128)[:, None]
    t = np.arange(128)[None, :]
    uf = np.zeros((128, 129), np.float32)
    uf[:, :128] = (s <= t)
    uf[:, 128] = 1.0
    ub = np.zeros((128, 129), np.float32)
    ub[:, :128] = (s >= t)
    ub[:, 128] = 1.0
    c["c_uf"] = uf
    c["c_ub"] = ub
    c["c_sf"] = (s > t).astype(np.float32)
    c["c_sb"] = (s < t).astype(np.float32)
    n = np.arange(MLEN)
    bk = _t5_bucket(639 - n)
    oh = np.zeros((32, MLEN), np.float32)
    oh[bk, n] = 1.0
    c["c_onehot"] = oh
    return c


class _Stop(Exception):
    pass


def build(nl=2, dbg=(), stop=None):
    nc = bass.Bass("TRN2", target_bir_lowering=False)

    def din(name, shape):
        return nc.dram_tensor(name, list(shape), F32, kind="ExternalInput").ap()

    x_d = din("x", [S, D])
    lnemb_g = din("ln_emb_g", [D])
    lnemb_b = din("ln_emb_b", [D])
    table_d = din("rel_bias_table", [32, 4])
    w_in_d = din("w_in", [2, D, DIN])
    lq1_d = din("lambda_q1", [2, 64])
    lk1_d = din("lambda_k1", [2, 64])
    lq2_d = din("lambda_q2", [2, 64])
    lk2_d = din("lambda_k2", [2, 64])
    dnw_d = din("diff_norm_w", [2, 128])
    gup_d = din("gla_gate_up", [2, 2, 16, 256])
    gbias_d = din("gla_gate_bias", [2, 2, 256])
    gnw_d = din("gla_norm_w", [2, 128])
    w_o_d = din("w_o", [2, D, D])
    ln1g_d = din("ln1_g", [2, D])
    ln1b_d = din("ln1_b", [2, D])
    w1_d = din("w_ffn1", [2, D, DFF])
    b1_d = din("b_ffn1", [2, DFF])
    w2_d = din("w_ffn2", [2, DFF, D])
    b2_d = din("b_ffn2", [2, D])
    ln2g_d = din("ln2_g", [2, D])
    ln2b_d = din("ln2_b", [2, D])
    c_ident = din("c_ident", [128, 128])
    c_J = din("c_J", [128, 128])
    c_uf = din("c_uf", [128, 129])
    c_ub = din("c_ub", [128, 129])
    c_sf = din("c_sf", [128, 128])
    c_sb = din("c_sb", [128, 128])
    c_onehot = din("c_onehot", [32, MLEN])
    out_d = nc.dram_tensor("out", [S, D], F32, kind="ExternalOutput").ap()
    xs_d = nc.dram_tensor("xs_scratch", [S, D], F32).ap()
    md_t = nc.dram_tensor("md_scratch", [4, MLEN], F32)
    eb_d = nc.dram_tensor("expb_scratch", [128, 4 * 1152], BF16).ap()
    md_d = md_t.ap()
    dbg_out = {}

    sc = Sched(nc)
    es = ExitStack()
    ARENA_BYTES = 207 * 1024
    arena = es.enter_context(nc.sbuf_tensor("arena", [128, ARENA_BYTES // 2], BF16))
    PSb = es.enter_context(nc.psum_tensor("ps", [128, 8, 1024], BF16))[:]
    PS = PSb.bitcast(F32)

    def view(off, nbytes, dt, pattern=None, **kw):
        assert off % 32 == 0, off
        a = arena[:, off // 2:(off + nbytes) // 2]
        if dt is F32:
            a = a.bitcast(F32)
        if pattern:
            a = a.rearrange(pattern, **kw)
        return a

    class Alloc:
        def __init__(self, base, size):
            self.base = base
            self.size = size
            self.pos = 0

        def reset(self):
            self.pos = 0

        def get(self, nbytes, dt, pattern=None, **kw):
            n = (nbytes + 31) // 32 * 32
            assert self.pos + n <= self.size, (self.pos, n, self.size)
            v = view(self.base + self.pos, nbytes, dt, pattern, **kw)
            self.pos += n
            return v

    R_XT = Alloc(0, 32768)
    R_X = Alloc(32768, 65536)
    R_D = Alloc(98304, 70656)
    R_W = Alloc(168960, 16384)
    R_C = Alloc(185344, ARENA_BYTES - 185344)

    XT = R_XT.get(32768, BF16, "p (c n) -> p c n", c=8)
    X = R_X.get(65536, F32, "p (t n) -> p t n", t=NT)
    WB = [R_W.get(8192, BF16, "p (c n) -> p c n", c=8) for _ in range(2)]
    R_W.reset()
    WO = R_W.get(16384, BF16, "p (c n) -> p c n", c=8)

    identb = R_C.get(256, BF16)
    Jb = R_C.get(256, BF16)
    Uf = R_C.get(516, F32)
    Ub = R_C.get(516, F32)
    Usf = R_C.get(512, F32)
    Usb = R_C.get(512, F32)
    gt = R_C.get(4096, F32)
    bt = R_C.get(4096, F32)
    b2t = R_C.get(4096, F32)
    wd_t = R_C.get(512, F32)
    wg_t = R_C.get(512, F32)
    b1c = R_C.get(128, F32)
    cb = R_C.get(32, F32, "p (s h) -> p s h", s=2)
    lamv = R_C.get(4 * 64 * 4, F32, "p (a n) -> p a n", a=4)
    lamp = R_C.get(2 * 64 * 4, F32, "p (a n) -> p a n", a=2)
    lams = R_C.get(32, F32)
    Wg = R_C.get(1024, BF16)
    st_ = [R_C.get(48, F32) for _ in range(2)]
    mv_ = [R_C.get(8, F32) for _ in range(2)]
    rs_ = [R_C.get(4, F32) for _ in range(2)]
    xb_ = [R_C.get(2048, BF16) for _ in range(2)]
    dsm = [R_C.get(64, F32) for _ in range(2)]
    gsm = [R_C.get(64, F32) for _ in range(2)]
    decs = R_C.get(2 * 2 * 16 * 4, F32, "p (d q t) -> p d q t", d=2, q=2)

    def psk(b):
        return [("ps", b, q) for q in range(4)]

    def bc_mid(ap2, n):
        a = ap2.ap
        return bass.AP(ap2.tensor, ap2.offset, [list(a[0]), [0, n], list(a[1])])

    def bc_last(ap2, n):
        a = ap2.ap
        return bass.AP(ap2.tensor, ap2.offset, [list(a[0]), list(a[1]), [0, n]])

    cur_layer = [-1]

    def dump(name, ap, shape, reads):
        nm = "%s@%d" % (name, cur_layer[0])
        if nm in dbg:
            name = nm
        elif name not in dbg or (cur_layer[0] >= 0 and cur_layer[0] != nl - 1):
            return
        t = nc.dram_tensor("dbg_" + name.replace("@", "_"), list(shape), ap.dtype, kind="ExternalOutput").ap()
        dbg_out[name] = sc.dma("sp", lambda e: e.dma_start(out=t, in_=ap), "dbg", reads=reads)

    try:
        sc.dma("pool", lambda e: e.dma_start(out=identb, in_=c_ident), writes=["identb"])
        sc.dma("pool", lambda e: e.dma_start(out=Jb, in_=c_J), writes=["Jb"])
        sc.dma("sp", lambda e: e.dma_start(out=Uf, in_=c_uf), writes=["Uf"])
        sc.dma("sp", lambda e: e.dma_start(out=Ub, in_=c_ub), writes=["Ub"])
        sc.dma("sp", lambda e: e.dma_start(out=Usf, in_=c_sf), writes=["Usf"])
        sc.dma("sp", lambda e: e.dma_start(out=Usb, in_=c_sb), writes=["Usb"])
        for si, row in enumerate((15, 31)):
            sc.dma("sp", lambda e, si=si, row=row: e.dma_start(out=cb[:, si, :], in_=table_d[row, :].partition_broadcast(128)),
                   writes=[("cb", si)])

        R_D.reset()
        tb = R_D.get(16, F32)
        oh = R_D.get(MLEN * 4, F32)
        msb = R_D.get(MLEN * 4, F32)
        sc.dma("sp", lambda e: e.dma_start(out=tb[0:32, :], in_=table_d), writes=["tb"])
        sc.dma("sp", lambda e: e.dma_start(out=oh[0:32, :], in_=c_onehot), writes=["oh"])
        sc.dma("sp", lambda e: e.dma_start(out=gt, in_=lnemb_g.partition_broadcast(128)), writes=["gt"])
        sc.dma("sp", lambda e: e.dma_start(out=bt, in_=lnemb_b.partition_broadcast(128)), writes=["bt"])
        for t in range(NT):
            sc.dma("sp", lambda e, t=t: e.dma_start(out=X[:, t, :], in_=x_d[t * 128:(t + 1) * 128, :]), writes=[("X", t)])
        for ci, (c0, cn) in enumerate(((0, 512), (512, 512), (1024, 256))):
            sc.op("pe", lambda e, ci=ci, c0=c0, cn=cn: e.matmul(PS[0:4, ci, 0:cn], lhsT=tb[0:32, :], rhs=oh[0:32, c0:c0 + cn], start=True, stop=True),
                  reads=["tb", "oh"], writes=psk(ci))
            sc.op("dve", lambda e, ci=ci, c0=c0, cn=cn: e.tensor_copy(out=msb[0:4, c0:c0 + cn], in_=PS[0:4, ci, 0:cn]),
                  reads=psk(ci), writes=["msb"])
        sc.dma("sp", lambda e: e.dma_start(out=md_d, in_=msb[0:4, :]), reads=["msb"], writes=["md"])
        R_D.reset()
        R_D.get(16384, BF16); R_D.get(16384, BF16); R_D.get(16 * 4 * 129 * 2, BF16)
        expB0 = R_D.get(4 * 1152 * 2, BF16, "p (h n) -> p h n", h=4)
        R_D.get(4096, BF16); R_D.get(1024, BF16)
        tmp_revs = [view(R_D.base + 16384 + i * 2304, 2304, BF16) for i in range(4)]
        for h in range(4):
            src = bass.AP(md_t, h * MLEN, [[1, 128], [1, 1152]])
            sc.dma("pool", lambda e, src=src, h=h: e.dma_start(out=tmp_revs[h], in_=src), reads=["md"], writes=[("tmp_rev", h)])
        for h in range(4):
            for ci, (c0, cn) in enumerate(((0, 512), (512, 512), (1024, 128))):
                sc.op("pe", lambda e, ci=ci, c0=c0, cn=cn, h=h: e.matmul(PS[:, ci, 0:cn], lhsT=Jb, rhs=tmp_revs[h][:, c0:c0 + cn], start=True, stop=True),
                      reads=["Jb", ("tmp_rev", h)], writes=psk(ci))
                sc.op("act", lambda e, h=h, ci=ci, c0=c0, cn=cn: e.activation(out=expB0[:, h, c0:c0 + cn], in_=PS[:, ci, 0:cn], func=AF.Exp),
                      reads=psk(ci), writes=[("expB", h)])
        sc.dma("sp", lambda e: e.dma_start(out=eb_d, in_=expB0.rearrange("p h n -> p (h n)")), reads=[("expB", h) for h in range(4)], writes=["eb_d"])

        def ln_a(t):
            Xt = X[:, t, :]
            kx = ("X", t)
            b = t % 2
            st, mv, rs = st_[b], mv_[b], rs_[b]
            sc.op("dve", lambda e: e.bn_stats(out=st[:, 0:6], in_=Xt[:, 0:512]), reads=[kx], writes=[("st", b, 0)])
            sc.op("dve", lambda e: e.bn_stats(out=st[:, 6:12], in_=Xt[:, 512:1024]), reads=[kx], writes=[("st", b, 1)])
            sc.op("dve", lambda e: e.bn_aggr(out=mv, in_=st), reads=[("st", b, 0), ("st", b, 1)], writes=[("mv", b)])
            sc.op("act", lambda e: e.activation(out=rs, in_=mv[:, 1:2], func=AF.Sqrt, bias=1e-5, scale=1.0), reads=[("mv", b)], writes=[("rs", b)])
            sc.op("dve", lambda e: e.reciprocal(out=rs, in_=rs), reads=[("rs", b)], writes=[("rs", b)])
            sc.op("dve", lambda e: e.tensor_scalar(out=Xt, in0=Xt, scalar1=mv[:, 0:1], scalar2=rs, op0=ALU.subtract, op1=ALU.mult),
                  reads=[kx, ("mv", b), ("rs", b)], writes=[kx])
            sc.op("dve", lambda e: e.tensor_tensor(out=Xt, in0=Xt, in1=gt, op=ALU.mult), reads=[kx, "gt"], writes=[kx])
            sc.op("pool", lambda e: e.tensor_tensor(out=Xt, in0=Xt, in1=bt, op=ALU.add), reads=[kx, "bt"], writes=[kx])

        def ln_b1(t, spill_to):
            Xt = X[:, t, :]
            kx = ("X", t)
            b = t % 2
            xb = xb_[b]
            if spill_to is not None:
                sc.dma("sp", lambda e: e.dma_start(out=spill_to[t * 128:(t + 1) * 128, :], in_=Xt), ("xs", t % 4), reads=[kx], writes=[("xsd", t)])
            if spill_to is out_d:
                return
            sc.op("act", lambda e: e.activation(out=xb, in_=Xt, func=AF.Copy), reads=[kx], writes=[("xb", b)])

        def ln_b2(t, spill_to):
            if spill_to is out_d:
                return
            b = t % 2
            xb = xb_[b]
            bank = 6 + b
            for c in range(8):
                sc.op("pe", lambda e, c=c: e.transpose(out=PSb[:, bank, c * 128:(c + 1) * 128], in_=xb[:, c * 128:(c + 1) * 128], identity=identb),
                      reads=[("xb", b), "identb"], writes=psk(bank))
            sc.op("act", lambda e: e.activation(out=XT[:, :, t * 128:(t + 1) * 128], in_=PSb[:, bank, :].rearrange("p (c n) -> p c n", c=8), func=AF.Copy),
                  reads=psk(bank), writes=[("XT", t)])

        def ln_all(spill_to):
            ln_a(0)
            for t in range(NT):
                if t + 1 < NT:
                    ln_a(t + 1)
                ln_b1(t, spill_to)
                ln_b2(t, spill_to)

        def load_ln_params(g_ap, b_ap):
            sc.dma("sp", lambda e: e.dma_start(out=gt, in_=g_ap.partition_broadcast(128)), writes=["gt"])
            sc.dma("sp", lambda e: e.dma_start(out=bt, in_=b_ap.partition_broadcast(128)), writes=["bt"])

        def load_win_block(l, blk, buf):
            c0 = blk * 512
            ncol = min(512, DIN - c0)
            src = w_in_d[l, :, c0:c0 + ncol].rearrange("(c p) n -> p c n", p=128)
            for hf in range(2):
                sc.dma("pool", lambda e, hf=hf: e.dma_start(out=WB[buf][:, hf * 4:(hf + 1) * 4, 0:ncol], in_=src[:, hf * 4:(hf + 1) * 4, :]),
                       writes=[("RW", buf, hf)])

        if stop == 'init':
            raise _Stop()
        load_win_block(0, 0, 0)
        load_win_block(0, 1, 1)
        ln_all(xs_d)
        dump("h0", X, [128, NT, 1024], [("X", t) for t in range(NT)])

        if stop == 'emb':
            raise _Stop()
        evac_rr = [0]

        def evac(out, in_, reads, writes, scale=None):
            evac_rr[0] ^= 1
            if evac_rr[0]:
                if scale is None:
                    sc.op("act", lambda e: e.activation(out=out, in_=in_, func=AF.Copy), reads=reads, writes=writes)
                else:
                    sc.op("act", lambda e: e.mul(out=out, in_=in_, mul=scale), reads=reads, writes=writes)
            else:
                if scale is None:
                    sc.op("dve", lambda e: e.tensor_copy(out=out, in_=in_), reads=reads, writes=writes)
                else:
                    sc.op("dve", lambda e: e.tensor_scalar(out=out, in0=in_, scalar1=scale, scalar2=None, op0=ALU.mult), reads=reads, writes=writes)

        def do_layer(l):
            lam_init = 0.8 - 0.6 * math.exp(-0.3 * l)
            cur_layer[0] = l
            if l == 0:
                sc.barrier()
            last = (l == nl - 1)
            R_D.reset()
            QT = R_D.get(16384, BF16, "p (h n) -> p h n", h=4)
            KT = R_D.get(16384, BF16, "p (h n) -> p h n", h=4)
            V = R_D.get(16 * 4 * 129 * 2, BF16, "p (t h e) -> p t h e", t=16, h=4)
            expB = R_D.get(4 * 1152 * 2, BF16, "p (h n) -> p h n", h=4)
            Eb = R_D.get(4096, BF16, "p (b m n) -> p b m n", b=2, m=2)
            d_y = R_D.get(4 * 128 * 2, BF16, "p (u n) -> p u n", u=4)
            _pu = R_D.pos
            silu_t = [R_D.get(2048, F32) for _ in range(2)]
            R_D.pos = _pu
            accS = R_D.get(8 * 129 * 4, F32, "p (a n) -> p a n", a=8)
            R_D.pos = _pu + 4608
            tmp_rev = R_D.get(1152 * 2, BF16)
            R_X.reset()
            gqT = R_X.get(8192, BF16, "p (c n) -> p c n", c=2)
            gkT = R_X.get(8192, BF16, "p (c n) -> p c n", c=2)
            gk_tok = R_X.get(8192, BF16, "p (t n) -> p t n", t=16)
            gv = R_X.get(16384, BF16, "p (t n) -> p t n", t=16)
            gr_s = R_X.get(16384, BF16, "p (t n) -> p t n", t=16)
            G33 = R_X.get(4096, BF16)

            for i, ap in enumerate((lq1_d, lk1_d, lq2_d, lk2_d)):
                sc.dma("sp", lambda e, i=i, ap=ap: e.dma_start(out=lamv[:, i, :], in_=ap[l, :].partition_broadcast(128)), writes=[("lamv", i)])
            sc.op("dve", lambda e: e.tensor_tensor(out=lamp[:, 0, :], in0=lamv[:, 0, :], in1=lamv[:, 1, :], op=ALU.mult), reads=[("lamv", 0), ("lamv", 1)], writes=["lamp"])
            sc.op("dve", lambda e: e.tensor_tensor(out=lamp[:, 1, :], in0=lamv[:, 2, :], in1=lamv[:, 3, :], op=ALU.mult), reads=[("lamv", 2), ("lamv", 3)], writes=["lamp"])
            sc.op("dve", lambda e: e.reduce_sum(out=lams[:, 0:2], in_=lamp, axis=mybir.AxisListType.X), reads=["lamp"], writes=["lams"])
            sc.op("act", lambda e: e.activation(out=lams[:, 0:2], in_=lams[:, 0:2], func=AF.Exp), reads=["lams"], writes=["lams"])
            sc.op("dve", lambda e: e.tensor_tensor(out=lams[:, 2:3], in0=lams[:, 0:1], in1=lams[:, 1:2], op=ALU.subtract), reads=["lams"], writes=["lams"])
            sc.op("dve", lambda e: e.tensor_scalar(out=lams[:, 3:4], in0=lams[:, 2:3], scalar1=lam_init, scalar2=-1.0, op0=ALU.add, op1=ALU.mult),
                  reads=["lams"], writes=["neglam"])
            neg_lam = lams[:, 3:4]
            sc.dma("sp", lambda e: e.dma_start(out=wd_t, in_=dnw_d[l, :].partition_broadcast(128)), writes=["wd"])
            sc.op("dve", lambda e: e.tensor_scalar(out=wd_t, in0=wd_t, scalar1=1.0 - lam_init, scalar2=None, op0=ALU.mult), reads=["wd"], writes=["wd"])
            sc.dma("sp", lambda e: e.dma_start(out=wg_t, in_=gnw_d[l, :].partition_broadcast(128)), writes=["wg"])
            sc.op("dve", lambda e: e.memset(Wg[0:33, :], 0.0), writes=["Wg"])
            sc.dma("pool", lambda e: e.dma_start(out=Wg[0:16, 0:256], in_=gup_d[l, 0]), writes=["Wg"])
            sc.dma("pool", lambda e: e.dma_start(out=Wg[16:32, 256:512], in_=gup_d[l, 1]), writes=["Wg"])
            sc.dma("pool", lambda e: e.dma_start(out=Wg[32:33, :], in_=gbias_d[l].rearrange("a n -> (a n)").partition_broadcast(1)), writes=["Wg"])
            sc.dma("sp", lambda e: e.dma_start(out=b1c, in_=b1_d[l].rearrange("(c p) -> p c", p=128), allow_slow_non_contiguous=True), writes=["b1c"])
            sc.dma("sp", lambda e: e.dma_start(out=b2t, in_=b2_d[l].partition_broadcast(128)), writes=["b2t"])
            sc.dma("sp", lambda e: e.dma_start(out=expB.rearrange("p h n -> p (h n)"), in_=eb_d), reads=["eb_d"], writes=[("expB", h) for h in range(4)])
            sc.op("dve", lambda e: e.memset(V[:, :, :, 128:129], 1.0), writes=[("V", t) for t in range(NT)])
            sc.op("dve", lambda e: e.memset(G33[32:33, :], 1.0), writes=["G33"])

            dump("XTin", XT, [128, 8, 2048], [("XT", t) for t in range(NT)])
            dump("Xin", X, [128, NT, 1024], [("X", t) for t in range(NT)])
            if stop == 'L' and l == nl - 1:
                raise _Stop()
            ps_rr = [0]

            def nextbank():
                b = ps_rr[0] % 6
                ps_rr[0] += 1
                return b

            xt_all = [("XT", t) for t in range(NT)]
            for blk in range(7):
                buf = blk % 2
                wb = WB[buf]
                kw = [("RW", buf, 0), ("RW", buf, 1)]
                if blk in (0, 1, 3, 6):
                    nch = 1 if blk == 6 else 4
                    for cc in range(nch):
                        for r in range(4):
                            bank = nextbank()
                            M = 32 if blk == 6 else 128
                            for c in range(8):
                                sc.op("pe", lambda e, c=c, cc=cc, r=r, bank=bank, M=M, wb=wb: e.matmul(
                                    PS[0:M, bank, :], lhsT=wb[:, c, cc * 128:cc * 128 + M], rhs=XT[:, c, r * 512:(r + 1) * 512],
                                    start=(c == 0), stop=(c == 7)), reads=kw + xt_all[r * 4:(r + 1) * 4], writes=psk(bank))
                            sl = slice(r * 512, (r + 1) * 512)
                            if blk == 0:
                                evac(QT[:, cc, sl], PS[:, bank, :], psk(bank), [("QT", cc, r)], scale=0.125)
                            elif blk == 1:
                                evac(KT[:, cc, sl], PS[:, bank, :], psk(bank), [("KT", cc, r)])
                            elif blk == 3:
                                if cc < 2:
                                    evac(gqT[:, cc, sl], PS[:, bank, :], psk(bank), [("gqT", 4 * r + i) for i in range(4)], scale=0.125)
                                else:
                                    evac(gkT[:, cc - 2, sl], PS[:, bank, :], psk(bank), [("gkT", 4 * r + i) for i in range(4)])
                            else:
                                evac(G33[0:32, sl], PS[0:32, bank, :], psk(bank), ["G33"])
                if blk in (2, 3, 4, 5):
                    for t in range(NT):
                        bank = nextbank()
                        c0, ncol = (256, 256) if blk == 3 else (0, 512)
                        for c in range(8):
                            sc.op("pe", lambda e, c=c, t=t, bank=bank, c0=c0, ncol=ncol, wb=wb: e.matmul(
                                PS[:, bank, 0:ncol], lhsT=XT[:, c, t * 128:(t + 1) * 128], rhs=wb[:, c, c0:c0 + ncol],
                                start=(c == 0), stop=(c == 7)), reads=kw + [("XT", t)], writes=psk(bank))
                        if blk == 2:
                            evac(V[:, t, :, 0:128], PS[:, bank, :].rearrange("p (h e) -> p h e", h=4), psk(bank), [("V", t)])
                        elif blk == 3:
                            evac(gk_tok[:, t, :], PS[:, bank, 0:256], psk(bank), [("gk_tok", t)])
                        elif blk == 4:
                            evac(gv[:, t, :], PS[:, bank, :], psk(bank), [("gv", t)])
                        else:
                            sb = t % 2
                            sc.op("act", lambda e, bank=bank, sb=sb: e.activation(out=silu_t[sb], in_=PS[:, bank, :], func=AF.Silu),
                                  reads=psk(bank), writes=[("silu", sb)])
                            sc.op("dve", lambda e, t=t, sb=sb: e.tensor_tensor(
                                out=gr_s[:, t, :].rearrange("p (h e) -> p h e", h=4), in0=silu_t[sb].rearrange("p (h e) -> p h e", h=4),
                                in1=bc_mid(wg_t, 4), op=ALU.mult), reads=[("silu", sb), "wg"], writes=[("gr_s", t)])
                if blk + 2 < 7:
                    load_win_block(l, blk + 2, buf)
            for hf in range(2):
                sc.dma("pool", lambda e, hf=hf: e.dma_start(out=WO[:, hf * 4:(hf + 1) * 4, :],
                                                             in_=w_o_d[l].rearrange("(c p) n -> p c n", p=128)[:, hf * 4:(hf + 1) * 4, :]),
                       writes=[("RW", hf, 0), ("RW", hf, 1)])
            dump("QT", QT, [128, 4, 2048], [("QT", a, b) for a in range(4) for b in range(4)])
            dump("KT", KT, [128, 4, 2048], [("KT", a, b) for a in range(4) for b in range(4)])
            dump("V", V, [128, 16, 4, 129], [("V", t) for t in range(NT)])
            dump("expB", expB, [128, 4, 1152], [("expB", h) for h in range(4)])
            dump("gqT", gqT, [128, 2, 2048], [("gqT", t) for t in range(NT)])
            dump("gr_s", gr_s, [128, 16, 512], [("gr_s", t) for t in range(NT)])
            dump("G33", G33[0:33, :], [33, 2048], ["G33"])

            if stop == 'P' and l == nl - 1:
                raise _Stop()
            steps = [(h, r, j) for h in range(4) for r in range(4) for j in range(16)]

            def acc_ap(m, u):
                idx = m * 4 + u
                return PS[:, 4 + idx // 3, (idx % 3) * 160:(idx % 3) * 160 + 129]

            def acc_keys(m, u):
                return psk(4 + (m * 4 + u) // 3)

            Eb3 = view(R_X.base + 61440, 2048, BF16, "p (m n) -> p m n", m=2)
            EbL = [Eb[:, 0, :, :], Eb[:, 1, :, :], Eb3]

            def d_scores(i):
                h, r, j = steps[i]
                d = j - 4 * r
                mixed = (-1 <= d <= 4)
                sb = i % 2
                eb = i % 3
                E = EbL[eb]
                for m in range(2):
                    bank = sb * 2 + m
                    sc.op("pe", lambda e, h=h, r=r, j=j, m=m, bank=bank: e.matmul(
                        PS[:, bank, :], lhsT=KT[64 * m:64 * m + 64, h, j * 128:(j + 1) * 128],
                        rhs=QT[64 * m:64 * m + 64, h, r * 512:(r + 1) * 512], start=True, stop=True),
                        reads=[("KT", h, j // 4), ("QT", h, r)], writes=psk(bank))
                pk2 = psk(sb * 2) + psk(sb * 2 + 1)
                ek = [("E", eb, 0), ("E", eb, 1)]
                if mixed:
                    c0 = (4 - d) * 128
                    sc.op("act", lambda e, sb=sb, E=E: e.activation(out=E, in_=PS[:, sb * 2:sb * 2 + 2, :], func=AF.Exp),
                          reads=pk2, writes=ek)
                    for m in range(2):
                        sc.op("dve", lambda e, E=E, m=m, h=h, c0=c0: e.tensor_tensor(out=E[:, m, :], in0=E[:, m, :], in1=expB[:, h, c0:c0 + 512], op=ALU.mult),
                              reads=[("E", eb, m), ("expB", h)], writes=[("E", eb, m)])
                else:
                    side = 0 if d < 0 else 1
                    sc.op("act", lambda e, sb=sb, E=E, side=side, h=h: e.activation(
                        out=E, in_=PS[:, sb * 2:sb * 2 + 2, :], func=AF.Exp, bias=cb[:, side, h:h + 1]),
                        reads=pk2 + [("cb", 0), ("cb", 1)], writes=ek)

            def d_av(i):
                h, r, j = steps[i]
                eb = i % 3
                E = EbL[eb]
                for m in range(2):
                    for u in range(4):
                        sc.op("pe", lambda e, h=h, j=j, m=m, u=u, E=E: e.matmul(
                            acc_ap(m, u), lhsT=E[:, m, u * 128:(u + 1) * 128], rhs=V[:, j, h, 0:129],
                            start=(j == 0 and (m * 4 + u) % 3 == 0), stop=(j == 15), skip_group_check=True),
                            reads=[("E", eb, m), ("V", j)], writes=acc_keys(m, u))

            sm = dsm[0]
            ka = ["accS", ("silu", 0), ("silu", 1)]

            def d_final(h, r):
                sc.op("dve", lambda e: e.tensor_copy(out=accS[:, 0:3, :], in_=PS[:, 4, 0:480].rearrange("p (a n) -> p a n", a=3)[:, :, 0:129]),
                      reads=psk(4), writes=ka)
                sc.op("dve", lambda e: e.tensor_copy(out=accS[:, 3:6, :], in_=PS[:, 5, 0:480].rearrange("p (a n) -> p a n", a=3)[:, :, 0:129]),
                      reads=psk(5), writes=ka)
                sc.op("dve", lambda e: e.tensor_copy(out=accS[:, 6:8, :], in_=PS[:, 6, 0:320].rearrange("p (a n) -> p a n", a=2)[:, :, 0:129]),
                      reads=psk(6), writes=ka)

            def d_final2(h, r):
                sc.op("dve", lambda e: e.reciprocal(out=sm[:, 0:8], in_=accS[:, :, 128]), reads=ka[:1], writes=["dsm"])
                sc.op("dve", lambda e: e.tensor_scalar(out=sm[:, 4:8], in0=sm[:, 4:8], scalar1=neg_lam, scalar2=None, op0=ALU.mult),
                      reads=["dsm", "neglam"], writes=["dsm"])
                sc.op("dve", lambda e: e.memset(sm[:, 8:12], 0.0), writes=["dss"])

            def d_final_u(u):
                if True:
                    sc.op("dve", lambda e, u=u: e.tensor_scalar(out=accS[:, u, 0:128], in0=accS[:, u, 0:128], scalar1=sm[:, u:u + 1], scalar2=None, op0=ALU.mult),
                          reads=["dsm"] + ka[:1], writes=ka[:1])
                    sc.op("dve", lambda e, u=u: e.scalar_tensor_tensor(out=accS[:, u, 0:128], in0=accS[:, 4 + u, 0:128], scalar=sm[:, 4 + u:5 + u],
                                                                       in1=accS[:, u, 0:128], op0=ALU.mult, op1=ALU.add),
                          reads=["dsm"] + ka[:1], writes=ka[:1])
                    sc.op("dve", lambda e, u=u: e.scalar_tensor_tensor(out=accS[:, 4 + u, 0:128], in0=accS[:, u, 0:128], scalar=1.0, in1=accS[:, u, 0:128],
                                                                       op0=ALU.mult, op1=ALU.mult, accum_out=sm[:, 8 + u:9 + u]),
                          reads=ka[:1], writes=ka[:1] + ["dss"])

            def d_final_b(h, r):
                sc.op("act", lambda e: e.activation(out=sm[:, 12:16], in_=sm[:, 8:12], func=AF.Ln, bias=1e-5, scale=1.0 / 128), reads=["dss"], writes=["drs"])
                sc.op("act", lambda e: e.activation(out=sm[:, 12:16], in_=sm[:, 12:16], func=AF.Exp, scale=-0.5), reads=["drs"], writes=["drs"])
                for u in range(4):
                    sc.op("dve", lambda e, u=u: e.scalar_tensor_tensor(out=d_y[:, u, :], in0=accS[:, u, 0:128], scalar=sm[:, 12 + u:13 + u], in1=wd_t,
                                                                       op0=ALU.mult, op1=ALU.mult),
                          reads=ka[:1] + ["drs", "wd"], writes=[("dy", u)])

            def d_final_pe(h, r):
                for u in range(4):
                    sc.op("pe", lambda e, u=u: e.transpose(out=PSb[:, 7, u * 128:(u + 1) * 128], in_=d_y[:, u, :], identity=identb),
                          reads=[("dy", u), "identb"], writes=psk(7))
                sc.op("dve", lambda e, h=h, r=r: e.tensor_copy(out=XT[:, h, r * 512:(r + 1) * 512], in_=PSb[:, 7, 0:512]),
                      reads=psk(7), writes=[("XT", 4 * r + i) for i in range(4)])

            pend = []
            pend_b = []
            pend_u = []
            d_scores(0)
            d_scores(1)
            for i in range(len(steps)):
                h, r, j = steps[i]
                if j == 15:
                    d_av(i)
                    d_final(h, r)
                    if i + 2 < len(steps):
                        d_scores(i + 2)
                    d_final2(h, r)
                else:
                    if i + 2 < len(steps):
                        d_scores(i + 2)
                    d_av(i)
                if j == 15:
                    pend.append((h, r))
                    pend_b.append((h, r))
                    pend_u.extend([0, 1, 2, 3])
                    d_final_u(pend_u.pop(0))
                elif pend_u:
                    d_final_u(pend_u.pop(0))
                elif j == 4 and pend_b:
                    d_final_b(*pend_b.pop(0))
                elif j == 7 and pend:
                    d_final_pe(*pend.pop(0))
            while pend_u:
                d_final_u(pend_u.pop(0))
            while pend_b:
                d_final_b(*pend_b.pop(0))
            while pend:
                d_final_pe(*pend.pop(0))
            dump("mixT_d", XT, [128, 8, 2048], [("XT", t) for t in range(NT)])

            if stop == 'D' and l == nl - 1:
                raise _Stop()
            sc.barrier()
            R_D.reset()
            qf = R_D.get(8192, BF16, "p (c n) -> p c n", c=2)
            kf = R_D.get(8192, BF16, "p (c n) -> p c n", c=2)
            kd_f = R_D.get(8192, BF16, "p (t n) -> p t n", t=16)
            Sbf = R_D.get(16384, BF16, "p (d q t e) -> p d q t e", d=2, q=2, t=16)
            stm2 = [R_D.get(4096, F32, "p (d q n) -> p d q n", d=2, q=2) for _ in range(2)]
            _p0 = R_D.pos
            sp_ = [R_D.get(2048, F32) for _ in range(2)]
            _p1 = R_D.pos
            ebt = [R_D.get(2 * 2 * 129 * 4, F32, "p (d q n) -> p d q n", d=2, q=2) for _ in range(2)]
            _p2 = R_D.pos
            enbt = [R_D.get(2 * 2 * 128 * 4, F32, "p (d q n) -> p d q n", d=2, q=2) for _ in range(2)]
            erem = [R_D.get(2048, F32) for _ in range(2)]
            dS = [R_D.get(1024, F32) for _ in range(4)]
            _pend = R_D.pos
            Am = [R_X.get(4 * 2 * 128 * 2, BF16, "p (h d n) -> p h d n", h=4, d=2) for _ in range(2)]
            R_D.pos = _p2
            g_y = [R_D.get(1024, BF16) for _ in range(2)]
            g_junk = R_D.get(512, F32)
            R_D.pos = _pend
            qb, kb, kd_b = gqT, gkT, gk_tok
            maskf = Uf[:, 0:128]
            maskb = Usf

            def tl(t):
                return slice(t * 128, (t + 1) * 128)

            def prep_A(t):
                b = t % 2
                sp = sp_[b]
                zb = 0 if b == 0 else 7
                sc.op("pe", lambda e: e.matmul(PS[:, zb, :], lhsT=G33[0:33, tl(t)], rhs=Wg[0:33, :], start=True, stop=True),
                      reads=["G33", "Wg"], writes=psk(zb))
                sc.op("act", lambda e: e.activation(out=sp, in_=PS[:, zb, :], func=AF.Exp, scale=-1.0), reads=psk(zb), writes=[("sp", b)])
                sc.op("act", lambda e: e.activation(out=sp, in_=sp, func=AF.Ln, bias=1.0, scale=1.0), reads=[("sp", b)], writes=[("sp", b)])

            prep_A(0)
            for t in range(NT):
                b = t % 2
                sp = sp_[b]
                if t + 1 < NT:
                    prep_A(t + 1)
                sc.op("pe", lambda e, sp=sp: e.matmul(PS[:, 1, 0:256], lhsT=Usf, rhs=sp[:, 0:256], start=True, stop=True), reads=[("sp", b), "Usf", "Usb"], writes=psk(1))
                sc.op("pe", lambda e, sp=sp: e.matmul(PS[:, 1, 256:512], lhsT=Usb, rhs=sp[:, 256:512], start=True, stop=True), reads=[("sp", b), "Usf", "Usb"], writes=psk(1))
                sc.op("act", lambda e, b=b: e.activation(out=erem[b], in_=PS[:, 1, :], func=AF.Exp, scale=-1.0 / 16), reads=psk(1), writes=[("erem", b)])
                sc.op("dve", lambda e, t=t, b=b: e.tensor_tensor(out=kd_f[:, t, :], in0=gk_tok[:, t, :], in1=erem[b][:, 0:256], op=ALU.mult),
                      reads=[("gk_tok", t), ("erem", b)], writes=[("kd_f", t)])
                sc.op("dve", lambda e, t=t, b=b: e.tensor_tensor(out=kd_b[:, t, :], in0=gk_tok[:, t, :], in1=erem[b][:, 256:512], op=ALU.mult),
                      reads=[("gk_tok", t), ("erem", b), ("kd_f", t)], writes=[("gk_tok", t)])
                for d in range(2):
                    U = Uf if d == 0 else Ub
                    for q in range(2):
                        sc.op("pe", lambda e, sp=sp, d=d, q=q, U=U: e.matmul(PS[:, 2 + d, q * 160:q * 160 + 129],
                                                                            lhsT=sp[:, d * 256 + q * 128:d * 256 + (q + 1) * 128], rhs=U, start=True, stop=True),
                              reads=[("sp", b), "Uf", "Ub"], writes=psk(2 + d))
                src4 = PS[:, 2:4, 0:320].rearrange("p a (q n) -> p a q n", q=2)
                sc.op("act", lambda e, b=b, src4=src4: e.activation(out=ebt[b], in_=src4[:, :, :, 0:129], func=AF.Exp, scale=-1.0 / 16),
                      reads=psk(2) + psk(3), writes=[("eb", b, 0), ("eb", b, 1)])
                sc.op("act", lambda e, b=b, src4=src4: e.activation(out=enbt[b], in_=src4[:, :, :, 0:128], func=AF.Exp, scale=1.0 / 16),
                      reads=psk(2) + psk(3), writes=[("enb", b, 0), ("enb", b, 1)])
                sc.op("dve", lambda e, t=t, b=b: e.tensor_tensor(out=qf[:, :, tl(t)], in0=gqT[:, :, tl(t)], in1=ebt[b][:, 0, :, 0:128], op=ALU.mult),
                      reads=[("gqT", t), ("eb", b, 0)], writes=[("qf", t)])
                sc.op("dve", lambda e, t=t, b=b: e.tensor_tensor(out=kf[:, :, tl(t)], in0=gkT[:, :, tl(t)], in1=enbt[b][:, 0, :, :], op=ALU.mult),
                      reads=[("gkT", t), ("enb", b, 0)], writes=[("kf", t)])
                sc.op("dve", lambda e, t=t, b=b: e.tensor_tensor(out=qb[:, :, tl(t)], in0=gqT[:, :, tl(t)], in1=ebt[b][:, 1, :, 0:128], op=ALU.mult),
                      reads=[("gqT", t), ("eb", b, 1), ("qf", t)], writes=[("gqT", t)])
                sc.op("dve", lambda e, t=t, b=b: e.tensor_tensor(out=kb[:, :, tl(t)], in0=gkT[:, :, tl(t)], in1=enbt[b][:, 1, :, :], op=ALU.mult),
                      reads=[("gkT", t), ("enb", b, 1), ("kf", t)], writes=[("gkT", t)])
                sc.op("dve", lambda e, t=t, b=b: e.tensor_copy(out=decs[:, :, :, t:t + 1], in_=ebt[b][:, :, :, 128:129]),
                      reads=[("eb", b, 0), ("eb", b, 1)], writes=["decs"])
            dump("qf", qf, [128, 2, 2048], [("qf", t) for t in range(NT)])
            dump("kd_f", kd_f, [128, 16, 256], [("kd_f", t) for t in range(NT)])
            dump("decs", decs, [128, 2, 2, 16], ["decs"])

            if stop == 'G1' and l == nl - 1:
                raise _Stop()
            sc.op("dve", lambda e: e.memset(stm2[0], 0.0), writes=[("stm", 0, d, q) for d in range(2) for q in range(2)])
            chains = [(d, q) for d in range(2) for q in range(2)]
            par = {c: 0 for c in chains}
            for i in range(NT):
                todo = []
                for ci, (d, q) in enumerate(chains):
                    t = i if d == 0 else NT - 1 - i
                    cur = par[(d, q)]
                    if i > 0:
                        sc.op("act", lambda e, d=d, q=q, t=t, cur=cur: e.activation(out=Sbf[0:64, d, q, t, :], in_=stm2[cur][0:64, d, q, 0:128], func=AF.Copy),
                              reads=[("stm", cur, d, q)], writes=[("Sbf", d, q, t)])
                        sc.op("dve", lambda e, d=d, q=q, t=t, cur=cur: e.tensor_copy(out=Sbf[64:128, d, q, t, :], in_=stm2[cur][64:128, d, q, 128:256]),
                              reads=[("stm", cur, d, q)], writes=[("Sbf", d, q, t)])
                    if i == NT - 1:
                        continue
                    kd = kd_f if d == 0 else kd_b
                    kkey = "kd_f" if d == 0 else "gk_tok"
                    pslot = ci % 2
                    pk = psk(4 + pslot)
                    sc.op("pe", lambda e, kd=kd, t=t, q=q, pslot=pslot: e.matmul(PS[:, 4 + pslot, 0:256], lhsT=kd[:, t, q * 128:(q + 1) * 128],
                                                                                rhs=gv[:, t, q * 256:(q + 1) * 256], start=True, stop=True),
                          reads=[(kkey, t), ("gv", t)], writes=pk)
                    if os.environ.get("GSKIP") != "evac":
                        sc.op("act", lambda e, ci=ci, pslot=pslot: e.activation(out=dS[ci], in_=PS[:, 4 + pslot, 0:256], func=AF.Copy),
                              reads=pk, writes=[("dS", ci)])
                    todo.append((ci, d, q, t, cur))
                for (ci, d, q, t, cur) in todo:
                    if os.environ.get("GSKIP") == "upd":
                        par[(d, q)] = 1 - cur
                        continue
                    sc.op("dve", lambda e, ci=ci, d=d, q=q, t=t, cur=cur: e.scalar_tensor_tensor(
                        out=stm2[1 - cur][:, d, q, :], in0=stm2[cur][:, d, q, :], scalar=decs[:, d, q, t:t + 1], in1=dS[ci],
                        op0=ALU.mult, op1=ALU.add), reads=[("stm", cur, d, q), "decs", ("dS", ci)], writes=[("stm", 1 - cur, d, q)])
                    par[(d, q)] = 1 - cur
            dump("Sbf", Sbf, [128, 2, 2, 16, 128], [("Sbf", d, q, t) for d in range(2) for q in range(2) for t in range(NT)])
            if stop == 'G2' and l == nl - 1:
                raise _Stop()

            def g_A(t):
                b = t % 2
                A = Am[b]
                for half in range(2):
                    sbank = (4 + half) if os.environ.get('GBANK') else (2 * b + half)
                    items = []
                    for sq in range(4):
                        h, d = half + 2 * (sq // 2), sq % 2
                        q = h // 2
                        base = (h % 2) * 64
                        kk = kf if d == 0 else kb
                        qq = qf if d == 0 else qb
                        kkey = ("kf", t) if d == 0 else ("gkT", t)
                        qkey = ("qf", t) if d == 0 else ("gqT", t)
                        sc.op("pe", lambda e, kk=kk, qq=qq, q=q, base=base, sbank=sbank, sq=sq: e.matmul(
                            PS[:, sbank, sq * 128:(sq + 1) * 128], lhsT=kk[base:base + 64, q, tl(t)], rhs=qq[base:base + 64, q, tl(t)], start=True, stop=True),
                            reads=[kkey, qkey], writes=psk(sbank))
                        items.append((sq, h, d))
                        if os.environ.get("GOLD"):
                            mk = maskf if d == 0 else maskb
                            sc.op("dve", lambda e, A=A, h=h, d=d, sbank=sbank, sq=sq, mk=mk: e.tensor_tensor(
                                out=A[:, h, d, :], in0=PS[:, sbank, sq * 128:(sq + 1) * 128], in1=mk, op=ALU.mult),
                                reads=psk(sbank) + ["Uf", "Usf"], writes=[("A", b, h, d), ("sp", b)])
                    if os.environ.get("GOLD"):
                        continue
                    for (sq, h, d) in items:
                        mk = maskf if d == 0 else maskb
                        sc.op("dve", lambda e, A=A, h=h, d=d, sbank=sbank, sq=sq, mk=mk: e.tensor_tensor(
                            out=A[:, h, d, :], in0=PS[:, sbank, sq * 128:(sq + 1) * 128], in1=mk, op=ALU.mult),
                            reads=psk(sbank) + ["Uf", "Usf"], writes=[("A", b, h, d), ("sp", b)])

            def g_B(t):
                b = t % 2
                A = Am[b]
                obank = (0 + b) if os.environ.get('GBANK') else (4 + b)
                for h in range(4):
                    q = h // 2
                    base = (h % 2) * 64
                    oh_ = PS[:, obank, h * 128:(h + 1) * 128]
                    ok = psk(obank)
                    inter_f = t > 0
                    inter_b = t < NT - 1
                    sc.op("pe", lambda e, A=A, h=h, oh_=oh_: e.matmul(oh_, lhsT=A[:, h, 0, :], rhs=gv[:, t, h * 128:(h + 1) * 128], start=True, stop=False),
                          reads=[("A", b, h, 0), ("gv", t)], writes=ok)
                    sc.op("pe", lambda e, A=A, h=h, oh_=oh_, fin=(not inter_f and not inter_b): e.matmul(
                        oh_, lhsT=A[:, h, 1, :], rhs=gv[:, t, h * 128:(h + 1) * 128], start=False, stop=fin),
                        reads=[("A", b, h, 1), ("gv", t)], writes=ok)
                    if inter_f:
                        sc.op("pe", lambda e, q=q, base=base, oh_=oh_, fin=(not inter_b): e.matmul(
                            oh_, lhsT=qf[base:base + 64, q, tl(t)], rhs=Sbf[base:base + 64, 0, q, t, :], start=False, stop=fin),
                            reads=[("qf", t), ("Sbf", 0, q, t)], writes=ok)
                    if inter_b:
                        sc.op("pe", lambda e, q=q, base=base, oh_=oh_: e.matmul(
                            oh_, lhsT=qb[base:base + 64, q, tl(t)], rhs=Sbf[base:base + 64, 1, q, t, :], start=False, stop=True),
                            reads=[("gqT", t), ("Sbf", 1, q, t)], writes=ok)

            def g_norm(t):
                b = t % 2
                sm = gsm[b]
                obank = (0 + b) if os.environ.get('GBANK') else (4 + b)
                okall = psk(obank)
                for h in range(4):
                    sc.op("act", lambda e, h=h, sm=sm: e.activation(out=g_junk, in_=PS[:, obank, h * 128:(h + 1) * 128], func=AF.Square, accum_out=sm[:, h:h + 1]),
                          reads=okall, writes=["gjunk", ("gss", b), ("enb", 1, 0), ("enb", 1, 1)])
                sc.op("act", lambda e, sm=sm: e.activation(out=sm[:, 4:8], in_=sm[:, 0:4], func=AF.Sqrt, bias=1e-5, scale=1.0 / 128),
                      reads=[("gss", b)], writes=[("grs", b)])
                sc.op("dve", lambda e, sm=sm: e.reciprocal(out=sm[:, 4:8], in_=sm[:, 4:8]), reads=[("grs", b)], writes=[("grs", b)])
                for h in range(4):
                    sc.op("dve", lambda e, h=h, sm=sm, b=b: e.scalar_tensor_tensor(
                        out=g_y[b][:, h * 128:(h + 1) * 128], in0=PS[:, obank, h * 128:(h + 1) * 128], scalar=sm[:, 4 + h:5 + h],
                        in1=gr_s[:, t, h * 128:(h + 1) * 128], op0=ALU.mult, op1=ALU.mult),
                        reads=psk(obank) + [("grs", b), ("gr_s", t)], writes=[("gy", b), ("enb", 0, 0), ("enb", 0, 1)])

            def g_tr(t):
                b = t % 2
                tk = psk(6 + b)
                for h in range(4):
                    sc.op("pe", lambda e, h=h, b=b: e.transpose(out=PSb[:, 6 + b, h * 128:(h + 1) * 128], in_=g_y[b][:, h * 128:(h + 1) * 128], identity=identb),
                          reads=[("gy", b), "identb"], writes=tk)
                sc.op("act", lambda e, b=b: e.activation(out=XT[:, 4:8, tl(t)], in_=PSb[:, 6 + b, 0:512].rearrange("p (c n) -> p c n", c=4), func=AF.Copy),
                      reads=tk, writes=[("XT", t)])

            g_A(0)
            for t in range(NT):
                if t + 1 < NT:
                    g_A(t + 1)
                if os.environ.get("GSKIP") == "B":
                    continue
                g_B(t)
                if os.environ.get("GSKIP") == "norm":
                    continue
                g_norm(t)
                if os.environ.get("GSKIP") == "tr":
                    continue
                if t > 0:
                    g_tr(t - 1)
            if not os.environ.get("GSKIP"):
                g_tr(NT - 1)
            dump("mixT", XT, [128, 8, 2048], [("XT", t) for t in range(NT)])

            if stop == 'G' and l == nl - 1:
                raise _Stop()
            sc.barrier()
            load_ln_params(ln1g_d[l], ln1b_d[l])
            R_D.reset()
            W1B = [R_D.get(8192, BF16, "p (c n) -> p c n", c=8) for _ in range(2)]
            W2B = [R_D.get(8192, BF16, "p (c n) -> p c n", c=4) for _ in range(2)]
            hT = R_D.get(16384, BF16, "p (c n) -> p c n", c=4)
            relu_t = [R_D.get(2048, F32) for _ in range(2)]

            def load_ffn_block(fb, buf):
                s1 = w1_d[l, :, fb * 512:(fb + 1) * 512].rearrange("(c p) n -> p c n", p=128)
                s2 = w2_d[l, fb * 512:(fb + 1) * 512, :].rearrange("(c p) n -> p c n", p=128)
                for hf in range(2):
                    sc.dma("pool", lambda e, hf=hf: e.dma_start(out=W1B[buf][:, hf * 4:(hf + 1) * 4, :], in_=s1[:, hf * 4:(hf + 1) * 4, :]),
                           writes=[("W1B", buf, hf)])
                for hf in range(2):
                    sc.dma("pool", lambda e, hf=hf: e.dma_start(out=W2B[buf][:, hf * 2:(hf + 1) * 2, :], in_=s2[:, hf * 2:(hf + 1) * 2, :]),
                           writes=[("W2B", buf, hf)])

            load_ffn_block(0, 0)
            load_ffn_block(1, 1)
            for t in range(NT):
                sc.dma("sp", lambda e, t=t: e.dma_start(out=X[:, t, :], in_=xs_d[t * 128:(t + 1) * 128, :]), reads=[("xsd", t)], writes=[("X", t)])
            def o_mm(t):
                yb = (t % 3) * 2
                for hf in range(2):
                    for c in range(8):
                        sc.op("pe", lambda e, c=c, hf=hf, t=t, yb=yb: e.matmul(PS[:, yb + hf, :], lhsT=XT[:, c, tl(t)], rhs=WO[:, c, hf * 512:(hf + 1) * 512],
                                                                              start=(c == 0), stop=(c == 7)),
                              reads=[("XT", t), ("RW", 0, 0), ("RW", 0, 1), ("RW", 1, 0), ("RW", 1, 1)], writes=psk(yb + hf))

            def o_ln(t):
                yb = (t % 3) * 2
                sc.op("dve", lambda e, t=t, yb=yb: e.scalar_tensor_tensor(out=X[:, t, :], in0=X[:, t, :], scalar=ALPHA,
                                                                          in1=PS[:, yb:yb + 2, :].rearrange("p a n -> p (a n)"), op0=ALU.mult, op1=ALU.add),
                      reads=[("X", t)] + psk(yb) + psk(yb + 1), writes=[("X", t)])
                ln_a(t)

            for t0 in range(3):
                o_mm(t0)
                o_ln(t0)
            for t in range(NT):
                if t + 3 < NT:
                    o_mm(t + 3)
                ln_b1(t, None)
                if t + 3 < NT:
                    o_ln(t + 3)
                ln_b2(t, None)
            dump("x1T", XT, [128, 8, 2048], [("XT", t) for t in range(NT)])

            if stop == 'O' and l == nl - 1:
                raise _Stop()
            if not last:
                load_win_block(l + 1, 0, 0)
                load_win_block(l + 1, 1, 1)
            hrr = [0]
            for fb in range(8):
                buf = fb % 2
                for r in range(4):
                    for fc in range(4):
                        bank = 4 + hrr[0] % 3
                        rb = hrr[0] % 2
                        hrr[0] += 1
                        for c in range(8):
                            sc.op("pe", lambda e, c=c, fc=fc, r=r, bank=bank, buf=buf: e.matmul(
                                PS[:, bank, :], lhsT=W1B[buf][:, c, fc * 128:(fc + 1) * 128], rhs=XT[:, c, r * 512:(r + 1) * 512],
                                start=(c == 0), stop=(c == 7)), reads=[("W1B", buf, 0), ("W1B", buf, 1)] + xt_all[r * 4:(r + 1) * 4], writes=psk(bank))
                        fcol = fb * 4 + fc
                        sc.op("act", lambda e, bank=bank, rb=rb, fcol=fcol: e.activation(out=relu_t[rb], in_=PS[:, bank, :], func=AF.Relu,
                                                                                         bias=b1c[:, fcol:fcol + 1], scale=1.0),
                              reads=psk(bank) + ["b1c"], writes=[("relu", rb)])
                        sc.op("dve", lambda e, rb=rb, fc=fc, r=r: e.tensor_tensor(out=hT[:, fc, r * 512:(r + 1) * 512], in0=relu_t[rb], in1=relu_t[rb], op=ALU.mult),
                              reads=[("relu", rb)], writes=[("hT", fc, r)])
                for t in range(NT):
                    yb = (t % 2) * 2
                    for hf in range(2):
                        for fc in range(4):
                            sc.op("pe", lambda e, fc=fc, hf=hf, t=t, yb=yb, buf=buf: e.matmul(
                                PS[:, yb + hf, :], lhsT=hT[:, fc, tl(t)], rhs=W2B[buf][:, fc, hf * 512:(hf + 1) * 512],
                                start=(fc == 0), stop=(fc == 3)), reads=[("hT", fc, t // 4), ("W2B", buf, 0), ("W2B", buf, 1)], writes=psk(yb + hf))
                    if fb == 0:
                        sc.op("dve", lambda e, t=t, yb=yb: e.scalar_tensor_tensor(out=X[:, t, :], in0=X[:, t, :], scalar=ALPHA,
                                                                                  in1=PS[:, yb:yb + 2, :].rearrange("p a n -> p (a n)"), op0=ALU.mult, op1=ALU.add),
                              reads=[("X", t)] + psk(yb) + psk(yb + 1), writes=[("X", t)])
                        sc.op("pool", lambda e, t=t: e.tensor_tensor(out=X[:, t, :], in0=X[:, t, :], in1=b2t, op=ALU.add), reads=[("X", t), "b2t"], writes=[("X", t)])
                    else:
                        sc.op("dve", lambda e, t=t, yb=yb: e.tensor_tensor(out=X[:, t, :], in0=X[:, t, :], in1=PS[:, yb:yb + 2, :].rearrange("p a n -> p (a n)"), op=ALU.add),
                              reads=[("X", t)] + psk(yb) + psk(yb + 1), writes=[("X", t)])
                if fb + 2 < 8:
                    load_ffn_block(fb + 2, buf)
            if stop == 'F' and l == nl - 1:
                raise _Stop()
            load_ln_params(ln2g_d[l], ln2b_d[l])
            ln_all(out_d if last else xs_d)
            sc.barrier()


        for _l in range(nl):
            do_layer(_l)
    except _Stop:
        pass
    out_dmas = [o for o in sc.ops if o.is_dma and o.dkey in [("xs", i) for i in range(4)]]
    fin = {}
    for o in out_dmas:
        fin[o.dkey] = o
    finals = list(fin.values()) + list(dbg_out.values())
    sc.emit(final_wait_ops=finals)
    es.close()
    return nc, sc


_CONST = None


def kernel(**inputs):
    global _CONST
    if _CONST is None:
        _CONST = _constants()
    nc, _ = build(2)
    x = np.ascontiguousarray(inputs["x"], dtype=np.float32)
    shared = {k: np.ascontiguousarray(v, dtype=np.float32) for k, v in inputs.items() if k != "x"}
    shared.update(_CONST)
    in_maps = []
    for b in range(8):
        m = dict(shared)
        m["x"] = x[b]
        in_maps.append(m)
    res = run_bass_kernel_spmd(nc, in_maps, core_ids=list(range(8)))
    return np.stack([r["out"] for r in res.results], axis=0).astype(np.float32)
```

```python
import math
import os
from contextlib import ExitStack

import numpy as np
import concourse.bass as bass
import concourse.mybir as mybir
from concourse.bass_utils import run_bass_kernel_spmd

F32 = mybir.dt.float32
BF16 = mybir.dt.bfloat16
AF = mybir.ActivationFunctionType
ALU = mybir.AluOpType

S = 2048
D = 1024
DIN = 3104
DFF = 4096
NT = 16
ALPHA = (2.0 * 2) ** 0.25
ENGS = ("pe", "act", "dve", "pool", "sp")
EPOCH = 30000


class _Res:
    __slots__ = ("last_w", "readers")

    def __init__(self):
        self.last_w = None
        self.readers = []


class _Op:
    __slots__ = ("eng", "fn", "deps", "signal", "tok", "is_dma", "dkey")

    def __init__(self, eng, fn, is_dma, dkey):
        self.eng = eng
        self.fn = fn
        self.deps = []
        self.signal = False
        self.tok = None
        self.is_dma = is_dma
        self.dkey = dkey


class Sched:
    def __init__(self, nc):
        self.nc = nc
        self.ops = []
        self.res = {}
        self.pending = {e: [] for e in ENGS}

    def _r(self, key):
        x = self.res.get(key)
        if x is None:
            x = self.res[key] = _Res()
        return x

    def _add(self, op, reads, writes):
        deps = set()
        for k in reads:
            rs = self._r(k)
            if rs.last_w is not None:
                deps.add(rs.last_w)
        for k in writes:
            rs = self._r(k)
            if rs.last_w is not None:
                deps.add(rs.last_w)
            deps.update(rs.readers)
        for k in reads:
            self._r(k).readers.append(op)
        for k in writes:
            rs = self._r(k)
            rs.last_w = op
            rs.readers = []
        if self.pending[op.eng]:
            deps.update(self.pending[op.eng])
            self.pending[op.eng] = []
        deps.discard(op)
        op.deps = list(deps)
        self.ops.append(op)
        return op

    def op(self, eng, fn, reads=(), writes=()):
        return self._add(_Op(eng, fn, False, None), reads, writes)

    def dma(self, eng, fn, dkey=None, reads=(), writes=()):
        if dkey is None:
            dkey = ("w", writes[0])
        return self._add(_Op(eng, fn, True, dkey), reads, writes)

    def barrier(self):
        last = {}
        for o in self.ops:
            last[(o.eng, o.dkey) if o.is_dma else o.eng] = o
        b = list(last.values())
        self.pending = {e: list(b) for e in ENGS}

    def emit(self, final_wait_ops=()):
        nc = self.nc
        ops = self.ops
        for o in ops:
            for d in o.deps:
                if d.is_dma:
                    d.signal = True
                elif d.eng == "pe" and o.eng == "pe" and not o.is_dma:
                    continue
                else:
                    d.signal = True
        with ExitStack() as es:
            eng_sems = {e: [] for e in ENGS}
            cnt = {e: 0 for e in ENGS}
            dma_sems = {}
            dma_cnt = {}
            for o in ops:
                if o.is_dma:
                    if o.dkey not in dma_sems:
                        dma_sems[o.dkey] = es.enter_context(nc.semaphore("d%d" % len(dma_sems)))
                        dma_cnt[o.dkey] = 0
                    dma_cnt[o.dkey] += 16
                    o.tok = (dma_sems[o.dkey], dma_cnt[o.dkey])
                elif o.signal:
                    ep = cnt[o.eng] // EPOCH
                    if ep >= len(eng_sems[o.eng]):
                        eng_sems[o.eng].append(es.enter_context(nc.semaphore("e_%s_%d" % (o.eng, ep))))
                    cnt[o.eng] += 1
                    o.tok = (eng_sems[o.eng][ep], cnt[o.eng] - ep * EPOCH)
            per_eng = {e: [o for o in ops if o.eng == e] for e in ENGS}
            self.stats = {e: len(per_eng[e]) for e in ENGS}
            self.stats["sems"] = sum(len(v) for v in eng_sems.values()) + len(dma_sems)

            def run(e, eng):
                waited = {}
                for o in per_eng[e]:
                    need = {}
                    for d in o.deps:
                        if d.tok is None:
                            continue
                        if (not d.is_dma) and d.eng == "pe" and e == "pe" and not o.is_dma:
                            continue
                        s, v = d.tok
                        k = id(s)
                        if waited.get(k, 0) >= v:
                            continue
                        if k not in need or need[k][1] < v:
                            need[k] = (s, v)
                    for k, (s, v) in need.items():
                        eng.wait_ge(s, v)
                        waited[k] = v
                    ins = o.fn(eng)
                    if o.tok is not None:
                        ins.then_inc(o.tok[0], 16 if o.is_dma else 1)
                if e == "sp":
                    for o in final_wait_ops:
                        s, v = o.tok
                        eng.wait_ge(s, v)

            with nc.Block() as block:
                @block.sync
                def _(eng):
                    run("sp", eng)

                @block.tensor
                def _(eng):
                    run("pe", eng)

                @block.scalar
                def _(eng):
                    run("act", eng)

                @block.vector
                def _(eng):
                    run("dve", eng)

                @block.gpsimd
                def _(eng):
                    run("pool", eng)


def _t5_bucket(rel):
    nb = 16
    me = 8
    ret = np.where(rel > 0, nb, 0)
    n = np.abs(rel)
    large = me + (np.log(np.maximum(n, 1).astype(np.float32) / np.float32(me))
                  / np.float32(math.log(128 / me)) * np.float32(nb - me)).astype(np.int32)
    large = np.minimum(large, nb - 1)
    return ret + np.where(n < me, n, large)


MLEN = 1280


def _constants():
    c = {}
    c["c_ident"] = np.eye(128, dtype=np.float32)
    c["c_J"] = np.eye(128, dtype=np.float32)[::-1].copy()
    s = np.arange(128)[:, None]
    t = np.arange(128)[None, :]
    uf = np.zeros((128, 129), np.float32)
    uf[:, :128] = (s <= t)
    uf[:, 128] = 1.0
    ub = np.zeros((128, 129), np.float32)
    ub[:, :128] = (s >= t)
    ub[:, 128] = 1.0
    c["c_uf"] = uf
    c["c_ub"] = ub
    c["c_sf"] = (s > t).astype(np.float32)
    c["c_sb"] = (s < t).astype(np.float32)
    n = np.arange(MLEN)
    bk = _t5_bucket(639 - n)
    oh = np.zeros((32, MLEN), np.float32)
    oh[bk, n] = 1.0
    c["c_onehot"] = oh
    return c


class _Stop(Exception):
    pass


def build(nl=2, dbg=(), stop=None):
    nc = bass.Bass("TRN2", target_bir_lowering=False)

    def din(name, shape):
        return nc.dram_tensor(name, list(shape), F32, kind="ExternalInput").ap()

    x_d = din("x", [S, D])
    lnemb_g = din("ln_emb_g", [D])
    lnemb_b = din("ln_emb_b", [D])
    table_d = din("rel_bias_table", [32, 4])
    w_in_d = din("w_in", [2, D, DIN])
    lq1_d = din("lambda_q1", [2, 64])
    lk1_d = din("lambda_k1", [2, 64])
    lq2_d = din("lambda_q2", [2, 64])
    lk2_d = din("lambda_k2", [2, 64])
    dnw_d = din("diff_norm_w", [2, 128])
    gup_d = din("gla_gate_up", [2, 2, 16, 256])
    gbias_d = din("gla_gate_bias", [2, 2, 256])
    gnw_d = din("gla_norm_w", [2, 128])
    w_o_d = din("w_o", [2, D, D])
    ln1g_d = din("ln1_g", [2, D])
    ln1b_d = din("ln1_b", [2, D])
    w1_d = din("w_ffn1", [2, D, DFF])
    b1_d = din("b_ffn1", [2, DFF])
    w2_d = din("w_ffn2", [2, DFF, D])
    b2_d = din("b_ffn2", [2, D])
    ln2g_d = din("ln2_g", [2, D])
    ln2b_d = din("ln2_b", [2, D])
    c_ident = din("c_ident", [128, 128])
    c_J = din("c_J", [128, 128])
    c_uf = din("c_uf", [128, 129])
    c_ub = din("c_ub", [128, 129])
    c_sf = din("c_sf", [128, 128])
    c_sb = din("c_sb", [128, 128])
    c_onehot = din("c_onehot", [32, MLEN])
    out_d = nc.dram_tensor("out", [S, D], F32, kind="ExternalOutput").ap()
    xs_d = nc.dram_tensor("xs_scratch", [S, D], F32).ap()
    md_t = nc.dram_tensor("md_scratch", [4, MLEN], F32)
    eb_d = nc.dram_tensor("expb_scratch", [128, 4 * 1152], BF16).ap()
    md_d = md_t.ap()
    dbg_out = {}

    sc = Sched(nc)
    es = ExitStack()
    ARENA_BYTES = 207 * 1024
    arena = es.enter_context(nc.sbuf_tensor("arena", [128, ARENA_BYTES // 2], BF16))
    PSb = es.enter_context(nc.psum_tensor("ps", [128, 8, 1024], BF16))[:]
    PS = PSb.bitcast(F32)

    def view(off, nbytes, dt, pattern=None, **kw):
        assert off % 32 == 0, off
        a = arena[:, off // 2:(off + nbytes) // 2]
        if dt is F32:
            a = a.bitcast(F32)
        if pattern:
            a = a.rearrange(pattern, **kw)
        return a

    class Alloc:
        def __init__(self, base, size):
            self.base = base
            self.size = size
            self.pos = 0

        def reset(self):
            self.pos = 0

        def get(self, nbytes, dt, pattern=None, **kw):
            n = (nbytes + 31) // 32 * 32
            assert self.pos + n <= self.size, (self.pos, n, self.size)
            v = view(self.base + self.pos, nbytes, dt, pattern, **kw)
            self.pos += n
            return v

    R_XT = Alloc(0, 32768)
    R_X = Alloc(32768, 65536)
    R_D = Alloc(98304, 70656)
    R_W = Alloc(168960, 16384)
    R_C = Alloc(185344, ARENA_BYTES - 185344)

    XT = R_XT.get(32768, BF16, "p (c n) -> p c n", c=8)
    X = R_X.get(65536, F32, "p (t n) -> p t n", t=NT)
    WB = [R_W.get(8192, BF16, "p (c n) -> p c n", c=8) for _ in range(2)]
    R_W.reset()
    WO = R_W.get(16384, BF16, "p (c n) -> p c n", c=8)

    identb = R_C.get(256, BF16)
    Jb = R_C.get(256, BF16)
    Uf = R_C.get(516, F32)
    Ub = R_C.get(516, F32)
    Usf = R_C.get(512, F32)
    Usb = R_C.get(512, F32)
    gt = R_C.get(4096, F32)
    bt = R_C.get(4096, F32)
    b2t = R_C.get(4096, F32)
    wd_t = R_C.get(512, F32)
    wg_t = R_C.get(512, F32)
    b1c = R_C.get(128, F32)
    cb = R_C.get(32, F32, "p (s h) -> p s h", s=2)
    lamv = R_C.get(4 * 64 * 4, F32, "p (a n) -> p a n", a=4)
    lamp = R_C.get(2 * 64 * 4, F32, "p (a n) -> p a n", a=2)
    lams = R_C.get(32, F32)
    Wg = R_C.get(1024, BF16)
    st_ = [R_C.get(48, F32) for _ in range(2)]
    mv_ = [R_C.get(8, F32) for _ in range(2)]
    rs_ = [R_C.get(4, F32) for _ in range(2)]
    xb_ = [R_C.get(2048, BF16) for _ in range(2)]
    dsm = [R_C.get(64, F32) for _ in range(2)]
    gsm = [R_C.get(64, F32) for _ in range(2)]
    decs = R_C.get(2 * 2 * 16 * 4, F32, "p (d q t) -> p d q t", d=2, q=2)

    def psk(b):
        return [("ps", b, q) for q in range(4)]

    def bc_mid(ap2, n):
        a = ap2.ap
        return bass.AP(ap2.tensor, ap2.offset, [list(a[0]), [0, n], list(a[1])])

    def bc_last(ap2, n):
        a = ap2.ap
        return bass.AP(ap2.tensor, ap2.offset, [list(a[0]), list(a[1]), [0, n]])

    cur_layer = [-1]

    def dump(name, ap, shape, reads):
        nm = "%s@%d" % (name, cur_layer[0])
        if nm in dbg:
            name = nm
        elif name not in dbg or (cur_layer[0] >= 0 and cur_layer[0] != nl - 1):
            return
        t = nc.dram_tensor("dbg_" + name.replace("@", "_"), list(shape), ap.dtype, kind="ExternalOutput").ap()
        dbg_out[name] = sc.dma("sp", lambda e: e.dma_start(out=t, in_=ap), "dbg", reads=reads)

    try:
        sc.dma("pool", lambda e: e.dma_start(out=identb, in_=c_ident), writes=["identb"])
        sc.dma("pool", lambda e: e.dma_start(out=Jb, in_=c_J), writes=["Jb"])
        sc.dma("sp", lambda e: e.dma_start(out=Uf, in_=c_uf), writes=["Uf"])
        sc.dma("sp", lambda e: e.dma_start(out=Ub, in_=c_ub), writes=["Ub"])
        sc.dma("sp", lambda e: e.dma_start(out=Usf, in_=c_sf), writes=["Usf"])
        sc.dma("sp", lambda e: e.dma_start(out=Usb, in_=c_sb), writes=["Usb"])
        for si, row in enumerate((15, 31)):
            sc.dma("sp", lambda e, si=si, row=row: e.dma_start(out=cb[:, si, :], in_=table_d[row, :].partition_broadcast(128)),
                   writes=[("cb", si)])

        R_D.reset()
        tb = R_D.get(16, F32)
        oh = R_D.get(MLEN * 4, F32)
        msb = R_D.get(MLEN * 4, F32)
        sc.dma("sp", lambda e: e.dma_start(out=tb[0:32, :], in_=table_d), writes=["tb"])
        sc.dma("sp", lambda e: e.dma_start(out=oh[0:32, :], in_=c_onehot), writes=["oh"])
        sc.dma("sp", lambda e: e.dma_start(out=gt, in_=lnemb_g.partition_broadcast(128)), writes=["gt"])
        sc.dma("sp", lambda e: e.dma_start(out=bt, in_=lnemb_b.partition_broadcast(128)), writes=["bt"])
        for t in range(NT):
            sc.dma("sp", lambda e, t=t: e.dma_start(out=X[:, t, :], in_=x_d[t * 128:(t + 1) * 128, :]), writes=[("X", t)])
        for ci, (c0, cn) in enumerate(((0, 512), (512, 512), (1024, 256))):
            sc.op("pe", lambda e, ci=ci, c0=c0, cn=cn: e.matmul(PS[0:4, ci, 0:cn], lhsT=tb[0:32, :], rhs=oh[0:32, c0:c0 + cn], start=True, stop=True),
                  reads=["tb", "oh"], writes=psk(ci))
            sc.op("dve", lambda e, ci=ci, c0=c0, cn=cn: e.tensor_copy(out=msb[0:4, c0:c0 + cn], in_=PS[0:4, ci, 0:cn]),
                  reads=psk(ci), writes=["msb"])
        sc.dma("sp", lambda e: e.dma_start(out=md_d, in_=msb[0:4, :]), reads=["msb"], writes=["md"])
        R_D.reset()
        R_D.get(16384, BF16); R_D.get(16384, BF16); R_D.get(16 * 4 * 129 * 2, BF16)
        expB0 = R_D.get(4 * 1152 * 2, BF16, "p (h n) -> p h n", h=4)
        R_D.get(4096, BF16); R_D.get(1024, BF16)
        tmp_revs = [view(R_D.base + 16384 + i * 2304, 2304, BF16) for i in range(4)]
        for h in range(4):
            src = bass.AP(md_t, h * MLEN, [[1, 128], [1, 1152]])
            sc.dma("pool", lambda e, src=src, h=h: e.dma_start(out=tmp_revs[h], in_=src), reads=["md"], writes=[("tmp_rev", h)])
        for h in range(4):
            for ci, (c0, cn) in enumerate(((0, 512), (512, 512), (1024, 128))):
                sc.op("pe", lambda e, ci=ci, c0=c0, cn=cn, h=h: e.matmul(PS[:, ci, 0:cn], lhsT=Jb, rhs=tmp_revs[h][:, c0:c0 + cn], start=True, stop=True),
                      reads=["Jb", ("tmp_rev", h)], writes=psk(ci))
                sc.op("act", lambda e, h=h, ci=ci, c0=c0, cn=cn: e.activation(out=expB0[:, h, c0:c0 + cn], in_=PS[:, ci, 0:cn], func=AF.Exp),
                      reads=psk(ci), writes=[("expB", h)])
        sc.dma("sp", lambda e: e.dma_start(out=eb_d, in_=expB0.rearrange("p h n -> p (h n)")), reads=[("expB", h) for h in range(4)], writes=["eb_d"])

        def ln_a(t):
            Xt = X[:, t, :]
            kx = ("X", t)
            b = t % 2
            st, mv, rs = st_[b], mv_[b], rs_[b]
            sc.op("dve", lambda e: e.bn_stats(out=st[:, 0:6], in_=Xt[:, 0:512]), reads=[kx], writes=[("st", b, 0)])
            sc.op("dve", lambda e: e.bn_stats(out=st[:, 6:12], in_=Xt[:, 512:1024]), reads=[kx], writes=[("st", b, 1)])
            sc.op("dve", lambda e: e.bn_aggr(out=mv, in_=st), reads=[("st", b, 0), ("st", b, 1)], writes=[("mv", b)])
            sc.op("act", lambda e: e.activation(out=rs, in_=mv[:, 1:2], func=AF.Sqrt, bias=1e-5, scale=1.0), reads=[("mv", b)], writes=[("rs", b)])
            sc.op("dve", lambda e: e.reciprocal(out=rs, in_=rs), reads=[("rs", b)], writes=[("rs", b)])
            sc.op("dve", lambda e: e.tensor_scalar(out=Xt, in0=Xt, scalar1=mv[:, 0:1], scalar2=rs, op0=ALU.subtract, op1=ALU.mult),
                  reads=[kx, ("mv", b), ("rs", b)], writes=[kx])
            sc.op("dve", lambda e: e.tensor_tensor(out=Xt, in0=Xt, in1=gt, op=ALU.mult), reads=[kx, "gt"], writes=[kx])
            sc.op("pool", lambda e: e.tensor_tensor(out=Xt, in0=Xt, in1=bt, op=ALU.add), reads=[kx, "bt"], writes=[kx])

        def ln_b1(t, spill_to):
            Xt = X[:, t, :]
            kx = ("X", t)
            b = t % 2
            xb = xb_[b]
            if spill_to is not None:
                sc.dma("sp", lambda e: e.dma_start(out=spill_to[t * 128:(t + 1) * 128, :], in_=Xt), ("xs", t % 4), reads=[kx], writes=[("xsd", t)])
            if spill_to is out_d:
                return
            sc.op("act", lambda e: e.activation(out=xb, in_=Xt, func=AF.Copy), reads=[kx], writes=[("xb", b)])

        def ln_b2(t, spill_to):
            if spill_to is out_d:
                return
            b = t % 2
            xb = xb_[b]
            bank = 6 + b
            for c in range(8):
                sc.op("pe", lambda e, c=c: e.transpose(out=PSb[:, bank, c * 128:(c + 1) * 128], in_=xb[:, c * 128:(c + 1) * 128], identity=identb),
                      reads=[("xb", b), "identb"], writes=psk(bank))
            sc.op("act", lambda e: e.activation(out=XT[:, :, t * 128:(t + 1) * 128], in_=PSb[:, bank, :].rearrange("p (c n) -> p c n", c=8), func=AF.Copy),
                  reads=psk(bank), writes=[("XT", t)])

        def ln_all(spill_to):
            ln_a(0)
            for t in range(NT):
                if t + 1 < NT:
                    ln_a(t + 1)
                ln_b1(t, spill_to)
                ln_b2(t, spill_to)

        def load_ln_params(g_ap, b_ap):
            sc.dma("sp", lambda e: e.dma_start(out=gt, in_=g_ap.partition_broadcast(128)), writes=["gt"])
            sc.dma("sp", lambda e: e.dma_start(out=bt, in_=b_ap.partition_broadcast(128)), writes=["bt"])

        def load_win_block(l, blk, buf):
            c0 = blk * 512
            ncol = min(512, DIN - c0)
            src = w_in_d[l, :, c0:c0 + ncol].rearrange("(c p) n -> p c n", p=128)
            for hf in range(2):
                sc.dma("pool", lambda e, hf=hf: e.dma_start(out=WB[buf][:, hf * 4:(hf + 1) * 4, 0:ncol], in_=src[:, hf * 4:(hf + 1) * 4, :]),
                       writes=[("RW", buf, hf)])

        if stop == 'init':
            raise _Stop()
        load_win_block(0, 0, 0)
        load_win_block(0, 1, 1)
        ln_all(xs_d)
        dump("h0", X, [128, NT, 1024], [("X", t) for t in range(NT)])

        if stop == 'emb':
            raise _Stop()
        evac_rr = [0]

        def evac(out, in_, reads, writes, scale=None):
            evac_rr[0] ^= 1
            if evac_rr[0]:
                if scale is None:
                    sc.op("act", lambda e: e.activation(out=out, in_=in_, func=AF.Copy), reads=reads, writes=writes)
                else:
                    sc.op("act", lambda e: e.mul(out=out, in_=in_, mul=scale), reads=reads, writes=writes)
            else:
                if scale is None:
                    sc.op("dve", lambda e: e.tensor_copy(out=out, in_=in_), reads=reads, writes=writes)
                else:
                    sc.op("dve", lambda e: e.tensor_scalar(out=out, in0=in_, scalar1=scale, scalar2=None, op0=ALU.mult), reads=reads, writes=writes)

        def do_layer(l):
            lam_init = 0.8 - 0.6 * math.exp(-0.3 * l)
            cur_layer[0] = l
            if l == 0:
                sc.barrier()
            last = (l == nl - 1)
            R_D.reset()
            QT = R_D.get(16384, BF16, "p (h n) -> p h n", h=4)
            KT = R_D.get(16384, BF16, "p (h n) -> p h n", h=4)
            V = R_D.get(16 * 4 * 129 * 2, BF16, "p (t h e) -> p t h e", t=16, h=4)
            expB = R_D.get(4 * 1152 * 2, BF16, "p (h n) -> p h n", h=4)
            Eb = R_D.get(4096, BF16, "p (b m n) -> p b m n", b=2, m=2)
            d_y = R_D.get(4 * 128 * 2, BF16, "p (u n) -> p u n", u=4)
            _pu = R_D.pos
            silu_t = [R_D.get(2048, F32) for _ in range(2)]
            R_D.pos = _pu
            accS = R_D.get(8 * 129 * 4, F32, "p (a n) -> p a n", a=8)
            R_D.pos = _pu + 4608
            tmp_rev = R_D.get(1152 * 2, BF16)
            R_X.reset()
            gqT = R_X.get(8192, BF16, "p (c n) -> p c n", c=2)
            gkT = R_X.get(8192, BF16, "p (c n) -> p c n", c=2)
            gk_tok = R_X.get(8192, BF16, "p (t n) -> p t n", t=16)
            gv = R_X.get(16384, BF16, "p (t n) -> p t n", t=16)
            gr_s = R_X.get(16384, BF16, "p (t n) -> p t n", t=16)
            G33 = R_X.get(4096, BF16)

            for i, ap in enumerate((lq1_d, lk1_d, lq2_d, lk2_d)):
                sc.dma("sp", lambda e, i=i, ap=ap: e.dma_start(out=lamv[:, i, :], in_=ap[l, :].partition_broadcast(128)), writes=[("lamv", i)])
            sc.op("dve", lambda e: e.tensor_tensor(out=lamp[:, 0, :], in0=lamv[:, 0, :], in1=lamv[:, 1, :], op=ALU.mult), reads=[("lamv", 0), ("lamv", 1)], writes=["lamp"])
            sc.op("dve", lambda e: e.tensor_tensor(out=lamp[:, 1, :], in0=lamv[:, 2, :], in1=lamv[:, 3, :], op=ALU.mult), reads=[("lamv", 2), ("lamv", 3)], writes=["lamp"])
            sc.op("dve", lambda e: e.reduce_sum(out=lams[:, 0:2], in_=lamp, axis=mybir.AxisListType.X), reads=["lamp"], writes=["lams"])
            sc.op("act", lambda e: e.activation(out=lams[:, 0:2], in_=lams[:, 0:2], func=AF.Exp), reads=["lams"], writes=["lams"])
            sc.op("dve", lambda e: e.tensor_tensor(out=lams[:, 2:3], in0=lams[:, 0:1], in1=lams[:, 1:2], op=ALU.subtract), reads=["lams"], writes=["lams"])
            sc.op("dve", lambda e: e.tensor_scalar(out=lams[:, 3:4], in0=lams[:, 2:3], scalar1=lam_init, scalar2=-1.0, op0=ALU.add, op1=ALU.mult),
                  reads=["lams"], writes=["neglam"])
            neg_lam = lams[:, 3:4]
            sc.dma("sp", lambda e: e.dma_start(out=wd_t, in_=dnw_d[l, :].partition_broadcast(128)), writes=["wd"])
            sc.op("dve", lambda e: e.tensor_scalar(out=wd_t, in0=wd_t, scalar1=1.0 - lam_init, scalar2=None, op0=ALU.mult), reads=["wd"], writes=["wd"])
            sc.dma("sp", lambda e: e.dma_start(out=wg_t, in_=gnw_d[l, :].partition_broadcast(128)), writes=["wg"])
            sc.op("dve", lambda e: e.memset(Wg[0:33, :], 0.0), writes=["Wg"])
            sc.dma("pool", lambda e: e.dma_start(out=Wg[0:16, 0:256], in_=gup_d[l, 0]), writes=["Wg"])
            sc.dma("pool", lambda e: e.dma_start(out=Wg[16:32, 256:512], in_=gup_d[l, 1]), writes=["Wg"])
            sc.dma("pool", lambda e: e.dma_start(out=Wg[32:33, :], in_=gbias_d[l].rearrange("a n -> (a n)").partition_broadcast(1)), writes=["Wg"])
            sc.dma("sp", lambda e: e.dma_start(out=b1c, in_=b1_d[l].rearrange("(c p) -> p c", p=128), allow_slow_non_contiguous=True), writes=["b1c"])
            sc.dma("sp", lambda e: e.dma_start(out=b2t, in_=b2_d[l].partition_broadcast(128)), writes=["b2t"])
            sc.dma("sp", lambda e: e.dma_start(out=expB.rearrange("p h n -> p (h n)"), in_=eb_d), reads=["eb_d"], writes=[("expB", h) for h in range(4)])
            sc.op("dve", lambda e: e.memset(V[:, :, :, 128:129], 1.0), writes=[("V", t) for t in range(NT)])
            sc.op("dve", lambda e: e.memset(G33[32:33, :], 1.0), writes=["G33"])

            dump("XTin", XT, [128, 8, 2048], [("XT", t) for t in range(NT)])
            dump("Xin", X, [128, NT, 1024], [("X", t) for t in range(NT)])
            if stop == 'L' and l == nl - 1:
                raise _Stop()
            ps_rr = [0]

            def nextbank():
                b = ps_rr[0] % 6
                ps_rr[0] += 1
                return b

            xt_all = [("XT", t) for t in range(NT)]
            for blk in range(7):
                buf = blk % 2
                wb = WB[buf]
                kw = [("RW", buf, 0), ("RW", buf, 1)]
                if blk in (0, 1, 3, 6):
                    nch = 1 if blk == 6 else 4
                    for cc in range(nch):
                        for r in range(4):
                            bank = nextbank()
                            M = 32 if blk == 6 else 128
                            for c in range(8):
                                sc.op("pe", lambda e, c=c, cc=cc, r=r, bank=bank, M=M, wb=wb: e.matmul(
                                    PS[0:M, bank, :], lhsT=wb[:, c, cc * 128:cc * 128 + M], rhs=XT[:, c, r * 512:(r + 1) * 512],
                                    start=(c == 0), stop=(c == 7)), reads=kw + xt_all[r * 4:(r + 1) * 4], writes=psk(bank))
                            sl = slice(r * 512, (r + 1) * 512)
                            if blk == 0:
                                evac(QT[:, cc, sl], PS[:, bank, :], psk(bank), [("QT", cc, r)], scale=0.125)
                            elif blk == 1:
                                evac(KT[:, cc, sl], PS[:, bank, :], psk(bank), [("KT", cc, r)])
                            elif blk == 3:
                                if cc < 2:
                                    evac(gqT[:, cc, sl], PS[:, bank, :], psk(bank), [("gqT", 4 * r + i) for i in range(4)], scale=0.125)
                                else:
                                    evac(gkT[:, cc - 2, sl], PS[:, bank, :], psk(bank), [("gkT", 4 * r + i) for i in range(4)])
                            else:
                                evac(G33[0:32, sl], PS[0:32, bank, :], psk(bank), ["G33"])
                if blk in (2, 3, 4, 5):
                    for t in range(NT):
                        bank = nextbank()
                        c0, ncol = (256, 256) if blk == 3 else (0, 512)
                        for c in range(8):
                            sc.op("pe", lambda e, c=c, t=t, bank=bank, c0=c0, ncol=ncol, wb=wb: e.matmul(
                                PS[:, bank, 0:ncol], lhsT=XT[:, c, t * 128:(t + 1) * 128], rhs=wb[:, c, c0:c0 + ncol],
                                start=(c == 0), stop=(c == 7)), reads=kw + [("XT", t)], writes=psk(bank))
                        if blk == 2:
                            evac(V[:, t, :, 0:128], PS[:, bank, :].rearrange("p (h e) -> p h e", h=4), psk(bank), [("V", t)])
                        elif blk == 3:
                            evac(gk_tok[:, t, :], PS[:, bank, 0:256], psk(bank), [("gk_tok", t)])
                        elif blk == 4:
                            evac(gv[:, t, :], PS[:, bank, :], psk(bank), [("gv", t)])
                        else:
                            sb = t % 2
                            sc.op("act", lambda e, bank=bank, sb=sb: e.activation(out=silu_t[sb], in_=PS[:, bank, :], func=AF.Silu),
                                  reads=psk(bank), writes=[("silu", sb)])
                            sc.op("dve", lambda e, t=t, sb=sb: e.tensor_tensor(
                                out=gr_s[:, t, :].rearrange("p (h e) -> p h e", h=4), in0=silu_t[sb].rearrange("p (h e) -> p h e", h=4),
                                in1=bc_mid(wg_t, 4), op=ALU.mult), reads=[("silu", sb), "wg"], writes=[("gr_s", t)])
                if blk + 2 < 7:
                    load_win_block(l, blk + 2, buf)
            for hf in range(2):
                sc.dma("pool", lambda e, hf=hf: e.dma_start(out=WO[:, hf * 4:(hf + 1) * 4, :],
                                                             in_=w_o_d[l].rearrange("(c p) n -> p c n", p=128)[:, hf * 4:(hf + 1) * 4, :]),
                       writes=[("RW", hf, 0), ("RW", hf, 1)])
            dump("QT", QT, [128, 4, 2048], [("QT", a, b) for a in range(4) for b in range(4)])
            dump("KT", KT, [128, 4, 2048], [("KT", a, b) for a in range(4) for b in range(4)])
            dump("V", V, [128, 16, 4, 129], [("V", t) for t in range(NT)])
            dump("expB", expB, [128, 4, 1152], [("expB", h) for h in range(4)])
            dump("gqT", gqT, [128, 2, 2048], [("gqT", t) for t in range(NT)])
            dump("gr_s", gr_s, [128, 16, 512], [("gr_s", t) for t in range(NT)])
            dump("G33", G33[0:33, :], [33, 2048], ["G33"])

            if stop == 'P' and l == nl - 1:
                raise _Stop()
            steps = [(h, r, j) for h in range(4) for r in range(4) for j in range(16)]

            def acc_ap(m, u):
                idx = m * 4 + u
                return PS[:, 4 + idx // 3, (idx % 3) * 160:(idx % 3) * 160 + 129]

            def acc_keys(m, u):
                return psk(4 + (m * 4 + u) // 3)

            Eb3 = view(R_X.base + 61440, 2048, BF16, "p (m n) -> p m n", m=2)
            EbL = [Eb[:, 0, :, :], Eb[:, 1, :, :], Eb3]

            def d_scores(i):
                h, r, j = steps[i]
                d = j - 4 * r
                mixed = (-1 <= d <= 4)
                sb = i % 2
                eb = i % 3
                E = EbL[eb]
                for m in range(2):
                    bank = sb * 2 + m
                    sc.op("pe", lambda e, h=h, r=r, j=j, m=m, bank=bank: e.matmul(
                        PS[:, bank, :], lhsT=KT[64 * m:64 * m + 64, h, j * 128:(j + 1) * 128],
                        rhs=QT[64 * m:64 * m + 64, h, r * 512:(r + 1) * 512], start=True, stop=True),
                        reads=[("KT", h, j // 4), ("QT", h, r)], writes=psk(bank))
                pk2 = psk(sb * 2) + psk(sb * 2 + 1)
                ek = [("E", eb, 0), ("E", eb, 1)]
                if mixed:
                    c0 = (4 - d) * 128
                    sc.op("act", lambda e, sb=sb, E=E: e.activation(out=E, in_=PS[:, sb * 2:sb * 2 + 2, :], func=AF.Exp),
                          reads=pk2, writes=ek)
                    for m in range(2):
                        sc.op("dve", lambda e, E=E, m=m, h=h, c0=c0: e.tensor_tensor(out=E[:, m, :], in0=E[:, m, :], in1=expB[:, h, c0:c0 + 512], op=ALU.mult),
                              reads=[("E", eb, m), ("expB", h)], writes=[("E", eb, m)])
                else:
                    side = 0 if d < 0 else 1
                    sc.op("act", lambda e, sb=sb, E=E, side=side, h=h: e.activation(
                        out=E, in_=PS[:, sb * 2:sb * 2 + 2, :], func=AF.Exp, bias=cb[:, side, h:h + 1]),
                        reads=pk2 + [("cb", 0), ("cb", 1)], writes=ek)

            def d_av(i):
                h, r, j = steps[i]
                eb = i % 3
                E = EbL[eb]
                for m in range(2):
                    for u in range(4):
                        sc.op("pe", lambda e, h=h, j=j, m=m, u=u, E=E: e.matmul(
                            acc_ap(m, u), lhsT=E[:, m, u * 128:(u + 1) * 128], rhs=V[:, j, h, 0:129],
                            start=(j == 0 and (m * 4 + u) % 3 == 0), stop=(j == 15), skip_group_check=True),
                            reads=[("E", eb, m), ("V", j)], writes=acc_keys(m, u))

            sm = dsm[0]
            ka = ["accS", ("silu", 0), ("silu", 1)]

            def d_final(h, r):
                sc.op("dve", lambda e: e.tensor_copy(out=accS[:, 0:3, :], in_=PS[:, 4, 0:480].rearrange("p (a n) -> p a n", a=3)[:, :, 0:129]),
                      reads=psk(4), writes=ka)
                sc.op("dve", lambda e: e.tensor_copy(out=accS[:, 3:6, :], in_=PS[:, 5, 0:480].rearrange("p (a n) -> p a n", a=3)[:, :, 0:129]),
                      reads=psk(5), writes=ka)
                sc.op("dve", lambda e: e.tensor_copy(out=accS[:, 6:8, :], in_=PS[:, 6, 0:320].rearrange("p (a n) -> p a n", a=2)[:, :, 0:129]),
                      reads=psk(6), writes=ka)

            def d_final2(h, r):
                sc.op("dve", lambda e: e.reciprocal(out=sm[:, 0:8], in_=accS[:, :, 128]), reads=ka[:1], writes=["dsm"])
                sc.op("dve", lambda e: e.tensor_scalar(out=sm[:, 4:8], in0=sm[:, 4:8], scalar1=neg_lam, scalar2=None, op0=ALU.mult),
                      reads=["dsm", "neglam"], writes=["dsm"])
                sc.op("dve", lambda e: e.memset(sm[:, 8:12], 0.0), writes=["dss"])

            def d_final_u(u):
                if True:
                    sc.op("dve", lambda e, u=u: e.tensor_scalar(out=accS[:, u, 0:128], in0=accS[:, u, 0:128], scalar1=sm[:, u:u + 1], scalar2=None, op0=ALU.mult),
                          reads=["dsm"] + ka[:1], writes=ka[:1])
                    sc.op("dve", lambda e, u=u: e.scalar_tensor_tensor(out=accS[:, u, 0:128], in0=accS[:, 4 + u, 0:128], scalar=sm[:, 4 + u:5 + u],
                                                                       in1=accS[:, u, 0:128], op0=ALU.mult, op1=ALU.add),
                          reads=["dsm"] + ka[:1], writes=ka[:1])
                    sc.op("dve", lambda e, u=u: e.scalar_tensor_tensor(out=accS[:, 4 + u, 0:128], in0=accS[:, u, 0:128], scalar=1.0, in1=accS[:, u, 0:128],
                                                                       op0=ALU.mult, op1=ALU.mult, accum_out=sm[:, 8 + u:9 + u]),
                          reads=ka[:1], writes=ka[:1] + ["dss"])

            def d_final_b(h, r):
                sc.op("act", lambda e: e.activation(out=sm[:, 12:16], in_=sm[:, 8:12], func=AF.Ln, bias=1e-5, scale=1.0 / 128), reads=["dss"], writes=["drs"])
                sc.op("act", lambda e: e.activation(out=sm[:, 12:16], in_=sm[:, 12:16], func=AF.Exp, scale=-0.5), reads=["drs"], writes=["drs"])
                for u in range(4):
                    sc.op("dve", lambda e, u=u: e.scalar_tensor_tensor(out=d_y[:, u, :], in0=accS[:, u, 0:128], scalar=sm[:, 12 + u:13 + u], in1=wd_t,
                                                                       op0=ALU.mult, op1=ALU.mult),
                          reads=ka[:1] + ["drs", "wd"], writes=[("dy", u)])

            def d_final_pe(h, r):
                for u in range(4):
                    sc.op("pe", lambda e, u=u: e.transpose(out=PSb[:, 7, u * 128:(u + 1) * 128], in_=d_y[:, u, :], identity=identb),
                          reads=[("dy", u), "identb"], writes=psk(7))
                sc.op("dve", lambda e, h=h, r=r: e.tensor_copy(out=XT[:, h, r * 512:(r + 1) * 512], in_=PSb[:, 7, 0:512]),
                      reads=psk(7), writes=[("XT", 4 * r + i) for i in range(4)])

            pend = []
            pend_b = []
            pend_u = []
            d_scores(0)
            d_scores(1)
            for i in range(len(steps)):
                h, r, j = steps[i]
                if j == 15:
                    d_av(i)
                    d_final(h, r)
                    if i + 2 < len(steps):
                        d_scores(i + 2)
                    d_final2(h, r)
                else:
                    if i + 2 < len(steps):
                        d_scores(i + 2)
                    d_av(i)
                if j == 15:
                    pend.append((h, r))
                    pend_b.append((h, r))
                    pend_u.extend([0, 1, 2, 3])
                    d_final_u(pend_u.pop(0))
                elif pend_u:
                    d_final_u(pend_u.pop(0))
                elif j == 4 and pend_b:
                    d_final_b(*pend_b.pop(0))
                elif j == 7 and pend:
                    d_final_pe(*pend.pop(0))
            while pend_u:
                d_final_u(pend_u.pop(0))
            while pend_b:
                d_final_b(*pend_b.pop(0))
            while pend:
                d_final_pe(*pend.pop(0))
            dump("mixT_d", XT, [128, 8, 2048], [("XT", t) for t in range(NT)])

            if stop == 'D' and l == nl - 1:
                raise _Stop()
            sc.barrier()
            R_D.reset()
            qf = R_D.get(8192, BF16, "p (c n) -> p c n", c=2)
            kf = R_D.get(8192, BF16, "p (c n) -> p c n", c=2)
            kd_f = R_D.get(8192, BF16, "p (t n) -> p t n", t=16)
            Sbf = R_D.get(16384, BF16, "p (d q t e) -> p d q t e", d=2, q=2, t=16)
            stm2 = [R_D.get(4096, F32, "p (d q n) -> p d q n", d=2, q=2) for _ in range(2)]
            _p0 = R_D.pos
            sp_ = [R_D.get(2048, F32) for _ in range(2)]
            _p1 = R_D.pos
            ebt = [R_D.get(2 * 2 * 129 * 4, F32, "p (d q n) -> p d q n", d=2, q=2) for _ in range(2)]
            _p2 = R_D.pos
            enbt = [R_D.get(2 * 2 * 128 * 4, F32, "p (d q n) -> p d q n", d=2, q=2) for _ in range(2)]
            erem = [R_D.get(2048, F32) for _ in range(2)]
            dS = [R_D.get(1024, F32) for _ in range(4)]
            _pend = R_D.pos
            Am = [R_X.get(4 * 2 * 128 * 2, BF16, "p (h d n) -> p h d n", h=4, d=2) for _ in range(2)]
            R_D.pos = _p2
            g_y = [R_D.get(1024, BF16) for _ in range(2)]
            g_junk = R_D.get(512, F32)
            R_D.pos = _pend
            qb, kb, kd_b = gqT, gkT, gk_tok
            maskf = Uf[:, 0:128]
            maskb = Usf

            def tl(t):
                return slice(t * 128, (t + 1) * 128)

            def prep_A(t):
                b = t % 2
                sp = sp_[b]
                zb = 0 if b == 0 else 7
                sc.op("pe", lambda e: e.matmul(PS[:, zb, :], lhsT=G33[0:33, tl(t)], rhs=Wg[0:33, :], start=True, stop=True),
                      reads=["G33", "Wg"], writes=psk(zb))
                sc.op("act", lambda e: e.activation(out=sp, in_=PS[:, zb, :], func=AF.Exp, scale=-1.0), reads=psk(zb), writes=[("sp", b)])
                sc.op("act", lambda e: e.activation(out=sp, in_=sp, func=AF.Ln, bias=1.0, scale=1.0), reads=[("sp", b)], writes=[("sp", b)])

            prep_A(0)
            for t in range(NT):
                b = t % 2
                sp = sp_[b]
                if t + 1 < NT:
                    prep_A(t + 1)
                sc.op("pe", lambda e, sp=sp: e.matmul(PS[:, 1, 0:256], lhsT=Usf, rhs=sp[:, 0:256], start=True, stop=True), reads=[("sp", b), "Usf", "Usb"], writes=psk(1))
                sc.op("pe", lambda e, sp=sp: e.matmul(PS[:, 1, 256:512], lhsT=Usb, rhs=sp[:, 256:512], start=True, stop=True), reads=[("sp", b), "Usf", "Usb"], writes=psk(1))
                sc.op("act", lambda e, b=b: e.activation(out=erem[b], in_=PS[:, 1, :], func=AF.Exp, scale=-1.0 / 16), reads=psk(1), writes=[("erem", b)])
                sc.op("dve", lambda e, t=t, b=b: e.tensor_tensor(out=kd_f[:, t, :], in0=gk_tok[:, t, :], in1=erem[b][:, 0:256], op=ALU.mult),
                      reads=[("gk_tok", t), ("erem", b)], writes=[("kd_f", t)])
                sc.op("dve", lambda e, t=t, b=b: e.tensor_tensor(out=kd_b[:, t, :], in0=gk_tok[:, t, :], in1=erem[b][:, 256:512], op=ALU.mult),
                      reads=[("gk_tok", t), ("erem", b), ("kd_f", t)], writes=[("gk_tok", t)])
                for d in range(2):
                    U = Uf if d == 0 else Ub
                    for q in range(2):
                        sc.op("pe", lambda e, sp=sp, d=d, q=q, U=U: e.matmul(PS[:, 2 + d, q * 160:q * 160 + 129],
                                                                            lhsT=sp[:, d * 256 + q * 128:d * 256 + (q + 1) * 128], rhs=U, start=True, stop=True),
                              reads=[("sp", b), "Uf", "Ub"], writes=psk(2 + d))
                src4 = PS[:, 2:4, 0:320].rearrange("p a (q n) -> p a q n", q=2)
                sc.op("act", lambda e, b=b, src4=src4: e.activation(out=ebt[b], in_=src4[:, :, :, 0:129], func=AF.Exp, scale=-1.0 / 16),
                      reads=psk(2) + psk(3), writes=[("eb", b, 0), ("eb", b, 1)])
                sc.op("act", lambda e, b=b, src4=src4: e.activation(out=enbt[b], in_=src4[:, :, :, 0:128], func=AF.Exp, scale=1.0 / 16),
                      reads=psk(2) + psk(3), writes=[("enb", b, 0), ("enb", b, 1)])
                sc.op("dve", lambda e, t=t, b=b: e.tensor_tensor(out=qf[:, :, tl(t)], in0=gqT[:, :, tl(t)], in1=ebt[b][:, 0, :, 0:128], op=ALU.mult),
                      reads=[("gqT", t), ("eb", b, 0)], writes=[("qf", t)])
                sc.op("dve", lambda e, t=t, b=b: e.tensor_tensor(out=kf[:, :, tl(t)], in0=gkT[:, :, tl(t)], in1=enbt[b][:, 0, :, :], op=ALU.mult),
                      reads=[("gkT", t), ("enb", b, 0)], writes=[("kf", t)])
                sc.op("dve", lambda e, t=t, b=b: e.tensor_tensor(out=qb[:, :, tl(t)], in0=gqT[:, :, tl(t)], in1=ebt[b][:, 1, :, 0:128], op=ALU.mult),
                      reads=[("gqT", t), ("eb", b, 1), ("qf", t)], writes=[("gqT", t)])
                sc.op("dve", lambda e, t=t, b=b: e.tensor_tensor(out=kb[:, :, tl(t)], in0=gkT[:, :, tl(t)], in1=enbt[b][:, 1, :, :], op=ALU.mult),
                      reads=[("gkT", t), ("enb", b, 1), ("kf", t)], writes=[("gkT", t)])
                sc.op("dve", lambda e, t=t, b=b: e.tensor_copy(out=decs[:, :, :, t:t + 1], in_=ebt[b][:, :, :, 128:129]),
                      reads=[("eb", b, 0), ("eb", b, 1)], writes=["decs"])
            dump("qf", qf, [128, 2, 2048], [("qf", t) for t in range(NT)])
            dump("kd_f", kd_f, [128, 16, 256], [("kd_f", t) for t in range(NT)])
            dump("decs", decs, [128, 2, 2, 16], ["decs"])

            if stop == 'G1' and l == nl - 1:
                raise _Stop()
            sc.op("dve", lambda e: e.memset(stm2[0], 0.0), writes=[("stm", 0, d, q) for d in range(2) for q in range(2)])
            chains = [(d, q) for d in range(2) for q in range(2)]
            par = {c: 0 for c in chains}
            for i in range(NT):
                todo = []
                for ci, (d, q) in enumerate(chains):
                    t = i if d == 0 else NT - 1 - i
                    cur = par[(d, q)]
                    if i > 0:
                        sc.op("act", lambda e, d=d, q=q, t=t, cur=cur: e.activation(out=Sbf[0:64, d, q, t, :], in_=stm2[cur][0:64, d, q, 0:128], func=AF.Copy),
                              reads=[("stm", cur, d, q)], writes=[("Sbf", d, q, t)])
                        sc.op("dve", lambda e, d=d, q=q, t=t, cur=cur: e.tensor_copy(out=Sbf[64:128, d, q, t, :], in_=stm2[cur][64:128, d, q, 128:256]),
                              reads=[("stm", cur, d, q)], writes=[("Sbf", d, q, t)])
                    if i == NT - 1:
                        continue
                    kd = kd_f if d == 0 else kd_b
                    kkey = "kd_f" if d == 0 else "gk_tok"
                    pslot = ci % 2
                    pk = psk(4 + pslot)
                    sc.op("pe", lambda e, kd=kd, t=t, q=q, pslot=pslot: e.matmul(PS[:, 4 + pslot, 0:256], lhsT=kd[:, t, q * 128:(q + 1) * 128],
                                                                                rhs=gv[:, t, q * 256:(q + 1) * 256], start=True, stop=True),
                          reads=[(kkey, t), ("gv", t)], writes=pk)
                    if os.environ.get("GSKIP") != "evac":
                        sc.op("act", lambda e, ci=ci, pslot=pslot: e.activation(out=dS[ci], in_=PS[:, 4 + pslot, 0:256], func=AF.Copy),
                              reads=pk, writes=[("dS", ci)])
                    todo.append((ci, d, q, t, cur))
                for (ci, d, q, t, cur) in todo:
                    if os.environ.get("GSKIP") == "upd":
                        par[(d, q)] = 1 - cur
                        continue
                    sc.op("dve", lambda e, ci=ci, d=d, q=q, t=t, cur=cur: e.scalar_tensor_tensor(
                        out=stm2[1 - cur][:, d, q, :], in0=stm2[cur][:, d, q, :], scalar=decs[:, d, q, t:t + 1], in1=dS[ci],
                        op0=ALU.mult, op1=ALU.add), reads=[("stm", cur, d, q), "decs", ("dS", ci)], writes=[("stm", 1 - cur, d, q)])
                    par[(d, q)] = 1 - cur
            dump("Sbf", Sbf, [128, 2, 2, 16, 128], [("Sbf", d, q, t) for d in range(2) for q in range(2) for t in range(NT)])
            if stop == 'G2' and l == nl - 1:
                raise _Stop()

            def g_A(t):
                b = t % 2
                A = Am[b]
                items = {0: [], 1: []}
                for sq in range(4):
                    for half in range(2):
                        sbank = 2 * b + half
                        h, d = half + 2 * (sq // 2), sq % 2
                        q = h // 2
                        base = (h % 2) * 64
                        kk = kf if d == 0 else kb
                        qq = qf if d == 0 else qb
                        kkey = ("kf", t) if d == 0 else ("gkT", t)
                        qkey = ("qf", t) if d == 0 else ("gqT", t)
                        sc.op("pe", lambda e, kk=kk, qq=qq, q=q, base=base, sbank=sbank, sq=sq: e.matmul(
                            PS[:, sbank, sq * 128:(sq + 1) * 128], lhsT=kk[base:base + 64, q, tl(t)], rhs=qq[base:base + 64, q, tl(t)], start=True, stop=True),
                            reads=[kkey, qkey], writes=psk(sbank))
                        items[half].append((sq, h, d))
                for half in range(2):
                    sbank = 2 * b + half
                    for (sq, h, d) in items[half]:
                        mk = maskf if d == 0 else maskb
                        sc.op("dve", lambda e, A=A, h=h, d=d, sbank=sbank, sq=sq, mk=mk: e.tensor_tensor(
                            out=A[:, h, d, :], in0=PS[:, sbank, sq * 128:(sq + 1) * 128], in1=mk, op=ALU.mult),
                            reads=psk(sbank) + ["Uf", "Usf"], writes=[("A", b, h, d), ("sp", b)])

            def g_B(t):
                b = t % 2
                A = Am[b]
                obank = (0 + b) if os.environ.get('GBANK') else (4 + b)
                for h in range(4):
                    q = h // 2
                    base = (h % 2) * 64
                    oh_ = PS[:, obank, h * 128:(h + 1) * 128]
                    ok = psk(obank)
                    inter_f = t > 0
                    inter_b = t < NT - 1
                    sc.op("pe", lambda e, A=A, h=h, oh_=oh_: e.matmul(oh_, lhsT=A[:, h, 0, :], rhs=gv[:, t, h * 128:(h + 1) * 128], start=True, stop=False),
                          reads=[("A", b, h, 0), ("gv", t)], writes=ok)
                    sc.op("pe", lambda e, A=A, h=h, oh_=oh_, fin=(not inter_f and not inter_b): e.matmul(
                        oh_, lhsT=A[:, h, 1, :], rhs=gv[:, t, h * 128:(h + 1) * 128], start=False, stop=fin),
                        reads=[("A", b, h, 1), ("gv", t)], writes=ok)
                    if inter_f:
                        sc.op("pe", lambda e, q=q, base=base, oh_=oh_, fin=(not inter_b): e.matmul(
                            oh_, lhsT=qf[base:base + 64, q, tl(t)], rhs=Sbf[base:base + 64, 0, q, t, :], start=False, stop=fin),
                            reads=[("qf", t), ("Sbf", 0, q, t)], writes=ok)
                    if inter_b:
                        sc.op("pe", lambda e, q=q, base=base, oh_=oh_: e.matmul(
                            oh_, lhsT=qb[base:base + 64, q, tl(t)], rhs=Sbf[base:base + 64, 1, q, t, :], start=False, stop=True),
                            reads=[("gqT", t), ("Sbf", 1, q, t)], writes=ok)

            def g_norm(t):
                b = t % 2
                sm = gsm[b]
                obank = (0 + b) if os.environ.get('GBANK') else (4 + b)
                okall = psk(obank)
                for h in range(4):
                    sc.op("act", lambda e, h=h, sm=sm: e.activation(out=g_junk, in_=PS[:, obank, h * 128:(h + 1) * 128], func=AF.Square, accum_out=sm[:, h:h + 1]),
                          reads=okall, writes=["gjunk", ("gss", b), ("enb", 1, 0), ("enb", 1, 1)])
                sc.op("act", lambda e, sm=sm: e.activation(out=sm[:, 4:8], in_=sm[:, 0:4], func=AF.Sqrt, bias=1e-5, scale=1.0 / 128),
                      reads=[("gss", b)], writes=[("grs", b)])
                sc.op("dve", lambda e, sm=sm: e.reciprocal(out=sm[:, 4:8], in_=sm[:, 4:8]), reads=[("grs", b)], writes=[("grs", b)])
                for h in range(4):
                    sc.op("dve", lambda e, h=h, sm=sm, b=b: e.scalar_tensor_tensor(
                        out=g_y[b][:, h * 128:(h + 1) * 128], in0=PS[:, obank, h * 128:(h + 1) * 128], scalar=sm[:, 4 + h:5 + h],
                        in1=gr_s[:, t, h * 128:(h + 1) * 128], op0=ALU.mult, op1=ALU.mult),
                        reads=psk(obank) + [("grs", b), ("gr_s", t)], writes=[("gy", b), ("enb", 0, 0), ("enb", 0, 1)])

            def g_tr(t):
                b = t % 2
                tk = psk(6 + b)
                for h in range(4):
                    sc.op("pe", lambda e, h=h, b=b: e.transpose(out=PSb[:, 6 + b, h * 128:(h + 1) * 128], in_=g_y[b][:, h * 128:(h + 1) * 128], identity=identb),
                          reads=[("gy", b), "identb"], writes=tk)
                sc.op("act", lambda e, b=b: e.activation(out=XT[:, 4:8, tl(t)], in_=PSb[:, 6 + b, 0:512].rearrange("p (c n) -> p c n", c=4), func=AF.Copy),
                      reads=tk, writes=[("XT", t)])

            g_A(0)
            for t in range(NT):
                if t + 1 < NT:
                    g_A(t + 1)
                if os.environ.get("GSKIP") == "B":
                    continue
                g_B(t)
                if os.environ.get("GSKIP") == "norm":
                    continue
                g_norm(t)
                if os.environ.get("GSKIP") == "tr":
                    continue
                if t > 0:
                    g_tr(t - 1)
            if not os.environ.get("GSKIP"):
                g_tr(NT - 1)
            dump("mixT", XT, [128, 8, 2048], [("XT", t) for t in range(NT)])

            if stop == 'G' and l == nl - 1:
                raise _Stop()
            sc.barrier()
            load_ln_params(ln1g_d[l], ln1b_d[l])
            R_D.reset()
            W1B = [R_D.get(8192, BF16, "p (c n) -> p c n", c=8) for _ in range(2)]
            W2B = [R_D.get(8192, BF16, "p (c n) -> p c n", c=4) for _ in range(2)]
            hT = R_D.get(16384, BF16, "p (c n) -> p c n", c=4)
            relu_t = [R_D.get(2048, F32) for _ in range(2)]

            def load_ffn_block(fb, buf):
                s1 = w1_d[l, :, fb * 512:(fb + 1) * 512].rearrange("(c p) n -> p c n", p=128)
                s2 = w2_d[l, fb * 512:(fb + 1) * 512, :].rearrange("(c p) n -> p c n", p=128)
                for hf in range(2):
                    sc.dma("pool", lambda e, hf=hf: e.dma_start(out=W1B[buf][:, hf * 4:(hf + 1) * 4, :], in_=s1[:, hf * 4:(hf + 1) * 4, :]),
                           writes=[("W1B", buf, hf)])
                for hf in range(2):
                    sc.dma("pool", lambda e, hf=hf: e.dma_start(out=W2B[buf][:, hf * 2:(hf + 1) * 2, :], in_=s2[:, hf * 2:(hf + 1) * 2, :]),
                           writes=[("W2B", buf, hf)])

            load_ffn_block(0, 0)
            load_ffn_block(1, 1)
            for t in range(NT):
                sc.dma("sp", lambda e, t=t: e.dma_start(out=X[:, t, :], in_=xs_d[t * 128:(t + 1) * 128, :]), reads=[("xsd", t)], writes=[("X", t)])
            def o_mm(t):
                yb = (t % 3) * 2
                for hf in range(2):
                    for c in range(8):
                        sc.op("pe", lambda e, c=c, hf=hf, t=t, yb=yb: e.matmul(PS[:, yb + hf, :], lhsT=XT[:, c, tl(t)], rhs=WO[:, c, hf * 512:(hf + 1) * 512],
                                                                              start=(c == 0), stop=(c == 7)),
                              reads=[("XT", t), ("RW", 0, 0), ("RW", 0, 1), ("RW", 1, 0), ("RW", 1, 1)], writes=psk(yb + hf))

            def o_ln(t):
                yb = (t % 3) * 2
                sc.op("dve", lambda e, t=t, yb=yb: e.scalar_tensor_tensor(out=X[:, t, :], in0=X[:, t, :], scalar=ALPHA,
                                                                          in1=PS[:, yb:yb + 2, :].rearrange("p a n -> p (a n)"), op0=ALU.mult, op1=ALU.add),
                      reads=[("X", t)] + psk(yb) + psk(yb + 1), writes=[("X", t)])
                ln_a(t)

            for t0 in range(3):
                o_mm(t0)
                o_ln(t0)
            for t in range(NT):
                if t + 3 < NT:
                    o_mm(t + 3)
                ln_b1(t, None)
                if t + 3 < NT:
                    o_ln(t + 3)
                ln_b2(t, None)
            dump("x1T", XT, [128, 8, 2048], [("XT", t) for t in range(NT)])

            if stop == 'O' and l == nl - 1:
                raise _Stop()
            if not last:
                load_win_block(l + 1, 0, 0)
                load_win_block(l + 1, 1, 1)
            hrr = [0]
            for fb in range(8):
                buf = fb % 2
                for r in range(4):
                    for fc in range(4):
                        bank = 4 + hrr[0] % 3
                        rb = hrr[0] % 2
                        hrr[0] += 1
                        for c in range(8):
                            sc.op("pe", lambda e, c=c, fc=fc, r=r, bank=bank, buf=buf: e.matmul(
                                PS[:, bank, :], lhsT=W1B[buf][:, c, fc * 128:(fc + 1) * 128], rhs=XT[:, c, r * 512:(r + 1) * 512],
                                start=(c == 0), stop=(c == 7)), reads=[("W1B", buf, 0), ("W1B", buf, 1)] + xt_all[r * 4:(r + 1) * 4], writes=psk(bank))
                        fcol = fb * 4 + fc
                        sc.op("act", lambda e, bank=bank, rb=rb, fcol=fcol: e.activation(out=relu_t[rb], in_=PS[:, bank, :], func=AF.Relu,
                                                                                         bias=b1c[:, fcol:fcol + 1], scale=1.0),
                              reads=psk(bank) + ["b1c"], writes=[("relu", rb)])
                        sc.op("dve", lambda e, rb=rb, fc=fc, r=r: e.tensor_tensor(out=hT[:, fc, r * 512:(r + 1) * 512], in0=relu_t[rb], in1=relu_t[rb], op=ALU.mult),
                              reads=[("relu", rb)], writes=[("hT", fc, r)])
                for t in range(NT):
                    yb = (t % 2) * 2
                    for hf in range(2):
                        for fc in range(4):
                            sc.op("pe", lambda e, fc=fc, hf=hf, t=t, yb=yb, buf=buf: e.matmul(
                                PS[:, yb + hf, :], lhsT=hT[:, fc, tl(t)], rhs=W2B[buf][:, fc, hf * 512:(hf + 1) * 512],
                                start=(fc == 0), stop=(fc == 3)), reads=[("hT", fc, t // 4), ("W2B", buf, 0), ("W2B", buf, 1)], writes=psk(yb + hf))
                    if fb == 0:
                        sc.op("dve", lambda e, t=t, yb=yb: e.scalar_tensor_tensor(out=X[:, t, :], in0=X[:, t, :], scalar=ALPHA,
                                                                                  in1=PS[:, yb:yb + 2, :].rearrange("p a n -> p (a n)"), op0=ALU.mult, op1=ALU.add),
                              reads=[("X", t)] + psk(yb) + psk(yb + 1), writes=[("X", t)])
                        sc.op("pool", lambda e, t=t: e.tensor_tensor(out=X[:, t, :], in0=X[:, t, :], in1=b2t, op=ALU.add), reads=[("X", t), "b2t"], writes=[("X", t)])
                    else:
                        sc.op("dve", lambda e, t=t, yb=yb: e.tensor_tensor(out=X[:, t, :], in0=X[:, t, :], in1=PS[:, yb:yb + 2, :].rearrange("p a n -> p (a n)"), op=ALU.add),
                              reads=[("X", t)] + psk(yb) + psk(yb + 1), writes=[("X", t)])
                if fb + 2 < 8:
                    load_ffn_block(fb + 2, buf)
            if stop == 'F' and l == nl - 1:
                raise _Stop()
            load_ln_params(ln2g_d[l], ln2b_d[l])
            ln_all(out_d if last else xs_d)
            sc.barrier()


        for _l in range(nl):
            do_layer(_l)
    except _Stop:
        pass
    out_dmas = [o for o in sc.ops if o.is_dma and o.dkey in [("xs", i) for i in range(4)]]
    fin = {}
    for o in out_dmas:
        fin[o.dkey] = o
    finals = list(fin.values()) + list(dbg_out.values())
    sc.emit(final_wait_ops=finals)
    es.close()
    return nc, sc


_CONST = None


def kernel(**inputs):
    global _CONST
    if _CONST is None:
        _CONST = _constants()
    nc, _ = build(2)
    x = np.ascontiguousarray(inputs["x"], dtype=np.float32)
    shared = {k: np.ascontiguousarray(v, dtype=np.float32) for k, v in inputs.items() if k != "x"}
    shared.update(_CONST)
    in_maps = []
    for b in range(8):
        m = dict(shared)
        m["x"] = x[b]
        in_maps.append(m)
    res = run_bass_kernel_spmd(nc, in_maps, core_ids=list(range(8)))
    return np.stack([r["out"] for r in res.results], axis=0).astype(np.float32)
```

```python
import math
import os
from contextlib import ExitStack

import numpy as np
import concourse.bass as bass
import concourse.mybir as mybir
from concourse.bass_utils import run_bass_kernel_spmd

F32 = mybir.dt.float32
BF16 = mybir.dt.bfloat16
AF = mybir.ActivationFunctionType
ALU = mybir.AluOpType

S = 2048
D = 1024
DIN = 3104
DFF = 4096
NT = 16
ALPHA = (2.0 * 2) ** 0.25
ENGS = ("pe", "act", "dve", "pool", "sp")
EPOCH = 30000


class _Res:
    __slots__ = ("last_w", "readers")

    def __init__(self):
        self.last_w = None
        self.readers = []


class _Op:
    __slots__ = ("eng", "fn", "deps", "signal", "tok", "is_dma", "dkey")

    def __init__(self, eng, fn, is_dma, dkey):
        self.eng = eng
        self.fn = fn
        self.deps = []
        self.signal = False
        self.tok = None
        self.is_dma = is_dma
        self.dkey = dkey


class Sched:
    def __init__(self, nc):
        self.nc = nc
        self.ops = []
        self.res = {}
        self.pending = {e: [] for e in ENGS}

    def _r(self, key):
        x = self.res.get(key)
        if x is None:
            x = self.res[key] = _Res()
        return x

    def _add(self, op, reads, writes):
        deps = set()
        for k in reads:
            rs = self._r(k)
            if rs.last_w is not None:
                deps.add(rs.last_w)
        for k in writes:
            rs = self._r(k)
            if rs.last_w is not None:
                deps.add(rs.last_w)
            deps.update(rs.readers)
        for k in reads:
            self._r(k).readers.append(op)
        for k in writes:
            rs = self._r(k)
            rs.last_w = op
            rs.readers = []
        if self.pending[op.eng]:
            deps.update(self.pending[op.eng])
            self.pending[op.eng] = []
        deps.discard(op)
        op.deps = list(deps)
        self.ops.append(op)
        return op

    def op(self, eng, fn, reads=(), writes=()):
        return self._add(_Op(eng, fn, False, None), reads, writes)

    def dma(self, eng, fn, dkey=None, reads=(), writes=()):
        if dkey is None:
            dkey = ("w", writes[0])
        return self._add(_Op(eng, fn, True, dkey), reads, writes)

    def barrier(self):
        last = {}
        for o in self.ops:
            last[(o.eng, o.dkey) if o.is_dma else o.eng] = o
        b = list(last.values())
        self.pending = {e: list(b) for e in ENGS}

    def emit(self, final_wait_ops=()):
        nc = self.nc
        ops = self.ops
        for o in ops:
            for d in o.deps:
                if d.is_dma:
                    d.signal = True
                elif d.eng == "pe" and o.eng == "pe" and not o.is_dma:
                    continue
                else:
                    d.signal = True
        with ExitStack() as es:
            eng_sems = {e: [] for e in ENGS}
            cnt = {e: 0 for e in ENGS}
            dma_sems = {}
            dma_cnt = {}
            for o in ops:
                if o.is_dma:
                    if o.dkey not in dma_sems:
                        dma_sems[o.dkey] = es.enter_context(nc.semaphore("d%d" % len(dma_sems)))
                        dma_cnt[o.dkey] = 0
                    dma_cnt[o.dkey] += 16
                    o.tok = (dma_sems[o.dkey], dma_cnt[o.dkey])
                elif o.signal:
                    ep = cnt[o.eng] // EPOCH
                    if ep >= len(eng_sems[o.eng]):
                        eng_sems[o.eng].append(es.enter_context(nc.semaphore("e_%s_%d" % (o.eng, ep))))
                    cnt[o.eng] += 1
                    o.tok = (eng_sems[o.eng][ep], cnt[o.eng] - ep * EPOCH)
            per_eng = {e: [o for o in ops if o.eng == e] for e in ENGS}
            self.stats = {e: len(per_eng[e]) for e in ENGS}
            self.stats["sems"] = sum(len(v) for v in eng_sems.values()) + len(dma_sems)

            def run(e, eng):
                waited = {}
                for o in per_eng[e]:
                    need = {}
                    for d in o.deps:
                        if d.tok is None:
                            continue
                        if (not d.is_dma) and d.eng == "pe" and e == "pe" and not o.is_dma:
                            continue
                        s, v = d.tok
                        k = id(s)
                        if waited.get(k, 0) >= v:
                            continue
                        if k not in need or need[k][1] < v:
                            need[k] = (s, v)
                    for k, (s, v) in need.items():
                        eng.wait_ge(s, v)
                        waited[k] = v
                    ins = o.fn(eng)
                    if o.tok is not None:
                        ins.then_inc(o.tok[0], 16 if o.is_dma else 1)
                if e == "sp":
                    for o in final_wait_ops:
                        s, v = o.tok
                        eng.wait_ge(s, v)

            with nc.Block() as block:
                @block.sync
                def _(eng):
                    run("sp", eng)

                @block.tensor
                def _(eng):
                    run("pe", eng)

                @block.scalar
                def _(eng):
                    run("act", eng)

                @block.vector
                def _(eng):
                    run("dve", eng)

                @block.gpsimd
                def _(eng):
                    run("pool", eng)


def _t5_bucket(rel):
    nb = 16
    me = 8
    ret = np.where(rel > 0, nb, 0)
    n = np.abs(rel)
    large = me + (np.log(np.maximum(n, 1).astype(np.float32) / np.float32(me))
                  / np.float32(math.log(128 / me)) * np.float32(nb - me)).astype(np.int32)
    large = np.minimum(large, nb - 1)
    return ret + np.where(n < me, n, large)


MLEN = 1280


def _constants():
    c = {}
    c["c_ident"] = np.eye(128, dtype=np.float32)
    c["c_J"] = np.eye(128, dtype=np.float32)[::-1].copy()
    s = np.arange(128)[:, None]
    t = np.arange(128)[None, :]
    uf = np.zeros((128, 129), np.float32)
    uf[:, :128] = (s <= t)
    uf[:, 128] = 1.0
    ub = np.zeros((128, 129), np.float32)
    ub[:, :128] = (s >= t)
    ub[:, 128] = 1.0
    c["c_uf"] = uf
    c["c_ub"] = ub
    c["c_sf"] = (s > t).astype(np.float32)
    c["c_sb"] = (s < t).astype(np.float32)
    n = np.arange(MLEN)
    bk = _t5_bucket(639 - n)
    oh = np.zeros((32, MLEN), np.float32)
    oh[bk, n] = 1.0
    c["c_onehot"] = oh
    return c


class _Stop(Exception):
    pass


def build(nl=2, dbg=(), stop=None):
    nc = bass.Bass("TRN2", target_bir_lowering=False)

    def din(name, shape):
        return nc.dram_tensor(name, list(shape), F32, kind="ExternalInput").ap()

    x_d = din("x", [S, D])
    lnemb_g = din("ln_emb_g", [D])
    lnemb_b = din("ln_emb_b", [D])
    table_d = din("rel_bias_table", [32, 4])
    w_in_d = din("w_in", [2, D, DIN])
    lq1_d = din("lambda_q1", [2, 64])
    lk1_d = din("lambda_k1", [2, 64])
    lq2_d = din("lambda_q2", [2, 64])
    lk2_d = din("lambda_k2", [2, 64])
    dnw_d = din("diff_norm_w", [2, 128])
    gup_d = din("gla_gate_up", [2, 2, 16, 256])
    gbias_d = din("gla_gate_bias", [2, 2, 256])
    gnw_d = din("gla_norm_w", [2, 128])
    w_o_d = din("w_o", [2, D, D])
    ln1g_d = din("ln1_g", [2, D])
    ln1b_d = din("ln1_b", [2, D])
    w1_d = din("w_ffn1", [2, D, DFF])
    b1_d = din("b_ffn1", [2, DFF])
    w2_d = din("w_ffn2", [2, DFF, D])
    b2_d = din("b_ffn2", [2, D])
    ln2g_d = din("ln2_g", [2, D])
    ln2b_d = din("ln2_b", [2, D])
    c_ident = din("c_ident", [128, 128])
    c_J = din("c_J", [128, 128])
    c_uf = din("c_uf", [128, 129])
    c_ub = din("c_ub", [128, 129])
    c_sf = din("c_sf", [128, 128])
    c_sb = din("c_sb", [128, 128])
    c_onehot = din("c_onehot", [32, MLEN])
    out_d = nc.dram_tensor("out", [S, D], F32, kind="ExternalOutput").ap()
    xs_d = nc.dram_tensor("xs_scratch", [S, D], F32).ap()
    md_t = nc.dram_tensor("md_scratch", [4, MLEN], F32)
    eb_d = nc.dram_tensor("expb_scratch", [128, 4 * 1152], BF16).ap()
    md_d = md_t.ap()
    dbg_out = {}

    sc = Sched(nc)
    es = ExitStack()
    ARENA_BYTES = 207 * 1024
    arena = es.enter_context(nc.sbuf_tensor("arena", [128, ARENA_BYTES // 2], BF16))
    PSb = es.enter_context(nc.psum_tensor("ps", [128, 8, 1024], BF16))[:]
    PS = PSb.bitcast(F32)

    def view(off, nbytes, dt, pattern=None, **kw):
        assert off % 32 == 0, off
        a = arena[:, off // 2:(off + nbytes) // 2]
        if dt is F32:
            a = a.bitcast(F32)
        if pattern:
            a = a.rearrange(pattern, **kw)
        return a

    class Alloc:
        def __init__(self, base, size):
            self.base = base
            self.size = size
            self.pos = 0

        def reset(self):
            self.pos = 0

        def get(self, nbytes, dt, pattern=None, **kw):
            n = (nbytes + 31) // 32 * 32
            assert self.pos + n <= self.size, (self.pos, n, self.size)
            v = view(self.base + self.pos, nbytes, dt, pattern, **kw)
            self.pos += n
            return v

    R_XT = Alloc(0, 32768)
    R_X = Alloc(32768, 65536)
    R_D = Alloc(98304, 70656)
    R_W = Alloc(168960, 16384)
    R_C = Alloc(185344, ARENA_BYTES - 185344)

    XT = R_XT.get(32768, BF16, "p (c n) -> p c n", c=8)
    X = R_X.get(65536, F32, "p (t n) -> p t n", t=NT)
    WB = [R_W.get(8192, BF16, "p (c n) -> p c n", c=8) for _ in range(2)]
    R_W.reset()
    WO = R_W.get(16384, BF16, "p (c n) -> p c n", c=8)

    identb = R_C.get(256, BF16)
    Jb = R_C.get(256, BF16)
    Uf = R_C.get(516, F32)
    Ub = R_C.get(516, F32)
    Usf = R_C.get(512, F32)
    Usb = R_C.get(512, F32)
    gt = R_C.get(4096, F32)
    bt = R_C.get(4096, F32)
    b2t = R_C.get(4096, F32)
    wd_t = R_C.get(512, F32)
    wg_t = R_C.get(512, F32)
    b1c = R_C.get(128, F32)
    cb = R_C.get(32, F32, "p (s h) -> p s h", s=2)
    lamv = R_C.get(4 * 64 * 4, F32, "p (a n) -> p a n", a=4)
    lamp = R_C.get(2 * 64 * 4, F32, "p (a n) -> p a n", a=2)
    lams = R_C.get(32, F32)
    Wg = R_C.get(1024, BF16)
    st_ = [R_C.get(48, F32) for _ in range(2)]
    mv_ = [R_C.get(8, F32) for _ in range(2)]
    rs_ = [R_C.get(4, F32) for _ in range(2)]
    xb_ = [R_C.get(2048, BF16) for _ in range(2)]
    dsm = [R_C.get(64, F32) for _ in range(2)]
    gsm = [R_C.get(64, F32) for _ in range(2)]
    decs = R_C.get(2 * 2 * 16 * 4, F32, "p (d q t) -> p d q t", d=2, q=2)

    def psk(b):
        return [("ps", b, q) for q in range(4)]

    def bc_mid(ap2, n):
        a = ap2.ap
        return bass.AP(ap2.tensor, ap2.offset, [list(a[0]), [0, n], list(a[1])])

    def bc_last(ap2, n):
        a = ap2.ap
        return bass.AP(ap2.tensor, ap2.offset, [list(a[0]), list(a[1]), [0, n]])

    cur_layer = [-1]

    def dump(name, ap, shape, reads):
        nm = "%s@%d" % (name, cur_layer[0])
        if nm in dbg:
            name = nm
        elif name not in dbg or (cur_layer[0] >= 0 and cur_layer[0] != nl - 1):
            return
        t = nc.dram_tensor("dbg_" + name.replace("@", "_"), list(shape), ap.dtype, kind="ExternalOutput").ap()
        dbg_out[name] = sc.dma("sp", lambda e: e.dma_start(out=t, in_=ap), "dbg", reads=reads)

    try:
        sc.dma("pool", lambda e: e.dma_start(out=identb, in_=c_ident), writes=["identb"])
        sc.dma("pool", lambda e: e.dma_start(out=Jb, in_=c_J), writes=["Jb"])
        sc.dma("sp", lambda e: e.dma_start(out=Uf, in_=c_uf), writes=["Uf"])
        sc.dma("sp", lambda e: e.dma_start(out=Ub, in_=c_ub), writes=["Ub"])
        sc.dma("sp", lambda e: e.dma_start(out=Usf, in_=c_sf), writes=["Usf"])
        sc.dma("sp", lambda e: e.dma_start(out=Usb, in_=c_sb), writes=["Usb"])
        for si, row in enumerate((15, 31)):
            sc.dma("sp", lambda e, si=si, row=row: e.dma_start(out=cb[:, si, :], in_=table_d[row, :].partition_broadcast(128)),
                   writes=[("cb", si)])

        R_D.reset()
        tb = R_D.get(16, F32)
        oh = R_D.get(MLEN * 4, F32)
        msb = R_D.get(MLEN * 4, F32)
        sc.dma("sp", lambda e: e.dma_start(out=tb[0:32, :], in_=table_d), writes=["tb"])
        sc.dma("sp", lambda e: e.dma_start(out=oh[0:32, :], in_=c_onehot), writes=["oh"])
        sc.dma("sp", lambda e: e.dma_start(out=gt, in_=lnemb_g.partition_broadcast(128)), writes=["gt"])
        sc.dma("sp", lambda e: e.dma_start(out=bt, in_=lnemb_b.partition_broadcast(128)), writes=["bt"])
        for t in range(NT):
            sc.dma("sp", lambda e, t=t: e.dma_start(out=X[:, t, :], in_=x_d[t * 128:(t + 1) * 128, :]), writes=[("X", t)])
        for ci, (c0, cn) in enumerate(((0, 512), (512, 512), (1024, 256))):
            sc.op("pe", lambda e, ci=ci, c0=c0, cn=cn: e.matmul(PS[0:4, ci, 0:cn], lhsT=tb[0:32, :], rhs=oh[0:32, c0:c0 + cn], start=True, stop=True),
                  reads=["tb", "oh"], writes=psk(ci))
            sc.op("dve", lambda e, ci=ci, c0=c0, cn=cn: e.tensor_copy(out=msb[0:4, c0:c0 + cn], in_=PS[0:4, ci, 0:cn]),
                  reads=psk(ci), writes=["msb"])
        sc.dma("sp", lambda e: e.dma_start(out=md_d, in_=msb[0:4, :]), reads=["msb"], writes=["md"])
        R_D.reset()
        R_D.get(16384, BF16); R_D.get(16384, BF16); R_D.get(16 * 4 * 129 * 2, BF16)
        expB0 = R_D.get(4 * 1152 * 2, BF16, "p (h n) -> p h n", h=4)
        R_D.get(4096, BF16); R_D.get(1024, BF16)
        tmp_revs = [view(R_D.base + 16384 + i * 2304, 2304, BF16) for i in range(4)]
        for h in range(4):
            src = bass.AP(md_t, h * MLEN, [[1, 128], [1, 1152]])
            sc.dma("pool", lambda e, src=src, h=h: e.dma_start(out=tmp_revs[h], in_=src), reads=["md"], writes=[("tmp_rev", h)])
        for h in range(4):
            for ci, (c0, cn) in enumerate(((0, 512), (512, 512), (1024, 128))):
                sc.op("pe", lambda e, ci=ci, c0=c0, cn=cn, h=h: e.matmul(PS[:, ci, 0:cn], lhsT=Jb, rhs=tmp_revs[h][:, c0:c0 + cn], start=True, stop=True),
                      reads=["Jb", ("tmp_rev", h)], writes=psk(ci))
                sc.op("act", lambda e, h=h, ci=ci, c0=c0, cn=cn: e.activation(out=expB0[:, h, c0:c0 + cn], in_=PS[:, ci, 0:cn], func=AF.Exp),
                      reads=psk(ci), writes=[("expB", h)])
        sc.dma("sp", lambda e: e.dma_start(out=eb_d, in_=expB0.rearrange("p h n -> p (h n)")), reads=[("expB", h) for h in range(4)], writes=["eb_d"])

        def ln_a(t):
            Xt = X[:, t, :]
            kx = ("X", t)
            b = t % 2
            st, mv, rs = st_[b], mv_[b], rs_[b]
            sc.op("dve", lambda e: e.bn_stats(out=st[:, 0:6], in_=Xt[:, 0:512]), reads=[kx], writes=[("st", b, 0)])
            sc.op("dve", lambda e: e.bn_stats(out=st[:, 6:12], in_=Xt[:, 512:1024]), reads=[kx], writes=[("st", b, 1)])
            sc.op("dve", lambda e: e.bn_aggr(out=mv, in_=st), reads=[("st", b, 0), ("st", b, 1)], writes=[("mv", b)])
            sc.op("act", lambda e: e.activation(out=rs, in_=mv[:, 1:2], func=AF.Sqrt, bias=1e-5, scale=1.0), reads=[("mv", b)], writes=[("rs", b)])
            sc.op("dve", lambda e: e.reciprocal(out=rs, in_=rs), reads=[("rs", b)], writes=[("rs", b)])
            sc.op("dve", lambda e: e.tensor_scalar(out=Xt, in0=Xt, scalar1=mv[:, 0:1], scalar2=rs, op0=ALU.subtract, op1=ALU.mult),
                  reads=[kx, ("mv", b), ("rs", b)], writes=[kx])
            sc.op("dve", lambda e: e.tensor_tensor(out=Xt, in0=Xt, in1=gt, op=ALU.mult), reads=[kx, "gt"], writes=[kx])
            sc.op("pool", lambda e: e.tensor_tensor(out=Xt, in0=Xt, in1=bt, op=ALU.add), reads=[kx, "bt"], writes=[kx])

        def ln_b1(t, spill_to):
            Xt = X[:, t, :]
            kx = ("X", t)
            b = t % 2
            xb = xb_[b]
            if spill_to is not None:
                sc.dma("sp", lambda e: e.dma_start(out=spill_to[t * 128:(t + 1) * 128, :], in_=Xt), ("xs", t % 4), reads=[kx], writes=[("xsd", t)])
            if spill_to is out_d:
                return
            sc.op("act", lambda e: e.activation(out=xb, in_=Xt, func=AF.Copy), reads=[kx], writes=[("xb", b)])

        def ln_b2(t, spill_to):
            if spill_to is out_d:
                return
            b = t % 2
            xb = xb_[b]
            bank = 6 + b
            for c in range(8):
                sc.op("pe", lambda e, c=c: e.transpose(out=PSb[:, bank, c * 128:(c + 1) * 128], in_=xb[:, c * 128:(c + 1) * 128], identity=identb),
                      reads=[("xb", b), "identb"], writes=psk(bank))
            sc.op("act", lambda e: e.activation(out=XT[:, :, t * 128:(t + 1) * 128], in_=PSb[:, bank, :].rearrange("p (c n) -> p c n", c=8), func=AF.Copy),
                  reads=psk(bank), writes=[("XT", t)])

        def ln_all(spill_to):
            ln_a(0)
            for t in range(NT):
                if t + 1 < NT:
                    ln_a(t + 1)
                ln_b1(t, spill_to)
                ln_b2(t, spill_to)

        def load_ln_params(g_ap, b_ap):
            sc.dma("sp", lambda e: e.dma_start(out=gt, in_=g_ap.partition_broadcast(128)), writes=["gt"])
            sc.dma("sp", lambda e: e.dma_start(out=bt, in_=b_ap.partition_broadcast(128)), writes=["bt"])

        def load_win_block(l, blk, buf):
            c0 = blk * 512
            ncol = min(512, DIN - c0)
            src = w_in_d[l, :, c0:c0 + ncol].rearrange("(c p) n -> p c n", p=128)
            for hf in range(2):
                sc.dma("pool", lambda e, hf=hf: e.dma_start(out=WB[buf][:, hf * 4:(hf + 1) * 4, 0:ncol], in_=src[:, hf * 4:(hf + 1) * 4, :]),
                       writes=[("RW", buf, hf)])

        if stop == 'init':
            raise _Stop()
        load_win_block(0, 0, 0)
        load_win_block(0, 1, 1)
        ln_all(xs_d)
        dump("h0", X, [128, NT, 1024], [("X", t) for t in range(NT)])

        if stop == 'emb':
            raise _Stop()
        evac_rr = [0]

        def evac(out, in_, reads, writes, scale=None):
            evac_rr[0] ^= 1
            if evac_rr[0]:
                if scale is None:
                    sc.op("act", lambda e: e.activation(out=out, in_=in_, func=AF.Copy), reads=reads, writes=writes)
                else:
                    sc.op("act", lambda e: e.mul(out=out, in_=in_, mul=scale), reads=reads, writes=writes)
            else:
                if scale is None:
                    sc.op("dve", lambda e: e.tensor_copy(out=out, in_=in_), reads=reads, writes=writes)
                else:
                    sc.op("dve", lambda e: e.tensor_scalar(out=out, in0=in_, scalar1=scale, scalar2=None, op0=ALU.mult), reads=reads, writes=writes)

        def do_layer(l):
            lam_init = 0.8 - 0.6 * math.exp(-0.3 * l)
            cur_layer[0] = l
            if l == 0:
                sc.barrier()
            last = (l == nl - 1)
            R_D.reset()
            QT = R_D.get(16384, BF16, "p (h n) -> p h n", h=4)
            KT = R_D.get(16384, BF16, "p (h n) -> p h n", h=4)
            V = R_D.get(16 * 4 * 129 * 2, BF16, "p (t h e) -> p t h e", t=16, h=4)
            expB = R_D.get(4 * 1152 * 2, BF16, "p (h n) -> p h n", h=4)
            Eb = R_D.get(4096, BF16, "p (b m n) -> p b m n", b=2, m=2)
            d_y = R_D.get(4 * 128 * 2, BF16, "p (u n) -> p u n", u=4)
            _pu = R_D.pos
            silu_t = [R_D.get(2048, F32) for _ in range(2)]
            R_D.pos = _pu
            accS = R_D.get(8 * 129 * 4, F32, "p (a n) -> p a n", a=8)
            R_D.pos = _pu + 4608
            tmp_rev = R_D.get(1152 * 2, BF16)
            R_X.reset()
            gqT = R_X.get(8192, BF16, "p (c n) -> p c n", c=2)
            gkT = R_X.get(8192, BF16, "p (c n) -> p c n", c=2)
            gk_tok = R_X.get(8192, BF16, "p (t n) -> p t n", t=16)
            gv = R_X.get(16384, BF16, "p (t n) -> p t n", t=16)
            gr_s = R_X.get(16384, BF16, "p (t n) -> p t n", t=16)
            G33 = R_X.get(4096, BF16)

            for i, ap in enumerate((lq1_d, lk1_d, lq2_d, lk2_d)):
                sc.dma("sp", lambda e, i=i, ap=ap: e.dma_start(out=lamv[:, i, :], in_=ap[l, :].partition_broadcast(128)), writes=[("lamv", i)])
            sc.op("dve", lambda e: e.tensor_tensor(out=lamp[:, 0, :], in0=lamv[:, 0, :], in1=lamv[:, 1, :], op=ALU.mult), reads=[("lamv", 0), ("lamv", 1)], writes=["lamp"])
            sc.op("dve", lambda e: e.tensor_tensor(out=lamp[:, 1, :], in0=lamv[:, 2, :], in1=lamv[:, 3, :], op=ALU.mult), reads=[("lamv", 2), ("lamv", 3)], writes=["lamp"])
            sc.op("dve", lambda e: e.reduce_sum(out=lams[:, 0:2], in_=lamp, axis=mybir.AxisListType.X), reads=["lamp"], writes=["lams"])
            sc.op("act", lambda e: e.activation(out=lams[:, 0:2], in_=lams[:, 0:2], func=AF.Exp), reads=["lams"], writes=["lams"])
            sc.op("dve", lambda e: e.tensor_tensor(out=lams[:, 2:3], in0=lams[:, 0:1], in1=lams[:, 1:2], op=ALU.subtract), reads=["lams"], writes=["lams"])
            sc.op("dve", lambda e: e.tensor_scalar(out=lams[:, 3:4], in0=lams[:, 2:3], scalar1=lam_init, scalar2=-1.0, op0=ALU.add, op1=ALU.mult),
                  reads=["lams"], writes=["neglam"])
            neg_lam = lams[:, 3:4]
            sc.dma("sp", lambda e: e.dma_start(out=wd_t, in_=dnw_d[l, :].partition_broadcast(128)), writes=["wd"])
            sc.op("dve", lambda e: e.tensor_scalar(out=wd_t, in0=wd_t, scalar1=1.0 - lam_init, scalar2=None, op0=ALU.mult), reads=["wd"], writes=["wd"])
            sc.dma("sp", lambda e: e.dma_start(out=wg_t, in_=gnw_d[l, :].partition_broadcast(128)), writes=["wg"])
            sc.op("dve", lambda e: e.memset(Wg[0:33, :], 0.0), writes=["Wg"])
            sc.dma("pool", lambda e: e.dma_start(out=Wg[0:16, 0:256], in_=gup_d[l, 0]), writes=["Wg"])
            sc.dma("pool", lambda e: e.dma_start(out=Wg[16:32, 256:512], in_=gup_d[l, 1]), writes=["Wg"])
            sc.dma("pool", lambda e: e.dma_start(out=Wg[32:33, :], in_=gbias_d[l].rearrange("a n -> (a n)").partition_broadcast(1)), writes=["Wg"])
            sc.dma("sp", lambda e: e.dma_start(out=b1c, in_=b1_d[l].rearrange("(c p) -> p c", p=128), allow_slow_non_contiguous=True), writes=["b1c"])
            sc.dma("sp", lambda e: e.dma_start(out=b2t, in_=b2_d[l].partition_broadcast(128)), writes=["b2t"])
            sc.dma("sp", lambda e: e.dma_start(out=expB.rearrange("p h n -> p (h n)"), in_=eb_d), reads=["eb_d"], writes=[("expB", h) for h in range(4)])
            sc.op("dve", lambda e: e.memset(V[:, :, :, 128:129], 1.0), writes=[("V", t) for t in range(NT)])
            sc.op("dve", lambda e: e.memset(G33[32:33, :], 1.0), writes=["G33"])

            dump("XTin", XT, [128, 8, 2048], [("XT", t) for t in range(NT)])
            dump("Xin", X, [128, NT, 1024], [("X", t) for t in range(NT)])
            if stop == 'L' and l == nl - 1:
                raise _Stop()
            ps_rr = [0]

            def nextbank():
                b = ps_rr[0] % 6
                ps_rr[0] += 1
                return b

            xt_all = [("XT", t) for t in range(NT)]
            for blk in range(7):
                buf = blk % 2
                wb = WB[buf]
                kw = [("RW", buf, 0), ("RW", buf, 1)]
                if blk in (0, 1, 3, 6):
                    nch = 1 if blk == 6 else 4
                    for cc in range(nch):
                        for r in range(4):
                            bank = nextbank()
                            M = 32 if blk == 6 else 128
                            for c in range(8):
                                sc.op("pe", lambda e, c=c, cc=cc, r=r, bank=bank, M=M, wb=wb: e.matmul(
                                    PS[0:M, bank, :], lhsT=wb[:, c, cc * 128:cc * 128 + M], rhs=XT[:, c, r * 512:(r + 1) * 512],
                                    start=(c == 0), stop=(c == 7)), reads=kw + xt_all[r * 4:(r + 1) * 4], writes=psk(bank))
                            sl = slice(r * 512, (r + 1) * 512)
                            if blk == 0:
                                evac(QT[:, cc, sl], PS[:, bank, :], psk(bank), [("QT", cc, r)], scale=0.125)
                            elif blk == 1:
                                evac(KT[:, cc, sl], PS[:, bank, :], psk(bank), [("KT", cc, r)])
                            elif blk == 3:
                                if cc < 2:
                                    evac(gqT[:, cc, sl], PS[:, bank, :], psk(bank), [("gqT", 4 * r + i) for i in range(4)], scale=0.125)
                                else:
                                    evac(gkT[:, cc - 2, sl], PS[:, bank, :], psk(bank), [("gkT", 4 * r + i) for i in range(4)])
                            else:
                                evac(G33[0:32, sl], PS[0:32, bank, :], psk(bank), ["G33"])
                if blk == 3:
                    for t in range(NT):
                        bank = nextbank()
                        for cc in range(2):
                            sc.op("pe", lambda e, t=t, cc=cc, bank=bank: e.transpose(out=PSb[:, bank, cc * 128:(cc + 1) * 128], in_=gkT[:, cc, t * 128:(t + 1) * 128], identity=identb),
                                  reads=[("gkT", t), "identb"], writes=psk(bank))
                        evac(gk_tok[:, t, :], PSb[:, bank, 0:256], psk(bank), [("gk_tok", t)])
                if blk in (2, 4, 5):
                    for t in range(NT):
                        bank = nextbank()
                        c0, ncol = (0, 512)
                        for c in range(8):
                            sc.op("pe", lambda e, c=c, t=t, bank=bank, c0=c0, ncol=ncol, wb=wb: e.matmul(
                                PS[:, bank, 0:ncol], lhsT=XT[:, c, t * 128:(t + 1) * 128], rhs=wb[:, c, c0:c0 + ncol],
                                start=(c == 0), stop=(c == 7)), reads=kw + [("XT", t)], writes=psk(bank))
                        if blk == 2:
                            evac(V[:, t, :, 0:128], PS[:, bank, :].rearrange("p (h e) -> p h e", h=4), psk(bank), [("V", t)])
                        elif blk == 4:
                            evac(gv[:, t, :], PS[:, bank, :], psk(bank), [("gv", t)])
                        else:
                            sb = t % 2
                            sc.op("act", lambda e, bank=bank, sb=sb: e.activation(out=silu_t[sb], in_=PS[:, bank, :], func=AF.Silu),
                                  reads=psk(bank), writes=[("silu", sb)])
                            sc.op("dve", lambda e, t=t, sb=sb: e.tensor_tensor(
                                out=gr_s[:, t, :].rearrange("p (h e) -> p h e", h=4), in0=silu_t[sb].rearrange("p (h e) -> p h e", h=4),
                                in1=bc_mid(wg_t, 4), op=ALU.mult), reads=[("silu", sb), "wg"], writes=[("gr_s", t)])
                if blk + 2 < 7:
                    load_win_block(l, blk + 2, buf)
            for hf in range(2):
                sc.dma("pool", lambda e, hf=hf: e.dma_start(out=WO[:, hf * 4:(hf + 1) * 4, :],
                                                             in_=w_o_d[l].rearrange("(c p) n -> p c n", p=128)[:, hf * 4:(hf + 1) * 4, :]),
                       writes=[("RW", hf, 0), ("RW", hf, 1)])
            dump("QT", QT, [128, 4, 2048], [("QT", a, b) for a in range(4) for b in range(4)])
            dump("KT", KT, [128, 4, 2048], [("KT", a, b) for a in range(4) for b in range(4)])
            dump("V", V, [128, 16, 4, 129], [("V", t) for t in range(NT)])
            dump("expB", expB, [128, 4, 1152], [("expB", h) for h in range(4)])
            dump("gqT", gqT, [128, 2, 2048], [("gqT", t) for t in range(NT)])
            dump("gr_s", gr_s, [128, 16, 512], [("gr_s", t) for t in range(NT)])
            dump("G33", G33[0:33, :], [33, 2048], ["G33"])

            if stop == 'P' and l == nl - 1:
                raise _Stop()
            steps = [(h, r, j) for h in range(4) for r in range(4) for j in range(16)]

            def acc_ap(m, u):
                idx = m * 4 + u
                return PS[:, 4 + idx // 3, (idx % 3) * 160:(idx % 3) * 160 + 129]

            def acc_keys(m, u):
                return psk(4 + (m * 4 + u) // 3)

            Eb3 = view(R_X.base + 61440, 2048, BF16, "p (m n) -> p m n", m=2)
            EbL = [Eb[:, 0, :, :], Eb[:, 1, :, :], Eb3]

            def d_scores(i):
                h, r, j = steps[i]
                d = j - 4 * r
                mixed = (-1 <= d <= 4)
                sb = i % 2
                eb = i % 3
                E = EbL[eb]
                for m in range(2):
                    bank = sb * 2 + m
                    sc.op("pe", lambda e, h=h, r=r, j=j, m=m, bank=bank: e.matmul(
                        PS[:, bank, :], lhsT=KT[64 * m:64 * m + 64, h, j * 128:(j + 1) * 128],
                        rhs=QT[64 * m:64 * m + 64, h, r * 512:(r + 1) * 512], start=True, stop=True),
                        reads=[("KT", h, j // 4), ("QT", h, r)], writes=psk(bank))
                pk2 = psk(sb * 2) + psk(sb * 2 + 1)
                ek = [("E", eb, 0), ("E", eb, 1)]
                if mixed:
                    c0 = (4 - d) * 128
                    sc.op("act", lambda e, sb=sb, E=E: e.activation(out=E, in_=PS[:, sb * 2:sb * 2 + 2, :], func=AF.Exp),
                          reads=pk2, writes=ek)
                    for m in range(2):
                        sc.op("dve", lambda e, E=E, m=m, h=h, c0=c0: e.tensor_tensor(out=E[:, m, :], in0=E[:, m, :], in1=expB[:, h, c0:c0 + 512], op=ALU.mult),
                              reads=[("E", eb, m), ("expB", h)], writes=[("E", eb, m)])
                else:
                    side = 0 if d < 0 else 1
                    sc.op("act", lambda e, sb=sb, E=E, side=side, h=h: e.activation(
                        out=E, in_=PS[:, sb * 2:sb * 2 + 2, :], func=AF.Exp, bias=cb[:, side, h:h + 1]),
                        reads=pk2 + [("cb", 0), ("cb", 1)], writes=ek)

            def d_av(i):
                h, r, j = steps[i]
                eb = i % 3
                E = EbL[eb]
                for m in range(2):
                    for u in range(4):
                        sc.op("pe", lambda e, h=h, j=j, m=m, u=u, E=E: e.matmul(
                            acc_ap(m, u), lhsT=E[:, m, u * 128:(u + 1) * 128], rhs=V[:, j, h, 0:129],
                            start=(j == 0 and (m * 4 + u) % 3 == 0), stop=(j == 15), skip_group_check=True),
                            reads=[("E", eb, m), ("V", j)], writes=acc_keys(m, u))

            sm = dsm[0]
            ka = ["accS", ("silu", 0), ("silu", 1)]

            def d_final(h, r):
                sc.op("dve", lambda e: e.tensor_copy(out=accS[:, 0:3, :], in_=PS[:, 4, 0:480].rearrange("p (a n) -> p a n", a=3)[:, :, 0:129]),
                      reads=psk(4), writes=ka)
                sc.op("dve", lambda e: e.tensor_copy(out=accS[:, 3:6, :], in_=PS[:, 5, 0:480].rearrange("p (a n) -> p a n", a=3)[:, :, 0:129]),
                      reads=psk(5), writes=ka)
                sc.op("dve", lambda e: e.tensor_copy(out=accS[:, 6:8, :], in_=PS[:, 6, 0:320].rearrange("p (a n) -> p a n", a=2)[:, :, 0:129]),
                      reads=psk(6), writes=ka)

            def d_final2(h, r):
                sc.op("dve", lambda e: e.reciprocal(out=sm[:, 0:8], in_=accS[:, :, 128]), reads=ka[:1], writes=["dsm"])
                sc.op("dve", lambda e: e.tensor_scalar(out=sm[:, 4:8], in0=sm[:, 4:8], scalar1=neg_lam, scalar2=None, op0=ALU.mult),
                      reads=["dsm", "neglam"], writes=["dsm"])
                sc.op("dve", lambda e: e.memset(sm[:, 8:12], 0.0), writes=["dss"])

            def d_final_u(u):
                if True:
                    sc.op("dve", lambda e, u=u: e.tensor_scalar(out=accS[:, u, 0:128], in0=accS[:, u, 0:128], scalar1=sm[:, u:u + 1], scalar2=None, op0=ALU.mult),
                          reads=["dsm"] + ka[:1], writes=ka[:1])
                    sc.op("dve", lambda e, u=u: e.scalar_tensor_tensor(out=accS[:, u, 0:128], in0=accS[:, 4 + u, 0:128], scalar=sm[:, 4 + u:5 + u],
                                                                       in1=accS[:, u, 0:128], op0=ALU.mult, op1=ALU.add),
                          reads=["dsm"] + ka[:1], writes=ka[:1])
                    sc.op("dve", lambda e, u=u: e.scalar_tensor_tensor(out=accS[:, 4 + u, 0:128], in0=accS[:, u, 0:128], scalar=1.0, in1=accS[:, u, 0:128],
                                                                       op0=ALU.mult, op1=ALU.mult, accum_out=sm[:, 8 + u:9 + u]),
                          reads=ka[:1], writes=ka[:1] + ["dss"])

            def d_final_b(h, r):
                sc.op("act", lambda e: e.activation(out=sm[:, 12:16], in_=sm[:, 8:12], func=AF.Ln, bias=1e-5, scale=1.0 / 128), reads=["dss"], writes=["drs"])
                sc.op("act", lambda e: e.activation(out=sm[:, 12:16], in_=sm[:, 12:16], func=AF.Exp, scale=-0.5), reads=["drs"], writes=["drs"])
                for u in range(4):
                    sc.op("dve", lambda e, u=u: e.scalar_tensor_tensor(out=d_y[:, u, :], in0=accS[:, u, 0:128], scalar=sm[:, 12 + u:13 + u], in1=wd_t,
                                                                       op0=ALU.mult, op1=ALU.mult),
                          reads=ka[:1] + ["drs", "wd"], writes=[("dy", u)])

            def d_final_pe(h, r):
                for u in range(4):
                    sc.op("pe", lambda e, u=u: e.transpose(out=PSb[:, 7, u * 128:(u + 1) * 128], in_=d_y[:, u, :], identity=identb),
                          reads=[("dy", u), "identb"], writes=psk(7))
                sc.op("dve", lambda e, h=h, r=r: e.tensor_copy(out=XT[:, h, r * 512:(r + 1) * 512], in_=PSb[:, 7, 0:512]),
                      reads=psk(7), writes=[("XT", 4 * r + i) for i in range(4)])

            pend = []
            pend_b = []
            pend_u = []
            d_scores(0)
            d_scores(1)
            for i in range(len(steps)):
                h, r, j = steps[i]
                if j == 15:
                    d_av(i)
                    d_final(h, r)
                    if i + 2 < len(steps):
                        d_scores(i + 2)
                    d_final2(h, r)
                else:
                    if i + 2 < len(steps):
                        d_scores(i + 2)
                    d_av(i)
                if j == 15:
                    pend.append((h, r))
                    pend_b.append((h, r))
                    pend_u.extend([0, 1, 2, 3])
                    d_final_u(pend_u.pop(0))
                elif pend_u:
                    d_final_u(pend_u.pop(0))
                elif j == 4 and pend_b:
                    d_final_b(*pend_b.pop(0))
                elif j == 7 and pend:
                    d_final_pe(*pend.pop(0))
            while pend_u:
                d_final_u(pend_u.pop(0))
            while pend_b:
                d_final_b(*pend_b.pop(0))
            while pend:
                d_final_pe(*pend.pop(0))
            dump("mixT_d", XT, [128, 8, 2048], [("XT", t) for t in range(NT)])

            if stop == 'D' and l == nl - 1:
                raise _Stop()
            sc.barrier()
            R_D.reset()
            qf = R_D.get(8192, BF16, "p (c n) -> p c n", c=2)
            kf = R_D.get(8192, BF16, "p (c n) -> p c n", c=2)
            kd_f = R_D.get(8192, BF16, "p (t n) -> p t n", t=16)
            Sbf = R_D.get(16384, BF16, "p (d q t e) -> p d q t e", d=2, q=2, t=16)
            stm2 = [R_D.get(4096, F32, "p (d q n) -> p d q n", d=2, q=2) for _ in range(2)]
            _p0 = R_D.pos
            sp_ = [R_D.get(2048, F32) for _ in range(2)]
            _p1 = R_D.pos
            ebt = [R_D.get(2 * 2 * 129 * 4, F32, "p (d q n) -> p d q n", d=2, q=2) for _ in range(2)]
            _p2 = R_D.pos
            enbt = [R_D.get(2 * 2 * 128 * 4, F32, "p (d q n) -> p d q n", d=2, q=2) for _ in range(2)]
            erem = [R_D.get(2048, F32) for _ in range(2)]
            dS = [R_D.get(1024, F32) for _ in range(4)]
            _pend = R_D.pos
            Am = [R_X.get(4 * 2 * 128 * 2, BF16, "p (h d n) -> p h d n", h=4, d=2) for _ in range(2)]
            R_D.pos = _p2
            g_y = [R_D.get(1024, BF16) for _ in range(2)]
            g_junk = R_D.get(512, F32)
            R_D.pos = _pend
            qb, kb, kd_b = gqT, gkT, gk_tok
            maskf = Uf[:, 0:128]
            maskb = Usf

            def tl(t):
                return slice(t * 128, (t + 1) * 128)

            def prep_A(t):
                b = t % 2
                sp = sp_[b]
                zb = 0 if b == 0 else 7
                sc.op("pe", lambda e: e.matmul(PS[:, zb, :], lhsT=G33[0:33, tl(t)], rhs=Wg[0:33, :], start=True, stop=True),
                      reads=["G33", "Wg"], writes=psk(zb))
                sc.op("act", lambda e: e.activation(out=sp, in_=PS[:, zb, :], func=AF.Exp, scale=-1.0), reads=psk(zb), writes=[("sp", b)])
                sc.op("act", lambda e: e.activation(out=sp, in_=sp, func=AF.Ln, bias=1.0, scale=1.0), reads=[("sp", b)], writes=[("sp", b)])

            prep_A(0)
            for t in range(NT):
                b = t % 2
                sp = sp_[b]
                if t + 1 < NT:
                    prep_A(t + 1)
                sc.op("pe", lambda e, sp=sp: e.matmul(PS[:, 1, 0:256], lhsT=Usf, rhs=sp[:, 0:256], start=True, stop=True), reads=[("sp", b), "Usf", "Usb"], writes=psk(1))
                sc.op("pe", lambda e, sp=sp: e.matmul(PS[:, 1, 256:512], lhsT=Usb, rhs=sp[:, 256:512], start=True, stop=True), reads=[("sp", b), "Usf", "Usb"], writes=psk(1))
                sc.op("act", lambda e, b=b: e.activation(out=erem[b], in_=PS[:, 1, :], func=AF.Exp, scale=-1.0 / 16), reads=psk(1), writes=[("erem", b)])
                sc.op("dve", lambda e, t=t, b=b: e.tensor_tensor(out=kd_f[:, t, :], in0=gk_tok[:, t, :], in1=erem[b][:, 0:256], op=ALU.mult),
                      reads=[("gk_tok", t), ("erem", b)], writes=[("kd_f", t)])
                sc.op("dve", lambda e, t=t, b=b: e.tensor_tensor(out=kd_b[:, t, :], in0=gk_tok[:, t, :], in1=erem[b][:, 256:512], op=ALU.mult),
                      reads=[("gk_tok", t), ("erem", b), ("kd_f", t)], writes=[("gk_tok", t)])
                for d in range(2):
                    U = Uf if d == 0 else Ub
                    for q in range(2):
                        sc.op("pe", lambda e, sp=sp, d=d, q=q, U=U: e.matmul(PS[:, 2 + d, q * 160:q * 160 + 129],
                                                                            lhsT=sp[:, d * 256 + q * 128:d * 256 + (q + 1) * 128], rhs=U, start=True, stop=True),
                              reads=[("sp", b), "Uf", "Ub"], writes=psk(2 + d))
                src4 = PS[:, 2:4, 0:320].rearrange("p a (q n) -> p a q n", q=2)
                sc.op("act", lambda e, b=b, src4=src4: e.activation(out=ebt[b], in_=src4[:, :, :, 0:129], func=AF.Exp, scale=-1.0 / 16),
                      reads=psk(2) + psk(3), writes=[("eb", b, 0), ("eb", b, 1)])
                sc.op("act", lambda e, b=b, src4=src4: e.activation(out=enbt[b], in_=src4[:, :, :, 0:128], func=AF.Exp, scale=1.0 / 16),
                      reads=psk(2) + psk(3), writes=[("enb", b, 0), ("enb", b, 1)])
                sc.op("dve", lambda e, t=t, b=b: e.tensor_tensor(out=qf[:, :, tl(t)], in0=gqT[:, :, tl(t)], in1=ebt[b][:, 0, :, 0:128], op=ALU.mult),
                      reads=[("gqT", t), ("eb", b, 0)], writes=[("qf", t)])
                sc.op("dve", lambda e, t=t, b=b: e.tensor_tensor(out=kf[:, :, tl(t)], in0=gkT[:, :, tl(t)], in1=enbt[b][:, 0, :, :], op=ALU.mult),
                      reads=[("gkT", t), ("enb", b, 0)], writes=[("kf", t)])
                sc.op("dve", lambda e, t=t, b=b: e.tensor_tensor(out=qb[:, :, tl(t)], in0=gqT[:, :, tl(t)], in1=ebt[b][:, 1, :, 0:128], op=ALU.mult),
                      reads=[("gqT", t), ("eb", b, 1), ("qf", t)], writes=[("gqT", t)])
                sc.op("dve", lambda e, t=t, b=b: e.tensor_tensor(out=kb[:, :, tl(t)], in0=gkT[:, :, tl(t)], in1=enbt[b][:, 1, :, :], op=ALU.mult),
                      reads=[("gkT", t), ("enb", b, 1), ("kf", t)], writes=[("gkT", t)])
                sc.op("dve", lambda e, t=t, b=b: e.tensor_copy(out=decs[:, :, :, t:t + 1], in_=ebt[b][:, :, :, 128:129]),
                      reads=[("eb", b, 0), ("eb", b, 1)], writes=["decs"])
            dump("qf", qf, [128, 2, 2048], [("qf", t) for t in range(NT)])
            dump("kd_f", kd_f, [128, 16, 256], [("kd_f", t) for t in range(NT)])
            dump("decs", decs, [128, 2, 2, 16], ["decs"])

            if stop == 'G1' and l == nl - 1:
                raise _Stop()
            sc.op("dve", lambda e: e.memset(stm2[0], 0.0), writes=[("stm", 0, d, q) for d in range(2) for q in range(2)])
            chains = [(d, q) for d in range(2) for q in range(2)]
            par = {c: 0 for c in chains}
            for i in range(NT):
                todo = []
                for ci, (d, q) in enumerate(chains):
                    t = i if d == 0 else NT - 1 - i
                    cur = par[(d, q)]
                    if i > 0:
                        sc.op("act", lambda e, d=d, q=q, t=t, cur=cur: e.activation(out=Sbf[0:64, d, q, t, :], in_=stm2[cur][0:64, d, q, 0:128], func=AF.Copy),
                              reads=[("stm", cur, d, q)], writes=[("Sbf", d, q, t)])
                        sc.op("dve", lambda e, d=d, q=q, t=t, cur=cur: e.tensor_copy(out=Sbf[64:128, d, q, t, :], in_=stm2[cur][64:128, d, q, 128:256]),
                              reads=[("stm", cur, d, q)], writes=[("Sbf", d, q, t)])
                    if i == NT - 1:
                        continue
                    kd = kd_f if d == 0 else kd_b
                    kkey = "kd_f" if d == 0 else "gk_tok"
                    pslot = ci % 2
                    pk = psk(4 + pslot)
                    sc.op("pe", lambda e, kd=kd, t=t, q=q, pslot=pslot: e.matmul(PS[:, 4 + pslot, 0:256], lhsT=kd[:, t, q * 128:(q + 1) * 128],
                                                                                rhs=gv[:, t, q * 256:(q + 1) * 256], start=True, stop=True),
                          reads=[(kkey, t), ("gv", t)], writes=pk)
                    if os.environ.get("GSKIP") != "evac":
                        sc.op("act", lambda e, ci=ci, pslot=pslot: e.activation(out=dS[ci], in_=PS[:, 4 + pslot, 0:256], func=AF.Copy),
                              reads=pk, writes=[("dS", ci)])
                    todo.append((ci, d, q, t, cur))
                for (ci, d, q, t, cur) in todo:
                    if os.environ.get("GSKIP") == "upd":
                        par[(d, q)] = 1 - cur
                        continue
                    sc.op("dve", lambda e, ci=ci, d=d, q=q, t=t, cur=cur: e.scalar_tensor_tensor(
                        out=stm2[1 - cur][:, d, q, :], in0=stm2[cur][:, d, q, :], scalar=decs[:, d, q, t:t + 1], in1=dS[ci],
                        op0=ALU.mult, op1=ALU.add), reads=[("stm", cur, d, q), "decs", ("dS", ci)], writes=[("stm", 1 - cur, d, q)])
                    par[(d, q)] = 1 - cur
            dump("Sbf", Sbf, [128, 2, 2, 16, 128], [("Sbf", d, q, t) for d in range(2) for q in range(2) for t in range(NT)])
            if stop == 'G2' and l == nl - 1:
                raise _Stop()

            def g_A(t):
                b = t % 2
                A = Am[b]
                for half in range(2):
                    sbank = (4 + half) if os.environ.get('GBANK') else (2 * b + half)
                    items = []
                    for sq in range(4):
                        h, d = half + 2 * (sq // 2), sq % 2
                        q = h // 2
                        base = (h % 2) * 64
                        kk = kf if d == 0 else kb
                        qq = qf if d == 0 else qb
                        kkey = ("kf", t) if d == 0 else ("gkT", t)
                        qkey = ("qf", t) if d == 0 else ("gqT", t)
                        sc.op("pe", lambda e, kk=kk, qq=qq, q=q, base=base, sbank=sbank, sq=sq: e.matmul(
                            PS[:, sbank, sq * 128:(sq + 1) * 128], lhsT=kk[base:base + 64, q, tl(t)], rhs=qq[base:base + 64, q, tl(t)], start=True, stop=True),
                            reads=[kkey, qkey], writes=psk(sbank))
                        items.append((sq, h, d))
                        if os.environ.get("GOLD"):
                            mk = maskf if d == 0 else maskb
                            sc.op("dve", lambda e, A=A, h=h, d=d, sbank=sbank, sq=sq, mk=mk: e.tensor_tensor(
                                out=A[:, h, d, :], in0=PS[:, sbank, sq * 128:(sq + 1) * 128], in1=mk, op=ALU.mult),
                                reads=psk(sbank) + ["Uf", "Usf"], writes=[("A", b, h, d), ("sp", b)])
                    if os.environ.get("GOLD"):
                        continue
                    for (sq, h, d) in items:
                        mk = maskf if d == 0 else maskb
                        sc.op("dve", lambda e, A=A, h=h, d=d, sbank=sbank, sq=sq, mk=mk: e.tensor_tensor(
                            out=A[:, h, d, :], in0=PS[:, sbank, sq * 128:(sq + 1) * 128], in1=mk, op=ALU.mult),
                            reads=psk(sbank) + ["Uf", "Usf"], writes=[("A", b, h, d), ("sp", b)])

            def g_B(t):
                b = t % 2
                A = Am[b]
                obank = (0 + b) if os.environ.get('GBANK') else (4 + b)
                for h in range(4):
                    q = h // 2
                    base = (h % 2) * 64
                    oh_ = PS[:, obank, h * 128:(h + 1) * 128]
                    ok = psk(obank)
                    inter_f = t > 0
                    inter_b = t < NT - 1
                    sc.op("pe", lambda e, A=A, h=h, oh_=oh_: e.matmul(oh_, lhsT=A[:, h, 0, :], rhs=gv[:, t, h * 128:(h + 1) * 128], start=True, stop=False),
                          reads=[("A", b, h, 0), ("gv", t)], writes=ok)
                    sc.op("pe", lambda e, A=A, h=h, oh_=oh_, fin=(not inter_f and not inter_b): e.matmul(
                        oh_, lhsT=A[:, h, 1, :], rhs=gv[:, t, h * 128:(h + 1) * 128], start=False, stop=fin),
                        reads=[("A", b, h, 1), ("gv", t)], writes=ok)
                    if inter_f:
                        sc.op("pe", lambda e, q=q, base=base, oh_=oh_, fin=(not inter_b): e.matmul(
                            oh_, lhsT=qf[base:base + 64, q, tl(t)], rhs=Sbf[base:base + 64, 0, q, t, :], start=False, stop=fin),
                            reads=[("qf", t), ("Sbf", 0, q, t)], writes=ok)
                    if inter_b:
                        sc.op("pe", lambda e, q=q, base=base, oh_=oh_: e.matmul(
                            oh_, lhsT=qb[base:base + 64, q, tl(t)], rhs=Sbf[base:base + 64, 1, q, t, :], start=False, stop=True),
                            reads=[("gqT", t), ("Sbf", 1, q, t)], writes=ok)

            def g_norm(t):
                b = t % 2
                sm = gsm[b]
                obank = (0 + b) if os.environ.get('GBANK') else (4 + b)
                okall = psk(obank)
                for h in range(4):
                    sc.op("act", lambda e, h=h, sm=sm: e.activation(out=g_junk, in_=PS[:, obank, h * 128:(h + 1) * 128], func=AF.Square, accum_out=sm[:, h:h + 1]),
                          reads=okall, writes=["gjunk", ("gss", b), ("enb", 1, 0), ("enb", 1, 1)])
                sc.op("act", lambda e, sm=sm: e.activation(out=sm[:, 4:8], in_=sm[:, 0:4], func=AF.Sqrt, bias=1e-5, scale=1.0 / 128),
                      reads=[("gss", b)], writes=[("grs", b)])
                sc.op("dve", lambda e, sm=sm: e.reciprocal(out=sm[:, 4:8], in_=sm[:, 4:8]), reads=[("grs", b)], writes=[("grs", b)])
                for h in range(4):
                    sc.op("dve", lambda e, h=h, sm=sm, b=b: e.scalar_tensor_tensor(
                        out=g_y[b][:, h * 128:(h + 1) * 128], in0=PS[:, obank, h * 128:(h + 1) * 128], scalar=sm[:, 4 + h:5 + h],
                        in1=gr_s[:, t, h * 128:(h + 1) * 128], op0=ALU.mult, op1=ALU.mult),
                        reads=psk(obank) + [("grs", b), ("gr_s", t)], writes=[("gy", b), ("enb", 0, 0), ("enb", 0, 1)])

            def g_tr(t):
                b = t % 2
                tk = psk(6 + b)
                for h in range(4):
                    sc.op("pe", lambda e, h=h, b=b: e.transpose(out=PSb[:, 6 + b, h * 128:(h + 1) * 128], in_=g_y[b][:, h * 128:(h + 1) * 128], identity=identb),
                          reads=[("gy", b), "identb"], writes=tk)
                sc.op("act", lambda e, b=b: e.activation(out=XT[:, 4:8, tl(t)], in_=PSb[:, 6 + b, 0:512].rearrange("p (c n) -> p c n", c=4), func=AF.Copy),
                      reads=tk, writes=[("XT", t)])

            g_A(0)
            for t in range(NT):
                if t + 1 < NT:
                    g_A(t + 1)
                if os.environ.get("GSKIP") == "B":
                    continue
                g_B(t)
                if os.environ.get("GSKIP") == "norm":
                    continue
                g_norm(t)
                if os.environ.get("GSKIP") == "tr":
                    continue
                if t > 0:
                    g_tr(t - 1)
            if not os.environ.get("GSKIP"):
                g_tr(NT - 1)
            dump("mixT", XT, [128, 8, 2048], [("XT", t) for t in range(NT)])

            if stop == 'G' and l == nl - 1:
                raise _Stop()
            sc.barrier()
            load_ln_params(ln1g_d[l], ln1b_d[l])
            R_D.reset()
            W1B = [R_D.get(8192, BF16, "p (c n) -> p c n", c=8) for _ in range(2)]
            W2B = [R_D.get(8192, BF16, "p (c n) -> p c n", c=4) for _ in range(2)]
            hT = R_D.get(16384, BF16, "p (c n) -> p c n", c=4)
            relu_t = [R_D.get(2048, F32) for _ in range(2)]

            def load_ffn_block(fb, buf):
                s1 = w1_d[l, :, fb * 512:(fb + 1) * 512].rearrange("(c p) n -> p c n", p=128)
                s2 = w2_d[l, fb * 512:(fb + 1) * 512, :].rearrange("(c p) n -> p c n", p=128)
                for hf in range(2):
                    sc.dma("pool", lambda e, hf=hf: e.dma_start(out=W1B[buf][:, hf * 4:(hf + 1) * 4, :], in_=s1[:, hf * 4:(hf + 1) * 4, :]),
                           writes=[("W1B", buf, hf)])
                for hf in range(2):
                    sc.dma("pool", lambda e, hf=hf: e.dma_start(out=W2B[buf][:, hf * 2:(hf + 1) * 2, :], in_=s2[:, hf * 2:(hf + 1) * 2, :]),
                           writes=[("W2B", buf, hf)])

            load_ffn_block(0, 0)
            load_ffn_block(1, 1)
            for t in range(NT):
                sc.dma("sp", lambda e, t=t: e.dma_start(out=X[:, t, :], in_=xs_d[t * 128:(t + 1) * 128, :]), reads=[("xsd", t)], writes=[("X", t)])
            def o_mm(t):
                yb = (t % 3) * 2
                for hf in range(2):
                    for c in range(8):
                        sc.op("pe", lambda e, c=c, hf=hf, t=t, yb=yb: e.matmul(PS[:, yb + hf, :], lhsT=XT[:, c, tl(t)], rhs=WO[:, c, hf * 512:(hf + 1) * 512],
                                                                              start=(c == 0), stop=(c == 7)),
                              reads=[("XT", t), ("RW", 0, 0), ("RW", 0, 1), ("RW", 1, 0), ("RW", 1, 1)], writes=psk(yb + hf))

            def o_ln(t):
                yb = (t % 3) * 2
                sc.op("dve", lambda e, t=t, yb=yb: e.scalar_tensor_tensor(out=X[:, t, :], in0=X[:, t, :], scalar=ALPHA,
                                                                          in1=PS[:, yb:yb + 2, :].rearrange("p a n -> p (a n)"), op0=ALU.mult, op1=ALU.add),
                      reads=[("X", t)] + psk(yb) + psk(yb + 1), writes=[("X", t)])
                ln_a(t)

            for t0 in range(3):
                o_mm(t0)
                o_ln(t0)
            for t in range(NT):
                if t + 3 < NT:
                    o_mm(t + 3)
                ln_b1(t, None)
                if t + 3 < NT:
                    o_ln(t + 3)
                ln_b2(t, None)
            dump("x1T", XT, [128, 8, 2048], [("XT", t) for t in range(NT)])

            if stop == 'O' and l == nl - 1:
                raise _Stop()
            if not last:
                load_win_block(l + 1, 0, 0)
                load_win_block(l + 1, 1, 1)
            hrr = [0]
            for fb in range(8):
                buf = fb % 2
                for r in range(4):
                    for fc in range(4):
                        bank = 4 + hrr[0] % 3
                        rb = hrr[0] % 2
                        hrr[0] += 1
                        for c in range(8):
                            sc.op("pe", lambda e, c=c, fc=fc, r=r, bank=bank, buf=buf: e.matmul(
                                PS[:, bank, :], lhsT=W1B[buf][:, c, fc * 128:(fc + 1) * 128], rhs=XT[:, c, r * 512:(r + 1) * 512],
                                start=(c == 0), stop=(c == 7)), reads=[("W1B", buf, 0), ("W1B", buf, 1)] + xt_all[r * 4:(r + 1) * 4], writes=psk(bank))
                        fcol = fb * 4 + fc
                        sc.op("act", lambda e, bank=bank, rb=rb, fcol=fcol: e.activation(out=relu_t[rb], in_=PS[:, bank, :], func=AF.Relu,
                                                                                         bias=b1c[:, fcol:fcol + 1], scale=1.0),
                              reads=psk(bank) + ["b1c"], writes=[("relu", rb)])
                        sc.op("dve", lambda e, rb=rb, fc=fc, r=r: e.tensor_tensor(out=hT[:, fc, r * 512:(r + 1) * 512], in0=relu_t[rb], in1=relu_t[rb], op=ALU.mult),
                              reads=[("relu", rb)], writes=[("hT", fc, r)])
                for t in range(NT):
                    yb = (t % 2) * 2
                    for hf in range(2):
                        for fc in range(4):
                            sc.op("pe", lambda e, fc=fc, hf=hf, t=t, yb=yb, buf=buf: e.matmul(
                                PS[:, yb + hf, :], lhsT=hT[:, fc, tl(t)], rhs=W2B[buf][:, fc, hf * 512:(hf + 1) * 512],
                                start=(fc == 0), stop=(fc == 3)), reads=[("hT", fc, t // 4), ("W2B", buf, 0), ("W2B", buf, 1)], writes=psk(yb + hf))
                    if fb == 0:
                        sc.op("dve", lambda e, t=t, yb=yb: e.scalar_tensor_tensor(out=X[:, t, :], in0=X[:, t, :], scalar=ALPHA,
                                                                                  in1=PS[:, yb:yb + 2, :].rearrange("p a n -> p (a n)"), op0=ALU.mult, op1=ALU.add),
                              reads=[("X", t)] + psk(yb) + psk(yb + 1), writes=[("X", t)])
                        sc.op("pool", lambda e, t=t: e.tensor_tensor(out=X[:, t, :], in0=X[:, t, :], in1=b2t, op=ALU.add), reads=[("X", t), "b2t"], writes=[("X", t)])
                    else:
                        sc.op("dve", lambda e, t=t, yb=yb: e.tensor_tensor(out=X[:, t, :], in0=X[:, t, :], in1=PS[:, yb:yb + 2, :].rearrange("p a n -> p (a n)"), op=ALU.add),
                              reads=[("X", t)] + psk(yb) + psk(yb + 1), writes=[("X", t)])
                if fb + 2 < 8:
                    load_ffn_block(fb + 2, buf)
            if stop == 'F' and l == nl - 1:
                raise _Stop()
            load_ln_params(ln2g_d[l], ln2b_d[l])
            ln_all(out_d if last else xs_d)
            sc.barrier()


        for _l in range(nl):
            do_layer(_l)
    except _Stop:
        pass
    out_dmas = [o for o in sc.ops if o.is_dma and o.dkey in [("xs", i) for i in range(4)]]
    fin = {}
    for o in out_dmas:
        fin[o.dkey] = o
    finals = list(fin.values()) + list(dbg_out.values())
    sc.emit(final_wait_ops=finals)
    es.close()
    return nc, sc


_CONST = None


def kernel(**inputs):
    global _CONST
    if _CONST is None:
        _CONST = _constants()
    nc, _ = build(2)
    x = np.ascontiguousarray(inputs["x"], dtype=np.float32)
    shared = {k: np.ascontiguousarray(v, dtype=np.float32) for k, v in inputs.items() if k != "x"}
    shared.update(_CONST)
    in_maps = []
    for b in range(8):
        m = dict(shared)
        m["x"] = x[b]
        in_maps.append(m)
    res = run_bass_kernel_spmd(nc, in_maps, core_ids=list(range(8)))
    return np.stack([r["out"] for r in res.results], axis=0).astype(np.float32)
```

```python
import math
import os
from contextlib import ExitStack

import numpy as np
import concourse.bass as bass
import concourse.mybir as mybir
from concourse.bass_utils import run_bass_kernel_spmd

F32 = mybir.dt.float32
BF16 = mybir.dt.bfloat16
AF = mybir.ActivationFunctionType
ALU = mybir.AluOpType

S = 2048
D = 1024
DIN = 3104
DFF = 4096
NT = 16
ALPHA = (2.0 * 2) ** 0.25
ENGS = ("pe", "act", "dve", "pool", "sp")
EPOCH = 30000


class _Res:
    __slots__ = ("last_w", "readers")

    def __init__(self):
        self.last_w = None
        self.readers = []


class _Op:
    __slots__ = ("eng", "fn", "deps", "signal", "tok", "is_dma", "dkey")

    def __init__(self, eng, fn, is_dma, dkey):
        self.eng = eng
        self.fn = fn
        self.deps = []
        self.signal = False
        self.tok = None
        self.is_dma = is_dma
        self.dkey = dkey


class Sched:
    def __init__(self, nc):
        self.nc = nc
        self.ops = []
        self.res = {}
        self.pending = {e: [] for e in ENGS}

    def _r(self, key):
        x = self.res.get(key)
        if x is None:
            x = self.res[key] = _Res()
        return x

    def _add(self, op, reads, writes):
        deps = set()
        for k in reads:
            rs = self._r(k)
            if rs.last_w is not None:
                deps.add(rs.last_w)
        for k in writes:
            rs = self._r(k)
            if rs.last_w is not None:
                deps.add(rs.last_w)
            deps.update(rs.readers)
        for k in reads:
            self._r(k).readers.append(op)
        for k in writes:
            rs = self._r(k)
            rs.last_w = op
            rs.readers = []
        if self.pending[op.eng]:
            deps.update(self.pending[op.eng])
            self.pending[op.eng] = []
        deps.discard(op)
        op.deps = list(deps)
        self.ops.append(op)
        return op

    def op(self, eng, fn, reads=(), writes=()):
        return self._add(_Op(eng, fn, False, None), reads, writes)

    def dma(self, eng, fn, dkey=None, reads=(), writes=()):
        if dkey is None:
            dkey = ("w", writes[0])
        return self._add(_Op(eng, fn, True, dkey), reads, writes)

    def barrier(self):
        last = {}
        for o in self.ops:
            last[(o.eng, o.dkey) if o.is_dma else o.eng] = o
        b = list(last.values())
        self.pending = {e: list(b) for e in ENGS}

    def emit(self, final_wait_ops=()):
        nc = self.nc
        ops = self.ops
        for o in ops:
            for d in o.deps:
                if d.is_dma:
                    d.signal = True
                elif d.eng == "pe" and o.eng == "pe" and not o.is_dma:
                    continue
                else:
                    d.signal = True
        with ExitStack() as es:
            eng_sems = {e: [] for e in ENGS}
            cnt = {e: 0 for e in ENGS}
            dma_sems = {}
            dma_cnt = {}
            for o in ops:
                if o.is_dma:
                    if o.dkey not in dma_sems:
                        dma_sems[o.dkey] = es.enter_context(nc.semaphore("d%d" % len(dma_sems)))
                        dma_cnt[o.dkey] = 0
                    dma_cnt[o.dkey] += 16
                    o.tok = (dma_sems[o.dkey], dma_cnt[o.dkey])
                elif o.signal:
                    ep = cnt[o.eng] // EPOCH
                    if ep >= len(eng_sems[o.eng]):
                        eng_sems[o.eng].append(es.enter_context(nc.semaphore("e_%s_%d" % (o.eng, ep))))
                    cnt[o.eng] += 1
                    o.tok = (eng_sems[o.eng][ep], cnt[o.eng] - ep * EPOCH)
            per_eng = {e: [o for o in ops if o.eng == e] for e in ENGS}
            self.stats = {e: len(per_eng[e]) for e in ENGS}
            self.stats["sems"] = sum(len(v) for v in eng_sems.values()) + len(dma_sems)

            def run(e, eng):
                waited = {}
                for o in per_eng[e]:
                    need = {}
                    for d in o.deps:
                        if d.tok is None:
                            continue
                        if (not d.is_dma) and d.eng == "pe" and e == "pe" and not o.is_dma:
                            continue
                        s, v = d.tok
                        k = id(s)
                        if waited.get(k, 0) >= v:
                            continue
                        if k not in need or need[k][1] < v:
                            need[k] = (s, v)
                    for k, (s, v) in need.items():
                        eng.wait_ge(s, v)
                        waited[k] = v
                    ins = o.fn(eng)
                    if o.tok is not None:
                        ins.then_inc(o.tok[0], 16 if o.is_dma else 1)
                if e == "sp":
                    for o in final_wait_ops:
                        s, v = o.tok
                        eng.wait_ge(s, v)

            with nc.Block() as block:
                @block.sync
                def _(eng):
                    run("sp", eng)

                @block.tensor
                def _(eng):
                    run("pe", eng)

                @block.scalar
                def _(eng):
                    run("act", eng)

                @block.vector
                def _(eng):
                    run("dve", eng)

                @block.gpsimd
                def _(eng):
                    run("pool", eng)


def _t5_bucket(rel):
    nb = 16
    me = 8
    ret = np.where(rel > 0, nb, 0)
    n = np.abs(rel)
    large = me + (np.log(np.maximum(n, 1).astype(np.float32) / np.float32(me))
                  / np.float32(math.log(128 / me)) * np.float32(nb - me)).astype(np.int32)
    large = np.minimum(large, nb - 1)
    return ret + np.where(n < me, n, large)


MLEN = 1280


def _constants():
    c = {}
    c["c_ident"] = np.eye(128, dtype=np.float32)
    c["c_J"] = np.eye(128, dtype=np.float32)[::-1].copy()
    s = np.arange(128)[:, None]
    t = np.arange(128)[None, :]
    uf = np.zeros((128, 129), np.float32)
    uf[:, :128] = (s <= t)
    uf[:, 128] = 1.0
    ub = np.zeros((128, 129), np.float32)
    ub[:, :128] = (s >= t)
    ub[:, 128] = 1.0
    c["c_uf"] = uf
    c["c_ub"] = ub
    c["c_sf"] = (s > t).astype(np.float32)
    c["c_sb"] = (s < t).astype(np.float32)
    n = np.arange(MLEN)
    bk = _t5_bucket(639 - n)
    oh = np.zeros((32, MLEN), np.float32)
    oh[bk, n] = 1.0
    c["c_onehot"] = oh
    return c


class _Stop(Exception):
    pass


def build(nl=2, dbg=(), stop=None):
    nc = bass.Bass("TRN2", target_bir_lowering=False)

    def din(name, shape):
        return nc.dram_tensor(name, list(shape), F32, kind="ExternalInput").ap()

    x_d = din("x", [S, D])
    lnemb_g = din("ln_emb_g", [D])
    lnemb_b = din("ln_emb_b", [D])
    table_d = din("rel_bias_table", [32, 4])
    w_in_d = din("w_in", [2, D, DIN])
    lq1_d = din("lambda_q1", [2, 64])
    lk1_d = din("lambda_k1", [2, 64])
    lq2_d = din("lambda_q2", [2, 64])
    lk2_d = din("lambda_k2", [2, 64])
    dnw_d = din("diff_norm_w", [2, 128])
    gup_d = din("gla_gate_up", [2, 2, 16, 256])
    gbias_d = din("gla_gate_bias", [2, 2, 256])
    gnw_d = din("gla_norm_w", [2, 128])
    w_o_d = din("w_o", [2, D, D])
    ln1g_d = din("ln1_g", [2, D])
    ln1b_d = din("ln1_b", [2, D])
    w1_d = din("w_ffn1", [2, D, DFF])
    b1_d = din("b_ffn1", [2, DFF])
    w2_d = din("w_ffn2", [2, DFF, D])
    b2_d = din("b_ffn2", [2, D])
    ln2g_d = din("ln2_g", [2, D])
    ln2b_d = din("ln2_b", [2, D])
    c_ident = din("c_ident", [128, 128])
    c_J = din("c_J", [128, 128])
    c_uf = din("c_uf", [128, 129])
    c_ub = din("c_ub", [128, 129])
    c_sf = din("c_sf", [128, 128])
    c_sb = din("c_sb", [128, 128])
    c_onehot = din("c_onehot", [32, MLEN])
    out_d = nc.dram_tensor("out", [S, D], F32, kind="ExternalOutput").ap()
    xs_d = nc.dram_tensor("xs_scratch", [S, D], F32).ap()
    md_t = nc.dram_tensor("md_scratch", [4, MLEN], F32)
    eb_d = nc.dram_tensor("expb_scratch", [128, 4 * 1152], BF16).ap()
    md_d = md_t.ap()
    dbg_out = {}

    sc = Sched(nc)
    es = ExitStack()
    ARENA_BYTES = 207 * 1024
    arena = es.enter_context(nc.sbuf_tensor("arena", [128, ARENA_BYTES // 2], BF16))
    PSb = es.enter_context(nc.psum_tensor("ps", [128, 8, 1024], BF16))[:]
    PS = PSb.bitcast(F32)

    def view(off, nbytes, dt, pattern=None, **kw):
        assert off % 32 == 0, off
        a = arena[:, off // 2:(off + nbytes) // 2]
        if dt is F32:
            a = a.bitcast(F32)
        if pattern:
            a = a.rearrange(pattern, **kw)
        return a

    class Alloc:
        def __init__(self, base, size):
            self.base = base
            self.size = size
            self.pos = 0

        def reset(self):
            self.pos = 0

        def get(self, nbytes, dt, pattern=None, **kw):
            n = (nbytes + 31) // 32 * 32
            assert self.pos + n <= self.size, (self.pos, n, self.size)
            v = view(self.base + self.pos, nbytes, dt, pattern, **kw)
            self.pos += n
            return v

    R_XT = Alloc(0, 32768)
    R_X = Alloc(32768, 65536)
    R_D = Alloc(98304, 70656)
    R_W = Alloc(168960, 16384)
    R_C = Alloc(185344, ARENA_BYTES - 185344)

    XT = R_XT.get(32768, BF16, "p (c n) -> p c n", c=8)
    X = R_X.get(65536, F32, "p (t n) -> p t n", t=NT)
    WB = [R_W.get(8192, BF16, "p (c n) -> p c n", c=8) for _ in range(2)]
    R_W.reset()
    WO = R_W.get(16384, BF16, "p (c n) -> p c n", c=8)

    identb = R_C.get(256, BF16)
    Jb = R_C.get(256, BF16)
    Uf = R_C.get(516, F32)
    Ub = R_C.get(516, F32)
    Usf = R_C.get(512, F32)
    Usb = R_C.get(512, F32)
    gt = R_C.get(4096, F32)
    bt = R_C.get(4096, F32)
    b2t = R_C.get(4096, F32)
    wd_t = R_C.get(512, F32)
    wg_t = R_C.get(512, F32)
    b1c = R_C.get(128, F32)
    cb = R_C.get(32, F32, "p (s h) -> p s h", s=2)
    lamv = R_C.get(4 * 64 * 4, F32, "p (a n) -> p a n", a=4)
    lamp = R_C.get(2 * 64 * 4, F32, "p (a n) -> p a n", a=2)
    lams = R_C.get(32, F32)
    Wg = R_C.get(1024, BF16)
    st_ = [R_C.get(48, F32) for _ in range(2)]
    mv_ = [R_C.get(8, F32) for _ in range(2)]
    rs_ = [R_C.get(4, F32) for _ in range(2)]
    xb_ = [R_C.get(2048, BF16) for _ in range(2)]
    dsm = [R_C.get(64, F32) for _ in range(2)]
    gsm = [R_C.get(64, F32) for _ in range(2)]
    decs = R_C.get(2 * 2 * 16 * 4, F32, "p (d q t) -> p d q t", d=2, q=2)

    def psk(b):
        return [("ps", b, q) for q in range(4)]

    def bc_mid(ap2, n):
        a = ap2.ap
        return bass.AP(ap2.tensor, ap2.offset, [list(a[0]), [0, n], list(a[1])])

    def bc_last(ap2, n):
        a = ap2.ap
        return bass.AP(ap2.tensor, ap2.offset, [list(a[0]), list(a[1]), [0, n]])

    cur_layer = [-1]

    def dump(name, ap, shape, reads):
        nm = "%s@%d" % (name, cur_layer[0])
        if nm in dbg:
            name = nm
        elif name not in dbg or (cur_layer[0] >= 0 and cur_layer[0] != nl - 1):
            return
        t = nc.dram_tensor("dbg_" + name.replace("@", "_"), list(shape), ap.dtype, kind="ExternalOutput").ap()
        dbg_out[name] = sc.dma("sp", lambda e: e.dma_start(out=t, in_=ap), "dbg", reads=reads)

    try:
        sc.dma("pool", lambda e: e.dma_start(out=identb, in_=c_ident), writes=["identb"])
        sc.dma("pool", lambda e: e.dma_start(out=Jb, in_=c_J), writes=["Jb"])
        sc.dma("sp", lambda e: e.dma_start(out=Uf, in_=c_uf), writes=["Uf"])
        sc.dma("sp", lambda e: e.dma_start(out=Ub, in_=c_ub), writes=["Ub"])
        sc.dma("sp", lambda e: e.dma_start(out=Usf, in_=c_sf), writes=["Usf"])
        sc.dma("sp", lambda e: e.dma_start(out=Usb, in_=c_sb), writes=["Usb"])
        for si, row in enumerate((15, 31)):
            sc.dma("sp", lambda e, si=si, row=row: e.dma_start(out=cb[:, si, :], in_=table_d[row, :].partition_broadcast(128)),
                   writes=[("cb", si)])

        R_D.reset()
        tb = R_D.get(16, F32)
        oh = R_D.get(MLEN * 4, F32)
        msb = R_D.get(MLEN * 4, F32)
        sc.dma("sp", lambda e: e.dma_start(out=tb[0:32, :], in_=table_d), writes=["tb"])
        sc.dma("sp", lambda e: e.dma_start(out=oh[0:32, :], in_=c_onehot), writes=["oh"])
        sc.dma("sp", lambda e: e.dma_start(out=gt, in_=lnemb_g.partition_broadcast(128)), writes=["gt"])
        sc.dma("sp", lambda e: e.dma_start(out=bt, in_=lnemb_b.partition_broadcast(128)), writes=["bt"])
        for t in range(NT):
            sc.dma("sp", lambda e, t=t: e.dma_start(out=X[:, t, :], in_=x_d[t * 128:(t + 1) * 128, :]), writes=[("X", t)])
        for ci, (c0, cn) in enumerate(((0, 512), (512, 512), (1024, 256))):
            sc.op("pe", lambda e, ci=ci, c0=c0, cn=cn: e.matmul(PS[0:4, ci, 0:cn], lhsT=tb[0:32, :], rhs=oh[0:32, c0:c0 + cn], start=True, stop=True),
                  reads=["tb", "oh"], writes=psk(ci))
            sc.op("dve", lambda e, ci=ci, c0=c0, cn=cn: e.tensor_copy(out=msb[0:4, c0:c0 + cn], in_=PS[0:4, ci, 0:cn]),
                  reads=psk(ci), writes=["msb"])
        sc.dma("sp", lambda e: e.dma_start(out=md_d, in_=msb[0:4, :]), reads=["msb"], writes=["md"])
        R_D.reset()
        R_D.get(16384, BF16); R_D.get(16384, BF16); R_D.get(16 * 4 * 129 * 2, BF16)
        expB0 = R_D.get(4 * 1152 * 2, BF16, "p (h n) -> p h n", h=4)
        R_D.get(4096, BF16); R_D.get(1024, BF16)
        tmp_revs = [view(R_D.base + 16384 + i * 2304, 2304, BF16) for i in range(4)]
        for h in range(4):
            src = bass.AP(md_t, h * MLEN, [[1, 128], [1, 1152]])
            sc.dma("pool", lambda e, src=src, h=h: e.dma_start(out=tmp_revs[h], in_=src), reads=["md"], writes=[("tmp_rev", h)])
        for h in range(4):
            for ci, (c0, cn) in enumerate(((0, 512), (512, 512), (1024, 128))):
                sc.op("pe", lambda e, ci=ci, c0=c0, cn=cn, h=h: e.matmul(PS[:, ci, 0:cn], lhsT=Jb, rhs=tmp_revs[h][:, c0:c0 + cn], start=True, stop=True),
                      reads=["Jb", ("tmp_rev", h)], writes=psk(ci))
                sc.op("act", lambda e, h=h, ci=ci, c0=c0, cn=cn: e.activation(out=expB0[:, h, c0:c0 + cn], in_=PS[:, ci, 0:cn], func=AF.Exp),
                      reads=psk(ci), writes=[("expB", h)])
        sc.dma("sp", lambda e: e.dma_start(out=eb_d, in_=expB0.rearrange("p h n -> p (h n)")), reads=[("expB", h) for h in range(4)], writes=["eb_d"])

        def ln_a(t):
            Xt = X[:, t, :]
            kx = ("X", t)
            b = t % 2
            st, mv, rs = st_[b], mv_[b], rs_[b]
            sc.op("dve", lambda e: e.bn_stats(out=st[:, 0:6], in_=Xt[:, 0:512]), reads=[kx], writes=[("st", b, 0)])
            sc.op("dve", lambda e: e.bn_stats(out=st[:, 6:12], in_=Xt[:, 512:1024]), reads=[kx], writes=[("st", b, 1)])
            sc.op("dve", lambda e: e.bn_aggr(out=mv, in_=st), reads=[("st", b, 0), ("st", b, 1)], writes=[("mv", b)])
            sc.op("act", lambda e: e.activation(out=rs, in_=mv[:, 1:2], func=AF.Sqrt, bias=1e-5, scale=1.0), reads=[("mv", b)], writes=[("rs", b)])
            sc.op("dve", lambda e: e.reciprocal(out=rs, in_=rs), reads=[("rs", b)], writes=[("rs", b)])
            sc.op("dve", lambda e: e.tensor_scalar(out=Xt, in0=Xt, scalar1=mv[:, 0:1], scalar2=rs, op0=ALU.subtract, op1=ALU.mult),
                  reads=[kx, ("mv", b), ("rs", b)], writes=[kx])
            sc.op("dve", lambda e: e.tensor_tensor(out=Xt, in0=Xt, in1=gt, op=ALU.mult), reads=[kx, "gt"], writes=[kx])
            sc.op("pool", lambda e: e.tensor_tensor(out=Xt, in0=Xt, in1=bt, op=ALU.add), reads=[kx, "bt"], writes=[kx])

        def ln_b1(t, spill_to):
            Xt = X[:, t, :]
            kx = ("X", t)
            b = t % 2
            xb = xb_[b]
            if spill_to is not None:
                sc.dma("sp", lambda e: e.dma_start(out=spill_to[t * 128:(t + 1) * 128, :], in_=Xt), ("xs", t % 4), reads=[kx], writes=[("xsd", t)])
            if spill_to is out_d:
                return
            sc.op("act", lambda e: e.activation(out=xb, in_=Xt, func=AF.Copy), reads=[kx], writes=[("xb", b)])

        def ln_b2(t, spill_to):
            if spill_to is out_d:
                return
            b = t % 2
            xb = xb_[b]
            bank = 6 + b
            for c in range(8):
                sc.op("pe", lambda e, c=c: e.transpose(out=PSb[:, bank, c * 128:(c + 1) * 128], in_=xb[:, c * 128:(c + 1) * 128], identity=identb),
                      reads=[("xb", b), "identb"], writes=psk(bank))
            sc.op("act", lambda e: e.activation(out=XT[:, :, t * 128:(t + 1) * 128], in_=PSb[:, bank, :].rearrange("p (c n) -> p c n", c=8), func=AF.Copy),
                  reads=psk(bank), writes=[("XT", t)])

        def ln_all(spill_to):
            ln_a(0)
            for t in range(NT):
                if t + 1 < NT:
                    ln_a(t + 1)
                ln_b1(t, spill_to)
                ln_b2(t, spill_to)

        def load_ln_params(g_ap, b_ap):
            sc.dma("sp", lambda e: e.dma_start(out=gt, in_=g_ap.partition_broadcast(128)), writes=["gt"])
            sc.dma("sp", lambda e: e.dma_start(out=bt, in_=b_ap.partition_broadcast(128)), writes=["bt"])

        def load_win_block(l, blk, buf):
            c0 = blk * 512
            ncol = min(512, DIN - c0)
            src = w_in_d[l, :, c0:c0 + ncol].rearrange("(c p) n -> p c n", p=128)
            for hf in range(2):
                sc.dma("pool", lambda e, hf=hf: e.dma_start(out=WB[buf][:, hf * 4:(hf + 1) * 4, 0:ncol], in_=src[:, hf * 4:(hf + 1) * 4, :]),
                       writes=[("RW", buf, hf)])

        if stop == 'init':
            raise _Stop()
        load_win_block(0, 0, 0)
        load_win_block(0, 1, 1)
        ln_all(xs_d)
        dump("h0", X, [128, NT, 1024], [("X", t) for t in range(NT)])

        if stop == 'emb':
            raise _Stop()
        evac_rr = [0]

        def evac(out, in_, reads, writes, scale=None):
            evac_rr[0] ^= 1
            if evac_rr[0]:
                if scale is None:
                    sc.op("act", lambda e: e.activation(out=out, in_=in_, func=AF.Copy), reads=reads, writes=writes)
                else:
                    sc.op("act", lambda e: e.mul(out=out, in_=in_, mul=scale), reads=reads, writes=writes)
            else:
                if scale is None:
                    sc.op("dve", lambda e: e.tensor_copy(out=out, in_=in_), reads=reads, writes=writes)
                else:
                    sc.op("dve", lambda e: e.tensor_scalar(out=out, in0=in_, scalar1=scale, scalar2=None, op0=ALU.mult), reads=reads, writes=writes)

        def do_layer(l):
            lam_init = 0.8 - 0.6 * math.exp(-0.3 * l)
            cur_layer[0] = l
            if l == 0:
                sc.barrier()
            last = (l == nl - 1)
            R_D.reset()
            QT = R_D.get(16384, BF16, "p (h n) -> p h n", h=4)
            KT = R_D.get(16384, BF16, "p (h n) -> p h n", h=4)
            V = R_D.get(16 * 4 * 129 * 2, BF16, "p (t h e) -> p t h e", t=16, h=4)
            expB = R_D.get(4 * 1152 * 2, BF16, "p (h n) -> p h n", h=4)
            Eb = R_D.get(4096, BF16, "p (b m n) -> p b m n", b=2, m=2)
            d_y = R_D.get(4 * 128 * 2, BF16, "p (u n) -> p u n", u=4)
            _pu = R_D.pos
            silu_t = [R_D.get(2048, F32) for _ in range(2)]
            R_D.pos = _pu
            accS = R_D.get(8 * 129 * 4, F32, "p (a n) -> p a n", a=8)
            R_D.pos = _pu + 4608
            tmp_rev = R_D.get(1152 * 2, BF16)
            R_X.reset()
            gqT = R_X.get(8192, BF16, "p (c n) -> p c n", c=2)
            gkT = R_X.get(8192, BF16, "p (c n) -> p c n", c=2)
            gk_tok = R_X.get(8192, BF16, "p (t n) -> p t n", t=16)
            gv = R_X.get(16384, BF16, "p (t n) -> p t n", t=16)
            gr_s = R_X.get(16384, BF16, "p (t n) -> p t n", t=16)
            G33 = R_X.get(4096, BF16)

            for i, ap in enumerate((lq1_d, lk1_d, lq2_d, lk2_d)):
                sc.dma("sp", lambda e, i=i, ap=ap: e.dma_start(out=lamv[:, i, :], in_=ap[l, :].partition_broadcast(128)), writes=[("lamv", i)])
            sc.op("dve", lambda e: e.tensor_tensor(out=lamp[:, 0, :], in0=lamv[:, 0, :], in1=lamv[:, 1, :], op=ALU.mult), reads=[("lamv", 0), ("lamv", 1)], writes=["lamp"])
            sc.op("dve", lambda e: e.tensor_tensor(out=lamp[:, 1, :], in0=lamv[:, 2, :], in1=lamv[:, 3, :], op=ALU.mult), reads=[("lamv", 2), ("lamv", 3)], writes=["lamp"])
            sc.op("dve", lambda e: e.reduce_sum(out=lams[:, 0:2], in_=lamp, axis=mybir.AxisListType.X), reads=["lamp"], writes=["lams"])
            sc.op("act", lambda e: e.activation(out=lams[:, 0:2], in_=lams[:, 0:2], func=AF.Exp), reads=["lams"], writes=["lams"])
            sc.op("dve", lambda e: e.tensor_tensor(out=lams[:, 2:3], in0=lams[:, 0:1], in1=lams[:, 1:2], op=ALU.subtract), reads=["lams"], writes=["lams"])
            sc.op("dve", lambda e: e.tensor_scalar(out=lams[:, 3:4], in0=lams[:, 2:3], scalar1=lam_init, scalar2=-1.0, op0=ALU.add, op1=ALU.mult),
                  reads=["lams"], writes=["neglam"])
            neg_lam = lams[:, 3:4]
            sc.dma("sp", lambda e: e.dma_start(out=wd_t, in_=dnw_d[l, :].partition_broadcast(128)), writes=["wd"])
            sc.op("dve", lambda e: e.tensor_scalar(out=wd_t, in0=wd_t, scalar1=1.0 - lam_init, scalar2=None, op0=ALU.mult), reads=["wd"], writes=["wd"])
            sc.dma("sp", lambda e: e.dma_start(out=wg_t, in_=gnw_d[l, :].partition_broadcast(128)), writes=["wg"])
            sc.op("dve", lambda e: e.memset(Wg[0:33, :], 0.0), writes=["Wg"])
            sc.dma("pool", lambda e: e.dma_start(out=Wg[0:16, 0:256], in_=gup_d[l, 0]), writes=["Wg"])
            sc.dma("pool", lambda e: e.dma_start(out=Wg[16:32, 256:512], in_=gup_d[l, 1]), writes=["Wg"])
            sc.dma("pool", lambda e: e.dma_start(out=Wg[32:33, :], in_=gbias_d[l].rearrange("a n -> (a n)").partition_broadcast(1)), writes=["Wg"])
            sc.dma("sp", lambda e: e.dma_start(out=b1c, in_=b1_d[l].rearrange("(c p) -> p c", p=128), allow_slow_non_contiguous=True), writes=["b1c"])
            sc.dma("sp", lambda e: e.dma_start(out=b2t, in_=b2_d[l].partition_broadcast(128)), writes=["b2t"])
            sc.dma("sp", lambda e: e.dma_start(out=expB.rearrange("p h n -> p (h n)"), in_=eb_d), reads=["eb_d"], writes=[("expB", h) for h in range(4)])
            sc.op("dve", lambda e: e.memset(V[:, :, :, 128:129], 1.0), writes=[("V", t) for t in range(NT)])
            sc.op("dve", lambda e: e.memset(G33[32:33, :], 1.0), writes=["G33"])

            dump("XTin", XT, [128, 8, 2048], [("XT", t) for t in range(NT)])
            dump("Xin", X, [128, NT, 1024], [("X", t) for t in range(NT)])
            if stop == 'L' and l == nl - 1:
                raise _Stop()
            ps_rr = [0]

            def nextbank():
                b = ps_rr[0] % 6
                ps_rr[0] += 1
                return b

            xt_all = [("XT", t) for t in range(NT)]
            for blk in range(7):
                buf = blk % 2
                wb = WB[buf]
                kw = [("RW", buf, 0), ("RW", buf, 1)]
                if blk in (0, 1, 3, 6):
                    nch = 1 if blk == 6 else 4
                    for cc in range(nch):
                        for r in range(4):
                            bank = nextbank()
                            M = 32 if blk == 6 else 128
                            for c in range(8):
                                sc.op("pe", lambda e, c=c, cc=cc, r=r, bank=bank, M=M, wb=wb: e.matmul(
                                    PS[0:M, bank, :], lhsT=wb[:, c, cc * 128:cc * 128 + M], rhs=XT[:, c, r * 512:(r + 1) * 512],
                                    start=(c == 0), stop=(c == 7)), reads=kw + xt_all[r * 4:(r + 1) * 4], writes=psk(bank))
                            sl = slice(r * 512, (r + 1) * 512)
                            if blk == 0:
                                evac(QT[:, cc, sl], PS[:, bank, :], psk(bank), [("QT", cc, r)], scale=0.125)
                            elif blk == 1:
                                evac(KT[:, cc, sl], PS[:, bank, :], psk(bank), [("KT", cc, r)])
                            elif blk == 3:
                                if cc < 2:
                                    evac(gqT[:, cc, sl], PS[:, bank, :], psk(bank), [("gqT", 4 * r + i) for i in range(4)], scale=0.125)
                                else:
                                    evac(gkT[:, cc - 2, sl], PS[:, bank, :], psk(bank), [("gkT", 4 * r + i) for i in range(4)])
                            else:
                                evac(G33[0:32, sl], PS[0:32, bank, :], psk(bank), ["G33"])
                if blk == 3:
                    for t in range(NT):
                        bank = nextbank()
                        for cc in range(2):
                            sc.op("pe", lambda e, t=t, cc=cc, bank=bank: e.transpose(out=PSb[:, bank, cc * 128:(cc + 1) * 128], in_=gkT[:, cc, t * 128:(t + 1) * 128], identity=identb),
                                  reads=[("gkT", t), "identb"], writes=psk(bank))
                        evac(gk_tok[:, t, :], PSb[:, bank, 0:256], psk(bank), [("gk_tok", t)])
                if blk in (2, 4, 5):
                    for t in range(NT):
                        bank = nextbank()
                        c0, ncol = (0, 512)
                        for c in range(8):
                            sc.op("pe", lambda e, c=c, t=t, bank=bank, c0=c0, ncol=ncol, wb=wb: e.matmul(
                                PS[:, bank, 0:ncol], lhsT=XT[:, c, t * 128:(t + 1) * 128], rhs=wb[:, c, c0:c0 + ncol],
                                start=(c == 0), stop=(c == 7)), reads=kw + [("XT", t)], writes=psk(bank))
                        if blk == 2:
                            evac(V[:, t, :, 0:128], PS[:, bank, :].rearrange("p (h e) -> p h e", h=4), psk(bank), [("V", t)])
                        elif blk == 4:
                            evac(gv[:, t, :], PS[:, bank, :], psk(bank), [("gv", t)])
                        else:
                            sb = t % 2
                            sc.op("act", lambda e, bank=bank, sb=sb: e.activation(out=silu_t[sb], in_=PS[:, bank, :], func=AF.Silu),
                                  reads=psk(bank), writes=[("silu", sb)])
                            sc.op("dve", lambda e, t=t, sb=sb: e.tensor_tensor(
                                out=gr_s[:, t, :].rearrange("p (h e) -> p h e", h=4), in0=silu_t[sb].rearrange("p (h e) -> p h e", h=4),
                                in1=bc_mid(wg_t, 4), op=ALU.mult), reads=[("silu", sb), "wg"], writes=[("gr_s", t)])
                if blk + 2 < 7:
                    load_win_block(l, blk + 2, buf)
            for hf in range(2):
                sc.dma("pool", lambda e, hf=hf: e.dma_start(out=WO[:, hf * 4:(hf + 1) * 4, :],
                                                             in_=w_o_d[l].rearrange("(c p) n -> p c n", p=128)[:, hf * 4:(hf + 1) * 4, :]),
                       writes=[("RW", hf, 0), ("RW", hf, 1)])
            dump("QT", QT, [128, 4, 2048], [("QT", a, b) for a in range(4) for b in range(4)])
            dump("KT", KT, [128, 4, 2048], [("KT", a, b) for a in range(4) for b in range(4)])
            dump("V", V, [128, 16, 4, 129], [("V", t) for t in range(NT)])
            dump("expB", expB, [128, 4, 1152], [("expB", h) for h in range(4)])
            dump("gqT", gqT, [128, 2, 2048], [("gqT", t) for t in range(NT)])
            dump("gr_s", gr_s, [128, 16, 512], [("gr_s", t) for t in range(NT)])
            dump("G33", G33[0:33, :], [33, 2048], ["G33"])

            if stop == 'P' and l == nl - 1:
                raise _Stop()
            steps = [(h, r, j) for h in range(4) for r in range(4) for j in range(16)]

            def acc_ap(m, u):
                idx = m * 4 + u
                return PS[:, 4 + idx // 3, (idx % 3) * 160:(idx % 3) * 160 + 129]

            def acc_keys(m, u):
                return psk(4 + (m * 4 + u) // 3)

            Eb3 = view(R_X.base + 61440, 2048, BF16, "p (m n) -> p m n", m=2)
            EbL = [Eb[:, 0, :, :], Eb[:, 1, :, :], Eb3]

            def d_scores(i):
                h, r, j = steps[i]
                d = j - 4 * r
                mixed = (-1 <= d <= 4)
                sb = i % 2
                eb = i % 3
                E = EbL[eb]
                for m in range(2):
                    bank = sb * 2 + m
                    sc.op("pe", lambda e, h=h, r=r, j=j, m=m, bank=bank: e.matmul(
                        PS[:, bank, :], lhsT=KT[64 * m:64 * m + 64, h, j * 128:(j + 1) * 128],
                        rhs=QT[64 * m:64 * m + 64, h, r * 512:(r + 1) * 512], start=True, stop=True),
                        reads=[("KT", h, j // 4), ("QT", h, r)], writes=psk(bank))
                pk2 = psk(sb * 2) + psk(sb * 2 + 1)
                ek = [("E", eb, 0), ("E", eb, 1)]
                if mixed:
                    c0 = (4 - d) * 128
                    sc.op("act", lambda e, sb=sb, E=E: e.activation(out=E, in_=PS[:, sb * 2:sb * 2 + 2, :], func=AF.Exp),
                          reads=pk2, writes=ek)
                    for m in range(2):
                        sc.op("dve", lambda e, E=E, m=m, h=h, c0=c0: e.tensor_tensor(out=E[:, m, :], in0=E[:, m, :], in1=expB[:, h, c0:c0 + 512], op=ALU.mult),
                              reads=[("E", eb, m), ("expB", h)], writes=[("E", eb, m)])
                else:
                    side = 0 if d < 0 else 1
                    sc.op("act", lambda e, sb=sb, E=E, side=side, h=h: e.activation(
                        out=E, in_=PS[:, sb * 2:sb * 2 + 2, :], func=AF.Exp, bias=cb[:, side, h:h + 1]),
                        reads=pk2 + [("cb", 0), ("cb", 1)], writes=ek)

            def d_av(i):
                h, r, j = steps[i]
                eb = i % 3
                E = EbL[eb]
                for m in range(2):
                    for u in range(4):
                        sc.op("pe", lambda e, h=h, j=j, m=m, u=u, E=E: e.matmul(
                            acc_ap(m, u), lhsT=E[:, m, u * 128:(u + 1) * 128], rhs=V[:, j, h, 0:129],
                            start=(j == 0 and (m * 4 + u) % 3 == 0), stop=(j == 15), skip_group_check=True),
                            reads=[("E", eb, m), ("V", j)], writes=acc_keys(m, u))

            sm = dsm[0]
            ka = ["accS", ("silu", 0), ("silu", 1)]

            def d_final(h, r):
                sc.op("dve", lambda e: e.tensor_copy(out=accS[:, 0:3, :], in_=PS[:, 4, 0:480].rearrange("p (a n) -> p a n", a=3)[:, :, 0:129]),
                      reads=psk(4), writes=ka)
                sc.op("dve", lambda e: e.tensor_copy(out=accS[:, 3:6, :], in_=PS[:, 5, 0:480].rearrange("p (a n) -> p a n", a=3)[:, :, 0:129]),
                      reads=psk(5), writes=ka)
                sc.op("dve", lambda e: e.tensor_copy(out=accS[:, 6:8, :], in_=PS[:, 6, 0:320].rearrange("p (a n) -> p a n", a=2)[:, :, 0:129]),
                      reads=psk(6), writes=ka)

            def d_final2(h, r):
                sc.op("dve", lambda e: e.reciprocal(out=sm[:, 0:8], in_=accS[:, :, 128]), reads=ka[:1], writes=["dsm"])
                sc.op("dve", lambda e: e.tensor_scalar(out=sm[:, 4:8], in0=sm[:, 4:8], scalar1=neg_lam, scalar2=None, op0=ALU.mult),
                      reads=["dsm", "neglam"], writes=["dsm"])
                sc.op("dve", lambda e: e.memset(sm[:, 8:12], 0.0), writes=["dss"])

            def d_final_u(u):
                if True:
                    sc.op("dve", lambda e, u=u: e.tensor_scalar(out=accS[:, u, 0:128], in0=accS[:, u, 0:128], scalar1=sm[:, u:u + 1], scalar2=None, op0=ALU.mult),
                          reads=["dsm"] + ka[:1], writes=ka[:1])
                    sc.op("dve", lambda e, u=u: e.scalar_tensor_tensor(out=accS[:, u, 0:128], in0=accS[:, 4 + u, 0:128], scalar=sm[:, 4 + u:5 + u],
                                                                       in1=accS[:, u, 0:128], op0=ALU.mult, op1=ALU.add),
                          reads=["dsm"] + ka[:1], writes=ka[:1])
                    sc.op("dve", lambda e, u=u: e.scalar_tensor_tensor(out=accS[:, 4 + u, 0:128], in0=accS[:, u, 0:128], scalar=1.0, in1=accS[:, u, 0:128],
                                                                       op0=ALU.mult, op1=ALU.mult, accum_out=sm[:, 8 + u:9 + u]),
                          reads=ka[:1], writes=ka[:1] + ["dss"])

            def d_final_b(h, r):
                sc.op("act", lambda e: e.activation(out=sm[:, 12:16], in_=sm[:, 8:12], func=AF.Ln, bias=1e-5, scale=1.0 / 128), reads=["dss"], writes=["drs"])
                sc.op("act", lambda e: e.activation(out=sm[:, 12:16], in_=sm[:, 12:16], func=AF.Exp, scale=-0.5), reads=["drs"], writes=["drs"])
                for u in range(4):
                    sc.op("dve", lambda e, u=u: e.scalar_tensor_tensor(out=d_y[:, u, :], in0=accS[:, u, 0:128], scalar=sm[:, 12 + u:13 + u], in1=wd_t,
                                                                       op0=ALU.mult, op1=ALU.mult),
                          reads=ka[:1] + ["drs", "wd"], writes=[("dy", u)])

            def d_final_pe(h, r):
                for u in range(4):
                    sc.op("pe", lambda e, u=u: e.transpose(out=PSb[:, 7, u * 128:(u + 1) * 128], in_=d_y[:, u, :], identity=identb),
                          reads=[("dy", u), "identb"], writes=psk(7))
                sc.op("dve", lambda e, h=h, r=r: e.tensor_copy(out=XT[:, h, r * 512:(r + 1) * 512], in_=PSb[:, 7, 0:512]),
                      reads=psk(7), writes=[("XT", 4 * r + i) for i in range(4)])

            pend = []
            pend_b = []
            pend_u = []
            d_scores(0)
            d_scores(1)
            for i in range(len(steps)):
                h, r, j = steps[i]
                if j == 15:
                    d_av(i)
                    d_final(h, r)
                    if i + 2 < len(steps):
                        d_scores(i + 2)
                    d_final2(h, r)
                else:
                    if i + 2 < len(steps):
                        d_scores(i + 2)
                    d_av(i)
                if j == 15:
                    pend.append((h, r))
                    pend_b.append((h, r))
                    pend_u.extend([0, 1, 2, 3])
                    d_final_u(pend_u.pop(0))
                elif pend_u:
                    d_final_u(pend_u.pop(0))
                elif j == 4 and pend_b:
                    d_final_b(*pend_b.pop(0))
                elif j == 7 and pend:
                    d_final_pe(*pend.pop(0))
            while pend_u:
                d_final_u(pend_u.pop(0))
            while pend_b:
                d_final_b(*pend_b.pop(0))
            while pend:
                d_final_pe(*pend.pop(0))
            dump("mixT_d", XT, [128, 8, 2048], [("XT", t) for t in range(NT)])

            if stop == 'D' and l == nl - 1:
                raise _Stop()
            sc.barrier()
            R_D.reset()
            qf = R_D.get(8192, BF16, "p (c n) -> p c n", c=2)
            kf = R_D.get(8192, BF16, "p (c n) -> p c n", c=2)
            kd_f = R_D.get(8192, BF16, "p (t n) -> p t n", t=16)
            Sbf = R_D.get(16384, BF16, "p (d q t e) -> p d q t e", d=2, q=2, t=16)
            stm2 = [R_D.get(4096, F32, "p (d q n) -> p d q n", d=2, q=2) for _ in range(2)]
            _p0 = R_D.pos
            sp_ = [R_D.get(2048, F32) for _ in range(2)]
            _p1 = R_D.pos
            ebt = [R_D.get(2 * 2 * 129 * 4, F32, "p (d q n) -> p d q n", d=2, q=2) for _ in range(2)]
            _p2 = R_D.pos
            enbt = [R_D.get(2 * 2 * 128 * 4, F32, "p (d q n) -> p d q n", d=2, q=2) for _ in range(2)]
            erem = [R_D.get(2048, F32) for _ in range(2)]
            dS = [R_D.get(1024, F32) for _ in range(4)]
            _pend = R_D.pos
            Am = [R_X.get(4 * 2 * 128 * 2, BF16, "p (h d n) -> p h d n", h=4, d=2) for _ in range(2)]
            R_D.pos = _p2
            g_y = [R_D.get(1024, BF16) for _ in range(2)]
            g_junk = R_D.get(512, F32)
            R_D.pos = _pend
            qb, kb, kd_b = gqT, gkT, gk_tok
            maskf = Uf[:, 0:128]
            maskb = Usf

            def tl(t):
                return slice(t * 128, (t + 1) * 128)

            def prep_A(t):
                b = t % 2
                sp = sp_[b]
                zb = 0 if b == 0 else 7
                sc.op("pe", lambda e: e.matmul(PS[:, zb, :], lhsT=G33[0:33, tl(t)], rhs=Wg[0:33, :], start=True, stop=True),
                      reads=["G33", "Wg"], writes=psk(zb))
                sc.op("act", lambda e: e.activation(out=sp, in_=PS[:, zb, :], func=AF.Exp, scale=-1.0), reads=psk(zb), writes=[("sp", b)])
                sc.op("act", lambda e: e.activation(out=sp, in_=sp, func=AF.Ln, bias=1.0, scale=1.0), reads=[("sp", b)], writes=[("sp", b)])

            prep_A(0)
            for t in range(NT):
                b = t % 2
                sp = sp_[b]
                if t + 1 < NT:
                    prep_A(t + 1)
                sc.op("pe", lambda e, sp=sp: e.matmul(PS[:, 1, 0:256], lhsT=Usf, rhs=sp[:, 0:256], start=True, stop=True), reads=[("sp", b), "Usf", "Usb"], writes=psk(1))
                sc.op("pe", lambda e, sp=sp: e.matmul(PS[:, 1, 256:512], lhsT=Usb, rhs=sp[:, 256:512], start=True, stop=True), reads=[("sp", b), "Usf", "Usb"], writes=psk(1))
                sc.op("act", lambda e, b=b: e.activation(out=erem[b], in_=PS[:, 1, :], func=AF.Exp, scale=-1.0 / 16), reads=psk(1), writes=[("erem", b)])
                sc.op("dve", lambda e, t=t, b=b: e.tensor_tensor(out=kd_f[:, t, :], in0=gk_tok[:, t, :], in1=erem[b][:, 0:256], op=ALU.mult),
                      reads=[("gk_tok", t), ("erem", b)], writes=[("kd_f", t)])
                sc.op("dve", lambda e, t=t, b=b: e.tensor_tensor(out=kd_b[:, t, :], in0=gk_tok[:, t, :], in1=erem[b][:, 256:512], op=ALU.mult),
                      reads=[("gk_tok", t), ("erem", b), ("kd_f", t)], writes=[("gk_tok", t)])
                for d in range(2):
                    U = Uf if d == 0 else Ub
                    for q in range(2):
                        sc.op("pe", lambda e, sp=sp, d=d, q=q, U=U: e.matmul(PS[:, 2 + d, q * 160:q * 160 + 129],
                                                                            lhsT=sp[:, d * 256 + q * 128:d * 256 + (q + 1) * 128], rhs=U, start=True, stop=True),
                              reads=[("sp", b), "Uf", "Ub"], writes=psk(2 + d))
                src4 = PS[:, 2:4, 0:320].rearrange("p a (q n) -> p a q n", q=2)
                sc.op("act", lambda e, b=b, src4=src4: e.activation(out=ebt[b], in_=src4[:, :, :, 0:129], func=AF.Exp, scale=-1.0 / 16),
                      reads=psk(2) + psk(3), writes=[("eb", b, 0), ("eb", b, 1)])
                sc.op("act", lambda e, b=b, src4=src4: e.activation(out=enbt[b], in_=src4[:, :, :, 0:128], func=AF.Exp, scale=1.0 / 16),
                      reads=psk(2) + psk(3), writes=[("enb", b, 0), ("enb", b, 1)])
                sc.op("dve", lambda e, t=t, b=b: e.tensor_tensor(out=qf[:, :, tl(t)], in0=gqT[:, :, tl(t)], in1=ebt[b][:, 0, :, 0:128], op=ALU.mult),
                      reads=[("gqT", t), ("eb", b, 0)], writes=[("qf", t)])
                sc.op("dve", lambda e, t=t, b=b: e.tensor_tensor(out=kf[:, :, tl(t)], in0=gkT[:, :, tl(t)], in1=enbt[b][:, 0, :, :], op=ALU.mult),
                      reads=[("gkT", t), ("enb", b, 0)], writes=[("kf", t)])
                sc.op("dve", lambda e, t=t, b=b: e.tensor_tensor(out=qb[:, :, tl(t)], in0=gqT[:, :, tl(t)], in1=ebt[b][:, 1, :, 0:128], op=ALU.mult),
                      reads=[("gqT", t), ("eb", b, 1), ("qf", t)], writes=[("gqT", t)])
                sc.op("dve", lambda e, t=t, b=b: e.tensor_tensor(out=kb[:, :, tl(t)], in0=gkT[:, :, tl(t)], in1=enbt[b][:, 1, :, :], op=ALU.mult),
                      reads=[("gkT", t), ("enb", b, 1), ("kf", t)], writes=[("gkT", t)])
                sc.op("dve", lambda e, t=t, b=b: e.tensor_copy(out=decs[:, :, :, t:t + 1], in_=ebt[b][:, :, :, 128:129]),
                      reads=[("eb", b, 0), ("eb", b, 1)], writes=["decs"])
            dump("qf", qf, [128, 2, 2048], [("qf", t) for t in range(NT)])
            dump("kd_f", kd_f, [128, 16, 256], [("kd_f", t) for t in range(NT)])
            dump("decs", decs, [128, 2, 2, 16], ["decs"])

            if stop == 'G1' and l == nl - 1:
                raise _Stop()
            sc.op("dve", lambda e: e.memset(stm2[0], 0.0), writes=[("stm", 0, d, q) for d in range(2) for q in range(2)])
            chains = [(d, q) for d in range(2) for q in range(2)]
            par = {c: 0 for c in chains}
            for i in range(NT):
                todo = []
                for ci, (d, q) in enumerate(chains):
                    t = i if d == 0 else NT - 1 - i
                    cur = par[(d, q)]
                    if i > 0:
                        sc.op("act", lambda e, d=d, q=q, t=t, cur=cur: e.activation(out=Sbf[0:64, d, q, t, :], in_=stm2[cur][0:64, d, q, 0:128], func=AF.Copy),
                              reads=[("stm", cur, d, q)], writes=[("Sbf", d, q, t)])
                        sc.op("dve", lambda e, d=d, q=q, t=t, cur=cur: e.tensor_copy(out=Sbf[64:128, d, q, t, :], in_=stm2[cur][64:128, d, q, 128:256]),
                              reads=[("stm", cur, d, q)], writes=[("Sbf", d, q, t)])
                    if i == NT - 1:
                        continue
                    kd = kd_f if d == 0 else kd_b
                    kkey = "kd_f" if d == 0 else "gk_tok"
                    pslot = ci % 2
                    pk = psk(4 + pslot)
                    sc.op("pe", lambda e, kd=kd, t=t, q=q, pslot=pslot: e.matmul(PS[:, 4 + pslot, 0:256], lhsT=kd[:, t, q * 128:(q + 1) * 128],
                                                                                rhs=gv[:, t, q * 256:(q + 1) * 256], start=True, stop=True),
                          reads=[(kkey, t), ("gv", t)], writes=pk)
                    if os.environ.get("GSKIP") != "evac":
                        sc.op("act", lambda e, ci=ci, pslot=pslot: e.activation(out=dS[ci], in_=PS[:, 4 + pslot, 0:256], func=AF.Copy),
                              reads=pk, writes=[("dS", ci)])
                    todo.append((ci, d, q, t, cur))
                for (ci, d, q, t, cur) in todo:
                    if os.environ.get("GSKIP") == "upd":
                        par[(d, q)] = 1 - cur
                        continue
                    sc.op("dve", lambda e, ci=ci, d=d, q=q, t=t, cur=cur: e.scalar_tensor_tensor(
                        out=stm2[1 - cur][:, d, q, :], in0=stm2[cur][:, d, q, :], scalar=decs[:, d, q, t:t + 1], in1=dS[ci],
                        op0=ALU.mult, op1=ALU.add), reads=[("stm", cur, d, q), "decs", ("dS", ci)], writes=[("stm", 1 - cur, d, q)])
                    par[(d, q)] = 1 - cur
            dump("Sbf", Sbf, [128, 2, 2, 16, 128], [("Sbf", d, q, t) for d in range(2) for q in range(2) for t in range(NT)])
            if stop == 'G2' and l == nl - 1:
                raise _Stop()

            def g_A(t):
                b = t % 2
                A = Am[b]
                for half in range(2):
                    sbank = (4 + half) if os.environ.get('GBANK') else (2 * b + half)
                    items = []
                    for sq in range(4):
                        h, d = half + 2 * (sq // 2), sq % 2
                        q = h // 2
                        base = (h % 2) * 64
                        kk = kf if d == 0 else kb
                        qq = qf if d == 0 else qb
                        kkey = ("kf", t) if d == 0 else ("gkT", t)
                        qkey = ("qf", t) if d == 0 else ("gqT", t)
                        sc.op("pe", lambda e, kk=kk, qq=qq, q=q, base=base, sbank=sbank, sq=sq: e.matmul(
                            PS[:, sbank, sq * 128:(sq + 1) * 128], lhsT=kk[base:base + 64, q, tl(t)], rhs=qq[base:base + 64, q, tl(t)], start=True, stop=True),
                            reads=[kkey, qkey], writes=psk(sbank))
                        items.append((sq, h, d))
                        if os.environ.get("GOLD"):
                            mk = maskf if d == 0 else maskb
                            sc.op("dve", lambda e, A=A, h=h, d=d, sbank=sbank, sq=sq, mk=mk: e.tensor_tensor(
                                out=A[:, h, d, :], in0=PS[:, sbank, sq * 128:(sq + 1) * 128], in1=mk, op=ALU.mult),
                                reads=psk(sbank) + ["Uf", "Usf"], writes=[("A", b, h, d), ("sp", b)])
                    if os.environ.get("GOLD"):
                        continue
                    for (sq, h, d) in items:
                        mk = maskf if d == 0 else maskb
                        sc.op("dve", lambda e, A=A, h=h, d=d, sbank=sbank, sq=sq, mk=mk: e.tensor_tensor(
                            out=A[:, h, d, :], in0=PS[:, sbank, sq * 128:(sq + 1) * 128], in1=mk, op=ALU.mult),
                            reads=psk(sbank) + ["Uf", "Usf"], writes=[("A", b, h, d), ("sp", b)])

            def g_B(t):
                b = t % 2
                A = Am[b]
                obank = (0 + b) if os.environ.get('GBANK') else (4 + b)
                for h in range(4):
                    q = h // 2
                    base = (h % 2) * 64
                    oh_ = PS[:, obank, h * 128:(h + 1) * 128]
                    ok = psk(obank)
                    inter_f = t > 0
                    inter_b = t < NT - 1
                    sc.op("pe", lambda e, A=A, h=h, oh_=oh_: e.matmul(oh_, lhsT=A[:, h, 0, :], rhs=gv[:, t, h * 128:(h + 1) * 128], start=True, stop=False),
                          reads=[("A", b, h, 0), ("gv", t)], writes=ok)
                    sc.op("pe", lambda e, A=A, h=h, oh_=oh_, fin=(not inter_f and not inter_b): e.matmul(
                        oh_, lhsT=A[:, h, 1, :], rhs=gv[:, t, h * 128:(h + 1) * 128], start=False, stop=fin),
                        reads=[("A", b, h, 1), ("gv", t)], writes=ok)
                    if inter_f:
                        sc.op("pe", lambda e, q=q, base=base, oh_=oh_, fin=(not inter_b): e.matmul(
                            oh_, lhsT=qf[base:base + 64, q, tl(t)], rhs=Sbf[base:base + 64, 0, q, t, :], start=False, stop=fin),
                            reads=[("qf", t), ("Sbf", 0, q, t)], writes=ok)
                    if inter_b:
                        sc.op("pe", lambda e, q=q, base=base, oh_=oh_: e.matmul(
                            oh_, lhsT=qb[base:base + 64, q, tl(t)], rhs=Sbf[base:base + 64, 1, q, t, :], start=False, stop=True),
                            reads=[("gqT", t), ("Sbf", 1, q, t)], writes=ok)

            def g_norm(t):
                b = t % 2
                sm = gsm[b]
                obank = (0 + b) if os.environ.get('GBANK') else (4 + b)
                okall = psk(obank)
                for h in range(4):
                    sc.op("act", lambda e, h=h, sm=sm: e.activation(out=g_junk, in_=PS[:, obank, h * 128:(h + 1) * 128], func=AF.Square, accum_out=sm[:, h:h + 1]),
                          reads=okall, writes=["gjunk", ("gss", b), ("enb", 1, 0), ("enb", 1, 1)])
                sc.op("act", lambda e, sm=sm: e.activation(out=sm[:, 4:8], in_=sm[:, 0:4], func=AF.Sqrt, bias=1e-5, scale=1.0 / 128),
                      reads=[("gss", b)], writes=[("grs", b)])
                sc.op("dve", lambda e, sm=sm: e.reciprocal(out=sm[:, 4:8], in_=sm[:, 4:8]), reads=[("grs", b)], writes=[("grs", b)])
                for h in range(4):
                    sc.op("dve", lambda e, h=h, sm=sm, b=b: e.scalar_tensor_tensor(
                        out=g_y[b][:, h * 128:(h + 1) * 128], in0=PS[:, obank, h * 128:(h + 1) * 128], scalar=sm[:, 4 + h:5 + h],
                        in1=gr_s[:, t, h * 128:(h + 1) * 128], op0=ALU.mult, op1=ALU.mult),
                        reads=psk(obank) + [("grs", b), ("gr_s", t)], writes=[("gy", b), ("enb", 0, 0), ("enb", 0, 1)])

            def g_tr(t):
                b = t % 2
                tk = psk(6 + b)
                for h in range(4):
                    sc.op("pe", lambda e, h=h, b=b: e.transpose(out=PSb[:, 6 + b, h * 128:(h + 1) * 128], in_=g_y[b][:, h * 128:(h + 1) * 128], identity=identb),
                          reads=[("gy", b), "identb"], writes=tk)
                sc.op("act", lambda e, b=b: e.activation(out=XT[:, 4:8, tl(t)], in_=PSb[:, 6 + b, 0:512].rearrange("p (c n) -> p c n", c=4), func=AF.Copy),
                      reads=tk, writes=[("XT", t)])

            g_A(0)
            for t in range(NT):
                if t + 1 < NT:
                    g_A(t + 1)
                if os.environ.get("GSKIP") == "B":
                    continue
                g_B(t)
                if os.environ.get("GSKIP") == "norm":
                    continue
                g_norm(t)
                if os.environ.get("GSKIP") == "tr":
                    continue
                if t > 0:
                    g_tr(t - 1)
            if not os.environ.get("GSKIP"):
                g_tr(NT - 1)
            dump("mixT", XT, [128, 8, 2048], [("XT", t) for t in range(NT)])

            if stop == 'G' and l == nl - 1:
                raise _Stop()
            sc.barrier()
            load_ln_params(ln1g_d[l], ln1b_d[l])
            R_D.reset()
            W1B = [R_D.get(8192, BF16, "p (c n) -> p c n", c=8) for _ in range(2)]
            W2B = [R_D.get(8192, BF16, "p (c n) -> p c n", c=4) for _ in range(2)]
            hT = R_D.get(16384, BF16, "p (c n) -> p c n", c=4)
            relu_t = [R_D.get(2048, F32) for _ in range(2)]

            def load_ffn_block(fb, buf):
                s1 = w1_d[l, :, fb * 512:(fb + 1) * 512].rearrange("(c p) n -> p c n", p=128)
                s2 = w2_d[l, fb * 512:(fb + 1) * 512, :].rearrange("(c p) n -> p c n", p=128)
                for hf in range(2):
                    sc.dma("pool", lambda e, hf=hf: e.dma_start(out=W1B[buf][:, hf * 4:(hf + 1) * 4, :], in_=s1[:, hf * 4:(hf + 1) * 4, :]),
                           writes=[("W1B", buf, hf)])
                for hf in range(2):
                    sc.dma("pool", lambda e, hf=hf: e.dma_start(out=W2B[buf][:, hf * 2:(hf + 1) * 2, :], in_=s2[:, hf * 2:(hf + 1) * 2, :]),
                           writes=[("W2B", buf, hf)])

            load_ffn_block(0, 0)
            load_ffn_block(1, 1)
            for t in range(NT):
                sc.dma("sp", lambda e, t=t: e.dma_start(out=X[:, t, :], in_=xs_d[t * 128:(t + 1) * 128, :]), reads=[("xsd", t)], writes=[("X", t)])
            def o_mm(t):
                yb = (t % 3) * 2
                for hf in range(2):
                    for c in range(8):
                        sc.op("pe", lambda e, c=c, hf=hf, t=t, yb=yb: e.matmul(PS[:, yb + hf, :], lhsT=XT[:, c, tl(t)], rhs=WO[:, c, hf * 512:(hf + 1) * 512],
                                                                              start=(c == 0), stop=(c == 7)),
                              reads=[("XT", t), ("RW", 0, 0), ("RW", 0, 1), ("RW", 1, 0), ("RW", 1, 1)], writes=psk(yb + hf))

            def o_ln(t):
                yb = (t % 3) * 2
                sc.op("dve", lambda e, t=t, yb=yb: e.scalar_tensor_tensor(out=X[:, t, :], in0=X[:, t, :], scalar=ALPHA,
                                                                          in1=PS[:, yb:yb + 2, :].rearrange("p a n -> p (a n)"), op0=ALU.mult, op1=ALU.add),
                      reads=[("X", t)] + psk(yb) + psk(yb + 1), writes=[("X", t)])
                ln_a(t)

            for t0 in range(3):
                o_mm(t0)
                o_ln(t0)
            for t in range(NT):
                if t + 3 < NT:
                    o_mm(t + 3)
                ln_b1(t, None)
                if t + 3 < NT:
                    o_ln(t + 3)
                ln_b2(t, None)
            dump("x1T", XT, [128, 8, 2048], [("XT", t) for t in range(NT)])

            if stop == 'O' and l == nl - 1:
                raise _Stop()
            if not last:
                load_win_block(l + 1, 0, 0)
                load_win_block(l + 1, 1, 1)
            hrr = [0]
            load_ln_params(ln2g_d[l], ln2b_d[l])
            spill2 = out_d if last else xs_d
            LAG = 3
            for fb in range(8):
                buf = fb % 2
                for r in range(4):
                    for fc in range(4):
                        bank = 4 + hrr[0] % 3
                        rb = hrr[0] % 2
                        hrr[0] += 1
                        for c in range(8):
                            sc.op("pe", lambda e, c=c, fc=fc, r=r, bank=bank, buf=buf: e.matmul(
                                PS[:, bank, :], lhsT=W1B[buf][:, c, fc * 128:(fc + 1) * 128], rhs=XT[:, c, r * 512:(r + 1) * 512],
                                start=(c == 0), stop=(c == 7)), reads=[("W1B", buf, 0), ("W1B", buf, 1)] + xt_all[r * 4:(r + 1) * 4], writes=psk(bank))
                        fcol = fb * 4 + fc
                        sc.op("act", lambda e, bank=bank, rb=rb, fcol=fcol: e.activation(out=relu_t[rb], in_=PS[:, bank, :], func=AF.Relu,
                                                                                         bias=b1c[:, fcol:fcol + 1], scale=1.0),
                              reads=psk(bank) + ["b1c"], writes=[("relu", rb)])
                        sc.op("dve", lambda e, rb=rb, fc=fc, r=r: e.tensor_tensor(out=hT[:, fc, r * 512:(r + 1) * 512], in0=relu_t[rb], in1=relu_t[rb], op=ALU.mult),
                              reads=[("relu", rb)], writes=[("hT", fc, r)])
                for t in range(NT):
                    yb = (t % 2) * 2
                    for hf in range(2):
                        for fc in range(4):
                            sc.op("pe", lambda e, fc=fc, hf=hf, t=t, yb=yb, buf=buf: e.matmul(
                                PS[:, yb + hf, :], lhsT=hT[:, fc, tl(t)], rhs=W2B[buf][:, fc, hf * 512:(hf + 1) * 512],
                                start=(fc == 0), stop=(fc == 3)), reads=[("hT", fc, t // 4), ("W2B", buf, 0), ("W2B", buf, 1)], writes=psk(yb + hf))
                    if fb == 7 and t >= LAG:
                        ln_b1(t - LAG, spill2)
                        ln_b2(t - LAG, spill2)
                    if fb == 0:
                        sc.op("dve", lambda e, t=t, yb=yb: e.scalar_tensor_tensor(out=X[:, t, :], in0=X[:, t, :], scalar=ALPHA,
                                                                                  in1=PS[:, yb:yb + 2, :].rearrange("p a n -> p (a n)"), op0=ALU.mult, op1=ALU.add),
                              reads=[("X", t)] + psk(yb) + psk(yb + 1), writes=[("X", t)])
                        sc.op("pool", lambda e, t=t: e.tensor_tensor(out=X[:, t, :], in0=X[:, t, :], in1=b2t, op=ALU.add), reads=[("X", t), "b2t"], writes=[("X", t)])
                    else:
                        sc.op("dve", lambda e, t=t, yb=yb: e.tensor_tensor(out=X[:, t, :], in0=X[:, t, :], in1=PS[:, yb:yb + 2, :].rearrange("p a n -> p (a n)"), op=ALU.add),
                              reads=[("X", t)] + psk(yb) + psk(yb + 1), writes=[("X", t)])
                    if fb == 7:
                        ln_a(t)
                if fb + 2 < 8:
                    load_ffn_block(fb + 2, buf)
            if stop == 'F' and l == nl - 1:
                raise _Stop()
            for t in range(NT - LAG, NT):
                ln_b1(t, spill2)
                ln_b2(t, spill2)
            sc.barrier()


        for _l in range(nl):
            do_layer(_l)
    except _Stop:
        pass
    out_dmas = [o for o in sc.ops if o.is_dma and o.dkey in [("xs", i) for i in range(4)]]
    fin = {}
    for o in out_dmas:
        fin[o.dkey] = o
    finals = list(fin.values()) + list(dbg_out.values())
    sc.emit(final_wait_ops=finals)
    es.close()
    return nc, sc


_CONST = None


def kernel(**inputs):
    global _CONST
    if _CONST is None:
        _CONST = _constants()
    nc, _ = build(2)
    x = np.ascontiguousarray(inputs["x"], dtype=np.float32)
    shared = {k: np.ascontiguousarray(v, dtype=np.float32) for k, v in inputs.items() if k != "x"}
    shared.update(_CONST)
    in_maps = []
    for b in range(8):
        m = dict(shared)
        m["x"] = x[b]
        in_maps.append(m)
    res = run_bass_kernel_spmd(nc, in_maps, core_ids=list(range(8)))
    return np.stack([r["out"] for r in res.results], axis=0).astype(np.float32)
```

```python
import math
import os
from contextlib import ExitStack

import numpy as np
import concourse.bass as bass
import concourse.mybir as mybir
from concourse.bass_utils import run_bass_kernel_spmd

F32 = mybir.dt.float32
BF16 = mybir.dt.bfloat16
AF = mybir.ActivationFunctionType
ALU = mybir.AluOpType

S = 2048
D = 1024
DIN = 3104
DFF = 4096
NT = 16
ALPHA = (2.0 * 2) ** 0.25
ENGS = ("pe", "act", "dve", "pool", "sp")
EPOCH = 30000


class _Res:
    __slots__ = ("last_w", "readers")

    def __init__(self):
        self.last_w = None
        self.readers = []


class _Op:
    __slots__ = ("eng", "fn", "deps", "signal", "tok", "is_dma", "dkey")

    def __init__(self, eng, fn, is_dma, dkey):
        self.eng = eng
        self.fn = fn
        self.deps = []
        self.signal = False
        self.tok = None
        self.is_dma = is_dma
        self.dkey = dkey


class Sched:
    def __init__(self, nc):
        self.nc = nc
        self.ops = []
        self.res = {}
        self.pending = {e: [] for e in ENGS}

    def _r(self, key):
        x = self.res.get(key)
        if x is None:
            x = self.res[key] = _Res()
        return x

    def _add(self, op, reads, writes):
        deps = set()
        for k in reads:
            rs = self._r(k)
            if rs.last_w is not None:
                deps.add(rs.last_w)
        for k in writes:
            rs = self._r(k)
            if rs.last_w is not None:
                deps.add(rs.last_w)
            deps.update(rs.readers)
        for k in reads:
            self._r(k).readers.append(op)
        for k in writes:
            rs = self._r(k)
            rs.last_w = op
            rs.readers = []
        if self.pending[op.eng]:
            deps.update(self.pending[op.eng])
            self.pending[op.eng] = []
        deps.discard(op)
        op.deps = list(deps)
        self.ops.append(op)
        return op

    def op(self, eng, fn, reads=(), writes=()):
        return self._add(_Op(eng, fn, False, None), reads, writes)

    def dma(self, eng, fn, dkey=None, reads=(), writes=()):
        if dkey is None:
            dkey = ("w", writes[0])
        return self._add(_Op(eng, fn, True, dkey), reads, writes)

    def barrier(self):
        last = {}
        for o in self.ops:
            last[(o.eng, o.dkey) if o.is_dma else o.eng] = o
        b = list(last.values())
        self.pending = {e: list(b) for e in ENGS}

    def emit(self, final_wait_ops=()):
        nc = self.nc
        ops = self.ops
        for o in ops:
            for d in o.deps:
                if d.is_dma:
                    d.signal = True
                elif d.eng == "pe" and o.eng == "pe" and not o.is_dma:
                    continue
                else:
                    d.signal = True
        with ExitStack() as es:
            eng_sems = {e: [] for e in ENGS}
            cnt = {e: 0 for e in ENGS}
            dma_sems = {}
            dma_cnt = {}
            for o in ops:
                if o.is_dma:
                    if o.dkey not in dma_sems:
                        dma_sems[o.dkey] = es.enter_context(nc.semaphore("d%d" % len(dma_sems)))
                        dma_cnt[o.dkey] = 0
                    dma_cnt[o.dkey] += 16
                    o.tok = (dma_sems[o.dkey], dma_cnt[o.dkey])
                elif o.signal:
                    ep = cnt[o.eng] // EPOCH
                    if ep >= len(eng_sems[o.eng]):
                        eng_sems[o.eng].append(es.enter_context(nc.semaphore("e_%s_%d" % (o.eng, ep))))
                    cnt[o.eng] += 1
                    o.tok = (eng_sems[o.eng][ep], cnt[o.eng] - ep * EPOCH)
            per_eng = {e: [o for o in ops if o.eng == e] for e in ENGS}
            self.stats = {e: len(per_eng[e]) for e in ENGS}
            self.stats["sems"] = sum(len(v) for v in eng_sems.values()) + len(dma_sems)

            def run(e, eng):
                waited = {}
                for o in per_eng[e]:
                    need = {}
                    for d in o.deps:
                        if d.tok is None:
                            continue
                        if (not d.is_dma) and d.eng == "pe" and e == "pe" and not o.is_dma:
                            continue
                        s, v = d.tok
                        k = id(s)
                        if waited.get(k, 0) >= v:
                            continue
                        if k not in need or need[k][1] < v:
                            need[k] = (s, v)
                    for k, (s, v) in need.items():
                        eng.wait_ge(s, v)
                        waited[k] = v
                    ins = o.fn(eng)
                    if o.tok is not None:
                        ins.then_inc(o.tok[0], 16 if o.is_dma else 1)
                if e == "sp":
                    for o in final_wait_ops:
                        s, v = o.tok
                        eng.wait_ge(s, v)

            with nc.Block() as block:
                @block.sync
                def _(eng):
                    run("sp", eng)

                @block.tensor
                def _(eng):
                    run("pe", eng)

                @block.scalar
                def _(eng):
                    run("act", eng)

                @block.vector
                def _(eng):
                    run("dve", eng)

                @block.gpsimd
                def _(eng):
                    run("pool", eng)


def _t5_bucket(rel):
    nb = 16
    me = 8
    ret = np.where(rel > 0, nb, 0)
    n = np.abs(rel)
    large = me + (np.log(np.maximum(n, 1).astype(np.float32) / np.float32(me))
                  / np.float32(math.log(128 / me)) * np.float32(nb - me)).astype(np.int32)
    large = np.minimum(large, nb - 1)
    return ret + np.where(n < me, n, large)


MLEN = 1280


def _constants():
    c = {}
    c["c_ident"] = np.eye(128, dtype=np.float32)
    c["c_J"] = np.eye(128, dtype=np.float32)[::-1].copy()
    s = np.arange(128)[:, None]
    t = np.arange(128)[None, :]
    uf = np.zeros((128, 129), np.float32)
    uf[:, :128] = (s <= t)
    uf[:, 128] = 1.0
    ub = np.zeros((128, 129), np.float32)
    ub[:, :128] = (s >= t)
    ub[:, 128] = 1.0
    c["c_uf"] = uf
    c["c_ub"] = ub
    c["c_sf"] = (s > t).astype(np.float32)
    c["c_sb"] = (s < t).astype(np.float32)
    n = np.arange(MLEN)
    bk = _t5_bucket(639 - n)
    oh = np.zeros((32, MLEN), np.float32)
    oh[bk, n] = 1.0
    c["c_onehot"] = oh
    return c


class _Stop(Exception):
    pass


def build(nl=2, dbg=(), stop=None):
    nc = bass.Bass("TRN2", target_bir_lowering=False)

    def din(name, shape):
        return nc.dram_tensor(name, list(shape), F32, kind="ExternalInput").ap()

    x_d = din("x", [S, D])
    lnemb_g = din("ln_emb_g", [D])
    lnemb_b = din("ln_emb_b", [D])
    table_d = din("rel_bias_table", [32, 4])
    w_in_d = din("w_in", [2, D, DIN])
    lq1_d = din("lambda_q1", [2, 64])
    lk1_d = din("lambda_k1", [2, 64])
    lq2_d = din("lambda_q2", [2, 64])
    lk2_d = din("lambda_k2", [2, 64])
    dnw_d = din("diff_norm_w", [2, 128])
    gup_d = din("gla_gate_up", [2, 2, 16, 256])
    gbias_d = din("gla_gate_bias", [2, 2, 256])
    gnw_d = din("gla_norm_w", [2, 128])
    w_o_d = din("w_o", [2, D, D])
    ln1g_d = din("ln1_g", [2, D])
    ln1b_d = din("ln1_b", [2, D])
    w1_d = din("w_ffn1", [2, D, DFF])
    b1_d = din("b_ffn1", [2, DFF])
    w2_d = din("w_ffn2", [2, DFF, D])
    b2_d = din("b_ffn2", [2, D])
    ln2g_d = din("ln2_g", [2, D])
    ln2b_d = din("ln2_b", [2, D])
    c_ident = din("c_ident", [128, 128])
    c_J = din("c_J", [128, 128])
    c_uf = din("c_uf", [128, 129])
    c_ub = din("c_ub", [128, 129])
    c_sf = din("c_sf", [128, 128])
    c_sb = din("c_sb", [128, 128])
    c_onehot = din("c_onehot", [32, MLEN])
    out_d = nc.dram_tensor("out", [S, D], F32, kind="ExternalOutput").ap()
    xs_d = nc.dram_tensor("xs_scratch", [S, D], F32).ap()
    md_t = nc.dram_tensor("md_scratch", [4, MLEN], F32)
    eb_d = nc.dram_tensor("expb_scratch", [128, 4 * 1152], BF16).ap()
    md_d = md_t.ap()
    dbg_out = {}

    sc = Sched(nc)
    es = ExitStack()
    ARENA_BYTES = 207 * 1024
    arena = es.enter_context(nc.sbuf_tensor("arena", [128, ARENA_BYTES // 2], BF16))
    PSb = es.enter_context(nc.psum_tensor("ps", [128, 8, 1024], BF16))[:]
    PS = PSb.bitcast(F32)

    def view(off, nbytes, dt, pattern=None, **kw):
        assert off % 32 == 0, off
        a = arena[:, off // 2:(off + nbytes) // 2]
        if dt is F32:
            a = a.bitcast(F32)
        if pattern:
            a = a.rearrange(pattern, **kw)
        return a

    class Alloc:
        def __init__(self, base, size):
            self.base = base
            self.size = size
            self.pos = 0

        def reset(self):
            self.pos = 0

        def get(self, nbytes, dt, pattern=None, **kw):
            n = (nbytes + 31) // 32 * 32
            assert self.pos + n <= self.size, (self.pos, n, self.size)
            v = view(self.base + self.pos, nbytes, dt, pattern, **kw)
            self.pos += n
            return v

    R_XT = Alloc(0, 32768)
    R_X = Alloc(32768, 65536)
    R_D = Alloc(98304, 70656)
    R_W = Alloc(168960, 16384)
    R_C = Alloc(185344, ARENA_BYTES - 185344)

    XT = R_XT.get(32768, BF16, "p (c n) -> p c n", c=8)
    X = R_X.get(65536, F32, "p (t n) -> p t n", t=NT)
    WB = [R_W.get(8192, BF16, "p (c n) -> p c n", c=8) for _ in range(2)]
    R_W.reset()
    WO = R_W.get(16384, BF16, "p (c n) -> p c n", c=8)

    identb = R_C.get(256, BF16)
    Jb = R_C.get(256, BF16)
    Uf = R_C.get(516, F32)
    Ub = R_C.get(516, F32)
    Usf = R_C.get(512, F32)
    Usb = R_C.get(512, F32)
    gt = R_C.get(4096, F32)
    bt = R_C.get(4096, F32)
    b2t = R_C.get(4096, F32)
    wd_t = R_C.get(512, F32)
    wg_t = R_C.get(512, F32)
    b1c = R_C.get(128, F32)
    cb = R_C.get(32, F32, "p (s h) -> p s h", s=2)
    lamv = R_C.get(4 * 64 * 4, F32, "p (a n) -> p a n", a=4)
    lamp = R_C.get(2 * 64 * 4, F32, "p (a n) -> p a n", a=2)
    lams = R_C.get(32, F32)
    Wg = R_C.get(1024, BF16)
    st_ = [R_C.get(48, F32) for _ in range(2)]
    mv_ = [R_C.get(8, F32) for _ in range(2)]
    rs_ = [R_C.get(4, F32) for _ in range(2)]
    xb_ = [R_C.get(2048, BF16) for _ in range(2)]
    dsm = [R_C.get(64, F32) for _ in range(2)]
    gsm = [R_C.get(64, F32) for _ in range(2)]
    decs = R_C.get(2 * 2 * 16 * 4, F32, "p (d q t) -> p d q t", d=2, q=2)

    def psk(b):
        return [("ps", b, q) for q in range(4)]

    def bc_mid(ap2, n):
        a = ap2.ap
        return bass.AP(ap2.tensor, ap2.offset, [list(a[0]), [0, n], list(a[1])])

    def bc_last(ap2, n):
        a = ap2.ap
        return bass.AP(ap2.tensor, ap2.offset, [list(a[0]), list(a[1]), [0, n]])

    cur_layer = [-1]

    def dump(name, ap, shape, reads):
        nm = "%s@%d" % (name, cur_layer[0])
        if nm in dbg:
            name = nm
        elif name not in dbg or (cur_layer[0] >= 0 and cur_layer[0] != nl - 1):
            return
        t = nc.dram_tensor("dbg_" + name.replace("@", "_"), list(shape), ap.dtype, kind="ExternalOutput").ap()
        dbg_out[name] = sc.dma("sp", lambda e: e.dma_start(out=t, in_=ap), "dbg", reads=reads)

    try:
        sc.dma("pool", lambda e: e.dma_start(out=identb, in_=c_ident), writes=["identb"])
        sc.dma("pool", lambda e: e.dma_start(out=Jb, in_=c_J), writes=["Jb"])
        sc.dma("sp", lambda e: e.dma_start(out=Uf, in_=c_uf), writes=["Uf"])
        sc.dma("sp", lambda e: e.dma_start(out=Ub, in_=c_ub), writes=["Ub"])
        sc.dma("sp", lambda e: e.dma_start(out=Usf, in_=c_sf), writes=["Usf"])
        sc.dma("sp", lambda e: e.dma_start(out=Usb, in_=c_sb), writes=["Usb"])
        for si, row in enumerate((15, 31)):
            sc.dma("sp", lambda e, si=si, row=row: e.dma_start(out=cb[:, si, :], in_=table_d[row, :].partition_broadcast(128)),
                   writes=[("cb", si)])

        R_D.reset()
        tb = R_D.get(16, F32)
        oh = R_D.get(MLEN * 4, F32)
        msb = R_D.get(MLEN * 4, F32)
        sc.dma("sp", lambda e: e.dma_start(out=tb[0:32, :], in_=table_d), writes=["tb"])
        sc.dma("sp", lambda e: e.dma_start(out=oh[0:32, :], in_=c_onehot), writes=["oh"])
        sc.dma("sp", lambda e: e.dma_start(out=gt, in_=lnemb_g.partition_broadcast(128)), writes=["gt"])
        sc.dma("sp", lambda e: e.dma_start(out=bt, in_=lnemb_b.partition_broadcast(128)), writes=["bt"])
        for t in range(NT):
            sc.dma("sp", lambda e, t=t: e.dma_start(out=X[:, t, :], in_=x_d[t * 128:(t + 1) * 128, :]), writes=[("X", t)])
        for ci, (c0, cn) in enumerate(((0, 512), (512, 512), (1024, 256))):
            sc.op("pe", lambda e, ci=ci, c0=c0, cn=cn: e.matmul(PS[0:4, ci, 0:cn], lhsT=tb[0:32, :], rhs=oh[0:32, c0:c0 + cn], start=True, stop=True),
                  reads=["tb", "oh"], writes=psk(ci))
            sc.op("dve", lambda e, ci=ci, c0=c0, cn=cn: e.tensor_copy(out=msb[0:4, c0:c0 + cn], in_=PS[0:4, ci, 0:cn]),
                  reads=psk(ci), writes=["msb"])
        sc.dma("sp", lambda e: e.dma_start(out=md_d, in_=msb[0:4, :]), reads=["msb"], writes=["md"])
        R_D.reset()
        R_D.get(16384, BF16); R_D.get(16384, BF16); R_D.get(16 * 4 * 129 * 2, BF16)
        expB0 = R_D.get(4 * 1152 * 2, BF16, "p (h n) -> p h n", h=4)
        R_D.get(4096, BF16); R_D.get(1024, BF16)
        tmp_revs = [view(R_D.base + 16384 + i * 2304, 2304, BF16) for i in range(4)]
        for h in range(4):
            src = bass.AP(md_t, h * MLEN, [[1, 128], [1, 1152]])
            sc.dma("pool", lambda e, src=src, h=h: e.dma_start(out=tmp_revs[h], in_=src), reads=["md"], writes=[("tmp_rev", h)])
        for h in range(4):
            for ci, (c0, cn) in enumerate(((0, 512), (512, 512), (1024, 128))):
                sc.op("pe", lambda e, ci=ci, c0=c0, cn=cn, h=h: e.matmul(PS[:, ci, 0:cn], lhsT=Jb, rhs=tmp_revs[h][:, c0:c0 + cn], start=True, stop=True),
                      reads=["Jb", ("tmp_rev", h)], writes=psk(ci))
                sc.op("act", lambda e, h=h, ci=ci, c0=c0, cn=cn: e.activation(out=expB0[:, h, c0:c0 + cn], in_=PS[:, ci, 0:cn], func=AF.Exp),
                      reads=psk(ci), writes=[("expB", h)])
        sc.dma("sp", lambda e: e.dma_start(out=eb_d, in_=expB0.rearrange("p h n -> p (h n)")), reads=[("expB", h) for h in range(4)], writes=["eb_d"])

        def ln_a(t):
            Xt = X[:, t, :]
            kx = ("X", t)
            b = t % 2
            st, mv, rs = st_[b], mv_[b], rs_[b]
            sc.op("dve", lambda e: e.bn_stats(out=st[:, 0:6], in_=Xt[:, 0:512]), reads=[kx], writes=[("st", b, 0)])
            sc.op("dve", lambda e: e.bn_stats(out=st[:, 6:12], in_=Xt[:, 512:1024]), reads=[kx], writes=[("st", b, 1)])
            sc.op("dve", lambda e: e.bn_aggr(out=mv, in_=st), reads=[("st", b, 0), ("st", b, 1)], writes=[("mv", b)])
            sc.op("act", lambda e: e.activation(out=rs, in_=mv[:, 1:2], func=AF.Sqrt, bias=1e-5, scale=1.0), reads=[("mv", b)], writes=[("rs", b)])
            sc.op("dve", lambda e: e.reciprocal(out=rs, in_=rs), reads=[("rs", b)], writes=[("rs", b)])
            sc.op("dve", lambda e: e.tensor_scalar(out=Xt, in0=Xt, scalar1=mv[:, 0:1], scalar2=rs, op0=ALU.subtract, op1=ALU.mult),
                  reads=[kx, ("mv", b), ("rs", b)], writes=[kx])
            sc.op("dve", lambda e: e.tensor_tensor(out=Xt, in0=Xt, in1=gt, op=ALU.mult), reads=[kx, "gt"], writes=[kx])
            sc.op("pool", lambda e: e.tensor_tensor(out=Xt, in0=Xt, in1=bt, op=ALU.add), reads=[kx, "bt"], writes=[kx])

        def ln_b1(t, spill_to):
            Xt = X[:, t, :]
            kx = ("X", t)
            b = t % 2
            xb = xb_[b]
            if spill_to is not None:
                sc.dma("sp", lambda e: e.dma_start(out=spill_to[t * 128:(t + 1) * 128, :], in_=Xt), ("xs", t % 4), reads=[kx], writes=[("xsd", t)])
            if spill_to is out_d:
                return
            sc.op("act", lambda e: e.activation(out=xb, in_=Xt, func=AF.Copy), reads=[kx], writes=[("xb", b)])

        def ln_b2(t, spill_to):
            if spill_to is out_d:
                return
            b = t % 2
            xb = xb_[b]
            bank = 6 + b
            for c in range(8):
                sc.op("pe", lambda e, c=c: e.transpose(out=PSb[:, bank, c * 128:(c + 1) * 128], in_=xb[:, c * 128:(c + 1) * 128], identity=identb),
                      reads=[("xb", b), "identb"], writes=psk(bank))
            sc.op("act", lambda e: e.activation(out=XT[:, :, t * 128:(t + 1) * 128], in_=PSb[:, bank, :].rearrange("p (c n) -> p c n", c=8), func=AF.Copy),
                  reads=psk(bank), writes=[("XT", t)])

        def ln_all(spill_to):
            ln_a(0)
            for t in range(NT):
                if t + 1 < NT:
                    ln_a(t + 1)
                ln_b1(t, spill_to)
                ln_b2(t, spill_to)

        def load_ln_params(g_ap, b_ap):
            sc.dma("sp", lambda e: e.dma_start(out=gt, in_=g_ap.partition_broadcast(128)), writes=["gt"])
            sc.dma("sp", lambda e: e.dma_start(out=bt, in_=b_ap.partition_broadcast(128)), writes=["bt"])

        def load_win_block(l, blk, buf):
            c0 = blk * 512
            ncol = min(512, DIN - c0)
            src = w_in_d[l, :, c0:c0 + ncol].rearrange("(c p) n -> p c n", p=128)
            for hf in range(2):
                sc.dma("pool", lambda e, hf=hf: e.dma_start(out=WB[buf][:, hf * 4:(hf + 1) * 4, 0:ncol], in_=src[:, hf * 4:(hf + 1) * 4, :]),
                       writes=[("RW", buf, hf)])

        if stop == 'init':
            raise _Stop()
        load_win_block(0, 0, 0)
        load_win_block(0, 1, 1)
        ln_all(xs_d)
        dump("h0", X, [128, NT, 1024], [("X", t) for t in range(NT)])

        if stop == 'emb':
            raise _Stop()
        evac_rr = [0]

        def evac(out, in_, reads, writes, scale=None):
            evac_rr[0] ^= 1
            if evac_rr[0]:
                if scale is None:
                    sc.op("act", lambda e: e.activation(out=out, in_=in_, func=AF.Copy), reads=reads, writes=writes)
                else:
                    sc.op("act", lambda e: e.mul(out=out, in_=in_, mul=scale), reads=reads, writes=writes)
            else:
                if scale is None:
                    sc.op("dve", lambda e: e.tensor_copy(out=out, in_=in_), reads=reads, writes=writes)
                else:
                    sc.op("dve", lambda e: e.tensor_scalar(out=out, in0=in_, scalar1=scale, scalar2=None, op0=ALU.mult), reads=reads, writes=writes)

        def do_layer(l):
            lam_init = 0.8 - 0.6 * math.exp(-0.3 * l)
            cur_layer[0] = l
            if l == 0:
                sc.barrier()
            last = (l == nl - 1)
            R_D.reset()
            QT = R_D.get(16384, BF16, "p (h n) -> p h n", h=4)
            KT = R_D.get(16384, BF16, "p (h n) -> p h n", h=4)
            V = R_D.get(16 * 4 * 129 * 2, BF16, "p (t h e) -> p t h e", t=16, h=4)
            expB = R_D.get(4 * 1152 * 2, BF16, "p (h n) -> p h n", h=4)
            Eb = R_D.get(4096, BF16, "p (b m n) -> p b m n", b=2, m=2)
            d_y = R_D.get(4 * 128 * 2, BF16, "p (u n) -> p u n", u=4)
            _pu = R_D.pos
            silu_t = [R_D.get(2048, F32) for _ in range(2)]
            R_D.pos = _pu
            accS = R_D.get(8 * 129 * 4, F32, "p (a n) -> p a n", a=8)
            R_D.pos = _pu + 4608
            tmp_rev = R_D.get(1152 * 2, BF16)
            R_X.reset()
            gqT = R_X.get(8192, BF16, "p (c n) -> p c n", c=2)
            gkT = R_X.get(8192, BF16, "p (c n) -> p c n", c=2)
            gk_tok = R_X.get(8192, BF16, "p (t n) -> p t n", t=16)
            gv = R_X.get(16384, BF16, "p (t n) -> p t n", t=16)
            gr_s = R_X.get(16384, BF16, "p (t n) -> p t n", t=16)
            G33 = R_X.get(4096, BF16)

            for i, ap in enumerate((lq1_d, lk1_d, lq2_d, lk2_d)):
                sc.dma("sp", lambda e, i=i, ap=ap: e.dma_start(out=lamv[:, i, :], in_=ap[l, :].partition_broadcast(128)), writes=[("lamv", i)])
            sc.op("dve", lambda e: e.tensor_tensor(out=lamp[:, 0, :], in0=lamv[:, 0, :], in1=lamv[:, 1, :], op=ALU.mult), reads=[("lamv", 0), ("lamv", 1)], writes=["lamp"])
            sc.op("dve", lambda e: e.tensor_tensor(out=lamp[:, 1, :], in0=lamv[:, 2, :], in1=lamv[:, 3, :], op=ALU.mult), reads=[("lamv", 2), ("lamv", 3)], writes=["lamp"])
            sc.op("dve", lambda e: e.reduce_sum(out=lams[:, 0:2], in_=lamp, axis=mybir.AxisListType.X), reads=["lamp"], writes=["lams"])
            sc.op("act", lambda e: e.activation(out=lams[:, 0:2], in_=lams[:, 0:2], func=AF.Exp), reads=["lams"], writes=["lams"])
            sc.op("dve", lambda e: e.tensor_tensor(out=lams[:, 2:3], in0=lams[:, 0:1], in1=lams[:, 1:2], op=ALU.subtract), reads=["lams"], writes=["lams"])
            sc.op("dve", lambda e: e.tensor_scalar(out=lams[:, 3:4], in0=lams[:, 2:3], scalar1=lam_init, scalar2=-1.0, op0=ALU.add, op1=ALU.mult),
                  reads=["lams"], writes=["neglam"])
            neg_lam = lams[:, 3:4]
            sc.dma("sp", lambda e: e.dma_start(out=wd_t, in_=dnw_d[l, :].partition_broadcast(128)), writes=["wd"])
            sc.op("dve", lambda e: e.tensor_scalar(out=wd_t, in0=wd_t, scalar1=1.0 - lam_init, scalar2=None, op0=ALU.mult), reads=["wd"], writes=["wd"])
            sc.dma("sp", lambda e: e.dma_start(out=wg_t, in_=gnw_d[l, :].partition_broadcast(128)), writes=["wg"])
            sc.op("dve", lambda e: e.memset(Wg[0:33, :], 0.0), writes=["Wg"])
            sc.dma("pool", lambda e: e.dma_start(out=Wg[0:16, 0:256], in_=gup_d[l, 0]), writes=["Wg"])
            sc.dma("pool", lambda e: e.dma_start(out=Wg[16:32, 256:512], in_=gup_d[l, 1]), writes=["Wg"])
            sc.dma("pool", lambda e: e.dma_start(out=Wg[32:33, :], in_=gbias_d[l].rearrange("a n -> (a n)").partition_broadcast(1)), writes=["Wg"])
            sc.dma("sp", lambda e: e.dma_start(out=b1c, in_=b1_d[l].rearrange("(c p) -> p c", p=128), allow_slow_non_contiguous=True), writes=["b1c"])
            sc.dma("sp", lambda e: e.dma_start(out=b2t, in_=b2_d[l].partition_broadcast(128)), writes=["b2t"])
            sc.dma("sp", lambda e: e.dma_start(out=expB.rearrange("p h n -> p (h n)"), in_=eb_d), reads=["eb_d"], writes=[("expB", h) for h in range(4)])
            sc.op("dve", lambda e: e.memset(V[:, :, :, 128:129], 1.0), writes=[("V", t) for t in range(NT)])
            sc.op("dve", lambda e: e.memset(G33[32:33, :], 1.0), writes=["G33"])

            dump("XTin", XT, [128, 8, 2048], [("XT", t) for t in range(NT)])
            dump("Xin", X, [128, NT, 1024], [("X", t) for t in range(NT)])
            if stop == 'L' and l == nl - 1:
                raise _Stop()
            ps_rr = [0]

            def nextbank():
                b = ps_rr[0] % 6
                ps_rr[0] += 1
                return b

            xt_all = [("XT", t) for t in range(NT)]
            for blk in range(7):
                buf = blk % 2
                wb = WB[buf]
                kw = [("RW", buf, 0), ("RW", buf, 1)]
                if blk in (0, 1, 3, 6):
                    nch = 1 if blk == 6 else 4
                    for cc in range(nch):
                        for r in range(4):
                            bank = nextbank()
                            M = 32 if blk == 6 else 128
                            for c in range(8):
                                sc.op("pe", lambda e, c=c, cc=cc, r=r, bank=bank, M=M, wb=wb: e.matmul(
                                    PS[0:M, bank, :], lhsT=wb[:, c, cc * 128:cc * 128 + M], rhs=XT[:, c, r * 512:(r + 1) * 512],
                                    start=(c == 0), stop=(c == 7)), reads=kw + xt_all[r * 4:(r + 1) * 4], writes=psk(bank))
                            sl = slice(r * 512, (r + 1) * 512)
                            if blk == 0:
                                evac(QT[:, cc, sl], PS[:, bank, :], psk(bank), [("QT", cc, r)], scale=0.125)
                            elif blk == 1:
                                evac(KT[:, cc, sl], PS[:, bank, :], psk(bank), [("KT", cc, r)])
                            elif blk == 3:
                                if cc < 2:
                                    evac(gqT[:, cc, sl], PS[:, bank, :], psk(bank), [("gqT", 4 * r + i) for i in range(4)], scale=0.125)
                                else:
                                    evac(gkT[:, cc - 2, sl], PS[:, bank, :], psk(bank), [("gkT", 4 * r + i) for i in range(4)])
                            else:
                                evac(G33[0:32, sl], PS[0:32, bank, :], psk(bank), ["G33"])
                if blk == 3:
                    for t in range(NT):
                        bank = nextbank()
                        for cc in range(2):
                            sc.op("pe", lambda e, t=t, cc=cc, bank=bank: e.transpose(out=PSb[:, bank, cc * 128:(cc + 1) * 128], in_=gkT[:, cc, t * 128:(t + 1) * 128], identity=identb),
                                  reads=[("gkT", t), "identb"], writes=psk(bank))
                        evac(gk_tok[:, t, :], PSb[:, bank, 0:256], psk(bank), [("gk_tok", t)])
                if blk in (2, 4, 5):
                    for t in range(NT):
                        bank = nextbank()
                        c0, ncol = (0, 512)
                        for c in range(8):
                            sc.op("pe", lambda e, c=c, t=t, bank=bank, c0=c0, ncol=ncol, wb=wb: e.matmul(
                                PS[:, bank, 0:ncol], lhsT=XT[:, c, t * 128:(t + 1) * 128], rhs=wb[:, c, c0:c0 + ncol],
                                start=(c == 0), stop=(c == 7)), reads=kw + [("XT", t)], writes=psk(bank))
                        if blk == 2:
                            evac(V[:, t, :, 0:128], PS[:, bank, :].rearrange("p (h e) -> p h e", h=4), psk(bank), [("V", t)])
                        elif blk == 4:
                            evac(gv[:, t, :], PS[:, bank, :], psk(bank), [("gv", t)])
                        else:
                            sb = t % 2
                            sc.op("act", lambda e, bank=bank, sb=sb: e.activation(out=silu_t[sb], in_=PS[:, bank, :], func=AF.Silu),
                                  reads=psk(bank), writes=[("silu", sb)])
                            sc.op("dve", lambda e, t=t, sb=sb: e.tensor_tensor(
                                out=gr_s[:, t, :].rearrange("p (h e) -> p h e", h=4), in0=silu_t[sb].rearrange("p (h e) -> p h e", h=4),
                                in1=bc_mid(wg_t, 4), op=ALU.mult), reads=[("silu", sb), "wg"], writes=[("gr_s", t)])
                if blk + 2 < 7:
                    load_win_block(l, blk + 2, buf)
            for hf in range(2):
                sc.dma("pool", lambda e, hf=hf: e.dma_start(out=WO[:, hf * 4:(hf + 1) * 4, :],
                                                             in_=w_o_d[l].rearrange("(c p) n -> p c n", p=128)[:, hf * 4:(hf + 1) * 4, :]),
                       writes=[("RW", hf, 0), ("RW", hf, 1)])
            dump("QT", QT, [128, 4, 2048], [("QT", a, b) for a in range(4) for b in range(4)])
            dump("KT", KT, [128, 4, 2048], [("KT", a, b) for a in range(4) for b in range(4)])
            dump("V", V, [128, 16, 4, 129], [("V", t) for t in range(NT)])
            dump("expB", expB, [128, 4, 1152], [("expB", h) for h in range(4)])
            dump("gqT", gqT, [128, 2, 2048], [("gqT", t) for t in range(NT)])
            dump("gr_s", gr_s, [128, 16, 512], [("gr_s", t) for t in range(NT)])
            dump("G33", G33[0:33, :], [33, 2048], ["G33"])

            if stop == 'P' and l == nl - 1:
                raise _Stop()
            steps = [(h, r, j) for h in range(4) for r in range(4) for j in range(16)]

            def acc_ap(m, u):
                idx = m * 4 + u
                return PS[:, 4 + idx // 3, (idx % 3) * 160:(idx % 3) * 160 + 129]

            def acc_keys(m, u):
                return psk(4 + (m * 4 + u) // 3)

            Eb3 = view(R_X.base + 61440, 2048, BF16, "p (m n) -> p m n", m=2)
            EbL = [Eb[:, 0, :, :], Eb[:, 1, :, :], Eb3]

            def d_scores(i):
                h, r, j = steps[i]
                d = j - 4 * r
                mixed = (-1 <= d <= 4)
                sb = i % 2
                eb = i % 3
                E = EbL[eb]
                for m in range(2):
                    bank = sb * 2 + m
                    sc.op("pe", lambda e, h=h, r=r, j=j, m=m, bank=bank: e.matmul(
                        PS[:, bank, :], lhsT=KT[64 * m:64 * m + 64, h, j * 128:(j + 1) * 128],
                        rhs=QT[64 * m:64 * m + 64, h, r * 512:(r + 1) * 512], start=True, stop=True),
                        reads=[("KT", h, j // 4), ("QT", h, r)], writes=psk(bank))
                pk2 = psk(sb * 2) + psk(sb * 2 + 1)
                ek = [("E", eb, 0), ("E", eb, 1)]
                if mixed:
                    c0 = (4 - d) * 128
                    sc.op("act", lambda e, sb=sb, E=E: e.activation(out=E, in_=PS[:, sb * 2:sb * 2 + 2, :], func=AF.Exp),
                          reads=pk2, writes=ek)
                    for m in range(2):
                        sc.op("dve", lambda e, E=E, m=m, h=h, c0=c0: e.tensor_tensor(out=E[:, m, :], in0=E[:, m, :], in1=expB[:, h, c0:c0 + 512], op=ALU.mult),
                              reads=[("E", eb, m), ("expB", h)], writes=[("E", eb, m)])
                else:
                    side = 0 if d < 0 else 1
                    sc.op("act", lambda e, sb=sb, E=E, side=side, h=h: e.activation(
                        out=E, in_=PS[:, sb * 2:sb * 2 + 2, :], func=AF.Exp, bias=cb[:, side, h:h + 1]),
                        reads=pk2 + [("cb", 0), ("cb", 1)], writes=ek)

            def d_av(i):
                h, r, j = steps[i]
                eb = i % 3
                E = EbL[eb]
                for m in range(2):
                    for u in range(4):
                        sc.op("pe", lambda e, h=h, j=j, m=m, u=u, E=E: e.matmul(
                            acc_ap(m, u), lhsT=E[:, m, u * 128:(u + 1) * 128], rhs=V[:, j, h, 0:129],
                            start=(j == 0 and (m * 4 + u) % 3 == 0), stop=(j == 15), skip_group_check=True),
                            reads=[("E", eb, m), ("V", j)], writes=acc_keys(m, u))

            sm = dsm[0]
            ka = ["accS", ("silu", 0), ("silu", 1)]

            def d_final(h, r):
                sc.op("dve", lambda e: e.tensor_copy(out=accS[:, 0:3, :], in_=PS[:, 4, 0:480].rearrange("p (a n) -> p a n", a=3)[:, :, 0:129]),
                      reads=psk(4), writes=ka)
                sc.op("dve", lambda e: e.tensor_copy(out=accS[:, 3:6, :], in_=PS[:, 5, 0:480].rearrange("p (a n) -> p a n", a=3)[:, :, 0:129]),
                      reads=psk(5), writes=ka)
                sc.op("dve", lambda e: e.tensor_copy(out=accS[:, 6:8, :], in_=PS[:, 6, 0:320].rearrange("p (a n) -> p a n", a=2)[:, :, 0:129]),
                      reads=psk(6), writes=ka)

            def d_final2(h, r):
                sc.op("dve", lambda e: e.reciprocal(out=sm[:, 0:8], in_=accS[:, :, 128]), reads=ka[:1], writes=["dsm"])
                sc.op("dve", lambda e: e.tensor_scalar(out=sm[:, 4:8], in0=sm[:, 4:8], scalar1=neg_lam, scalar2=None, op0=ALU.mult),
                      reads=["dsm", "neglam"], writes=["dsm"])
                sc.op("dve", lambda e: e.memset(sm[:, 8:12], 0.0), writes=["dss"])

            def d_final_u(u):
                if True:
                    sc.op("dve", lambda e, u=u: e.tensor_scalar(out=accS[:, u, 0:128], in0=accS[:, u, 0:128], scalar1=sm[:, u:u + 1], scalar2=None, op0=ALU.mult),
                          reads=["dsm"] + ka[:1], writes=ka[:1])
                    sc.op("dve", lambda e, u=u: e.scalar_tensor_tensor(out=accS[:, u, 0:128], in0=accS[:, 4 + u, 0:128], scalar=sm[:, 4 + u:5 + u],
                                                                       in1=accS[:, u, 0:128], op0=ALU.mult, op1=ALU.add),
                          reads=["dsm"] + ka[:1], writes=ka[:1])
                    sc.op("dve", lambda e, u=u: e.scalar_tensor_tensor(out=accS[:, 4 + u, 0:128], in0=accS[:, u, 0:128], scalar=1.0, in1=accS[:, u, 0:128],
                                                                       op0=ALU.mult, op1=ALU.mult, accum_out=sm[:, 8 + u:9 + u]),
                          reads=ka[:1], writes=ka[:1] + ["dss"])

            def d_final_b(h, r):
                sc.op("act", lambda e: e.activation(out=sm[:, 12:16], in_=sm[:, 8:12], func=AF.Ln, bias=1e-5, scale=1.0 / 128), reads=["dss"], writes=["drs"])
                sc.op("act", lambda e: e.activation(out=sm[:, 12:16], in_=sm[:, 12:16], func=AF.Exp, scale=-0.5), reads=["drs"], writes=["drs"])
                for u in range(4):
                    sc.op("dve", lambda e, u=u: e.scalar_tensor_tensor(out=d_y[:, u, :], in0=accS[:, u, 0:128], scalar=sm[:, 12 + u:13 + u], in1=wd_t,
                                                                       op0=ALU.mult, op1=ALU.mult),
                          reads=ka[:1] + ["drs", "wd"], writes=[("dy", u)])

            def d_final_pe(h, r):
                for u in range(4):
                    sc.op("pe", lambda e, u=u: e.transpose(out=PSb[:, 7, u * 128:(u + 1) * 128], in_=d_y[:, u, :], identity=identb),
                          reads=[("dy", u), "identb"], writes=psk(7))
                sc.op("dve", lambda e, h=h, r=r: e.tensor_copy(out=XT[:, h, r * 512:(r + 1) * 512], in_=PSb[:, 7, 0:512]),
                      reads=psk(7), writes=[("XT", 4 * r + i) for i in range(4)])

            pend = []
            pend_b = []
            pend_u = []
            d_scores(0)
            d_scores(1)
            for i in range(len(steps)):
                h, r, j = steps[i]
                if j == 15:
                    d_av(i)
                    d_final(h, r)
                    if i + 2 < len(steps):
                        d_scores(i + 2)
                    d_final2(h, r)
                else:
                    if i + 2 < len(steps):
                        d_scores(i + 2)
                    d_av(i)
                if j == 15:
                    pend.append((h, r))
                    pend_b.append((h, r))
                    pend_u.extend([0, 1, 2, 3])
                    d_final_u(pend_u.pop(0))
                elif pend_u:
                    d_final_u(pend_u.pop(0))
                elif j == 4 and pend_b:
                    d_final_b(*pend_b.pop(0))
                elif j == 7 and pend:
                    d_final_pe(*pend.pop(0))
            while pend_u:
                d_final_u(pend_u.pop(0))
            while pend_b:
                d_final_b(*pend_b.pop(0))
            while pend:
                d_final_pe(*pend.pop(0))
            dump("mixT_d", XT, [128, 8, 2048], [("XT", t) for t in range(NT)])

            if stop == 'D' and l == nl - 1:
                raise _Stop()
            sc.barrier()
            R_D.reset()
            qf = R_D.get(8192, BF16, "p (c n) -> p c n", c=2)
            kf = R_D.get(8192, BF16, "p (c n) -> p c n", c=2)
            kd_f = R_D.get(8192, BF16, "p (t n) -> p t n", t=16)
            Sbf = R_D.get(16384, BF16, "p (d q t e) -> p d q t e", d=2, q=2, t=16)
            stm2 = [R_D.get(4096, F32, "p (d q n) -> p d q n", d=2, q=2) for _ in range(2)]
            _p0 = R_D.pos
            sp_ = [R_D.get(2048, F32) for _ in range(2)]
            _p1 = R_D.pos
            ebt = [R_D.get(2 * 2 * 129 * 4, F32, "p (d q n) -> p d q n", d=2, q=2) for _ in range(2)]
            _p2 = R_D.pos
            enbt = [R_D.get(2 * 2 * 128 * 4, F32, "p (d q n) -> p d q n", d=2, q=2) for _ in range(2)]
            erem = [R_D.get(2048, F32) for _ in range(2)]
            dS = [R_D.get(1024, F32) for _ in range(4)]
            _pend = R_D.pos
            Am = [R_X.get(4 * 2 * 128 * 2, BF16, "p (h d n) -> p h d n", h=4, d=2) for _ in range(2)]
            R_D.pos = _p2
            g_y = [R_D.get(1024, BF16) for _ in range(2)]
            g_junk = R_D.get(512, F32)
            R_D.pos = _pend
            qb, kb, kd_b = gqT, gkT, gk_tok
            maskf = Uf[:, 0:128]
            maskb = Usf

            def tl(t):
                return slice(t * 128, (t + 1) * 128)

            def prep_A(t):
                b = t % 2
                sp = sp_[b]
                zb = 0 if b == 0 else 7
                sc.op("pe", lambda e: e.matmul(PS[:, zb, :], lhsT=G33[0:33, tl(t)], rhs=Wg[0:33, :], start=True, stop=True),
                      reads=["G33", "Wg"], writes=psk(zb))
                sc.op("act", lambda e: e.activation(out=sp, in_=PS[:, zb, :], func=AF.Exp, scale=-1.0), reads=psk(zb), writes=[("sp", b)])
                sc.op("act", lambda e: e.activation(out=sp, in_=sp, func=AF.Ln, bias=1.0, scale=1.0), reads=[("sp", b)], writes=[("sp", b)])

            prep_A(0)
            for t in range(NT):
                b = t % 2
                sp = sp_[b]
                if t + 1 < NT:
                    prep_A(t + 1)
                sc.op("pe", lambda e, sp=sp: e.matmul(PS[:, 1, 0:256], lhsT=Usf, rhs=sp[:, 0:256], start=True, stop=True), reads=[("sp", b), "Usf", "Usb"], writes=psk(1))
                sc.op("pe", lambda e, sp=sp: e.matmul(PS[:, 1, 256:512], lhsT=Usb, rhs=sp[:, 256:512], start=True, stop=True), reads=[("sp", b), "Usf", "Usb"], writes=psk(1))
                sc.op("act", lambda e, b=b: e.activation(out=erem[b], in_=PS[:, 1, :], func=AF.Exp, scale=-1.0 / 16), reads=psk(1), writes=[("erem", b)])
                sc.op("dve", lambda e, t=t, b=b: e.tensor_tensor(out=kd_f[:, t, :], in0=gk_tok[:, t, :], in1=erem[b][:, 0:256], op=ALU.mult),
                      reads=[("gk_tok", t), ("erem", b)], writes=[("kd_f", t)])
                sc.op("dve", lambda e, t=t, b=b: e.tensor_tensor(out=kd_b[:, t, :], in0=gk_tok[:, t, :], in1=erem[b][:, 256:512], op=ALU.mult),
                      reads=[("gk_tok", t), ("erem", b), ("kd_f", t)], writes=[("gk_tok", t)])
                for d in range(2):
                    U = Uf if d == 0 else Ub
                    for q in range(2):
                        sc.op("pe", lambda e, sp=sp, d=d, q=q, U=U: e.matmul(PS[:, 2 + d, q * 160:q * 160 + 129],
                                                                            lhsT=sp[:, d * 256 + q * 128:d * 256 + (q + 1) * 128], rhs=U, start=True, stop=True),
                              reads=[("sp", b), "Uf", "Ub"], writes=psk(2 + d))
                src4 = PS[:, 2:4, 0:320].rearrange("p a (q n) -> p a q n", q=2)
                sc.op("act", lambda e, b=b, src4=src4: e.activation(out=ebt[b], in_=src4[:, :, :, 0:129], func=AF.Exp, scale=-1.0 / 16),
                      reads=psk(2) + psk(3), writes=[("eb", b, 0), ("eb", b, 1)])
                sc.op("act", lambda e, b=b, src4=src4: e.activation(out=enbt[b], in_=src4[:, :, :, 0:128], func=AF.Exp, scale=1.0 / 16),
                      reads=psk(2) + psk(3), writes=[("enb", b, 0), ("enb", b, 1)])
                sc.op("dve", lambda e, t=t, b=b: e.tensor_tensor(out=qf[:, :, tl(t)], in0=gqT[:, :, tl(t)], in1=ebt[b][:, 0, :, 0:128], op=ALU.mult),
                      reads=[("gqT", t), ("eb", b, 0)], writes=[("qf", t)])
                sc.op("dve", lambda e, t=t, b=b: e.tensor_tensor(out=kf[:, :, tl(t)], in0=gkT[:, :, tl(t)], in1=enbt[b][:, 0, :, :], op=ALU.mult),
                      reads=[("gkT", t), ("enb", b, 0)], writes=[("kf", t)])
                sc.op("dve", lambda e, t=t, b=b: e.tensor_tensor(out=qb[:, :, tl(t)], in0=gqT[:, :, tl(t)], in1=ebt[b][:, 1, :, 0:128], op=ALU.mult),
                      reads=[("gqT", t), ("eb", b, 1), ("qf", t)], writes=[("gqT", t)])
                sc.op("dve", lambda e, t=t, b=b: e.tensor_tensor(out=kb[:, :, tl(t)], in0=gkT[:, :, tl(t)], in1=enbt[b][:, 1, :, :], op=ALU.mult),
                      reads=[("gkT", t), ("enb", b, 1), ("kf", t)], writes=[("gkT", t)])
                sc.op("dve", lambda e, t=t, b=b: e.tensor_copy(out=decs[:, :, :, t:t + 1], in_=ebt[b][:, :, :, 128:129]),
                      reads=[("eb", b, 0), ("eb", b, 1)], writes=["decs"])
            dump("qf", qf, [128, 2, 2048], [("qf", t) for t in range(NT)])
            dump("kd_f", kd_f, [128, 16, 256], [("kd_f", t) for t in range(NT)])
            dump("decs", decs, [128, 2, 2, 16], ["decs"])

            if stop == 'G1' and l == nl - 1:
                raise _Stop()
            sc.op("dve", lambda e: e.memset(stm2[0], 0.0), writes=[("stm", 0, d, q) for d in range(2) for q in range(2)])
            chains = [(d, q) for d in range(2) for q in range(2)]
            par = {c: 0 for c in chains}
            for i in range(NT):
                todo = []
                for ci, (d, q) in enumerate(chains):
                    t = i if d == 0 else NT - 1 - i
                    cur = par[(d, q)]
                    if i > 0:
                        sc.op("act", lambda e, d=d, q=q, t=t, cur=cur: e.activation(out=Sbf[0:64, d, q, t, :], in_=stm2[cur][0:64, d, q, 0:128], func=AF.Copy),
                              reads=[("stm", cur, d, q)], writes=[("Sbf", d, q, t)])
                        sc.op("dve", lambda e, d=d, q=q, t=t, cur=cur: e.tensor_copy(out=Sbf[64:128, d, q, t, :], in_=stm2[cur][64:128, d, q, 128:256]),
                              reads=[("stm", cur, d, q)], writes=[("Sbf", d, q, t)])
                    if i == NT - 1:
                        continue
                    kd = kd_f if d == 0 else kd_b
                    kkey = "kd_f" if d == 0 else "gk_tok"
                    pslot = ci
                    pk = psk(4 + pslot)
                    sc.op("pe", lambda e, kd=kd, t=t, q=q, pslot=pslot: e.matmul(PS[:, 4 + pslot, 0:256], lhsT=kd[:, t, q * 128:(q + 1) * 128],
                                                                                rhs=gv[:, t, q * 256:(q + 1) * 256], start=True, stop=True),
                          reads=[(kkey, t), ("gv", t)], writes=pk)
                    todo.append((ci, d, q, t, cur))
                for (ci, d, q, t, cur) in todo:
                    if os.environ.get("GSKIP") == "upd":
                        par[(d, q)] = 1 - cur
                        continue
                    sc.op("dve", lambda e, ci=ci, d=d, q=q, t=t, cur=cur: e.scalar_tensor_tensor(
                        out=stm2[1 - cur][:, d, q, :], in0=stm2[cur][:, d, q, :], scalar=decs[:, d, q, t:t + 1], in1=PS[:, 4 + ci, 0:256],
                        op0=ALU.mult, op1=ALU.add), reads=[("stm", cur, d, q), "decs"] + psk(4 + ci), writes=[("stm", 1 - cur, d, q)])
                    par[(d, q)] = 1 - cur
            dump("Sbf", Sbf, [128, 2, 2, 16, 128], [("Sbf", d, q, t) for d in range(2) for q in range(2) for t in range(NT)])
            if stop == 'G2' and l == nl - 1:
                raise _Stop()

            def g_A(t):
                b = t % 2
                A = Am[b]
                for half in range(2):
                    sbank = (4 + half) if os.environ.get('GBANK') else (2 * b + half)
                    items = []
                    for sq in range(4):
                        h, d = half + 2 * (sq // 2), sq % 2
                        q = h // 2
                        base = (h % 2) * 64
                        kk = kf if d == 0 else kb
                        qq = qf if d == 0 else qb
                        kkey = ("kf", t) if d == 0 else ("gkT", t)
                        qkey = ("qf", t) if d == 0 else ("gqT", t)
                        sc.op("pe", lambda e, kk=kk, qq=qq, q=q, base=base, sbank=sbank, sq=sq: e.matmul(
                            PS[:, sbank, sq * 128:(sq + 1) * 128], lhsT=kk[base:base + 64, q, tl(t)], rhs=qq[base:base + 64, q, tl(t)], start=True, stop=True),
                            reads=[kkey, qkey], writes=psk(sbank))
                        items.append((sq, h, d))
                        if os.environ.get("GOLD"):
                            mk = maskf if d == 0 else maskb
                            sc.op("dve", lambda e, A=A, h=h, d=d, sbank=sbank, sq=sq, mk=mk: e.tensor_tensor(
                                out=A[:, h, d, :], in0=PS[:, sbank, sq * 128:(sq + 1) * 128], in1=mk, op=ALU.mult),
                                reads=psk(sbank) + ["Uf", "Usf"], writes=[("A", b, h, d), ("sp", b)])
                    if os.environ.get("GOLD"):
                        continue
                    for (sq, h, d) in items:
                        mk = maskf if d == 0 else maskb
                        sc.op("dve", lambda e, A=A, h=h, d=d, sbank=sbank, sq=sq, mk=mk: e.tensor_tensor(
                            out=A[:, h, d, :], in0=PS[:, sbank, sq * 128:(sq + 1) * 128], in1=mk, op=ALU.mult),
                            reads=psk(sbank) + ["Uf", "Usf"], writes=[("A", b, h, d), ("sp", b)])

            def g_B(t):
                b = t % 2
                A = Am[b]
                obank = (0 + b) if os.environ.get('GBANK') else (4 + b)
                for h in range(4):
                    q = h // 2
                    base = (h % 2) * 64
                    oh_ = PS[:, obank, h * 128:(h + 1) * 128]
                    ok = psk(obank)
                    inter_f = t > 0
                    inter_b = t < NT - 1
                    sc.op("pe", lambda e, A=A, h=h, oh_=oh_: e.matmul(oh_, lhsT=A[:, h, 0, :], rhs=gv[:, t, h * 128:(h + 1) * 128], start=True, stop=False),
                          reads=[("A", b, h, 0), ("gv", t)], writes=ok)
                    sc.op("pe", lambda e, A=A, h=h, oh_=oh_, fin=(not inter_f and not inter_b): e.matmul(
                        oh_, lhsT=A[:, h, 1, :], rhs=gv[:, t, h * 128:(h + 1) * 128], start=False, stop=fin),
                        reads=[("A", b, h, 1), ("gv", t)], writes=ok)
                    if inter_f:
                        sc.op("pe", lambda e, q=q, base=base, oh_=oh_, fin=(not inter_b): e.matmul(
                            oh_, lhsT=qf[base:base + 64, q, tl(t)], rhs=Sbf[base:base + 64, 0, q, t, :], start=False, stop=fin),
                            reads=[("qf", t), ("Sbf", 0, q, t)], writes=ok)
                    if inter_b:
                        sc.op("pe", lambda e, q=q, base=base, oh_=oh_: e.matmul(
                            oh_, lhsT=qb[base:base + 64, q, tl(t)], rhs=Sbf[base:base + 64, 1, q, t, :], start=False, stop=True),
                            reads=[("gqT", t), ("Sbf", 1, q, t)], writes=ok)

            def g_norm(t):
                b = t % 2
                sm = gsm[b]
                obank = (0 + b) if os.environ.get('GBANK') else (4 + b)
                okall = psk(obank)
                for h in range(4):
                    sc.op("act", lambda e, h=h, sm=sm: e.activation(out=g_junk, in_=PS[:, obank, h * 128:(h + 1) * 128], func=AF.Square, accum_out=sm[:, h:h + 1]),
                          reads=okall, writes=["gjunk", ("gss", b), ("enb", 1, 0), ("enb", 1, 1)])
                sc.op("act", lambda e, sm=sm: e.activation(out=sm[:, 4:8], in_=sm[:, 0:4], func=AF.Sqrt, bias=1e-5, scale=1.0 / 128),
                      reads=[("gss", b)], writes=[("grs", b)])
                sc.op("dve", lambda e, sm=sm: e.reciprocal(out=sm[:, 4:8], in_=sm[:, 4:8]), reads=[("grs", b)], writes=[("grs", b)])
                for h in range(4):
                    sc.op("dve", lambda e, h=h, sm=sm, b=b: e.scalar_tensor_tensor(
                        out=g_y[b][:, h * 128:(h + 1) * 128], in0=PS[:, obank, h * 128:(h + 1) * 128], scalar=sm[:, 4 + h:5 + h],
                        in1=gr_s[:, t, h * 128:(h + 1) * 128], op0=ALU.mult, op1=ALU.mult),
                        reads=psk(obank) + [("grs", b), ("gr_s", t)], writes=[("gy", b), ("enb", 0, 0), ("enb", 0, 1)])

            def g_tr(t):
                b = t % 2
                tk = psk(6 + b)
                for h in range(4):
                    sc.op("pe", lambda e, h=h, b=b: e.transpose(out=PSb[:, 6 + b, h * 128:(h + 1) * 128], in_=g_y[b][:, h * 128:(h + 1) * 128], identity=identb),
                          reads=[("gy", b), "identb"], writes=tk)
                sc.op("act", lambda e, b=b: e.activation(out=XT[:, 4:8, tl(t)], in_=PSb[:, 6 + b, 0:512].rearrange("p (c n) -> p c n", c=4), func=AF.Copy),
                      reads=tk, writes=[("XT", t)])

            g_A(0)
            for t in range(NT):
                if t + 1 < NT:
                    g_A(t + 1)
                if os.environ.get("GSKIP") == "B":
                    continue
                g_B(t)
                if os.environ.get("GSKIP") == "norm":
                    continue
                g_norm(t)
                if os.environ.get("GSKIP") == "tr":
                    continue
                if t > 0:
                    g_tr(t - 1)
            if not os.environ.get("GSKIP"):
                g_tr(NT - 1)
            dump("mixT", XT, [128, 8, 2048], [("XT", t) for t in range(NT)])

            if stop == 'G' and l == nl - 1:
                raise _Stop()
            sc.barrier()
            load_ln_params(ln1g_d[l], ln1b_d[l])
            R_D.reset()
            W1B = [R_D.get(8192, BF16, "p (c n) -> p c n", c=8) for _ in range(2)]
            W2B = [R_D.get(8192, BF16, "p (c n) -> p c n", c=4) for _ in range(2)]
            hT = R_D.get(16384, BF16, "p (c n) -> p c n", c=4)
            relu_t = [R_D.get(2048, F32) for _ in range(2)]

            def load_ffn_block(fb, buf):
                s1 = w1_d[l, :, fb * 512:(fb + 1) * 512].rearrange("(c p) n -> p c n", p=128)
                s2 = w2_d[l, fb * 512:(fb + 1) * 512, :].rearrange("(c p) n -> p c n", p=128)
                for hf in range(2):
                    sc.dma("pool", lambda e, hf=hf: e.dma_start(out=W1B[buf][:, hf * 4:(hf + 1) * 4, :], in_=s1[:, hf * 4:(hf + 1) * 4, :]),
                           writes=[("W1B", buf, hf)])
                for hf in range(2):
                    sc.dma("pool", lambda e, hf=hf: e.dma_start(out=W2B[buf][:, hf * 2:(hf + 1) * 2, :], in_=s2[:, hf * 2:(hf + 1) * 2, :]),
                           writes=[("W2B", buf, hf)])

            load_ffn_block(0, 0)
            load_ffn_block(1, 1)
            for t in range(NT):
                sc.dma("sp", lambda e, t=t: e.dma_start(out=X[:, t, :], in_=xs_d[t * 128:(t + 1) * 128, :]), reads=[("xsd", t)], writes=[("X", t)])
            def o_mm(t):
                yb = (t % 3) * 2
                for hf in range(2):
                    for c in range(8):
                        sc.op("pe", lambda e, c=c, hf=hf, t=t, yb=yb: e.matmul(PS[:, yb + hf, :], lhsT=XT[:, c, tl(t)], rhs=WO[:, c, hf * 512:(hf + 1) * 512],
                                                                              start=(c == 0), stop=(c == 7)),
                              reads=[("XT", t), ("RW", 0, 0), ("RW", 0, 1), ("RW", 1, 0), ("RW", 1, 1)], writes=psk(yb + hf))

            def o_ln(t):
                yb = (t % 3) * 2
                sc.op("dve", lambda e, t=t, yb=yb: e.scalar_tensor_tensor(out=X[:, t, :], in0=X[:, t, :], scalar=ALPHA,
                                                                          in1=PS[:, yb:yb + 2, :].rearrange("p a n -> p (a n)"), op0=ALU.mult, op1=ALU.add),
                      reads=[("X", t)] + psk(yb) + psk(yb + 1), writes=[("X", t)])
                ln_a(t)

            for t0 in range(3):
                o_mm(t0)
                o_ln(t0)
            for t in range(NT):
                if t + 3 < NT:
                    o_mm(t + 3)
                ln_b1(t, None)
                if t + 3 < NT:
                    o_ln(t + 3)
                ln_b2(t, None)
            dump("x1T", XT, [128, 8, 2048], [("XT", t) for t in range(NT)])

            if stop == 'O' and l == nl - 1:
                raise _Stop()
            if not last:
                load_win_block(l + 1, 0, 0)
                load_win_block(l + 1, 1, 1)
            hrr = [0]
            for fb in range(8):
                buf = fb % 2
                for r in range(4):
                    for fc in range(4):
                        bank = 4 + hrr[0] % 3
                        rb = hrr[0] % 2
                        hrr[0] += 1
                        for c in range(8):
                            sc.op("pe", lambda e, c=c, fc=fc, r=r, bank=bank, buf=buf: e.matmul(
                                PS[:, bank, :], lhsT=W1B[buf][:, c, fc * 128:(fc + 1) * 128], rhs=XT[:, c, r * 512:(r + 1) * 512],
                                start=(c == 0), stop=(c == 7)), reads=[("W1B", buf, 0), ("W1B", buf, 1)] + xt_all[r * 4:(r + 1) * 4], writes=psk(bank))
                        fcol = fb * 4 + fc
                        sc.op("act", lambda e, bank=bank, rb=rb, fcol=fcol: e.activation(out=relu_t[rb], in_=PS[:, bank, :], func=AF.Relu,
                                                                                         bias=b1c[:, fcol:fcol + 1], scale=1.0),
                              reads=psk(bank) + ["b1c"], writes=[("relu", rb)])
                        sc.op("dve", lambda e, rb=rb, fc=fc, r=r: e.tensor_tensor(out=hT[:, fc, r * 512:(r + 1) * 512], in0=relu_t[rb], in1=relu_t[rb], op=ALU.mult),
                              reads=[("relu", rb)], writes=[("hT", fc, r)])
                for t in range(NT):
                    yb = (t % 2) * 2
                    for hf in range(2):
                        for fc in range(4):
                            sc.op("pe", lambda e, fc=fc, hf=hf, t=t, yb=yb, buf=buf: e.matmul(
                                PS[:, yb + hf, :], lhsT=hT[:, fc, tl(t)], rhs=W2B[buf][:, fc, hf * 512:(hf + 1) * 512],
                                start=(fc == 0), stop=(fc == 3)), reads=[("hT", fc, t // 4), ("W2B", buf, 0), ("W2B", buf, 1)], writes=psk(yb + hf))
                    if fb == 0:
                        sc.op("dve", lambda e, t=t, yb=yb: e.scalar_tensor_tensor(out=X[:, t, :], in0=X[:, t, :], scalar=ALPHA,
                                                                                  in1=PS[:, yb:yb + 2, :].rearrange("p a n -> p (a n)"), op0=ALU.mult, op1=ALU.add),
                              reads=[("X", t)] + psk(yb) + psk(yb + 1), writes=[("X", t)])
                        sc.op("pool", lambda e, t=t: e.tensor_tensor(out=X[:, t, :], in0=X[:, t, :], in1=b2t, op=ALU.add), reads=[("X", t), "b2t"], writes=[("X", t)])
                    else:
                        sc.op("dve", lambda e, t=t, yb=yb: e.tensor_tensor(out=X[:, t, :], in0=X[:, t, :], in1=PS[:, yb:yb + 2, :].rearrange("p a n -> p (a n)"), op=ALU.add),
                              reads=[("X", t)] + psk(yb) + psk(yb + 1), writes=[("X", t)])
                if fb + 2 < 8:
                    load_ffn_block(fb + 2, buf)
            if stop == 'F' and l == nl - 1:
                raise _Stop()
            load_ln_params(ln2g_d[l], ln2b_d[l])
            ln_all(out_d if last else xs_d)
            sc.barrier()


        for _l in range(nl):
            do_layer(_l)
    except _Stop:
        pass
    out_dmas = [o for o in sc.ops if o.is_dma and o.dkey in [("xs", i) for i in range(4)]]
    fin = {}
    for o in out_dmas:
        fin[o.dkey] = o
    finals = list(fin.values()) + list(dbg_out.values())
    sc.emit(final_wait_ops=finals)
    es.close()
    return nc, sc


_CONST = None


def kernel(**inputs):
    global _CONST
    if _CONST is None:
        _CONST = _constants()
    nc, _ = build(2)
    x = np.ascontiguousarray(inputs["x"], dtype=np.float32)
    shared = {k: np.ascontiguousarray(v, dtype=np.float32) for k, v in inputs.items() if k != "x"}
    shared.update(_CONST)
    in_maps = []
    for b in range(8):
        m = dict(shared)
        m["x"] = x[b]
        in_maps.append(m)
    res = run_bass_kernel_spmd(nc, in_maps, core_ids=list(range(8)))
    return np.stack([r["out"] for r in res.results], axis=0).astype(np.float32)
```

```python
import math
import os
from contextlib import ExitStack

import numpy as np
import concourse.bass as bass
import concourse.mybir as mybir
from concourse.bass_utils import run_bass_kernel_spmd

F32 = mybir.dt.float32
BF16 = mybir.dt.bfloat16
AF = mybir.ActivationFunctionType
ALU = mybir.AluOpType

S = 2048
D = 1024
DIN = 3104
DFF = 4096
NT = 16
ALPHA = (2.0 * 2) ** 0.25
ENGS = ("pe", "act", "dve", "pool", "sp")
EPOCH = 30000


class _Res:
    __slots__ = ("last_w", "readers")

    def __init__(self):
        self.last_w = None
        self.readers = []


class _Op:
    __slots__ = ("eng", "fn", "deps", "signal", "tok", "is_dma", "dkey")

    def __init__(self, eng, fn, is_dma, dkey):
        self.eng = eng
        self.fn = fn
        self.deps = []
        self.signal = False
        self.tok = None
        self.is_dma = is_dma
        self.dkey = dkey


class Sched:
    def __init__(self, nc):
        self.nc = nc
        self.ops = []
        self.res = {}
        self.pending = {e: [] for e in ENGS}

    def _r(self, key):
        x = self.res.get(key)
        if x is None:
            x = self.res[key] = _Res()
        return x

    def _add(self, op, reads, writes):
        deps = set()
        for k in reads:
            rs = self._r(k)
            if rs.last_w is not None:
                deps.add(rs.last_w)
        for k in writes:
            rs = self._r(k)
            if rs.last_w is not None:
                deps.add(rs.last_w)
            deps.update(rs.readers)
        for k in reads:
            self._r(k).readers.append(op)
        for k in writes:
            rs = self._r(k)
            rs.last_w = op
            rs.readers = []
        if self.pending[op.eng]:
            deps.update(self.pending[op.eng])
            self.pending[op.eng] = []
        deps.discard(op)
        op.deps = list(deps)
        self.ops.append(op)
        return op

    def op(self, eng, fn, reads=(), writes=()):
        return self._add(_Op(eng, fn, False, None), reads, writes)

    def dma(self, eng, fn, dkey=None, reads=(), writes=()):
        if dkey is None:
            dkey = ("w", writes[0])
        return self._add(_Op(eng, fn, True, dkey), reads, writes)

    def barrier(self):
        last = {}
        for o in self.ops:
            last[(o.eng, o.dkey) if o.is_dma else o.eng] = o
        b = list(last.values())
        self.pending = {e: list(b) for e in ENGS}

    def emit(self, final_wait_ops=()):
        nc = self.nc
        ops = self.ops
        for o in ops:
            for d in o.deps:
                if d.is_dma:
                    d.signal = True
                elif d.eng == "pe" and o.eng == "pe" and not o.is_dma:
                    continue
                else:
                    d.signal = True
        with ExitStack() as es:
            eng_sems = {e: [] for e in ENGS}
            cnt = {e: 0 for e in ENGS}
            dma_sems = {}
            dma_cnt = {}
            for o in ops:
                if o.is_dma:
                    if o.dkey not in dma_sems:
                        dma_sems[o.dkey] = es.enter_context(nc.semaphore("d%d" % len(dma_sems)))
                        dma_cnt[o.dkey] = 0
                    dma_cnt[o.dkey] += 16
                    o.tok = (dma_sems[o.dkey], dma_cnt[o.dkey])
                elif o.signal:
                    ep = cnt[o.eng] // EPOCH
                    if ep >= len(eng_sems[o.eng]):
                        eng_sems[o.eng].append(es.enter_context(nc.semaphore("e_%s_%d" % (o.eng, ep))))
                    cnt[o.eng] += 1
                    o.tok = (eng_sems[o.eng][ep], cnt[o.eng] - ep * EPOCH)
            per_eng = {e: [o for o in ops if o.eng == e] for e in ENGS}
            self.stats = {e: len(per_eng[e]) for e in ENGS}
            self.stats["sems"] = sum(len(v) for v in eng_sems.values()) + len(dma_sems)

            def run(e, eng):
                waited = {}
                for o in per_eng[e]:
                    need = {}
                    for d in o.deps:
                        if d.tok is None:
                            continue
                        if (not d.is_dma) and d.eng == "pe" and e == "pe" and not o.is_dma:
                            continue
                        s, v = d.tok
                        k = id(s)
                        if waited.get(k, 0) >= v:
                            continue
                        if k not in need or need[k][1] < v:
                            need[k] = (s, v)
                    for k, (s, v) in need.items():
                        eng.wait_ge(s, v)
                        waited[k] = v
                    ins = o.fn(eng)
                    if o.tok is not None:
                        ins.then_inc(o.tok[0], 16 if o.is_dma else 1)
                if e == "sp":
                    for o in final_wait_ops:
                        s, v = o.tok
                        eng.wait_ge(s, v)

            with nc.Block() as block:
                @block.sync
                def _(eng):
                    run("sp", eng)

                @block.tensor
                def _(eng):
                    run("pe", eng)

                @block.scalar
                def _(eng):
                    run("act", eng)

                @block.vector
                def _(eng):
                    run("dve", eng)

                @block.gpsimd
                def _(eng):
                    run("pool", eng)


def _t5_bucket(rel):
    nb = 16
    me = 8
    ret = np.where(rel > 0, nb, 0)
    n = np.abs(rel)
    large = me + (np.log(np.maximum(n, 1).astype(np.float32) / np.float32(me))
                  / np.float32(math.log(128 / me)) * np.float32(nb - me)).astype(np.int32)
    large = np.minimum(large, nb - 1)
    return ret + np.where(n < me, n, large)


MLEN = 1280


def _constants():
    c = {}
    c["c_ident"] = np.eye(128, dtype=np.float32)
    c["c_J"] = np.eye(128, dtype=np.float32)[::-1].copy()
    s = np.arange(128)[:, None]
    t = np.arange(128)[None, :]
    uf = np.zeros((128, 129), np.float32)
    uf[:, :128] = (s <= t)
    uf[:, 128] = 1.0
    ub = np.zeros((128, 129), np.float32)
    ub[:, :128] = (s >= t)
    ub[:, 128] = 1.0
    c["c_uf"] = uf
    c["c_ub"] = ub
    c["c_sf"] = (s > t).astype(np.float32)
    c["c_sb"] = (s < t).astype(np.float32)
    n = np.arange(MLEN)
    bk = _t5_bucket(639 - n)
    oh = np.zeros((32, MLEN), np.float32)
    oh[bk, n] = 1.0
    c["c_onehot"] = oh
    return c


class _Stop(Exception):
    pass


def build(nl=2, dbg=(), stop=None):
    nc = bass.Bass("TRN2", target_bir_lowering=False)

    def din(name, shape):
        return nc.dram_tensor(name, list(shape), F32, kind="ExternalInput").ap()

    x_d = din("x", [S, D])
    lnemb_g = din("ln_emb_g", [D])
    lnemb_b = din("ln_emb_b", [D])
    table_d = din("rel_bias_table", [32, 4])
    w_in_d = din("w_in", [2, D, DIN])
    lq1_d = din("lambda_q1", [2, 64])
    lk1_d = din("lambda_k1", [2, 64])
    lq2_d = din("lambda_q2", [2, 64])
    lk2_d = din("lambda_k2", [2, 64])
    dnw_d = din("diff_norm_w", [2, 128])
    gup_d = din("gla_gate_up", [2, 2, 16, 256])
    gbias_d = din("gla_gate_bias", [2, 2, 256])
    gnw_d = din("gla_norm_w", [2, 128])
    w_o_d = din("w_o", [2, D, D])
    ln1g_d = din("ln1_g", [2, D])
    ln1b_d = din("ln1_b", [2, D])
    w1_d = din("w_ffn1", [2, D, DFF])
    b1_d = din("b_ffn1", [2, DFF])
    w2_d = din("w_ffn2", [2, DFF, D])
    b2_d = din("b_ffn2", [2, D])
    ln2g_d = din("ln2_g", [2, D])
    ln2b_d = din("ln2_b", [2, D])
    c_ident = din("c_ident", [128, 128])
    c_J = din("c_J", [128, 128])
    c_uf = din("c_uf", [128, 129])
    c_ub = din("c_ub", [128, 129])
    c_sf = din("c_sf", [128, 128])
    c_sb = din("c_sb", [128, 128])
    c_onehot = din("c_onehot", [32, MLEN])
    out_d = nc.dram_tensor("out", [S, D], F32, kind="ExternalOutput").ap()
    xs_d = nc.dram_tensor("xs_scratch", [S, D], F32).ap()
    md_t = nc.dram_tensor("md_scratch", [4, MLEN], F32)
    eb_d = nc.dram_tensor("expb_scratch", [128, 4 * 1152], BF16).ap()
    md_d = md_t.ap()
    dbg_out = {}

    sc = Sched(nc)
    es = ExitStack()
    ARENA_BYTES = 207 * 1024
    arena = es.enter_context(nc.sbuf_tensor("arena", [128, ARENA_BYTES // 2], BF16))
    PSb = es.enter_context(nc.psum_tensor("ps", [128, 8, 1024], BF16))[:]
    PS = PSb.bitcast(F32)

    def view(off, nbytes, dt, pattern=None, **kw):
        assert off % 32 == 0, off
        a = arena[:, off // 2:(off + nbytes) // 2]
        if dt is F32:
            a = a.bitcast(F32)
        if pattern:
            a = a.rearrange(pattern, **kw)
        return a

    class Alloc:
        def __init__(self, base, size):
            self.base = base
            self.size = size
            self.pos = 0

        def reset(self):
            self.pos = 0

        def get(self, nbytes, dt, pattern=None, **kw):
            n = (nbytes + 31) // 32 * 32
            assert self.pos + n <= self.size, (self.pos, n, self.size)
            v = view(self.base + self.pos, nbytes, dt, pattern, **kw)
            self.pos += n
            return v

    R_XT = Alloc(0, 32768)
    R_X = Alloc(32768, 65536)
    R_D = Alloc(98304, 70656)
    R_W = Alloc(168960, 16384)
    R_C = Alloc(185344, ARENA_BYTES - 185344)

    XT = R_XT.get(32768, BF16, "p (c n) -> p c n", c=8)
    X = R_X.get(65536, F32, "p (t n) -> p t n", t=NT)
    WB = [R_W.get(8192, BF16, "p (c n) -> p c n", c=8) for _ in range(2)]
    R_W.reset()
    WO = R_W.get(16384, BF16, "p (c n) -> p c n", c=8)

    identb = R_C.get(256, BF16)
    Jb = R_C.get(256, BF16)
    Uf = R_C.get(516, F32)
    Ub = R_C.get(516, F32)
    Usf = R_C.get(512, F32)
    Usb = R_C.get(512, F32)
    gt = R_C.get(4096, F32)
    bt = R_C.get(4096, F32)
    b2t = R_C.get(4096, F32)
    wd_t = R_C.get(512, F32)
    wg_t = R_C.get(512, F32)
    b1c = R_C.get(128, F32)
    cb = R_C.get(32, F32, "p (s h) -> p s h", s=2)
    lamv = R_C.get(4 * 64 * 4, F32, "p (a n) -> p a n", a=4)
    lamp = R_C.get(2 * 64 * 4, F32, "p (a n) -> p a n", a=2)
    lams = R_C.get(32, F32)
    Wg = R_C.get(1024, BF16)
    st_ = [R_C.get(48, F32) for _ in range(2)]
    mv_ = [R_C.get(8, F32) for _ in range(2)]
    rs_ = [R_C.get(4, F32) for _ in range(2)]
    xb_ = [R_C.get(2048, BF16) for _ in range(2)]
    dsm = [R_C.get(64, F32) for _ in range(2)]
    gsm = [R_C.get(64, F32) for _ in range(2)]
    decs = R_C.get(2 * 2 * 16 * 4, F32, "p (d q t) -> p d q t", d=2, q=2)

    def psk(b):
        return [("ps", b, q) for q in range(4)]

    def bc_mid(ap2, n):
        a = ap2.ap
        return bass.AP(ap2.tensor, ap2.offset, [list(a[0]), [0, n], list(a[1])])

    def bc_last(ap2, n):
        a = ap2.ap
        return bass.AP(ap2.tensor, ap2.offset, [list(a[0]), list(a[1]), [0, n]])

    cur_layer = [-1]

    def dump(name, ap, shape, reads):
        nm = "%s@%d" % (name, cur_layer[0])
        if nm in dbg:
            name = nm
        elif name not in dbg or (cur_layer[0] >= 0 and cur_layer[0] != nl - 1):
            return
        t = nc.dram_tensor("dbg_" + name.replace("@", "_"), list(shape), ap.dtype, kind="ExternalOutput").ap()
        dbg_out[name] = sc.dma("sp", lambda e: e.dma_start(out=t, in_=ap), "dbg", reads=reads)

    try:
        sc.dma("pool", lambda e: e.dma_start(out=identb, in_=c_ident), writes=["identb"])
        sc.dma("pool", lambda e: e.dma_start(out=Jb, in_=c_J), writes=["Jb"])
        sc.dma("sp", lambda e: e.dma_start(out=Uf, in_=c_uf), writes=["Uf"])
        sc.dma("sp", lambda e: e.dma_start(out=Ub, in_=c_ub), writes=["Ub"])
        sc.dma("sp", lambda e: e.dma_start(out=Usf, in_=c_sf), writes=["Usf"])
        sc.dma("sp", lambda e: e.dma_start(out=Usb, in_=c_sb), writes=["Usb"])
        for si, row in enumerate((15, 31)):
            sc.dma("sp", lambda e, si=si, row=row: e.dma_start(out=cb[:, si, :], in_=table_d[row, :].partition_broadcast(128)),
                   writes=[("cb", si)])

        R_D.reset()
        tb = R_D.get(16, F32)
        oh = R_D.get(MLEN * 4, F32)
        msb = R_D.get(MLEN * 4, F32)
        sc.dma("sp", lambda e: e.dma_start(out=tb[0:32, :], in_=table_d), writes=["tb"])
        sc.dma("sp", lambda e: e.dma_start(out=oh[0:32, :], in_=c_onehot), writes=["oh"])
        sc.dma("sp", lambda e: e.dma_start(out=gt, in_=lnemb_g.partition_broadcast(128)), writes=["gt"])
        sc.dma("sp", lambda e: e.dma_start(out=bt, in_=lnemb_b.partition_broadcast(128)), writes=["bt"])
        for t in range(NT):
            sc.dma("sp", lambda e, t=t: e.dma_start(out=X[:, t, :], in_=x_d[t * 128:(t + 1) * 128, :]), writes=[("X", t)])
        for ci, (c0, cn) in enumerate(((0, 512), (512, 512), (1024, 256))):
            sc.op("pe", lambda e, ci=ci, c0=c0, cn=cn: e.matmul(PS[0:4, ci, 0:cn], lhsT=tb[0:32, :], rhs=oh[0:32, c0:c0 + cn], start=True, stop=True),
                  reads=["tb", "oh"], writes=psk(ci))
            sc.op("dve", lambda e, ci=ci, c0=c0, cn=cn: e.tensor_copy(out=msb[0:4, c0:c0 + cn], in_=PS[0:4, ci, 0:cn]),
                  reads=psk(ci), writes=["msb"])
        sc.dma("sp", lambda e: e.dma_start(out=md_d, in_=msb[0:4, :]), reads=["msb"], writes=["md"])
        R_D.reset()
        R_D.get(16384, BF16); R_D.get(16384, BF16); R_D.get(16 * 4 * 129 * 2, BF16)
        expB0 = R_D.get(4 * 1152 * 2, BF16, "p (h n) -> p h n", h=4)
        R_D.get(4096, BF16); R_D.get(1024, BF16)
        tmp_revs = [view(R_D.base + 16384 + i * 2304, 2304, BF16) for i in range(4)]
        for h in range(4):
            src = bass.AP(md_t, h * MLEN, [[1, 128], [1, 1152]])
            sc.dma("pool", lambda e, src=src, h=h: e.dma_start(out=tmp_revs[h], in_=src), reads=["md"], writes=[("tmp_rev", h)])
        for h in range(4):
            for ci, (c0, cn) in enumerate(((0, 512), (512, 512), (1024, 128))):
                sc.op("pe", lambda e, ci=ci, c0=c0, cn=cn, h=h: e.matmul(PS[:, ci, 0:cn], lhsT=Jb, rhs=tmp_revs[h][:, c0:c0 + cn], start=True, stop=True),
                      reads=["Jb", ("tmp_rev", h)], writes=psk(ci))
                sc.op("act", lambda e, h=h, ci=ci, c0=c0, cn=cn: e.activation(out=expB0[:, h, c0:c0 + cn], in_=PS[:, ci, 0:cn], func=AF.Exp),
                      reads=psk(ci), writes=[("expB", h)])
        sc.dma("sp", lambda e: e.dma_start(out=eb_d, in_=expB0.rearrange("p h n -> p (h n)")), reads=[("expB", h) for h in range(4)], writes=["eb_d"])

        def ln_a(t):
            Xt = X[:, t, :]
            kx = ("X", t)
            b = t % 2
            st, mv, rs = st_[b], mv_[b], rs_[b]
            sc.op("dve", lambda e: e.bn_stats(out=st[:, 0:6], in_=Xt[:, 0:512]), reads=[kx], writes=[("st", b, 0)])
            sc.op("dve", lambda e: e.bn_stats(out=st[:, 6:12], in_=Xt[:, 512:1024]), reads=[kx], writes=[("st", b, 1)])
            sc.op("dve", lambda e: e.bn_aggr(out=mv, in_=st), reads=[("st", b, 0), ("st", b, 1)], writes=[("mv", b)])
            sc.op("act", lambda e: e.activation(out=rs, in_=mv[:, 1:2], func=AF.Sqrt, bias=1e-5, scale=1.0), reads=[("mv", b)], writes=[("rs", b)])
            sc.op("dve", lambda e: e.reciprocal(out=rs, in_=rs), reads=[("rs", b)], writes=[("rs", b)])
            sc.op("dve", lambda e: e.tensor_scalar(out=Xt, in0=Xt, scalar1=mv[:, 0:1], scalar2=rs, op0=ALU.subtract, op1=ALU.mult),
                  reads=[kx, ("mv", b), ("rs", b)], writes=[kx])
            sc.op("dve", lambda e: e.tensor_tensor(out=Xt, in0=Xt, in1=gt, op=ALU.mult), reads=[kx, "gt"], writes=[kx])
            sc.op("pool", lambda e: e.tensor_tensor(out=Xt, in0=Xt, in1=bt, op=ALU.add), reads=[kx, "bt"], writes=[kx])

        def ln_b1(t, spill_to):
            Xt = X[:, t, :]
            kx = ("X", t)
            b = t % 2
            xb = xb_[b]
            if spill_to is not None:
                sc.dma("sp", lambda e: e.dma_start(out=spill_to[t * 128:(t + 1) * 128, :], in_=Xt), ("xs", t % 4), reads=[kx], writes=[("xsd", t)])
            if spill_to is out_d:
                return
            sc.op("act", lambda e: e.activation(out=xb, in_=Xt, func=AF.Copy), reads=[kx], writes=[("xb", b)])

        def ln_b2(t, spill_to):
            if spill_to is out_d:
                return
            b = t % 2
            xb = xb_[b]
            bank = 6 + b
            for c in range(8):
                sc.op("pe", lambda e, c=c: e.transpose(out=PSb[:, bank, c * 128:(c + 1) * 128], in_=xb[:, c * 128:(c + 1) * 128], identity=identb),
                      reads=[("xb", b), "identb"], writes=psk(bank))
            sc.op("act", lambda e: e.activation(out=XT[:, :, t * 128:(t + 1) * 128], in_=PSb[:, bank, :].rearrange("p (c n) -> p c n", c=8), func=AF.Copy),
                  reads=psk(bank), writes=[("XT", t)])

        def ln_all(spill_to):
            ln_a(0)
            for t in range(NT):
                if t + 1 < NT:
                    ln_a(t + 1)
                ln_b1(t, spill_to)
                ln_b2(t, spill_to)

        def load_ln_params(g_ap, b_ap):
            sc.dma("sp", lambda e: e.dma_start(out=gt, in_=g_ap.partition_broadcast(128)), writes=["gt"])
            sc.dma("sp", lambda e: e.dma_start(out=bt, in_=b_ap.partition_broadcast(128)), writes=["bt"])

        def load_win_block(l, blk, buf):
            c0 = blk * 512
            ncol = min(512, DIN - c0)
            src = w_in_d[l, :, c0:c0 + ncol].rearrange("(c p) n -> p c n", p=128)
            for hf in range(2):
                sc.dma("pool", lambda e, hf=hf: e.dma_start(out=WB[buf][:, hf * 4:(hf + 1) * 4, 0:ncol], in_=src[:, hf * 4:(hf + 1) * 4, :]),
                       writes=[("RW", buf, hf)])

        if stop == 'init':
            raise _Stop()
        load_win_block(0, 0, 0)
        load_win_block(0, 1, 1)
        ln_all(xs_d)
        dump("h0", X, [128, NT, 1024], [("X", t) for t in range(NT)])

        if stop == 'emb':
            raise _Stop()
        evac_rr = [0]

        def evac(out, in_, reads, writes, scale=None):
            evac_rr[0] ^= 1
            if evac_rr[0]:
                if scale is None:
                    sc.op("act", lambda e: e.activation(out=out, in_=in_, func=AF.Copy), reads=reads, writes=writes)
                else:
                    sc.op("act", lambda e: e.mul(out=out, in_=in_, mul=scale), reads=reads, writes=writes)
            else:
                if scale is None:
                    sc.op("dve", lambda e: e.tensor_copy(out=out, in_=in_), reads=reads, writes=writes)
                else:
                    sc.op("dve", lambda e: e.tensor_scalar(out=out, in0=in_, scalar1=scale, scalar2=None, op0=ALU.mult), reads=reads, writes=writes)

        def do_layer(l):
            lam_init = 0.8 - 0.6 * math.exp(-0.3 * l)
            cur_layer[0] = l
            if l == 0:
                sc.barrier()
            last = (l == nl - 1)
            R_D.reset()
            QT = R_D.get(16384, BF16, "p (h n) -> p h n", h=4)
            KT = R_D.get(16384, BF16, "p (h n) -> p h n", h=4)
            V = R_D.get(16 * 4 * 129 * 2, BF16, "p (t h e) -> p t h e", t=16, h=4)
            expB = R_D.get(4 * 1152 * 2, BF16, "p (h n) -> p h n", h=4)
            Eb = R_D.get(4096, BF16, "p (b m n) -> p b m n", b=2, m=2)
            d_y = R_D.get(4 * 128 * 2, BF16, "p (u n) -> p u n", u=4)
            _pu = R_D.pos
            silu_t = [R_D.get(2048, F32) for _ in range(2)]
            R_D.pos = _pu
            accS = R_D.get(8 * 129 * 4, F32, "p (a n) -> p a n", a=8)
            R_D.pos = _pu + 4608
            tmp_rev = R_D.get(1152 * 2, BF16)
            R_X.reset()
            gqT = R_X.get(8192, BF16, "p (c n) -> p c n", c=2)
            gkT = R_X.get(8192, BF16, "p (c n) -> p c n", c=2)
            gk_tok = R_X.get(8192, BF16, "p (t n) -> p t n", t=16)
            gv = R_X.get(16384, BF16, "p (t n) -> p t n", t=16)
            gr_s = R_X.get(16384, BF16, "p (t n) -> p t n", t=16)
            G33 = R_X.get(4096, BF16)

            for i, ap in enumerate((lq1_d, lk1_d, lq2_d, lk2_d)):
                sc.dma("sp", lambda e, i=i, ap=ap: e.dma_start(out=lamv[:, i, :], in_=ap[l, :].partition_broadcast(128)), writes=[("lamv", i)])
            sc.op("dve", lambda e: e.tensor_tensor(out=lamp[:, 0, :], in0=lamv[:, 0, :], in1=lamv[:, 1, :], op=ALU.mult), reads=[("lamv", 0), ("lamv", 1)], writes=["lamp"])
            sc.op("dve", lambda e: e.tensor_tensor(out=lamp[:, 1, :], in0=lamv[:, 2, :], in1=lamv[:, 3, :], op=ALU.mult), reads=[("lamv", 2), ("lamv", 3)], writes=["lamp"])
            sc.op("dve", lambda e: e.reduce_sum(out=lams[:, 0:2], in_=lamp, axis=mybir.AxisListType.X), reads=["lamp"], writes=["lams"])
            sc.op("act", lambda e: e.activation(out=lams[:, 0:2], in_=lams[:, 0:2], func=AF.Exp), reads=["lams"], writes=["lams"])
            sc.op("dve", lambda e: e.tensor_tensor(out=lams[:, 2:3], in0=lams[:, 0:1], in1=lams[:, 1:2], op=ALU.subtract), reads=["lams"], writes=["lams"])
            sc.op("dve", lambda e: e.tensor_scalar(out=lams[:, 3:4], in0=lams[:, 2:3], scalar1=lam_init, scalar2=-1.0, op0=ALU.add, op1=ALU.mult),
                  reads=["lams"], writes=["neglam"])
            neg_lam = lams[:, 3:4]
            sc.dma("sp", lambda e: e.dma_start(out=wd_t, in_=dnw_d[l, :].partition_broadcast(128)), writes=["wd"])
            sc.op("dve", lambda e: e.tensor_scalar(out=wd_t, in0=wd_t, scalar1=1.0 - lam_init, scalar2=None, op0=ALU.mult), reads=["wd"], writes=["wd"])
            sc.dma("sp", lambda e: e.dma_start(out=wg_t, in_=gnw_d[l, :].partition_broadcast(128)), writes=["wg"])
            sc.op("dve", lambda e: e.memset(Wg[0:33, :], 0.0), writes=["Wg"])
            sc.dma("pool", lambda e: e.dma_start(out=Wg[0:16, 0:256], in_=gup_d[l, 0]), writes=["Wg"])
            sc.dma("pool", lambda e: e.dma_start(out=Wg[16:32, 256:512], in_=gup_d[l, 1]), writes=["Wg"])
            sc.dma("pool", lambda e: e.dma_start(out=Wg[32:33, :], in_=gbias_d[l].rearrange("a n -> (a n)").partition_broadcast(1)), writes=["Wg"])
            sc.dma("sp", lambda e: e.dma_start(out=b1c, in_=b1_d[l].rearrange("(c p) -> p c", p=128), allow_slow_non_contiguous=True), writes=["b1c"])
            sc.dma("sp", lambda e: e.dma_start(out=b2t, in_=b2_d[l].partition_broadcast(128)), writes=["b2t"])
            sc.dma("sp", lambda e: e.dma_start(out=expB.rearrange("p h n -> p (h n)"), in_=eb_d), reads=["eb_d"], writes=[("expB", h) for h in range(4)])
            sc.op("dve", lambda e: e.memset(V[:, :, :, 128:129], 1.0), writes=[("V", t) for t in range(NT)])
            sc.op("dve", lambda e: e.memset(G33[32:33, :], 1.0), writes=["G33"])

            dump("XTin", XT, [128, 8, 2048], [("XT", t) for t in range(NT)])
            dump("Xin", X, [128, NT, 1024], [("X", t) for t in range(NT)])
            if stop == 'L' and l == nl - 1:
                raise _Stop()
            ps_rr = [0]

            def nextbank():
                b = ps_rr[0] % 6
                ps_rr[0] += 1
                return b

            xt_all = [("XT", t) for t in range(NT)]
            for blk in range(7):
                buf = blk % 2
                wb = WB[buf]
                kw = [("RW", buf, 0), ("RW", buf, 1)]
                if blk in (0, 1, 3, 6):
                    nch = 1 if blk == 6 else 4
                    for cc in range(nch):
                        for r in range(4):
                            bank = nextbank()
                            M = 32 if blk == 6 else 128
                            for c in range(8):
                                sc.op("pe", lambda e, c=c, cc=cc, r=r, bank=bank, M=M, wb=wb: e.matmul(
                                    PS[0:M, bank, :], lhsT=wb[:, c, cc * 128:cc * 128 + M], rhs=XT[:, c, r * 512:(r + 1) * 512],
                                    start=(c == 0), stop=(c == 7)), reads=kw + xt_all[r * 4:(r + 1) * 4], writes=psk(bank))
                            sl = slice(r * 512, (r + 1) * 512)
                            if blk == 0:
                                evac(QT[:, cc, sl], PS[:, bank, :], psk(bank), [("QT", cc, r)], scale=0.125)
                            elif blk == 1:
                                evac(KT[:, cc, sl], PS[:, bank, :], psk(bank), [("KT", cc, r)])
                            elif blk == 3:
                                if cc < 2:
                                    evac(gqT[:, cc, sl], PS[:, bank, :], psk(bank), [("gqT", 4 * r + i) for i in range(4)], scale=0.125)
                                else:
                                    evac(gkT[:, cc - 2, sl], PS[:, bank, :], psk(bank), [("gkT", 4 * r + i) for i in range(4)])
                            else:
                                evac(G33[0:32, sl], PS[0:32, bank, :], psk(bank), ["G33"])
                if blk == 3:
                    for t in range(NT):
                        bank = nextbank()
                        for cc in range(2):
                            sc.op("pe", lambda e, t=t, cc=cc, bank=bank: e.transpose(out=PSb[:, bank, cc * 128:(cc + 1) * 128], in_=gkT[:, cc, t * 128:(t + 1) * 128], identity=identb),
                                  reads=[("gkT", t), "identb"], writes=psk(bank))
                        evac(gk_tok[:, t, :], PSb[:, bank, 0:256], psk(bank), [("gk_tok", t)])
                if blk in (2, 4, 5):
                    for t in range(NT):
                        bank = nextbank()
                        c0, ncol = (0, 512)
                        for c in range(8):
                            sc.op("pe", lambda e, c=c, t=t, bank=bank, c0=c0, ncol=ncol, wb=wb: e.matmul(
                                PS[:, bank, 0:ncol], lhsT=XT[:, c, t * 128:(t + 1) * 128], rhs=wb[:, c, c0:c0 + ncol],
                                start=(c == 0), stop=(c == 7)), reads=kw + [("XT", t)], writes=psk(bank))
                        if blk == 2:
                            evac(V[:, t, :, 0:128], PS[:, bank, :].rearrange("p (h e) -> p h e", h=4), psk(bank), [("V", t)])
                        elif blk == 4:
                            evac(gv[:, t, :], PS[:, bank, :], psk(bank), [("gv", t)])
                        else:
                            sb = t % 2
                            sc.op("act", lambda e, bank=bank, sb=sb: e.activation(out=silu_t[sb], in_=PS[:, bank, :], func=AF.Silu),
                                  reads=psk(bank), writes=[("silu", sb)])
                            sc.op("dve", lambda e, t=t, sb=sb: e.tensor_tensor(
                                out=gr_s[:, t, :].rearrange("p (h e) -> p h e", h=4), in0=silu_t[sb].rearrange("p (h e) -> p h e", h=4),
                                in1=bc_mid(wg_t, 4), op=ALU.mult), reads=[("silu", sb), "wg"], writes=[("gr_s", t)])
                if blk + 2 < 7:
                    load_win_block(l, blk + 2, buf)
            for hf in range(2):
                sc.dma("pool", lambda e, hf=hf: e.dma_start(out=WO[:, hf * 4:(hf + 1) * 4, :],
                                                             in_=w_o_d[l].rearrange("(c p) n -> p c n", p=128)[:, hf * 4:(hf + 1) * 4, :]),
                       writes=[("RW", hf, 0), ("RW", hf, 1)])
            dump("QT", QT, [128, 4, 2048], [("QT", a, b) for a in range(4) for b in range(4)])
            dump("KT", KT, [128, 4, 2048], [("KT", a, b) for a in range(4) for b in range(4)])
            dump("V", V, [128, 16, 4, 129], [("V", t) for t in range(NT)])
            dump("expB", expB, [128, 4, 1152], [("expB", h) for h in range(4)])
            dump("gqT", gqT, [128, 2, 2048], [("gqT", t) for t in range(NT)])
            dump("gr_s", gr_s, [128, 16, 512], [("gr_s", t) for t in range(NT)])
            dump("G33", G33[0:33, :], [33, 2048], ["G33"])

            if stop == 'P' and l == nl - 1:
                raise _Stop()
            steps = [(h, r, j) for h in range(4) for r in range(4) for j in range(16)]

            def acc_ap(m, u):
                idx = m * 4 + u
                return PS[:, 4 + idx // 3, (idx % 3) * 160:(idx % 3) * 160 + 129]

            def acc_keys(m, u):
                return psk(4 + (m * 4 + u) // 3)

            Eb3 = view(R_X.base + 61440, 2048, BF16, "p (m n) -> p m n", m=2)
            EbL = [Eb[:, 0, :, :], Eb[:, 1, :, :], Eb3]

            def d_scores(i):
                h, r, j = steps[i]
                d = j - 4 * r
                mixed = (-1 <= d <= 4)
                sb = i % 2
                eb = i % 3
                E = EbL[eb]
                for m in range(2):
                    bank = sb * 2 + m
                    sc.op("pe", lambda e, h=h, r=r, j=j, m=m, bank=bank: e.matmul(
                        PS[:, bank, :], lhsT=KT[64 * m:64 * m + 64, h, j * 128:(j + 1) * 128],
                        rhs=QT[64 * m:64 * m + 64, h, r * 512:(r + 1) * 512], start=True, stop=True),
                        reads=[("KT", h, j // 4), ("QT", h, r)], writes=psk(bank))
                pk2 = psk(sb * 2) + psk(sb * 2 + 1)
                ek = [("E", eb, 0), ("E", eb, 1)]
                if mixed:
                    c0 = (4 - d) * 128
                    sc.op("act", lambda e, sb=sb, E=E: e.activation(out=E, in_=PS[:, sb * 2:sb * 2 + 2, :], func=AF.Exp),
                          reads=pk2, writes=ek)
                    for m in range(2):
                        sc.op("dve", lambda e, E=E, m=m, h=h, c0=c0: e.tensor_tensor(out=E[:, m, :], in0=E[:, m, :], in1=expB[:, h, c0:c0 + 512], op=ALU.mult),
                              reads=[("E", eb, m), ("expB", h)], writes=[("E", eb, m)])
                else:
                    side = 0 if d < 0 else 1
                    sc.op("act", lambda e, sb=sb, E=E, side=side, h=h: e.activation(
                        out=E, in_=PS[:, sb * 2:sb * 2 + 2, :], func=AF.Exp, bias=cb[:, side, h:h + 1]),
                        reads=pk2 + [("cb", 0), ("cb", 1)], writes=ek)

            def d_av(i):
                h, r, j = steps[i]
                eb = i % 3
                E = EbL[eb]
                for m in range(2):
                    for u in range(4):
                        sc.op("pe", lambda e, h=h, j=j, m=m, u=u, E=E: e.matmul(
                            acc_ap(m, u), lhsT=E[:, m, u * 128:(u + 1) * 128], rhs=V[:, j, h, 0:129],
                            start=(j == 0 and (m * 4 + u) % 3 == 0), stop=(j == 15), skip_group_check=True),
                            reads=[("E", eb, m), ("V", j)], writes=acc_keys(m, u))

            sm = dsm[0]
            ka = ["accS", ("silu", 0), ("silu", 1)]

            def d_final(h, r):
                sc.op("dve", lambda e: e.tensor_copy(out=accS[:, 0:3, :], in_=PS[:, 4, 0:480].rearrange("p (a n) -> p a n", a=3)[:, :, 0:129]),
                      reads=psk(4), writes=ka)
                sc.op("dve", lambda e: e.tensor_copy(out=accS[:, 3:6, :], in_=PS[:, 5, 0:480].rearrange("p (a n) -> p a n", a=3)[:, :, 0:129]),
                      reads=psk(5), writes=ka)
                sc.op("dve", lambda e: e.tensor_copy(out=accS[:, 6:8, :], in_=PS[:, 6, 0:320].rearrange("p (a n) -> p a n", a=2)[:, :, 0:129]),
                      reads=psk(6), writes=ka)

            def d_final2(h, r):
                sc.op("dve", lambda e: e.reciprocal(out=sm[:, 0:8], in_=accS[:, :, 128]), reads=ka[:1], writes=["dsm"])
                sc.op("dve", lambda e: e.tensor_scalar(out=sm[:, 4:8], in0=sm[:, 4:8], scalar1=neg_lam, scalar2=None, op0=ALU.mult),
                      reads=["dsm", "neglam"], writes=["dsm"])
                sc.op("dve", lambda e: e.memset(sm[:, 8:12], 0.0), writes=["dss"])

            def d_final_u(u):
                if True:
                    sc.op("dve", lambda e, u=u: e.tensor_scalar(out=accS[:, u, 0:128], in0=accS[:, u, 0:128], scalar1=sm[:, u:u + 1], scalar2=None, op0=ALU.mult),
                          reads=["dsm"] + ka[:1], writes=ka[:1])
                    sc.op("dve", lambda e, u=u: e.scalar_tensor_tensor(out=accS[:, u, 0:128], in0=accS[:, 4 + u, 0:128], scalar=sm[:, 4 + u:5 + u],
                                                                       in1=accS[:, u, 0:128], op0=ALU.mult, op1=ALU.add),
                          reads=["dsm"] + ka[:1], writes=ka[:1])
                    sc.op("dve", lambda e, u=u: e.scalar_tensor_tensor(out=accS[:, 4 + u, 0:128], in0=accS[:, u, 0:128], scalar=1.0, in1=accS[:, u, 0:128],
                                                                       op0=ALU.mult, op1=ALU.mult, accum_out=sm[:, 8 + u:9 + u]),
                          reads=ka[:1], writes=ka[:1] + ["dss"])

            def d_final_b(h, r):
                sc.op("act", lambda e: e.activation(out=sm[:, 12:16], in_=sm[:, 8:12], func=AF.Ln, bias=1e-5, scale=1.0 / 128), reads=["dss"], writes=["drs"])
                sc.op("act", lambda e: e.activation(out=sm[:, 12:16], in_=sm[:, 12:16], func=AF.Exp, scale=-0.5), reads=["drs"], writes=["drs"])
                for u in range(4):
                    sc.op("dve", lambda e, u=u: e.scalar_tensor_tensor(out=d_y[:, u, :], in0=accS[:, u, 0:128], scalar=sm[:, 12 + u:13 + u], in1=wd_t,
                                                                       op0=ALU.mult, op1=ALU.mult),
                          reads=ka[:1] + ["drs", "wd"], writes=[("dy", u)])

            def d_final_pe(h, r):
                for u in range(4):
                    sc.op("pe", lambda e, u=u: e.transpose(out=PSb[:, 7, u * 128:(u + 1) * 128], in_=d_y[:, u, :], identity=identb),
                          reads=[("dy", u), "identb"], writes=psk(7))
                sc.op("dve", lambda e, h=h, r=r: e.tensor_copy(out=XT[:, h, r * 512:(r + 1) * 512], in_=PSb[:, 7, 0:512]),
                      reads=psk(7), writes=[("XT", 4 * r + i) for i in range(4)])

            pend = []
            pend_b = []
            pend_u = []
            d_scores(0)
            d_scores(1)
            for i in range(len(steps)):
                h, r, j = steps[i]
                if j == 15:
                    d_av(i)
                    d_final(h, r)
                    if i + 2 < len(steps):
                        d_scores(i + 2)
                    d_final2(h, r)
                else:
                    if i + 2 < len(steps):
                        d_scores(i + 2)
                    d_av(i)
                if j == 15:
                    pend.append((h, r))
                    pend_b.append((h, r))
                    pend_u.extend([0, 1, 2, 3])
                    d_final_u(pend_u.pop(0))
                elif pend_u:
                    d_final_u(pend_u.pop(0))
                elif j == 4 and pend_b:
                    d_final_b(*pend_b.pop(0))
                elif j == 7 and pend:
                    d_final_pe(*pend.pop(0))
            while pend_u:
                d_final_u(pend_u.pop(0))
            while pend_b:
                d_final_b(*pend_b.pop(0))
            while pend:
                d_final_pe(*pend.pop(0))
            dump("mixT_d", XT, [128, 8, 2048], [("XT", t) for t in range(NT)])

            if stop == 'D' and l == nl - 1:
                raise _Stop()
            sc.barrier()
            R_D.reset()
            qf = R_D.get(8192, BF16, "p (c n) -> p c n", c=2)
            kf = R_D.get(8192, BF16, "p (c n) -> p c n", c=2)
            kd_f = R_D.get(8192, BF16, "p (t n) -> p t n", t=16)
            Sbf = R_D.get(16384, BF16, "p (d q t e) -> p d q t e", d=2, q=2, t=16)
            stm2 = [R_D.get(4096, F32, "p (d q n) -> p d q n", d=2, q=2) for _ in range(2)]
            _p0 = R_D.pos
            sp_ = [R_D.get(2048, F32) for _ in range(2)]
            _p1 = R_D.pos
            ebt = [R_D.get(2 * 2 * 129 * 4, F32, "p (d q n) -> p d q n", d=2, q=2) for _ in range(2)]
            _p2 = R_D.pos
            enbt = [R_D.get(2 * 2 * 128 * 4, F32, "p (d q n) -> p d q n", d=2, q=2) for _ in range(2)]
            erem = [R_D.get(2048, F32) for _ in range(2)]
            dS = [R_D.get(1024, F32) for _ in range(4)]
            _pend = R_D.pos
            Am = [R_X.get(4 * 2 * 128 * 2, BF16, "p (h d n) -> p h d n", h=4, d=2) for _ in range(2)]
            R_D.pos = _p2
            g_y = [R_D.get(1024, BF16) for _ in range(2)]
            g_junk = R_D.get(512, F32)
            R_D.pos = _pend
            qb, kb, kd_b = gqT, gkT, gk_tok
            maskf = Uf[:, 0:128]
            maskb = Usf

            def tl(t):
                return slice(t * 128, (t + 1) * 128)

            def prep_A(t):
                b = t % 2
                sp = sp_[b]
                zb = 0 if b == 0 else 7
                sc.op("pe", lambda e: e.matmul(PS[:, zb, :], lhsT=G33[0:33, tl(t)], rhs=Wg[0:33, :], start=True, stop=True),
                      reads=["G33", "Wg"], writes=psk(zb))
                sc.op("act", lambda e: e.activation(out=sp, in_=PS[:, zb, :], func=AF.Exp, scale=-1.0), reads=psk(zb), writes=[("sp", b)])
                sc.op("act", lambda e: e.activation(out=sp, in_=sp, func=AF.Ln, bias=1.0, scale=1.0), reads=[("sp", b)], writes=[("sp", b)])

            prep_A(0)
            for t in range(NT):
                b = t % 2
                sp = sp_[b]
                if t + 1 < NT:
                    prep_A(t + 1)
                sc.op("pe", lambda e, sp=sp: e.matmul(PS[:, 1, 0:256], lhsT=Usf, rhs=sp[:, 0:256], start=True, stop=True), reads=[("sp", b), "Usf", "Usb"], writes=psk(1))
                sc.op("pe", lambda e, sp=sp: e.matmul(PS[:, 1, 256:512], lhsT=Usb, rhs=sp[:, 256:512], start=True, stop=True), reads=[("sp", b), "Usf", "Usb"], writes=psk(1))
                sc.op("act", lambda e, b=b: e.activation(out=erem[b], in_=PS[:, 1, :], func=AF.Exp, scale=-1.0 / 16), reads=psk(1), writes=[("erem", b)])
                sc.op("dve", lambda e, t=t, b=b: e.tensor_tensor(out=kd_f[:, t, :], in0=gk_tok[:, t, :], in1=erem[b][:, 0:256], op=ALU.mult),
                      reads=[("gk_tok", t), ("erem", b)], writes=[("kd_f", t)])
                sc.op("dve", lambda e, t=t, b=b: e.tensor_tensor(out=kd_b[:, t, :], in0=gk_tok[:, t, :], in1=erem[b][:, 256:512], op=ALU.mult),
                      reads=[("gk_tok", t), ("erem", b), ("kd_f", t)], writes=[("gk_tok", t)])
                for d in range(2):
                    U = Uf if d == 0 else Ub
                    for q in range(2):
                        sc.op("pe", lambda e, sp=sp, d=d, q=q, U=U: e.matmul(PS[:, 2 + d, q * 160:q * 160 + 129],
                                                                            lhsT=sp[:, d * 256 + q * 128:d * 256 + (q + 1) * 128], rhs=U, start=True, stop=True),
                              reads=[("sp", b), "Uf", "Ub"], writes=psk(2 + d))
                src4 = PS[:, 2:4, 0:320].rearrange("p a (q n) -> p a q n", q=2)
                sc.op("act", lambda e, b=b, src4=src4: e.activation(out=ebt[b], in_=src4[:, :, :, 0:129], func=AF.Exp, scale=-1.0 / 16),
                      reads=psk(2) + psk(3), writes=[("eb", b, 0), ("eb", b, 1)])
                sc.op("act", lambda e, b=b, src4=src4: e.activation(out=enbt[b], in_=src4[:, :, :, 0:128], func=AF.Exp, scale=1.0 / 16),
                      reads=psk(2) + psk(3), writes=[("enb", b, 0), ("enb", b, 1)])
                sc.op("dve", lambda e, t=t, b=b: e.tensor_tensor(out=qf[:, :, tl(t)], in0=gqT[:, :, tl(t)], in1=ebt[b][:, 0, :, 0:128], op=ALU.mult),
                      reads=[("gqT", t), ("eb", b, 0)], writes=[("qf", t)])
                sc.op("dve", lambda e, t=t, b=b: e.tensor_tensor(out=kf[:, :, tl(t)], in0=gkT[:, :, tl(t)], in1=enbt[b][:, 0, :, :], op=ALU.mult),
                      reads=[("gkT", t), ("enb", b, 0)], writes=[("kf", t)])
                sc.op("dve", lambda e, t=t, b=b: e.tensor_tensor(out=qb[:, :, tl(t)], in0=gqT[:, :, tl(t)], in1=ebt[b][:, 1, :, 0:128], op=ALU.mult),
                      reads=[("gqT", t), ("eb", b, 1), ("qf", t)], writes=[("gqT", t)])
                sc.op("dve", lambda e, t=t, b=b: e.tensor_tensor(out=kb[:, :, tl(t)], in0=gkT[:, :, tl(t)], in1=enbt[b][:, 1, :, :], op=ALU.mult),
                      reads=[("gkT", t), ("enb", b, 1), ("kf", t)], writes=[("gkT", t)])
                sc.op("dve", lambda e, t=t, b=b: e.tensor_copy(out=decs[:, :, :, t:t + 1], in_=ebt[b][:, :, :, 128:129]),
                      reads=[("eb", b, 0), ("eb", b, 1)], writes=["decs"])
            dump("qf", qf, [128, 2, 2048], [("qf", t) for t in range(NT)])
            dump("kd_f", kd_f, [128, 16, 256], [("kd_f", t) for t in range(NT)])
            dump("decs", decs, [128, 2, 2, 16], ["decs"])

            if stop == 'G1' and l == nl - 1:
                raise _Stop()
            sc.op("dve", lambda e: e.memset(stm2[0], 0.0), writes=[("stm", 0, d, q) for d in range(2) for q in range(2)])
            chains = [(d, q) for d in range(2) for q in range(2)]
            par = {c: 0 for c in chains}
            for i in range(NT):
                todo = []
                for ci, (d, q) in enumerate(chains):
                    t = i if d == 0 else NT - 1 - i
                    cur = par[(d, q)]
                    if i > 0:
                        sc.op("act", lambda e, d=d, q=q, t=t, cur=cur: e.activation(out=Sbf[0:64, d, q, t, :], in_=stm2[cur][0:64, d, q, 0:128], func=AF.Copy),
                              reads=[("stm", cur, d, q)], writes=[("Sbf", d, q, t)])
                        sc.op("dve", lambda e, d=d, q=q, t=t, cur=cur: e.tensor_copy(out=Sbf[64:128, d, q, t, :], in_=stm2[cur][64:128, d, q, 128:256]),
                              reads=[("stm", cur, d, q)], writes=[("Sbf", d, q, t)])
                    if i == NT - 1:
                        continue
                    kd = kd_f if d == 0 else kd_b
                    kkey = "kd_f" if d == 0 else "gk_tok"
                    pslot = ci % 2
                    pk = psk(4 + pslot)
                    sc.op("pe", lambda e, kd=kd, t=t, q=q, pslot=pslot: e.matmul(PS[:, 4 + pslot, 0:256], lhsT=kd[:, t, q * 128:(q + 1) * 128],
                                                                                rhs=gv[:, t, q * 256:(q + 1) * 256], start=True, stop=True),
                          reads=[(kkey, t), ("gv", t)], writes=pk)
                    if os.environ.get("GSKIP") != "evac":
                        sc.op("act", lambda e, ci=ci, pslot=pslot: e.activation(out=dS[ci], in_=PS[:, 4 + pslot, 0:256], func=AF.Copy),
                              reads=pk, writes=[("dS", ci)])
                    todo.append((ci, d, q, t, cur))
                for (ci, d, q, t, cur) in todo:
                    if os.environ.get("GSKIP") == "upd":
                        par[(d, q)] = 1 - cur
                        continue
                    sc.op("dve", lambda e, ci=ci, d=d, q=q, t=t, cur=cur: e.scalar_tensor_tensor(
                        out=stm2[1 - cur][:, d, q, :], in0=stm2[cur][:, d, q, :], scalar=decs[:, d, q, t:t + 1], in1=dS[ci],
                        op0=ALU.mult, op1=ALU.add), reads=[("stm", cur, d, q), "decs", ("dS", ci)], writes=[("stm", 1 - cur, d, q)])
                    par[(d, q)] = 1 - cur
            dump("Sbf", Sbf, [128, 2, 2, 16, 128], [("Sbf", d, q, t) for d in range(2) for q in range(2) for t in range(NT)])
            if stop == 'G2' and l == nl - 1:
                raise _Stop()

            def g_A(t):
                b = t % 2
                A = Am[b]
                for half in range(2):
                    sbank = (4 + half) if os.environ.get('GBANK') else (2 * b + half)
                    items = []
                    for sq in range(4):
                        h, d = half + 2 * (sq // 2), sq % 2
                        q = h // 2
                        base = (h % 2) * 64
                        kk = kf if d == 0 else kb
                        qq = qf if d == 0 else qb
                        kkey = ("kf", t) if d == 0 else ("gkT", t)
                        qkey = ("qf", t) if d == 0 else ("gqT", t)
                        sc.op("pe", lambda e, kk=kk, qq=qq, q=q, base=base, sbank=sbank, sq=sq: e.matmul(
                            PS[:, sbank, sq * 128:(sq + 1) * 128], lhsT=kk[base:base + 64, q, tl(t)], rhs=qq[base:base + 64, q, tl(t)], start=True, stop=True),
                            reads=[kkey, qkey], writes=psk(sbank))
                        items.append((sq, h, d))
                        if os.environ.get("GOLD"):
                            mk = maskf if d == 0 else maskb
                            sc.op("dve", lambda e, A=A, h=h, d=d, sbank=sbank, sq=sq, mk=mk: e.tensor_tensor(
                                out=A[:, h, d, :], in0=PS[:, sbank, sq * 128:(sq + 1) * 128], in1=mk, op=ALU.mult),
                                reads=psk(sbank) + ["Uf", "Usf"], writes=[("A", b, h, d)])
                    if os.environ.get("GOLD"):
                        continue
                    for (sq, h, d) in items:
                        mk = maskf if d == 0 else maskb
                        sc.op("dve", lambda e, A=A, h=h, d=d, sbank=sbank, sq=sq, mk=mk: e.tensor_tensor(
                            out=A[:, h, d, :], in0=PS[:, sbank, sq * 128:(sq + 1) * 128], in1=mk, op=ALU.mult),
                            reads=psk(sbank) + ["Uf", "Usf"], writes=[("A", b, h, d)])

            def g_B(t):
                b = t % 2
                A = Am[b]
                obank = (0 + b) if os.environ.get('GBANK') else (4 + b)
                for h in range(4):
                    q = h // 2
                    base = (h % 2) * 64
                    oh_ = PS[:, obank, h * 128:(h + 1) * 128]
                    ok = psk(obank)
                    inter_f = t > 0
                    inter_b = t < NT - 1
                    sc.op("pe", lambda e, A=A, h=h, oh_=oh_: e.matmul(oh_, lhsT=A[:, h, 0, :], rhs=gv[:, t, h * 128:(h + 1) * 128], start=True, stop=False),
                          reads=[("A", b, h, 0), ("gv", t)], writes=ok)
                    sc.op("pe", lambda e, A=A, h=h, oh_=oh_, fin=(not inter_f and not inter_b): e.matmul(
                        oh_, lhsT=A[:, h, 1, :], rhs=gv[:, t, h * 128:(h + 1) * 128], start=False, stop=fin),
                        reads=[("A", b, h, 1), ("gv", t)], writes=ok)
                    if inter_f:
                        sc.op("pe", lambda e, q=q, base=base, oh_=oh_, fin=(not inter_b): e.matmul(
                            oh_, lhsT=qf[base:base + 64, q, tl(t)], rhs=Sbf[base:base + 64, 0, q, t, :], start=False, stop=fin),
                            reads=[("qf", t), ("Sbf", 0, q, t)], writes=ok)
                    if inter_b:
                        sc.op("pe", lambda e, q=q, base=base, oh_=oh_: e.matmul(
                            oh_, lhsT=qb[base:base + 64, q, tl(t)], rhs=Sbf[base:base + 64, 1, q, t, :], start=False, stop=True),
                            reads=[("gqT", t), ("Sbf", 1, q, t)], writes=ok)

            def g_norm(t):
                b = t % 2
                sm = gsm[b]
                obank = (0 + b) if os.environ.get('GBANK') else (4 + b)
                okall = psk(obank)
                for h in range(4):
                    sc.op("act", lambda e, h=h, sm=sm: e.activation(out=g_junk, in_=PS[:, obank, h * 128:(h + 1) * 128], func=AF.Square, accum_out=sm[:, h:h + 1]),
                          reads=okall, writes=[("gss", b, h)] + ([("enb", 1, 0), ("enb", 1, 1)] if h == 0 else []))
                sc.op("act", lambda e, sm=sm: e.activation(out=sm[:, 4:8], in_=sm[:, 0:4], func=AF.Sqrt, bias=1e-5, scale=1.0 / 128),
                      reads=[("gss", b, h) for h in range(4)], writes=[("grs", b)])
                sc.op("dve", lambda e, sm=sm: e.reciprocal(out=sm[:, 4:8], in_=sm[:, 4:8]), reads=[("grs", b)], writes=[("grs", b)])
                for h in range(4):
                    sc.op("dve", lambda e, h=h, sm=sm, b=b: e.scalar_tensor_tensor(
                        out=g_y[b][:, h * 128:(h + 1) * 128], in0=PS[:, obank, h * 128:(h + 1) * 128], scalar=sm[:, 4 + h:5 + h],
                        in1=gr_s[:, t, h * 128:(h + 1) * 128], op0=ALU.mult, op1=ALU.mult),
                        reads=psk(obank) + [("grs", b), ("gr_s", t)], writes=[("gy", b, h)] + ([("enb", 0, 0), ("enb", 0, 1)] if h == 0 else []))

            def g_tr(t):
                b = t % 2
                tk = psk(6 + b)
                for h in range(4):
                    sc.op("pe", lambda e, h=h, b=b: e.transpose(out=PSb[:, 6 + b, h * 128:(h + 1) * 128], in_=g_y[b][:, h * 128:(h + 1) * 128], identity=identb),
                          reads=[("gy", b, h), "identb"], writes=tk)
                sc.op("act", lambda e, b=b: e.activation(out=XT[:, 4:8, tl(t)], in_=PSb[:, 6 + b, 0:512].rearrange("p (c n) -> p c n", c=4), func=AF.Copy),
                      reads=tk, writes=[("XT", t)])

            g_A(0)
            for t in range(NT):
                if t + 1 < NT:
                    g_A(t + 1)
                if os.environ.get("GSKIP") == "B":
                    continue
                g_B(t)
                if os.environ.get("GSKIP") == "norm":
                    continue
                g_norm(t)
                if os.environ.get("GSKIP") == "tr":
                    continue
                if t > 0:
                    g_tr(t - 1)
            if not os.environ.get("GSKIP"):
                g_tr(NT - 1)
            dump("mixT", XT, [128, 8, 2048], [("XT", t) for t in range(NT)])

            if stop == 'G' and l == nl - 1:
                raise _Stop()
            sc.barrier()
            load_ln_params(ln1g_d[l], ln1b_d[l])
            R_D.reset()
            W1B = [R_D.get(8192, BF16, "p (c n) -> p c n", c=8) for _ in range(2)]
            W2B = [R_D.get(8192, BF16, "p (c n) -> p c n", c=4) for _ in range(2)]
            hT = R_D.get(16384, BF16, "p (c n) -> p c n", c=4)
            relu_t = [R_D.get(2048, F32) for _ in range(2)]

            def load_ffn_block(fb, buf):
                s1 = w1_d[l, :, fb * 512:(fb + 1) * 512].rearrange("(c p) n -> p c n", p=128)
                s2 = w2_d[l, fb * 512:(fb + 1) * 512, :].rearrange("(c p) n -> p c n", p=128)
                for hf in range(2):
                    sc.dma("pool", lambda e, hf=hf: e.dma_start(out=W1B[buf][:, hf * 4:(hf + 1) * 4, :], in_=s1[:, hf * 4:(hf + 1) * 4, :]),
                           writes=[("W1B", buf, hf)])
                for hf in range(2):
                    sc.dma("pool", lambda e, hf=hf: e.dma_start(out=W2B[buf][:, hf * 2:(hf + 1) * 2, :], in_=s2[:, hf * 2:(hf + 1) * 2, :]),
                           writes=[("W2B", buf, hf)])

            load_ffn_block(0, 0)
            load_ffn_block(1, 1)
            for t in range(NT):
                sc.dma("sp", lambda e, t=t: e.dma_start(out=X[:, t, :], in_=xs_d[t * 128:(t + 1) * 128, :]), reads=[("xsd", t)], writes=[("X", t)])
            def o_mm(t):
                yb = (t % 3) * 2
                for hf in range(2):
                    for c in range(8):
                        sc.op("pe", lambda e, c=c, hf=hf, t=t, yb=yb: e.matmul(PS[:, yb + hf, :], lhsT=XT[:, c, tl(t)], rhs=WO[:, c, hf * 512:(hf + 1) * 512],
                                                                              start=(c == 0), stop=(c == 7)),
                              reads=[("XT", t), ("RW", 0, 0), ("RW", 0, 1), ("RW", 1, 0), ("RW", 1, 1)], writes=psk(yb + hf))

            def o_ln(t):
                yb = (t % 3) * 2
                sc.op("dve", lambda e, t=t, yb=yb: e.scalar_tensor_tensor(out=X[:, t, :], in0=X[:, t, :], scalar=ALPHA,
                                                                          in1=PS[:, yb:yb + 2, :].rearrange("p a n -> p (a n)"), op0=ALU.mult, op1=ALU.add),
                      reads=[("X", t)] + psk(yb) + psk(yb + 1), writes=[("X", t)])
                ln_a(t)

            for t0 in range(3):
                o_mm(t0)
                o_ln(t0)
            for t in range(NT):
                if t + 3 < NT:
                    o_mm(t + 3)
                ln_b1(t, None)
                if t + 3 < NT:
                    o_ln(t + 3)
                ln_b2(t, None)
            dump("x1T", XT, [128, 8, 2048], [("XT", t) for t in range(NT)])

            if stop == 'O' and l == nl - 1:
                raise _Stop()
            if not last:
                load_win_block(l + 1, 0, 0)
                load_win_block(l + 1, 1, 1)
            hrr = [0]
            for fb in range(8):
                buf = fb % 2
                for r in range(4):
                    for fc in range(4):
                        bank = 4 + hrr[0] % 3
                        rb = hrr[0] % 2
                        hrr[0] += 1
                        for c in range(8):
                            sc.op("pe", lambda e, c=c, fc=fc, r=r, bank=bank, buf=buf: e.matmul(
                                PS[:, bank, :], lhsT=W1B[buf][:, c, fc * 128:(fc + 1) * 128], rhs=XT[:, c, r * 512:(r + 1) * 512],
                                start=(c == 0), stop=(c == 7)), reads=[("W1B", buf, 0), ("W1B", buf, 1)] + xt_all[r * 4:(r + 1) * 4], writes=psk(bank))
                        fcol = fb * 4 + fc
                        sc.op("act", lambda e, bank=bank, rb=rb, fcol=fcol: e.activation(out=relu_t[rb], in_=PS[:, bank, :], func=AF.Relu,
                                                                                         bias=b1c[:, fcol:fcol + 1], scale=1.0),
                              reads=psk(bank) + ["b1c"], writes=[("relu", rb)])
                        sc.op("dve", lambda e, rb=rb, fc=fc, r=r: e.tensor_tensor(out=hT[:, fc, r * 512:(r + 1) * 512], in0=relu_t[rb], in1=relu_t[rb], op=ALU.mult),
                              reads=[("relu", rb)], writes=[("hT", fc, r)])
                for t in range(NT):
                    yb = (t % 2) * 2
                    for hf in range(2):
                        for fc in range(4):
                            sc.op("pe", lambda e, fc=fc, hf=hf, t=t, yb=yb, buf=buf: e.matmul(
                                PS[:, yb + hf, :], lhsT=hT[:, fc, tl(t)], rhs=W2B[buf][:, fc, hf * 512:(hf + 1) * 512],
                                start=(fc == 0), stop=(fc == 3)), reads=[("hT", fc, t // 4), ("W2B", buf, 0), ("W2B", buf, 1)], writes=psk(yb + hf))
                    if fb == 0:
                        sc.op("dve", lambda e, t=t, yb=yb: e.scalar_tensor_tensor(out=X[:, t, :], in0=X[:, t, :], scalar=ALPHA,
                                                                                  in1=PS[:, yb:yb + 2, :].rearrange("p a n -> p (a n)"), op0=ALU.mult, op1=ALU.add),
                              reads=[("X", t)] + psk(yb) + psk(yb + 1), writes=[("X", t)])
                        sc.op("pool", lambda e, t=t: e.tensor_tensor(out=X[:, t, :], in0=X[:, t, :], in1=b2t, op=ALU.add), reads=[("X", t), "b2t"], writes=[("X", t)])
                    else:
                        sc.op("dve", lambda e, t=t, yb=yb: e.tensor_tensor(out=X[:, t, :], in0=X[:, t, :], in1=PS[:, yb:yb + 2, :].rearrange("p a n -> p (a n)"), op=ALU.add),
                              reads=[("X", t)] + psk(yb) + psk(yb + 1), writes=[("X", t)])
                if fb + 2 < 8:
                    load_ffn_block(fb + 2, buf)
            if stop == 'F' and l == nl - 1:
                raise _Stop()
            load_ln_params(ln2g_d[l], ln2b_d[l])
            ln_all(out_d if last else xs_d)
            sc.barrier()


        for _l in range(nl):
            do_layer(_l)
    except _Stop:
        pass
    out_dmas = [o for o in sc.ops if o.is_dma and o.dkey in [("xs", i) for i in range(4)]]
    fin = {}
    for o in out_dmas:
        fin[o.dkey] = o
    finals = list(fin.values()) + list(dbg_out.values())
    sc.emit(final_wait_ops=finals)
    es.close()
    return nc, sc


_CONST = None


def kernel(**inputs):
    global _CONST
    if _CONST is None:
        _CONST = _constants()
    nc, _ = build(2)
    x = np.ascontiguousarray(inputs["x"], dtype=np.float32)
    shared = {k: np.ascontiguousarray(v, dtype=np.float32) for k, v in inputs.items() if k != "x"}
    shared.update(_CONST)
    in_maps = []
    for b in range(8):
        m = dict(shared)
        m["x"] = x[b]
        in_maps.append(m)
    res = run_bass_kernel_spmd(nc, in_maps, core_ids=list(range(8)))
    return np.stack([r["out"] for r in res.results], axis=0).astype(np.float32)
```

```python
import math
import os
from contextlib import ExitStack

import numpy as np
import concourse.bass as bass
import concourse.mybir as mybir
from concourse.bass_utils import run_bass_kernel_spmd

F32 = mybir.dt.float32
BF16 = mybir.dt.bfloat16
AF = mybir.ActivationFunctionType
ALU = mybir.AluOpType

S = 2048
D = 1024
DIN = 3104
DFF = 4096
NT = 16
ALPHA = (2.0 * 2) ** 0.25
ENGS = ("pe", "act", "dve", "pool", "sp")
EPOCH = 30000


class _Res:
    __slots__ = ("last_w", "readers")

    def __init__(self):
        self.last_w = None
        self.readers = []


class _Op:
    __slots__ = ("eng", "fn", "deps", "signal", "tok", "is_dma", "dkey")

    def __init__(self, eng, fn, is_dma, dkey):
        self.eng = eng
        self.fn = fn
        self.deps = []
        self.signal = False
        self.tok = None
        self.is_dma = is_dma
        self.dkey = dkey


class Sched:
    def __init__(self, nc):
        self.nc = nc
        self.ops = []
        self.res = {}
        self.pending = {e: [] for e in ENGS}

    def _r(self, key):
        x = self.res.get(key)
        if x is None:
            x = self.res[key] = _Res()
        return x

    def _add(self, op, reads, writes):
        deps = set()
        for k in reads:
            rs = self._r(k)
            if rs.last_w is not None:
                deps.add(rs.last_w)
        for k in writes:
            rs = self._r(k)
            if rs.last_w is not None:
                deps.add(rs.last_w)
            deps.update(rs.readers)
        for k in reads:
            self._r(k).readers.append(op)
        for k in writes:
            rs = self._r(k)
            rs.last_w = op
            rs.readers = []
        if self.pending[op.eng]:
            deps.update(self.pending[op.eng])
            self.pending[op.eng] = []
        deps.discard(op)
        op.deps = list(deps)
        self.ops.append(op)
        return op

    def op(self, eng, fn, reads=(), writes=()):
        return self._add(_Op(eng, fn, False, None), reads, writes)

    def dma(self, eng, fn, dkey=None, reads=(), writes=()):
        if dkey is None:
            dkey = ("w", writes[0])
        return self._add(_Op(eng, fn, True, dkey), reads, writes)

    def barrier(self):
        last = {}
        for o in self.ops:
            last[(o.eng, o.dkey) if o.is_dma else o.eng] = o
        b = list(last.values())
        self.pending = {e: list(b) for e in ENGS}

    def emit(self, final_wait_ops=()):
        nc = self.nc
        ops = self.ops
        for o in ops:
            for d in o.deps:
                if d.is_dma:
                    d.signal = True
                elif d.eng == "pe" and o.eng == "pe" and not o.is_dma:
                    continue
                else:
                    d.signal = True
        with ExitStack() as es:
            eng_sems = {e: [] for e in ENGS}
            cnt = {e: 0 for e in ENGS}
            dma_sems = {}
            dma_cnt = {}
            for o in ops:
                if o.is_dma:
                    if o.dkey not in dma_sems:
                        dma_sems[o.dkey] = es.enter_context(nc.semaphore("d%d" % len(dma_sems)))
                        dma_cnt[o.dkey] = 0
                    dma_cnt[o.dkey] += 16
                    o.tok = (dma_sems[o.dkey], dma_cnt[o.dkey])
                elif o.signal:
                    ep = cnt[o.eng] // EPOCH
                    if ep >= len(eng_sems[o.eng]):
                        eng_sems[o.eng].append(es.enter_context(nc.semaphore("e_%s_%d" % (o.eng, ep))))
                    cnt[o.eng] += 1
                    o.tok = (eng_sems[o.eng][ep], cnt[o.eng] - ep * EPOCH)
            per_eng = {e: [o for o in ops if o.eng == e] for e in ENGS}
            self.stats = {e: len(per_eng[e]) for e in ENGS}
            self.stats["sems"] = sum(len(v) for v in eng_sems.values()) + len(dma_sems)

            def run(e, eng):
                waited = {}
                for o in per_eng[e]:
                    need = {}
                    for d in o.deps:
                        if d.tok is None:
                            continue
                        if (not d.is_dma) and d.eng == "pe" and e == "pe" and not o.is_dma:
                            continue
                        s, v = d.tok
                        k = id(s)
                        if waited.get(k, 0) >= v:
                            continue
                        if k not in need or need[k][1] < v:
                            need[k] = (s, v)
                    for k, (s, v) in need.items():
                        eng.wait_ge(s, v)
                        waited[k] = v
                    ins = o.fn(eng)
                    if o.tok is not None:
                        ins.then_inc(o.tok[0], 16 if o.is_dma else 1)
                if e == "sp":
                    for o in final_wait_ops:
                        s, v = o.tok
                        eng.wait_ge(s, v)

            with nc.Block() as block:
                @block.sync
                def _(eng):
                    run("sp", eng)

                @block.tensor
                def _(eng):
                    run("pe", eng)

                @block.scalar
                def _(eng):
                    run("act", eng)

                @block.vector
                def _(eng):
                    run("dve", eng)

                @block.gpsimd
                def _(eng):
                    run("pool", eng)


def _t5_bucket(rel):
    nb = 16
    me = 8
    ret = np.where(rel > 0, nb, 0)
    n = np.abs(rel)
    large = me + (np.log(np.maximum(n, 1).astype(np.float32) / np.float32(me))
                  / np.float32(math.log(128 / me)) * np.float32(nb - me)).astype(np.int32)
    large = np.minimum(large, nb - 1)
    return ret + np.where(n < me, n, large)


MLEN = 1280


def _constants():
    c = {}
    c["c_ident"] = np.eye(128, dtype=np.float32)
    c["c_J"] = np.eye(128, dtype=np.float32)[::-1].copy()
    s = np.arange(128)[:, None]
    t = np.arange(128)[None, :]
    uf = np.zeros((128, 129), np.float32)
    uf[:, :128] = (s <= t)
    uf[:, 128] = 1.0
    ub = np.zeros((128, 129), np.float32)
    ub[:, :128] = (s >= t)
    ub[:, 128] = 1.0
    c["c_uf"] = uf
    c["c_ub"] = ub
    c["c_sf"] = (s > t).astype(np.float32)
    c["c_sb"] = (s < t).astype(np.float32)
    n = np.arange(MLEN)
    bk = _t5_bucket(639 - n)
    oh = np.zeros((32, MLEN), np.float32)
    oh[bk, n] = 1.0
    c["c_onehot"] = oh
    return c


class _Stop(Exception):
    pass


def build(nl=2, dbg=(), stop=None):
    nc = bass.Bass("TRN2", target_bir_lowering=False)

    def din(name, shape):
        return nc.dram_tensor(name, list(shape), F32, kind="ExternalInput").ap()

    x_d = din("x", [S, D])
    lnemb_g = din("ln_emb_g", [D])
    lnemb_b = din("ln_emb_b", [D])
    table_d = din("rel_bias_table", [32, 4])
    w_in_d = din("w_in", [2, D, DIN])
    lq1_d = din("lambda_q1", [2, 64])
    lk1_d = din("lambda_k1", [2, 64])
    lq2_d = din("lambda_q2", [2, 64])
    lk2_d = din("lambda_k2", [2, 64])
    dnw_d = din("diff_norm_w", [2, 128])
    gup_d = din("gla_gate_up", [2, 2, 16, 256])
    gbias_d = din("gla_gate_bias", [2, 2, 256])
    gnw_d = din("gla_norm_w", [2, 128])
    w_o_d = din("w_o", [2, D, D])
    ln1g_d = din("ln1_g", [2, D])
    ln1b_d = din("ln1_b", [2, D])
    w1_d = din("w_ffn1", [2, D, DFF])
    b1_d = din("b_ffn1", [2, DFF])
    w2_d = din("w_ffn2", [2, DFF, D])
    b2_d = din("b_ffn2", [2, D])
    ln2g_d = din("ln2_g", [2, D])
    ln2b_d = din("ln2_b", [2, D])
    c_ident = din("c_ident", [128, 128])
    c_J = din("c_J", [128, 128])
    c_uf = din("c_uf", [128, 129])
    c_ub = din("c_ub", [128, 129])
    c_sf = din("c_sf", [128, 128])
    c_sb = din("c_sb", [128, 128])
    c_onehot = din("c_onehot", [32, MLEN])
    out_d = nc.dram_tensor("out", [S, D], F32, kind="ExternalOutput").ap()
    xs_d = nc.dram_tensor("xs_scratch", [S, D], F32).ap()
    md_t = nc.dram_tensor("md_scratch", [4, MLEN], F32)
    eb_d = nc.dram_tensor("expb_scratch", [128, 4 * 1152], BF16).ap()
    md_d = md_t.ap()
    dbg_out = {}

    sc = Sched(nc)
    es = ExitStack()
    ARENA_BYTES = 207 * 1024
    arena = es.enter_context(nc.sbuf_tensor("arena", [128, ARENA_BYTES // 2], BF16))
    PSb = es.enter_context(nc.psum_tensor("ps", [128, 8, 1024], BF16))[:]
    PS = PSb.bitcast(F32)

    def view(off, nbytes, dt, pattern=None, **kw):
        assert off % 32 == 0, off
        a = arena[:, off // 2:(off + nbytes) // 2]
        if dt is F32:
            a = a.bitcast(F32)
        if pattern:
            a = a.rearrange(pattern, **kw)
        return a

    class Alloc:
        def __init__(self, base, size):
            self.base = base
            self.size = size
            self.pos = 0

        def reset(self):
            self.pos = 0

        def get(self, nbytes, dt, pattern=None, **kw):
            n = (nbytes + 31) // 32 * 32
            assert self.pos + n <= self.size, (self.pos, n, self.size)
            v = view(self.base + self.pos, nbytes, dt, pattern, **kw)
            self.pos += n
            return v

    R_XT = Alloc(0, 32768)
    R_X = Alloc(32768, 65536)
    R_D = Alloc(98304, 70656)
    R_W = Alloc(168960, 16384)
    R_C = Alloc(185344, ARENA_BYTES - 185344)

    XT = R_XT.get(32768, BF16, "p (c n) -> p c n", c=8)
    X = R_X.get(65536, F32, "p (t n) -> p t n", t=NT)
    WB = [R_W.get(8192, BF16, "p (c n) -> p c n", c=8) for _ in range(2)]
    R_W.reset()
    WO = R_W.get(16384, BF16, "p (c n) -> p c n", c=8)

    identb = R_C.get(256, BF16)
    Jb = R_C.get(256, BF16)
    Uf = R_C.get(516, F32)
    Ub = R_C.get(516, F32)
    Usf = R_C.get(512, F32)
    Usb = R_C.get(512, F32)
    gt = R_C.get(4096, F32)
    bt = R_C.get(4096, F32)
    b2t = R_C.get(4096, F32)
    wd_t = R_C.get(512, F32)
    wg_t = R_C.get(512, F32)
    b1c = R_C.get(128, F32)
    cb = R_C.get(32, F32, "p (s h) -> p s h", s=2)
    lamv = R_C.get(4 * 64 * 4, F32, "p (a n) -> p a n", a=4)
    lamp = R_C.get(2 * 64 * 4, F32, "p (a n) -> p a n", a=2)
    lams = R_C.get(32, F32)
    Wg = R_C.get(1024, BF16)
    st_ = [R_C.get(48, F32) for _ in range(2)]
    mv_ = [R_C.get(8, F32) for _ in range(2)]
    rs_ = [R_C.get(4, F32) for _ in range(2)]
    xb_ = [R_C.get(2048, BF16) for _ in range(2)]
    dsm = [R_C.get(64, F32) for _ in range(2)]
    gsm = [R_C.get(64, F32) for _ in range(2)]
    decs = R_C.get(2 * 2 * 16 * 4, F32, "p (d q t) -> p d q t", d=2, q=2)

    def psk(b):
        return [("ps", b, q) for q in range(4)]

    def bc_mid(ap2, n):
        a = ap2.ap
        return bass.AP(ap2.tensor, ap2.offset, [list(a[0]), [0, n], list(a[1])])

    def bc_last(ap2, n):
        a = ap2.ap
        return bass.AP(ap2.tensor, ap2.offset, [list(a[0]), list(a[1]), [0, n]])

    cur_layer = [-1]

    def dump(name, ap, shape, reads):
        nm = "%s@%d" % (name, cur_layer[0])
        if nm in dbg:
            name = nm
        elif name not in dbg or (cur_layer[0] >= 0 and cur_layer[0] != nl - 1):
            return
        t = nc.dram_tensor("dbg_" + name.replace("@", "_"), list(shape), ap.dtype, kind="ExternalOutput").ap()
        dbg_out[name] = sc.dma("sp", lambda e: e.dma_start(out=t, in_=ap), "dbg", reads=reads)

    try:
        sc.dma("pool", lambda e: e.dma_start(out=identb, in_=c_ident), writes=["identb"])
        sc.dma("pool", lambda e: e.dma_start(out=Jb, in_=c_J), writes=["Jb"])
        sc.dma("sp", lambda e: e.dma_start(out=Uf, in_=c_uf), writes=["Uf"])
        sc.dma("sp", lambda e: e.dma_start(out=Ub, in_=c_ub), writes=["Ub"])
        sc.dma("sp", lambda e: e.dma_start(out=Usf, in_=c_sf), writes=["Usf"])
        sc.dma("sp", lambda e: e.dma_start(out=Usb, in_=c_sb), writes=["Usb"])
        for si, row in enumerate((15, 31)):
            sc.dma("sp", lambda e, si=si, row=row: e.dma_start(out=cb[:, si, :], in_=table_d[row, :].partition_broadcast(128)),
                   writes=[("cb", si)])

        R_D.reset()
        tb = R_D.get(16, F32)
        oh = R_D.get(MLEN * 4, F32)
        msb = R_D.get(MLEN * 4, F32)
        sc.dma("sp", lambda e: e.dma_start(out=tb[0:32, :], in_=table_d), writes=["tb"])
        sc.dma("sp", lambda e: e.dma_start(out=oh[0:32, :], in_=c_onehot), writes=["oh"])
        sc.dma("sp", lambda e: e.dma_start(out=gt, in_=lnemb_g.partition_broadcast(128)), writes=["gt"])
        sc.dma("sp", lambda e: e.dma_start(out=bt, in_=lnemb_b.partition_broadcast(128)), writes=["bt"])
        for t in range(NT):
            sc.dma("sp", lambda e, t=t: e.dma_start(out=X[:, t, :], in_=x_d[t * 128:(t + 1) * 128, :]), writes=[("X", t)])
        for ci, (c0, cn) in enumerate(((0, 512), (512, 512), (1024, 256))):
            sc.op("pe", lambda e, ci=ci, c0=c0, cn=cn: e.matmul(PS[0:4, ci, 0:cn], lhsT=tb[0:32, :], rhs=oh[0:32, c0:c0 + cn], start=True, stop=True),
                  reads=["tb", "oh"], writes=psk(ci))
            sc.op("dve", lambda e, ci=ci, c0=c0, cn=cn: e.tensor_copy(out=msb[0:4, c0:c0 + cn], in_=PS[0:4, ci, 0:cn]),
                  reads=psk(ci), writes=["msb"])
        sc.dma("sp", lambda e: e.dma_start(out=md_d, in_=msb[0:4, :]), reads=["msb"], writes=["md"])
        R_D.reset()
        R_D.get(16384, BF16); R_D.get(16384, BF16); R_D.get(16 * 4 * 129 * 2, BF16)
        expB0 = R_D.get(4 * 1152 * 2, BF16, "p (h n) -> p h n", h=4)
        R_D.get(4096, BF16); R_D.get(1024, BF16)
        tmp_revs = [view(R_D.base + 16384 + i * 2304, 2304, BF16) for i in range(4)]
        for h in range(4):
            src = bass.AP(md_t, h * MLEN, [[1, 128], [1, 1152]])
            sc.dma("pool", lambda e, src=src, h=h: e.dma_start(out=tmp_revs[h], in_=src), reads=["md"], writes=[("tmp_rev", h)])
        for h in range(4):
            for ci, (c0, cn) in enumerate(((0, 512), (512, 512), (1024, 128))):
                sc.op("pe", lambda e, ci=ci, c0=c0, cn=cn, h=h: e.matmul(PS[:, ci, 0:cn], lhsT=Jb, rhs=tmp_revs[h][:, c0:c0 + cn], start=True, stop=True),
                      reads=["Jb", ("tmp_rev", h)], writes=psk(ci))
                sc.op("act", lambda e, h=h, ci=ci, c0=c0, cn=cn: e.activation(out=expB0[:, h, c0:c0 + cn], in_=PS[:, ci, 0:cn], func=AF.Exp),
                      reads=psk(ci), writes=[("expB", h)])
        sc.dma("sp", lambda e: e.dma_start(out=eb_d, in_=expB0.rearrange("p h n -> p (h n)")), reads=[("expB", h) for h in range(4)], writes=["eb_d"])

        def ln_a(t):
            Xt = X[:, t, :]
            kx = ("X", t)
            b = t % 2
            st, mv, rs = st_[b], mv_[b], rs_[b]
            sc.op("dve", lambda e: e.bn_stats(out=st[:, 0:6], in_=Xt[:, 0:512]), reads=[kx], writes=[("st", b, 0)])
            sc.op("dve", lambda e: e.bn_stats(out=st[:, 6:12], in_=Xt[:, 512:1024]), reads=[kx], writes=[("st", b, 1)])
            sc.op("dve", lambda e: e.bn_aggr(out=mv, in_=st), reads=[("st", b, 0), ("st", b, 1)], writes=[("mv", b)])
            sc.op("act", lambda e: e.activation(out=rs, in_=mv[:, 1:2], func=AF.Sqrt, bias=1e-5, scale=1.0), reads=[("mv", b)], writes=[("rs", b)])
            sc.op("dve", lambda e: e.reciprocal(out=rs, in_=rs), reads=[("rs", b)], writes=[("rs", b)])
            sc.op("dve", lambda e: e.tensor_scalar(out=Xt, in0=Xt, scalar1=mv[:, 0:1], scalar2=rs, op0=ALU.subtract, op1=ALU.mult),
                  reads=[kx, ("mv", b), ("rs", b)], writes=[kx])
            sc.op("dve", lambda e: e.tensor_tensor(out=Xt, in0=Xt, in1=gt, op=ALU.mult), reads=[kx, "gt"], writes=[kx])
            sc.op("pool", lambda e: e.tensor_tensor(out=Xt, in0=Xt, in1=bt, op=ALU.add), reads=[kx, "bt"], writes=[kx])

        def ln_b1(t, spill_to):
            Xt = X[:, t, :]
            kx = ("X", t)
            b = t % 2
            xb = xb_[b]
            if spill_to is not None:
                sc.dma("sp", lambda e: e.dma_start(out=spill_to[t * 128:(t + 1) * 128, :], in_=Xt), ("xs", t % 4), reads=[kx], writes=[("xsd", t)])
            if spill_to is out_d:
                return
            sc.op("act", lambda e: e.activation(out=xb, in_=Xt, func=AF.Copy), reads=[kx], writes=[("xb", b)])

        def ln_b2(t, spill_to):
            if spill_to is out_d:
                return
            b = t % 2
            xb = xb_[b]
            bank = 6 + b
            for c in range(8):
                sc.op("pe", lambda e, c=c: e.transpose(out=PSb[:, bank, c * 128:(c + 1) * 128], in_=xb[:, c * 128:(c + 1) * 128], identity=identb),
                      reads=[("xb", b), "identb"], writes=psk(bank))
            sc.op("act", lambda e: e.activation(out=XT[:, :, t * 128:(t + 1) * 128], in_=PSb[:, bank, :].rearrange("p (c n) -> p c n", c=8), func=AF.Copy),
                  reads=psk(bank), writes=[("XT", t)])

        def ln_all(spill_to):
            ln_a(0)
            for t in range(NT):
                if t + 1 < NT:
                    ln_a(t + 1)
                ln_b1(t, spill_to)
                ln_b2(t, spill_to)

        def load_ln_params(g_ap, b_ap):
            sc.dma("sp", lambda e: e.dma_start(out=gt, in_=g_ap.partition_broadcast(128)), writes=["gt"])
            sc.dma("sp", lambda e: e.dma_start(out=bt, in_=b_ap.partition_broadcast(128)), writes=["bt"])

        def load_win_block(l, blk, buf):
            c0 = blk * 512
            ncol = min(512, DIN - c0)
            src = w_in_d[l, :, c0:c0 + ncol].rearrange("(c p) n -> p c n", p=128)
            for hf in range(2):
                sc.dma("pool", lambda e, hf=hf: e.dma_start(out=WB[buf][:, hf * 4:(hf + 1) * 4, 0:ncol], in_=src[:, hf * 4:(hf + 1) * 4, :]),
                       writes=[("RW", buf, hf)])

        if stop == 'init':
            raise _Stop()
        load_win_block(0, 0, 0)
        load_win_block(0, 1, 1)
        ln_all(xs_d)
        dump("h0", X, [128, NT, 1024], [("X", t) for t in range(NT)])

        if stop == 'emb':
            raise _Stop()
        evac_rr = [0]

        def evac(out, in_, reads, writes, scale=None):
            evac_rr[0] ^= 1
            if evac_rr[0]:
                if scale is None:
                    sc.op("act", lambda e: e.activation(out=out, in_=in_, func=AF.Copy), reads=reads, writes=writes)
                else:
                    sc.op("act", lambda e: e.mul(out=out, in_=in_, mul=scale), reads=reads, writes=writes)
            else:
                if scale is None:
                    sc.op("dve", lambda e: e.tensor_copy(out=out, in_=in_), reads=reads, writes=writes)
                else:
                    sc.op("dve", lambda e: e.tensor_scalar(out=out, in0=in_, scalar1=scale, scalar2=None, op0=ALU.mult), reads=reads, writes=writes)

        def do_layer(l):
            lam_init = 0.8 - 0.6 * math.exp(-0.3 * l)
            cur_layer[0] = l
            if l == 0:
                sc.barrier()
            last = (l == nl - 1)
            R_D.reset()
            QT = R_D.get(16384, BF16, "p (h n) -> p h n", h=4)
            KT = R_D.get(16384, BF16, "p (h n) -> p h n", h=4)
            V = R_D.get(16 * 4 * 129 * 2, BF16, "p (t h e) -> p t h e", t=16, h=4)
            expB = R_D.get(4 * 1152 * 2, BF16, "p (h n) -> p h n", h=4)
            Eb = R_D.get(4096, BF16, "p (b m n) -> p b m n", b=2, m=2)
            d_y = R_D.get(4 * 128 * 2, BF16, "p (u n) -> p u n", u=4)
            _pu = R_D.pos
            silu_t = [R_D.get(2048, F32) for _ in range(2)]
            R_D.pos = _pu
            accS = R_D.get(8 * 129 * 4, F32, "p (a n) -> p a n", a=8)
            R_D.pos = _pu + 4608
            tmp_rev = R_D.get(1152 * 2, BF16)
            R_X.reset()
            gqT = R_X.get(8192, BF16, "p (c n) -> p c n", c=2)
            gkT = R_X.get(8192, BF16, "p (c n) -> p c n", c=2)
            gk_tok = R_X.get(8192, BF16, "p (t n) -> p t n", t=16)
            gv = R_X.get(16384, BF16, "p (t n) -> p t n", t=16)
            gr_s = R_X.get(16384, BF16, "p (t n) -> p t n", t=16)
            G33 = R_X.get(4096, BF16)

            for i, ap in enumerate((lq1_d, lk1_d, lq2_d, lk2_d)):
                sc.dma("sp", lambda e, i=i, ap=ap: e.dma_start(out=lamv[:, i, :], in_=ap[l, :].partition_broadcast(128)), writes=[("lamv", i)])
            sc.op("dve", lambda e: e.tensor_tensor(out=lamp[:, 0, :], in0=lamv[:, 0, :], in1=lamv[:, 1, :], op=ALU.mult), reads=[("lamv", 0), ("lamv", 1)], writes=["lamp"])
            sc.op("dve", lambda e: e.tensor_tensor(out=lamp[:, 1, :], in0=lamv[:, 2, :], in1=lamv[:, 3, :], op=ALU.mult), reads=[("lamv", 2), ("lamv", 3)], writes=["lamp"])
            sc.op("dve", lambda e: e.reduce_sum(out=lams[:, 0:2], in_=lamp, axis=mybir.AxisListType.X), reads=["lamp"], writes=["lams"])
            sc.op("act", lambda e: e.activation(out=lams[:, 0:2], in_=lams[:, 0:2], func=AF.Exp), reads=["lams"], writes=["lams"])
            sc.op("dve", lambda e: e.tensor_tensor(out=lams[:, 2:3], in0=lams[:, 0:1], in1=lams[:, 1:2], op=ALU.subtract), reads=["lams"], writes=["lams"])
            sc.op("dve", lambda e: e.tensor_scalar(out=lams[:, 3:4], in0=lams[:, 2:3], scalar1=lam_init, scalar2=-1.0, op0=ALU.add, op1=ALU.mult),
                  reads=["lams"], writes=["neglam"])
            neg_lam = lams[:, 3:4]
            sc.dma("sp", lambda e: e.dma_start(out=wd_t, in_=dnw_d[l, :].partition_broadcast(128)), writes=["wd"])
            sc.op("dve", lambda e: e.tensor_scalar(out=wd_t, in0=wd_t, scalar1=1.0 - lam_init, scalar2=None, op0=ALU.mult), reads=["wd"], writes=["wd"])
            sc.dma("sp", lambda e: e.dma_start(out=wg_t, in_=gnw_d[l, :].partition_broadcast(128)), writes=["wg"])
            sc.op("dve", lambda e: e.memset(Wg[0:33, :], 0.0), writes=["Wg"])
            sc.dma("pool", lambda e: e.dma_start(out=Wg[0:16, 0:256], in_=gup_d[l, 0]), writes=["Wg"])
            sc.dma("pool", lambda e: e.dma_start(out=Wg[16:32, 256:512], in_=gup_d[l, 1]), writes=["Wg"])
            sc.dma("pool", lambda e: e.dma_start(out=Wg[32:33, :], in_=gbias_d[l].rearrange("a n -> (a n)").partition_broadcast(1)), writes=["Wg"])
            sc.dma("sp", lambda e: e.dma_start(out=b1c, in_=b1_d[l].rearrange("(c p) -> p c", p=128), allow_slow_non_contiguous=True), writes=["b1c"])
            sc.dma("sp", lambda e: e.dma_start(out=b2t, in_=b2_d[l].partition_broadcast(128)), writes=["b2t"])
            sc.dma("sp", lambda e: e.dma_start(out=expB.rearrange("p h n -> p (h n)"), in_=eb_d), reads=["eb_d"], writes=[("expB", h) for h in range(4)])
            sc.op("dve", lambda e: e.memset(V[:, :, :, 128:129], 1.0), writes=[("V", t) for t in range(NT)])
            sc.op("dve", lambda e: e.memset(G33[32:33, :], 1.0), writes=["G33"])

            dump("XTin", XT, [128, 8, 2048], [("XT", t) for t in range(NT)])
            dump("Xin", X, [128, NT, 1024], [("X", t) for t in range(NT)])
            if stop == 'L' and l == nl - 1:
                raise _Stop()
            ps_rr = [0]

            def nextbank():
                b = ps_rr[0] % 6
                ps_rr[0] += 1
                return b

            xt_all = [("XT", t) for t in range(NT)]
            for blk in range(7):
                buf = blk % 2
                wb = WB[buf]
                kw = [("RW", buf, 0), ("RW", buf, 1)]
                if blk in (0, 1, 3, 6):
                    nch = 1 if blk == 6 else 4
                    for cc in range(nch):
                        for r in range(4):
                            bank = nextbank()
                            M = 32 if blk == 6 else 128
                            for c in range(8):
                                sc.op("pe", lambda e, c=c, cc=cc, r=r, bank=bank, M=M, wb=wb: e.matmul(
                                    PS[0:M, bank, :], lhsT=wb[:, c, cc * 128:cc * 128 + M], rhs=XT[:, c, r * 512:(r + 1) * 512],
                                    start=(c == 0), stop=(c == 7)), reads=kw + xt_all[r * 4:(r + 1) * 4], writes=psk(bank))
                            sl = slice(r * 512, (r + 1) * 512)
                            if blk == 0:
                                evac(QT[:, cc, sl], PS[:, bank, :], psk(bank), [("QT", cc, r)], scale=0.125)
                            elif blk == 1:
                                evac(KT[:, cc, sl], PS[:, bank, :], psk(bank), [("KT", cc, r)])
                            elif blk == 3:
                                if cc < 2:
                                    evac(gqT[:, cc, sl], PS[:, bank, :], psk(bank), [("gqT", 4 * r + i) for i in range(4)], scale=0.125)
                                else:
                                    evac(gkT[:, cc - 2, sl], PS[:, bank, :], psk(bank), [("gkT", 4 * r + i) for i in range(4)])
                            else:
                                evac(G33[0:32, sl], PS[0:32, bank, :], psk(bank), ["G33"])
                if blk == 3:
                    for t in range(NT):
                        bank = nextbank()
                        for cc in range(2):
                            sc.op("pe", lambda e, t=t, cc=cc, bank=bank: e.transpose(out=PSb[:, bank, cc * 128:(cc + 1) * 128], in_=gkT[:, cc, t * 128:(t + 1) * 128], identity=identb),
                                  reads=[("gkT", t), "identb"], writes=psk(bank))
                        evac(gk_tok[:, t, :], PSb[:, bank, 0:256], psk(bank), [("gk_tok", t)])
                if blk in (2, 4, 5):
                    for t in range(NT):
                        bank = nextbank()
                        c0, ncol = (0, 512)
                        for c in range(8):
                            sc.op("pe", lambda e, c=c, t=t, bank=bank, c0=c0, ncol=ncol, wb=wb: e.matmul(
                                PS[:, bank, 0:ncol], lhsT=XT[:, c, t * 128:(t + 1) * 128], rhs=wb[:, c, c0:c0 + ncol],
                                start=(c == 0), stop=(c == 7)), reads=kw + [("XT", t)], writes=psk(bank))
                        if blk == 2:
                            evac(V[:, t, :, 0:128], PS[:, bank, :].rearrange("p (h e) -> p h e", h=4), psk(bank), [("V", t)])
                        elif blk == 4:
                            evac(gv[:, t, :], PS[:, bank, :], psk(bank), [("gv", t)])
                        else:
                            sb = t % 2
                            sc.op("act", lambda e, bank=bank, sb=sb: e.activation(out=silu_t[sb], in_=PS[:, bank, :], func=AF.Silu),
                                  reads=psk(bank), writes=[("silu", sb)])
                            sc.op("dve", lambda e, t=t, sb=sb: e.tensor_tensor(
                                out=gr_s[:, t, :].rearrange("p (h e) -> p h e", h=4), in0=silu_t[sb].rearrange("p (h e) -> p h e", h=4),
                                in1=bc_mid(wg_t, 4), op=ALU.mult), reads=[("silu", sb), "wg"], writes=[("gr_s", t)])
                if blk + 2 < 7:
                    load_win_block(l, blk + 2, buf)
            for hf in range(2):
                sc.dma("pool", lambda e, hf=hf: e.dma_start(out=WO[:, hf * 4:(hf + 1) * 4, :],
                                                             in_=w_o_d[l].rearrange("(c p) n -> p c n", p=128)[:, hf * 4:(hf + 1) * 4, :]),
                       writes=[("RW", hf, 0), ("RW", hf, 1)])
            dump("QT", QT, [128, 4, 2048], [("QT", a, b) for a in range(4) for b in range(4)])
            dump("KT", KT, [128, 4, 2048], [("KT", a, b) for a in range(4) for b in range(4)])
            dump("V", V, [128, 16, 4, 129], [("V", t) for t in range(NT)])
            dump("expB", expB, [128, 4, 1152], [("expB", h) for h in range(4)])
            dump("gqT", gqT, [128, 2, 2048], [("gqT", t) for t in range(NT)])
            dump("gr_s", gr_s, [128, 16, 512], [("gr_s", t) for t in range(NT)])
            dump("G33", G33[0:33, :], [33, 2048], ["G33"])

            if stop == 'P' and l == nl - 1:
                raise _Stop()
            steps = [(h, r, j) for h in range(4) for r in range(4) for j in range(16)]

            def acc_ap(m, u):
                idx = m * 4 + u
                return PS[:, 4 + idx // 3, (idx % 3) * 160:(idx % 3) * 160 + 129]

            def acc_keys(m, u):
                return psk(4 + (m * 4 + u) // 3)

            Eb3 = view(R_X.base + 61440, 2048, BF16, "p (m n) -> p m n", m=2)
            EbL = [Eb[:, 0, :, :], Eb[:, 1, :, :], Eb3]

            def d_scores(i):
                h, r, j = steps[i]
                d = j - 4 * r
                mixed = (-1 <= d <= 4)
                sb = i % 2
                eb = i % 3
                E = EbL[eb]
                for m in range(2):
                    bank = sb * 2 + m
                    sc.op("pe", lambda e, h=h, r=r, j=j, m=m, bank=bank: e.matmul(
                        PS[:, bank, :], lhsT=KT[64 * m:64 * m + 64, h, j * 128:(j + 1) * 128],
                        rhs=QT[64 * m:64 * m + 64, h, r * 512:(r + 1) * 512], start=True, stop=True),
                        reads=[("KT", h, j // 4), ("QT", h, r)], writes=psk(bank))
                pk2 = psk(sb * 2) + psk(sb * 2 + 1)
                ek = [("E", eb, 0), ("E", eb, 1)]
                if mixed:
                    c0 = (4 - d) * 128
                    sc.op("act", lambda e, sb=sb, E=E: e.activation(out=E, in_=PS[:, sb * 2:sb * 2 + 2, :], func=AF.Exp),
                          reads=pk2, writes=ek)
                    for m in range(2):
                        sc.op("dve", lambda e, E=E, m=m, h=h, c0=c0: e.tensor_tensor(out=E[:, m, :], in0=E[:, m, :], in1=expB[:, h, c0:c0 + 512], op=ALU.mult),
                              reads=[("E", eb, m), ("expB", h)], writes=[("E", eb, m)])
                else:
                    side = 0 if d < 0 else 1
                    sc.op("act", lambda e, sb=sb, E=E, side=side, h=h: e.activation(
                        out=E, in_=PS[:, sb * 2:sb * 2 + 2, :], func=AF.Exp, bias=cb[:, side, h:h + 1]),
                        reads=pk2 + [("cb", 0), ("cb", 1)], writes=ek)

            def d_av(i):
                h, r, j = steps[i]
                eb = i % 3
                E = EbL[eb]
                for m in range(2):
                    for u in range(4):
                        sc.op("pe", lambda e, h=h, j=j, m=m, u=u, E=E: e.matmul(
                            acc_ap(m, u), lhsT=E[:, m, u * 128:(u + 1) * 128], rhs=V[:, j, h, 0:129],
                            start=(j == 0 and (m * 4 + u) % 3 == 0), stop=(j == 15), skip_group_check=True),
                            reads=[("E", eb, m), ("V", j)], writes=acc_keys(m, u))

            sm = dsm[0]
            def ak(*idx):
                return [("accS", i) for i in idx]

            def d_final(h, r):
                sc.op("dve", lambda e: e.tensor_copy(out=accS[:, 0:3, :], in_=PS[:, 4, 0:480].rearrange("p (a n) -> p a n", a=3)[:, :, 0:129]),
                      reads=psk(4), writes=ak(0, 1, 2) + [("silu", 0), ("silu", 1)])
                sc.op("dve", lambda e: e.tensor_copy(out=accS[:, 3:6, :], in_=PS[:, 5, 0:480].rearrange("p (a n) -> p a n", a=3)[:, :, 0:129]),
                      reads=psk(5), writes=ak(3, 4, 5))
                sc.op("dve", lambda e: e.tensor_copy(out=accS[:, 6:8, :], in_=PS[:, 6, 0:320].rearrange("p (a n) -> p a n", a=2)[:, :, 0:129]),
                      reads=psk(6), writes=ak(6, 7))

            def d_final2(h, r):
                sc.op("dve", lambda e: e.reciprocal(out=sm[:, 0:8], in_=accS[:, :, 128]), reads=ak(*range(8)), writes=["dsm"])
                sc.op("dve", lambda e: e.tensor_scalar(out=sm[:, 4:8], in0=sm[:, 4:8], scalar1=neg_lam, scalar2=None, op0=ALU.mult),
                      reads=["dsm", "neglam"], writes=["dsm"])
                sc.op("dve", lambda e: e.memset(sm[:, 8:12], 0.0), writes=[("dss", u) for u in range(4)])

            def d_final_u(u):
                if True:
                    sc.op("dve", lambda e, u=u: e.tensor_scalar(out=accS[:, u, 0:128], in0=accS[:, u, 0:128], scalar1=sm[:, u:u + 1], scalar2=None, op0=ALU.mult),
                          reads=["dsm"] + ak(u), writes=ak(u))
                    sc.op("dve", lambda e, u=u: e.scalar_tensor_tensor(out=accS[:, u, 0:128], in0=accS[:, 4 + u, 0:128], scalar=sm[:, 4 + u:5 + u],
                                                                       in1=accS[:, u, 0:128], op0=ALU.mult, op1=ALU.add),
                          reads=["dsm"] + ak(u, 4 + u), writes=ak(u))
                    sc.op("dve", lambda e, u=u: e.scalar_tensor_tensor(out=accS[:, 4 + u, 0:128], in0=accS[:, u, 0:128], scalar=1.0, in1=accS[:, u, 0:128],
                                                                       op0=ALU.mult, op1=ALU.mult, accum_out=sm[:, 8 + u:9 + u]),
                          reads=ak(u), writes=ak(4 + u) + [("dss", u)])

            def d_final_b(h, r):
                sc.op("act", lambda e: e.activation(out=sm[:, 12:16], in_=sm[:, 8:12], func=AF.Ln, bias=1e-5, scale=1.0 / 128),
                      reads=[("dss", u) for u in range(4)], writes=["drs"])
                sc.op("act", lambda e: e.activation(out=sm[:, 12:16], in_=sm[:, 12:16], func=AF.Exp, scale=-0.5), reads=["drs"], writes=["drs"])
                for u in range(4):
                    sc.op("dve", lambda e, u=u: e.scalar_tensor_tensor(out=d_y[:, u, :], in0=accS[:, u, 0:128], scalar=sm[:, 12 + u:13 + u], in1=wd_t,
                                                                       op0=ALU.mult, op1=ALU.mult),
                          reads=ak(u) + ["drs", "wd"], writes=[("dy", u)])

            def d_final_pe(h, r):
                for u in range(4):
                    sc.op("pe", lambda e, u=u: e.transpose(out=PSb[:, 7, u * 128:(u + 1) * 128], in_=d_y[:, u, :], identity=identb),
                          reads=[("dy", u), "identb"], writes=psk(7))
                sc.op("dve", lambda e, h=h, r=r: e.tensor_copy(out=XT[:, h, r * 512:(r + 1) * 512], in_=PSb[:, 7, 0:512]),
                      reads=psk(7), writes=[("XT", 4 * r + i) for i in range(4)])

            pend = []
            pend_b = []
            pend_u = []
            d_scores(0)
            d_scores(1)
            for i in range(len(steps)):
                h, r, j = steps[i]
                if j == 15:
                    d_av(i)
                    d_final(h, r)
                    if i + 2 < len(steps):
                        d_scores(i + 2)
                    d_final2(h, r)
                else:
                    if i + 2 < len(steps):
                        d_scores(i + 2)
                    d_av(i)
                if j == 15:
                    pend.append((h, r))
                    pend_b.append((h, r))
                    pend_u.extend([0, 1, 2, 3])
                    d_final_u(pend_u.pop(0))
                elif pend_u:
                    d_final_u(pend_u.pop(0))
                elif j == 4 and pend_b:
                    d_final_b(*pend_b.pop(0))
                elif j == 7 and pend:
                    d_final_pe(*pend.pop(0))
            while pend_u:
                d_final_u(pend_u.pop(0))
            while pend_b:
                d_final_b(*pend_b.pop(0))
            while pend:
                d_final_pe(*pend.pop(0))
            dump("mixT_d", XT, [128, 8, 2048], [("XT", t) for t in range(NT)])

            if stop == 'D' and l == nl - 1:
                raise _Stop()
            sc.barrier()
            R_D.reset()
            qf = R_D.get(8192, BF16, "p (c n) -> p c n", c=2)
            kf = R_D.get(8192, BF16, "p (c n) -> p c n", c=2)
            kd_f = R_D.get(8192, BF16, "p (t n) -> p t n", t=16)
            Sbf = R_D.get(16384, BF16, "p (d q t e) -> p d q t e", d=2, q=2, t=16)
            stm2 = [R_D.get(4096, F32, "p (d q n) -> p d q n", d=2, q=2) for _ in range(2)]
            _p0 = R_D.pos
            sp_ = [R_D.get(2048, F32) for _ in range(2)]
            _p1 = R_D.pos
            ebt = [R_D.get(2 * 2 * 129 * 4, F32, "p (d q n) -> p d q n", d=2, q=2) for _ in range(2)]
            _p2 = R_D.pos
            enbt = [R_D.get(2 * 2 * 128 * 4, F32, "p (d q n) -> p d q n", d=2, q=2) for _ in range(2)]
            erem = [R_D.get(2048, F32) for _ in range(2)]
            dS = [R_D.get(1024, F32) for _ in range(4)]
            _pend = R_D.pos
            Am = [R_X.get(4 * 2 * 128 * 2, BF16, "p (h d n) -> p h d n", h=4, d=2) for _ in range(2)]
            R_D.pos = _p2
            g_y = [R_D.get(1024, BF16) for _ in range(2)]
            g_junk = R_D.get(512, F32)
            R_D.pos = _pend
            qb, kb, kd_b = gqT, gkT, gk_tok
            maskf = Uf[:, 0:128]
            maskb = Usf

            def tl(t):
                return slice(t * 128, (t + 1) * 128)

            def prep_A(t):
                b = t % 2
                sp = sp_[b]
                zb = 0 if b == 0 else 7
                sc.op("pe", lambda e: e.matmul(PS[:, zb, :], lhsT=G33[0:33, tl(t)], rhs=Wg[0:33, :], start=True, stop=True),
                      reads=["G33", "Wg"], writes=psk(zb))
                sc.op("act", lambda e: e.activation(out=sp, in_=PS[:, zb, :], func=AF.Exp, scale=-1.0), reads=psk(zb), writes=[("sp", b)])
                sc.op("act", lambda e: e.activation(out=sp, in_=sp, func=AF.Ln, bias=1.0, scale=1.0), reads=[("sp", b)], writes=[("sp", b)])

            prep_A(0)
            for t in range(NT):
                b = t % 2
                sp = sp_[b]
                if t + 1 < NT:
                    prep_A(t + 1)
                sc.op("pe", lambda e, sp=sp: e.matmul(PS[:, 1, 0:256], lhsT=Usf, rhs=sp[:, 0:256], start=True, stop=True), reads=[("sp", b), "Usf", "Usb"], writes=psk(1))
                sc.op("pe", lambda e, sp=sp: e.matmul(PS[:, 1, 256:512], lhsT=Usb, rhs=sp[:, 256:512], start=True, stop=True), reads=[("sp", b), "Usf", "Usb"], writes=psk(1))
                sc.op("act", lambda e, b=b: e.activation(out=erem[b], in_=PS[:, 1, :], func=AF.Exp, scale=-1.0 / 16), reads=psk(1), writes=[("erem", b)])
                sc.op("dve", lambda e, t=t, b=b: e.tensor_tensor(out=kd_f[:, t, :], in0=gk_tok[:, t, :], in1=erem[b][:, 0:256], op=ALU.mult),
                      reads=[("gk_tok", t), ("erem", b)], writes=[("kd_f", t)])
                sc.op("dve", lambda e, t=t, b=b: e.tensor_tensor(out=kd_b[:, t, :], in0=gk_tok[:, t, :], in1=erem[b][:, 256:512], op=ALU.mult),
                      reads=[("gk_tok", t), ("erem", b), ("kd_f", t)], writes=[("gk_tok", t)])
                for d in range(2):
                    U = Uf if d == 0 else Ub
                    for q in range(2):
                        sc.op("pe", lambda e, sp=sp, d=d, q=q, U=U: e.matmul(PS[:, 2 + d, q * 160:q * 160 + 129],
                                                                            lhsT=sp[:, d * 256 + q * 128:d * 256 + (q + 1) * 128], rhs=U, start=True, stop=True),
                              reads=[("sp", b), "Uf", "Ub"], writes=psk(2 + d))
                src4 = PS[:, 2:4, 0:320].rearrange("p a (q n) -> p a q n", q=2)
                sc.op("act", lambda e, b=b, src4=src4: e.activation(out=ebt[b], in_=src4[:, :, :, 0:129], func=AF.Exp, scale=-1.0 / 16),
                      reads=psk(2) + psk(3), writes=[("eb", b, 0), ("eb", b, 1)])
                sc.op("act", lambda e, b=b, src4=src4: e.activation(out=enbt[b], in_=src4[:, :, :, 0:128], func=AF.Exp, scale=1.0 / 16),
                      reads=psk(2) + psk(3), writes=[("enb", b, 0), ("enb", b, 1)])
                sc.op("dve", lambda e, t=t, b=b: e.tensor_tensor(out=qf[:, :, tl(t)], in0=gqT[:, :, tl(t)], in1=ebt[b][:, 0, :, 0:128], op=ALU.mult),
                      reads=[("gqT", t), ("eb", b, 0)], writes=[("qf", t)])
                sc.op("dve", lambda e, t=t, b=b: e.tensor_tensor(out=kf[:, :, tl(t)], in0=gkT[:, :, tl(t)], in1=enbt[b][:, 0, :, :], op=ALU.mult),
                      reads=[("gkT", t), ("enb", b, 0)], writes=[("kf", t)])
                sc.op("dve", lambda e, t=t, b=b: e.tensor_tensor(out=qb[:, :, tl(t)], in0=gqT[:, :, tl(t)], in1=ebt[b][:, 1, :, 0:128], op=ALU.mult),
                      reads=[("gqT", t), ("eb", b, 1), ("qf", t)], writes=[("gqT", t)])
                sc.op("dve", lambda e, t=t, b=b: e.tensor_tensor(out=kb[:, :, tl(t)], in0=gkT[:, :, tl(t)], in1=enbt[b][:, 1, :, :], op=ALU.mult),
                      reads=[("gkT", t), ("enb", b, 1), ("kf", t)], writes=[("gkT", t)])
                sc.op("dve", lambda e, t=t, b=b: e.tensor_copy(out=decs[:, :, :, t:t + 1], in_=ebt[b][:, :, :, 128:129]),
                      reads=[("eb", b, 0), ("eb", b, 1)], writes=[("decs", t)])
            dump("qf", qf, [128, 2, 2048], [("qf", t) for t in range(NT)])
            dump("kd_f", kd_f, [128, 16, 256], [("kd_f", t) for t in range(NT)])
            dump("decs", decs, [128, 2, 2, 16], [("decs", t) for t in range(NT)])

            if stop == 'G1' and l == nl - 1:
                raise _Stop()
            sc.op("dve", lambda e: e.memset(stm2[0], 0.0), writes=[("stm", 0, d, q) for d in range(2) for q in range(2)])
            chains = [(d, q) for d in range(2) for q in range(2)]
            par = {c: 0 for c in chains}
            for i in range(NT):
                todo = []
                for ci, (d, q) in enumerate(chains):
                    t = i if d == 0 else NT - 1 - i
                    cur = par[(d, q)]
                    if i > 0:
                        sc.op("act", lambda e, d=d, q=q, t=t, cur=cur: e.activation(out=Sbf[0:64, d, q, t, :], in_=stm2[cur][0:64, d, q, 0:128], func=AF.Copy),
                              reads=[("stm", cur, d, q)], writes=[("Sbf", d, q, t)])
                        sc.op("dve", lambda e, d=d, q=q, t=t, cur=cur: e.tensor_copy(out=Sbf[64:128, d, q, t, :], in_=stm2[cur][64:128, d, q, 128:256]),
                              reads=[("stm", cur, d, q)], writes=[("Sbf", d, q, t)])
                    if i == NT - 1:
                        continue
                    kd = kd_f if d == 0 else kd_b
                    kkey = "kd_f" if d == 0 else "gk_tok"
                    pslot = ci % 2
                    pk = psk(4 + pslot)
                    sc.op("pe", lambda e, kd=kd, t=t, q=q, pslot=pslot: e.matmul(PS[:, 4 + pslot, 0:256], lhsT=kd[:, t, q * 128:(q + 1) * 128],
                                                                                rhs=gv[:, t, q * 256:(q + 1) * 256], start=True, stop=True),
                          reads=[(kkey, t), ("gv", t)], writes=pk)
                    if os.environ.get("GSKIP") != "evac":
                        sc.op("act", lambda e, ci=ci, pslot=pslot: e.activation(out=dS[ci], in_=PS[:, 4 + pslot, 0:256], func=AF.Copy),
                              reads=pk, writes=[("dS", ci)])
                    todo.append((ci, d, q, t, cur))
                for (ci, d, q, t, cur) in todo:
                    if os.environ.get("GSKIP") == "upd":
                        par[(d, q)] = 1 - cur
                        continue
                    sc.op("dve", lambda e, ci=ci, d=d, q=q, t=t, cur=cur: e.scalar_tensor_tensor(
                        out=stm2[1 - cur][:, d, q, :], in0=stm2[cur][:, d, q, :], scalar=decs[:, d, q, t:t + 1], in1=dS[ci],
                        op0=ALU.mult, op1=ALU.add), reads=[("stm", cur, d, q), ("decs", t), ("dS", ci)], writes=[("stm", 1 - cur, d, q)])
                    par[(d, q)] = 1 - cur
            dump("Sbf", Sbf, [128, 2, 2, 16, 128], [("Sbf", d, q, t) for d in range(2) for q in range(2) for t in range(NT)])
            if stop == 'G2' and l == nl - 1:
                raise _Stop()

            def g_A(t):
                b = t % 2
                A = Am[b]
                for half in range(2):
                    sbank = (4 + half) if os.environ.get('GBANK') else (2 * b + half)
                    items = []
                    for sq in range(4):
                        h, d = half + 2 * (sq // 2), sq % 2
                        q = h // 2
                        base = (h % 2) * 64
                        kk = kf if d == 0 else kb
                        qq = qf if d == 0 else qb
                        kkey = ("kf", t) if d == 0 else ("gkT", t)
                        qkey = ("qf", t) if d == 0 else ("gqT", t)
                        sc.op("pe", lambda e, kk=kk, qq=qq, q=q, base=base, sbank=sbank, sq=sq: e.matmul(
                            PS[:, sbank, sq * 128:(sq + 1) * 128], lhsT=kk[base:base + 64, q, tl(t)], rhs=qq[base:base + 64, q, tl(t)], start=True, stop=True),
                            reads=[kkey, qkey], writes=psk(sbank))
                        items.append((sq, h, d))
                        if os.environ.get("GOLD"):
                            mk = maskf if d == 0 else maskb
                            sc.op("dve", lambda e, A=A, h=h, d=d, sbank=sbank, sq=sq, mk=mk: e.tensor_tensor(
                                out=A[:, h, d, :], in0=PS[:, sbank, sq * 128:(sq + 1) * 128], in1=mk, op=ALU.mult),
                                reads=psk(sbank) + ["Uf", "Usf"], writes=[("A", b, h, d)])
                    if os.environ.get("GOLD"):
                        continue
                    for (sq, h, d) in items:
                        mk = maskf if d == 0 else maskb
                        sc.op("dve", lambda e, A=A, h=h, d=d, sbank=sbank, sq=sq, mk=mk: e.tensor_tensor(
                            out=A[:, h, d, :], in0=PS[:, sbank, sq * 128:(sq + 1) * 128], in1=mk, op=ALU.mult),
                            reads=psk(sbank) + ["Uf", "Usf"], writes=[("A", b, h, d)])

            def g_B(t):
                b = t % 2
                A = Am[b]
                obank = (0 + b) if os.environ.get('GBANK') else (4 + b)
                for h in range(4):
                    q = h // 2
                    base = (h % 2) * 64
                    oh_ = PS[:, obank, h * 128:(h + 1) * 128]
                    ok = psk(obank)
                    inter_f = t > 0
                    inter_b = t < NT - 1
                    sc.op("pe", lambda e, A=A, h=h, oh_=oh_: e.matmul(oh_, lhsT=A[:, h, 0, :], rhs=gv[:, t, h * 128:(h + 1) * 128], start=True, stop=False),
                          reads=[("A", b, h, 0), ("gv", t)], writes=ok)
                    sc.op("pe", lambda e, A=A, h=h, oh_=oh_, fin=(not inter_f and not inter_b): e.matmul(
                        oh_, lhsT=A[:, h, 1, :], rhs=gv[:, t, h * 128:(h + 1) * 128], start=False, stop=fin),
                        reads=[("A", b, h, 1), ("gv", t)], writes=ok)
                    if inter_f:
                        sc.op("pe", lambda e, q=q, base=base, oh_=oh_, fin=(not inter_b): e.matmul(
                            oh_, lhsT=qf[base:base + 64, q, tl(t)], rhs=Sbf[base:base + 64, 0, q, t, :], start=False, stop=fin),
                            reads=[("qf", t), ("Sbf", 0, q, t)], writes=ok)
                    if inter_b:
                        sc.op("pe", lambda e, q=q, base=base, oh_=oh_: e.matmul(
                            oh_, lhsT=qb[base:base + 64, q, tl(t)], rhs=Sbf[base:base + 64, 1, q, t, :], start=False, stop=True),
                            reads=[("gqT", t), ("Sbf", 1, q, t)], writes=ok)

            def g_norm(t):
                b = t % 2
                sm = gsm[b]
                obank = (0 + b) if os.environ.get('GBANK') else (4 + b)
                okall = psk(obank)
                for h in range(4):
                    sc.op("act", lambda e, h=h, sm=sm: e.activation(out=g_junk, in_=PS[:, obank, h * 128:(h + 1) * 128], func=AF.Square, accum_out=sm[:, h:h + 1]),
                          reads=okall, writes=[("gss", b, h)] + ([("enb", 1, 0), ("enb", 1, 1)] if h == 0 else []))
                sc.op("act", lambda e, sm=sm: e.activation(out=sm[:, 4:8], in_=sm[:, 0:4], func=AF.Sqrt, bias=1e-5, scale=1.0 / 128),
                      reads=[("gss", b, h) for h in range(4)], writes=[("grs", b)])
                sc.op("dve", lambda e, sm=sm: e.reciprocal(out=sm[:, 4:8], in_=sm[:, 4:8]), reads=[("grs", b)], writes=[("grs", b)])
                for h in range(4):
                    sc.op("dve", lambda e, h=h, sm=sm, b=b: e.scalar_tensor_tensor(
                        out=g_y[b][:, h * 128:(h + 1) * 128], in0=PS[:, obank, h * 128:(h + 1) * 128], scalar=sm[:, 4 + h:5 + h],
                        in1=gr_s[:, t, h * 128:(h + 1) * 128], op0=ALU.mult, op1=ALU.mult),
                        reads=psk(obank) + [("grs", b), ("gr_s", t)], writes=[("gy", b, h)] + ([("enb", 0, 0), ("enb", 0, 1)] if h == 0 else []))

            def g_tr(t):
                b = t % 2
                tk = psk(6 + b)
                for h in range(4):
                    sc.op("pe", lambda e, h=h, b=b: e.transpose(out=PSb[:, 6 + b, h * 128:(h + 1) * 128], in_=g_y[b][:, h * 128:(h + 1) * 128], identity=identb),
                          reads=[("gy", b, h), "identb"], writes=tk)
                sc.op("act", lambda e, b=b: e.activation(out=XT[:, 4:8, tl(t)], in_=PSb[:, 6 + b, 0:512].rearrange("p (c n) -> p c n", c=4), func=AF.Copy),
                      reads=tk, writes=[("XT", t)])

            g_A(0)
            for t in range(NT):
                if t + 1 < NT:
                    g_A(t + 1)
                if os.environ.get("GSKIP") == "B":
                    continue
                g_B(t)
                if os.environ.get("GSKIP") == "norm":
                    continue
                g_norm(t)
                if os.environ.get("GSKIP") == "tr":
                    continue
                if t > 0:
                    g_tr(t - 1)
            if not os.environ.get("GSKIP"):
                g_tr(NT - 1)
            dump("mixT", XT, [128, 8, 2048], [("XT", t) for t in range(NT)])

            if stop == 'G' and l == nl - 1:
                raise _Stop()
            sc.barrier()
            load_ln_params(ln1g_d[l], ln1b_d[l])
            R_D.reset()
            W1B = [R_D.get(8192, BF16, "p (c n) -> p c n", c=8) for _ in range(2)]
            W2B = [R_D.get(8192, BF16, "p (c n) -> p c n", c=4) for _ in range(2)]
            hT = R_D.get(16384, BF16, "p (c n) -> p c n", c=4)
            relu_t = [R_D.get(2048, F32) for _ in range(2)]

            def load_ffn_block(fb, buf):
                s1 = w1_d[l, :, fb * 512:(fb + 1) * 512].rearrange("(c p) n -> p c n", p=128)
                s2 = w2_d[l, fb * 512:(fb + 1) * 512, :].rearrange("(c p) n -> p c n", p=128)
                for hf in range(2):
                    sc.dma("pool", lambda e, hf=hf: e.dma_start(out=W1B[buf][:, hf * 4:(hf + 1) * 4, :], in_=s1[:, hf * 4:(hf + 1) * 4, :]),
                           writes=[("W1B", buf, hf)])
                for hf in range(2):
                    sc.dma("pool", lambda e, hf=hf: e.dma_start(out=W2B[buf][:, hf * 2:(hf + 1) * 2, :], in_=s2[:, hf * 2:(hf + 1) * 2, :]),
                           writes=[("W2B", buf, hf)])

            load_ffn_block(0, 0)
            load_ffn_block(1, 1)
            for t in range(NT):
                sc.dma("sp", lambda e, t=t: e.dma_start(out=X[:, t, :], in_=xs_d[t * 128:(t + 1) * 128, :]), reads=[("xsd", t)], writes=[("X", t)])
            def o_mm(t):
                yb = (t % 3) * 2
                for hf in range(2):
                    for c in range(8):
                        sc.op("pe", lambda e, c=c, hf=hf, t=t, yb=yb: e.matmul(PS[:, yb + hf, :], lhsT=XT[:, c, tl(t)], rhs=WO[:, c, hf * 512:(hf + 1) * 512],
                                                                              start=(c == 0), stop=(c == 7)),
                              reads=[("XT", t), ("RW", 0, 0), ("RW", 0, 1), ("RW", 1, 0), ("RW", 1, 1)], writes=psk(yb + hf))

            def o_ln(t):
                yb = (t % 3) * 2
                sc.op("dve", lambda e, t=t, yb=yb: e.scalar_tensor_tensor(out=X[:, t, :], in0=X[:, t, :], scalar=ALPHA,
                                                                          in1=PS[:, yb:yb + 2, :].rearrange("p a n -> p (a n)"), op0=ALU.mult, op1=ALU.add),
                      reads=[("X", t)] + psk(yb) + psk(yb + 1), writes=[("X", t)])
                ln_a(t)

            for t0 in range(3):
                o_mm(t0)
                o_ln(t0)
            for t in range(NT):
                if t + 3 < NT:
                    o_mm(t + 3)
                ln_b1(t, None)
                if t + 3 < NT:
                    o_ln(t + 3)
                ln_b2(t, None)
            dump("x1T", XT, [128, 8, 2048], [("XT", t) for t in range(NT)])

            if stop == 'O' and l == nl - 1:
                raise _Stop()
            if not last:
                load_win_block(l + 1, 0, 0)
                load_win_block(l + 1, 1, 1)
            hrr = [0]
            for fb in range(8):
                buf = fb % 2
                for r in range(4):
                    for fc in range(4):
                        bank = 4 + hrr[0] % 3
                        rb = hrr[0] % 2
                        hrr[0] += 1
                        for c in range(8):
                            sc.op("pe", lambda e, c=c, fc=fc, r=r, bank=bank, buf=buf: e.matmul(
                                PS[:, bank, :], lhsT=W1B[buf][:, c, fc * 128:(fc + 1) * 128], rhs=XT[:, c, r * 512:(r + 1) * 512],
                                start=(c == 0), stop=(c == 7)), reads=[("W1B", buf, 0), ("W1B", buf, 1)] + xt_all[r * 4:(r + 1) * 4], writes=psk(bank))
                        fcol = fb * 4 + fc
                        sc.op("act", lambda e, bank=bank, rb=rb, fcol=fcol: e.activation(out=relu_t[rb], in_=PS[:, bank, :], func=AF.Relu,
                                                                                         bias=b1c[:, fcol:fcol + 1], scale=1.0),
                              reads=psk(bank) + ["b1c"], writes=[("relu", rb)])
                        sc.op("dve", lambda e, rb=rb, fc=fc, r=r: e.tensor_tensor(out=hT[:, fc, r * 512:(r + 1) * 512], in0=relu_t[rb], in1=relu_t[rb], op=ALU.mult),
                              reads=[("relu", rb)], writes=[("hT", fc, r)])
                for t in range(NT):
                    yb = (t % 2) * 2
                    for hf in range(2):
                        for fc in range(4):
                            sc.op("pe", lambda e, fc=fc, hf=hf, t=t, yb=yb, buf=buf: e.matmul(
                                PS[:, yb + hf, :], lhsT=hT[:, fc, tl(t)], rhs=W2B[buf][:, fc, hf * 512:(hf + 1) * 512],
                                start=(fc == 0), stop=(fc == 3)), reads=[("hT", fc, t // 4), ("W2B", buf, 0), ("W2B", buf, 1)], writes=psk(yb + hf))
                    if fb == 0:
                        sc.op("dve", lambda e, t=t, yb=yb: e.scalar_tensor_tensor(out=X[:, t, :], in0=X[:, t, :], scalar=ALPHA,
                                                                                  in1=PS[:, yb:yb + 2, :].rearrange("p a n -> p (a n)"), op0=ALU.mult, op1=ALU.add),
                              reads=[("X", t)] + psk(yb) + psk(yb + 1), writes=[("X", t)])
                        sc.op("pool", lambda e, t=t: e.tensor_tensor(out=X[:, t, :], in0=X[:, t, :], in1=b2t, op=ALU.add), reads=[("X", t), "b2t"], writes=[("X", t)])
                    else:
                        sc.op("dve", lambda e, t=t, yb=yb: e.tensor_tensor(out=X[:, t, :], in0=X[:, t, :], in1=PS[:, yb:yb + 2, :].rearrange("p a n -> p (a n)"), op=ALU.add),
                              reads=[("X", t)] + psk(yb) + psk(yb + 1), writes=[("X", t)])
                if fb + 2 < 8:
                    load_ffn_block(fb + 2, buf)
            if stop == 'F' and l == nl - 1:
                raise _Stop()
            load_ln_params(ln2g_d[l], ln2b_d[l])
            ln_all(out_d if last else xs_d)
            sc.barrier()


        for _l in range(nl):
            do_layer(_l)
    except _Stop:
        pass
    out_dmas = [o for o in sc.ops if o.is_dma and o.dkey in [("xs", i) for i in range(4)]]
    fin = {}
    for o in out_dmas:
        fin[o.dkey] = o
    finals = list(fin.values()) + list(dbg_out.values())
    sc.emit(final_wait_ops=finals)
    es.close()
    return nc, sc


_CONST = None


def kernel(**inputs):
    global _CONST
    if _CONST is None:
        _CONST = _constants()
    nc, _ = build(2)
    x = np.ascontiguousarray(inputs["x"], dtype=np.float32)
    shared = {k: np.ascontiguousarray(v, dtype=np.float32) for k, v in inputs.items() if k != "x"}
    shared.update(_CONST)
    in_maps = []
    for b in range(8):
        m = dict(shared)
        m["x"] = x[b]
        in_maps.append(m)
    res = run_bass_kernel_spmd(nc, in_maps, core_ids=list(range(8)))
    return np.stack([r["out"] for r in res.results], axis=0).astype(np.float32)
```

```python
import math
import os
from contextlib import ExitStack

import numpy as np
import concourse.bass as bass
import concourse.mybir as mybir
from concourse.bass_utils import run_bass_kernel_spmd

F32 = mybir.dt.float32
BF16 = mybir.dt.bfloat16
AF = mybir.ActivationFunctionType
ALU = mybir.AluOpType

S = 2048
D = 1024
DIN = 3104
DFF = 4096
NT = 16
ALPHA = (2.0 * 2) ** 0.25
ENGS = ("pe", "act", "dve", "pool", "sp")
EPOCH = 30000


class _Res:
    __slots__ = ("last_w", "readers")

    def __init__(self):
        self.last_w = None
        self.readers = []


class _Op:
    __slots__ = ("eng", "fn", "deps", "signal", "tok", "is_dma", "dkey")

    def __init__(self, eng, fn, is_dma, dkey):
        self.eng = eng
        self.fn = fn
        self.deps = []
        self.signal = False
        self.tok = None
        self.is_dma = is_dma
        self.dkey = dkey


class Sched:
    def __init__(self, nc):
        self.nc = nc
        self.ops = []
        self.res = {}
        self.pending = {e: [] for e in ENGS}

    def _r(self, key):
        x = self.res.get(key)
        if x is None:
            x = self.res[key] = _Res()
        return x

    def _add(self, op, reads, writes):
        deps = set()
        for k in reads:
            rs = self._r(k)
            if rs.last_w is not None:
                deps.add(rs.last_w)
        for k in writes:
            rs = self._r(k)
            if rs.last_w is not None:
                deps.add(rs.last_w)
            deps.update(rs.readers)
        for k in reads:
            self._r(k).readers.append(op)
        for k in writes:
            rs = self._r(k)
            rs.last_w = op
            rs.readers = []
        if self.pending[op.eng]:
            deps.update(self.pending[op.eng])
            self.pending[op.eng] = []
        deps.discard(op)
        op.deps = list(deps)
        self.ops.append(op)
        return op

    def op(self, eng, fn, reads=(), writes=()):
        return self._add(_Op(eng, fn, False, None), reads, writes)

    def dma(self, eng, fn, dkey=None, reads=(), writes=()):
        if dkey is None:
            dkey = ("w", writes[0])
        return self._add(_Op(eng, fn, True, dkey), reads, writes)

    def barrier(self):
        last = {}
        for o in self.ops:
            last[(o.eng, o.dkey) if o.is_dma else o.eng] = o
        b = list(last.values())
        self.pending = {e: list(b) for e in ENGS}

    def emit(self, final_wait_ops=()):
        nc = self.nc
        ops = self.ops
        for o in ops:
            for d in o.deps:
                if d.is_dma:
                    d.signal = True
                elif d.eng == "pe" and o.eng == "pe" and not o.is_dma:
                    continue
                else:
                    d.signal = True
        with ExitStack() as es:
            eng_sems = {e: [] for e in ENGS}
            cnt = {e: 0 for e in ENGS}
            dma_sems = {}
            dma_cnt = {}
            for o in ops:
                if o.is_dma:
                    if o.dkey not in dma_sems:
                        dma_sems[o.dkey] = es.enter_context(nc.semaphore("d%d" % len(dma_sems)))
                        dma_cnt[o.dkey] = 0
                    dma_cnt[o.dkey] += 16
                    o.tok = (dma_sems[o.dkey], dma_cnt[o.dkey])
                elif o.signal:
                    ep = cnt[o.eng] // EPOCH
                    if ep >= len(eng_sems[o.eng]):
                        eng_sems[o.eng].append(es.enter_context(nc.semaphore("e_%s_%d" % (o.eng, ep))))
                    cnt[o.eng] += 1
                    o.tok = (eng_sems[o.eng][ep], cnt[o.eng] - ep * EPOCH)
            per_eng = {e: [o for o in ops if o.eng == e] for e in ENGS}
            self.stats = {e: len(per_eng[e]) for e in ENGS}
            self.stats["sems"] = sum(len(v) for v in eng_sems.values()) + len(dma_sems)

            def run(e, eng):
                waited = {}
                for o in per_eng[e]:
                    need = {}
                    for d in o.deps:
                        if d.tok is None:
                            continue
                        if (not d.is_dma) and d.eng == "pe" and e == "pe" and not o.is_dma:
                            continue
                        s, v = d.tok
                        k = id(s)
                        if waited.get(k, 0) >= v:
                            continue
                        if k not in need or need[k][1] < v:
                            need[k] = (s, v)
                    for k, (s, v) in need.items():
                        eng.wait_ge(s, v)
                        waited[k] = v
                    ins = o.fn(eng)
                    if o.tok is not None:
                        ins.then_inc(o.tok[0], 16 if o.is_dma else 1)
                if e == "sp":
                    for o in final_wait_ops:
                        s, v = o.tok
                        eng.wait_ge(s, v)

            with nc.Block() as block:
                @block.sync
                def _(eng):
                    run("sp", eng)

                @block.tensor
                def _(eng):
                    run("pe", eng)

                @block.scalar
                def _(eng):
                    run("act", eng)

                @block.vector
                def _(eng):
                    run("dve", eng)

                @block.gpsimd
                def _(eng):
                    run("pool", eng)


def _t5_bucket(rel):
    nb = 16
    me = 8
    ret = np.where(rel > 0, nb, 0)
    n = np.abs(rel)
    large = me + (np.log(np.maximum(n, 1).astype(np.float32) / np.float32(me))
                  / np.float32(math.log(128 / me)) * np.float32(nb - me)).astype(np.int32)
    large = np.minimum(large, nb - 1)
    return ret + np.where(n < me, n, large)


MLEN = 1280


def _constants():
    c = {}
    c["c_ident"] = np.eye(128, dtype=np.float32)
    c["c_J"] = np.eye(128, dtype=np.float32)[::-1].copy()
    s = np.arange(128)[:, None]
    t = np.arange(128)[None, :]
    uf = np.zeros((128, 129), np.float32)
    uf[:, :128] = (s <= t)
    uf[:, 128] = 1.0
    ub = np.zeros((128, 129), np.float32)
    ub[:, :128] = (s >= t)
    ub[:, 128] = 1.0
    c["c_uf"] = uf
    c["c_ub"] = ub
    c["c_sf"] = (s > t).astype(np.float32)
    c["c_sb"] = (s < t).astype(np.float32)
    n = np.arange(MLEN)
    bk = _t5_bucket(639 - n)
    oh = np.zeros((32, MLEN), np.float32)
    oh[bk, n] = 1.0
    c["c_onehot"] = oh
    return c


class _Stop(Exception):
    pass


def build(nl=2, dbg=(), stop=None):
    nc = bass.Bass("TRN2", target_bir_lowering=False)

    def din(name, shape):
        return nc.dram_tensor(name, list(shape), F32, kind="ExternalInput").ap()

    x_d = din("x", [S, D])
    lnemb_g = din("ln_emb_g", [D])
    lnemb_b = din("ln_emb_b", [D])
    table_d = din("rel_bias_table", [32, 4])
    w_in_d = din("w_in", [2, D, DIN])
    lq1_d = din("lambda_q1", [2, 64])
    lk1_d = din("lambda_k1", [2, 64])
    lq2_d = din("lambda_q2", [2, 64])
    lk2_d = din("lambda_k2", [2, 64])
    dnw_d = din("diff_norm_w", [2, 128])
    gup_d = din("gla_gate_up", [2, 2, 16, 256])
    gbias_d = din("gla_gate_bias", [2, 2, 256])
    gnw_d = din("gla_norm_w", [2, 128])
    w_o_d = din("w_o", [2, D, D])
    ln1g_d = din("ln1_g", [2, D])
    ln1b_d = din("ln1_b", [2, D])
    w1_d = din("w_ffn1", [2, D, DFF])
    b1_d = din("b_ffn1", [2, DFF])
    w2_d = din("w_ffn2", [2, DFF, D])
    b2_d = din("b_ffn2", [2, D])
    ln2g_d = din("ln2_g", [2, D])
    ln2b_d = din("ln2_b", [2, D])
    c_ident = din("c_ident", [128, 128])
    c_J = din("c_J", [128, 128])
    c_uf = din("c_uf", [128, 129])
    c_ub = din("c_ub", [128, 129])
    c_sf = din("c_sf", [128, 128])
    c_sb = din("c_sb", [128, 128])
    c_onehot = din("c_onehot", [32, MLEN])
    out_d = nc.dram_tensor("out", [S, D], F32, kind="ExternalOutput").ap()
    xs_d = nc.dram_tensor("xs_scratch", [S, D], F32).ap()
    md_t = nc.dram_tensor("md_scratch", [4, MLEN], F32)
    eb_d = nc.dram_tensor("expb_scratch", [128, 4 * 1152], BF16).ap()
    md_d = md_t.ap()
    dbg_out = {}

    sc = Sched(nc)
    es = ExitStack()
    ARENA_BYTES = 207 * 1024
    arena = es.enter_context(nc.sbuf_tensor("arena", [128, ARENA_BYTES // 2], BF16))
    PSb = es.enter_context(nc.psum_tensor("ps", [128, 8, 1024], BF16))[:]
    PS = PSb.bitcast(F32)

    def view(off, nbytes, dt, pattern=None, **kw):
        assert off % 32 == 0, off
        a = arena[:, off // 2:(off + nbytes) // 2]
        if dt is F32:
            a = a.bitcast(F32)
        if pattern:
            a = a.rearrange(pattern, **kw)
        return a

    class Alloc:
        def __init__(self, base, size):
            self.base = base
            self.size = size
            self.pos = 0

        def reset(self):
            self.pos = 0

        def get(self, nbytes, dt, pattern=None, **kw):
            n = (nbytes + 31) // 32 * 32
            assert self.pos + n <= self.size, (self.pos, n, self.size)
            v = view(self.base + self.pos, nbytes, dt, pattern, **kw)
            self.pos += n
            return v

    R_XT = Alloc(0, 32768)
    R_X = Alloc(32768, 65536)
    R_D = Alloc(98304, 70656)
    R_W = Alloc(168960, 16384)
    R_C = Alloc(185344, ARENA_BYTES - 185344)

    XT = R_XT.get(32768, BF16, "p (c n) -> p c n", c=8)
    X = R_X.get(65536, F32, "p (t n) -> p t n", t=NT)
    WB = [R_W.get(8192, BF16, "p (c n) -> p c n", c=8) for _ in range(2)]
    R_W.reset()
    WO = R_W.get(16384, BF16, "p (c n) -> p c n", c=8)

    identb = R_C.get(256, BF16)
    Jb = R_C.get(256, BF16)
    Uf = R_C.get(516, F32)
    Ub = R_C.get(516, F32)
    Usf = R_C.get(512, F32)
    Usb = R_C.get(512, F32)
    gt = R_C.get(4096, F32)
    bt = R_C.get(4096, F32)
    b2t = R_C.get(4096, F32)
    wd_t = R_C.get(512, F32)
    wg_t = R_C.get(512, F32)
    b1c = R_C.get(128, F32)
    cb = R_C.get(32, F32, "p (s h) -> p s h", s=2)
    lamv = R_C.get(4 * 64 * 4, F32, "p (a n) -> p a n", a=4)
    lamp = R_C.get(2 * 64 * 4, F32, "p (a n) -> p a n", a=2)
    lams = R_C.get(32, F32)
    Wg = R_C.get(1024, BF16)
    st_ = [R_C.get(48, F32) for _ in range(2)]
    mv_ = [R_C.get(8, F32) for _ in range(2)]
    rs_ = [R_C.get(4, F32) for _ in range(2)]
    xb_ = [R_C.get(2048, BF16) for _ in range(2)]
    dsm = [R_C.get(64, F32) for _ in range(2)]
    gsm = [R_C.get(64, F32) for _ in range(2)]
    decs = R_C.get(2 * 2 * 16 * 4, F32, "p (d q t) -> p d q t", d=2, q=2)

    def psk(b):
        return [("ps", b, q) for q in range(4)]

    def bc_mid(ap2, n):
        a = ap2.ap
        return bass.AP(ap2.tensor, ap2.offset, [list(a[0]), [0, n], list(a[1])])

    def bc_last(ap2, n):
        a = ap2.ap
        return bass.AP(ap2.tensor, ap2.offset, [list(a[0]), list(a[1]), [0, n]])

    cur_layer = [-1]

    def dump(name, ap, shape, reads):
        nm = "%s@%d" % (name, cur_layer[0])
        if nm in dbg:
            name = nm
        elif name not in dbg or (cur_layer[0] >= 0 and cur_layer[0] != nl - 1):
            return
        t = nc.dram_tensor("dbg_" + name.replace("@", "_"), list(shape), ap.dtype, kind="ExternalOutput").ap()
        dbg_out[name] = sc.dma("sp", lambda e: e.dma_start(out=t, in_=ap), "dbg", reads=reads)

    try:
        sc.dma("pool", lambda e: e.dma_start(out=identb, in_=c_ident), writes=["identb"])
        sc.dma("pool", lambda e: e.dma_start(out=Jb, in_=c_J), writes=["Jb"])
        sc.dma("sp", lambda e: e.dma_start(out=Uf, in_=c_uf), writes=["Uf"])
        sc.dma("sp", lambda e: e.dma_start(out=Ub, in_=c_ub), writes=["Ub"])
        sc.dma("sp", lambda e: e.dma_start(out=Usf, in_=c_sf), writes=["Usf"])
        sc.dma("sp", lambda e: e.dma_start(out=Usb, in_=c_sb), writes=["Usb"])
        for si, row in enumerate((15, 31)):
            sc.dma("sp", lambda e, si=si, row=row: e.dma_start(out=cb[:, si, :], in_=table_d[row, :].partition_broadcast(128)),
                   writes=[("cb", si)])

        R_D.reset()
        tb = R_D.get(16, F32)
        oh = R_D.get(MLEN * 4, F32)
        msb = R_D.get(MLEN * 4, F32)
        sc.dma("sp", lambda e: e.dma_start(out=tb[0:32, :], in_=table_d), writes=["tb"])
        sc.dma("sp", lambda e: e.dma_start(out=oh[0:32, :], in_=c_onehot), writes=["oh"])
        sc.dma("sp", lambda e: e.dma_start(out=gt, in_=lnemb_g.partition_broadcast(128)), writes=["gt"])
        sc.dma("sp", lambda e: e.dma_start(out=bt, in_=lnemb_b.partition_broadcast(128)), writes=["bt"])
        for t in range(NT):
            sc.dma("sp", lambda e, t=t: e.dma_start(out=X[:, t, :], in_=x_d[t * 128:(t + 1) * 128, :]), writes=[("X", t)])
        for ci, (c0, cn) in enumerate(((0, 512), (512, 512), (1024, 256))):
            sc.op("pe", lambda e, ci=ci, c0=c0, cn=cn: e.matmul(PS[0:4, ci, 0:cn], lhsT=tb[0:32, :], rhs=oh[0:32, c0:c0 + cn], start=True, stop=True),
                  reads=["tb", "oh"], writes=psk(ci))
            sc.op("dve", lambda e, ci=ci, c0=c0, cn=cn: e.tensor_copy(out=msb[0:4, c0:c0 + cn], in_=PS[0:4, ci, 0:cn]),
                  reads=psk(ci), writes=["msb"])
        sc.dma("sp", lambda e: e.dma_start(out=md_d, in_=msb[0:4, :]), reads=["msb"], writes=["md"])
        R_D.reset()
        R_D.get(16384, BF16); R_D.get(16384, BF16); R_D.get(16 * 4 * 129 * 2, BF16)
        expB0 = R_D.get(4 * 1152 * 2, BF16, "p (h n) -> p h n", h=4)
        R_D.get(4096, BF16); R_D.get(1024, BF16)
        tmp_revs = [view(R_D.base + 16384 + i * 2304, 2304, BF16) for i in range(4)]
        for h in range(4):
            src = bass.AP(md_t, h * MLEN, [[1, 128], [1, 1152]])
            sc.dma("pool", lambda e, src=src, h=h: e.dma_start(out=tmp_revs[h], in_=src), reads=["md"], writes=[("tmp_rev", h)])
        for h in range(4):
            for ci, (c0, cn) in enumerate(((0, 512), (512, 512), (1024, 128))):
                sc.op("pe", lambda e, ci=ci, c0=c0, cn=cn, h=h: e.matmul(PS[:, ci, 0:cn], lhsT=Jb, rhs=tmp_revs[h][:, c0:c0 + cn], start=True, stop=True),
                      reads=["Jb", ("tmp_rev", h)], writes=psk(ci))
                sc.op("act", lambda e, h=h, ci=ci, c0=c0, cn=cn: e.activation(out=expB0[:, h, c0:c0 + cn], in_=PS[:, ci, 0:cn], func=AF.Exp),
                      reads=psk(ci), writes=[("expB", h)])
        sc.dma("sp", lambda e: e.dma_start(out=eb_d, in_=expB0.rearrange("p h n -> p (h n)")), reads=[("expB", h) for h in range(4)], writes=["eb_d"])

        def ln_a_stages(t):
            Xt = X[:, t, :]
            kx = ("X", t)
            b = t % 2
            st, mv, rs = st_[b], mv_[b], rs_[b]
            return [
                lambda: sc.op("dve", lambda e: e.bn_stats(out=st[:, 0:6], in_=Xt[:, 0:512]), reads=[kx], writes=[("st", b, 0)]),
                lambda: sc.op("dve", lambda e: e.bn_stats(out=st[:, 6:12], in_=Xt[:, 512:1024]), reads=[kx], writes=[("st", b, 1)]),
                lambda: sc.op("dve", lambda e: e.bn_aggr(out=mv, in_=st), reads=[("st", b, 0), ("st", b, 1)], writes=[("mv", b)]),
                lambda: sc.op("act", lambda e: e.activation(out=rs, in_=mv[:, 1:2], func=AF.Sqrt, bias=1e-5, scale=1.0), reads=[("mv", b)], writes=[("rs", b)]),
                lambda: sc.op("dve", lambda e: e.reciprocal(out=rs, in_=rs), reads=[("rs", b)], writes=[("rs", b)]),
                lambda: sc.op("dve", lambda e: e.tensor_scalar(out=Xt, in0=Xt, scalar1=mv[:, 0:1], scalar2=rs, op0=ALU.subtract, op1=ALU.mult),
                              reads=[kx, ("mv", b), ("rs", b)], writes=[kx]),
                lambda: sc.op("dve", lambda e: e.tensor_tensor(out=Xt, in0=Xt, in1=gt, op=ALU.mult), reads=[kx, "gt"], writes=[kx]),
                lambda: sc.op("pool", lambda e: e.tensor_tensor(out=Xt, in0=Xt, in1=bt, op=ALU.add), reads=[kx, "bt"], writes=[kx]),
            ]

        def ln_a(t):
            for f in ln_a_stages(t):
                f()

        def ln_a_pair(t0, t1):
            sa, sb = ln_a_stages(t0), ln_a_stages(t1)
            for fa, fb in zip(sa, sb):
                fa()
                fb()

        def ln_b1(t, spill_to):
            Xt = X[:, t, :]
            kx = ("X", t)
            b = t % 2
            xb = xb_[b]
            if spill_to is not None:
                sc.dma("sp", lambda e: e.dma_start(out=spill_to[t * 128:(t + 1) * 128, :], in_=Xt), ("xs", t % 4), reads=[kx], writes=[("xsd", t)])
            if spill_to is out_d:
                return
            sc.op("act", lambda e: e.activation(out=xb, in_=Xt, func=AF.Copy), reads=[kx], writes=[("xb", b)])

        def ln_b2(t, spill_to):
            if spill_to is out_d:
                return
            b = t % 2
            xb = xb_[b]
            bank = 6 + b
            for c in range(8):
                sc.op("pe", lambda e, c=c: e.transpose(out=PSb[:, bank, c * 128:(c + 1) * 128], in_=xb[:, c * 128:(c + 1) * 128], identity=identb),
                      reads=[("xb", b), "identb"], writes=psk(bank))
            sc.op("act", lambda e: e.activation(out=XT[:, :, t * 128:(t + 1) * 128], in_=PSb[:, bank, :].rearrange("p (c n) -> p c n", c=8), func=AF.Copy),
                  reads=psk(bank), writes=[("XT", t)])

        def ln_all(spill_to):
            ln_a_pair(0, 1)
            for t in range(0, NT, 2):
                if t + 2 < NT:
                    ln_a_pair(t + 2, t + 3)
                ln_b1(t, spill_to)
                ln_b1(t + 1, spill_to)
                ln_b2(t, spill_to)
                ln_b2(t + 1, spill_to)

        def load_ln_params(g_ap, b_ap):
            sc.dma("sp", lambda e: e.dma_start(out=gt, in_=g_ap.partition_broadcast(128)), writes=["gt"])
            sc.dma("sp", lambda e: e.dma_start(out=bt, in_=b_ap.partition_broadcast(128)), writes=["bt"])

        def load_win_block(l, blk, buf):
            c0 = blk * 512
            ncol = min(512, DIN - c0)
            src = w_in_d[l, :, c0:c0 + ncol].rearrange("(c p) n -> p c n", p=128)
            for hf in range(2):
                sc.dma("pool", lambda e, hf=hf: e.dma_start(out=WB[buf][:, hf * 4:(hf + 1) * 4, 0:ncol], in_=src[:, hf * 4:(hf + 1) * 4, :]),
                       writes=[("RW", buf, hf)])

        if stop == 'init':
            raise _Stop()
        load_win_block(0, 0, 0)
        load_win_block(0, 1, 1)
        ln_all(xs_d)
        dump("h0", X, [128, NT, 1024], [("X", t) for t in range(NT)])

        if stop == 'emb':
            raise _Stop()
        evac_rr = [0]

        def evac(out, in_, reads, writes, scale=None):
            evac_rr[0] ^= 1
            if evac_rr[0]:
                if scale is None:
                    sc.op("act", lambda e: e.activation(out=out, in_=in_, func=AF.Copy), reads=reads, writes=writes)
                else:
                    sc.op("act", lambda e: e.mul(out=out, in_=in_, mul=scale), reads=reads, writes=writes)
            else:
                if scale is None:
                    sc.op("dve", lambda e: e.tensor_copy(out=out, in_=in_), reads=reads, writes=writes)
                else:
                    sc.op("dve", lambda e: e.tensor_scalar(out=out, in0=in_, scalar1=scale, scalar2=None, op0=ALU.mult), reads=reads, writes=writes)

        def do_layer(l):
            lam_init = 0.8 - 0.6 * math.exp(-0.3 * l)
            cur_layer[0] = l
            if l == 0:
                sc.barrier()
            last = (l == nl - 1)
            R_D.reset()
            QT = R_D.get(16384, BF16, "p (h n) -> p h n", h=4)
            KT = R_D.get(16384, BF16, "p (h n) -> p h n", h=4)
            V = R_D.get(16 * 4 * 129 * 2, BF16, "p (t h e) -> p t h e", t=16, h=4)
            expB = R_D.get(4 * 1152 * 2, BF16, "p (h n) -> p h n", h=4)
            Eb = R_D.get(4096, BF16, "p (b m n) -> p b m n", b=2, m=2)
            d_y = R_D.get(4 * 128 * 2, BF16, "p (u n) -> p u n", u=4)
            _pu = R_D.pos
            silu_t = [R_D.get(2048, F32) for _ in range(2)]
            R_D.pos = _pu
            accS = R_D.get(8 * 129 * 4, F32, "p (a n) -> p a n", a=8)
            R_D.pos = _pu + 4608
            tmp_rev = R_D.get(1152 * 2, BF16)
            R_X.reset()
            gqT = R_X.get(8192, BF16, "p (c n) -> p c n", c=2)
            gkT = R_X.get(8192, BF16, "p (c n) -> p c n", c=2)
            gk_tok = R_X.get(8192, BF16, "p (t n) -> p t n", t=16)
            gv = R_X.get(16384, BF16, "p (t n) -> p t n", t=16)
            gr_s = R_X.get(16384, BF16, "p (t n) -> p t n", t=16)
            G33 = R_X.get(4096, BF16)

            for i, ap in enumerate((lq1_d, lk1_d, lq2_d, lk2_d)):
                sc.dma("sp", lambda e, i=i, ap=ap: e.dma_start(out=lamv[:, i, :], in_=ap[l, :].partition_broadcast(128)), writes=[("lamv", i)])
            sc.op("dve", lambda e: e.tensor_tensor(out=lamp[:, 0, :], in0=lamv[:, 0, :], in1=lamv[:, 1, :], op=ALU.mult), reads=[("lamv", 0), ("lamv", 1)], writes=["lamp"])
            sc.op("dve", lambda e: e.tensor_tensor(out=lamp[:, 1, :], in0=lamv[:, 2, :], in1=lamv[:, 3, :], op=ALU.mult), reads=[("lamv", 2), ("lamv", 3)], writes=["lamp"])
            sc.op("dve", lambda e: e.reduce_sum(out=lams[:, 0:2], in_=lamp, axis=mybir.AxisListType.X), reads=["lamp"], writes=["lams"])
            sc.op("act", lambda e: e.activation(out=lams[:, 0:2], in_=lams[:, 0:2], func=AF.Exp), reads=["lams"], writes=["lams"])
            sc.op("dve", lambda e: e.tensor_tensor(out=lams[:, 2:3], in0=lams[:, 0:1], in1=lams[:, 1:2], op=ALU.subtract), reads=["lams"], writes=["lams"])
            sc.op("dve", lambda e: e.tensor_scalar(out=lams[:, 3:4], in0=lams[:, 2:3], scalar1=lam_init, scalar2=-1.0, op0=ALU.add, op1=ALU.mult),
                  reads=["lams"], writes=["neglam"])
            neg_lam = lams[:, 3:4]
            sc.dma("sp", lambda e: e.dma_start(out=wd_t, in_=dnw_d[l, :].partition_broadcast(128)), writes=["wd"])
            sc.op("dve", lambda e: e.tensor_scalar(out=wd_t, in0=wd_t, scalar1=1.0 - lam_init, scalar2=None, op0=ALU.mult), reads=["wd"], writes=["wd"])
            sc.dma("sp", lambda e: e.dma_start(out=wg_t, in_=gnw_d[l, :].partition_broadcast(128)), writes=["wg"])
            sc.op("dve", lambda e: e.memset(Wg[0:33, :], 0.0), writes=["Wg"])
            sc.dma("pool", lambda e: e.dma_start(out=Wg[0:16, 0:256], in_=gup_d[l, 0]), writes=["Wg"])
            sc.dma("pool", lambda e: e.dma_start(out=Wg[16:32, 256:512], in_=gup_d[l, 1]), writes=["Wg"])
            sc.dma("pool", lambda e: e.dma_start(out=Wg[32:33, :], in_=gbias_d[l].rearrange("a n -> (a n)").partition_broadcast(1)), writes=["Wg"])
            sc.dma("sp", lambda e: e.dma_start(out=b1c, in_=b1_d[l].rearrange("(c p) -> p c", p=128), allow_slow_non_contiguous=True), writes=["b1c"])
            sc.dma("sp", lambda e: e.dma_start(out=b2t, in_=b2_d[l].partition_broadcast(128)), writes=["b2t"])
            sc.dma("sp", lambda e: e.dma_start(out=expB.rearrange("p h n -> p (h n)"), in_=eb_d), reads=["eb_d"], writes=[("expB", h) for h in range(4)])
            sc.op("dve", lambda e: e.memset(V[:, :, :, 128:129], 1.0), writes=[("V", t) for t in range(NT)])
            sc.op("dve", lambda e: e.memset(G33[32:33, :], 1.0), writes=["G33"])

            dump("XTin", XT, [128, 8, 2048], [("XT", t) for t in range(NT)])
            dump("Xin", X, [128, NT, 1024], [("X", t) for t in range(NT)])
            if stop == 'L' and l == nl - 1:
                raise _Stop()
            ps_rr = [0]

            def nextbank():
                b = ps_rr[0] % 6
                ps_rr[0] += 1
                return b

            xt_all = [("XT", t) for t in range(NT)]
            for blk in range(7):
                buf = blk % 2
                wb = WB[buf]
                kw = [("RW", buf, 0), ("RW", buf, 1)]
                if blk in (0, 1, 3, 6):
                    nch = 1 if blk == 6 else 4
                    for cc in range(nch):
                        for r in range(4):
                            bank = nextbank()
                            M = 32 if blk == 6 else 128
                            for c in range(8):
                                sc.op("pe", lambda e, c=c, cc=cc, r=r, bank=bank, M=M, wb=wb: e.matmul(
                                    PS[0:M, bank, :], lhsT=wb[:, c, cc * 128:cc * 128 + M], rhs=XT[:, c, r * 512:(r + 1) * 512],
                                    start=(c == 0), stop=(c == 7)), reads=kw + xt_all[r * 4:(r + 1) * 4], writes=psk(bank))
                            sl = slice(r * 512, (r + 1) * 512)
                            if blk == 0:
                                evac(QT[:, cc, sl], PS[:, bank, :], psk(bank), [("QT", cc, r)], scale=0.125)
                            elif blk == 1:
                                evac(KT[:, cc, sl], PS[:, bank, :], psk(bank), [("KT", cc, r)])
                            elif blk == 3:
                                if cc < 2:
                                    evac(gqT[:, cc, sl], PS[:, bank, :], psk(bank), [("gqT", 4 * r + i) for i in range(4)], scale=0.125)
                                else:
                                    evac(gkT[:, cc - 2, sl], PS[:, bank, :], psk(bank), [("gkT", 4 * r + i) for i in range(4)])
                            else:
                                evac(G33[0:32, sl], PS[0:32, bank, :], psk(bank), ["G33"])
                if blk == 3:
                    for t in range(NT):
                        bank = nextbank()
                        for cc in range(2):
                            sc.op("pe", lambda e, t=t, cc=cc, bank=bank: e.transpose(out=PSb[:, bank, cc * 128:(cc + 1) * 128], in_=gkT[:, cc, t * 128:(t + 1) * 128], identity=identb),
                                  reads=[("gkT", t), "identb"], writes=psk(bank))
                        evac(gk_tok[:, t, :], PSb[:, bank, 0:256], psk(bank), [("gk_tok", t)])
                if blk in (2, 4, 5):
                    for t in range(NT):
                        bank = nextbank()
                        c0, ncol = (0, 512)
                        for c in range(8):
                            sc.op("pe", lambda e, c=c, t=t, bank=bank, c0=c0, ncol=ncol, wb=wb: e.matmul(
                                PS[:, bank, 0:ncol], lhsT=XT[:, c, t * 128:(t + 1) * 128], rhs=wb[:, c, c0:c0 + ncol],
                                start=(c == 0), stop=(c == 7)), reads=kw + [("XT", t)], writes=psk(bank))
                        if blk == 2:
                            evac(V[:, t, :, 0:128], PS[:, bank, :].rearrange("p (h e) -> p h e", h=4), psk(bank), [("V", t)])
                        elif blk == 4:
                            evac(gv[:, t, :], PS[:, bank, :], psk(bank), [("gv", t)])
                        else:
                            sb = t % 2
                            sc.op("act", lambda e, bank=bank, sb=sb: e.activation(out=silu_t[sb], in_=PS[:, bank, :], func=AF.Silu),
                                  reads=psk(bank), writes=[("silu", sb)])
                            sc.op("dve", lambda e, t=t, sb=sb: e.tensor_tensor(
                                out=gr_s[:, t, :].rearrange("p (h e) -> p h e", h=4), in0=silu_t[sb].rearrange("p (h e) -> p h e", h=4),
                                in1=bc_mid(wg_t, 4), op=ALU.mult), reads=[("silu", sb), "wg"], writes=[("gr_s", t)])
                if blk + 2 < 7:
                    load_win_block(l, blk + 2, buf)
            for hf in range(2):
                sc.dma("pool", lambda e, hf=hf: e.dma_start(out=WO[:, hf * 4:(hf + 1) * 4, :],
                                                             in_=w_o_d[l].rearrange("(c p) n -> p c n", p=128)[:, hf * 4:(hf + 1) * 4, :]),
                       writes=[("RW", hf, 0), ("RW", hf, 1)])
            dump("QT", QT, [128, 4, 2048], [("QT", a, b) for a in range(4) for b in range(4)])
            dump("KT", KT, [128, 4, 2048], [("KT", a, b) for a in range(4) for b in range(4)])
            dump("V", V, [128, 16, 4, 129], [("V", t) for t in range(NT)])
            dump("expB", expB, [128, 4, 1152], [("expB", h) for h in range(4)])
            dump("gqT", gqT, [128, 2, 2048], [("gqT", t) for t in range(NT)])
            dump("gr_s", gr_s, [128, 16, 512], [("gr_s", t) for t in range(NT)])
            dump("G33", G33[0:33, :], [33, 2048], ["G33"])

            if stop == 'P' and l == nl - 1:
                raise _Stop()
            steps = [(h, r, j) for h in range(4) for r in range(4) for j in range(16)]

            def acc_ap(m, u):
                idx = m * 4 + u
                return PS[:, 4 + idx // 3, (idx % 3) * 160:(idx % 3) * 160 + 129]

            def acc_keys(m, u):
                return psk(4 + (m * 4 + u) // 3)

            Eb3 = view(R_X.base + 61440, 2048, BF16, "p (m n) -> p m n", m=2)
            EbL = [Eb[:, 0, :, :], Eb[:, 1, :, :], Eb3]

            def d_scores(i):
                h, r, j = steps[i]
                d = j - 4 * r
                mixed = (-1 <= d <= 4)
                sb = i % 2
                eb = i % 3
                E = EbL[eb]
                for m in range(2):
                    bank = sb * 2 + m
                    sc.op("pe", lambda e, h=h, r=r, j=j, m=m, bank=bank: e.matmul(
                        PS[:, bank, :], lhsT=KT[64 * m:64 * m + 64, h, j * 128:(j + 1) * 128],
                        rhs=QT[64 * m:64 * m + 64, h, r * 512:(r + 1) * 512], start=True, stop=True),
                        reads=[("KT", h, j // 4), ("QT", h, r)], writes=psk(bank))
                pk2 = psk(sb * 2) + psk(sb * 2 + 1)
                ek = [("E", eb, 0), ("E", eb, 1)]
                if mixed:
                    c0 = (4 - d) * 128
                    sc.op("act", lambda e, sb=sb, E=E: e.activation(out=E, in_=PS[:, sb * 2:sb * 2 + 2, :], func=AF.Exp),
                          reads=pk2, writes=ek)
                    for m in range(2):
                        sc.op("dve", lambda e, E=E, m=m, h=h, c0=c0: e.tensor_tensor(out=E[:, m, :], in0=E[:, m, :], in1=expB[:, h, c0:c0 + 512], op=ALU.mult),
                              reads=[("E", eb, m), ("expB", h)], writes=[("E", eb, m)])
                else:
                    side = 0 if d < 0 else 1
                    sc.op("act", lambda e, sb=sb, E=E, side=side, h=h: e.activation(
                        out=E, in_=PS[:, sb * 2:sb * 2 + 2, :], func=AF.Exp, bias=cb[:, side, h:h + 1]),
                        reads=pk2 + [("cb", 0), ("cb", 1)], writes=ek)

            def d_av(i):
                h, r, j = steps[i]
                eb = i % 3
                E = EbL[eb]
                for m in range(2):
                    for u in range(4):
                        sc.op("pe", lambda e, h=h, j=j, m=m, u=u, E=E: e.matmul(
                            acc_ap(m, u), lhsT=E[:, m, u * 128:(u + 1) * 128], rhs=V[:, j, h, 0:129],
                            start=(j == 0 and (m * 4 + u) % 3 == 0), stop=(j == 15), skip_group_check=True),
                            reads=[("E", eb, m), ("V", j)], writes=acc_keys(m, u))

            sm = dsm[0]
            def ak(*idx):
                return [("accS", i) for i in idx]

            def d_final(h, r):
                sc.op("dve", lambda e: e.tensor_copy(out=accS[:, 0:3, :], in_=PS[:, 4, 0:480].rearrange("p (a n) -> p a n", a=3)[:, :, 0:129]),
                      reads=psk(4), writes=ak(0, 1, 2) + [("silu", 0), ("silu", 1)])
                sc.op("dve", lambda e: e.tensor_copy(out=accS[:, 3:6, :], in_=PS[:, 5, 0:480].rearrange("p (a n) -> p a n", a=3)[:, :, 0:129]),
                      reads=psk(5), writes=ak(3, 4, 5))
                sc.op("dve", lambda e: e.tensor_copy(out=accS[:, 6:8, :], in_=PS[:, 6, 0:320].rearrange("p (a n) -> p a n", a=2)[:, :, 0:129]),
                      reads=psk(6), writes=ak(6, 7))

            def d_final2(h, r):
                sc.op("dve", lambda e: e.reciprocal(out=sm[:, 0:8], in_=accS[:, :, 128]), reads=ak(*range(8)), writes=["dsm"])
                sc.op("dve", lambda e: e.tensor_scalar(out=sm[:, 4:8], in0=sm[:, 4:8], scalar1=neg_lam, scalar2=None, op0=ALU.mult),
                      reads=["dsm", "neglam"], writes=["dsm"])
                sc.op("dve", lambda e: e.memset(sm[:, 8:12], 0.0), writes=[("dss", u) for u in range(4)])

            def d_final_u(u):
                if True:
                    sc.op("dve", lambda e, u=u: e.tensor_scalar(out=accS[:, u, 0:128], in0=accS[:, u, 0:128], scalar1=sm[:, u:u + 1], scalar2=None, op0=ALU.mult),
                          reads=["dsm"] + ak(u), writes=ak(u))
                    sc.op("dve", lambda e, u=u: e.scalar_tensor_tensor(out=accS[:, u, 0:128], in0=accS[:, 4 + u, 0:128], scalar=sm[:, 4 + u:5 + u],
                                                                       in1=accS[:, u, 0:128], op0=ALU.mult, op1=ALU.add),
                          reads=["dsm"] + ak(u, 4 + u), writes=ak(u))
                    sc.op("dve", lambda e, u=u: e.scalar_tensor_tensor(out=accS[:, 4 + u, 0:128], in0=accS[:, u, 0:128], scalar=1.0, in1=accS[:, u, 0:128],
                                                                       op0=ALU.mult, op1=ALU.mult, accum_out=sm[:, 8 + u:9 + u]),
                          reads=ak(u), writes=ak(4 + u) + [("dss", u)])

            def d_final_b(h, r):
                sc.op("act", lambda e: e.activation(out=sm[:, 12:16], in_=sm[:, 8:12], func=AF.Ln, bias=1e-5, scale=1.0 / 128),
                      reads=[("dss", u) for u in range(4)], writes=["drs"])
                sc.op("act", lambda e: e.activation(out=sm[:, 12:16], in_=sm[:, 12:16], func=AF.Exp, scale=-0.5), reads=["drs"], writes=["drs"])
                for u in range(4):
                    sc.op("dve", lambda e, u=u: e.scalar_tensor_tensor(out=d_y[:, u, :], in0=accS[:, u, 0:128], scalar=sm[:, 12 + u:13 + u], in1=wd_t,
                                                                       op0=ALU.mult, op1=ALU.mult),
                          reads=ak(u) + ["drs", "wd"], writes=[("dy", u)])

            def d_final_pe(h, r):
                for u in range(4):
                    sc.op("pe", lambda e, u=u: e.transpose(out=PSb[:, 7, u * 128:(u + 1) * 128], in_=d_y[:, u, :], identity=identb),
                          reads=[("dy", u), "identb"], writes=psk(7))
                sc.op("dve", lambda e, h=h, r=r: e.tensor_copy(out=XT[:, h, r * 512:(r + 1) * 512], in_=PSb[:, 7, 0:512]),
                      reads=psk(7), writes=[("XT", 4 * r + i) for i in range(4)])

            pend = []
            pend_b = []
            pend_u = []
            d_scores(0)
            d_scores(1)
            for i in range(len(steps)):
                h, r, j = steps[i]
                if j == 15:
                    d_av(i)
                    d_final(h, r)
                    if i + 2 < len(steps):
                        d_scores(i + 2)
                    d_final2(h, r)
                else:
                    if i + 2 < len(steps):
                        d_scores(i + 2)
                    d_av(i)
                if j == 15:
                    pend.append((h, r))
                    pend_b.append((h, r))
                    pend_u.extend([0, 1, 2, 3])
                    d_final_u(pend_u.pop(0))
                elif pend_u:
                    d_final_u(pend_u.pop(0))
                elif j == 4 and pend_b:
                    d_final_b(*pend_b.pop(0))
                elif j == 7 and pend:
                    d_final_pe(*pend.pop(0))
            while pend_u:
                d_final_u(pend_u.pop(0))
            while pend_b:
                d_final_b(*pend_b.pop(0))
            while pend:
                d_final_pe(*pend.pop(0))
            dump("mixT_d", XT, [128, 8, 2048], [("XT", t) for t in range(NT)])

            if stop == 'D' and l == nl - 1:
                raise _Stop()
            sc.barrier()
            R_D.reset()
            qf = R_D.get(8192, BF16, "p (c n) -> p c n", c=2)
            kf = R_D.get(8192, BF16, "p (c n) -> p c n", c=2)
            kd_f = R_D.get(8192, BF16, "p (t n) -> p t n", t=16)
            Sbf = R_D.get(16384, BF16, "p (d q t e) -> p d q t e", d=2, q=2, t=16)
            stm2 = [R_D.get(4096, F32, "p (d q n) -> p d q n", d=2, q=2) for _ in range(2)]
            _p0 = R_D.pos
            sp_ = [R_D.get(2048, F32) for _ in range(2)]
            _p1 = R_D.pos
            ebt = [R_D.get(2 * 2 * 129 * 4, F32, "p (d q n) -> p d q n", d=2, q=2) for _ in range(2)]
            _p2 = R_D.pos
            enbt = [R_D.get(2 * 2 * 128 * 4, F32, "p (d q n) -> p d q n", d=2, q=2) for _ in range(2)]
            erem = [R_D.get(2048, F32) for _ in range(2)]
            dS = [R_D.get(1024, F32) for _ in range(4)]
            _pend = R_D.pos
            Am = [R_X.get(4 * 2 * 128 * 2, BF16, "p (h d n) -> p h d n", h=4, d=2) for _ in range(2)]
            R_D.pos = _p2
            g_y = [R_D.get(1024, BF16) for _ in range(2)]
            g_junk = R_D.get(512, F32)
            R_D.pos = _pend
            qb, kb, kd_b = gqT, gkT, gk_tok
            maskf = Uf[:, 0:128]
            maskb = Usf

            def tl(t):
                return slice(t * 128, (t + 1) * 128)

            def prep_A(t):
                b = t % 2
                sp = sp_[b]
                zb = 0 if b == 0 else 7
                sc.op("pe", lambda e: e.matmul(PS[:, zb, :], lhsT=G33[0:33, tl(t)], rhs=Wg[0:33, :], start=True, stop=True),
                      reads=["G33", "Wg"], writes=psk(zb))
                sc.op("act", lambda e: e.activation(out=sp, in_=PS[:, zb, :], func=AF.Exp, scale=-1.0), reads=psk(zb), writes=[("sp", b)])
                sc.op("act", lambda e: e.activation(out=sp, in_=sp, func=AF.Ln, bias=1.0, scale=1.0), reads=[("sp", b)], writes=[("sp", b)])

            prep_A(0)
            for t in range(NT):
                b = t % 2
                sp = sp_[b]
                if t + 1 < NT:
                    prep_A(t + 1)
                sc.op("pe", lambda e, sp=sp: e.matmul(PS[:, 1, 0:256], lhsT=Usf, rhs=sp[:, 0:256], start=True, stop=True), reads=[("sp", b), "Usf", "Usb"], writes=psk(1))
                sc.op("pe", lambda e, sp=sp: e.matmul(PS[:, 1, 256:512], lhsT=Usb, rhs=sp[:, 256:512], start=True, stop=True), reads=[("sp", b), "Usf", "Usb"], writes=psk(1))
                sc.op("act", lambda e, b=b: e.activation(out=erem[b], in_=PS[:, 1, :], func=AF.Exp, scale=-1.0 / 16), reads=psk(1), writes=[("erem", b)])
                sc.op("dve", lambda e, t=t, b=b: e.tensor_tensor(out=kd_f[:, t, :], in0=gk_tok[:, t, :], in1=erem[b][:, 0:256], op=ALU.mult),
                      reads=[("gk_tok", t), ("erem", b)], writes=[("kd_f", t)])
                sc.op("dve", lambda e, t=t, b=b: e.tensor_tensor(out=kd_b[:, t, :], in0=gk_tok[:, t, :], in1=erem[b][:, 256:512], op=ALU.mult),
                      reads=[("gk_tok", t), ("erem", b), ("kd_f", t)], writes=[("gk_tok", t)])
                for d in range(2):
                    U = Uf if d == 0 else Ub
                    for q in range(2):
                        sc.op("pe", lambda e, sp=sp, d=d, q=q, U=U: e.matmul(PS[:, 2 + d, q * 160:q * 160 + 129],
                                                                            lhsT=sp[:, d * 256 + q * 128:d * 256 + (q + 1) * 128], rhs=U, start=True, stop=True),
                              reads=[("sp", b), "Uf", "Ub"], writes=psk(2 + d))
                src4 = PS[:, 2:4, 0:320].rearrange("p a (q n) -> p a q n", q=2)
                sc.op("act", lambda e, b=b, src4=src4: e.activation(out=ebt[b], in_=src4[:, :, :, 0:129], func=AF.Exp, scale=-1.0 / 16),
                      reads=psk(2) + psk(3), writes=[("eb", b, 0), ("eb", b, 1)])
                sc.op("act", lambda e, b=b, src4=src4: e.activation(out=enbt[b], in_=src4[:, :, :, 0:128], func=AF.Exp, scale=1.0 / 16),
                      reads=psk(2) + psk(3), writes=[("enb", b, 0), ("enb", b, 1)])
                sc.op("dve", lambda e, t=t, b=b: e.tensor_tensor(out=qf[:, :, tl(t)], in0=gqT[:, :, tl(t)], in1=ebt[b][:, 0, :, 0:128], op=ALU.mult),
                      reads=[("gqT", t), ("eb", b, 0)], writes=[("qf", t)])
                sc.op("dve", lambda e, t=t, b=b: e.tensor_tensor(out=kf[:, :, tl(t)], in0=gkT[:, :, tl(t)], in1=enbt[b][:, 0, :, :], op=ALU.mult),
                      reads=[("gkT", t), ("enb", b, 0)], writes=[("kf", t)])
                sc.op("dve", lambda e, t=t, b=b: e.tensor_tensor(out=qb[:, :, tl(t)], in0=gqT[:, :, tl(t)], in1=ebt[b][:, 1, :, 0:128], op=ALU.mult),
                      reads=[("gqT", t), ("eb", b, 1), ("qf", t)], writes=[("gqT", t)])
                sc.op("dve", lambda e, t=t, b=b: e.tensor_tensor(out=kb[:, :, tl(t)], in0=gkT[:, :, tl(t)], in1=enbt[b][:, 1, :, :], op=ALU.mult),
                      reads=[("gkT", t), ("enb", b, 1), ("kf", t)], writes=[("gkT", t)])
                sc.op("dve", lambda e, t=t, b=b: e.tensor_copy(out=decs[:, :, :, t:t + 1], in_=ebt[b][:, :, :, 128:129]),
                      reads=[("eb", b, 0), ("eb", b, 1)], writes=[("decs", t)])
            dump("qf", qf, [128, 2, 2048], [("qf", t) for t in range(NT)])
            dump("kd_f", kd_f, [128, 16, 256], [("kd_f", t) for t in range(NT)])
            dump("decs", decs, [128, 2, 2, 16], [("decs", t) for t in range(NT)])

            if stop == 'G1' and l == nl - 1:
                raise _Stop()
            sc.op("dve", lambda e: e.memset(stm2[0], 0.0), writes=[("stm", 0, d, q) for d in range(2) for q in range(2)])
            chains = [(d, q) for d in range(2) for q in range(2)]
            par = {c: 0 for c in chains}
            for i in range(NT):
                todo = []
                for ci, (d, q) in enumerate(chains):
                    t = i if d == 0 else NT - 1 - i
                    cur = par[(d, q)]
                    if i > 0:
                        sc.op("act", lambda e, d=d, q=q, t=t, cur=cur: e.activation(out=Sbf[0:64, d, q, t, :], in_=stm2[cur][0:64, d, q, 0:128], func=AF.Copy),
                              reads=[("stm", cur, d, q)], writes=[("Sbf", d, q, t)])
                        sc.op("dve", lambda e, d=d, q=q, t=t, cur=cur: e.tensor_copy(out=Sbf[64:128, d, q, t, :], in_=stm2[cur][64:128, d, q, 128:256]),
                              reads=[("stm", cur, d, q)], writes=[("Sbf", d, q, t)])
                    if i == NT - 1:
                        continue
                    kd = kd_f if d == 0 else kd_b
                    kkey = "kd_f" if d == 0 else "gk_tok"
                    pslot = ci % 2
                    pk = psk(4 + pslot)
                    sc.op("pe", lambda e, kd=kd, t=t, q=q, pslot=pslot: e.matmul(PS[:, 4 + pslot, 0:256], lhsT=kd[:, t, q * 128:(q + 1) * 128],
                                                                                rhs=gv[:, t, q * 256:(q + 1) * 256], start=True, stop=True),
                          reads=[(kkey, t), ("gv", t)], writes=pk)
                    if os.environ.get("GSKIP") != "evac":
                        sc.op("act", lambda e, ci=ci, pslot=pslot: e.activation(out=dS[ci], in_=PS[:, 4 + pslot, 0:256], func=AF.Copy),
                              reads=pk, writes=[("dS", ci)])
                    todo.append((ci, d, q, t, cur))
                for (ci, d, q, t, cur) in todo:
                    if os.environ.get("GSKIP") == "upd":
                        par[(d, q)] = 1 - cur
                        continue
                    sc.op("dve", lambda e, ci=ci, d=d, q=q, t=t, cur=cur: e.scalar_tensor_tensor(
                        out=stm2[1 - cur][:, d, q, :], in0=stm2[cur][:, d, q, :], scalar=decs[:, d, q, t:t + 1], in1=dS[ci],
                        op0=ALU.mult, op1=ALU.add), reads=[("stm", cur, d, q), ("decs", t), ("dS", ci)], writes=[("stm", 1 - cur, d, q)])
                    par[(d, q)] = 1 - cur
            dump("Sbf", Sbf, [128, 2, 2, 16, 128], [("Sbf", d, q, t) for d in range(2) for q in range(2) for t in range(NT)])
            if stop == 'G2' and l == nl - 1:
                raise _Stop()

            def g_A(t):
                b = t % 2
                A = Am[b]
                for half in range(2):
                    sbank = (4 + half) if os.environ.get('GBANK') else (2 * b + half)
                    items = []
                    for sq in range(4):
                        h, d = half + 2 * (sq // 2), sq % 2
                        q = h // 2
                        base = (h % 2) * 64
                        kk = kf if d == 0 else kb
                        qq = qf if d == 0 else qb
                        kkey = ("kf", t) if d == 0 else ("gkT", t)
                        qkey = ("qf", t) if d == 0 else ("gqT", t)
                        sc.op("pe", lambda e, kk=kk, qq=qq, q=q, base=base, sbank=sbank, sq=sq: e.matmul(
                            PS[:, sbank, sq * 128:(sq + 1) * 128], lhsT=kk[base:base + 64, q, tl(t)], rhs=qq[base:base + 64, q, tl(t)], start=True, stop=True),
                            reads=[kkey, qkey], writes=psk(sbank))
                        items.append((sq, h, d))
                        if os.environ.get("GOLD"):
                            mk = maskf if d == 0 else maskb
                            sc.op("dve", lambda e, A=A, h=h, d=d, sbank=sbank, sq=sq, mk=mk: e.tensor_tensor(
                                out=A[:, h, d, :], in0=PS[:, sbank, sq * 128:(sq + 1) * 128], in1=mk, op=ALU.mult),
                                reads=psk(sbank) + ["Uf", "Usf"], writes=[("A", b, h, d)])
                    if os.environ.get("GOLD"):
                        continue
                    for (sq, h, d) in items:
                        mk = maskf if d == 0 else maskb
                        sc.op("dve", lambda e, A=A, h=h, d=d, sbank=sbank, sq=sq, mk=mk: e.tensor_tensor(
                            out=A[:, h, d, :], in0=PS[:, sbank, sq * 128:(sq + 1) * 128], in1=mk, op=ALU.mult),
                            reads=psk(sbank) + ["Uf", "Usf"], writes=[("A", b, h, d)])

            def g_B(t):
                b = t % 2
                A = Am[b]
                obank = (0 + b) if os.environ.get('GBANK') else (4 + b)
                for h in range(4):
                    q = h // 2
                    base = (h % 2) * 64
                    oh_ = PS[:, obank, h * 128:(h + 1) * 128]
                    ok = psk(obank)
                    inter_f = t > 0
                    inter_b = t < NT - 1
                    sc.op("pe", lambda e, A=A, h=h, oh_=oh_: e.matmul(oh_, lhsT=A[:, h, 0, :], rhs=gv[:, t, h * 128:(h + 1) * 128], start=True, stop=False),
                          reads=[("A", b, h, 0), ("gv", t)], writes=ok)
                    sc.op("pe", lambda e, A=A, h=h, oh_=oh_, fin=(not inter_f and not inter_b): e.matmul(
                        oh_, lhsT=A[:, h, 1, :], rhs=gv[:, t, h * 128:(h + 1) * 128], start=False, stop=fin),
                        reads=[("A", b, h, 1), ("gv", t)], writes=ok)
                    if inter_f:
                        sc.op("pe", lambda e, q=q, base=base, oh_=oh_, fin=(not inter_b): e.matmul(
                            oh_, lhsT=qf[base:base + 64, q, tl(t)], rhs=Sbf[base:base + 64, 0, q, t, :], start=False, stop=fin),
                            reads=[("qf", t), ("Sbf", 0, q, t)], writes=ok)
                    if inter_b:
                        sc.op("pe", lambda e, q=q, base=base, oh_=oh_: e.matmul(
                            oh_, lhsT=qb[base:base + 64, q, tl(t)], rhs=Sbf[base:base + 64, 1, q, t, :], start=False, stop=True),
                            reads=[("gqT", t), ("Sbf", 1, q, t)], writes=ok)

            def g_norm(t):
                b = t % 2
                sm = gsm[b]
                obank = (0 + b) if os.environ.get('GBANK') else (4 + b)
                okall = psk(obank)
                for h in range(4):
                    sc.op("act", lambda e, h=h, sm=sm: e.activation(out=g_junk, in_=PS[:, obank, h * 128:(h + 1) * 128], func=AF.Square, accum_out=sm[:, h:h + 1]),
                          reads=okall, writes=[("gss", b, h)] + ([("enb", 1, 0), ("enb", 1, 1)] if h == 0 else []))
                sc.op("act", lambda e, sm=sm: e.activation(out=sm[:, 4:8], in_=sm[:, 0:4], func=AF.Sqrt, bias=1e-5, scale=1.0 / 128),
                      reads=[("gss", b, h) for h in range(4)], writes=[("grs", b)])
                sc.op("dve", lambda e, sm=sm: e.reciprocal(out=sm[:, 4:8], in_=sm[:, 4:8]), reads=[("grs", b)], writes=[("grs", b)])
                for h in range(4):
                    sc.op("dve", lambda e, h=h, sm=sm, b=b: e.scalar_tensor_tensor(
                        out=g_y[b][:, h * 128:(h + 1) * 128], in0=PS[:, obank, h * 128:(h + 1) * 128], scalar=sm[:, 4 + h:5 + h],
                        in1=gr_s[:, t, h * 128:(h + 1) * 128], op0=ALU.mult, op1=ALU.mult),
                        reads=psk(obank) + [("grs", b), ("gr_s", t)], writes=[("gy", b, h)] + ([("enb", 0, 0), ("enb", 0, 1)] if h == 0 else []))

            def g_tr(t):
                b = t % 2
                tk = psk(6 + b)
                for h in range(4):
                    sc.op("pe", lambda e, h=h, b=b: e.transpose(out=PSb[:, 6 + b, h * 128:(h + 1) * 128], in_=g_y[b][:, h * 128:(h + 1) * 128], identity=identb),
                          reads=[("gy", b, h), "identb"], writes=tk)
                sc.op("act", lambda e, b=b: e.activation(out=XT[:, 4:8, tl(t)], in_=PSb[:, 6 + b, 0:512].rearrange("p (c n) -> p c n", c=4), func=AF.Copy),
                      reads=tk, writes=[("XT", t)])

            g_A(0)
            for t in range(NT):
                if t + 1 < NT:
                    g_A(t + 1)
                if os.environ.get("GSKIP") == "B":
                    continue
                g_B(t)
                if os.environ.get("GSKIP") == "norm":
                    continue
                g_norm(t)
                if os.environ.get("GSKIP") == "tr":
                    continue
                if t > 0:
                    g_tr(t - 1)
            if not os.environ.get("GSKIP"):
                g_tr(NT - 1)
            dump("mixT", XT, [128, 8, 2048], [("XT", t) for t in range(NT)])

            if stop == 'G' and l == nl - 1:
                raise _Stop()
            sc.barrier()
            load_ln_params(ln1g_d[l], ln1b_d[l])
            R_D.reset()
            W1B = [R_D.get(8192, BF16, "p (c n) -> p c n", c=8) for _ in range(2)]
            W2B = [R_D.get(8192, BF16, "p (c n) -> p c n", c=4) for _ in range(2)]
            hT = R_D.get(16384, BF16, "p (c n) -> p c n", c=4)
            relu_t = [R_D.get(2048, F32) for _ in range(2)]

            def load_ffn_block(fb, buf):
                s1 = w1_d[l, :, fb * 512:(fb + 1) * 512].rearrange("(c p) n -> p c n", p=128)
                s2 = w2_d[l, fb * 512:(fb + 1) * 512, :].rearrange("(c p) n -> p c n", p=128)
                for hf in range(2):
                    sc.dma("pool", lambda e, hf=hf: e.dma_start(out=W1B[buf][:, hf * 4:(hf + 1) * 4, :], in_=s1[:, hf * 4:(hf + 1) * 4, :]),
                           writes=[("W1B", buf, hf)])
                for hf in range(2):
                    sc.dma("pool", lambda e, hf=hf: e.dma_start(out=W2B[buf][:, hf * 2:(hf + 1) * 2, :], in_=s2[:, hf * 2:(hf + 1) * 2, :]),
                           writes=[("W2B", buf, hf)])

            load_ffn_block(0, 0)
            load_ffn_block(1, 1)
            for t in range(NT):
                sc.dma("sp", lambda e, t=t: e.dma_start(out=X[:, t, :], in_=xs_d[t * 128:(t + 1) * 128, :]), reads=[("xsd", t)], writes=[("X", t)])
            def o_mm(t):
                yb = (t % 3) * 2
                for hf in range(2):
                    for c in range(8):
                        sc.op("pe", lambda e, c=c, hf=hf, t=t, yb=yb: e.matmul(PS[:, yb + hf, :], lhsT=XT[:, c, tl(t)], rhs=WO[:, c, hf * 512:(hf + 1) * 512],
                                                                              start=(c == 0), stop=(c == 7)),
                              reads=[("XT", t), ("RW", 0, 0), ("RW", 0, 1), ("RW", 1, 0), ("RW", 1, 1)], writes=psk(yb + hf))

            def o_ln(t):
                yb = (t % 3) * 2
                sc.op("dve", lambda e, t=t, yb=yb: e.scalar_tensor_tensor(out=X[:, t, :], in0=X[:, t, :], scalar=ALPHA,
                                                                          in1=PS[:, yb:yb + 2, :].rearrange("p a n -> p (a n)"), op0=ALU.mult, op1=ALU.add),
                      reads=[("X", t)] + psk(yb) + psk(yb + 1), writes=[("X", t)])
                ln_a(t)

            for t0 in range(3):
                o_mm(t0)
                o_ln(t0)
            for t in range(NT):
                if t + 3 < NT:
                    o_mm(t + 3)
                ln_b1(t, None)
                if t + 3 < NT:
                    o_ln(t + 3)
                ln_b2(t, None)
            dump("x1T", XT, [128, 8, 2048], [("XT", t) for t in range(NT)])

            if stop == 'O' and l == nl - 1:
                raise _Stop()
            if not last:
                load_win_block(l + 1, 0, 0)
                load_win_block(l + 1, 1, 1)
            hrr = [0]
            for fb in range(8):
                buf = fb % 2
                for r in range(4):
                    for fc in range(4):
                        bank = 4 + hrr[0] % 3
                        rb = hrr[0] % 2
                        hrr[0] += 1
                        for c in range(8):
                            sc.op("pe", lambda e, c=c, fc=fc, r=r, bank=bank, buf=buf: e.matmul(
                                PS[:, bank, :], lhsT=W1B[buf][:, c, fc * 128:(fc + 1) * 128], rhs=XT[:, c, r * 512:(r + 1) * 512],
                                start=(c == 0), stop=(c == 7)), reads=[("W1B", buf, 0), ("W1B", buf, 1)] + xt_all[r * 4:(r + 1) * 4], writes=psk(bank))
                        fcol = fb * 4 + fc
                        sc.op("act", lambda e, bank=bank, rb=rb, fcol=fcol: e.activation(out=relu_t[rb], in_=PS[:, bank, :], func=AF.Relu,
                                                                                         bias=b1c[:, fcol:fcol + 1], scale=1.0),
                              reads=psk(bank) + ["b1c"], writes=[("relu", rb)])
                        sc.op("dve", lambda e, rb=rb, fc=fc, r=r: e.tensor_tensor(out=hT[:, fc, r * 512:(r + 1) * 512], in0=relu_t[rb], in1=relu_t[rb], op=ALU.mult),
                              reads=[("relu", rb)], writes=[("hT", fc, r)])
                for t in range(NT):
                    yb = (t % 2) * 2
                    for hf in range(2):
                        for fc in range(4):
                            sc.op("pe", lambda e, fc=fc, hf=hf, t=t, yb=yb, buf=buf: e.matmul(
                                PS[:, yb + hf, :], lhsT=hT[:, fc, tl(t)], rhs=W2B[buf][:, fc, hf * 512:(hf + 1) * 512],
                                start=(fc == 0), stop=(fc == 3)), reads=[("hT", fc, t // 4), ("W2B", buf, 0), ("W2B", buf, 1)], writes=psk(yb + hf))
                    if fb == 0:
                        sc.op("dve", lambda e, t=t, yb=yb: e.scalar_tensor_tensor(out=X[:, t, :], in0=X[:, t, :], scalar=ALPHA,
                                                                                  in1=PS[:, yb:yb + 2, :].rearrange("p a n -> p (a n)"), op0=ALU.mult, op1=ALU.add),
                              reads=[("X", t)] + psk(yb) + psk(yb + 1), writes=[("X", t)])
                        sc.op("pool", lambda e, t=t: e.tensor_tensor(out=X[:, t, :], in0=X[:, t, :], in1=b2t, op=ALU.add), reads=[("X", t), "b2t"], writes=[("X", t)])
                    else:
                        sc.op("dve", lambda e, t=t, yb=yb: e.tensor_tensor(out=X[:, t, :], in0=X[:, t, :], in1=PS[:, yb:yb + 2, :].rearrange("p a n -> p (a n)"), op=ALU.add),
                              reads=[("X", t)] + psk(yb) + psk(yb + 1), writes=[("X", t)])
                if fb + 2 < 8:
                    load_ffn_block(fb + 2, buf)
            if stop == 'F' and l == nl - 1:
                raise _Stop()
            load_ln_params(ln2g_d[l], ln2b_d[l])
            ln_all(out_d if last else xs_d)
            sc.barrier()


        for _l in range(nl):
            do_layer(_l)
    except _Stop:
        pass
    out_dmas = [o for o in sc.ops if o.is_dma and o.dkey in [("xs", i) for i in range(4)]]
    fin = {}
    for o in out_dmas:
        fin[o.dkey] = o
    finals = list(fin.values()) + list(dbg_out.values())
    sc.emit(final_wait_ops=finals)
    es.close()
    return nc, sc


_CONST = None


def kernel(**inputs):
    global _CONST
    if _CONST is None:
        _CONST = _constants()
    nc, _ = build(2)
    x = np.ascontiguousarray(inputs["x"], dtype=np.float32)
    shared = {k: np.ascontiguousarray(v, dtype=np.float32) for k, v in inputs.items() if k != "x"}
    shared.update(_CONST)
    in_maps = []
    for b in range(8):
        m = dict(shared)
        m["x"] = x[b]
        in_maps.append(m)
    res = run_bass_kernel_spmd(nc, in_maps, core_ids=list(range(8)))
    return np.stack([r["out"] for r in res.results], axis=0).astype(np.float32)
```

```python
import math
import os
from contextlib import ExitStack

import numpy as np
import concourse.bass as bass
import concourse.mybir as mybir
from concourse.bass_utils import run_bass_kernel_spmd

F32 = mybir.dt.float32
BF16 = mybir.dt.bfloat16
AF = mybir.ActivationFunctionType
ALU = mybir.AluOpType

S = 2048
D = 1024
DIN = 3104
DFF = 4096
NT = 16
ALPHA = (2.0 * 2) ** 0.25
ENGS = ("pe", "act", "dve", "pool", "sp")
EPOCH = 30000


class _Res:
    __slots__ = ("last_w", "readers")

    def __init__(self):
        self.last_w = None
        self.readers = []


class _Op:
    __slots__ = ("eng", "fn", "deps", "signal", "tok", "is_dma", "dkey")

    def __init__(self, eng, fn, is_dma, dkey):
        self.eng = eng
        self.fn = fn
        self.deps = []
        self.signal = False
        self.tok = None
        self.is_dma = is_dma
        self.dkey = dkey


class Sched:
    def __init__(self, nc):
        self.nc = nc
        self.ops = []
        self.res = {}
        self.pending = {e: [] for e in ENGS}

    def _r(self, key):
        x = self.res.get(key)
        if x is None:
            x = self.res[key] = _Res()
        return x

    def _add(self, op, reads, writes):
        deps = set()
        for k in reads:
            rs = self._r(k)
            if rs.last_w is not None:
                deps.add(rs.last_w)
        for k in writes:
            rs = self._r(k)
            if rs.last_w is not None:
                deps.add(rs.last_w)
            deps.update(rs.readers)
        for k in reads:
            self._r(k).readers.append(op)
        for k in writes:
            rs = self._r(k)
            rs.last_w = op
            rs.readers = []
        if self.pending[op.eng]:
            deps.update(self.pending[op.eng])
            self.pending[op.eng] = []
        deps.discard(op)
        op.deps = list(deps)
        self.ops.append(op)
        return op

    def op(self, eng, fn, reads=(), writes=()):
        return self._add(_Op(eng, fn, False, None), reads, writes)

    def dma(self, eng, fn, dkey=None, reads=(), writes=()):
        if dkey is None:
            dkey = ("w", writes[0])
        return self._add(_Op(eng, fn, True, dkey), reads, writes)

    def barrier(self):
        last = {}
        for o in self.ops:
            last[(o.eng, o.dkey) if o.is_dma else o.eng] = o
        b = list(last.values())
        self.pending = {e: list(b) for e in ENGS}

    def emit(self, final_wait_ops=()):
        nc = self.nc
        ops = self.ops
        for o in ops:
            for d in o.deps:
                if d.is_dma:
                    d.signal = True
                elif d.eng == "pe" and o.eng == "pe" and not o.is_dma:
                    continue
                else:
                    d.signal = True
        with ExitStack() as es:
            eng_sems = {e: [] for e in ENGS}
            cnt = {e: 0 for e in ENGS}
            dma_sems = {}
            dma_cnt = {}
            for o in ops:
                if o.is_dma:
                    if o.dkey not in dma_sems:
                        dma_sems[o.dkey] = es.enter_context(nc.semaphore("d%d" % len(dma_sems)))
                        dma_cnt[o.dkey] = 0
                    dma_cnt[o.dkey] += 16
                    o.tok = (dma_sems[o.dkey], dma_cnt[o.dkey])
                elif o.signal:
                    ep = cnt[o.eng] // EPOCH
                    if ep >= len(eng_sems[o.eng]):
                        eng_sems[o.eng].append(es.enter_context(nc.semaphore("e_%s_%d" % (o.eng, ep))))
                    cnt[o.eng] += 1
                    o.tok = (eng_sems[o.eng][ep], cnt[o.eng] - ep * EPOCH)
            per_eng = {e: [o for o in ops if o.eng == e] for e in ENGS}
            self.stats = {e: len(per_eng[e]) for e in ENGS}
            self.stats["sems"] = sum(len(v) for v in eng_sems.values()) + len(dma_sems)

            def run(e, eng):
                waited = {}
                for o in per_eng[e]:
                    need = {}
                    for d in o.deps:
                        if d.tok is None:
                            continue
                        if (not d.is_dma) and d.eng == "pe" and e == "pe" and not o.is_dma:
                            continue
                        s, v = d.tok
                        k = id(s)
                        if waited.get(k, 0) >= v:
                            continue
                        if k not in need or need[k][1] < v:
                            need[k] = (s, v)
                    for k, (s, v) in need.items():
                        eng.wait_ge(s, v)
                        waited[k] = v
                    ins = o.fn(eng)
                    if o.tok is not None:
                        ins.then_inc(o.tok[0], 16 if o.is_dma else 1)
                if e == "sp":
                    for o in final_wait_ops:
                        s, v = o.tok
                        eng.wait_ge(s, v)

            with nc.Block() as block:
                @block.sync
                def _(eng):
                    run("sp", eng)

                @block.tensor
                def _(eng):
                    run("pe", eng)

                @block.scalar
                def _(eng):
                    run("act", eng)

                @block.vector
                def _(eng):
                    run("dve", eng)

                @block.gpsimd
                def _(eng):
                    run("pool", eng)


def _t5_bucket(rel):
    nb = 16
    me = 8
    ret = np.where(rel > 0, nb, 0)
    n = np.abs(rel)
    large = me + (np.log(np.maximum(n, 1).astype(np.float32) / np.float32(me))
                  / np.float32(math.log(128 / me)) * np.float32(nb - me)).astype(np.int32)
    large = np.minimum(large, nb - 1)
    return ret + np.where(n < me, n, large)


MLEN = 1280


def _constants():
    c = {}
    c["c_ident"] = np.eye(128, dtype=np.float32)
    c["c_J"] = np.eye(128, dtype=np.float32)[::-1].copy()
    s = np.arange(128)[:, None]
    t = np.arange(128)[None, :]
    uf = np.zeros((128, 129), np.float32)
    uf[:, :128] = (s <= t)
    uf[:, 128] = 1.0
    ub = np.zeros((128, 129), np.float32)
    ub[:, :128] = (s >= t)
    ub[:, 128] = 1.0
    c["c_uf"] = uf
    c["c_ub"] = ub
    c["c_sf"] = (s > t).astype(np.float32)
    c["c_sb"] = (s < t).astype(np.float32)
    n = np.arange(MLEN)
    bk = _t5_bucket(639 - n)
    oh = np.zeros((32, MLEN), np.float32)
    oh[bk, n] = 1.0
    c["c_onehot"] = oh
    return c


class _Stop(Exception):
    pass


def build(nl=2, dbg=(), stop=None):
    nc = bass.Bass("TRN2", target_bir_lowering=False)

    def din(name, shape):
        return nc.dram_tensor(name, list(shape), F32, kind="ExternalInput").ap()

    x_d = din("x", [S, D])
    lnemb_g = din("ln_emb_g", [D])
    lnemb_b = din("ln_emb_b", [D])
    table_d = din("rel_bias_table", [32, 4])
    w_in_d = din("w_in", [2, D, DIN])
    lq1_d = din("lambda_q1", [2, 64])
    lk1_d = din("lambda_k1", [2, 64])
    lq2_d = din("lambda_q2", [2, 64])
    lk2_d = din("lambda_k2", [2, 64])
    dnw_d = din("diff_norm_w", [2, 128])
    gup_d = din("gla_gate_up", [2, 2, 16, 256])
    gbias_d = din("gla_gate_bias", [2, 2, 256])
    gnw_d = din("gla_norm_w", [2, 128])
    w_o_d = din("w_o", [2, D, D])
    ln1g_d = din("ln1_g", [2, D])
    ln1b_d = din("ln1_b", [2, D])
    w1_d = din("w_ffn1", [2, D, DFF])
    b1_d = din("b_ffn1", [2, DFF])
    w2_d = din("w_ffn2", [2, DFF, D])
    b2_d = din("b_ffn2", [2, D])
    ln2g_d = din("ln2_g", [2, D])
    ln2b_d = din("ln2_b", [2, D])
    c_ident = din("c_ident", [128, 128])
    c_J = din("c_J", [128, 128])
    c_uf = din("c_uf", [128, 129])
    c_ub = din("c_ub", [128, 129])
    c_sf = din("c_sf", [128, 128])
    c_sb = din("c_sb", [128, 128])
    c_onehot = din("c_onehot", [32, MLEN])
    out_d = nc.dram_tensor("out", [S, D], F32, kind="ExternalOutput").ap()
    xs_d = nc.dram_tensor("xs_scratch", [S, D], F32).ap()
    md_t = nc.dram_tensor("md_scratch", [4, MLEN], F32)
    eb_d = nc.dram_tensor("expb_scratch", [128, 4 * 1152], BF16).ap()
    md_d = md_t.ap()
    dbg_out = {}

    sc = Sched(nc)
    es = ExitStack()
    ARENA_BYTES = 207 * 1024
    arena = es.enter_context(nc.sbuf_tensor("arena", [128, ARENA_BYTES // 2], BF16))
    PSb = es.enter_context(nc.psum_tensor("ps", [128, 8, 1024], BF16))[:]
    PS = PSb.bitcast(F32)

    def view(off, nbytes, dt, pattern=None, **kw):
        assert off % 32 == 0, off
        a = arena[:, off // 2:(off + nbytes) // 2]
        if dt is F32:
            a = a.bitcast(F32)
        if pattern:
            a = a.rearrange(pattern, **kw)
        return a

    class Alloc:
        def __init__(self, base, size):
            self.base = base
            self.size = size
            self.pos = 0

        def reset(self):
            self.pos = 0

        def get(self, nbytes, dt, pattern=None, **kw):
            n = (nbytes + 31) // 32 * 32
            assert self.pos + n <= self.size, (self.pos, n, self.size)
            v = view(self.base + self.pos, nbytes, dt, pattern, **kw)
            self.pos += n
            return v

    R_XT = Alloc(0, 32768)
    R_X = Alloc(32768, 65536)
    R_D = Alloc(98304, 70656)
    R_W = Alloc(168960, 16384)
    R_C = Alloc(185344, ARENA_BYTES - 185344)

    XT = R_XT.get(32768, BF16, "p (c n) -> p c n", c=8)
    X = R_X.get(65536, F32, "p (t n) -> p t n", t=NT)
    WB = [R_W.get(8192, BF16, "p (c n) -> p c n", c=8) for _ in range(2)]
    R_W.reset()
    WO = R_W.get(16384, BF16, "p (c n) -> p c n", c=8)

    identb = R_C.get(256, BF16)
    Jb = R_C.get(256, BF16)
    Uf = R_C.get(516, F32)
    Ub = R_C.get(516, F32)
    Usf = R_C.get(512, F32)
    Usb = R_C.get(512, F32)
    gt = R_C.get(4096, F32)
    bt = R_C.get(4096, F32)
    b2t = R_C.get(4096, F32)
    wd_t = R_C.get(512, F32)
    wg_t = R_C.get(512, F32)
    b1c = R_C.get(128, F32)
    cb = R_C.get(32, F32, "p (s h) -> p s h", s=2)
    lamv = R_C.get(4 * 64 * 4, F32, "p (a n) -> p a n", a=4)
    lamp = R_C.get(2 * 64 * 4, F32, "p (a n) -> p a n", a=2)
    lams = R_C.get(32, F32)
    Wg = R_C.get(1024, BF16)
    st_ = [R_C.get(48, F32) for _ in range(2)]
    mv_ = [R_C.get(8, F32) for _ in range(2)]
    rs_ = [R_C.get(4, F32) for _ in range(2)]
    xb_ = [R_C.get(2048, BF16) for _ in range(2)]
    dsm = [R_C.get(64, F32) for _ in range(2)]
    gsm = [R_C.get(64, F32) for _ in range(2)]
    decs = R_C.get(2 * 2 * 16 * 4, F32, "p (d q t) -> p d q t", d=2, q=2)

    def psk(b):
        return [("ps", b, q) for q in range(4)]

    def bc_mid(ap2, n):
        a = ap2.ap
        return bass.AP(ap2.tensor, ap2.offset, [list(a[0]), [0, n], list(a[1])])

    def bc_last(ap2, n):
        a = ap2.ap
        return bass.AP(ap2.tensor, ap2.offset, [list(a[0]), list(a[1]), [0, n]])

    cur_layer = [-1]

    def dump(name, ap, shape, reads):
        nm = "%s@%d" % (name, cur_layer[0])
        if nm in dbg:
            name = nm
        elif name not in dbg or (cur_layer[0] >= 0 and cur_layer[0] != nl - 1):
            return
        t = nc.dram_tensor("dbg_" + name.replace("@", "_"), list(shape), ap.dtype, kind="ExternalOutput").ap()
        dbg_out[name] = sc.dma("sp", lambda e: e.dma_start(out=t, in_=ap), "dbg", reads=reads)

    try:
        sc.dma("pool", lambda e: e.dma_start(out=identb, in_=c_ident), writes=["identb"])
        sc.dma("pool", lambda e: e.dma_start(out=Jb, in_=c_J), writes=["Jb"])
        sc.dma("sp", lambda e: e.dma_start(out=Uf, in_=c_uf), writes=["Uf"])
        sc.dma("sp", lambda e: e.dma_start(out=Ub, in_=c_ub), writes=["Ub"])
        sc.dma("sp", lambda e: e.dma_start(out=Usf, in_=c_sf), writes=["Usf"])
        sc.dma("sp", lambda e: e.dma_start(out=Usb, in_=c_sb), writes=["Usb"])
        for si, row in enumerate((15, 31)):
            sc.dma("sp", lambda e, si=si, row=row: e.dma_start(out=cb[:, si, :], in_=table_d[row, :].partition_broadcast(128)),
                   writes=[("cb", si)])

        R_D.reset()
        tb = R_D.get(16, F32)
        oh = R_D.get(MLEN * 4, F32)
        msb = R_D.get(MLEN * 4, F32)
        sc.dma("sp", lambda e: e.dma_start(out=tb[0:32, :], in_=table_d), writes=["tb"])
        sc.dma("sp", lambda e: e.dma_start(out=oh[0:32, :], in_=c_onehot), writes=["oh"])
        sc.dma("sp", lambda e: e.dma_start(out=gt, in_=lnemb_g.partition_broadcast(128)), writes=["gt"])
        sc.dma("sp", lambda e: e.dma_start(out=bt, in_=lnemb_b.partition_broadcast(128)), writes=["bt"])
        for t in range(NT):
            sc.dma("sp", lambda e, t=t: e.dma_start(out=X[:, t, :], in_=x_d[t * 128:(t + 1) * 128, :]), writes=[("X", t)])
        for ci, (c0, cn) in enumerate(((0, 512), (512, 512), (1024, 256))):
            sc.op("pe", lambda e, ci=ci, c0=c0, cn=cn: e.matmul(PS[0:4, ci, 0:cn], lhsT=tb[0:32, :], rhs=oh[0:32, c0:c0 + cn], start=True, stop=True),
                  reads=["tb", "oh"], writes=psk(ci))
            sc.op("dve", lambda e, ci=ci, c0=c0, cn=cn: e.tensor_copy(out=msb[0:4, c0:c0 + cn], in_=PS[0:4, ci, 0:cn]),
                  reads=psk(ci), writes=["msb"])
        sc.dma("sp", lambda e: e.dma_start(out=md_d, in_=msb[0:4, :]), reads=["msb"], writes=["md"])
        R_D.reset()
        R_D.get(16384, BF16); R_D.get(16384, BF16); R_D.get(16 * 4 * 129 * 2, BF16)
        expB0 = R_D.get(4 * 1152 * 2, BF16, "p (h n) -> p h n", h=4)
        R_D.get(4096, BF16); R_D.get(1024, BF16)
        tmp_revs = [view(R_D.base + 16384 + i * 2304, 2304, BF16) for i in range(4)]
        for h in range(4):
            src = bass.AP(md_t, h * MLEN, [[1, 128], [1, 1152]])
            sc.dma("pool", lambda e, src=src, h=h: e.dma_start(out=tmp_revs[h], in_=src), reads=["md"], writes=[("tmp_rev", h)])
        for h in range(4):
            for ci, (c0, cn) in enumerate(((0, 512), (512, 512), (1024, 128))):
                sc.op("pe", lambda e, ci=ci, c0=c0, cn=cn, h=h: e.matmul(PS[:, ci, 0:cn], lhsT=Jb, rhs=tmp_revs[h][:, c0:c0 + cn], start=True, stop=True),
                      reads=["Jb", ("tmp_rev", h)], writes=psk(ci))
                sc.op("act", lambda e, h=h, ci=ci, c0=c0, cn=cn: e.activation(out=expB0[:, h, c0:c0 + cn], in_=PS[:, ci, 0:cn], func=AF.Exp),
                      reads=psk(ci), writes=[("expB", h)])
        sc.dma("sp", lambda e: e.dma_start(out=eb_d, in_=expB0.rearrange("p h n -> p (h n)")), reads=[("expB", h) for h in range(4)], writes=["eb_d"])

        def ln_a_stages(t):
            Xt = X[:, t, :]
            kx = ("X", t)
            b = t % 2
            st, mv, rs = st_[b], mv_[b], rs_[b]
            return [
                lambda: sc.op("dve", lambda e: e.bn_stats(out=st[:, 0:6], in_=Xt[:, 0:512]), reads=[kx], writes=[("st", b, 0)]),
                lambda: sc.op("dve", lambda e: e.bn_stats(out=st[:, 6:12], in_=Xt[:, 512:1024]), reads=[kx], writes=[("st", b, 1)]),
                lambda: sc.op("dve", lambda e: e.bn_aggr(out=mv, in_=st), reads=[("st", b, 0), ("st", b, 1)], writes=[("mv", b)]),
                lambda: sc.op("act", lambda e: e.activation(out=rs, in_=mv[:, 1:2], func=AF.Sqrt, bias=1e-5, scale=1.0), reads=[("mv", b)], writes=[("rs", b)]),
                lambda: sc.op("dve", lambda e: e.reciprocal(out=rs, in_=rs), reads=[("rs", b)], writes=[("rs", b)]),
                lambda: sc.op("dve", lambda e: e.tensor_scalar(out=Xt, in0=Xt, scalar1=mv[:, 0:1], scalar2=rs, op0=ALU.subtract, op1=ALU.mult),
                              reads=[kx, ("mv", b), ("rs", b)], writes=[kx]),
                lambda: sc.op("dve", lambda e: e.tensor_tensor(out=Xt, in0=Xt, in1=gt, op=ALU.mult), reads=[kx, "gt"], writes=[kx]),
                lambda: sc.op("pool", lambda e: e.tensor_tensor(out=Xt, in0=Xt, in1=bt, op=ALU.add), reads=[kx, "bt"], writes=[kx]),
            ]

        def ln_a(t):
            for f in ln_a_stages(t):
                f()

        def ln_a_pair(t0, t1):
            sa, sb = ln_a_stages(t0), ln_a_stages(t1)
            for fa, fb in zip(sa, sb):
                fa()
                fb()

        def ln_b1(t, spill_to):
            Xt = X[:, t, :]
            kx = ("X", t)
            b = t % 2
            xb = xb_[b]
            if spill_to is not None:
                sc.dma("sp", lambda e: e.dma_start(out=spill_to[t * 128:(t + 1) * 128, :], in_=Xt), ("xs", t % 4), reads=[kx], writes=[("xsd", t)])
            if spill_to is out_d:
                return
            sc.op("act", lambda e: e.activation(out=xb, in_=Xt, func=AF.Copy), reads=[kx], writes=[("xb", b)])

        def ln_b2(t, spill_to):
            if spill_to is out_d:
                return
            b = t % 2
            xb = xb_[b]
            bank = 6 + b
            for c in range(8):
                sc.op("pe", lambda e, c=c: e.transpose(out=PSb[:, bank, c * 128:(c + 1) * 128], in_=xb[:, c * 128:(c + 1) * 128], identity=identb),
                      reads=[("xb", b), "identb"], writes=psk(bank))
            sc.op("act", lambda e: e.activation(out=XT[:, :, t * 128:(t + 1) * 128], in_=PSb[:, bank, :].rearrange("p (c n) -> p c n", c=8), func=AF.Copy),
                  reads=psk(bank), writes=[("XT", t)])

        def ln_all(spill_to):
            ln_a_pair(0, 1)
            for t in range(0, NT, 2):
                if t + 2 < NT:
                    ln_a_pair(t + 2, t + 3)
                ln_b1(t, spill_to)
                ln_b1(t + 1, spill_to)
                ln_b2(t, spill_to)
                ln_b2(t + 1, spill_to)

        def load_ln_params(g_ap, b_ap):
            sc.dma("sp", lambda e: e.dma_start(out=gt, in_=g_ap.partition_broadcast(128)), writes=["gt"])
            sc.dma("sp", lambda e: e.dma_start(out=bt, in_=b_ap.partition_broadcast(128)), writes=["bt"])

        def load_win_block(l, blk, buf):
            c0 = blk * 512
            ncol = min(512, DIN - c0)
            src = w_in_d[l, :, c0:c0 + ncol].rearrange("(c p) n -> p c n", p=128)
            for hf in range(2):
                sc.dma("pool", lambda e, hf=hf: e.dma_start(out=WB[buf][:, hf * 4:(hf + 1) * 4, 0:ncol], in_=src[:, hf * 4:(hf + 1) * 4, :]),
                       writes=[("RW", buf, hf)])

        if stop == 'init':
            raise _Stop()
        load_win_block(0, 0, 0)
        load_win_block(0, 1, 1)
        ln_all(xs_d)
        dump("h0", X, [128, NT, 1024], [("X", t) for t in range(NT)])

        if stop == 'emb':
            raise _Stop()
        evac_rr = [0]

        def evac(out, in_, reads, writes, scale=None):
            evac_rr[0] ^= 1
            if evac_rr[0]:
                if scale is None:
                    sc.op("act", lambda e: e.activation(out=out, in_=in_, func=AF.Copy), reads=reads, writes=writes)
                else:
                    sc.op("act", lambda e: e.mul(out=out, in_=in_, mul=scale), reads=reads, writes=writes)
            else:
                if scale is None:
                    sc.op("dve", lambda e: e.tensor_copy(out=out, in_=in_), reads=reads, writes=writes)
                else:
                    sc.op("dve", lambda e: e.tensor_scalar(out=out, in0=in_, scalar1=scale, scalar2=None, op0=ALU.mult), reads=reads, writes=writes)

        def do_layer(l):
            lam_init = 0.8 - 0.6 * math.exp(-0.3 * l)
            cur_layer[0] = l
            if l == 0:
                sc.barrier()
            last = (l == nl - 1)
            R_D.reset()
            QT = R_D.get(16384, BF16, "p (h n) -> p h n", h=4)
            KT = R_D.get(16384, BF16, "p (h n) -> p h n", h=4)
            V = R_D.get(16 * 4 * 129 * 2, BF16, "p (t h e) -> p t h e", t=16, h=4)
            expB = R_D.get(4 * 1152 * 2, BF16, "p (h n) -> p h n", h=4)
            Eb = R_D.get(4096, BF16, "p (b m n) -> p b m n", b=2, m=2)
            d_y = R_D.get(4 * 128 * 2, BF16, "p (u n) -> p u n", u=4)
            _pu = R_D.pos
            silu_t = [R_D.get(2048, F32) for _ in range(2)]
            R_D.pos = _pu
            accS = R_D.get(8 * 129 * 4, F32, "p (a n) -> p a n", a=8)
            R_D.pos = _pu + 4608
            tmp_rev = R_D.get(1152 * 2, BF16)
            R_X.reset()
            gqT = R_X.get(8192, BF16, "p (c n) -> p c n", c=2)
            gkT = R_X.get(8192, BF16, "p (c n) -> p c n", c=2)
            gk_tok = R_X.get(8192, BF16, "p (t n) -> p t n", t=16)
            gv = R_X.get(16384, BF16, "p (t n) -> p t n", t=16)
            gr_s = R_X.get(16384, BF16, "p (t n) -> p t n", t=16)
            G33 = R_X.get(4096, BF16)

            for i, ap in enumerate((lq1_d, lk1_d, lq2_d, lk2_d)):
                sc.dma("sp", lambda e, i=i, ap=ap: e.dma_start(out=lamv[:, i, :], in_=ap[l, :].partition_broadcast(128)), writes=[("lamv", i)])
            sc.op("dve", lambda e: e.tensor_tensor(out=lamp[:, 0, :], in0=lamv[:, 0, :], in1=lamv[:, 1, :], op=ALU.mult), reads=[("lamv", 0), ("lamv", 1)], writes=["lamp"])
            sc.op("dve", lambda e: e.tensor_tensor(out=lamp[:, 1, :], in0=lamv[:, 2, :], in1=lamv[:, 3, :], op=ALU.mult), reads=[("lamv", 2), ("lamv", 3)], writes=["lamp"])
            sc.op("dve", lambda e: e.reduce_sum(out=lams[:, 0:2], in_=lamp, axis=mybir.AxisListType.X), reads=["lamp"], writes=["lams"])
            sc.op("act", lambda e: e.activation(out=lams[:, 0:2], in_=lams[:, 0:2], func=AF.Exp), reads=["lams"], writes=["lams"])
            sc.op("dve", lambda e: e.tensor_tensor(out=lams[:, 2:3], in0=lams[:, 0:1], in1=lams[:, 1:2], op=ALU.subtract), reads=["lams"], writes=["lams"])
            sc.op("dve", lambda e: e.tensor_scalar(out=lams[:, 3:4], in0=lams[:, 2:3], scalar1=lam_init, scalar2=-1.0, op0=ALU.add, op1=ALU.mult),
                  reads=["lams"], writes=["neglam"])
            neg_lam = lams[:, 3:4]
            sc.dma("sp", lambda e: e.dma_start(out=wd_t, in_=dnw_d[l, :].partition_broadcast(128)), writes=["wd"])
            sc.op("dve", lambda e: e.tensor_scalar(out=wd_t, in0=wd_t, scalar1=1.0 - lam_init, scalar2=None, op0=ALU.mult), reads=["wd"], writes=["wd"])
            sc.dma("sp", lambda e: e.dma_start(out=wg_t, in_=gnw_d[l, :].partition_broadcast(128)), writes=["wg"])
            sc.op("dve", lambda e: e.memset(Wg[0:33, :], 0.0), writes=["Wg"])
            sc.dma("pool", lambda e: e.dma_start(out=Wg[0:16, 0:256], in_=gup_d[l, 0]), writes=["Wg"])
            sc.dma("pool", lambda e: e.dma_start(out=Wg[16:32, 256:512], in_=gup_d[l, 1]), writes=["Wg"])
            sc.dma("pool", lambda e: e.dma_start(out=Wg[32:33, :], in_=gbias_d[l].rearrange("a n -> (a n)").partition_broadcast(1)), writes=["Wg"])
            sc.dma("sp", lambda e: e.dma_start(out=b1c, in_=b1_d[l].rearrange("(c p) -> p c", p=128), allow_slow_non_contiguous=True), writes=["b1c"])
            sc.dma("sp", lambda e: e.dma_start(out=b2t, in_=b2_d[l].partition_broadcast(128)), writes=["b2t"])
            sc.dma("sp", lambda e: e.dma_start(out=expB.rearrange("p h n -> p (h n)"), in_=eb_d), reads=["eb_d"], writes=[("expB", h) for h in range(4)])
            sc.op("dve", lambda e: e.memset(V[:, :, :, 128:129], 1.0), writes=[("V", t) for t in range(NT)])
            sc.op("dve", lambda e: e.memset(G33[32:33, :], 1.0), writes=["G33"])

            dump("XTin", XT, [128, 8, 2048], [("XT", t) for t in range(NT)])
            dump("Xin", X, [128, NT, 1024], [("X", t) for t in range(NT)])
            if stop == 'L' and l == nl - 1:
                raise _Stop()
            ps_rr = [0]

            def nextbank():
                b = ps_rr[0] % 6
                ps_rr[0] += 1
                return b

            xt_all = [("XT", t) for t in range(NT)]
            for blk in range(7):
                buf = blk % 2
                wb = WB[buf]
                kw = [("RW", buf, 0), ("RW", buf, 1)]
                if blk in (0, 1, 3, 6):
                    nch = 1 if blk == 6 else 4
                    for cc in range(nch):
                        for r in range(4):
                            bank = nextbank()
                            M = 32 if blk == 6 else 128
                            for c in range(8):
                                sc.op("pe", lambda e, c=c, cc=cc, r=r, bank=bank, M=M, wb=wb: e.matmul(
                                    PS[0:M, bank, :], lhsT=wb[:, c, cc * 128:cc * 128 + M], rhs=XT[:, c, r * 512:(r + 1) * 512],
                                    start=(c == 0), stop=(c == 7)), reads=kw + xt_all[r * 4:(r + 1) * 4], writes=psk(bank))
                            sl = slice(r * 512, (r + 1) * 512)
                            if blk == 0:
                                evac(QT[:, cc, sl], PS[:, bank, :], psk(bank), [("QT", cc, r)], scale=0.125)
                            elif blk == 1:
                                evac(KT[:, cc, sl], PS[:, bank, :], psk(bank), [("KT", cc, r)])
                            elif blk == 3:
                                if cc < 2:
                                    evac(gqT[:, cc, sl], PS[:, bank, :], psk(bank), [("gqT", 4 * r + i) for i in range(4)], scale=0.125)
                                else:
                                    evac(gkT[:, cc - 2, sl], PS[:, bank, :], psk(bank), [("gkT", 4 * r + i) for i in range(4)])
                            else:
                                evac(G33[0:32, sl], PS[0:32, bank, :], psk(bank), ["G33"])
                if blk == 3:
                    for t in range(NT):
                        bank = nextbank()
                        for cc in range(2):
                            sc.op("pe", lambda e, t=t, cc=cc, bank=bank: e.transpose(out=PSb[:, bank, cc * 128:(cc + 1) * 128], in_=gkT[:, cc, t * 128:(t + 1) * 128], identity=identb),
                                  reads=[("gkT", t), "identb"], writes=psk(bank))
                        evac(gk_tok[:, t, :], PSb[:, bank, 0:256], psk(bank), [("gk_tok", t)])
                if blk in (2, 4, 5):
                    for t in range(NT):
                        bank = nextbank()
                        c0, ncol = (0, 512)
                        for c in range(8):
                            sc.op("pe", lambda e, c=c, t=t, bank=bank, c0=c0, ncol=ncol, wb=wb: e.matmul(
                                PS[:, bank, 0:ncol], lhsT=XT[:, c, t * 128:(t + 1) * 128], rhs=wb[:, c, c0:c0 + ncol],
                                start=(c == 0), stop=(c == 7)), reads=kw + [("XT", t)], writes=psk(bank))
                        if blk == 2:
                            evac(V[:, t, :, 0:128], PS[:, bank, :].rearrange("p (h e) -> p h e", h=4), psk(bank), [("V", t)])
                        elif blk == 4:
                            evac(gv[:, t, :], PS[:, bank, :], psk(bank), [("gv", t)])
                        else:
                            sb = t % 2
                            sc.op("act", lambda e, bank=bank, sb=sb: e.activation(out=silu_t[sb], in_=PS[:, bank, :], func=AF.Silu),
                                  reads=psk(bank), writes=[("silu", sb)])
                            sc.op("dve", lambda e, t=t, sb=sb: e.tensor_tensor(
                                out=gr_s[:, t, :].rearrange("p (h e) -> p h e", h=4), in0=silu_t[sb].rearrange("p (h e) -> p h e", h=4),
                                in1=bc_mid(wg_t, 4), op=ALU.mult), reads=[("silu", sb), "wg"], writes=[("gr_s", t)])
                if blk + 2 < 7:
                    load_win_block(l, blk + 2, buf)
            for hf in range(2):
                sc.dma("pool", lambda e, hf=hf: e.dma_start(out=WO[:, hf * 4:(hf + 1) * 4, :],
                                                             in_=w_o_d[l].rearrange("(c p) n -> p c n", p=128)[:, hf * 4:(hf + 1) * 4, :]),
                       writes=[("RW", hf, 0), ("RW", hf, 1)])
            dump("QT", QT, [128, 4, 2048], [("QT", a, b) for a in range(4) for b in range(4)])
            dump("KT", KT, [128, 4, 2048], [("KT", a, b) for a in range(4) for b in range(4)])
            dump("V", V, [128, 16, 4, 129], [("V", t) for t in range(NT)])
            dump("expB", expB, [128, 4, 1152], [("expB", h) for h in range(4)])
            dump("gqT", gqT, [128, 2, 2048], [("gqT", t) for t in range(NT)])
            dump("gr_s", gr_s, [128, 16, 512], [("gr_s", t) for t in range(NT)])
            dump("G33", G33[0:33, :], [33, 2048], ["G33"])

            if stop == 'P' and l == nl - 1:
                raise _Stop()
            steps = [(h, r, j) for h in range(4) for r in range(4) for j in range(16)]

            def acc_ap(m, u):
                idx = m * 4 + u
                return PS[:, 4 + idx // 3, (idx % 3) * 160:(idx % 3) * 160 + 129]

            def acc_keys(m, u):
                return psk(4 + (m * 4 + u) // 3)

            Eb3 = view(R_X.base + 61440, 2048, BF16, "p (m n) -> p m n", m=2)
            EbL = [Eb[:, 0, :, :], Eb[:, 1, :, :], Eb3]

            def d_scores(i):
                h, r, j = steps[i]
                d = j - 4 * r
                mixed = (-1 <= d <= 4)
                sb = i % 2
                eb = i % 3
                E = EbL[eb]
                for m in range(2):
                    bank = sb * 2 + m
                    sc.op("pe", lambda e, h=h, r=r, j=j, m=m, bank=bank: e.matmul(
                        PS[:, bank, :], lhsT=KT[64 * m:64 * m + 64, h, j * 128:(j + 1) * 128],
                        rhs=QT[64 * m:64 * m + 64, h, r * 512:(r + 1) * 512], start=True, stop=True),
                        reads=[("KT", h, j // 4), ("QT", h, r)], writes=psk(bank))
                pk2 = psk(sb * 2) + psk(sb * 2 + 1)
                ek = [("E", eb, 0), ("E", eb, 1)]
                if mixed:
                    c0 = (4 - d) * 128
                    sc.op("act", lambda e, sb=sb, E=E: e.activation(out=E, in_=PS[:, sb * 2:sb * 2 + 2, :], func=AF.Exp),
                          reads=pk2, writes=ek)
                    for m in range(2):
                        sc.op("dve", lambda e, E=E, m=m, h=h, c0=c0: e.tensor_tensor(out=E[:, m, :], in0=E[:, m, :], in1=expB[:, h, c0:c0 + 512], op=ALU.mult),
                              reads=[("E", eb, m), ("expB", h)], writes=[("E", eb, m)])
                else:
                    side = 0 if d < 0 else 1
                    sc.op("act", lambda e, sb=sb, E=E, side=side, h=h: e.activation(
                        out=E, in_=PS[:, sb * 2:sb * 2 + 2, :], func=AF.Exp, bias=cb[:, side, h:h + 1]),
                        reads=pk2 + [("cb", 0), ("cb", 1)], writes=ek)

            def d_av(i):
                h, r, j = steps[i]
                eb = i % 3
                E = EbL[eb]
                for m in range(2):
                    for u in range(4):
                        sc.op("pe", lambda e, h=h, j=j, m=m, u=u, E=E: e.matmul(
                            acc_ap(m, u), lhsT=E[:, m, u * 128:(u + 1) * 128], rhs=V[:, j, h, 0:129],
                            start=(j == 0 and (m * 4 + u) % 3 == 0), stop=(j == 15), skip_group_check=True),
                            reads=[("E", eb, m), ("V", j)], writes=acc_keys(m, u))

            sm = dsm[0]
            def ak(*idx):
                return [("accS", i) for i in idx]

            def d_final(h, r):
                sc.op("dve", lambda e: e.tensor_copy(out=accS[:, 0:3, :], in_=PS[:, 4, 0:480].rearrange("p (a n) -> p a n", a=3)[:, :, 0:129]),
                      reads=psk(4), writes=ak(0, 1, 2) + [("silu", 0), ("silu", 1)])
                sc.op("dve", lambda e: e.tensor_copy(out=accS[:, 3:6, :], in_=PS[:, 5, 0:480].rearrange("p (a n) -> p a n", a=3)[:, :, 0:129]),
                      reads=psk(5), writes=ak(3, 4, 5))
                sc.op("dve", lambda e: e.tensor_copy(out=accS[:, 6:8, :], in_=PS[:, 6, 0:320].rearrange("p (a n) -> p a n", a=2)[:, :, 0:129]),
                      reads=psk(6), writes=ak(6, 7))

            def d_final2(h, r):
                sc.op("dve", lambda e: e.reciprocal(out=sm[:, 0:8], in_=accS[:, :, 128]), reads=ak(*range(8)), writes=["dsm"])
                sc.op("dve", lambda e: e.tensor_scalar(out=sm[:, 4:8], in0=sm[:, 4:8], scalar1=neg_lam, scalar2=None, op0=ALU.mult),
                      reads=["dsm", "neglam"], writes=["dsm"])
                sc.op("dve", lambda e: e.memset(sm[:, 8:12], 0.0), writes=[("dss", u) for u in range(4)])

            def d_final_u(u):
                if True:
                    sc.op("dve", lambda e, u=u: e.tensor_scalar(out=accS[:, u, 0:128], in0=accS[:, u, 0:128], scalar1=sm[:, u:u + 1], scalar2=None, op0=ALU.mult),
                          reads=["dsm"] + ak(u), writes=ak(u))
                    sc.op("dve", lambda e, u=u: e.scalar_tensor_tensor(out=accS[:, u, 0:128], in0=accS[:, 4 + u, 0:128], scalar=sm[:, 4 + u:5 + u],
                                                                       in1=accS[:, u, 0:128], op0=ALU.mult, op1=ALU.add),
                          reads=["dsm"] + ak(u, 4 + u), writes=ak(u))
                    sc.op("dve", lambda e, u=u: e.scalar_tensor_tensor(out=accS[:, 4 + u, 0:128], in0=accS[:, u, 0:128], scalar=1.0, in1=accS[:, u, 0:128],
                                                                       op0=ALU.mult, op1=ALU.mult, accum_out=sm[:, 8 + u:9 + u]),
                          reads=ak(u), writes=ak(4 + u) + [("dss", u)])

            def d_final_b(h, r):
                sc.op("act", lambda e: e.activation(out=sm[:, 12:16], in_=sm[:, 8:12], func=AF.Ln, bias=1e-5, scale=1.0 / 128),
                      reads=[("dss", u) for u in range(4)], writes=["drs"])
                sc.op("act", lambda e: e.activation(out=sm[:, 12:16], in_=sm[:, 12:16], func=AF.Exp, scale=-0.5), reads=["drs"], writes=["drs"])
                for u in range(4):
                    sc.op("dve", lambda e, u=u: e.scalar_tensor_tensor(out=d_y[:, u, :], in0=accS[:, u, 0:128], scalar=sm[:, 12 + u:13 + u], in1=wd_t,
                                                                       op0=ALU.mult, op1=ALU.mult),
                          reads=ak(u) + ["drs", "wd"], writes=[("dy", u)])

            def d_final_pe(h, r):
                for u in range(4):
                    sc.op("pe", lambda e, u=u: e.transpose(out=PSb[:, 7, u * 128:(u + 1) * 128], in_=d_y[:, u, :], identity=identb),
                          reads=[("dy", u), "identb"], writes=psk(7))
                sc.op("dve", lambda e, h=h, r=r: e.tensor_copy(out=XT[:, h, r * 512:(r + 1) * 512], in_=PSb[:, 7, 0:512]),
                      reads=psk(7), writes=[("XT", 4 * r + i) for i in range(4)])

            pend = []
            pend_b = []
            pend_u = []
            d_scores(0)
            d_scores(1)
            for i in range(len(steps)):
                h, r, j = steps[i]
                if j == 15:
                    d_av(i)
                    d_final(h, r)
                    if i + 2 < len(steps):
                        d_scores(i + 2)
                    d_final2(h, r)
                else:
                    if i + 2 < len(steps):
                        d_scores(i + 2)
                    d_av(i)
                if j == 15:
                    pend.append((h, r))
                    pend_b.append((h, r))
                    pend_u.extend([0, 1, 2, 3])
                    d_final_u(pend_u.pop(0))
                elif pend_u:
                    d_final_u(pend_u.pop(0))
                elif j == 4 and pend_b:
                    d_final_b(*pend_b.pop(0))
                elif j == 7 and pend:
                    d_final_pe(*pend.pop(0))
            while pend_u:
                d_final_u(pend_u.pop(0))
            while pend_b:
                d_final_b(*pend_b.pop(0))
            while pend:
                d_final_pe(*pend.pop(0))
            dump("mixT_d", XT, [128, 8, 2048], [("XT", t) for t in range(NT)])

            if stop == 'D' and l == nl - 1:
                raise _Stop()
            sc.barrier()
            R_D.reset()
            qf = R_D.get(8192, BF16, "p (c n) -> p c n", c=2)
            kf = R_D.get(8192, BF16, "p (c n) -> p c n", c=2)
            kd_f = R_D.get(8192, BF16, "p (t n) -> p t n", t=16)
            Sbf = R_D.get(16384, BF16, "p (d q t e) -> p d q t e", d=2, q=2, t=16)
            stm2 = [R_D.get(4096, F32, "p (d q n) -> p d q n", d=2, q=2) for _ in range(2)]
            _p0 = R_D.pos
            sp_ = [R_D.get(2048, F32) for _ in range(2)]
            _p1 = R_D.pos
            ebt = [R_D.get(2 * 2 * 129 * 4, F32, "p (d q n) -> p d q n", d=2, q=2) for _ in range(2)]
            _p2 = R_D.pos
            enbt = [R_D.get(2 * 2 * 128 * 4, F32, "p (d q n) -> p d q n", d=2, q=2) for _ in range(2)]
            erem = [R_D.get(2048, F32) for _ in range(2)]
            dS = [R_D.get(1024, F32) for _ in range(4)]
            _pend = R_D.pos
            Am = [R_X.get(4 * 2 * 128 * 2, BF16, "p (h d n) -> p h d n", h=4, d=2) for _ in range(2)]
            R_D.pos = _p2
            g_y = [R_D.get(1024, BF16) for _ in range(2)]
            g_junk = R_D.get(512, F32)
            R_D.pos = _pend
            qb, kb, kd_b = gqT, gkT, gk_tok
            maskf = Uf[:, 0:128]
            maskb = Usf

            def tl(t):
                return slice(t * 128, (t + 1) * 128)

            def prep_A(t):
                b = t % 2
                sp = sp_[b]
                zb = 0 if b == 0 else 7
                sc.op("pe", lambda e: e.matmul(PS[:, zb, :], lhsT=G33[0:33, tl(t)], rhs=Wg[0:33, :], start=True, stop=True),
                      reads=["G33", "Wg"], writes=psk(zb))
                sc.op("act", lambda e: e.activation(out=sp, in_=PS[:, zb, :], func=AF.Exp, scale=-1.0), reads=psk(zb), writes=[("sp", b)])
                sc.op("act", lambda e: e.activation(out=sp, in_=sp, func=AF.Ln, bias=1.0, scale=1.0), reads=[("sp", b)], writes=[("sp", b)])

            prep_A(0)
            for t in range(NT):
                b = t % 2
                sp = sp_[b]
                if t + 1 < NT:
                    prep_A(t + 1)
                sc.op("pe", lambda e, sp=sp: e.matmul(PS[:, 1, 0:256], lhsT=Usf, rhs=sp[:, 0:256], start=True, stop=True), reads=[("sp", b), "Usf", "Usb"], writes=psk(1))
                sc.op("pe", lambda e, sp=sp: e.matmul(PS[:, 1, 256:512], lhsT=Usb, rhs=sp[:, 256:512], start=True, stop=True), reads=[("sp", b), "Usf", "Usb"], writes=psk(1))
                sc.op("act", lambda e, b=b: e.activation(out=erem[b], in_=PS[:, 1, :], func=AF.Exp, scale=-1.0 / 16), reads=psk(1), writes=[("erem", b)])
                sc.op("dve", lambda e, t=t, b=b: e.tensor_tensor(out=kd_f[:, t, :], in0=gk_tok[:, t, :], in1=erem[b][:, 0:256], op=ALU.mult),
                      reads=[("gk_tok", t), ("erem", b)], writes=[("kd_f", t)])
                sc.op("dve", lambda e, t=t, b=b: e.tensor_tensor(out=kd_b[:, t, :], in0=gk_tok[:, t, :], in1=erem[b][:, 256:512], op=ALU.mult),
                      reads=[("gk_tok", t), ("erem", b), ("kd_f", t)], writes=[("gk_tok", t)])
                for d in range(2):
                    U = Uf if d == 0 else Ub
                    for q in range(2):
                        sc.op("pe", lambda e, sp=sp, d=d, q=q, U=U: e.matmul(PS[:, 2 + d, q * 160:q * 160 + 129],
                                                                            lhsT=sp[:, d * 256 + q * 128:d * 256 + (q + 1) * 128], rhs=U, start=True, stop=True),
                              reads=[("sp", b), "Uf", "Ub"], writes=psk(2 + d))
                src4 = PS[:, 2:4, 0:320].rearrange("p a (q n) -> p a q n", q=2)
                sc.op("act", lambda e, b=b, src4=src4: e.activation(out=ebt[b], in_=src4[:, :, :, 0:129], func=AF.Exp, scale=-1.0 / 16),
                      reads=psk(2) + psk(3), writes=[("eb", b, 0), ("eb", b, 1)])
                sc.op("act", lambda e, b=b, src4=src4: e.activation(out=enbt[b], in_=src4[:, :, :, 0:128], func=AF.Exp, scale=1.0 / 16),
                      reads=psk(2) + psk(3), writes=[("enb", b, 0), ("enb", b, 1)])
                sc.op("dve", lambda e, t=t, b=b: e.tensor_tensor(out=qf[:, :, tl(t)], in0=gqT[:, :, tl(t)], in1=ebt[b][:, 0, :, 0:128], op=ALU.mult),
                      reads=[("gqT", t), ("eb", b, 0)], writes=[("qf", t)])
                sc.op("dve", lambda e, t=t, b=b: e.tensor_tensor(out=kf[:, :, tl(t)], in0=gkT[:, :, tl(t)], in1=enbt[b][:, 0, :, :], op=ALU.mult),
                      reads=[("gkT", t), ("enb", b, 0)], writes=[("kf", t)])
                sc.op("dve", lambda e, t=t, b=b: e.tensor_tensor(out=qb[:, :, tl(t)], in0=gqT[:, :, tl(t)], in1=ebt[b][:, 1, :, 0:128], op=ALU.mult),
                      reads=[("gqT", t), ("eb", b, 1), ("qf", t)], writes=[("gqT", t)])
                sc.op("dve", lambda e, t=t, b=b: e.tensor_tensor(out=kb[:, :, tl(t)], in0=gkT[:, :, tl(t)], in1=enbt[b][:, 1, :, :], op=ALU.mult),
                      reads=[("gkT", t), ("enb", b, 1), ("kf", t)], writes=[("gkT", t)])
                sc.op("dve", lambda e, t=t, b=b: e.tensor_copy(out=decs[:, :, :, t:t + 1], in_=ebt[b][:, :, :, 128:129]),
                      reads=[("eb", b, 0), ("eb", b, 1)], writes=[("decs", t)])
            dump("qf", qf, [128, 2, 2048], [("qf", t) for t in range(NT)])
            dump("kd_f", kd_f, [128, 16, 256], [("kd_f", t) for t in range(NT)])
            dump("decs", decs, [128, 2, 2, 16], [("decs", t) for t in range(NT)])

            if stop == 'G1' and l == nl - 1:
                raise _Stop()
            sc.op("dve", lambda e: e.memset(stm2[0], 0.0), writes=[("stm", 0, d, q) for d in range(2) for q in range(2)])
            chains = [(d, q) for d in range(2) for q in range(2)]
            par = {c: 0 for c in chains}
            for i in range(NT):
                todo = []
                for ci, (d, q) in enumerate(chains):
                    t = i if d == 0 else NT - 1 - i
                    cur = par[(d, q)]
                    if i > 0:
                        sc.op("act", lambda e, d=d, q=q, t=t, cur=cur: e.activation(out=Sbf[0:64, d, q, t, :], in_=stm2[cur][0:64, d, q, 0:128], func=AF.Copy),
                              reads=[("stm", cur, d, q)], writes=[("Sbf", d, q, t)])
                        sc.op("dve", lambda e, d=d, q=q, t=t, cur=cur: e.tensor_copy(out=Sbf[64:128, d, q, t, :], in_=stm2[cur][64:128, d, q, 128:256]),
                              reads=[("stm", cur, d, q)], writes=[("Sbf", d, q, t)])
                    if i == NT - 1:
                        continue
                    kd = kd_f if d == 0 else kd_b
                    kkey = "kd_f" if d == 0 else "gk_tok"
                    pslot = ci % 2
                    pk = psk(4 + pslot)
                    sc.op("pe", lambda e, kd=kd, t=t, q=q, pslot=pslot: e.matmul(PS[:, 4 + pslot, 0:256], lhsT=kd[:, t, q * 128:(q + 1) * 128],
                                                                                rhs=gv[:, t, q * 256:(q + 1) * 256], start=True, stop=True),
                          reads=[(kkey, t), ("gv", t)], writes=pk)
                    if os.environ.get("GSKIP") != "evac":
                        sc.op("act", lambda e, ci=ci, pslot=pslot: e.activation(out=dS[ci], in_=PS[:, 4 + pslot, 0:256], func=AF.Copy),
                              reads=pk, writes=[("dS", ci)])
                    todo.append((ci, d, q, t, cur))
                for (ci, d, q, t, cur) in todo:
                    if os.environ.get("GSKIP") == "upd":
                        par[(d, q)] = 1 - cur
                        continue
                    sc.op("dve", lambda e, ci=ci, d=d, q=q, t=t, cur=cur: e.scalar_tensor_tensor(
                        out=stm2[1 - cur][:, d, q, :], in0=stm2[cur][:, d, q, :], scalar=decs[:, d, q, t:t + 1], in1=dS[ci],
                        op0=ALU.mult, op1=ALU.add), reads=[("stm", cur, d, q), ("decs", t), ("dS", ci)], writes=[("stm", 1 - cur, d, q)])
                    par[(d, q)] = 1 - cur
            dump("Sbf", Sbf, [128, 2, 2, 16, 128], [("Sbf", d, q, t) for d in range(2) for q in range(2) for t in range(NT)])
            if stop == 'G2' and l == nl - 1:
                raise _Stop()

            def g_A(t):
                b = t % 2
                A = Am[b]
                for half in range(2):
                    sbank = (4 + half) if os.environ.get('GBANK') else (2 * b + half)
                    items = []
                    for sq in range(4):
                        h, d = half + 2 * (sq // 2), sq % 2
                        q = h // 2
                        base = (h % 2) * 64
                        kk = kf if d == 0 else kb
                        qq = qf if d == 0 else qb
                        kkey = ("kf", t) if d == 0 else ("gkT", t)
                        qkey = ("qf", t) if d == 0 else ("gqT", t)
                        sc.op("pe", lambda e, kk=kk, qq=qq, q=q, base=base, sbank=sbank, sq=sq: e.matmul(
                            PS[:, sbank, sq * 128:(sq + 1) * 128], lhsT=kk[base:base + 64, q, tl(t)], rhs=qq[base:base + 64, q, tl(t)], start=True, stop=True),
                            reads=[kkey, qkey], writes=psk(sbank))
                        items.append((sq, h, d))
                        if os.environ.get("GOLD"):
                            mk = maskf if d == 0 else maskb
                            sc.op("dve", lambda e, A=A, h=h, d=d, sbank=sbank, sq=sq, mk=mk: e.tensor_tensor(
                                out=A[:, h, d, :], in0=PS[:, sbank, sq * 128:(sq + 1) * 128], in1=mk, op=ALU.mult),
                                reads=psk(sbank) + ["Uf", "Usf"], writes=[("A", b, h, d)])
                    if os.environ.get("GOLD"):
                        continue
                    for (sq, h, d) in items:
                        mk = maskf if d == 0 else maskb
                        sc.op("dve", lambda e, A=A, h=h, d=d, sbank=sbank, sq=sq, mk=mk: e.tensor_tensor(
                            out=A[:, h, d, :], in0=PS[:, sbank, sq * 128:(sq + 1) * 128], in1=mk, op=ALU.mult),
                            reads=psk(sbank) + ["Uf", "Usf"], writes=[("A", b, h, d)])

            def g_B(t):
                b = t % 2
                A = Am[b]
                obank = (0 + b) if os.environ.get('GBANK') else (4 + b)
                for h in range(4):
                    q = h // 2
                    base = (h % 2) * 64
                    oh_ = PS[:, obank, h * 128:(h + 1) * 128]
                    ok = psk(obank)
                    inter_f = t > 0
                    inter_b = t < NT - 1
                    sc.op("pe", lambda e, A=A, h=h, oh_=oh_: e.matmul(oh_, lhsT=A[:, h, 0, :], rhs=gv[:, t, h * 128:(h + 1) * 128], start=True, stop=False),
                          reads=[("A", b, h, 0), ("gv", t)], writes=ok)
                    sc.op("pe", lambda e, A=A, h=h, oh_=oh_, fin=(not inter_f and not inter_b): e.matmul(
                        oh_, lhsT=A[:, h, 1, :], rhs=gv[:, t, h * 128:(h + 1) * 128], start=False, stop=fin),
                        reads=[("A", b, h, 1), ("gv", t)], writes=ok)
                    if inter_f:
                        sc.op("pe", lambda e, q=q, base=base, oh_=oh_, fin=(not inter_b): e.matmul(
                            oh_, lhsT=qf[base:base + 64, q, tl(t)], rhs=Sbf[base:base + 64, 0, q, t, :], start=False, stop=fin),
                            reads=[("qf", t), ("Sbf", 0, q, t)], writes=ok)
                    if inter_b:
                        sc.op("pe", lambda e, q=q, base=base, oh_=oh_: e.matmul(
                            oh_, lhsT=qb[base:base + 64, q, tl(t)], rhs=Sbf[base:base + 64, 1, q, t, :], start=False, stop=True),
                            reads=[("gqT", t), ("Sbf", 1, q, t)], writes=ok)

            def g_norm(t):
                b = t % 2
                sm = gsm[b]
                obank = (0 + b) if os.environ.get('GBANK') else (4 + b)
                okall = psk(obank)
                for h in range(4):
                    sc.op("act", lambda e, h=h, sm=sm: e.activation(out=g_junk, in_=PS[:, obank, h * 128:(h + 1) * 128], func=AF.Square, accum_out=sm[:, h:h + 1]),
                          reads=okall, writes=[("gss", b, h)] + ([("enb", 1, 0), ("enb", 1, 1)] if h == 0 else []))
                sc.op("act", lambda e, sm=sm: e.activation(out=sm[:, 4:8], in_=sm[:, 0:4], func=AF.Sqrt, bias=1e-5, scale=1.0 / 128),
                      reads=[("gss", b, h) for h in range(4)], writes=[("grs", b)])
                sc.op("dve", lambda e, sm=sm: e.reciprocal(out=sm[:, 4:8], in_=sm[:, 4:8]), reads=[("grs", b)], writes=[("grs", b)])
                for h in range(4):
                    sc.op("dve", lambda e, h=h, sm=sm, b=b: e.scalar_tensor_tensor(
                        out=g_y[b][:, h * 128:(h + 1) * 128], in0=PS[:, obank, h * 128:(h + 1) * 128], scalar=sm[:, 4 + h:5 + h],
                        in1=gr_s[:, t, h * 128:(h + 1) * 128], op0=ALU.mult, op1=ALU.mult),
                        reads=psk(obank) + [("grs", b), ("gr_s", t)], writes=[("gy", b, h)] + ([("enb", 0, 0), ("enb", 0, 1)] if h == 0 else []))

            def g_tr(t):
                b = t % 2
                tk = psk(6 + b)
                for h in range(4):
                    sc.op("pe", lambda e, h=h, b=b: e.transpose(out=PSb[:, 6 + b, h * 128:(h + 1) * 128], in_=g_y[b][:, h * 128:(h + 1) * 128], identity=identb),
                          reads=[("gy", b, h), "identb"], writes=tk)
                sc.op("act", lambda e, b=b: e.activation(out=XT[:, 4:8, tl(t)], in_=PSb[:, 6 + b, 0:512].rearrange("p (c n) -> p c n", c=4), func=AF.Copy),
                      reads=tk, writes=[("XT", t)])

            g_A(0)
            for t in range(NT):
                if t + 1 < NT:
                    g_A(t + 1)
                if os.environ.get("GSKIP") == "B":
                    continue
                g_B(t)
                if os.environ.get("GSKIP") == "norm":
                    continue
                g_norm(t)
                if os.environ.get("GSKIP") == "tr":
                    continue
                if t > 0:
                    g_tr(t - 1)
            if not os.environ.get("GSKIP"):
                g_tr(NT - 1)
            dump("mixT", XT, [128, 8, 2048], [("XT", t) for t in range(NT)])

            if stop == 'G' and l == nl - 1:
                raise _Stop()
            sc.barrier()
            load_ln_params(ln1g_d[l], ln1b_d[l])
            R_D.reset()
            W1B = [R_D.get(8192, BF16, "p (c n) -> p c n", c=8) for _ in range(2)]
            W2B = [R_D.get(8192, BF16, "p (c n) -> p c n", c=4) for _ in range(2)]
            hT = R_D.get(16384, BF16, "p (c n) -> p c n", c=4)
            relu_t = [R_D.get(2048, F32) for _ in range(2)]

            def load_ffn_block(fb, buf):
                s1 = w1_d[l, :, fb * 512:(fb + 1) * 512].rearrange("(c p) n -> p c n", p=128)
                s2 = w2_d[l, fb * 512:(fb + 1) * 512, :].rearrange("(c p) n -> p c n", p=128)
                for hf in range(2):
                    sc.dma("pool", lambda e, hf=hf: e.dma_start(out=W1B[buf][:, hf * 4:(hf + 1) * 4, :], in_=s1[:, hf * 4:(hf + 1) * 4, :]),
                           writes=[("W1B", buf, hf)])
                for hf in range(2):
                    sc.dma("pool", lambda e, hf=hf: e.dma_start(out=W2B[buf][:, hf * 2:(hf + 1) * 2, :], in_=s2[:, hf * 2:(hf + 1) * 2, :]),
                           writes=[("W2B", buf, hf)])

            load_ffn_block(0, 0)
            load_ffn_block(1, 1)
            for t in range(NT):
                sc.dma("sp", lambda e, t=t: e.dma_start(out=X[:, t, :], in_=xs_d[t * 128:(t + 1) * 128, :]), reads=[("xsd", t)], writes=[("X", t)])
            def o_mm(t):
                yb = (t % 3) * 2
                for hf in range(2):
                    for c in range(8):
                        sc.op("pe", lambda e, c=c, hf=hf, t=t, yb=yb: e.matmul(PS[:, yb + hf, :], lhsT=XT[:, c, tl(t)], rhs=WO[:, c, hf * 512:(hf + 1) * 512],
                                                                              start=(c == 0), stop=(c == 7)),
                              reads=[("XT", t), ("RW", 0, 0), ("RW", 0, 1), ("RW", 1, 0), ("RW", 1, 1)], writes=psk(yb + hf))

            def o_ln_stages(t):
                yb = (t % 3) * 2
                resid = lambda: sc.op("dve", lambda e: e.scalar_tensor_tensor(out=X[:, t, :], in0=X[:, t, :], scalar=ALPHA,
                                                                              in1=PS[:, yb:yb + 2, :].rearrange("p a n -> p (a n)"), op0=ALU.mult, op1=ALU.add),
                                      reads=[("X", t)] + psk(yb) + psk(yb + 1), writes=[("X", t)])
                return [resid] + ln_a_stages(t)

            def o_ln_group(ts):
                lists = [o_ln_stages(t) for t in ts]
                for k in range(len(lists[0])):
                    for lst in lists:
                        lst[k]()

            for t0 in range(3):
                o_mm(t0)
            o_ln_group([0, 1])
            o_ln_group([2])
            for t in range(0, NT, 2):
                nxt = [u for u in (t + 3, t + 4) if u < NT]
                for u in nxt:
                    o_mm(u)
                ln_b1(t, None)
                ln_b1(t + 1, None)
                if nxt:
                    o_ln_group(nxt)
                ln_b2(t, None)
                ln_b2(t + 1, None)
            dump("x1T", XT, [128, 8, 2048], [("XT", t) for t in range(NT)])

            if stop == 'O' and l == nl - 1:
                raise _Stop()
            if not last:
                load_win_block(l + 1, 0, 0)
                load_win_block(l + 1, 1, 1)
            hrr = [0]
            for fb in range(8):
                buf = fb % 2
                for r in range(4):
                    for fc in range(4):
                        bank = 4 + hrr[0] % 3
                        rb = hrr[0] % 2
                        hrr[0] += 1
                        for c in range(8):
                            sc.op("pe", lambda e, c=c, fc=fc, r=r, bank=bank, buf=buf: e.matmul(
                                PS[:, bank, :], lhsT=W1B[buf][:, c, fc * 128:(fc + 1) * 128], rhs=XT[:, c, r * 512:(r + 1) * 512],
                                start=(c == 0), stop=(c == 7)), reads=[("W1B", buf, 0), ("W1B", buf, 1)] + xt_all[r * 4:(r + 1) * 4], writes=psk(bank))
                        fcol = fb * 4 + fc
                        sc.op("act", lambda e, bank=bank, rb=rb, fcol=fcol: e.activation(out=relu_t[rb], in_=PS[:, bank, :], func=AF.Relu,
                                                                                         bias=b1c[:, fcol:fcol + 1], scale=1.0),
                              reads=psk(bank) + ["b1c"], writes=[("relu", rb)])
                        sc.op("dve", lambda e, rb=rb, fc=fc, r=r: e.tensor_tensor(out=hT[:, fc, r * 512:(r + 1) * 512], in0=relu_t[rb], in1=relu_t[rb], op=ALU.mult),
                              reads=[("relu", rb)], writes=[("hT", fc, r)])
                for t in range(NT):
                    yb = (t % 2) * 2
                    for hf in range(2):
                        for fc in range(4):
                            sc.op("pe", lambda e, fc=fc, hf=hf, t=t, yb=yb, buf=buf: e.matmul(
                                PS[:, yb + hf, :], lhsT=hT[:, fc, tl(t)], rhs=W2B[buf][:, fc, hf * 512:(hf + 1) * 512],
                                start=(fc == 0), stop=(fc == 3)), reads=[("hT", fc, t // 4), ("W2B", buf, 0), ("W2B", buf, 1)], writes=psk(yb + hf))
                    if fb == 0:
                        sc.op("dve", lambda e, t=t, yb=yb: e.scalar_tensor_tensor(out=X[:, t, :], in0=X[:, t, :], scalar=ALPHA,
                                                                                  in1=PS[:, yb:yb + 2, :].rearrange("p a n -> p (a n)"), op0=ALU.mult, op1=ALU.add),
                              reads=[("X", t)] + psk(yb) + psk(yb + 1), writes=[("X", t)])
                        sc.op("pool", lambda e, t=t: e.tensor_tensor(out=X[:, t, :], in0=X[:, t, :], in1=b2t, op=ALU.add), reads=[("X", t), "b2t"], writes=[("X", t)])
                    else:
                        sc.op("dve", lambda e, t=t, yb=yb: e.tensor_tensor(out=X[:, t, :], in0=X[:, t, :], in1=PS[:, yb:yb + 2, :].rearrange("p a n -> p (a n)"), op=ALU.add),
                              reads=[("X", t)] + psk(yb) + psk(yb + 1), writes=[("X", t)])
                if fb + 2 < 8:
                    load_ffn_block(fb + 2, buf)
            if stop == 'F' and l == nl - 1:
                raise _Stop()
            load_ln_params(ln2g_d[l], ln2b_d[l])
            ln_all(out_d if last else xs_d)
            sc.barrier()


        for _l in range(nl):
            do_layer(_l)
    except _Stop:
        pass
    out_dmas = [o for o in sc.ops if o.is_dma and o.dkey in [("xs", i) for i in range(4)]]
    fin = {}
    for o in out_dmas:
        fin[o.dkey] = o
    finals = list(fin.values()) + list(dbg_out.values())
    sc.emit(final_wait_ops=finals)
    es.close()
    return nc, sc


_CONST = None


def kernel(**inputs):
    global _CONST
    if _CONST is None:
        _CONST = _constants()
    nc, _ = build(2)
    x = np.ascontiguousarray(inputs["x"], dtype=np.float32)
    shared = {k: np.ascontiguousarray(v, dtype=np.float32) for k, v in inputs.items() if k != "x"}
    shared.update(_CONST)
    in_maps = []
    for b in range(8):
        m = dict(shared)
        m["x"] = x[b]
        in_maps.append(m)
    res = run_bass_kernel_spmd(nc, in_maps, core_ids=list(range(8)))
    return np.stack([r["out"] for r in res.results], axis=0).astype(np.float32)
```

```python
import math
import os
from contextlib import ExitStack

import numpy as np
import concourse.bass as bass
import concourse.mybir as mybir
from concourse.bass_utils import run_bass_kernel_spmd

F32 = mybir.dt.float32
BF16 = mybir.dt.bfloat16
AF = mybir.ActivationFunctionType
ALU = mybir.AluOpType

S = 2048
D = 1024
DIN = 3104
DFF = 4096
NT = 16
ALPHA = (2.0 * 2) ** 0.25
ENGS = ("pe", "act", "dve", "pool", "sp")
EPOCH = 30000


class _Res:
    __slots__ = ("last_w", "readers")

    def __init__(self):
        self.last_w = None
        self.readers = []


class _Op:
    __slots__ = ("eng", "fn", "deps", "signal", "tok", "is_dma", "dkey")

    def __init__(self, eng, fn, is_dma, dkey):
        self.eng = eng
        self.fn = fn
        self.deps = []
        self.signal = False
        self.tok = None
        self.is_dma = is_dma
        self.dkey = dkey


class Sched:
    def __init__(self, nc):
        self.nc = nc
        self.ops = []
        self.res = {}
        self.pending = {e: [] for e in ENGS}

    def _r(self, key):
        x = self.res.get(key)
        if x is None:
            x = self.res[key] = _Res()
        return x

    def _add(self, op, reads, writes):
        deps = set()
        for k in reads:
            rs = self._r(k)
            if rs.last_w is not None:
                deps.add(rs.last_w)
        for k in writes:
            rs = self._r(k)
            if rs.last_w is not None:
                deps.add(rs.last_w)
            deps.update(rs.readers)
        for k in reads:
            self._r(k).readers.append(op)
        for k in writes:
            rs = self._r(k)
            rs.last_w = op
            rs.readers = []
        if self.pending[op.eng]:
            deps.update(self.pending[op.eng])
            self.pending[op.eng] = []
        deps.discard(op)
        op.deps = list(deps)
        self.ops.append(op)
        return op

    def op(self, eng, fn, reads=(), writes=()):
        return self._add(_Op(eng, fn, False, None), reads, writes)

    def dma(self, eng, fn, dkey=None, reads=(), writes=()):
        if dkey is None:
            dkey = ("w", writes[0])
        return self._add(_Op(eng, fn, True, dkey), reads, writes)

    def barrier(self):
        last = {}
        for o in self.ops:
            last[(o.eng, o.dkey) if o.is_dma else o.eng] = o
        b = list(last.values())
        self.pending = {e: list(b) for e in ENGS}

    def emit(self, final_wait_ops=()):
        nc = self.nc
        ops = self.ops
        for o in ops:
            for d in o.deps:
                if d.is_dma:
                    d.signal = True
                elif d.eng == "pe" and o.eng == "pe" and not o.is_dma:
                    continue
                else:
                    d.signal = True
        with ExitStack() as es:
            eng_sems = {e: [] for e in ENGS}
            cnt = {e: 0 for e in ENGS}
            dma_sems = {}
            dma_cnt = {}
            for o in ops:
                if o.is_dma:
                    if o.dkey not in dma_sems:
                        dma_sems[o.dkey] = es.enter_context(nc.semaphore("d%d" % len(dma_sems)))
                        dma_cnt[o.dkey] = 0
                    dma_cnt[o.dkey] += 16
                    o.tok = (dma_sems[o.dkey], dma_cnt[o.dkey])
                elif o.signal:
                    ep = cnt[o.eng] // EPOCH
                    if ep >= len(eng_sems[o.eng]):
                        eng_sems[o.eng].append(es.enter_context(nc.semaphore("e_%s_%d" % (o.eng, ep))))
                    cnt[o.eng] += 1
                    o.tok = (eng_sems[o.eng][ep], cnt[o.eng] - ep * EPOCH)
            per_eng = {e: [o for o in ops if o.eng == e] for e in ENGS}
            self.stats = {e: len(per_eng[e]) for e in ENGS}
            self.stats["sems"] = sum(len(v) for v in eng_sems.values()) + len(dma_sems)

            def run(e, eng):
                waited = {}
                for o in per_eng[e]:
                    need = {}
                    for d in o.deps:
                        if d.tok is None:
                            continue
                        if (not d.is_dma) and d.eng == "pe" and e == "pe" and not o.is_dma:
                            continue
                        s, v = d.tok
                        k = id(s)
                        if waited.get(k, 0) >= v:
                            continue
                        if k not in need or need[k][1] < v:
                            need[k] = (s, v)
                    for k, (s, v) in need.items():
                        eng.wait_ge(s, v)
                        waited[k] = v
                    ins = o.fn(eng)
                    if o.tok is not None:
                        ins.then_inc(o.tok[0], 16 if o.is_dma else 1)
                if e == "sp":
                    for o in final_wait_ops:
                        s, v = o.tok
                        eng.wait_ge(s, v)

            with nc.Block() as block:
                @block.sync
                def _(eng):
                    run("sp", eng)

                @block.tensor
                def _(eng):
                    run("pe", eng)

                @block.scalar
                def _(eng):
                    run("act", eng)

                @block.vector
                def _(eng):
                    run("dve", eng)

                @block.gpsimd
                def _(eng):
                    run("pool", eng)


def _t5_bucket(rel):
    nb = 16
    me = 8
    ret = np.where(rel > 0, nb, 0)
    n = np.abs(rel)
    large = me + (np.log(np.maximum(n, 1).astype(np.float32) / np.float32(me))
                  / np.float32(math.log(128 / me)) * np.float32(nb - me)).astype(np.int32)
    large = np.minimum(large, nb - 1)
    return ret + np.where(n < me, n, large)


MLEN = 1280


def _constants():
    c = {}
    c["c_ident"] = np.eye(128, dtype=np.float32)
    c["c_J"] = np.eye(128, dtype=np.float32)[::-1].copy()
    s = np.arange(128)[:, None]
    t = np.arange(128)[None, :]
    uf = np.zeros((128, 129), np.float32)
    uf[:, :128] = (s <= t)
    uf[:, 128] = 1.0
    ub = np.zeros((128, 129), np.float32)
    ub[:, :128] = (s >= t)
    ub[:, 128] = 1.0
    c["c_uf"] = uf
    c["c_ub"] = ub
    c["c_sf"] = (s > t).astype(np.float32)
    c["c_sb"] = (s < t).astype(np.float32)
    n = np.arange(MLEN)
    bk = _t5_bucket(639 - n)
    oh = np.zeros((32, MLEN), np.float32)
    oh[bk, n] = 1.0
    c["c_onehot"] = oh
    return c


class _Stop(Exception):
    pass


def build(nl=2, dbg=(), stop=None):
    nc = bass.Bass("TRN2", target_bir_lowering=False)

    def din(name, shape):
        return nc.dram_tensor(name, list(shape), F32, kind="ExternalInput").ap()

    x_d = din("x", [S, D])
    lnemb_g = din("ln_emb_g", [D])
    lnemb_b = din("ln_emb_b", [D])
    table_d = din("rel_bias_table", [32, 4])
    w_in_d = din("w_in", [2, D, DIN])
    lq1_d = din("lambda_q1", [2, 64])
    lk1_d = din("lambda_k1", [2, 64])
    lq2_d = din("lambda_q2", [2, 64])
    lk2_d = din("lambda_k2", [2, 64])
    dnw_d = din("diff_norm_w", [2, 128])
    gup_d = din("gla_gate_up", [2, 2, 16, 256])
    gbias_d = din("gla_gate_bias", [2, 2, 256])
    gnw_d = din("gla_norm_w", [2, 128])
    w_o_d = din("w_o", [2, D, D])
    ln1g_d = din("ln1_g", [2, D])
    ln1b_d = din("ln1_b", [2, D])
    w1_d = din("w_ffn1", [2, D, DFF])
    b1_d = din("b_ffn1", [2, DFF])
    w2_d = din("w_ffn2", [2, DFF, D])
    b2_d = din("b_ffn2", [2, D])
    ln2g_d = din("ln2_g", [2, D])
    ln2b_d = din("ln2_b", [2, D])
    c_ident = din("c_ident", [128, 128])
    c_J = din("c_J", [128, 128])
    c_uf = din("c_uf", [128, 129])
    c_ub = din("c_ub", [128, 129])
    c_sf = din("c_sf", [128, 128])
    c_sb = din("c_sb", [128, 128])
    c_onehot = din("c_onehot", [32, MLEN])
    out_d = nc.dram_tensor("out", [S, D], F32, kind="ExternalOutput").ap()
    xs_d = nc.dram_tensor("xs_scratch", [S, D], F32).ap()
    md_t = nc.dram_tensor("md_scratch", [4, MLEN], F32)
    eb_d = nc.dram_tensor("expb_scratch", [128, 4 * 1152], BF16).ap()
    md_d = md_t.ap()
    dbg_out = {}

    sc = Sched(nc)
    es = ExitStack()
    ARENA_BYTES = 207 * 1024
    arena = es.enter_context(nc.sbuf_tensor("arena", [128, ARENA_BYTES // 2], BF16))
    PSb = es.enter_context(nc.psum_tensor("ps", [128, 8, 1024], BF16))[:]
    PS = PSb.bitcast(F32)

    def view(off, nbytes, dt, pattern=None, **kw):
        assert off % 32 == 0, off
        a = arena[:, off // 2:(off + nbytes) // 2]
        if dt is F32:
            a = a.bitcast(F32)
        if pattern:
            a = a.rearrange(pattern, **kw)
        return a

    class Alloc:
        def __init__(self, base, size):
            self.base = base
            self.size = size
            self.pos = 0

        def reset(self):
            self.pos = 0

        def get(self, nbytes, dt, pattern=None, **kw):
            n = (nbytes + 31) // 32 * 32
            assert self.pos + n <= self.size, (self.pos, n, self.size)
            v = view(self.base + self.pos, nbytes, dt, pattern, **kw)
            self.pos += n
            return v

    R_XT = Alloc(0, 32768)
    R_X = Alloc(32768, 65536)
    R_D = Alloc(98304, 70656)
    R_W = Alloc(168960, 16384)
    R_C = Alloc(185344, ARENA_BYTES - 185344)

    XT = R_XT.get(32768, BF16, "p (c n) -> p c n", c=8)
    X = R_X.get(65536, F32, "p (t n) -> p t n", t=NT)
    WB = [R_W.get(8192, BF16, "p (c n) -> p c n", c=8) for _ in range(2)]
    R_W.reset()
    WO = R_W.get(16384, BF16, "p (c n) -> p c n", c=8)

    identb = R_C.get(256, BF16)
    Jb = R_C.get(256, BF16)
    Uf = R_C.get(516, F32)
    Ub = R_C.get(516, F32)
    Usf = R_C.get(512, F32)
    Usb = R_C.get(512, F32)
    gt = R_C.get(4096, F32)
    bt = R_C.get(4096, F32)
    b2t = R_C.get(4096, F32)
    wd_t = R_C.get(512, F32)
    wg_t = R_C.get(512, F32)
    b1c = R_C.get(128, F32)
    cb = R_C.get(32, F32, "p (s h) -> p s h", s=2)
    lamv = R_C.get(4 * 64 * 4, F32, "p (a n) -> p a n", a=4)
    lamp = R_C.get(2 * 64 * 4, F32, "p (a n) -> p a n", a=2)
    lams = R_C.get(32, F32)
    Wg = R_C.get(1024, BF16)
    st_ = [R_C.get(48, F32) for _ in range(2)]
    mv_ = [R_C.get(8, F32) for _ in range(2)]
    rs_ = [R_C.get(4, F32) for _ in range(2)]
    xb_ = [R_C.get(2048, BF16) for _ in range(2)]
    dsm = [R_C.get(64, F32) for _ in range(2)]
    gsm = [R_C.get(64, F32) for _ in range(2)]
    decs = R_C.get(2 * 2 * 16 * 4, F32, "p (d q t) -> p d q t", d=2, q=2)

    def psk(b):
        return [("ps", b, q) for q in range(4)]

    def bc_mid(ap2, n):
        a = ap2.ap
        return bass.AP(ap2.tensor, ap2.offset, [list(a[0]), [0, n], list(a[1])])

    def bc_last(ap2, n):
        a = ap2.ap
        return bass.AP(ap2.tensor, ap2.offset, [list(a[0]), list(a[1]), [0, n]])

    cur_layer = [-1]

    def dump(name, ap, shape, reads):
        nm = "%s@%d" % (name, cur_layer[0])
        if nm in dbg:
            name = nm
        elif name not in dbg or (cur_layer[0] >= 0 and cur_layer[0] != nl - 1):
            return
        t = nc.dram_tensor("dbg_" + name.replace("@", "_"), list(shape), ap.dtype, kind="ExternalOutput").ap()
        dbg_out[name] = sc.dma("sp", lambda e: e.dma_start(out=t, in_=ap), "dbg", reads=reads)

    try:
        sc.dma("pool", lambda e: e.dma_start(out=identb, in_=c_ident), writes=["identb"])
        sc.dma("pool", lambda e: e.dma_start(out=Jb, in_=c_J), writes=["Jb"])
        sc.dma("sp", lambda e: e.dma_start(out=Uf, in_=c_uf), writes=["Uf"])
        sc.dma("sp", lambda e: e.dma_start(out=Ub, in_=c_ub), writes=["Ub"])
        sc.dma("sp", lambda e: e.dma_start(out=Usf, in_=c_sf), writes=["Usf"])
        sc.dma("sp", lambda e: e.dma_start(out=Usb, in_=c_sb), writes=["Usb"])
        for si, row in enumerate((15, 31)):
            sc.dma("sp", lambda e, si=si, row=row: e.dma_start(out=cb[:, si, :], in_=table_d[row, :].partition_broadcast(128)),
                   writes=[("cb", si)])

        R_D.reset()
        tb = R_D.get(16, F32)
        oh = R_D.get(MLEN * 4, F32)
        msb = R_D.get(MLEN * 4, F32)
        sc.dma("sp", lambda e: e.dma_start(out=tb[0:32, :], in_=table_d), writes=["tb"])
        sc.dma("sp", lambda e: e.dma_start(out=oh[0:32, :], in_=c_onehot), writes=["oh"])
        sc.dma("sp", lambda e: e.dma_start(out=gt, in_=lnemb_g.partition_broadcast(128)), writes=["gt"])
        sc.dma("sp", lambda e: e.dma_start(out=bt, in_=lnemb_b.partition_broadcast(128)), writes=["bt"])
        for t in range(NT):
            sc.dma("sp", lambda e, t=t: e.dma_start(out=X[:, t, :], in_=x_d[t * 128:(t + 1) * 128, :]), writes=[("X", t)])
        for ci, (c0, cn) in enumerate(((0, 512), (512, 512), (1024, 256))):
            sc.op("pe", lambda e, ci=ci, c0=c0, cn=cn: e.matmul(PS[0:4, ci, 0:cn], lhsT=tb[0:32, :], rhs=oh[0:32, c0:c0 + cn], start=True, stop=True),
                  reads=["tb", "oh"], writes=psk(ci))
            sc.op("dve", lambda e, ci=ci, c0=c0, cn=cn: e.tensor_copy(out=msb[0:4, c0:c0 + cn], in_=PS[0:4, ci, 0:cn]),
                  reads=psk(ci), writes=["msb"])
        sc.dma("sp", lambda e: e.dma_start(out=md_d, in_=msb[0:4, :]), reads=["msb"], writes=["md"])
        R_D.reset()
        R_D.get(16384, BF16); R_D.get(16384, BF16); R_D.get(16 * 4 * 129 * 2, BF16)
        expB0 = R_D.get(4 * 1152 * 2, BF16, "p (h n) -> p h n", h=4)
        R_D.get(4096, BF16); R_D.get(1024, BF16)
        tmp_revs = [view(R_D.base + 16384 + i * 2304, 2304, BF16) for i in range(4)]
        for h in range(4):
            src = bass.AP(md_t, h * MLEN, [[1, 128], [1, 1152]])
            sc.dma("pool", lambda e, src=src, h=h: e.dma_start(out=tmp_revs[h], in_=src), reads=["md"], writes=[("tmp_rev", h)])
        for h in range(4):
            for ci, (c0, cn) in enumerate(((0, 512), (512, 512), (1024, 128))):
                sc.op("pe", lambda e, ci=ci, c0=c0, cn=cn, h=h: e.matmul(PS[:, ci, 0:cn], lhsT=Jb, rhs=tmp_revs[h][:, c0:c0 + cn], start=True, stop=True),
                      reads=["Jb", ("tmp_rev", h)], writes=psk(ci))
                sc.op("act", lambda e, h=h, ci=ci, c0=c0, cn=cn: e.activation(out=expB0[:, h, c0:c0 + cn], in_=PS[:, ci, 0:cn], func=AF.Exp),
                      reads=psk(ci), writes=[("expB", h)])
        sc.dma("sp", lambda e: e.dma_start(out=eb_d, in_=expB0.rearrange("p h n -> p (h n)")), reads=[("expB", h) for h in range(4)], writes=["eb_d"])

        def ln_a_stages(t):
            Xt = X[:, t, :]
            kx = ("X", t)
            b = t % 2
            st, mv, rs = st_[b], mv_[b], rs_[b]
            return [
                lambda: sc.op("dve", lambda e: e.bn_stats(out=st[:, 0:6], in_=Xt[:, 0:512]), reads=[kx], writes=[("st", b, 0)]),
                lambda: sc.op("dve", lambda e: e.bn_stats(out=st[:, 6:12], in_=Xt[:, 512:1024]), reads=[kx], writes=[("st", b, 1)]),
                lambda: sc.op("dve", lambda e: e.bn_aggr(out=mv, in_=st), reads=[("st", b, 0), ("st", b, 1)], writes=[("mv", b)]),
                lambda: sc.op("act", lambda e: e.activation(out=rs, in_=mv[:, 1:2], func=AF.Sqrt, bias=1e-5, scale=1.0), reads=[("mv", b)], writes=[("rs", b)]),
                lambda: sc.op("dve", lambda e: e.reciprocal(out=rs, in_=rs), reads=[("rs", b)], writes=[("rs", b)]),
                lambda: sc.op("dve", lambda e: e.tensor_scalar(out=Xt, in0=Xt, scalar1=mv[:, 0:1], scalar2=rs, op0=ALU.subtract, op1=ALU.mult),
                              reads=[kx, ("mv", b), ("rs", b)], writes=[kx]),
                lambda: sc.op("dve", lambda e: e.tensor_tensor(out=Xt, in0=Xt, in1=gt, op=ALU.mult), reads=[kx, "gt"], writes=[kx]),
                lambda: sc.op("pool", lambda e: e.tensor_tensor(out=Xt, in0=Xt, in1=bt, op=ALU.add), reads=[kx, "bt"], writes=[kx]),
            ]

        def ln_a(t):
            for f in ln_a_stages(t):
                f()

        def ln_a_pair(t0, t1):
            sa, sb = ln_a_stages(t0), ln_a_stages(t1)
            for fa, fb in zip(sa, sb):
                fa()
                fb()

        def ln_b1(t, spill_to):
            Xt = X[:, t, :]
            kx = ("X", t)
            b = t % 2
            xb = xb_[b]
            if spill_to is not None:
                sc.dma("sp", lambda e: e.dma_start(out=spill_to[t * 128:(t + 1) * 128, :], in_=Xt), ("xs", t % 4), reads=[kx], writes=[("xsd", t)])
            if spill_to is out_d:
                return
            sc.op("act", lambda e: e.activation(out=xb, in_=Xt, func=AF.Copy), reads=[kx], writes=[("xb", b)])

        def ln_b2(t, spill_to):
            if spill_to is out_d:
                return
            b = t % 2
            xb = xb_[b]
            bank = 6 + b
            for c in range(8):
                sc.op("pe", lambda e, c=c: e.transpose(out=PSb[:, bank, c * 128:(c + 1) * 128], in_=xb[:, c * 128:(c + 1) * 128], identity=identb),
                      reads=[("xb", b), "identb"], writes=psk(bank))
            sc.op("act", lambda e: e.activation(out=XT[:, :, t * 128:(t + 1) * 128], in_=PSb[:, bank, :].rearrange("p (c n) -> p c n", c=8), func=AF.Copy),
                  reads=psk(bank), writes=[("XT", t)])

        def ln_all(spill_to):
            ln_a_pair(0, 1)
            for t in range(0, NT, 2):
                if t + 2 < NT:
                    ln_a_pair(t + 2, t + 3)
                ln_b1(t, spill_to)
                ln_b1(t + 1, spill_to)
                ln_b2(t, spill_to)
                ln_b2(t + 1, spill_to)

        def load_ln_params(g_ap, b_ap):
            sc.dma("sp", lambda e: e.dma_start(out=gt, in_=g_ap.partition_broadcast(128)), writes=["gt"])
            sc.dma("sp", lambda e: e.dma_start(out=bt, in_=b_ap.partition_broadcast(128)), writes=["bt"])

        def load_win_block(l, blk, buf):
            c0 = blk * 512
            ncol = min(512, DIN - c0)
            src = w_in_d[l, :, c0:c0 + ncol].rearrange("(c p) n -> p c n", p=128)
            for hf in range(2):
                sc.dma("pool", lambda e, hf=hf: e.dma_start(out=WB[buf][:, hf * 4:(hf + 1) * 4, 0:ncol], in_=src[:, hf * 4:(hf + 1) * 4, :]),
                       writes=[("RW", buf, hf)])

        if stop == 'init':
            raise _Stop()
        load_win_block(0, 0, 0)
        load_win_block(0, 1, 1)
        ln_all(xs_d)
        dump("h0", X, [128, NT, 1024], [("X", t) for t in range(NT)])

        if stop == 'emb':
            raise _Stop()
        evac_rr = [0]

        def evac(out, in_, reads, writes, scale=None):
            evac_rr[0] ^= 1
            if evac_rr[0]:
                if scale is None:
                    sc.op("act", lambda e: e.activation(out=out, in_=in_, func=AF.Copy), reads=reads, writes=writes)
                else:
                    sc.op("act", lambda e: e.mul(out=out, in_=in_, mul=scale), reads=reads, writes=writes)
            else:
                if scale is None:
                    sc.op("dve", lambda e: e.tensor_copy(out=out, in_=in_), reads=reads, writes=writes)
                else:
                    sc.op("dve", lambda e: e.tensor_scalar(out=out, in0=in_, scalar1=scale, scalar2=None, op0=ALU.mult), reads=reads, writes=writes)

        def do_layer(l):
            lam_init = 0.8 - 0.6 * math.exp(-0.3 * l)
            cur_layer[0] = l
            if l == 0:
                sc.barrier()
            last = (l == nl - 1)
            R_D.reset()
            QT = R_D.get(16384, BF16, "p (h n) -> p h n", h=4)
            KT = R_D.get(16384, BF16, "p (h n) -> p h n", h=4)
            V = R_D.get(16 * 4 * 129 * 2, BF16, "p (t h e) -> p t h e", t=16, h=4)
            expB = R_D.get(4 * 1152 * 2, BF16, "p (h n) -> p h n", h=4)
            Eb = R_D.get(4096, BF16, "p (b m n) -> p b m n", b=2, m=2)
            d_y = R_D.get(4 * 128 * 2, BF16, "p (u n) -> p u n", u=4)
            _pu = R_D.pos
            silu_t = [R_D.get(2048, F32) for _ in range(2)]
            R_D.pos = _pu
            accS = R_D.get(8 * 129 * 4, F32, "p (a n) -> p a n", a=8)
            R_D.pos = _pu + 4608
            tmp_rev = R_D.get(1152 * 2, BF16)
            R_X.reset()
            gqT = R_X.get(8192, BF16, "p (c n) -> p c n", c=2)
            gkT = R_X.get(8192, BF16, "p (c n) -> p c n", c=2)
            gk_tok = R_X.get(8192, BF16, "p (t n) -> p t n", t=16)
            gv = R_X.get(16384, BF16, "p (t n) -> p t n", t=16)
            gr_s = R_X.get(16384, BF16, "p (t n) -> p t n", t=16)
            G33 = R_X.get(4096, BF16)

            for i, ap in enumerate((lq1_d, lk1_d, lq2_d, lk2_d)):
                sc.dma("sp", lambda e, i=i, ap=ap: e.dma_start(out=lamv[:, i, :], in_=ap[l, :].partition_broadcast(128)), writes=[("lamv", i)])
            sc.op("dve", lambda e: e.tensor_tensor(out=lamp[:, 0, :], in0=lamv[:, 0, :], in1=lamv[:, 1, :], op=ALU.mult), reads=[("lamv", 0), ("lamv", 1)], writes=["lamp"])
            sc.op("dve", lambda e: e.tensor_tensor(out=lamp[:, 1, :], in0=lamv[:, 2, :], in1=lamv[:, 3, :], op=ALU.mult), reads=[("lamv", 2), ("lamv", 3)], writes=["lamp"])
            sc.op("dve", lambda e: e.reduce_sum(out=lams[:, 0:2], in_=lamp, axis=mybir.AxisListType.X), reads=["lamp"], writes=["lams"])
            sc.op("act", lambda e: e.activation(out=lams[:, 0:2], in_=lams[:, 0:2], func=AF.Exp), reads=["lams"], writes=["lams"])
            sc.op("dve", lambda e: e.tensor_tensor(out=lams[:, 2:3], in0=lams[:, 0:1], in1=lams[:, 1:2], op=ALU.subtract), reads=["lams"], writes=["lams"])
            sc.op("dve", lambda e: e.tensor_scalar(out=lams[:, 3:4], in0=lams[:, 2:3], scalar1=lam_init, scalar2=-1.0, op0=ALU.add, op1=ALU.mult),
                  reads=["lams"], writes=["neglam"])
            neg_lam = lams[:, 3:4]
            sc.dma("sp", lambda e: e.dma_start(out=wd_t, in_=dnw_d[l, :].partition_broadcast(128)), writes=["wd"])
            sc.op("dve", lambda e: e.tensor_scalar(out=wd_t, in0=wd_t, scalar1=1.0 - lam_init, scalar2=None, op0=ALU.mult), reads=["wd"], writes=["wd"])
            sc.dma("sp", lambda e: e.dma_start(out=wg_t, in_=gnw_d[l, :].partition_broadcast(128)), writes=["wg"])
            sc.op("dve", lambda e: e.memset(Wg[0:33, :], 0.0), writes=["Wg"])
            sc.dma("pool", lambda e: e.dma_start(out=Wg[0:16, 0:256], in_=gup_d[l, 0]), writes=["Wg"])
            sc.dma("pool", lambda e: e.dma_start(out=Wg[16:32, 256:512], in_=gup_d[l, 1]), writes=["Wg"])
            sc.dma("pool", lambda e: e.dma_start(out=Wg[32:33, :], in_=gbias_d[l].rearrange("a n -> (a n)").partition_broadcast(1)), writes=["Wg"])
            sc.dma("sp", lambda e: e.dma_start(out=b1c, in_=b1_d[l].rearrange("(c p) -> p c", p=128), allow_slow_non_contiguous=True), writes=["b1c"])
            sc.dma("sp", lambda e: e.dma_start(out=b2t, in_=b2_d[l].partition_broadcast(128)), writes=["b2t"])
            sc.dma("sp", lambda e: e.dma_start(out=expB.rearrange("p h n -> p (h n)"), in_=eb_d), reads=["eb_d"], writes=[("expB", h) for h in range(4)])
            sc.op("dve", lambda e: e.memset(V[:, :, :, 128:129], 1.0), writes=[("V", t) for t in range(NT)])
            sc.op("dve", lambda e: e.memset(G33[32:33, :], 1.0), writes=["G33"])

            dump("XTin", XT, [128, 8, 2048], [("XT", t) for t in range(NT)])
            dump("Xin", X, [128, NT, 1024], [("X", t) for t in range(NT)])
            if stop == 'L' and l == nl - 1:
                raise _Stop()
            ps_rr = [0]

            def nextbank():
                b = ps_rr[0] % 6
                ps_rr[0] += 1
                return b

            xt_all = [("XT", t) for t in range(NT)]
            for blk in range(7):
                buf = blk % 2
                wb = WB[buf]
                kw = [("RW", buf, 0), ("RW", buf, 1)]
                if blk in (0, 1, 3, 6):
                    nch = 1 if blk == 6 else 4
                    for cc in range(nch):
                        for r in range(4):
                            bank = nextbank()
                            M = 32 if blk == 6 else 128
                            for c in range(8):
                                sc.op("pe", lambda e, c=c, cc=cc, r=r, bank=bank, M=M, wb=wb: e.matmul(
                                    PS[0:M, bank, :], lhsT=wb[:, c, cc * 128:cc * 128 + M], rhs=XT[:, c, r * 512:(r + 1) * 512],
                                    start=(c == 0), stop=(c == 7)), reads=kw + xt_all[r * 4:(r + 1) * 4], writes=psk(bank))
                            sl = slice(r * 512, (r + 1) * 512)
                            if blk == 0:
                                evac(QT[:, cc, sl], PS[:, bank, :], psk(bank), [("QT", cc, r)], scale=0.125)
                            elif blk == 1:
                                evac(KT[:, cc, sl], PS[:, bank, :], psk(bank), [("KT", cc, r)])
                            elif blk == 3:
                                if cc < 2:
                                    evac(gqT[:, cc, sl], PS[:, bank, :], psk(bank), [("gqT", 4 * r + i) for i in range(4)], scale=0.125)
                                else:
                                    evac(gkT[:, cc - 2, sl], PS[:, bank, :], psk(bank), [("gkT", 4 * r + i) for i in range(4)])
                            else:
                                evac(G33[0:32, sl], PS[0:32, bank, :], psk(bank), ["G33"])
                if blk == 3:
                    for t in range(NT):
                        bank = nextbank()
                        for cc in range(2):
                            sc.op("pe", lambda e, t=t, cc=cc, bank=bank: e.transpose(out=PSb[:, bank, cc * 128:(cc + 1) * 128], in_=gkT[:, cc, t * 128:(t + 1) * 128], identity=identb),
                                  reads=[("gkT", t), "identb"], writes=psk(bank))
                        evac(gk_tok[:, t, :], PSb[:, bank, 0:256], psk(bank), [("gk_tok", t)])
                if blk in (2, 4, 5):
                    for t in range(NT):
                        bank = nextbank()
                        c0, ncol = (0, 512)
                        for c in range(8):
                            sc.op("pe", lambda e, c=c, t=t, bank=bank, c0=c0, ncol=ncol, wb=wb: e.matmul(
                                PS[:, bank, 0:ncol], lhsT=XT[:, c, t * 128:(t + 1) * 128], rhs=wb[:, c, c0:c0 + ncol],
                                start=(c == 0), stop=(c == 7)), reads=kw + [("XT", t)], writes=psk(bank))
                        if blk == 2:
                            evac(V[:, t, :, 0:128], PS[:, bank, :].rearrange("p (h e) -> p h e", h=4), psk(bank), [("V", t)])
                        elif blk == 4:
                            evac(gv[:, t, :], PS[:, bank, :], psk(bank), [("gv", t)])
                        else:
                            sb = t % 2
                            sc.op("act", lambda e, bank=bank, sb=sb: e.activation(out=silu_t[sb], in_=PS[:, bank, :], func=AF.Silu),
                                  reads=psk(bank), writes=[("silu", sb)])
                            sc.op("dve", lambda e, t=t, sb=sb: e.tensor_tensor(
                                out=gr_s[:, t, :].rearrange("p (h e) -> p h e", h=4), in0=silu_t[sb].rearrange("p (h e) -> p h e", h=4),
                                in1=bc_mid(wg_t, 4), op=ALU.mult), reads=[("silu", sb), "wg"], writes=[("gr_s", t)])
                if blk + 2 < 7:
                    load_win_block(l, blk + 2, buf)
            for hf in range(2):
                sc.dma("pool", lambda e, hf=hf: e.dma_start(out=WO[:, hf * 4:(hf + 1) * 4, :],
                                                             in_=w_o_d[l].rearrange("(c p) n -> p c n", p=128)[:, hf * 4:(hf + 1) * 4, :]),
                       writes=[("RW", hf, 0), ("RW", hf, 1)])
            dump("QT", QT, [128, 4, 2048], [("QT", a, b) for a in range(4) for b in range(4)])
            dump("KT", KT, [128, 4, 2048], [("KT", a, b) for a in range(4) for b in range(4)])
            dump("V", V, [128, 16, 4, 129], [("V", t) for t in range(NT)])
            dump("expB", expB, [128, 4, 1152], [("expB", h) for h in range(4)])
            dump("gqT", gqT, [128, 2, 2048], [("gqT", t) for t in range(NT)])
            dump("gr_s", gr_s, [128, 16, 512], [("gr_s", t) for t in range(NT)])
            dump("G33", G33[0:33, :], [33, 2048], ["G33"])

            if stop == 'P' and l == nl - 1:
                raise _Stop()
            steps = [(h, r, j) for h in range(4) for r in range(4) for j in range(16)]

            def acc_ap(m, u):
                idx = m * 4 + u
                return PS[:, 4 + idx // 3, (idx % 3) * 160:(idx % 3) * 160 + 129]

            def acc_keys(m, u):
                return psk(4 + (m * 4 + u) // 3)

            Eb3 = view(R_X.base + 61440, 2048, BF16, "p (m n) -> p m n", m=2)
            EbL = [Eb[:, 0, :, :], Eb[:, 1, :, :], Eb3]

            def d_scores(i):
                h, r, j = steps[i]
                d = j - 4 * r
                mixed = (-1 <= d <= 4)
                sb = i % 2
                eb = i % 3
                E = EbL[eb]
                for m in range(2):
                    bank = sb * 2 + m
                    sc.op("pe", lambda e, h=h, r=r, j=j, m=m, bank=bank: e.matmul(
                        PS[:, bank, :], lhsT=KT[64 * m:64 * m + 64, h, j * 128:(j + 1) * 128],
                        rhs=QT[64 * m:64 * m + 64, h, r * 512:(r + 1) * 512], start=True, stop=True),
                        reads=[("KT", h, j // 4), ("QT", h, r)], writes=psk(bank))
                pk2 = psk(sb * 2) + psk(sb * 2 + 1)
                ek = [("E", eb, 0), ("E", eb, 1)]
                if mixed:
                    c0 = (4 - d) * 128
                    sc.op("act", lambda e, sb=sb, E=E: e.activation(out=E, in_=PS[:, sb * 2:sb * 2 + 2, :], func=AF.Exp),
                          reads=pk2, writes=ek)
                    for m in range(2):
                        sc.op("dve", lambda e, E=E, m=m, h=h, c0=c0: e.tensor_tensor(out=E[:, m, :], in0=E[:, m, :], in1=expB[:, h, c0:c0 + 512], op=ALU.mult),
                              reads=[("E", eb, m), ("expB", h)], writes=[("E", eb, m)])
                else:
                    side = 0 if d < 0 else 1
                    sc.op("act", lambda e, sb=sb, E=E, side=side, h=h: e.activation(
                        out=E, in_=PS[:, sb * 2:sb * 2 + 2, :], func=AF.Exp, bias=cb[:, side, h:h + 1]),
                        reads=pk2 + [("cb", 0), ("cb", 1)], writes=ek)

            def d_av(i):
                h, r, j = steps[i]
                eb = i % 3
                E = EbL[eb]
                for m in range(2):
                    for u in range(4):
                        sc.op("pe", lambda e, h=h, j=j, m=m, u=u, E=E: e.matmul(
                            acc_ap(m, u), lhsT=E[:, m, u * 128:(u + 1) * 128], rhs=V[:, j, h, 0:129],
                            start=(j == 0 and (m * 4 + u) % 3 == 0), stop=(j == 15), skip_group_check=True),
                            reads=[("E", eb, m), ("V", j)], writes=acc_keys(m, u))

            sm = dsm[0]
            def ak(*idx):
                return [("accS", i) for i in idx]

            def d_final(h, r):
                sc.op("dve", lambda e: e.tensor_copy(out=accS[:, 0:3, :], in_=PS[:, 4, 0:480].rearrange("p (a n) -> p a n", a=3)[:, :, 0:129]),
                      reads=psk(4), writes=ak(0, 1, 2) + [("silu", 0), ("silu", 1)])
                sc.op("dve", lambda e: e.tensor_copy(out=accS[:, 3:6, :], in_=PS[:, 5, 0:480].rearrange("p (a n) -> p a n", a=3)[:, :, 0:129]),
                      reads=psk(5), writes=ak(3, 4, 5))
                sc.op("dve", lambda e: e.tensor_copy(out=accS[:, 6:8, :], in_=PS[:, 6, 0:320].rearrange("p (a n) -> p a n", a=2)[:, :, 0:129]),
                      reads=psk(6), writes=ak(6, 7))

            def d_final2(h, r):
                sc.op("dve", lambda e: e.reciprocal(out=sm[:, 0:8], in_=accS[:, :, 128]), reads=ak(*range(8)), writes=["dsm"])
                sc.op("dve", lambda e: e.tensor_scalar(out=sm[:, 4:8], in0=sm[:, 4:8], scalar1=neg_lam, scalar2=None, op0=ALU.mult),
                      reads=["dsm", "neglam"], writes=["dsm"])
                sc.op("dve", lambda e: e.memset(sm[:, 8:12], 0.0), writes=[("dss", u) for u in range(4)])

            def d_final_u(u):
                if True:
                    sc.op("dve", lambda e, u=u: e.tensor_scalar(out=accS[:, u, 0:128], in0=accS[:, u, 0:128], scalar1=sm[:, u:u + 1], scalar2=None, op0=ALU.mult),
                          reads=["dsm"] + ak(u), writes=ak(u))
                    sc.op("dve", lambda e, u=u: e.scalar_tensor_tensor(out=accS[:, u, 0:128], in0=accS[:, 4 + u, 0:128], scalar=sm[:, 4 + u:5 + u],
                                                                       in1=accS[:, u, 0:128], op0=ALU.mult, op1=ALU.add),
                          reads=["dsm"] + ak(u, 4 + u), writes=ak(u))
                    sc.op("dve", lambda e, u=u: e.scalar_tensor_tensor(out=accS[:, 4 + u, 0:128], in0=accS[:, u, 0:128], scalar=1.0, in1=accS[:, u, 0:128],
                                                                       op0=ALU.mult, op1=ALU.mult, accum_out=sm[:, 8 + u:9 + u]),
                          reads=ak(u), writes=ak(4 + u) + [("dss", u)])

            def d_final_b(h, r):
                sc.op("act", lambda e: e.activation(out=sm[:, 12:16], in_=sm[:, 8:12], func=AF.Ln, bias=1e-5, scale=1.0 / 128),
                      reads=[("dss", u) for u in range(4)], writes=["drs"])
                sc.op("act", lambda e: e.activation(out=sm[:, 12:16], in_=sm[:, 12:16], func=AF.Exp, scale=-0.5), reads=["drs"], writes=["drs"])
                for u in range(4):
                    sc.op("dve", lambda e, u=u: e.scalar_tensor_tensor(out=d_y[:, u, :], in0=accS[:, u, 0:128], scalar=sm[:, 12 + u:13 + u], in1=wd_t,
                                                                       op0=ALU.mult, op1=ALU.mult),
                          reads=ak(u) + ["drs", "wd"], writes=[("dy", u)])

            def d_final_pe(h, r):
                for u in range(4):
                    sc.op("pe", lambda e, u=u: e.transpose(out=PSb[:, 7, u * 128:(u + 1) * 128], in_=d_y[:, u, :], identity=identb),
                          reads=[("dy", u), "identb"], writes=psk(7))
                sc.op("dve", lambda e, h=h, r=r: e.tensor_copy(out=XT[:, h, r * 512:(r + 1) * 512], in_=PSb[:, 7, 0:512]),
                      reads=psk(7), writes=[("XT", 4 * r + i) for i in range(4)])

            pend = []
            pend_b = []
            pend_u = []
            d_scores(0)
            d_scores(1)
            for i in range(len(steps)):
                h, r, j = steps[i]
                if j == 15:
                    d_av(i)
                    d_final(h, r)
                    if i + 2 < len(steps):
                        d_scores(i + 2)
                    d_final2(h, r)
                else:
                    if i + 2 < len(steps):
                        d_scores(i + 2)
                    d_av(i)
                if j == 15:
                    pend.append((h, r))
                    pend_b.append((h, r))
                    pend_u.extend([0, 1, 2, 3])
                    d_final_u(pend_u.pop(0))
                elif pend_u:
                    d_final_u(pend_u.pop(0))
                elif j == 4 and pend_b:
                    d_final_b(*pend_b.pop(0))
                elif j == 7 and pend:
                    d_final_pe(*pend.pop(0))
            while pend_u:
                d_final_u(pend_u.pop(0))
            while pend_b:
                d_final_b(*pend_b.pop(0))
            while pend:
                d_final_pe(*pend.pop(0))
            dump("mixT_d", XT, [128, 8, 2048], [("XT", t) for t in range(NT)])

            if stop == 'D' and l == nl - 1:
                raise _Stop()
            sc.barrier()
            R_D.reset()
            qf = R_D.get(8192, BF16, "p (c n) -> p c n", c=2)
            kf = R_D.get(8192, BF16, "p (c n) -> p c n", c=2)
            kd_f = R_D.get(8192, BF16, "p (t n) -> p t n", t=16)
            Sbf = R_D.get(16384, BF16, "p (d q t e) -> p d q t e", d=2, q=2, t=16)
            stm2 = [R_D.get(4096, F32, "p (d q n) -> p d q n", d=2, q=2) for _ in range(2)]
            _p0 = R_D.pos
            sp_ = [R_D.get(2048, F32) for _ in range(2)]
            _p1 = R_D.pos
            ebt = [R_D.get(2 * 2 * 129 * 4, F32, "p (d q n) -> p d q n", d=2, q=2) for _ in range(2)]
            _p2 = R_D.pos
            enbt = [R_D.get(2 * 2 * 128 * 4, F32, "p (d q n) -> p d q n", d=2, q=2) for _ in range(2)]
            erem = [R_D.get(2048, F32) for _ in range(2)]
            dS = [R_D.get(1024, F32) for _ in range(4)]
            _pend = R_D.pos
            Am = [R_X.get(4 * 2 * 128 * 2, BF16, "p (h d n) -> p h d n", h=4, d=2) for _ in range(2)]
            R_D.pos = _p2
            g_y = [R_D.get(1024, BF16) for _ in range(2)]
            g_junk = R_D.get(512, F32)
            R_D.pos = _pend
            qb, kb, kd_b = gqT, gkT, gk_tok
            maskf = Uf[:, 0:128]
            maskb = Usf

            def tl(t):
                return slice(t * 128, (t + 1) * 128)

            def prep_A(t):
                b = t % 2
                sp = sp_[b]
                zb = 0 if b == 0 else 7
                sc.op("pe", lambda e: e.matmul(PS[:, zb, :], lhsT=G33[0:33, tl(t)], rhs=Wg[0:33, :], start=True, stop=True),
                      reads=["G33", "Wg"], writes=psk(zb))
                sc.op("act", lambda e: e.activation(out=sp, in_=PS[:, zb, :], func=AF.Exp, scale=-1.0), reads=psk(zb), writes=[("sp", b)])
                sc.op("act", lambda e: e.activation(out=sp, in_=sp, func=AF.Ln, bias=1.0, scale=1.0), reads=[("sp", b)], writes=[("sp", b)])

            prep_A(0)
            for t in range(NT):
                b = t % 2
                sp = sp_[b]
                if t + 1 < NT:
                    prep_A(t + 1)
                sc.op("pe", lambda e, sp=sp: e.matmul(PS[:, 1, 0:256], lhsT=Usf, rhs=sp[:, 0:256], start=True, stop=True), reads=[("sp", b), "Usf", "Usb"], writes=psk(1))
                sc.op("pe", lambda e, sp=sp: e.matmul(PS[:, 1, 256:512], lhsT=Usb, rhs=sp[:, 256:512], start=True, stop=True), reads=[("sp", b), "Usf", "Usb"], writes=psk(1))
                sc.op("act", lambda e, b=b: e.activation(out=erem[b], in_=PS[:, 1, :], func=AF.Exp, scale=-1.0 / 16), reads=psk(1), writes=[("erem", b)])
                sc.op("dve", lambda e, t=t, b=b: e.tensor_tensor(out=kd_f[:, t, :], in0=gk_tok[:, t, :], in1=erem[b][:, 0:256], op=ALU.mult),
                      reads=[("gk_tok", t), ("erem", b)], writes=[("kd_f", t)])
                sc.op("dve", lambda e, t=t, b=b: e.tensor_tensor(out=kd_b[:, t, :], in0=gk_tok[:, t, :], in1=erem[b][:, 256:512], op=ALU.mult),
                      reads=[("gk_tok", t), ("erem", b), ("kd_f", t)], writes=[("gk_tok", t)])
                for d in range(2):
                    U = Uf if d == 0 else Ub
                    for q in range(2):
                        sc.op("pe", lambda e, sp=sp, d=d, q=q, U=U: e.matmul(PS[:, 2 + d, q * 160:q * 160 + 129],
                                                                            lhsT=sp[:, d * 256 + q * 128:d * 256 + (q + 1) * 128], rhs=U, start=True, stop=True),
                              reads=[("sp", b), "Uf", "Ub"], writes=psk(2 + d))
                src4 = PS[:, 2:4, 0:320].rearrange("p a (q n) -> p a q n", q=2)
                sc.op("act", lambda e, b=b, src4=src4: e.activation(out=ebt[b], in_=src4[:, :, :, 0:129], func=AF.Exp, scale=-1.0 / 16),
                      reads=psk(2) + psk(3), writes=[("eb", b, 0), ("eb", b, 1)])
                sc.op("act", lambda e, b=b, src4=src4: e.activation(out=enbt[b], in_=src4[:, :, :, 0:128], func=AF.Exp, scale=1.0 / 16),
                      reads=psk(2) + psk(3), writes=[("enb", b, 0), ("enb", b, 1)])
                sc.op("dve", lambda e, t=t, b=b: e.tensor_tensor(out=qf[:, :, tl(t)], in0=gqT[:, :, tl(t)], in1=ebt[b][:, 0, :, 0:128], op=ALU.mult),
                      reads=[("gqT", t), ("eb", b, 0)], writes=[("qf", t)])
                sc.op("dve", lambda e, t=t, b=b: e.tensor_tensor(out=kf[:, :, tl(t)], in0=gkT[:, :, tl(t)], in1=enbt[b][:, 0, :, :], op=ALU.mult),
                      reads=[("gkT", t), ("enb", b, 0)], writes=[("kf", t)])
                sc.op("dve", lambda e, t=t, b=b: e.tensor_tensor(out=qb[:, :, tl(t)], in0=gqT[:, :, tl(t)], in1=ebt[b][:, 1, :, 0:128], op=ALU.mult),
                      reads=[("gqT", t), ("eb", b, 1), ("qf", t)], writes=[("gqT", t)])
                sc.op("dve", lambda e, t=t, b=b: e.tensor_tensor(out=kb[:, :, tl(t)], in0=gkT[:, :, tl(t)], in1=enbt[b][:, 1, :, :], op=ALU.mult),
                      reads=[("gkT", t), ("enb", b, 1), ("kf", t)], writes=[("gkT", t)])
                sc.op("dve", lambda e, t=t, b=b: e.tensor_copy(out=decs[:, :, :, t:t + 1], in_=ebt[b][:, :, :, 128:129]),
                      reads=[("eb", b, 0), ("eb", b, 1)], writes=[("decs", t)])
            dump("qf", qf, [128, 2, 2048], [("qf", t) for t in range(NT)])
            dump("kd_f", kd_f, [128, 16, 256], [("kd_f", t) for t in range(NT)])
            dump("decs", decs, [128, 2, 2, 16], [("decs", t) for t in range(NT)])

            if stop == 'G1' and l == nl - 1:
                raise _Stop()
            sc.op("dve", lambda e: e.memset(stm2[0], 0.0), writes=[("stm", 0, d, q) for d in range(2) for q in range(2)])
            chains = [(d, q) for d in range(2) for q in range(2)]
            par = {c: 0 for c in chains}
            for i in range(NT):
                todo = []
                for ci, (d, q) in enumerate(chains):
                    t = i if d == 0 else NT - 1 - i
                    cur = par[(d, q)]
                    if i > 0:
                        sc.op("act", lambda e, d=d, q=q, t=t, cur=cur: e.activation(out=Sbf[0:64, d, q, t, :], in_=stm2[cur][0:64, d, q, 0:128], func=AF.Copy),
                              reads=[("stm", cur, d, q)], writes=[("Sbf", d, q, t, 0)])
                        sc.op("dve", lambda e, d=d, q=q, t=t, cur=cur: e.tensor_copy(out=Sbf[64:128, d, q, t, :], in_=stm2[cur][64:128, d, q, 128:256]),
                              reads=[("stm", cur, d, q)], writes=[("Sbf", d, q, t, 1)])
                    if i == NT - 1:
                        continue
                    kd = kd_f if d == 0 else kd_b
                    kkey = "kd_f" if d == 0 else "gk_tok"
                    pslot = ci % 2
                    pk = psk(4 + pslot)
                    sc.op("pe", lambda e, kd=kd, t=t, q=q, pslot=pslot: e.matmul(PS[:, 4 + pslot, 0:256], lhsT=kd[:, t, q * 128:(q + 1) * 128],
                                                                                rhs=gv[:, t, q * 256:(q + 1) * 256], start=True, stop=True),
                          reads=[(kkey, t), ("gv", t)], writes=pk)
                    if os.environ.get("GSKIP") != "evac":
                        sc.op("act", lambda e, ci=ci, pslot=pslot: e.activation(out=dS[ci], in_=PS[:, 4 + pslot, 0:256], func=AF.Copy),
                              reads=pk, writes=[("dS", ci)])
                    todo.append((ci, d, q, t, cur))
                for (ci, d, q, t, cur) in todo:
                    if os.environ.get("GSKIP") == "upd":
                        par[(d, q)] = 1 - cur
                        continue
                    sc.op("dve", lambda e, ci=ci, d=d, q=q, t=t, cur=cur: e.scalar_tensor_tensor(
                        out=stm2[1 - cur][:, d, q, :], in0=stm2[cur][:, d, q, :], scalar=decs[:, d, q, t:t + 1], in1=dS[ci],
                        op0=ALU.mult, op1=ALU.add), reads=[("stm", cur, d, q), ("decs", t), ("dS", ci)], writes=[("stm", 1 - cur, d, q)])
                    par[(d, q)] = 1 - cur
            dump("Sbf", Sbf, [128, 2, 2, 16, 128], [("Sbf", d, q, t, hh) for d in range(2) for q in range(2) for t in range(NT) for hh in range(2)])
            if stop == 'G2' and l == nl - 1:
                raise _Stop()

            def g_A(t):
                b = t % 2
                A = Am[b]
                for half in range(2):
                    sbank = (4 + half) if os.environ.get('GBANK') else (2 * b + half)
                    items = []
                    for sq in range(4):
                        h, d = half + 2 * (sq // 2), sq % 2
                        q = h // 2
                        base = (h % 2) * 64
                        kk = kf if d == 0 else kb
                        qq = qf if d == 0 else qb
                        kkey = ("kf", t) if d == 0 else ("gkT", t)
                        qkey = ("qf", t) if d == 0 else ("gqT", t)
                        sc.op("pe", lambda e, kk=kk, qq=qq, q=q, base=base, sbank=sbank, sq=sq: e.matmul(
                            PS[:, sbank, sq * 128:(sq + 1) * 128], lhsT=kk[base:base + 64, q, tl(t)], rhs=qq[base:base + 64, q, tl(t)], start=True, stop=True),
                            reads=[kkey, qkey], writes=psk(sbank))
                        items.append((sq, h, d))
                        if os.environ.get("GOLD"):
                            mk = maskf if d == 0 else maskb
                            sc.op("dve", lambda e, A=A, h=h, d=d, sbank=sbank, sq=sq, mk=mk: e.tensor_tensor(
                                out=A[:, h, d, :], in0=PS[:, sbank, sq * 128:(sq + 1) * 128], in1=mk, op=ALU.mult),
                                reads=psk(sbank) + ["Uf", "Usf"], writes=[("A", b, h, d)])
                    if os.environ.get("GOLD"):
                        continue
                    for (sq, h, d) in items:
                        mk = maskf if d == 0 else maskb
                        sc.op("dve", lambda e, A=A, h=h, d=d, sbank=sbank, sq=sq, mk=mk: e.tensor_tensor(
                            out=A[:, h, d, :], in0=PS[:, sbank, sq * 128:(sq + 1) * 128], in1=mk, op=ALU.mult),
                            reads=psk(sbank) + ["Uf", "Usf"], writes=[("A", b, h, d)])

            def g_B(t):
                b = t % 2
                A = Am[b]
                obank = (0 + b) if os.environ.get('GBANK') else (4 + b)
                for h in range(4):
                    q = h // 2
                    base = (h % 2) * 64
                    oh_ = PS[:, obank, h * 128:(h + 1) * 128]
                    ok = psk(obank)
                    inter_f = t > 0
                    inter_b = t < NT - 1
                    sc.op("pe", lambda e, A=A, h=h, oh_=oh_: e.matmul(oh_, lhsT=A[:, h, 0, :], rhs=gv[:, t, h * 128:(h + 1) * 128], start=True, stop=False),
                          reads=[("A", b, h, 0), ("gv", t)], writes=ok)
                    sc.op("pe", lambda e, A=A, h=h, oh_=oh_, fin=(not inter_f and not inter_b): e.matmul(
                        oh_, lhsT=A[:, h, 1, :], rhs=gv[:, t, h * 128:(h + 1) * 128], start=False, stop=fin),
                        reads=[("A", b, h, 1), ("gv", t)], writes=ok)
                    if inter_f:
                        sc.op("pe", lambda e, q=q, base=base, oh_=oh_, fin=(not inter_b): e.matmul(
                            oh_, lhsT=qf[base:base + 64, q, tl(t)], rhs=Sbf[base:base + 64, 0, q, t, :], start=False, stop=fin),
                            reads=[("qf", t), ("Sbf", 0, q, t, h % 2)], writes=ok)
                    if inter_b:
                        sc.op("pe", lambda e, q=q, base=base, oh_=oh_: e.matmul(
                            oh_, lhsT=qb[base:base + 64, q, tl(t)], rhs=Sbf[base:base + 64, 1, q, t, :], start=False, stop=True),
                            reads=[("gqT", t), ("Sbf", 1, q, t, h % 2)], writes=ok)

            def g_norm(t):
                b = t % 2
                sm = gsm[b]
                obank = (0 + b) if os.environ.get('GBANK') else (4 + b)
                okall = psk(obank)
                for h in range(4):
                    sc.op("act", lambda e, h=h, sm=sm: e.activation(out=g_junk, in_=PS[:, obank, h * 128:(h + 1) * 128], func=AF.Square, accum_out=sm[:, h:h + 1]),
                          reads=okall, writes=[("gss", b, h)] + ([("enb", 1, 0), ("enb", 1, 1)] if h == 0 else []))
                sc.op("act", lambda e, sm=sm: e.activation(out=sm[:, 4:8], in_=sm[:, 0:4], func=AF.Sqrt, bias=1e-5, scale=1.0 / 128),
                      reads=[("gss", b, h) for h in range(4)], writes=[("grs", b)])
                sc.op("dve", lambda e, sm=sm: e.reciprocal(out=sm[:, 4:8], in_=sm[:, 4:8]), reads=[("grs", b)], writes=[("grs", b)])
                for h in range(4):
                    sc.op("dve", lambda e, h=h, sm=sm, b=b: e.scalar_tensor_tensor(
                        out=g_y[b][:, h * 128:(h + 1) * 128], in0=PS[:, obank, h * 128:(h + 1) * 128], scalar=sm[:, 4 + h:5 + h],
                        in1=gr_s[:, t, h * 128:(h + 1) * 128], op0=ALU.mult, op1=ALU.mult),
                        reads=psk(obank) + [("grs", b), ("gr_s", t)], writes=[("gy", b, h)] + ([("enb", 0, 0), ("enb", 0, 1)] if h == 0 else []))

            def g_tr(t):
                b = t % 2
                tk = psk(6 + b)
                for h in range(4):
                    sc.op("pe", lambda e, h=h, b=b: e.transpose(out=PSb[:, 6 + b, h * 128:(h + 1) * 128], in_=g_y[b][:, h * 128:(h + 1) * 128], identity=identb),
                          reads=[("gy", b, h), "identb"], writes=tk)
                sc.op("act", lambda e, b=b: e.activation(out=XT[:, 4:8, tl(t)], in_=PSb[:, 6 + b, 0:512].rearrange("p (c n) -> p c n", c=4), func=AF.Copy),
                      reads=tk, writes=[("XT", t)])

            g_A(0)
            for t in range(NT):
                if t + 1 < NT:
                    g_A(t + 1)
                if os.environ.get("GSKIP") == "B":
                    continue
                g_B(t)
                if os.environ.get("GSKIP") == "norm":
                    continue
                g_norm(t)
                if os.environ.get("GSKIP") == "tr":
                    continue
                if t > 0:
                    g_tr(t - 1)
            if not os.environ.get("GSKIP"):
                g_tr(NT - 1)
            dump("mixT", XT, [128, 8, 2048], [("XT", t) for t in range(NT)])

            if stop == 'G' and l == nl - 1:
                raise _Stop()
            sc.barrier()
            load_ln_params(ln1g_d[l], ln1b_d[l])
            R_D.reset()
            W1B = [R_D.get(8192, BF16, "p (c n) -> p c n", c=8) for _ in range(2)]
            W2B = [R_D.get(8192, BF16, "p (c n) -> p c n", c=4) for _ in range(2)]
            hT = R_D.get(16384, BF16, "p (c n) -> p c n", c=4)
            relu_t = [R_D.get(2048, F32) for _ in range(2)]

            def load_ffn_block(fb, buf):
                s1 = w1_d[l, :, fb * 512:(fb + 1) * 512].rearrange("(c p) n -> p c n", p=128)
                s2 = w2_d[l, fb * 512:(fb + 1) * 512, :].rearrange("(c p) n -> p c n", p=128)
                for hf in range(2):
                    sc.dma("pool", lambda e, hf=hf: e.dma_start(out=W1B[buf][:, hf * 4:(hf + 1) * 4, :], in_=s1[:, hf * 4:(hf + 1) * 4, :]),
                           writes=[("W1B", buf, hf)])
                for hf in range(2):
                    sc.dma("pool", lambda e, hf=hf: e.dma_start(out=W2B[buf][:, hf * 2:(hf + 1) * 2, :], in_=s2[:, hf * 2:(hf + 1) * 2, :]),
                           writes=[("W2B", buf, hf)])

            load_ffn_block(0, 0)
            load_ffn_block(1, 1)
            for t in range(NT):
                sc.dma("sp", lambda e, t=t: e.dma_start(out=X[:, t, :], in_=xs_d[t * 128:(t + 1) * 128, :]), reads=[("xsd", t)], writes=[("X", t)])
            def o_mm(t):
                yb = (t % 3) * 2
                for hf in range(2):
                    for c in range(8):
                        sc.op("pe", lambda e, c=c, hf=hf, t=t, yb=yb: e.matmul(PS[:, yb + hf, :], lhsT=XT[:, c, tl(t)], rhs=WO[:, c, hf * 512:(hf + 1) * 512],
                                                                              start=(c == 0), stop=(c == 7)),
                              reads=[("XT", t), ("RW", 0, 0), ("RW", 0, 1), ("RW", 1, 0), ("RW", 1, 1)], writes=psk(yb + hf))

            def o_ln_stages(t):
                yb = (t % 3) * 2
                resid = lambda: sc.op("dve", lambda e: e.scalar_tensor_tensor(out=X[:, t, :], in0=X[:, t, :], scalar=ALPHA,
                                                                              in1=PS[:, yb:yb + 2, :].rearrange("p a n -> p (a n)"), op0=ALU.mult, op1=ALU.add),
                                      reads=[("X", t)] + psk(yb) + psk(yb + 1), writes=[("X", t)])
                return [resid] + ln_a_stages(t)

            def o_ln_group(ts):
                lists = [o_ln_stages(t) for t in ts]
                for k in range(len(lists[0])):
                    for lst in lists:
                        lst[k]()

            for t0 in range(3):
                o_mm(t0)
            o_ln_group([0, 1])
            o_ln_group([2])
            for t in range(0, NT, 2):
                nxt = [u for u in (t + 3, t + 4) if u < NT]
                for u in nxt:
                    o_mm(u)
                ln_b1(t, None)
                ln_b1(t + 1, None)
                if nxt:
                    o_ln_group(nxt)
                ln_b2(t, None)
                ln_b2(t + 1, None)
            dump("x1T", XT, [128, 8, 2048], [("XT", t) for t in range(NT)])

            if stop == 'O' and l == nl - 1:
                raise _Stop()
            if not last:
                load_win_block(l + 1, 0, 0)
                load_win_block(l + 1, 1, 1)
            hrr = [0]
            for fb in range(8):
                buf = fb % 2
                for r in range(4):
                    for fc in range(4):
                        bank = 4 + hrr[0] % 3
                        rb = hrr[0] % 2
                        hrr[0] += 1
                        for c in range(8):
                            sc.op("pe", lambda e, c=c, fc=fc, r=r, bank=bank, buf=buf: e.matmul(
                                PS[:, bank, :], lhsT=W1B[buf][:, c, fc * 128:(fc + 1) * 128], rhs=XT[:, c, r * 512:(r + 1) * 512],
                                start=(c == 0), stop=(c == 7)), reads=[("W1B", buf, 0), ("W1B", buf, 1)] + xt_all[r * 4:(r + 1) * 4], writes=psk(bank))
                        fcol = fb * 4 + fc
                        sc.op("act", lambda e, bank=bank, rb=rb, fcol=fcol: e.activation(out=relu_t[rb], in_=PS[:, bank, :], func=AF.Relu,
                                                                                         bias=b1c[:, fcol:fcol + 1], scale=1.0),
                              reads=psk(bank) + ["b1c"], writes=[("relu", rb)])
                        sc.op("dve", lambda e, rb=rb, fc=fc, r=r: e.tensor_tensor(out=hT[:, fc, r * 512:(r + 1) * 512], in0=relu_t[rb], in1=relu_t[rb], op=ALU.mult),
                              reads=[("relu", rb)], writes=[("hT", fc, r)])
                for t in range(NT):
                    yb = (t % 2) * 2
                    for hf in range(2):
                        for fc in range(4):
                            sc.op("pe", lambda e, fc=fc, hf=hf, t=t, yb=yb, buf=buf: e.matmul(
                                PS[:, yb + hf, :], lhsT=hT[:, fc, tl(t)], rhs=W2B[buf][:, fc, hf * 512:(hf + 1) * 512],
                                start=(fc == 0), stop=(fc == 3)), reads=[("hT", fc, t // 4), ("W2B", buf, 0), ("W2B", buf, 1)], writes=psk(yb + hf))
                    if fb == 0:
                        sc.op("dve", lambda e, t=t, yb=yb: e.scalar_tensor_tensor(out=X[:, t, :], in0=X[:, t, :], scalar=ALPHA,
                                                                                  in1=PS[:, yb:yb + 2, :].rearrange("p a n -> p (a n)"), op0=ALU.mult, op1=ALU.add),
                              reads=[("X", t)] + psk(yb) + psk(yb + 1), writes=[("X", t)])
                        sc.op("pool", lambda e, t=t: e.tensor_tensor(out=X[:, t, :], in0=X[:, t, :], in1=b2t, op=ALU.add), reads=[("X", t), "b2t"], writes=[("X", t)])
                    else:
                        sc.op("dve", lambda e, t=t, yb=yb: e.tensor_tensor(out=X[:, t, :], in0=X[:, t, :], in1=PS[:, yb:yb + 2, :].rearrange("p a n -> p (a n)"), op=ALU.add),
                              reads=[("X", t)] + psk(yb) + psk(yb + 1), writes=[("X", t)])
                if fb + 2 < 8:
                    load_ffn_block(fb + 2, buf)
            if stop == 'F' and l == nl - 1:
                raise _Stop()
            load_ln_params(ln2g_d[l], ln2b_d[l])
            ln_all(out_d if last else xs_d)
            sc.barrier()


        for _l in range(nl):
            do_layer(_l)
    except _Stop:
        pass
    out_dmas = [o for o in sc.ops if o.is_dma and o.dkey in [("xs", i) for i in range(4)]]
    fin = {}
    for o in out_dmas:
        fin[o.dkey] = o
    finals = list(fin.values()) + list(dbg_out.values())
    sc.emit(final_wait_ops=finals)
    es.close()
    return nc, sc


_CONST = None


def kernel(**inputs):
    global _CONST
    if _CONST is None:
        _CONST = _constants()
    nc, _ = build(2)
    x = np.ascontiguousarray(inputs["x"], dtype=np.float32)
    shared = {k: np.ascontiguousarray(v, dtype=np.float32) for k, v in inputs.items() if k != "x"}
    shared.update(_CONST)
    in_maps = []
    for b in range(8):
        m = dict(shared)
        m["x"] = x[b]
        in_maps.append(m)
    res = run_bass_kernel_spmd(nc, in_maps, core_ids=list(range(8)))
    return np.stack([r["out"] for r in res.results], axis=0).astype(np.float32)
```

```python
import math
import os
from contextlib import ExitStack

import numpy as np
import concourse.bass as bass
import concourse.mybir as mybir
from concourse.bass_utils import run_bass_kernel_spmd

F32 = mybir.dt.float32
BF16 = mybir.dt.bfloat16
AF = mybir.ActivationFunctionType
ALU = mybir.AluOpType

S = 2048
D = 1024
DIN = 3104
DFF = 4096
NT = 16
ALPHA = (2.0 * 2) ** 0.25
ENGS = ("pe", "act", "dve", "pool", "sp")
EPOCH = 30000


class _Res:
    __slots__ = ("last_w", "readers")

    def __init__(self):
        self.last_w = None
        self.readers = []


class _Op:
    __slots__ = ("eng", "fn", "deps", "signal", "tok", "is_dma", "dkey")

    def __init__(self, eng, fn, is_dma, dkey):
        self.eng = eng
        self.fn = fn
        self.deps = []
        self.signal = False
        self.tok = None
        self.is_dma = is_dma
        self.dkey = dkey


class Sched:
    def __init__(self, nc):
        self.nc = nc
        self.ops = []
        self.res = {}
        self.pending = {e: [] for e in ENGS}

    def _r(self, key):
        x = self.res.get(key)
        if x is None:
            x = self.res[key] = _Res()
        return x

    def _add(self, op, reads, writes):
        deps = set()
        for k in reads:
            rs = self._r(k)
            if rs.last_w is not None:
                deps.add(rs.last_w)
        for k in writes:
            rs = self._r(k)
            if rs.last_w is not None:
                deps.add(rs.last_w)
            deps.update(rs.readers)
        for k in reads:
            self._r(k).readers.append(op)
        for k in writes:
            rs = self._r(k)
            rs.last_w = op
            rs.readers = []
        if self.pending[op.eng]:
            deps.update(self.pending[op.eng])
            self.pending[op.eng] = []
        deps.discard(op)
        op.deps = list(deps)
        self.ops.append(op)
        return op

    def op(self, eng, fn, reads=(), writes=()):
        return self._add(_Op(eng, fn, False, None), reads, writes)

    def dma(self, eng, fn, dkey=None, reads=(), writes=()):
        if dkey is None:
            dkey = ("w", writes[0])
        return self._add(_Op(eng, fn, True, dkey), reads, writes)

    def barrier(self):
        last = {}
        for o in self.ops:
            last[(o.eng, o.dkey) if o.is_dma else o.eng] = o
        b = list(last.values())
        self.pending = {e: list(b) for e in ENGS}

    def emit(self, final_wait_ops=()):
        nc = self.nc
        ops = self.ops
        for o in ops:
            for d in o.deps:
                if d.is_dma:
                    d.signal = True
                elif d.eng == "pe" and o.eng == "pe" and not o.is_dma:
                    continue
                else:
                    d.signal = True
        with ExitStack() as es:
            eng_sems = {e: [] for e in ENGS}
            cnt = {e: 0 for e in ENGS}
            dma_sems = {}
            dma_cnt = {}
            for o in ops:
                if o.is_dma:
                    if o.dkey not in dma_sems:
                        dma_sems[o.dkey] = es.enter_context(nc.semaphore("d%d" % len(dma_sems)))
                        dma_cnt[o.dkey] = 0
                    dma_cnt[o.dkey] += 16
                    o.tok = (dma_sems[o.dkey], dma_cnt[o.dkey])
                elif o.signal:
                    ep = cnt[o.eng] // EPOCH
                    if ep >= len(eng_sems[o.eng]):
                        eng_sems[o.eng].append(es.enter_context(nc.semaphore("e_%s_%d" % (o.eng, ep))))
                    cnt[o.eng] += 1
                    o.tok = (eng_sems[o.eng][ep], cnt[o.eng] - ep * EPOCH)
            per_eng = {e: [o for o in ops if o.eng == e] for e in ENGS}
            self.stats = {e: len(per_eng[e]) for e in ENGS}
            self.stats["sems"] = sum(len(v) for v in eng_sems.values()) + len(dma_sems)

            def run(e, eng):
                waited = {}
                for o in per_eng[e]:
                    need = {}
                    for d in o.deps:
                        if d.tok is None:
                            continue
                        if (not d.is_dma) and d.eng == "pe" and e == "pe" and not o.is_dma:
                            continue
                        s, v = d.tok
                        k = id(s)
                        if waited.get(k, 0) >= v:
                            continue
                        if k not in need or need[k][1] < v:
                            need[k] = (s, v)
                    for k, (s, v) in need.items():
                        eng.wait_ge(s, v)
                        waited[k] = v
                    ins = o.fn(eng)
                    if o.tok is not None:
                        ins.then_inc(o.tok[0], 16 if o.is_dma else 1)
                if e == "sp":
                    for o in final_wait_ops:
                        s, v = o.tok
                        eng.wait_ge(s, v)

            with nc.Block() as block:
                @block.sync
                def _(eng):
                    run("sp", eng)

                @block.tensor
                def _(eng):
                    run("pe", eng)

                @block.scalar
                def _(eng):
                    run("act", eng)

                @block.vector
                def _(eng):
                    run("dve", eng)

                @block.gpsimd
                def _(eng):
                    run("pool", eng)


def _t5_bucket(rel):
    nb = 16
    me = 8
    ret = np.where(rel > 0, nb, 0)
    n = np.abs(rel)
    large = me + (np.log(np.maximum(n, 1).astype(np.float32) / np.float32(me))
                  / np.float32(math.log(128 / me)) * np.float32(nb - me)).astype(np.int32)
    large = np.minimum(large, nb - 1)
    return ret + np.where(n < me, n, large)


MLEN = 1280


def _constants():
    c = {}
    c["c_ident"] = np.eye(128, dtype=np.float32)
    c["c_J"] = np.eye(128, dtype=np.float32)[::-1].copy()
    s = np.arange(128)[:, None]
    t = np.arange(128)[None, :]
    uf = np.zeros((128, 129), np.float32)
    uf[:, :128] = (s <= t)
    uf[:, 128] = 1.0
    ub = np.zeros((128, 129), np.float32)
    ub[:, :128] = (s >= t)
    ub[:, 128] = 1.0
    c["c_uf"] = uf
    c["c_ub"] = ub
    c["c_sf"] = (s > t).astype(np.float32)
    c["c_sb"] = (s < t).astype(np.float32)
    n = np.arange(MLEN)
    bk = _t5_bucket(639 - n)
    oh = np.zeros((32, MLEN), np.float32)
    oh[bk, n] = 1.0
    c["c_onehot"] = oh
    return c


class _Stop(Exception):
    pass


def build(nl=2, dbg=(), stop=None):
    nc = bass.Bass("TRN2", target_bir_lowering=False)

    def din(name, shape):
        return nc.dram_tensor(name, list(shape), F32, kind="ExternalInput").ap()

    x_d = din("x", [S, D])
    lnemb_g = din("ln_emb_g", [D])
    lnemb_b = din("ln_emb_b", [D])
    table_d = din("rel_bias_table", [32, 4])
    w_in_d = din("w_in", [2, D, DIN])
    lq1_d = din("lambda_q1", [2, 64])
    lk1_d = din("lambda_k1", [2, 64])
    lq2_d = din("lambda_q2", [2, 64])
    lk2_d = din("lambda_k2", [2, 64])
    dnw_d = din("diff_norm_w", [2, 128])
    gup_d = din("gla_gate_up", [2, 2, 16, 256])
    gbias_d = din("gla_gate_bias", [2, 2, 256])
    gnw_d = din("gla_norm_w", [2, 128])
    w_o_d = din("w_o", [2, D, D])
    ln1g_d = din("ln1_g", [2, D])
    ln1b_d = din("ln1_b", [2, D])
    w1_d = din("w_ffn1", [2, D, DFF])
    b1_d = din("b_ffn1", [2, DFF])
    w2_d = din("w_ffn2", [2, DFF, D])
    b2_d = din("b_ffn2", [2, D])
    ln2g_d = din("ln2_g", [2, D])
    ln2b_d = din("ln2_b", [2, D])
    c_ident = din("c_ident", [128, 128])
    c_J = din("c_J", [128, 128])
    c_uf = din("c_uf", [128, 129])
    c_ub = din("c_ub", [128, 129])
    c_sf = din("c_sf", [128, 128])
    c_sb = din("c_sb", [128, 128])
    c_onehot = din("c_onehot", [32, MLEN])
    out_d = nc.dram_tensor("out", [S, D], F32, kind="ExternalOutput").ap()
    xs_d = nc.dram_tensor("xs_scratch", [S, D], F32).ap()
    md_t = nc.dram_tensor("md_scratch", [4, MLEN], F32)
    eb_d = nc.dram_tensor("expb_scratch", [128, 4 * 1152], BF16).ap()
    md_d = md_t.ap()
    dbg_out = {}

    sc = Sched(nc)
    es = ExitStack()
    ARENA_BYTES = 207 * 1024
    arena = es.enter_context(nc.sbuf_tensor("arena", [128, ARENA_BYTES // 2], BF16))
    PSb = es.enter_context(nc.psum_tensor("ps", [128, 8, 1024], BF16))[:]
    PS = PSb.bitcast(F32)

    def view(off, nbytes, dt, pattern=None, **kw):
        assert off % 32 == 0, off
        a = arena[:, off // 2:(off + nbytes) // 2]
        if dt is F32:
            a = a.bitcast(F32)
        if pattern:
            a = a.rearrange(pattern, **kw)
        return a

    class Alloc:
        def __init__(self, base, size):
            self.base = base
            self.size = size
            self.pos = 0

        def reset(self):
            self.pos = 0

        def get(self, nbytes, dt, pattern=None, **kw):
            n = (nbytes + 31) // 32 * 32
            assert self.pos + n <= self.size, (self.pos, n, self.size)
            v = view(self.base + self.pos, nbytes, dt, pattern, **kw)
            self.pos += n
            return v

    R_XT = Alloc(0, 32768)
    R_X = Alloc(32768, 65536)
    R_D = Alloc(98304, 70656)
    R_W = Alloc(168960, 16384)
    R_C = Alloc(185344, ARENA_BYTES - 185344)

    XT = R_XT.get(32768, BF16, "p (c n) -> p c n", c=8)
    X = R_X.get(65536, F32, "p (t n) -> p t n", t=NT)
    WB = [R_W.get(8192, BF16, "p (c n) -> p c n", c=8) for _ in range(2)]
    R_W.reset()
    WO = R_W.get(16384, BF16, "p (c n) -> p c n", c=8)

    identb = R_C.get(256, BF16)
    Jb = R_C.get(256, BF16)
    Uf = R_C.get(516, F32)
    Ub = R_C.get(516, F32)
    Usf = R_C.get(512, F32)
    Usb = R_C.get(512, F32)
    gt = R_C.get(4096, F32)
    bt = R_C.get(4096, F32)
    b2t = R_C.get(4096, F32)
    wd_t = R_C.get(512, F32)
    wg_t = R_C.get(512, F32)
    b1c = R_C.get(128, F32)
    cb = R_C.get(32, F32, "p (s h) -> p s h", s=2)
    lamv = R_C.get(4 * 64 * 4, F32, "p (a n) -> p a n", a=4)
    lamp = R_C.get(2 * 64 * 4, F32, "p (a n) -> p a n", a=2)
    lams = R_C.get(32, F32)
    Wg = R_C.get(1024, BF16)
    st_ = [R_C.get(48, F32) for _ in range(2)]
    mv_ = [R_C.get(8, F32) for _ in range(2)]
    rs_ = [R_C.get(4, F32) for _ in range(2)]
    xb_ = [R_C.get(2048, BF16) for _ in range(2)]
    dsm = [R_C.get(64, F32) for _ in range(2)]
    gsm = [R_C.get(64, F32) for _ in range(2)]
    decs = R_C.get(2 * 2 * 16 * 4, F32, "p (d q t) -> p d q t", d=2, q=2)

    def psk(b):
        return [("ps", b, q) for q in range(4)]

    def bc_mid(ap2, n):
        a = ap2.ap
        return bass.AP(ap2.tensor, ap2.offset, [list(a[0]), [0, n], list(a[1])])

    def bc_last(ap2, n):
        a = ap2.ap
        return bass.AP(ap2.tensor, ap2.offset, [list(a[0]), list(a[1]), [0, n]])

    cur_layer = [-1]

    def dump(name, ap, shape, reads):
        nm = "%s@%d" % (name, cur_layer[0])
        if nm in dbg:
            name = nm
        elif name not in dbg or (cur_layer[0] >= 0 and cur_layer[0] != nl - 1):
            return
        t = nc.dram_tensor("dbg_" + name.replace("@", "_"), list(shape), ap.dtype, kind="ExternalOutput").ap()
        dbg_out[name] = sc.dma("sp", lambda e: e.dma_start(out=t, in_=ap), "dbg", reads=reads)

    try:
        sc.dma("pool", lambda e: e.dma_start(out=identb, in_=c_ident), writes=["identb"])
        sc.dma("pool", lambda e: e.dma_start(out=Jb, in_=c_J), writes=["Jb"])
        sc.dma("sp", lambda e: e.dma_start(out=Uf, in_=c_uf), writes=["Uf"])
        sc.dma("sp", lambda e: e.dma_start(out=Ub, in_=c_ub), writes=["Ub"])
        sc.dma("sp", lambda e: e.dma_start(out=Usf, in_=c_sf), writes=["Usf"])
        sc.dma("sp", lambda e: e.dma_start(out=Usb, in_=c_sb), writes=["Usb"])
        for si, row in enumerate((15, 31)):
            sc.dma("sp", lambda e, si=si, row=row: e.dma_start(out=cb[:, si, :], in_=table_d[row, :].partition_broadcast(128)),
                   writes=[("cb", si)])

        R_D.reset()
        tb = R_D.get(16, F32)
        oh = R_D.get(MLEN * 4, F32)
        msb = R_D.get(MLEN * 4, F32)
        sc.dma("sp", lambda e: e.dma_start(out=tb[0:32, :], in_=table_d), writes=["tb"])
        sc.dma("sp", lambda e: e.dma_start(out=oh[0:32, :], in_=c_onehot), writes=["oh"])
        sc.dma("sp", lambda e: e.dma_start(out=gt, in_=lnemb_g.partition_broadcast(128)), writes=["gt"])
        sc.dma("sp", lambda e: e.dma_start(out=bt, in_=lnemb_b.partition_broadcast(128)), writes=["bt"])
        for t in range(NT):
            sc.dma("sp", lambda e, t=t: e.dma_start(out=X[:, t, :], in_=x_d[t * 128:(t + 1) * 128, :]), writes=[("X", t)])
        for ci, (c0, cn) in enumerate(((0, 512), (512, 512), (1024, 256))):
            sc.op("pe", lambda e, ci=ci, c0=c0, cn=cn: e.matmul(PS[0:4, ci, 0:cn], lhsT=tb[0:32, :], rhs=oh[0:32, c0:c0 + cn], start=True, stop=True),
                  reads=["tb", "oh"], writes=psk(ci))
            sc.op("dve", lambda e, ci=ci, c0=c0, cn=cn: e.tensor_copy(out=msb[0:4, c0:c0 + cn], in_=PS[0:4, ci, 0:cn]),
                  reads=psk(ci), writes=["msb"])
        sc.dma("sp", lambda e: e.dma_start(out=md_d, in_=msb[0:4, :]), reads=["msb"], writes=["md"])
        R_D.reset()
        R_D.get(16384, BF16); R_D.get(16384, BF16); R_D.get(16 * 4 * 129 * 2, BF16)
        expB0 = R_D.get(4 * 1152 * 2, BF16, "p (h n) -> p h n", h=4)
        R_D.get(4096, BF16); R_D.get(1024, BF16)
        tmp_revs = [view(R_D.base + 16384 + i * 2304, 2304, BF16) for i in range(4)]
        for h in range(4):
            src = bass.AP(md_t, h * MLEN, [[1, 128], [1, 1152]])
            sc.dma("pool", lambda e, src=src, h=h: e.dma_start(out=tmp_revs[h], in_=src), reads=["md"], writes=[("tmp_rev", h)])
        for h in range(4):
            for ci, (c0, cn) in enumerate(((0, 512), (512, 512), (1024, 128))):
                sc.op("pe", lambda e, ci=ci, c0=c0, cn=cn, h=h: e.matmul(PS[:, ci, 0:cn], lhsT=Jb, rhs=tmp_revs[h][:, c0:c0 + cn], start=True, stop=True),
                      reads=["Jb", ("tmp_rev", h)], writes=psk(ci))
                sc.op("act", lambda e, h=h, ci=ci, c0=c0, cn=cn: e.activation(out=expB0[:, h, c0:c0 + cn], in_=PS[:, ci, 0:cn], func=AF.Exp),
                      reads=psk(ci), writes=[("expB", h)])
        sc.dma("sp", lambda e: e.dma_start(out=eb_d, in_=expB0.rearrange("p h n -> p (h n)")), reads=[("expB", h) for h in range(4)], writes=["eb_d"])

        def ln_a_stages(t):
            Xt = X[:, t, :]
            kx = ("X", t)
            b = t % 2
            st, mv, rs = st_[b], mv_[b], rs_[b]
            return [
                lambda: sc.op("dve", lambda e: e.bn_stats(out=st[:, 0:6], in_=Xt[:, 0:512]), reads=[kx], writes=[("st", b, 0)]),
                lambda: sc.op("dve", lambda e: e.bn_stats(out=st[:, 6:12], in_=Xt[:, 512:1024]), reads=[kx], writes=[("st", b, 1)]),
                lambda: sc.op("dve", lambda e: e.bn_aggr(out=mv, in_=st), reads=[("st", b, 0), ("st", b, 1)], writes=[("mv", b)]),
                lambda: sc.op("act", lambda e: e.activation(out=rs, in_=mv[:, 1:2], func=AF.Sqrt, bias=1e-5, scale=1.0), reads=[("mv", b)], writes=[("rs", b)]),
                lambda: sc.op("dve", lambda e: e.reciprocal(out=rs, in_=rs), reads=[("rs", b)], writes=[("rs", b)]),
                lambda: sc.op("dve", lambda e: e.tensor_scalar(out=Xt, in0=Xt, scalar1=mv[:, 0:1], scalar2=rs, op0=ALU.subtract, op1=ALU.mult),
                              reads=[kx, ("mv", b), ("rs", b)], writes=[kx]),
                lambda: sc.op("dve", lambda e: e.tensor_tensor(out=Xt, in0=Xt, in1=gt, op=ALU.mult), reads=[kx, "gt"], writes=[kx]),
                lambda: sc.op("pool", lambda e: e.tensor_tensor(out=Xt, in0=Xt, in1=bt, op=ALU.add), reads=[kx, "bt"], writes=[kx]),
            ]

        def ln_a(t):
            for f in ln_a_stages(t):
                f()

        def ln_a_pair(t0, t1):
            sa, sb = ln_a_stages(t0), ln_a_stages(t1)
            for fa, fb in zip(sa, sb):
                fa()
                fb()

        def ln_b1(t, spill_to):
            Xt = X[:, t, :]
            kx = ("X", t)
            b = t % 2
            xb = xb_[b]
            if spill_to is not None:
                sc.dma("sp", lambda e: e.dma_start(out=spill_to[t * 128:(t + 1) * 128, :], in_=Xt), ("xs", t % 4), reads=[kx], writes=[("xsd", t)])
            if spill_to is out_d:
                return
            sc.op("act", lambda e: e.activation(out=xb, in_=Xt, func=AF.Copy), reads=[kx], writes=[("xb", b)])

        def ln_b2(t, spill_to):
            if spill_to is out_d:
                return
            b = t % 2
            xb = xb_[b]
            bank = 6 + b
            for c in range(8):
                sc.op("pe", lambda e, c=c: e.transpose(out=PSb[:, bank, c * 128:(c + 1) * 128], in_=xb[:, c * 128:(c + 1) * 128], identity=identb),
                      reads=[("xb", b), "identb"], writes=psk(bank))
            sc.op("act", lambda e: e.activation(out=XT[:, :, t * 128:(t + 1) * 128], in_=PSb[:, bank, :].rearrange("p (c n) -> p c n", c=8), func=AF.Copy),
                  reads=psk(bank), writes=[("XT", t)])

        def ln_all(spill_to):
            ln_a_pair(0, 1)
            for t in range(0, NT, 2):
                if t + 2 < NT:
                    ln_a_pair(t + 2, t + 3)
                ln_b1(t, spill_to)
                ln_b1(t + 1, spill_to)
                ln_b2(t, spill_to)
                ln_b2(t + 1, spill_to)

        def load_ln_params(g_ap, b_ap):
            sc.dma("sp", lambda e: e.dma_start(out=gt, in_=g_ap.partition_broadcast(128)), writes=["gt"])
            sc.dma("sp", lambda e: e.dma_start(out=bt, in_=b_ap.partition_broadcast(128)), writes=["bt"])

        def load_win_block(l, blk, buf):
            c0 = blk * 512
            ncol = min(512, DIN - c0)
            src = w_in_d[l, :, c0:c0 + ncol].rearrange("(c p) n -> p c n", p=128)
            for hf in range(2):
                sc.dma("pool", lambda e, hf=hf: e.dma_start(out=WB[buf][:, hf * 4:(hf + 1) * 4, 0:ncol], in_=src[:, hf * 4:(hf + 1) * 4, :]),
                       writes=[("RW", buf, hf)])

        if stop == 'init':
            raise _Stop()
        load_win_block(0, 0, 0)
        load_win_block(0, 1, 1)
        ln_all(xs_d)
        dump("h0", X, [128, NT, 1024], [("X", t) for t in range(NT)])

        if stop == 'emb':
            raise _Stop()
        evac_rr = [0]

        def evac(out, in_, reads, writes, scale=None):
            evac_rr[0] ^= 1
            if evac_rr[0]:
                if scale is None:
                    sc.op("act", lambda e: e.activation(out=out, in_=in_, func=AF.Copy), reads=reads, writes=writes)
                else:
                    sc.op("act", lambda e: e.mul(out=out, in_=in_, mul=scale), reads=reads, writes=writes)
            else:
                if scale is None:
                    sc.op("dve", lambda e: e.tensor_copy(out=out, in_=in_), reads=reads, writes=writes)
                else:
                    sc.op("dve", lambda e: e.tensor_scalar(out=out, in0=in_, scalar1=scale, scalar2=None, op0=ALU.mult), reads=reads, writes=writes)

        def do_layer(l):
            lam_init = 0.8 - 0.6 * math.exp(-0.3 * l)
            cur_layer[0] = l
            if l == 0:
                sc.barrier()
            last = (l == nl - 1)
            R_D.reset()
            QT = R_D.get(16384, BF16, "p (h n) -> p h n", h=4)
            KT = R_D.get(16384, BF16, "p (h n) -> p h n", h=4)
            V = R_D.get(16 * 4 * 129 * 2, BF16, "p (t h e) -> p t h e", t=16, h=4)
            expB = R_D.get(4 * 1152 * 2, BF16, "p (h n) -> p h n", h=4)
            Eb = R_D.get(4096, BF16, "p (b m n) -> p b m n", b=2, m=2)
            d_y = R_D.get(4 * 128 * 2, BF16, "p (u n) -> p u n", u=4)
            _pu = R_D.pos
            silu_t = [R_D.get(2048, F32) for _ in range(2)]
            R_D.pos = _pu
            accS = R_D.get(8 * 129 * 4, F32, "p (a n) -> p a n", a=8)
            R_D.pos = _pu + 4608
            tmp_rev = R_D.get(1152 * 2, BF16)
            R_X.reset()
            gqT = R_X.get(8192, BF16, "p (c n) -> p c n", c=2)
            gkT = R_X.get(8192, BF16, "p (c n) -> p c n", c=2)
            gk_tok = R_X.get(8192, BF16, "p (t n) -> p t n", t=16)
            gv = R_X.get(16384, BF16, "p (t n) -> p t n", t=16)
            gr_s = R_X.get(16384, BF16, "p (t n) -> p t n", t=16)
            G33 = R_X.get(4096, BF16)

            for i, ap in enumerate((lq1_d, lk1_d, lq2_d, lk2_d)):
                sc.dma("sp", lambda e, i=i, ap=ap: e.dma_start(out=lamv[:, i, :], in_=ap[l, :].partition_broadcast(128)), writes=[("lamv", i)])
            sc.op("dve", lambda e: e.tensor_tensor(out=lamp[:, 0, :], in0=lamv[:, 0, :], in1=lamv[:, 1, :], op=ALU.mult), reads=[("lamv", 0), ("lamv", 1)], writes=["lamp"])
            sc.op("dve", lambda e: e.tensor_tensor(out=lamp[:, 1, :], in0=lamv[:, 2, :], in1=lamv[:, 3, :], op=ALU.mult), reads=[("lamv", 2), ("lamv", 3)], writes=["lamp"])
            sc.op("dve", lambda e: e.reduce_sum(out=lams[:, 0:2], in_=lamp, axis=mybir.AxisListType.X), reads=["lamp"], writes=["lams"])
            sc.op("act", lambda e: e.activation(out=lams[:, 0:2], in_=lams[:, 0:2], func=AF.Exp), reads=["lams"], writes=["lams"])
            sc.op("dve", lambda e: e.tensor_tensor(out=lams[:, 2:3], in0=lams[:, 0:1], in1=lams[:, 1:2], op=ALU.subtract), reads=["lams"], writes=["lams"])
            sc.op("dve", lambda e: e.tensor_scalar(out=lams[:, 3:4], in0=lams[:, 2:3], scalar1=lam_init, scalar2=-1.0, op0=ALU.add, op1=ALU.mult),
                  reads=["lams"], writes=["neglam"])
            neg_lam = lams[:, 3:4]
            sc.dma("sp", lambda e: e.dma_start(out=wd_t, in_=dnw_d[l, :].partition_broadcast(128)), writes=["wd"])
            sc.op("dve", lambda e: e.tensor_scalar(out=wd_t, in0=wd_t, scalar1=1.0 - lam_init, scalar2=None, op0=ALU.mult), reads=["wd"], writes=["wd"])
            sc.dma("sp", lambda e: e.dma_start(out=wg_t, in_=gnw_d[l, :].partition_broadcast(128)), writes=["wg"])
            sc.op("dve", lambda e: e.memset(Wg[0:33, :], 0.0), writes=["Wg"])
            sc.dma("pool", lambda e: e.dma_start(out=Wg[0:16, 0:256], in_=gup_d[l, 0]), writes=["Wg"])
            sc.dma("pool", lambda e: e.dma_start(out=Wg[16:32, 256:512], in_=gup_d[l, 1]), writes=["Wg"])
            sc.dma("pool", lambda e: e.dma_start(out=Wg[32:33, :], in_=gbias_d[l].rearrange("a n -> (a n)").partition_broadcast(1)), writes=["Wg"])
            sc.dma("sp", lambda e: e.dma_start(out=b1c, in_=b1_d[l].rearrange("(c p) -> p c", p=128), allow_slow_non_contiguous=True), writes=["b1c"])
            sc.dma("sp", lambda e: e.dma_start(out=b2t, in_=b2_d[l].partition_broadcast(128)), writes=["b2t"])
            sc.dma("sp", lambda e: e.dma_start(out=expB.rearrange("p h n -> p (h n)"), in_=eb_d), reads=["eb_d"], writes=[("expB", h) for h in range(4)])
            sc.op("dve", lambda e: e.memset(V[:, :, :, 128:129], 1.0), writes=[("V", t) for t in range(NT)])
            sc.op("dve", lambda e: e.memset(G33[32:33, :], 1.0), writes=["G33"])

            dump("XTin", XT, [128, 8, 2048], [("XT", t) for t in range(NT)])
            dump("Xin", X, [128, NT, 1024], [("X", t) for t in range(NT)])
            if stop == 'L' and l == nl - 1:
                raise _Stop()
            ps_rr = [0]

            def nextbank():
                b = ps_rr[0] % 6
                ps_rr[0] += 1
                return b

            xt_all = [("XT", t) for t in range(NT)]
            for blk in range(7):
                buf = blk % 2
                wb = WB[buf]
                kw = [("RW", buf, 0), ("RW", buf, 1)]
                if blk in (0, 1, 3, 6):
                    nch = 1 if blk == 6 else 4
                    for cc in range(nch):
                        for r in range(4):
                            bank = nextbank()
                            M = 32 if blk == 6 else 128
                            for c in range(8):
                                sc.op("pe", lambda e, c=c, cc=cc, r=r, bank=bank, M=M, wb=wb: e.matmul(
                                    PS[0:M, bank, :], lhsT=wb[:, c, cc * 128:cc * 128 + M], rhs=XT[:, c, r * 512:(r + 1) * 512],
                                    start=(c == 0), stop=(c == 7)), reads=kw + xt_all[r * 4:(r + 1) * 4], writes=psk(bank))
                            sl = slice(r * 512, (r + 1) * 512)
                            if blk == 0:
                                evac(QT[:, cc, sl], PS[:, bank, :], psk(bank), [("QT", cc, r)], scale=0.125)
                            elif blk == 1:
                                evac(KT[:, cc, sl], PS[:, bank, :], psk(bank), [("KT", cc, r)])
                            elif blk == 3:
                                if cc < 2:
                                    evac(gqT[:, cc, sl], PS[:, bank, :], psk(bank), [("gqT", 4 * r + i) for i in range(4)], scale=0.125)
                                else:
                                    evac(gkT[:, cc - 2, sl], PS[:, bank, :], psk(bank), [("gkT", 4 * r + i) for i in range(4)])
                            else:
                                evac(G33[0:32, sl], PS[0:32, bank, :], psk(bank), ["G33"])
                if blk == 3:
                    for t in range(NT):
                        bank = nextbank()
                        for cc in range(2):
                            sc.op("pe", lambda e, t=t, cc=cc, bank=bank: e.transpose(out=PSb[:, bank, cc * 128:(cc + 1) * 128], in_=gkT[:, cc, t * 128:(t + 1) * 128], identity=identb),
                                  reads=[("gkT", t), "identb"], writes=psk(bank))
                        evac(gk_tok[:, t, :], PSb[:, bank, 0:256], psk(bank), [("gk_tok", t)])
                if blk in (2, 4, 5):
                    for t in range(NT):
                        bank = nextbank()
                        c0, ncol = (0, 512)
                        for c in range(8):
                            sc.op("pe", lambda e, c=c, t=t, bank=bank, c0=c0, ncol=ncol, wb=wb: e.matmul(
                                PS[:, bank, 0:ncol], lhsT=XT[:, c, t * 128:(t + 1) * 128], rhs=wb[:, c, c0:c0 + ncol],
                                start=(c == 0), stop=(c == 7)), reads=kw + [("XT", t)], writes=psk(bank))
                        if blk == 2:
                            evac(V[:, t, :, 0:128], PS[:, bank, :].rearrange("p (h e) -> p h e", h=4), psk(bank), [("V", t)])
                        elif blk == 4:
                            evac(gv[:, t, :], PS[:, bank, :], psk(bank), [("gv", t)])
                        else:
                            sb = t % 2
                            sc.op("act", lambda e, bank=bank, sb=sb: e.activation(out=silu_t[sb], in_=PS[:, bank, :], func=AF.Silu),
                                  reads=psk(bank), writes=[("silu", sb)])
                            sc.op("dve", lambda e, t=t, sb=sb: e.tensor_tensor(
                                out=gr_s[:, t, :].rearrange("p (h e) -> p h e", h=4), in0=silu_t[sb].rearrange("p (h e) -> p h e", h=4),
                                in1=bc_mid(wg_t, 4), op=ALU.mult), reads=[("silu", sb), "wg"], writes=[("gr_s", t)])
                if blk + 2 < 7:
                    load_win_block(l, blk + 2, buf)
            for hf in range(2):
                sc.dma("pool", lambda e, hf=hf: e.dma_start(out=WO[:, hf * 4:(hf + 1) * 4, :],
                                                             in_=w_o_d[l].rearrange("(c p) n -> p c n", p=128)[:, hf * 4:(hf + 1) * 4, :]),
                       writes=[("RW", hf, 0), ("RW", hf, 1)])
            dump("QT", QT, [128, 4, 2048], [("QT", a, b) for a in range(4) for b in range(4)])
            dump("KT", KT, [128, 4, 2048], [("KT", a, b) for a in range(4) for b in range(4)])
            dump("V", V, [128, 16, 4, 129], [("V", t) for t in range(NT)])
            dump("expB", expB, [128, 4, 1152], [("expB", h) for h in range(4)])
            dump("gqT", gqT, [128, 2, 2048], [("gqT", t) for t in range(NT)])
            dump("gr_s", gr_s, [128, 16, 512], [("gr_s", t) for t in range(NT)])
            dump("G33", G33[0:33, :], [33, 2048], ["G33"])

            if stop == 'P' and l == nl - 1:
                raise _Stop()
            steps = [(h, r, j) for h in range(4) for r in range(4) for j in range(16)]

            def acc_ap(m, u):
                idx = m * 4 + u
                return PS[:, 4 + idx // 3, (idx % 3) * 160:(idx % 3) * 160 + 129]

            def acc_keys(m, u):
                return psk(4 + (m * 4 + u) // 3)

            Eb3 = view(R_X.base + 61440, 2048, BF16, "p (m n) -> p m n", m=2)
            EbL = [Eb[:, 0, :, :], Eb[:, 1, :, :], Eb3]

            def d_scores(i):
                h, r, j = steps[i]
                d = j - 4 * r
                mixed = (-1 <= d <= 4)
                sb = i % 2
                eb = i % 3
                E = EbL[eb]
                for m in range(2):
                    bank = sb * 2 + m
                    sc.op("pe", lambda e, h=h, r=r, j=j, m=m, bank=bank: e.matmul(
                        PS[:, bank, :], lhsT=KT[64 * m:64 * m + 64, h, j * 128:(j + 1) * 128],
                        rhs=QT[64 * m:64 * m + 64, h, r * 512:(r + 1) * 512], start=True, stop=True),
                        reads=[("KT", h, j // 4), ("QT", h, r)], writes=psk(bank))
                pk2 = psk(sb * 2) + psk(sb * 2 + 1)
                ek = [("E", eb, 0), ("E", eb, 1)]
                if mixed:
                    c0 = (4 - d) * 128
                    sc.op("act", lambda e, sb=sb, E=E: e.activation(out=E, in_=PS[:, sb * 2:sb * 2 + 2, :], func=AF.Exp),
                          reads=pk2, writes=ek)
                    for m in range(2):
                        sc.op("dve", lambda e, E=E, m=m, h=h, c0=c0: e.tensor_tensor(out=E[:, m, :], in0=E[:, m, :], in1=expB[:, h, c0:c0 + 512], op=ALU.mult),
                              reads=[("E", eb, m), ("expB", h)], writes=[("E", eb, m)])
                else:
                    side = 0 if d < 0 else 1
                    sc.op("act", lambda e, sb=sb, E=E, side=side, h=h: e.activation(
                        out=E, in_=PS[:, sb * 2:sb * 2 + 2, :], func=AF.Exp, bias=cb[:, side, h:h + 1]),
                        reads=pk2 + [("cb", 0), ("cb", 1)], writes=ek)

            def d_av(i):
                h, r, j = steps[i]
                eb = i % 3
                E = EbL[eb]
                for m in range(2):
                    for u in range(4):
                        sc.op("pe", lambda e, h=h, j=j, m=m, u=u, E=E: e.matmul(
                            acc_ap(m, u), lhsT=E[:, m, u * 128:(u + 1) * 128], rhs=V[:, j, h, 0:129],
                            start=(j == 0 and (m * 4 + u) % 3 == 0), stop=(j == 15), skip_group_check=True),
                            reads=[("E", eb, m), ("V", j)], writes=acc_keys(m, u))

            sm = dsm[0]
            def ak(*idx):
                return [("accS", i) for i in idx]

            def d_final(h, r):
                sc.op("dve", lambda e: e.tensor_copy(out=accS[:, 0:3, :], in_=PS[:, 4, 0:480].rearrange("p (a n) -> p a n", a=3)[:, :, 0:129]),
                      reads=psk(4), writes=ak(0, 1, 2) + [("silu", 0), ("silu", 1)])
                sc.op("dve", lambda e: e.tensor_copy(out=accS[:, 3:6, :], in_=PS[:, 5, 0:480].rearrange("p (a n) -> p a n", a=3)[:, :, 0:129]),
                      reads=psk(5), writes=ak(3, 4, 5))
                sc.op("dve", lambda e: e.tensor_copy(out=accS[:, 6:8, :], in_=PS[:, 6, 0:320].rearrange("p (a n) -> p a n", a=2)[:, :, 0:129]),
                      reads=psk(6), writes=ak(6, 7))

            def d_final2(h, r):
                sc.op("dve", lambda e: e.reciprocal(out=sm[:, 0:8], in_=accS[:, :, 128]), reads=ak(*range(8)), writes=["dsm"])
                sc.op("dve", lambda e: e.tensor_scalar(out=sm[:, 4:8], in0=sm[:, 4:8], scalar1=neg_lam, scalar2=None, op0=ALU.mult),
                      reads=["dsm", "neglam"], writes=["dsm"])
                sc.op("dve", lambda e: e.memset(sm[:, 8:12], 0.0), writes=[("dss", u) for u in range(4)])

            def d_final_u_stages(u):
                return [
                    lambda: sc.op("dve", lambda e: e.tensor_scalar(out=accS[:, u, 0:128], in0=accS[:, u, 0:128], scalar1=sm[:, u:u + 1], scalar2=None, op0=ALU.mult),
                                  reads=["dsm"] + ak(u), writes=ak(u)),
                    lambda: sc.op("dve", lambda e: e.scalar_tensor_tensor(out=accS[:, u, 0:128], in0=accS[:, 4 + u, 0:128], scalar=sm[:, 4 + u:5 + u],
                                                                          in1=accS[:, u, 0:128], op0=ALU.mult, op1=ALU.add),
                                  reads=["dsm"] + ak(u, 4 + u), writes=ak(u)),
                    lambda: sc.op("dve", lambda e: e.scalar_tensor_tensor(out=accS[:, 4 + u, 0:128], in0=accS[:, u, 0:128], scalar=1.0, in1=accS[:, u, 0:128],
                                                                          op0=ALU.mult, op1=ALU.mult, accum_out=sm[:, 8 + u:9 + u]),
                                  reads=ak(u), writes=ak(4 + u) + [("dss", u)]),
                ]

            def d_final_u(u0):
                sa, sb = d_final_u_stages(u0), d_final_u_stages(u0 + 1)
                for fa, fb in zip(sa, sb):
                    fa()
                    fb()

            def d_final_b(h, r):
                sc.op("act", lambda e: e.activation(out=sm[:, 12:16], in_=sm[:, 8:12], func=AF.Ln, bias=1e-5, scale=1.0 / 128),
                      reads=[("dss", u) for u in range(4)], writes=["drs"])
                sc.op("act", lambda e: e.activation(out=sm[:, 12:16], in_=sm[:, 12:16], func=AF.Exp, scale=-0.5), reads=["drs"], writes=["drs"])
                for u in range(4):
                    sc.op("dve", lambda e, u=u: e.scalar_tensor_tensor(out=d_y[:, u, :], in0=accS[:, u, 0:128], scalar=sm[:, 12 + u:13 + u], in1=wd_t,
                                                                       op0=ALU.mult, op1=ALU.mult),
                          reads=ak(u) + ["drs", "wd"], writes=[("dy", u)])

            def d_final_pe(h, r):
                for u in range(4):
                    sc.op("pe", lambda e, u=u: e.transpose(out=PSb[:, 7, u * 128:(u + 1) * 128], in_=d_y[:, u, :], identity=identb),
                          reads=[("dy", u), "identb"], writes=psk(7))
                sc.op("dve", lambda e, h=h, r=r: e.tensor_copy(out=XT[:, h, r * 512:(r + 1) * 512], in_=PSb[:, 7, 0:512]),
                      reads=psk(7), writes=[("XT", 4 * r + i) for i in range(4)])

            pend = []
            pend_b = []
            pend_u = []
            d_scores(0)
            d_scores(1)
            for i in range(len(steps)):
                h, r, j = steps[i]
                if j == 15:
                    d_av(i)
                    d_final(h, r)
                    if i + 2 < len(steps):
                        d_scores(i + 2)
                    d_final2(h, r)
                else:
                    if i + 2 < len(steps):
                        d_scores(i + 2)
                    d_av(i)
                if j == 15:
                    pend.append((h, r))
                    pend_b.append((h, r))
                    pend_u.extend([0, 2])
                    d_final_u(pend_u.pop(0))
                elif pend_u:
                    d_final_u(pend_u.pop(0))
                elif j == 4 and pend_b:
                    d_final_b(*pend_b.pop(0))
                elif j == 7 and pend:
                    d_final_pe(*pend.pop(0))
            while pend_u:
                d_final_u(pend_u.pop(0))
            while pend_b:
                d_final_b(*pend_b.pop(0))
            while pend:
                d_final_pe(*pend.pop(0))
            dump("mixT_d", XT, [128, 8, 2048], [("XT", t) for t in range(NT)])

            if stop == 'D' and l == nl - 1:
                raise _Stop()
            sc.barrier()
            R_D.reset()
            qf = R_D.get(8192, BF16, "p (c n) -> p c n", c=2)
            kf = R_D.get(8192, BF16, "p (c n) -> p c n", c=2)
            kd_f = R_D.get(8192, BF16, "p (t n) -> p t n", t=16)
            Sbf = R_D.get(16384, BF16, "p (d q t e) -> p d q t e", d=2, q=2, t=16)
            stm2 = [R_D.get(4096, F32, "p (d q n) -> p d q n", d=2, q=2) for _ in range(2)]
            _p0 = R_D.pos
            sp_ = [R_D.get(2048, F32) for _ in range(2)]
            _p1 = R_D.pos
            ebt = [R_D.get(2 * 2 * 129 * 4, F32, "p (d q n) -> p d q n", d=2, q=2) for _ in range(2)]
            _p2 = R_D.pos
            enbt = [R_D.get(2 * 2 * 128 * 4, F32, "p (d q n) -> p d q n", d=2, q=2) for _ in range(2)]
            erem = [R_D.get(2048, F32) for _ in range(2)]
            dS = [R_D.get(1024, F32) for _ in range(4)]
            _pend = R_D.pos
            Am = [R_X.get(4 * 2 * 128 * 2, BF16, "p (h d n) -> p h d n", h=4, d=2) for _ in range(2)]
            R_D.pos = _p2
            g_y = [R_D.get(1024, BF16) for _ in range(2)]
            g_junk = R_D.get(512, F32)
            R_D.pos = _pend
            qb, kb, kd_b = gqT, gkT, gk_tok
            maskf = Uf[:, 0:128]
            maskb = Usf

            def tl(t):
                return slice(t * 128, (t + 1) * 128)

            def prep_A(t):
                b = t % 2
                sp = sp_[b]
                zb = 0 if b == 0 else 7
                sc.op("pe", lambda e: e.matmul(PS[:, zb, :], lhsT=G33[0:33, tl(t)], rhs=Wg[0:33, :], start=True, stop=True),
                      reads=["G33", "Wg"], writes=psk(zb))
                sc.op("act", lambda e: e.activation(out=sp, in_=PS[:, zb, :], func=AF.Exp, scale=-1.0), reads=psk(zb), writes=[("sp", b)])
                sc.op("act", lambda e: e.activation(out=sp, in_=sp, func=AF.Ln, bias=1.0, scale=1.0), reads=[("sp", b)], writes=[("sp", b)])

            prep_A(0)
            for t in range(NT):
                b = t % 2
                sp = sp_[b]
                if t + 1 < NT:
                    prep_A(t + 1)
                sc.op("pe", lambda e, sp=sp: e.matmul(PS[:, 1, 0:256], lhsT=Usf, rhs=sp[:, 0:256], start=True, stop=True), reads=[("sp", b), "Usf", "Usb"], writes=psk(1))
                sc.op("pe", lambda e, sp=sp: e.matmul(PS[:, 1, 256:512], lhsT=Usb, rhs=sp[:, 256:512], start=True, stop=True), reads=[("sp", b), "Usf", "Usb"], writes=psk(1))
                sc.op("act", lambda e, b=b: e.activation(out=erem[b], in_=PS[:, 1, :], func=AF.Exp, scale=-1.0 / 16), reads=psk(1), writes=[("erem", b)])
                sc.op("dve", lambda e, t=t, b=b: e.tensor_tensor(out=kd_f[:, t, :], in0=gk_tok[:, t, :], in1=erem[b][:, 0:256], op=ALU.mult),
                      reads=[("gk_tok", t), ("erem", b)], writes=[("kd_f", t)])
                sc.op("dve", lambda e, t=t, b=b: e.tensor_tensor(out=kd_b[:, t, :], in0=gk_tok[:, t, :], in1=erem[b][:, 256:512], op=ALU.mult),
                      reads=[("gk_tok", t), ("erem", b), ("kd_f", t)], writes=[("gk_tok", t)])
                for d in range(2):
                    U = Uf if d == 0 else Ub
                    for q in range(2):
                        sc.op("pe", lambda e, sp=sp, d=d, q=q, U=U: e.matmul(PS[:, 2 + d, q * 160:q * 160 + 129],
                                                                            lhsT=sp[:, d * 256 + q * 128:d * 256 + (q + 1) * 128], rhs=U, start=True, stop=True),
                              reads=[("sp", b), "Uf", "Ub"], writes=psk(2 + d))
                src4 = PS[:, 2:4, 0:320].rearrange("p a (q n) -> p a q n", q=2)
                sc.op("act", lambda e, b=b, src4=src4: e.activation(out=ebt[b], in_=src4[:, :, :, 0:129], func=AF.Exp, scale=-1.0 / 16),
                      reads=psk(2) + psk(3), writes=[("eb", b, 0), ("eb", b, 1)])
                sc.op("act", lambda e, b=b, src4=src4: e.activation(out=enbt[b], in_=src4[:, :, :, 0:128], func=AF.Exp, scale=1.0 / 16),
                      reads=psk(2) + psk(3), writes=[("enb", b, 0), ("enb", b, 1)])
                sc.op("dve", lambda e, t=t, b=b: e.tensor_tensor(out=qf[:, :, tl(t)], in0=gqT[:, :, tl(t)], in1=ebt[b][:, 0, :, 0:128], op=ALU.mult),
                      reads=[("gqT", t), ("eb", b, 0)], writes=[("qf", t)])
                sc.op("dve", lambda e, t=t, b=b: e.tensor_tensor(out=kf[:, :, tl(t)], in0=gkT[:, :, tl(t)], in1=enbt[b][:, 0, :, :], op=ALU.mult),
                      reads=[("gkT", t), ("enb", b, 0)], writes=[("kf", t)])
                sc.op("dve", lambda e, t=t, b=b: e.tensor_tensor(out=qb[:, :, tl(t)], in0=gqT[:, :, tl(t)], in1=ebt[b][:, 1, :, 0:128], op=ALU.mult),
                      reads=[("gqT", t), ("eb", b, 1), ("qf", t)], writes=[("gqT", t)])
                sc.op("dve", lambda e, t=t, b=b: e.tensor_tensor(out=kb[:, :, tl(t)], in0=gkT[:, :, tl(t)], in1=enbt[b][:, 1, :, :], op=ALU.mult),
                      reads=[("gkT", t), ("enb", b, 1), ("kf", t)], writes=[("gkT", t)])
                sc.op("dve", lambda e, t=t, b=b: e.tensor_copy(out=decs[:, :, :, t:t + 1], in_=ebt[b][:, :, :, 128:129]),
                      reads=[("eb", b, 0), ("eb", b, 1)], writes=[("decs", t)])
            dump("qf", qf, [128, 2, 2048], [("qf", t) for t in range(NT)])
            dump("kd_f", kd_f, [128, 16, 256], [("kd_f", t) for t in range(NT)])
            dump("decs", decs, [128, 2, 2, 16], [("decs", t) for t in range(NT)])

            if stop == 'G1' and l == nl - 1:
                raise _Stop()
            sc.op("dve", lambda e: e.memset(stm2[0], 0.0), writes=[("stm", 0, d, q) for d in range(2) for q in range(2)])
            chains = [(d, q) for d in range(2) for q in range(2)]
            par = {c: 0 for c in chains}
            for i in range(NT):
                todo = []
                for ci, (d, q) in enumerate(chains):
                    t = i if d == 0 else NT - 1 - i
                    cur = par[(d, q)]
                    if i > 0:
                        sc.op("act", lambda e, d=d, q=q, t=t, cur=cur: e.activation(out=Sbf[0:64, d, q, t, :], in_=stm2[cur][0:64, d, q, 0:128], func=AF.Copy),
                              reads=[("stm", cur, d, q)], writes=[("Sbf", d, q, t, 0)])
                        sc.op("dve", lambda e, d=d, q=q, t=t, cur=cur: e.tensor_copy(out=Sbf[64:128, d, q, t, :], in_=stm2[cur][64:128, d, q, 128:256]),
                              reads=[("stm", cur, d, q)], writes=[("Sbf", d, q, t, 1)])
                    if i == NT - 1:
                        continue
                    kd = kd_f if d == 0 else kd_b
                    kkey = "kd_f" if d == 0 else "gk_tok"
                    pslot = ci % 2
                    pk = psk(4 + pslot)
                    sc.op("pe", lambda e, kd=kd, t=t, q=q, pslot=pslot: e.matmul(PS[:, 4 + pslot, 0:256], lhsT=kd[:, t, q * 128:(q + 1) * 128],
                                                                                rhs=gv[:, t, q * 256:(q + 1) * 256], start=True, stop=True),
                          reads=[(kkey, t), ("gv", t)], writes=pk)
                    if os.environ.get("GSKIP") != "evac":
                        sc.op("act", lambda e, ci=ci, pslot=pslot: e.activation(out=dS[ci], in_=PS[:, 4 + pslot, 0:256], func=AF.Copy),
                              reads=pk, writes=[("dS", ci)])
                    todo.append((ci, d, q, t, cur))
                for (ci, d, q, t, cur) in todo:
                    if os.environ.get("GSKIP") == "upd":
                        par[(d, q)] = 1 - cur
                        continue
                    sc.op("dve", lambda e, ci=ci, d=d, q=q, t=t, cur=cur: e.scalar_tensor_tensor(
                        out=stm2[1 - cur][:, d, q, :], in0=stm2[cur][:, d, q, :], scalar=decs[:, d, q, t:t + 1], in1=dS[ci],
                        op0=ALU.mult, op1=ALU.add), reads=[("stm", cur, d, q), ("decs", t), ("dS", ci)], writes=[("stm", 1 - cur, d, q)])
                    par[(d, q)] = 1 - cur
            dump("Sbf", Sbf, [128, 2, 2, 16, 128], [("Sbf", d, q, t, hh) for d in range(2) for q in range(2) for t in range(NT) for hh in range(2)])
            if stop == 'G2' and l == nl - 1:
                raise _Stop()

            def g_A(t):
                b = t % 2
                A = Am[b]
                for half in range(2):
                    sbank = (4 + half) if os.environ.get('GBANK') else (2 * b + half)
                    items = []
                    for sq in range(4):
                        h, d = half + 2 * (sq // 2), sq % 2
                        q = h // 2
                        base = (h % 2) * 64
                        kk = kf if d == 0 else kb
                        qq = qf if d == 0 else qb
                        kkey = ("kf", t) if d == 0 else ("gkT", t)
                        qkey = ("qf", t) if d == 0 else ("gqT", t)
                        sc.op("pe", lambda e, kk=kk, qq=qq, q=q, base=base, sbank=sbank, sq=sq: e.matmul(
                            PS[:, sbank, sq * 128:(sq + 1) * 128], lhsT=kk[base:base + 64, q, tl(t)], rhs=qq[base:base + 64, q, tl(t)], start=True, stop=True),
                            reads=[kkey, qkey], writes=psk(sbank))
                        items.append((sq, h, d))
                        if os.environ.get("GOLD"):
                            mk = maskf if d == 0 else maskb
                            sc.op("dve", lambda e, A=A, h=h, d=d, sbank=sbank, sq=sq, mk=mk: e.tensor_tensor(
                                out=A[:, h, d, :], in0=PS[:, sbank, sq * 128:(sq + 1) * 128], in1=mk, op=ALU.mult),
                                reads=psk(sbank) + ["Uf", "Usf"], writes=[("A", b, h, d)])
                    if os.environ.get("GOLD"):
                        continue
                    for (sq, h, d) in items:
                        mk = maskf if d == 0 else maskb
                        sc.op("dve", lambda e, A=A, h=h, d=d, sbank=sbank, sq=sq, mk=mk: e.tensor_tensor(
                            out=A[:, h, d, :], in0=PS[:, sbank, sq * 128:(sq + 1) * 128], in1=mk, op=ALU.mult),
                            reads=psk(sbank) + ["Uf", "Usf"], writes=[("A", b, h, d)])

            def g_B(t):
                b = t % 2
                A = Am[b]
                obank = (0 + b) if os.environ.get('GBANK') else (4 + b)
                for h in range(4):
                    q = h // 2
                    base = (h % 2) * 64
                    oh_ = PS[:, obank, h * 128:(h + 1) * 128]
                    ok = psk(obank)
                    inter_f = t > 0
                    inter_b = t < NT - 1
                    sc.op("pe", lambda e, A=A, h=h, oh_=oh_: e.matmul(oh_, lhsT=A[:, h, 0, :], rhs=gv[:, t, h * 128:(h + 1) * 128], start=True, stop=False),
                          reads=[("A", b, h, 0), ("gv", t)], writes=ok)
                    sc.op("pe", lambda e, A=A, h=h, oh_=oh_, fin=(not inter_f and not inter_b): e.matmul(
                        oh_, lhsT=A[:, h, 1, :], rhs=gv[:, t, h * 128:(h + 1) * 128], start=False, stop=fin),
                        reads=[("A", b, h, 1), ("gv", t)], writes=ok)
                    if inter_f:
                        sc.op("pe", lambda e, q=q, base=base, oh_=oh_, fin=(not inter_b): e.matmul(
                            oh_, lhsT=qf[base:base + 64, q, tl(t)], rhs=Sbf[base:base + 64, 0, q, t, :], start=False, stop=fin),
                            reads=[("qf", t), ("Sbf", 0, q, t, h % 2)], writes=ok)
                    if inter_b:
                        sc.op("pe", lambda e, q=q, base=base, oh_=oh_: e.matmul(
                            oh_, lhsT=qb[base:base + 64, q, tl(t)], rhs=Sbf[base:base + 64, 1, q, t, :], start=False, stop=True),
                            reads=[("gqT", t), ("Sbf", 1, q, t, h % 2)], writes=ok)

            def g_norm(t):
                b = t % 2
                sm = gsm[b]
                obank = (0 + b) if os.environ.get('GBANK') else (4 + b)
                okall = psk(obank)
                for h in range(4):
                    sc.op("act", lambda e, h=h, sm=sm: e.activation(out=g_junk, in_=PS[:, obank, h * 128:(h + 1) * 128], func=AF.Square, accum_out=sm[:, h:h + 1]),
                          reads=okall, writes=[("gss", b, h)] + ([("enb", 1, 0), ("enb", 1, 1)] if h == 0 else []))
                sc.op("act", lambda e, sm=sm: e.activation(out=sm[:, 4:8], in_=sm[:, 0:4], func=AF.Sqrt, bias=1e-5, scale=1.0 / 128),
                      reads=[("gss", b, h) for h in range(4)], writes=[("grs", b)])
                sc.op("dve", lambda e, sm=sm: e.reciprocal(out=sm[:, 4:8], in_=sm[:, 4:8]), reads=[("grs", b)], writes=[("grs", b)])
                for h in range(4):
                    sc.op("dve", lambda e, h=h, sm=sm, b=b: e.scalar_tensor_tensor(
                        out=g_y[b][:, h * 128:(h + 1) * 128], in0=PS[:, obank, h * 128:(h + 1) * 128], scalar=sm[:, 4 + h:5 + h],
                        in1=gr_s[:, t, h * 128:(h + 1) * 128], op0=ALU.mult, op1=ALU.mult),
                        reads=psk(obank) + [("grs", b), ("gr_s", t)], writes=[("gy", b, h)] + ([("enb", 0, 0), ("enb", 0, 1)] if h == 0 else []))

            def g_tr(t):
                b = t % 2
                tk = psk(6 + b)
                for h in range(4):
                    sc.op("pe", lambda e, h=h, b=b: e.transpose(out=PSb[:, 6 + b, h * 128:(h + 1) * 128], in_=g_y[b][:, h * 128:(h + 1) * 128], identity=identb),
                          reads=[("gy", b, h), "identb"], writes=tk)
                sc.op("act", lambda e, b=b: e.activation(out=XT[:, 4:8, tl(t)], in_=PSb[:, 6 + b, 0:512].rearrange("p (c n) -> p c n", c=4), func=AF.Copy),
                      reads=tk, writes=[("XT", t)])

            g_A(0)
            for t in range(NT):
                if t + 1 < NT:
                    g_A(t + 1)
                if os.environ.get("GSKIP") == "B":
                    continue
                g_B(t)
                if os.environ.get("GSKIP") == "norm":
                    continue
                g_norm(t)
                if os.environ.get("GSKIP") == "tr":
                    continue
                if t > 0:
                    g_tr(t - 1)
            if not os.environ.get("GSKIP"):
                g_tr(NT - 1)
            dump("mixT", XT, [128, 8, 2048], [("XT", t) for t in range(NT)])

            if stop == 'G' and l == nl - 1:
                raise _Stop()
            sc.barrier()
            load_ln_params(ln1g_d[l], ln1b_d[l])
            R_D.reset()
            W1B = [R_D.get(8192, BF16, "p (c n) -> p c n", c=8) for _ in range(2)]
            W2B = [R_D.get(8192, BF16, "p (c n) -> p c n", c=4) for _ in range(2)]
            hT = R_D.get(16384, BF16, "p (c n) -> p c n", c=4)
            relu_t = [R_D.get(2048, F32) for _ in range(2)]

            def load_ffn_block(fb, buf):
                s1 = w1_d[l, :, fb * 512:(fb + 1) * 512].rearrange("(c p) n -> p c n", p=128)
                s2 = w2_d[l, fb * 512:(fb + 1) * 512, :].rearrange("(c p) n -> p c n", p=128)
                for hf in range(2):
                    sc.dma("pool", lambda e, hf=hf: e.dma_start(out=W1B[buf][:, hf * 4:(hf + 1) * 4, :], in_=s1[:, hf * 4:(hf + 1) * 4, :]),
                           writes=[("W1B", buf, hf)])
                for hf in range(2):
                    sc.dma("pool", lambda e, hf=hf: e.dma_start(out=W2B[buf][:, hf * 2:(hf + 1) * 2, :], in_=s2[:, hf * 2:(hf + 1) * 2, :]),
                           writes=[("W2B", buf, hf)])

            load_ffn_block(0, 0)
            load_ffn_block(1, 1)
            for t in range(NT):
                sc.dma("sp", lambda e, t=t: e.dma_start(out=X[:, t, :], in_=xs_d[t * 128:(t + 1) * 128, :]), reads=[("xsd", t)], writes=[("X", t)])
            def o_mm(t):
                yb = (t % 3) * 2
                for hf in range(2):
                    for c in range(8):
                        sc.op("pe", lambda e, c=c, hf=hf, t=t, yb=yb: e.matmul(PS[:, yb + hf, :], lhsT=XT[:, c, tl(t)], rhs=WO[:, c, hf * 512:(hf + 1) * 512],
                                                                              start=(c == 0), stop=(c == 7)),
                              reads=[("XT", t), ("RW", 0, 0), ("RW", 0, 1), ("RW", 1, 0), ("RW", 1, 1)], writes=psk(yb + hf))

            def o_ln_stages(t):
                yb = (t % 3) * 2
                resid = lambda: sc.op("dve", lambda e: e.scalar_tensor_tensor(out=X[:, t, :], in0=X[:, t, :], scalar=ALPHA,
                                                                              in1=PS[:, yb:yb + 2, :].rearrange("p a n -> p (a n)"), op0=ALU.mult, op1=ALU.add),
                                      reads=[("X", t)] + psk(yb) + psk(yb + 1), writes=[("X", t)])
                return [resid] + ln_a_stages(t)

            def o_ln_group(ts):
                lists = [o_ln_stages(t) for t in ts]
                for k in range(len(lists[0])):
                    for lst in lists:
                        lst[k]()

            for t0 in range(3):
                o_mm(t0)
            o_ln_group([0, 1])
            o_ln_group([2])
            for t in range(0, NT, 2):
                nxt = [u for u in (t + 3, t + 4) if u < NT]
                for u in nxt:
                    o_mm(u)
                ln_b1(t, None)
                ln_b1(t + 1, None)
                if nxt:
                    o_ln_group(nxt)
                ln_b2(t, None)
                ln_b2(t + 1, None)
            dump("x1T", XT, [128, 8, 2048], [("XT", t) for t in range(NT)])

            if stop == 'O' and l == nl - 1:
                raise _Stop()
            if not last:
                load_win_block(l + 1, 0, 0)
                load_win_block(l + 1, 1, 1)
            hrr = [0]
            for fb in range(8):
                buf = fb % 2
                for r in range(4):
                    for fc in range(4):
                        bank = 4 + hrr[0] % 3
                        rb = hrr[0] % 2
                        hrr[0] += 1
                        for c in range(8):
                            sc.op("pe", lambda e, c=c, fc=fc, r=r, bank=bank, buf=buf: e.matmul(
                                PS[:, bank, :], lhsT=W1B[buf][:, c, fc * 128:(fc + 1) * 128], rhs=XT[:, c, r * 512:(r + 1) * 512],
                                start=(c == 0), stop=(c == 7)), reads=[("W1B", buf, 0), ("W1B", buf, 1)] + xt_all[r * 4:(r + 1) * 4], writes=psk(bank))
                        fcol = fb * 4 + fc
                        sc.op("act", lambda e, bank=bank, rb=rb, fcol=fcol: e.activation(out=relu_t[rb], in_=PS[:, bank, :], func=AF.Relu,
                                                                                         bias=b1c[:, fcol:fcol + 1], scale=1.0),
                              reads=psk(bank) + ["b1c"], writes=[("relu", rb)])
                        sc.op("dve", lambda e, rb=rb, fc=fc, r=r: e.tensor_tensor(out=hT[:, fc, r * 512:(r + 1) * 512], in0=relu_t[rb], in1=relu_t[rb], op=ALU.mult),
                              reads=[("relu", rb)], writes=[("hT", fc, r)])
                for t in range(NT):
                    yb = (t % 2) * 2
                    for hf in range(2):
                        for fc in range(4):
                            sc.op("pe", lambda e, fc=fc, hf=hf, t=t, yb=yb, buf=buf: e.matmul(
                                PS[:, yb + hf, :], lhsT=hT[:, fc, tl(t)], rhs=W2B[buf][:, fc, hf * 512:(hf + 1) * 512],
                                start=(fc == 0), stop=(fc == 3)), reads=[("hT", fc, t // 4), ("W2B", buf, 0), ("W2B", buf, 1)], writes=psk(yb + hf))
                    if fb == 0:
                        sc.op("dve", lambda e, t=t, yb=yb: e.scalar_tensor_tensor(out=X[:, t, :], in0=X[:, t, :], scalar=ALPHA,
                                                                                  in1=PS[:, yb:yb + 2, :].rearrange("p a n -> p (a n)"), op0=ALU.mult, op1=ALU.add),
                              reads=[("X", t)] + psk(yb) + psk(yb + 1), writes=[("X", t)])
                        sc.op("pool", lambda e, t=t: e.tensor_tensor(out=X[:, t, :], in0=X[:, t, :], in1=b2t, op=ALU.add), reads=[("X", t), "b2t"], writes=[("X", t)])
                    else:
                        sc.op("dve", lambda e, t=t, yb=yb: e.tensor_tensor(out=X[:, t, :], in0=X[:, t, :], in1=PS[:, yb:yb + 2, :].rearrange("p a n -> p (a n)"), op=ALU.add),
                              reads=[("X", t)] + psk(yb) + psk(yb + 1), writes=[("X", t)])
                if fb + 2 < 8:
                    load_ffn_block(fb + 2, buf)
            if stop == 'F' and l == nl - 1:
                raise _Stop()
            load_ln_params(ln2g_d[l], ln2b_d[l])
            ln_all(out_d if last else xs_d)
            sc.barrier()


        for _l in range(nl):
            do_layer(_l)
    except _Stop:
        pass
    out_dmas = [o for o in sc.ops if o.is_dma and o.dkey in [("xs", i) for i in range(4)]]
    fin = {}
    for o in out_dmas:
        fin[o.dkey] = o
    finals = list(fin.values()) + list(dbg_out.values())
    sc.emit(final_wait_ops=finals)
    es.close()
    return nc, sc


_CONST = None


def kernel(**inputs):
    global _CONST
    if _CONST is None:
        _CONST = _constants()
    nc, _ = build(2)
    x = np.ascontiguousarray(inputs["x"], dtype=np.float32)
    shared = {k: np.ascontiguousarray(v, dtype=np.float32) for k, v in inputs.items() if k != "x"}
    shared.update(_CONST)
    in_maps = []
    for b in range(8):
        m = dict(shared)
        m["x"] = x[b]
        in_maps.append(m)
    res = run_bass_kernel_spmd(nc, in_maps, core_ids=list(range(8)))
    return np.stack([r["out"] for r in res.results], axis=0).astype(np.float32)
```

```python
import math
import os
from contextlib import ExitStack

import numpy as np
import concourse.bass as bass
import concourse.mybir as mybir
from concourse.bass_utils import run_bass_kernel_spmd

F32 = mybir.dt.float32
BF16 = mybir.dt.bfloat16
AF = mybir.ActivationFunctionType
ALU = mybir.AluOpType

S = 2048
D = 1024
DIN = 3104
DFF = 4096
NT = 16
ALPHA = (2.0 * 2) ** 0.25
ENGS = ("pe", "act", "dve", "pool", "sp")
EPOCH = 30000


class _Res:
    __slots__ = ("last_w", "readers")

    def __init__(self):
        self.last_w = None
        self.readers = []


class _Op:
    __slots__ = ("eng", "fn", "deps", "signal", "tok", "is_dma", "dkey")

    def __init__(self, eng, fn, is_dma, dkey):
        self.eng = eng
        self.fn = fn
        self.deps = []
        self.signal = False
        self.tok = None
        self.is_dma = is_dma
        self.dkey = dkey


class Sched:
    def __init__(self, nc):
        self.nc = nc
        self.ops = []
        self.res = {}
        self.pending = {e: [] for e in ENGS}

    def _r(self, key):
        x = self.res.get(key)
        if x is None:
            x = self.res[key] = _Res()
        return x

    def _add(self, op, reads, writes):
        deps = set()
        for k in reads:
            rs = self._r(k)
            if rs.last_w is not None:
                deps.add(rs.last_w)
        for k in writes:
            rs = self._r(k)
            if rs.last_w is not None:
                deps.add(rs.last_w)
            deps.update(rs.readers)
        for k in reads:
            self._r(k).readers.append(op)
        for k in writes:
            rs = self._r(k)
            rs.last_w = op
            rs.readers = []
        if self.pending[op.eng]:
            deps.update(self.pending[op.eng])
            self.pending[op.eng] = []
        deps.discard(op)
        op.deps = list(deps)
        self.ops.append(op)
        return op

    def op(self, eng, fn, reads=(), writes=()):
        return self._add(_Op(eng, fn, False, None), reads, writes)

    def dma(self, eng, fn, dkey=None, reads=(), writes=()):
        if dkey is None:
            dkey = ("w", writes[0])
        return self._add(_Op(eng, fn, True, dkey), reads, writes)

    def barrier(self):
        last = {}
        for o in self.ops:
            last[(o.eng, o.dkey) if o.is_dma else o.eng] = o
        b = list(last.values())
        self.pending = {e: list(b) for e in ENGS}

    def emit(self, final_wait_ops=()):
        nc = self.nc
        ops = self.ops
        for o in ops:
            for d in o.deps:
                if d.is_dma:
                    d.signal = True
                elif d.eng == "pe" and o.eng == "pe" and not o.is_dma:
                    continue
                else:
                    d.signal = True
        with ExitStack() as es:
            eng_sems = {e: [] for e in ENGS}
            cnt = {e: 0 for e in ENGS}
            dma_sems = {}
            dma_cnt = {}
            for o in ops:
                if o.is_dma:
                    if o.dkey not in dma_sems:
                        dma_sems[o.dkey] = es.enter_context(nc.semaphore("d%d" % len(dma_sems)))
                        dma_cnt[o.dkey] = 0
                    dma_cnt[o.dkey] += 16
                    o.tok = (dma_sems[o.dkey], dma_cnt[o.dkey])
                elif o.signal:
                    ep = cnt[o.eng] // EPOCH
                    if ep >= len(eng_sems[o.eng]):
                        eng_sems[o.eng].append(es.enter_context(nc.semaphore("e_%s_%d" % (o.eng, ep))))
                    cnt[o.eng] += 1
                    o.tok = (eng_sems[o.eng][ep], cnt[o.eng] - ep * EPOCH)
            per_eng = {e: [o for o in ops if o.eng == e] for e in ENGS}
            self.stats = {e: len(per_eng[e]) for e in ENGS}
            self.stats["sems"] = sum(len(v) for v in eng_sems.values()) + len(dma_sems)

            def run(e, eng):
                waited = {}
                for o in per_eng[e]:
                    need = {}
                    for d in o.deps:
                        if d.tok is None:
                            continue
                        if (not d.is_dma) and d.eng == "pe" and e == "pe" and not o.is_dma:
                            continue
                        s, v = d.tok
                        k = id(s)
                        if waited.get(k, 0) >= v:
                            continue
                        if k not in need or need[k][1] < v:
                            need[k] = (s, v)
                    for k, (s, v) in need.items():
                        eng.wait_ge(s, v)
                        waited[k] = v
                    ins = o.fn(eng)
                    if o.tok is not None:
                        ins.then_inc(o.tok[0], 16 if o.is_dma else 1)
                if e == "sp":
                    for o in final_wait_ops:
                        s, v = o.tok
                        eng.wait_ge(s, v)

            with nc.Block() as block:
                @block.sync
                def _(eng):
                    run("sp", eng)

                @block.tensor
                def _(eng):
                    run("pe", eng)

                @block.scalar
                def _(eng):
                    run("act", eng)

                @block.vector
                def _(eng):
                    run("dve", eng)

                @block.gpsimd
                def _(eng):
                    run("pool", eng)


def _t5_bucket(rel):
    nb = 16
    me = 8
    ret = np.where(rel > 0, nb, 0)
    n = np.abs(rel)
    large = me + (np.log(np.maximum(n, 1).astype(np.float32) / np.float32(me))
                  / np.float32(math.log(128 / me)) * np.float32(nb - me)).astype(np.int32)
    large = np.minimum(large, nb - 1)
    return ret + np.where(n < me, n, large)


MLEN = 1280


def _constants():
    c = {}
    c["c_ident"] = np.eye(128, dtype=np.float32)
    c["c_J"] = np.eye(128, dtype=np.float32)[::-1].copy()
    s = np.arange(128)[:, None]
    t = np.arange(128)[None, :]
    uf = np.zeros((128, 129), np.float32)
    uf[:, :128] = (s <= t)
    uf[:, 128] = 1.0
    ub = np.zeros((128, 129), np.float32)
    ub[:, :128] = (s >= t)
    ub[:, 128] = 1.0
    c["c_uf"] = uf
    c["c_ub"] = ub
    c["c_sf"] = (s > t).astype(np.float32)
    c["c_sb"] = (s < t).astype(np.float32)
    n = np.arange(MLEN)
    bk = _t5_bucket(639 - n)
    oh = np.zeros((32, MLEN), np.float32)
    oh[bk, n] = 1.0
    c["c_onehot"] = oh
    return c


class _Stop(Exception):
    pass


def build(nl=2, dbg=(), stop=None):
    nc = bass.Bass("TRN2", target_bir_lowering=False)

    def din(name, shape):
        return nc.dram_tensor(name, list(shape), F32, kind="ExternalInput").ap()

    x_d = din("x", [S, D])
    lnemb_g = din("ln_emb_g", [D])
    lnemb_b = din("ln_emb_b", [D])
    table_d = din("rel_bias_table", [32, 4])
    w_in_d = din("w_in", [2, D, DIN])
    lq1_d = din("lambda_q1", [2, 64])
    lk1_d = din("lambda_k1", [2, 64])
    lq2_d = din("lambda_q2", [2, 64])
    lk2_d = din("lambda_k2", [2, 64])
    dnw_d = din("diff_norm_w", [2, 128])
    gup_d = din("gla_gate_up", [2, 2, 16, 256])
    gbias_d = din("gla_gate_bias", [2, 2, 256])
    gnw_d = din("gla_norm_w", [2, 128])
    w_o_d = din("w_o", [2, D, D])
    ln1g_d = din("ln1_g", [2, D])
    ln1b_d = din("ln1_b", [2, D])
    w1_d = din("w_ffn1", [2, D, DFF])
    b1_d = din("b_ffn1", [2, DFF])
    w2_d = din("w_ffn2", [2, DFF, D])
    b2_d = din("b_ffn2", [2, D])
    ln2g_d = din("ln2_g", [2, D])
    ln2b_d = din("ln2_b", [2, D])
    c_ident = din("c_ident", [128, 128])
    c_J = din("c_J", [128, 128])
    c_uf = din("c_uf", [128, 129])
    c_ub = din("c_ub", [128, 129])
    c_sf = din("c_sf", [128, 128])
    c_sb = din("c_sb", [128, 128])
    c_onehot = din("c_onehot", [32, MLEN])
    out_d = nc.dram_tensor("out", [S, D], F32, kind="ExternalOutput").ap()
    xs_d = nc.dram_tensor("xs_scratch", [S, D], F32).ap()
    md_t = nc.dram_tensor("md_scratch", [4, MLEN], F32)
    eb_d = nc.dram_tensor("expb_scratch", [128, 4 * 1152], BF16).ap()
    md_d = md_t.ap()
    dbg_out = {}

    sc = Sched(nc)
    es = ExitStack()
    ARENA_BYTES = 207 * 1024
    arena = es.enter_context(nc.sbuf_tensor("arena", [128, ARENA_BYTES // 2], BF16))
    PSb = es.enter_context(nc.psum_tensor("ps", [128, 8, 1024], BF16))[:]
    PS = PSb.bitcast(F32)

    def view(off, nbytes, dt, pattern=None, **kw):
        assert off % 32 == 0, off
        a = arena[:, off // 2:(off + nbytes) // 2]
        if dt is F32:
            a = a.bitcast(F32)
        if pattern:
            a = a.rearrange(pattern, **kw)
        return a

    class Alloc:
        def __init__(self, base, size):
            self.base = base
            self.size = size
            self.pos = 0

        def reset(self):
            self.pos = 0

        def get(self, nbytes, dt, pattern=None, **kw):
            n = (nbytes + 31) // 32 * 32
            assert self.pos + n <= self.size, (self.pos, n, self.size)
            v = view(self.base + self.pos, nbytes, dt, pattern, **kw)
            self.pos += n
            return v

    R_XT = Alloc(0, 32768)
    R_X = Alloc(32768, 65536)
    R_D = Alloc(98304, 70656)
    R_W = Alloc(168960, 16384)
    R_C = Alloc(185344, ARENA_BYTES - 185344)

    XT = R_XT.get(32768, BF16, "p (c n) -> p c n", c=8)
    X = R_X.get(65536, F32, "p (t n) -> p t n", t=NT)
    WB = [R_W.get(8192, BF16, "p (c n) -> p c n", c=8) for _ in range(2)]
    R_W.reset()
    WO = R_W.get(16384, BF16, "p (c n) -> p c n", c=8)

    identb = R_C.get(256, BF16)
    Jb = R_C.get(256, BF16)
    Uf = R_C.get(516, F32)
    Ub = R_C.get(516, F32)
    Usf = R_C.get(512, F32)
    Usb = R_C.get(512, F32)
    gt = R_C.get(4096, F32)
    bt = R_C.get(4096, F32)
    b2t = R_C.get(4096, F32)
    wd_t = R_C.get(512, F32)
    wg_t = R_C.get(512, F32)
    b1c = R_C.get(128, F32)
    cb = R_C.get(32, F32, "p (s h) -> p s h", s=2)
    lamv = R_C.get(4 * 64 * 4, F32, "p (a n) -> p a n", a=4)
    lamp = R_C.get(2 * 64 * 4, F32, "p (a n) -> p a n", a=2)
    lams = R_C.get(32, F32)
    Wg = R_C.get(1024, BF16)
    st_ = [R_C.get(48, F32) for _ in range(2)]
    mv_ = [R_C.get(8, F32) for _ in range(2)]
    rs_ = [R_C.get(4, F32) for _ in range(2)]
    xb_ = [R_C.get(2048, BF16) for _ in range(2)]
    dsm = [R_C.get(64, F32) for _ in range(2)]
    gsm = [R_C.get(64, F32) for _ in range(2)]
    decs = R_C.get(2 * 2 * 16 * 4, F32, "p (d q t) -> p d q t", d=2, q=2)

    def psk(b):
        return [("ps", b, q) for q in range(4)]

    def bc_mid(ap2, n):
        a = ap2.ap
        return bass.AP(ap2.tensor, ap2.offset, [list(a[0]), [0, n], list(a[1])])

    def bc_last(ap2, n):
        a = ap2.ap
        return bass.AP(ap2.tensor, ap2.offset, [list(a[0]), list(a[1]), [0, n]])

    cur_layer = [-1]

    def dump(name, ap, shape, reads):
        nm = "%s@%d" % (name, cur_layer[0])
        if nm in dbg:
            name = nm
        elif name not in dbg or (cur_layer[0] >= 0 and cur_layer[0] != nl - 1):
            return
        t = nc.dram_tensor("dbg_" + name.replace("@", "_"), list(shape), ap.dtype, kind="ExternalOutput").ap()
        dbg_out[name] = sc.dma("sp", lambda e: e.dma_start(out=t, in_=ap), "dbg", reads=reads)

    try:
        sc.dma("pool", lambda e: e.dma_start(out=identb, in_=c_ident), writes=["identb"])
        sc.dma("pool", lambda e: e.dma_start(out=Jb, in_=c_J), writes=["Jb"])
        sc.dma("sp", lambda e: e.dma_start(out=Uf, in_=c_uf), writes=["Uf"])
        sc.dma("sp", lambda e: e.dma_start(out=Ub, in_=c_ub), writes=["Ub"])
        sc.dma("sp", lambda e: e.dma_start(out=Usf, in_=c_sf), writes=["Usf"])
        sc.dma("sp", lambda e: e.dma_start(out=Usb, in_=c_sb), writes=["Usb"])
        for si, row in enumerate((15, 31)):
            sc.dma("sp", lambda e, si=si, row=row: e.dma_start(out=cb[:, si, :], in_=table_d[row, :].partition_broadcast(128)),
                   writes=[("cb", si)])

        R_D.reset()
        tb = R_D.get(16, F32)
        oh = R_D.get(MLEN * 4, F32)
        msb = R_D.get(MLEN * 4, F32)
        sc.dma("sp", lambda e: e.dma_start(out=tb[0:32, :], in_=table_d), writes=["tb"])
        sc.dma("sp", lambda e: e.dma_start(out=oh[0:32, :], in_=c_onehot), writes=["oh"])
        sc.dma("sp", lambda e: e.dma_start(out=gt, in_=lnemb_g.partition_broadcast(128)), writes=["gt"])
        sc.dma("sp", lambda e: e.dma_start(out=bt, in_=lnemb_b.partition_broadcast(128)), writes=["bt"])
        for t in range(NT):
            sc.dma("sp", lambda e, t=t: e.dma_start(out=X[:, t, :], in_=x_d[t * 128:(t + 1) * 128, :]), writes=[("X", t)])
        for ci, (c0, cn) in enumerate(((0, 512), (512, 512), (1024, 256))):
            sc.op("pe", lambda e, ci=ci, c0=c0, cn=cn: e.matmul(PS[0:4, ci, 0:cn], lhsT=tb[0:32, :], rhs=oh[0:32, c0:c0 + cn], start=True, stop=True),
                  reads=["tb", "oh"], writes=psk(ci))
            sc.op("dve", lambda e, ci=ci, c0=c0, cn=cn: e.tensor_copy(out=msb[0:4, c0:c0 + cn], in_=PS[0:4, ci, 0:cn]),
                  reads=psk(ci), writes=["msb"])
        sc.dma("sp", lambda e: e.dma_start(out=md_d, in_=msb[0:4, :]), reads=["msb"], writes=["md"])
        R_D.reset()
        R_D.get(16384, BF16); R_D.get(16384, BF16); R_D.get(16 * 4 * 129 * 2, BF16)
        expB0 = R_D.get(4 * 1152 * 2, BF16, "p (h n) -> p h n", h=4)
        R_D.get(4096, BF16); R_D.get(1024, BF16)
        tmp_revs = [view(R_D.base + 16384 + i * 2304, 2304, BF16) for i in range(4)]
        for h in range(4):
            src = bass.AP(md_t, h * MLEN, [[1, 128], [1, 1152]])
            sc.dma("pool", lambda e, src=src, h=h: e.dma_start(out=tmp_revs[h], in_=src), reads=["md"], writes=[("tmp_rev", h)])
        for h in range(4):
            for ci, (c0, cn) in enumerate(((0, 512), (512, 512), (1024, 128))):
                sc.op("pe", lambda e, ci=ci, c0=c0, cn=cn, h=h: e.matmul(PS[:, ci, 0:cn], lhsT=Jb, rhs=tmp_revs[h][:, c0:c0 + cn], start=True, stop=True),
                      reads=["Jb", ("tmp_rev", h)], writes=psk(ci))
                sc.op("act", lambda e, h=h, ci=ci, c0=c0, cn=cn: e.activation(out=expB0[:, h, c0:c0 + cn], in_=PS[:, ci, 0:cn], func=AF.Exp),
                      reads=psk(ci), writes=[("expB", h)])
        sc.dma("sp", lambda e: e.dma_start(out=eb_d, in_=expB0.rearrange("p h n -> p (h n)")), reads=[("expB", h) for h in range(4)], writes=["eb_d"])

        def ln_a_stages(t):
            Xt = X[:, t, :]
            kx = ("X", t)
            b = t % 2
            st, mv, rs = st_[b], mv_[b], rs_[b]
            return [
                lambda: sc.op("dve", lambda e: e.bn_stats(out=st[:, 0:6], in_=Xt[:, 0:512]), reads=[kx], writes=[("st", b, 0)]),
                lambda: sc.op("dve", lambda e: e.bn_stats(out=st[:, 6:12], in_=Xt[:, 512:1024]), reads=[kx], writes=[("st", b, 1)]),
                lambda: sc.op("dve", lambda e: e.bn_aggr(out=mv, in_=st), reads=[("st", b, 0), ("st", b, 1)], writes=[("mv", b)]),
                lambda: sc.op("act", lambda e: e.activation(out=rs, in_=mv[:, 1:2], func=AF.Ln, bias=1e-5, scale=1.0), reads=[("mv", b)], writes=[("rs", b)]),
                lambda: sc.op("act", lambda e: e.activation(out=rs, in_=rs, func=AF.Exp, scale=-0.5), reads=[("rs", b)], writes=[("rs", b)]),
                lambda: sc.op("dve", lambda e: e.tensor_scalar(out=Xt, in0=Xt, scalar1=mv[:, 0:1], scalar2=rs, op0=ALU.subtract, op1=ALU.mult),
                              reads=[kx, ("mv", b), ("rs", b)], writes=[kx]),
                lambda: sc.op("dve", lambda e: e.tensor_tensor(out=Xt, in0=Xt, in1=gt, op=ALU.mult), reads=[kx, "gt"], writes=[kx]),
                lambda: sc.op("pool", lambda e: e.tensor_tensor(out=Xt, in0=Xt, in1=bt, op=ALU.add), reads=[kx, "bt"], writes=[kx]),
            ]

        def ln_a(t):
            for f in ln_a_stages(t):
                f()

        def ln_a_pair(t0, t1):
            sa, sb = ln_a_stages(t0), ln_a_stages(t1)
            for fa, fb in zip(sa, sb):
                fa()
                fb()

        def ln_b1(t, spill_to):
            Xt = X[:, t, :]
            kx = ("X", t)
            b = t % 2
            xb = xb_[b]
            if spill_to is not None:
                sc.dma("sp", lambda e: e.dma_start(out=spill_to[t * 128:(t + 1) * 128, :], in_=Xt), ("xs", t % 4), reads=[kx], writes=[("xsd", t)])
            if spill_to is out_d:
                return
            sc.op("act", lambda e: e.activation(out=xb, in_=Xt, func=AF.Copy), reads=[kx], writes=[("xb", b)])

        def ln_b2(t, spill_to):
            if spill_to is out_d:
                return
            b = t % 2
            xb = xb_[b]
            bank = 6 + b
            for c in range(8):
                sc.op("pe", lambda e, c=c: e.transpose(out=PSb[:, bank, c * 128:(c + 1) * 128], in_=xb[:, c * 128:(c + 1) * 128], identity=identb),
                      reads=[("xb", b), "identb"], writes=psk(bank))
            sc.op("act", lambda e: e.activation(out=XT[:, :, t * 128:(t + 1) * 128], in_=PSb[:, bank, :].rearrange("p (c n) -> p c n", c=8), func=AF.Copy),
                  reads=psk(bank), writes=[("XT", t)])

        def ln_all(spill_to):
            ln_a_pair(0, 1)
            for t in range(0, NT, 2):
                if t + 2 < NT:
                    ln_a_pair(t + 2, t + 3)
                ln_b1(t, spill_to)
                ln_b1(t + 1, spill_to)
                ln_b2(t, spill_to)
                ln_b2(t + 1, spill_to)

        def load_ln_params(g_ap, b_ap):
            sc.dma("sp", lambda e: e.dma_start(out=gt, in_=g_ap.partition_broadcast(128)), writes=["gt"])
            sc.dma("sp", lambda e: e.dma_start(out=bt, in_=b_ap.partition_broadcast(128)), writes=["bt"])

        def load_win_block(l, blk, buf):
            c0 = blk * 512
            ncol = min(512, DIN - c0)
            src = w_in_d[l, :, c0:c0 + ncol].rearrange("(c p) n -> p c n", p=128)
            for hf in range(2):
                sc.dma("pool", lambda e, hf=hf: e.dma_start(out=WB[buf][:, hf * 4:(hf + 1) * 4, 0:ncol], in_=src[:, hf * 4:(hf + 1) * 4, :]),
                       writes=[("RW", buf, hf)])

        if stop == 'init':
            raise _Stop()
        load_win_block(0, 0, 0)
        load_win_block(0, 1, 1)
        ln_all(xs_d)
        dump("h0", X, [128, NT, 1024], [("X", t) for t in range(NT)])

        if stop == 'emb':
            raise _Stop()
        evac_rr = [0]

        def evac(out, in_, reads, writes, scale=None):
            evac_rr[0] ^= 1
            if evac_rr[0]:
                if scale is None:
                    sc.op("act", lambda e: e.activation(out=out, in_=in_, func=AF.Copy), reads=reads, writes=writes)
                else:
                    sc.op("act", lambda e: e.mul(out=out, in_=in_, mul=scale), reads=reads, writes=writes)
            else:
                if scale is None:
                    sc.op("dve", lambda e: e.tensor_copy(out=out, in_=in_), reads=reads, writes=writes)
                else:
                    sc.op("dve", lambda e: e.tensor_scalar(out=out, in0=in_, scalar1=scale, scalar2=None, op0=ALU.mult), reads=reads, writes=writes)

        def do_layer(l):
            lam_init = 0.8 - 0.6 * math.exp(-0.3 * l)
            cur_layer[0] = l
            if l == 0:
                sc.barrier()
            last = (l == nl - 1)
            R_D.reset()
            QT = R_D.get(16384, BF16, "p (h n) -> p h n", h=4)
            KT = R_D.get(16384, BF16, "p (h n) -> p h n", h=4)
            V = R_D.get(16 * 4 * 129 * 2, BF16, "p (t h e) -> p t h e", t=16, h=4)
            expB = R_D.get(4 * 1152 * 2, BF16, "p (h n) -> p h n", h=4)
            Eb = R_D.get(4096, BF16, "p (b m n) -> p b m n", b=2, m=2)
            d_y = R_D.get(4 * 128 * 2, BF16, "p (u n) -> p u n", u=4)
            _pu = R_D.pos
            silu_t = [R_D.get(2048, F32) for _ in range(2)]
            R_D.pos = _pu
            accS = R_D.get(8 * 129 * 4, F32, "p (a n) -> p a n", a=8)
            R_D.pos = _pu + 4608
            tmp_rev = R_D.get(1152 * 2, BF16)
            R_X.reset()
            gqT = R_X.get(8192, BF16, "p (c n) -> p c n", c=2)
            gkT = R_X.get(8192, BF16, "p (c n) -> p c n", c=2)
            gk_tok = R_X.get(8192, BF16, "p (t n) -> p t n", t=16)
            gv = R_X.get(16384, BF16, "p (t n) -> p t n", t=16)
            gr_s = R_X.get(16384, BF16, "p (t n) -> p t n", t=16)
            G33 = R_X.get(4096, BF16)

            for i, ap in enumerate((lq1_d, lk1_d, lq2_d, lk2_d)):
                sc.dma("sp", lambda e, i=i, ap=ap: e.dma_start(out=lamv[:, i, :], in_=ap[l, :].partition_broadcast(128)), writes=[("lamv", i)])
            sc.op("dve", lambda e: e.tensor_tensor(out=lamp[:, 0, :], in0=lamv[:, 0, :], in1=lamv[:, 1, :], op=ALU.mult), reads=[("lamv", 0), ("lamv", 1)], writes=["lamp"])
            sc.op("dve", lambda e: e.tensor_tensor(out=lamp[:, 1, :], in0=lamv[:, 2, :], in1=lamv[:, 3, :], op=ALU.mult), reads=[("lamv", 2), ("lamv", 3)], writes=["lamp"])
            sc.op("dve", lambda e: e.reduce_sum(out=lams[:, 0:2], in_=lamp, axis=mybir.AxisListType.X), reads=["lamp"], writes=["lams"])
            sc.op("act", lambda e: e.activation(out=lams[:, 0:2], in_=lams[:, 0:2], func=AF.Exp), reads=["lams"], writes=["lams"])
            sc.op("dve", lambda e: e.tensor_tensor(out=lams[:, 2:3], in0=lams[:, 0:1], in1=lams[:, 1:2], op=ALU.subtract), reads=["lams"], writes=["lams"])
            sc.op("dve", lambda e: e.tensor_scalar(out=lams[:, 3:4], in0=lams[:, 2:3], scalar1=lam_init, scalar2=-1.0, op0=ALU.add, op1=ALU.mult),
                  reads=["lams"], writes=["neglam"])
            neg_lam = lams[:, 3:4]
            sc.dma("sp", lambda e: e.dma_start(out=wd_t, in_=dnw_d[l, :].partition_broadcast(128)), writes=["wd"])
            sc.op("dve", lambda e: e.tensor_scalar(out=wd_t, in0=wd_t, scalar1=1.0 - lam_init, scalar2=None, op0=ALU.mult), reads=["wd"], writes=["wd"])
            sc.dma("sp", lambda e: e.dma_start(out=wg_t, in_=gnw_d[l, :].partition_broadcast(128)), writes=["wg"])
            sc.op("dve", lambda e: e.memset(Wg[0:33, :], 0.0), writes=["Wg"])
            sc.dma("pool", lambda e: e.dma_start(out=Wg[0:16, 0:256], in_=gup_d[l, 0]), writes=["Wg"])
            sc.dma("pool", lambda e: e.dma_start(out=Wg[16:32, 256:512], in_=gup_d[l, 1]), writes=["Wg"])
            sc.dma("pool", lambda e: e.dma_start(out=Wg[32:33, :], in_=gbias_d[l].rearrange("a n -> (a n)").partition_broadcast(1)), writes=["Wg"])
            sc.dma("sp", lambda e: e.dma_start(out=b1c, in_=b1_d[l].rearrange("(c p) -> p c", p=128), allow_slow_non_contiguous=True), writes=["b1c"])
            sc.dma("sp", lambda e: e.dma_start(out=b2t, in_=b2_d[l].partition_broadcast(128)), writes=["b2t"])
            sc.dma("sp", lambda e: e.dma_start(out=expB.rearrange("p h n -> p (h n)"), in_=eb_d), reads=["eb_d"], writes=[("expB", h) for h in range(4)])
            sc.op("dve", lambda e: e.memset(V[:, :, :, 128:129], 1.0), writes=[("V", t) for t in range(NT)])
            sc.op("dve", lambda e: e.memset(G33[32:33, :], 1.0), writes=["G33"])

            dump("XTin", XT, [128, 8, 2048], [("XT", t) for t in range(NT)])
            dump("Xin", X, [128, NT, 1024], [("X", t) for t in range(NT)])
            if stop == 'L' and l == nl - 1:
                raise _Stop()
            ps_rr = [0]

            def nextbank():
                b = ps_rr[0] % 6
                ps_rr[0] += 1
                return b

            xt_all = [("XT", t) for t in range(NT)]
            for blk in range(7):
                buf = blk % 2
                wb = WB[buf]
                kw = [("RW", buf, 0), ("RW", buf, 1)]
                if blk in (0, 1, 3, 6):
                    nch = 1 if blk == 6 else 4
                    for cc in range(nch):
                        for r in range(4):
                            bank = nextbank()
                            M = 32 if blk == 6 else 128
                            for c in range(8):
                                sc.op("pe", lambda e, c=c, cc=cc, r=r, bank=bank, M=M, wb=wb: e.matmul(
                                    PS[0:M, bank, :], lhsT=wb[:, c, cc * 128:cc * 128 + M], rhs=XT[:, c, r * 512:(r + 1) * 512],
                                    start=(c == 0), stop=(c == 7)), reads=kw + xt_all[r * 4:(r + 1) * 4], writes=psk(bank))
                            sl = slice(r * 512, (r + 1) * 512)
                            if blk == 0:
                                evac(QT[:, cc, sl], PS[:, bank, :], psk(bank), [("QT", cc, r)], scale=0.125)
                            elif blk == 1:
                                evac(KT[:, cc, sl], PS[:, bank, :], psk(bank), [("KT", cc, r)])
                            elif blk == 3:
                                if cc < 2:
                                    evac(gqT[:, cc, sl], PS[:, bank, :], psk(bank), [("gqT", 4 * r + i) for i in range(4)], scale=0.125)
                                else:
                                    evac(gkT[:, cc - 2, sl], PS[:, bank, :], psk(bank), [("gkT", 4 * r + i) for i in range(4)])
                            else:
                                evac(G33[0:32, sl], PS[0:32, bank, :], psk(bank), ["G33"])
                if blk == 3:
                    for t in range(NT):
                        bank = nextbank()
                        for cc in range(2):
                            sc.op("pe", lambda e, t=t, cc=cc, bank=bank: e.transpose(out=PSb[:, bank, cc * 128:(cc + 1) * 128], in_=gkT[:, cc, t * 128:(t + 1) * 128], identity=identb),
                                  reads=[("gkT", t), "identb"], writes=psk(bank))
                        evac(gk_tok[:, t, :], PSb[:, bank, 0:256], psk(bank), [("gk_tok", t)])
                if blk in (2, 4, 5):
                    for t in range(NT):
                        bank = nextbank()
                        c0, ncol = (0, 512)
                        for c in range(8):
                            sc.op("pe", lambda e, c=c, t=t, bank=bank, c0=c0, ncol=ncol, wb=wb: e.matmul(
                                PS[:, bank, 0:ncol], lhsT=XT[:, c, t * 128:(t + 1) * 128], rhs=wb[:, c, c0:c0 + ncol],
                                start=(c == 0), stop=(c == 7)), reads=kw + [("XT", t)], writes=psk(bank))
                        if blk == 2:
                            evac(V[:, t, :, 0:128], PS[:, bank, :].rearrange("p (h e) -> p h e", h=4), psk(bank), [("V", t)])
                        elif blk == 4:
                            evac(gv[:, t, :], PS[:, bank, :], psk(bank), [("gv", t)])
                        else:
                            sb = t % 2
                            sc.op("act", lambda e, bank=bank, sb=sb: e.activation(out=silu_t[sb], in_=PS[:, bank, :], func=AF.Silu),
                                  reads=psk(bank), writes=[("silu", sb)])
                            sc.op("dve", lambda e, t=t, sb=sb: e.tensor_tensor(
                                out=gr_s[:, t, :].rearrange("p (h e) -> p h e", h=4), in0=silu_t[sb].rearrange("p (h e) -> p h e", h=4),
                                in1=bc_mid(wg_t, 4), op=ALU.mult), reads=[("silu", sb), "wg"], writes=[("gr_s", t)])
                if blk + 2 < 7:
                    load_win_block(l, blk + 2, buf)
            for hf in range(2):
                sc.dma("pool", lambda e, hf=hf: e.dma_start(out=WO[:, hf * 4:(hf + 1) * 4, :],
                                                             in_=w_o_d[l].rearrange("(c p) n -> p c n", p=128)[:, hf * 4:(hf + 1) * 4, :]),
                       writes=[("RW", hf, 0), ("RW", hf, 1)])
            dump("QT", QT, [128, 4, 2048], [("QT", a, b) for a in range(4) for b in range(4)])
            dump("KT", KT, [128, 4, 2048], [("KT", a, b) for a in range(4) for b in range(4)])
            dump("V", V, [128, 16, 4, 129], [("V", t) for t in range(NT)])
            dump("expB", expB, [128, 4, 1152], [("expB", h) for h in range(4)])
            dump("gqT", gqT, [128, 2, 2048], [("gqT", t) for t in range(NT)])
            dump("gr_s", gr_s, [128, 16, 512], [("gr_s", t) for t in range(NT)])
            dump("G33", G33[0:33, :], [33, 2048], ["G33"])

            if stop == 'P' and l == nl - 1:
                raise _Stop()
            steps = [(h, r, j) for h in range(4) for r in range(4) for j in range(16)]

            def acc_ap(m, u):
                idx = m * 4 + u
                return PS[:, 4 + idx // 3, (idx % 3) * 160:(idx % 3) * 160 + 129]

            def acc_keys(m, u):
                return psk(4 + (m * 4 + u) // 3)

            Eb3 = view(R_X.base + 61440, 2048, BF16, "p (m n) -> p m n", m=2)
            EbL = [Eb[:, 0, :, :], Eb[:, 1, :, :], Eb3]

            def d_scores(i):
                h, r, j = steps[i]
                d = j - 4 * r
                mixed = (-1 <= d <= 4)
                sb = i % 2
                eb = i % 3
                E = EbL[eb]
                for m in range(2):
                    bank = sb * 2 + m
                    sc.op("pe", lambda e, h=h, r=r, j=j, m=m, bank=bank: e.matmul(
                        PS[:, bank, :], lhsT=KT[64 * m:64 * m + 64, h, j * 128:(j + 1) * 128],
                        rhs=QT[64 * m:64 * m + 64, h, r * 512:(r + 1) * 512], start=True, stop=True),
                        reads=[("KT", h, j // 4), ("QT", h, r)], writes=psk(bank))
                pk2 = psk(sb * 2) + psk(sb * 2 + 1)
                ek = [("E", eb, 0), ("E", eb, 1)]
                if mixed:
                    c0 = (4 - d) * 128
                    sc.op("act", lambda e, sb=sb, E=E: e.activation(out=E, in_=PS[:, sb * 2:sb * 2 + 2, :], func=AF.Exp),
                          reads=pk2, writes=ek)
                    for m in range(2):
                        sc.op("dve", lambda e, E=E, m=m, h=h, c0=c0: e.tensor_tensor(out=E[:, m, :], in0=E[:, m, :], in1=expB[:, h, c0:c0 + 512], op=ALU.mult),
                              reads=[("E", eb, m), ("expB", h)], writes=[("E", eb, m)])
                else:
                    side = 0 if d < 0 else 1
                    sc.op("act", lambda e, sb=sb, E=E, side=side, h=h: e.activation(
                        out=E, in_=PS[:, sb * 2:sb * 2 + 2, :], func=AF.Exp, bias=cb[:, side, h:h + 1]),
                        reads=pk2 + [("cb", 0), ("cb", 1)], writes=ek)

            def d_av(i):
                h, r, j = steps[i]
                eb = i % 3
                E = EbL[eb]
                for m in range(2):
                    for u in range(4):
                        sc.op("pe", lambda e, h=h, j=j, m=m, u=u, E=E: e.matmul(
                            acc_ap(m, u), lhsT=E[:, m, u * 128:(u + 1) * 128], rhs=V[:, j, h, 0:129],
                            start=(j == 0 and (m * 4 + u) % 3 == 0), stop=(j == 15), skip_group_check=True),
                            reads=[("E", eb, m), ("V", j)], writes=acc_keys(m, u))

            sm = dsm[0]
            def ak(*idx):
                return [("accS", i) for i in idx]

            def d_final(h, r):
                sc.op("dve", lambda e: e.tensor_copy(out=accS[:, 0:3, :], in_=PS[:, 4, 0:480].rearrange("p (a n) -> p a n", a=3)[:, :, 0:129]),
                      reads=psk(4), writes=ak(0, 1, 2) + [("silu", 0), ("silu", 1)])
                sc.op("dve", lambda e: e.tensor_copy(out=accS[:, 3:6, :], in_=PS[:, 5, 0:480].rearrange("p (a n) -> p a n", a=3)[:, :, 0:129]),
                      reads=psk(5), writes=ak(3, 4, 5))
                sc.op("dve", lambda e: e.tensor_copy(out=accS[:, 6:8, :], in_=PS[:, 6, 0:320].rearrange("p (a n) -> p a n", a=2)[:, :, 0:129]),
                      reads=psk(6), writes=ak(6, 7))

            def d_final2(h, r):
                sc.op("dve", lambda e: e.reciprocal(out=sm[:, 0:8], in_=accS[:, :, 128]), reads=ak(*range(8)), writes=["dsm"])
                sc.op("dve", lambda e: e.tensor_scalar(out=sm[:, 4:8], in0=sm[:, 4:8], scalar1=neg_lam, scalar2=None, op0=ALU.mult),
                      reads=["dsm", "neglam"], writes=["dsm"])
                sc.op("dve", lambda e: e.memset(sm[:, 8:12], 0.0), writes=[("dss", u) for u in range(4)])

            def d_final_u_stages(u):
                return [
                    lambda: sc.op("dve", lambda e: e.tensor_scalar(out=accS[:, u, 0:128], in0=accS[:, u, 0:128], scalar1=sm[:, u:u + 1], scalar2=None, op0=ALU.mult),
                                  reads=["dsm"] + ak(u), writes=ak(u)),
                    lambda: sc.op("dve", lambda e: e.scalar_tensor_tensor(out=accS[:, u, 0:128], in0=accS[:, 4 + u, 0:128], scalar=sm[:, 4 + u:5 + u],
                                                                          in1=accS[:, u, 0:128], op0=ALU.mult, op1=ALU.add),
                                  reads=["dsm"] + ak(u, 4 + u), writes=ak(u)),
                    lambda: sc.op("dve", lambda e: e.scalar_tensor_tensor(out=accS[:, 4 + u, 0:128], in0=accS[:, u, 0:128], scalar=1.0, in1=accS[:, u, 0:128],
                                                                          op0=ALU.mult, op1=ALU.mult, accum_out=sm[:, 8 + u:9 + u]),
                                  reads=ak(u), writes=ak(4 + u) + [("dss", u)]),
                ]

            def d_final_u(u0):
                sa, sb = d_final_u_stages(u0), d_final_u_stages(u0 + 1)
                for fa, fb in zip(sa, sb):
                    fa()
                    fb()

            def d_final_b(h, r):
                sc.op("act", lambda e: e.activation(out=sm[:, 12:16], in_=sm[:, 8:12], func=AF.Ln, bias=1e-5, scale=1.0 / 128),
                      reads=[("dss", u) for u in range(4)], writes=["drs"])
                sc.op("act", lambda e: e.activation(out=sm[:, 12:16], in_=sm[:, 12:16], func=AF.Exp, scale=-0.5), reads=["drs"], writes=["drs"])
                for u in range(4):
                    sc.op("dve", lambda e, u=u: e.scalar_tensor_tensor(out=d_y[:, u, :], in0=accS[:, u, 0:128], scalar=sm[:, 12 + u:13 + u], in1=wd_t,
                                                                       op0=ALU.mult, op1=ALU.mult),
                          reads=ak(u) + ["drs", "wd"], writes=[("dy", u)])

            def d_final_pe(h, r):
                for u in range(4):
                    sc.op("pe", lambda e, u=u: e.transpose(out=PSb[:, 7, u * 128:(u + 1) * 128], in_=d_y[:, u, :], identity=identb),
                          reads=[("dy", u), "identb"], writes=psk(7))
                sc.op("dve", lambda e, h=h, r=r: e.tensor_copy(out=XT[:, h, r * 512:(r + 1) * 512], in_=PSb[:, 7, 0:512]),
                      reads=psk(7), writes=[("XT", 4 * r + i) for i in range(4)])

            pend = []
            pend_b = []
            pend_u = []
            d_scores(0)
            d_scores(1)
            for i in range(len(steps)):
                h, r, j = steps[i]
                if j == 15:
                    d_av(i)
                    d_final(h, r)
                    if i + 2 < len(steps):
                        d_scores(i + 2)
                    d_final2(h, r)
                else:
                    if i + 2 < len(steps):
                        d_scores(i + 2)
                    d_av(i)
                if j == 15:
                    pend.append((h, r))
                    pend_b.append((h, r))
                    pend_u.extend([0, 2])
                    d_final_u(pend_u.pop(0))
                elif pend_u:
                    d_final_u(pend_u.pop(0))
                elif j == 4 and pend_b:
                    d_final_b(*pend_b.pop(0))
                elif j == 7 and pend:
                    d_final_pe(*pend.pop(0))
            while pend_u:
                d_final_u(pend_u.pop(0))
            while pend_b:
                d_final_b(*pend_b.pop(0))
            while pend:
                d_final_pe(*pend.pop(0))
            dump("mixT_d", XT, [128, 8, 2048], [("XT", t) for t in range(NT)])

            if stop == 'D' and l == nl - 1:
                raise _Stop()
            sc.barrier()
            R_D.reset()
            qf = R_D.get(8192, BF16, "p (c n) -> p c n", c=2)
            kf = R_D.get(8192, BF16, "p (c n) -> p c n", c=2)
            kd_f = R_D.get(8192, BF16, "p (t n) -> p t n", t=16)
            Sbf = R_D.get(16384, BF16, "p (d q t e) -> p d q t e", d=2, q=2, t=16)
            stm2 = [R_D.get(4096, F32, "p (d q n) -> p d q n", d=2, q=2) for _ in range(2)]
            _p0 = R_D.pos
            sp_ = [R_D.get(2048, F32) for _ in range(2)]
            _p1 = R_D.pos
            ebt = [R_D.get(2 * 2 * 129 * 4, F32, "p (d q n) -> p d q n", d=2, q=2) for _ in range(2)]
            _p2 = R_D.pos
            enbt = [R_D.get(2 * 2 * 128 * 4, F32, "p (d q n) -> p d q n", d=2, q=2) for _ in range(2)]
            erem = [R_D.get(2048, F32) for _ in range(2)]
            dS = [R_D.get(1024, F32) for _ in range(4)]
            _pend = R_D.pos
            Am = [R_X.get(4 * 2 * 128 * 2, BF16, "p (h d n) -> p h d n", h=4, d=2) for _ in range(2)]
            R_D.pos = _p2
            g_y = [R_D.get(1024, BF16) for _ in range(2)]
            g_junk = R_D.get(512, F32)
            R_D.pos = _pend
            qb, kb, kd_b = gqT, gkT, gk_tok
            maskf = Uf[:, 0:128]
            maskb = Usf

            def tl(t):
                return slice(t * 128, (t + 1) * 128)

            def prep_A(t):
                b = t % 2
                sp = sp_[b]
                zb = 0 if b == 0 else 7
                sc.op("pe", lambda e: e.matmul(PS[:, zb, :], lhsT=G33[0:33, tl(t)], rhs=Wg[0:33, :], start=True, stop=True),
                      reads=["G33", "Wg"], writes=psk(zb))
                sc.op("act", lambda e: e.activation(out=sp, in_=PS[:, zb, :], func=AF.Exp, scale=-1.0), reads=psk(zb), writes=[("sp", b)])
                sc.op("act", lambda e: e.activation(out=sp, in_=sp, func=AF.Ln, bias=1.0, scale=1.0), reads=[("sp", b)], writes=[("sp", b)])

            prep_A(0)
            for t in range(NT):
                b = t % 2
                sp = sp_[b]
                if t + 1 < NT:
                    prep_A(t + 1)
                sc.op("pe", lambda e, sp=sp: e.matmul(PS[:, 1, 0:256], lhsT=Usf, rhs=sp[:, 0:256], start=True, stop=True), reads=[("sp", b), "Usf", "Usb"], writes=psk(1))
                sc.op("pe", lambda e, sp=sp: e.matmul(PS[:, 1, 256:512], lhsT=Usb, rhs=sp[:, 256:512], start=True, stop=True), reads=[("sp", b), "Usf", "Usb"], writes=psk(1))
                sc.op("act", lambda e, b=b: e.activation(out=erem[b], in_=PS[:, 1, :], func=AF.Exp, scale=-1.0 / 16), reads=psk(1), writes=[("erem", b)])
                sc.op("dve", lambda e, t=t, b=b: e.tensor_tensor(out=kd_f[:, t, :], in0=gk_tok[:, t, :], in1=erem[b][:, 0:256], op=ALU.mult),
                      reads=[("gk_tok", t), ("erem", b)], writes=[("kd_f", t)])
                sc.op("dve", lambda e, t=t, b=b: e.tensor_tensor(out=kd_b[:, t, :], in0=gk_tok[:, t, :], in1=erem[b][:, 256:512], op=ALU.mult),
                      reads=[("gk_tok", t), ("erem", b), ("kd_f", t)], writes=[("gk_tok", t)])
                for d in range(2):
                    U = Uf if d == 0 else Ub
                    for q in range(2):
                        sc.op("pe", lambda e, sp=sp, d=d, q=q, U=U: e.matmul(PS[:, 2 + d, q * 160:q * 160 + 129],
                                                                            lhsT=sp[:, d * 256 + q * 128:d * 256 + (q + 1) * 128], rhs=U, start=True, stop=True),
                              reads=[("sp", b), "Uf", "Ub"], writes=psk(2 + d))
                src4 = PS[:, 2:4, 0:320].rearrange("p a (q n) -> p a q n", q=2)
                sc.op("act", lambda e, b=b, src4=src4: e.activation(out=ebt[b], in_=src4[:, :, :, 0:129], func=AF.Exp, scale=-1.0 / 16),
                      reads=psk(2) + psk(3), writes=[("eb", b, 0), ("eb", b, 1)])
                sc.op("act", lambda e, b=b, src4=src4: e.activation(out=enbt[b], in_=src4[:, :, :, 0:128], func=AF.Exp, scale=1.0 / 16),
                      reads=psk(2) + psk(3), writes=[("enb", b, 0), ("enb", b, 1)])
                sc.op("dve", lambda e, t=t, b=b: e.tensor_tensor(out=qf[:, :, tl(t)], in0=gqT[:, :, tl(t)], in1=ebt[b][:, 0, :, 0:128], op=ALU.mult),
                      reads=[("gqT", t), ("eb", b, 0)], writes=[("qf", t)])
                sc.op("dve", lambda e, t=t, b=b: e.tensor_tensor(out=kf[:, :, tl(t)], in0=gkT[:, :, tl(t)], in1=enbt[b][:, 0, :, :], op=ALU.mult),
                      reads=[("gkT", t), ("enb", b, 0)], writes=[("kf", t)])
                sc.op("dve", lambda e, t=t, b=b: e.tensor_tensor(out=qb[:, :, tl(t)], in0=gqT[:, :, tl(t)], in1=ebt[b][:, 1, :, 0:128], op=ALU.mult),
                      reads=[("gqT", t), ("eb", b, 1), ("qf", t)], writes=[("gqT", t)])
                sc.op("dve", lambda e, t=t, b=b: e.tensor_tensor(out=kb[:, :, tl(t)], in0=gkT[:, :, tl(t)], in1=enbt[b][:, 1, :, :], op=ALU.mult),
                      reads=[("gkT", t), ("enb", b, 1), ("kf", t)], writes=[("gkT", t)])
                sc.op("dve", lambda e, t=t, b=b: e.tensor_copy(out=decs[:, :, :, t:t + 1], in_=ebt[b][:, :, :, 128:129]),
                      reads=[("eb", b, 0), ("eb", b, 1)], writes=[("decs", t)])
            dump("qf", qf, [128, 2, 2048], [("qf", t) for t in range(NT)])
            dump("kd_f", kd_f, [128, 16, 256], [("kd_f", t) for t in range(NT)])
            dump("decs", decs, [128, 2, 2, 16], [("decs", t) for t in range(NT)])

            if stop == 'G1' and l == nl - 1:
                raise _Stop()
            sc.op("dve", lambda e: e.memset(stm2[0], 0.0), writes=[("stm", 0, d, q) for d in range(2) for q in range(2)])
            chains = [(d, q) for d in range(2) for q in range(2)]
            par = {c: 0 for c in chains}
            for i in range(NT):
                todo = []
                for ci, (d, q) in enumerate(chains):
                    t = i if d == 0 else NT - 1 - i
                    cur = par[(d, q)]
                    if i > 0:
                        sc.op("act", lambda e, d=d, q=q, t=t, cur=cur: e.activation(out=Sbf[0:64, d, q, t, :], in_=stm2[cur][0:64, d, q, 0:128], func=AF.Copy),
                              reads=[("stm", cur, d, q)], writes=[("Sbf", d, q, t, 0)])
                        sc.op("dve", lambda e, d=d, q=q, t=t, cur=cur: e.tensor_copy(out=Sbf[64:128, d, q, t, :], in_=stm2[cur][64:128, d, q, 128:256]),
                              reads=[("stm", cur, d, q)], writes=[("Sbf", d, q, t, 1)])
                    if i == NT - 1:
                        continue
                    kd = kd_f if d == 0 else kd_b
                    kkey = "kd_f" if d == 0 else "gk_tok"
                    pslot = ci % 2
                    pk = psk(4 + pslot)
                    sc.op("pe", lambda e, kd=kd, t=t, q=q, pslot=pslot: e.matmul(PS[:, 4 + pslot, 0:256], lhsT=kd[:, t, q * 128:(q + 1) * 128],
                                                                                rhs=gv[:, t, q * 256:(q + 1) * 256], start=True, stop=True),
                          reads=[(kkey, t), ("gv", t)], writes=pk)
                    if os.environ.get("GSKIP") != "evac":
                        sc.op("act", lambda e, ci=ci, pslot=pslot: e.activation(out=dS[ci], in_=PS[:, 4 + pslot, 0:256], func=AF.Copy),
                              reads=pk, writes=[("dS", ci)])
                    todo.append((ci, d, q, t, cur))
                for (ci, d, q, t, cur) in todo:
                    if os.environ.get("GSKIP") == "upd":
                        par[(d, q)] = 1 - cur
                        continue
                    sc.op("dve", lambda e, ci=ci, d=d, q=q, t=t, cur=cur: e.scalar_tensor_tensor(
                        out=stm2[1 - cur][:, d, q, :], in0=stm2[cur][:, d, q, :], scalar=decs[:, d, q, t:t + 1], in1=dS[ci],
                        op0=ALU.mult, op1=ALU.add), reads=[("stm", cur, d, q), ("decs", t), ("dS", ci)], writes=[("stm", 1 - cur, d, q)])
                    par[(d, q)] = 1 - cur
            dump("Sbf", Sbf, [128, 2, 2, 16, 128], [("Sbf", d, q, t, hh) for d in range(2) for q in range(2) for t in range(NT) for hh in range(2)])
            if stop == 'G2' and l == nl - 1:
                raise _Stop()

            def g_A(t):
                b = t % 2
                A = Am[b]
                for half in range(2):
                    sbank = (4 + half) if os.environ.get('GBANK') else (2 * b + half)
                    items = []
                    for sq in range(4):
                        h, d = half + 2 * (sq // 2), sq % 2
                        q = h // 2
                        base = (h % 2) * 64
                        kk = kf if d == 0 else kb
                        qq = qf if d == 0 else qb
                        kkey = ("kf", t) if d == 0 else ("gkT", t)
                        qkey = ("qf", t) if d == 0 else ("gqT", t)
                        sc.op("pe", lambda e, kk=kk, qq=qq, q=q, base=base, sbank=sbank, sq=sq: e.matmul(
                            PS[:, sbank, sq * 128:(sq + 1) * 128], lhsT=kk[base:base + 64, q, tl(t)], rhs=qq[base:base + 64, q, tl(t)], start=True, stop=True),
                            reads=[kkey, qkey], writes=psk(sbank))
                        items.append((sq, h, d))
                        if os.environ.get("GOLD"):
                            mk = maskf if d == 0 else maskb
                            sc.op("dve", lambda e, A=A, h=h, d=d, sbank=sbank, sq=sq, mk=mk: e.tensor_tensor(
                                out=A[:, h, d, :], in0=PS[:, sbank, sq * 128:(sq + 1) * 128], in1=mk, op=ALU.mult),
                                reads=psk(sbank) + ["Uf", "Usf"], writes=[("A", b, h, d)])
                    if os.environ.get("GOLD"):
                        continue
                    for (sq, h, d) in items:
                        mk = maskf if d == 0 else maskb
                        sc.op("dve", lambda e, A=A, h=h, d=d, sbank=sbank, sq=sq, mk=mk: e.tensor_tensor(
                            out=A[:, h, d, :], in0=PS[:, sbank, sq * 128:(sq + 1) * 128], in1=mk, op=ALU.mult),
                            reads=psk(sbank) + ["Uf", "Usf"], writes=[("A", b, h, d)])

            def g_B(t):
                b = t % 2
                A = Am[b]
                obank = (0 + b) if os.environ.get('GBANK') else (4 + b)
                for h in range(4):
                    q = h // 2
                    base = (h % 2) * 64
                    oh_ = PS[:, obank, h * 128:(h + 1) * 128]
                    ok = psk(obank)
                    inter_f = t > 0
                    inter_b = t < NT - 1
                    sc.op("pe", lambda e, A=A, h=h, oh_=oh_: e.matmul(oh_, lhsT=A[:, h, 0, :], rhs=gv[:, t, h * 128:(h + 1) * 128], start=True, stop=False),
                          reads=[("A", b, h, 0), ("gv", t)], writes=ok)
                    sc.op("pe", lambda e, A=A, h=h, oh_=oh_, fin=(not inter_f and not inter_b): e.matmul(
                        oh_, lhsT=A[:, h, 1, :], rhs=gv[:, t, h * 128:(h + 1) * 128], start=False, stop=fin),
                        reads=[("A", b, h, 1), ("gv", t)], writes=ok)
                    if inter_f:
                        sc.op("pe", lambda e, q=q, base=base, oh_=oh_, fin=(not inter_b): e.matmul(
                            oh_, lhsT=qf[base:base + 64, q, tl(t)], rhs=Sbf[base:base + 64, 0, q, t, :], start=False, stop=fin),
                            reads=[("qf", t), ("Sbf", 0, q, t, h % 2)], writes=ok)
                    if inter_b:
                        sc.op("pe", lambda e, q=q, base=base, oh_=oh_: e.matmul(
                            oh_, lhsT=qb[base:base + 64, q, tl(t)], rhs=Sbf[base:base + 64, 1, q, t, :], start=False, stop=True),
                            reads=[("gqT", t), ("Sbf", 1, q, t, h % 2)], writes=ok)

            def g_norm(t):
                b = t % 2
                sm = gsm[b]
                obank = (0 + b) if os.environ.get('GBANK') else (4 + b)
                okall = psk(obank)
                for h in range(4):
                    sc.op("act", lambda e, h=h, sm=sm: e.activation(out=g_junk, in_=PS[:, obank, h * 128:(h + 1) * 128], func=AF.Square, accum_out=sm[:, h:h + 1]),
                          reads=okall, writes=[("gss", b, h)] + ([("enb", 1, 0), ("enb", 1, 1)] if h == 0 else []))
                sc.op("act", lambda e, sm=sm: e.activation(out=sm[:, 4:8], in_=sm[:, 0:4], func=AF.Sqrt, bias=1e-5, scale=1.0 / 128),
                      reads=[("gss", b, h) for h in range(4)], writes=[("grs", b)])
                sc.op("dve", lambda e, sm=sm: e.reciprocal(out=sm[:, 4:8], in_=sm[:, 4:8]), reads=[("grs", b)], writes=[("grs", b)])
                for h in range(4):
                    sc.op("dve", lambda e, h=h, sm=sm, b=b: e.scalar_tensor_tensor(
                        out=g_y[b][:, h * 128:(h + 1) * 128], in0=PS[:, obank, h * 128:(h + 1) * 128], scalar=sm[:, 4 + h:5 + h],
                        in1=gr_s[:, t, h * 128:(h + 1) * 128], op0=ALU.mult, op1=ALU.mult),
                        reads=psk(obank) + [("grs", b), ("gr_s", t)], writes=[("gy", b, h)] + ([("enb", 0, 0), ("enb", 0, 1)] if h == 0 else []))

            def g_tr(t):
                b = t % 2
                tk = psk(6 + b)
                for h in range(4):
                    sc.op("pe", lambda e, h=h, b=b: e.transpose(out=PSb[:, 6 + b, h * 128:(h + 1) * 128], in_=g_y[b][:, h * 128:(h + 1) * 128], identity=identb),
                          reads=[("gy", b, h), "identb"], writes=tk)
                sc.op("act", lambda e, b=b: e.activation(out=XT[:, 4:8, tl(t)], in_=PSb[:, 6 + b, 0:512].rearrange("p (c n) -> p c n", c=4), func=AF.Copy),
                      reads=tk, writes=[("XT", t)])

            g_A(0)
            for t in range(NT):
                if t + 1 < NT:
                    g_A(t + 1)
                if os.environ.get("GSKIP") == "B":
                    continue
                g_B(t)
                if os.environ.get("GSKIP") == "norm":
                    continue
                g_norm(t)
                if os.environ.get("GSKIP") == "tr":
                    continue
                if t > 0:
                    g_tr(t - 1)
            if not os.environ.get("GSKIP"):
                g_tr(NT - 1)
            dump("mixT", XT, [128, 8, 2048], [("XT", t) for t in range(NT)])

            if stop == 'G' and l == nl - 1:
                raise _Stop()
            sc.barrier()
            load_ln_params(ln1g_d[l], ln1b_d[l])
            R_D.reset()
            W1B = [R_D.get(8192, BF16, "p (c n) -> p c n", c=8) for _ in range(2)]
            W2B = [R_D.get(8192, BF16, "p (c n) -> p c n", c=4) for _ in range(2)]
            hT = R_D.get(16384, BF16, "p (c n) -> p c n", c=4)
            relu_t = [R_D.get(2048, F32) for _ in range(2)]

            def load_ffn_block(fb, buf):
                s1 = w1_d[l, :, fb * 512:(fb + 1) * 512].rearrange("(c p) n -> p c n", p=128)
                s2 = w2_d[l, fb * 512:(fb + 1) * 512, :].rearrange("(c p) n -> p c n", p=128)
                for hf in range(2):
                    sc.dma("pool", lambda e, hf=hf: e.dma_start(out=W1B[buf][:, hf * 4:(hf + 1) * 4, :], in_=s1[:, hf * 4:(hf + 1) * 4, :]),
                           writes=[("W1B", buf, hf)])
                for hf in range(2):
                    sc.dma("pool", lambda e, hf=hf: e.dma_start(out=W2B[buf][:, hf * 2:(hf + 1) * 2, :], in_=s2[:, hf * 2:(hf + 1) * 2, :]),
                           writes=[("W2B", buf, hf)])

            load_ffn_block(0, 0)
            load_ffn_block(1, 1)
            for t in range(NT):
                sc.dma("sp", lambda e, t=t: e.dma_start(out=X[:, t, :], in_=xs_d[t * 128:(t + 1) * 128, :]), reads=[("xsd", t)], writes=[("X", t)])
            def o_mm(t):
                yb = (t % 3) * 2
                for hf in range(2):
                    for c in range(8):
                        sc.op("pe", lambda e, c=c, hf=hf, t=t, yb=yb: e.matmul(PS[:, yb + hf, :], lhsT=XT[:, c, tl(t)], rhs=WO[:, c, hf * 512:(hf + 1) * 512],
                                                                              start=(c == 0), stop=(c == 7)),
                              reads=[("XT", t), ("RW", 0, 0), ("RW", 0, 1), ("RW", 1, 0), ("RW", 1, 1)], writes=psk(yb + hf))

            def o_ln_stages(t):
                yb = (t % 3) * 2
                resid = lambda: sc.op("dve", lambda e: e.scalar_tensor_tensor(out=X[:, t, :], in0=X[:, t, :], scalar=ALPHA,
                                                                              in1=PS[:, yb:yb + 2, :].rearrange("p a n -> p (a n)"), op0=ALU.mult, op1=ALU.add),
                                      reads=[("X", t)] + psk(yb) + psk(yb + 1), writes=[("X", t)])
                return [resid] + ln_a_stages(t)

            def o_ln_group(ts):
                lists = [o_ln_stages(t) for t in ts]
                for k in range(len(lists[0])):
                    for lst in lists:
                        lst[k]()

            for t0 in range(3):
                o_mm(t0)
            o_ln_group([0, 1])
            o_ln_group([2])
            for t in range(0, NT, 2):
                nxt = [u for u in (t + 3, t + 4) if u < NT]
                for u in nxt:
                    o_mm(u)
                ln_b1(t, None)
                ln_b1(t + 1, None)
                if nxt:
                    o_ln_group(nxt)
                ln_b2(t, None)
                ln_b2(t + 1, None)
            dump("x1T", XT, [128, 8, 2048], [("XT", t) for t in range(NT)])

            if stop == 'O' and l == nl - 1:
                raise _Stop()
            if not last:
                load_win_block(l + 1, 0, 0)
                load_win_block(l + 1, 1, 1)
            hrr = [0]
            for fb in range(8):
                buf = fb % 2
                for r in range(4):
                    for fc in range(4):
                        bank = 4 + hrr[0] % 3
                        rb = hrr[0] % 2
                        hrr[0] += 1
                        for c in range(8):
                            sc.op("pe", lambda e, c=c, fc=fc, r=r, bank=bank, buf=buf: e.matmul(
                                PS[:, bank, :], lhsT=W1B[buf][:, c, fc * 128:(fc + 1) * 128], rhs=XT[:, c, r * 512:(r + 1) * 512],
                                start=(c == 0), stop=(c == 7)), reads=[("W1B", buf, 0), ("W1B", buf, 1)] + xt_all[r * 4:(r + 1) * 4], writes=psk(bank))
                        fcol = fb * 4 + fc
                        sc.op("act", lambda e, bank=bank, rb=rb, fcol=fcol: e.activation(out=relu_t[rb], in_=PS[:, bank, :], func=AF.Relu,
                                                                                         bias=b1c[:, fcol:fcol + 1], scale=1.0),
                              reads=psk(bank) + ["b1c"], writes=[("relu", rb)])
                        sc.op("dve", lambda e, rb=rb, fc=fc, r=r: e.tensor_tensor(out=hT[:, fc, r * 512:(r + 1) * 512], in0=relu_t[rb], in1=relu_t[rb], op=ALU.mult),
                              reads=[("relu", rb)], writes=[("hT", fc, r)])
                for t in range(NT):
                    yb = (t % 2) * 2
                    for hf in range(2):
                        for fc in range(4):
                            sc.op("pe", lambda e, fc=fc, hf=hf, t=t, yb=yb, buf=buf: e.matmul(
                                PS[:, yb + hf, :], lhsT=hT[:, fc, tl(t)], rhs=W2B[buf][:, fc, hf * 512:(hf + 1) * 512],
                                start=(fc == 0), stop=(fc == 3)), reads=[("hT", fc, t // 4), ("W2B", buf, 0), ("W2B", buf, 1)], writes=psk(yb + hf))
                    if fb == 0:
                        sc.op("dve", lambda e, t=t, yb=yb: e.scalar_tensor_tensor(out=X[:, t, :], in0=X[:, t, :], scalar=ALPHA,
                                                                                  in1=PS[:, yb:yb + 2, :].rearrange("p a n -> p (a n)"), op0=ALU.mult, op1=ALU.add),
                              reads=[("X", t)] + psk(yb) + psk(yb + 1), writes=[("X", t)])
                        sc.op("pool", lambda e, t=t: e.tensor_tensor(out=X[:, t, :], in0=X[:, t, :], in1=b2t, op=ALU.add), reads=[("X", t), "b2t"], writes=[("X", t)])
                    else:
                        sc.op("dve", lambda e, t=t, yb=yb: e.tensor_tensor(out=X[:, t, :], in0=X[:, t, :], in1=PS[:, yb:yb + 2, :].rearrange("p a n -> p (a n)"), op=ALU.add),
                              reads=[("X", t)] + psk(yb) + psk(yb + 1), writes=[("X", t)])
                if fb + 2 < 8:
                    load_ffn_block(fb + 2, buf)
            if stop == 'F' and l == nl - 1:
                raise _Stop()
            load_ln_params(ln2g_d[l], ln2b_d[l])
            ln_all(out_d if last else xs_d)
            sc.barrier()


        for _l in range(nl):
            do_layer(_l)
    except _Stop:
        pass
    out_dmas = [o for o in sc.ops if o.is_dma and o.dkey in [("xs", i) for i in range(4)]]
    fin = {}
    for o in out_dmas:
        fin[o.dkey] = o
    finals = list(fin.values()) + list(dbg_out.values())
    sc.emit(final_wait_ops=finals)
    es.close()
    return nc, sc


_CONST = None


def kernel(**inputs):
    global _CONST
    if _CONST is None:
        _CONST = _constants()
    nc, _ = build(2)
    x = np.ascontiguousarray(inputs["x"], dtype=np.float32)
    shared = {k: np.ascontiguousarray(v, dtype=np.float32) for k, v in inputs.items() if k != "x"}
    shared.update(_CONST)
    in_maps = []
    for b in range(8):
        m = dict(shared)
        m["x"] = x[b]
        in_maps.append(m)
    res = run_bass_kernel_spmd(nc, in_maps, core_ids=list(range(8)))
    return np.stack([r["out"] for r in res.results], axis=0).astype(np.float32)
```
